# Optimizing a Trainium2 kernel written in Bass

```python
import jax, jax.numpy as jnp
from jax import lax
import numpy as np

D_MODEL = 1024
BATCH = 8
SEQ = 4096
DEPTH = 1

HEAD_DIM = 64
RWKV_HEADS = D_MODEL // HEAD_DIM
RWKV_WIDTH = RWKV_HEADS * HEAD_DIM
DECAY_LORA = 64
AAA_LORA = 64
GATE_LORA = 160
RWKV_COLS = 3 * RWKV_WIDTH + DECAY_LORA + AAA_LORA + GATE_LORA
GN_EPS = 64e-5
ATTN_PAIRS = ((128, 1), (512, 4), (2048, 16))
ATTN_GROUPS = 3
ATTN_HEADS_PER_GROUP = 4
ATTN_HEADS = ATTN_GROUPS * ATTN_HEADS_PER_GROUP
ATTN_WIDTH = ATTN_HEADS * HEAD_DIM
N_BRANCHES = 2
IN_COLS = RWKV_COLS + 3 * ATTN_WIDTH + N_BRANCHES * D_MODEL
D_FF = 2816
RMS_EPS = 1e-6
NEG_INF = -1e30

kernel_name = 'hybrid_rwkv7_dilated_attn_macaron'


def _split_last(t, sizes):
    out, start = [], 0
    for n in sizes:
        out.append(t[..., start:start + n])
        start += n
    return out


def rms_norm(x, g):
    xf = x.astype(jnp.float32)
    y = xf * lax.rsqrt(jnp.mean(xf * xf, axis=-1, keepdims=True) + RMS_EPS)
    return y.astype(x.dtype) * g


def swiglu(h, w_in, w_out):
    gate, up = _split_last(h @ w_in, (D_FF, D_FF))
    return (jax.nn.silu(gate) * up) @ w_out


def token_shift(p, mu):
    prev = jnp.pad(p, ((0, 0), (1, 0), (0, 0)))[:, :-1]
    return p + (prev - p) * mu


def wkv7_scan(r, w, k, v, a, b):
    B, S, H, N = r.shape
    to_time = lambda t: jnp.moveaxis(t, 1, 0)

    def step(state, inp):
        r_t, w_t, k_t, v_t, a_t, b_t = inp
        sa = jnp.einsum('bhvk,bhk->bhv', state, a_t)
        state = (state * w_t[:, :, None, :] + sa[..., None] * b_t[:, :, None, :]
                 + v_t[..., None] * k_t[:, :, None, :])
        return state, jnp.einsum('bhvk,bhk->bhv', state, r_t)

    state0 = jnp.zeros((B, H, N, N), jnp.float32)
    _, out = lax.scan(step, state0, (to_time(r), to_time(w), to_time(k),
                                      to_time(v), to_time(a), to_time(b)))
    return jnp.moveaxis(out, 0, 1)


def rwkv7_time_mix(p, mu, w0, w2, a0, a2, g2, k_k, k_a, r_k, ln_w, ln_b):
    B, S, _ = p.shape
    H, N = RWKV_HEADS, HEAD_DIM
    p = token_shift(p.astype(jnp.float32), mu)
    r, k, v, wd, ad, gd = _split_last(
        p, (RWKV_WIDTH, RWKV_WIDTH, RWKV_WIDTH, DECAY_LORA, AAA_LORA, GATE_LORA))
    w = -jax.nn.softplus(-(w0 + jnp.tanh(wd) @ w2)) - 0.5
    decay = jnp.exp(-jnp.exp(w))
    a = jax.nn.sigmoid(a0 + ad @ a2)
    g = jax.nn.sigmoid(gd) @ g2
    heads = lambda t: t.reshape(B, S, H, N)
    kk = heads(k * k_k)
    kk = kk / jnp.maximum(jnp.sqrt(jnp.sum(kk * kk, axis=-1, keepdims=True)), 1e-12)
    k = k * (1.0 + (a - 1.0) * k_a)
    rh, kh, vh, ah = heads(r), heads(k), heads(v), heads(a)
    wkv = wkv7_scan(rh, heads(decay), kh, vh, -kk, kk * ah)
    mean = jnp.mean(wkv, axis=-1, keepdims=True)
    var = jnp.mean(jnp.square(wkv - mean), axis=-1, keepdims=True)
    y = ((wkv - mean) * lax.rsqrt(var + GN_EPS)).reshape(B, S, RWKV_WIDTH) * ln_w + ln_b
    bonus = jnp.sum(rh * kh * r_k, axis=-1, keepdims=True) * vh
    return (y + bonus.reshape(B, S, RWKV_WIDTH)) * g


def dilated_group_attention(q, k, v, window, dilation):
    B, S, Hg, E = q.shape
    span = window // dilation
    blk = span
    sub = S // dilation
    nb = -(-sub // blk)
    pad = nb * blk - sub

    def to_sub(t):
        t = t.reshape(B, sub, dilation, Hg, E).transpose(0, 2, 1, 3, 4)
        t = jnp.pad(t, ((0, 0), (0, 0), (0, pad), (0, 0), (0, 0)))
        return t.reshape(B, dilation, nb, blk, Hg, E)

    def with_prev(t):
        tp = jnp.pad(t, ((0, 0), (0, 0), (1, 0), (0, 0), (0, 0), (0, 0)))
        return jnp.concatenate([tp[:, :, :-1], tp[:, :, 1:]], axis=3)

    qb = to_sub(q)
    kw, vw = with_prev(to_sub(k)), with_prev(to_sub(v))
    s = jnp.einsum('bdnqhe,bdnkhe->bdnhqk', qb, kw)
    qi = jnp.arange(blk)[:, None]
    kj = jnp.arange(2 * blk)[None, :]
    dist = qi + blk - kj
    bidx = jnp.arange(nb)[:, None, None]
    valid = (dist >= 0) & (dist <= span) & (bidx * blk + kj - blk >= 0)
    s = jnp.where(valid[None, None, :, None], s, NEG_INF)
    m = jnp.max(s, axis=-1, keepdims=True)
    pexp = jnp.exp(s - m)
    den = jnp.sum(pexp, axis=-1, keepdims=True)
    o = jnp.einsum('bdnhqk,bdnkhe->bdnqhe', pexp / den, vw)
    lse = (m + jnp.log(den))[..., 0].transpose(0, 1, 2, 4, 3)
    o = o.reshape(B, dilation, nb * blk, Hg, E)[:, :, :sub]
    o = o.transpose(0, 2, 1, 3, 4).reshape(B, S, Hg, E)
    lse = lse.reshape(B, dilation, nb * blk, Hg)[:, :, :sub]
    lse = lse.transpose(0, 2, 1, 3).reshape(B, S, Hg)
    return o, lse


def dilated_attention(pq, pk, pv, q_gain, k_gain):
    B, S, _ = pq.shape
    heads = lambda t: t.astype(jnp.float32).reshape(B, S, ATTN_HEADS, HEAD_DIM)
    q = rms_norm(heads(pq), q_gain) * (HEAD_DIM ** -0.5)
    k = rms_norm(heads(pk), k_gain)
    v = heads(pv)
    outs, lses = [], []
    for gi, (window, dilation) in enumerate(ATTN_PAIRS):
        sl = slice(gi * ATTN_HEADS_PER_GROUP, (gi + 1) * ATTN_HEADS_PER_GROUP)
        o, lse = dilated_group_attention(q[:, :, sl], k[:, :, sl], v[:, :, sl], window, dilation)
        outs.append(o)
        lses.append(lse)
    o = jnp.stack(outs, axis=2)
    alpha = jax.nn.softmax(jnp.stack(lses, axis=2), axis=2)
    return (o * alpha[..., None]).reshape(B, S, ATTN_WIDTH)


def setup_inputs(seed: int = 0) -> dict:
    key = jax.random.key(seed)
    ks = jax.random.split(key, 27)
    f32 = jnp.float32
    L, D = DEPTH, D_MODEL

    def nrm(k, shape, scale):
        return scale * jax.random.normal(k, shape, f32)

    def gain(k, shape):
        return 1.0 + 0.1 * jax.random.normal(k, shape, f32)

    return {
        'x': jax.random.normal(ks[0], (BATCH, SEQ, D), f32),
        'ffn1_norm': gain(ks[1], (L, D)),
        'ffn1_w_in': nrm(ks[2], (L, D, 2 * D_FF), D ** -0.5),
        'ffn1_w_out': nrm(ks[3], (L, D_FF, D), D_FF ** -0.5),
        'mix_norm': gain(ks[4], (L, D)),
        'w_in': nrm(ks[5], (L, D, IN_COLS), D ** -0.5),
        'b_gate': nrm(ks[6], (L, N_BRANCHES * D), 0.1),
        'rwkv_mu': jax.random.uniform(ks[7], (L, RWKV_COLS), f32),
        'rwkv_w0': jax.random.uniform(ks[8], (L, RWKV_WIDTH), f32, -6.0, 0.0),
        'rwkv_w2': nrm(ks[9], (L, DECAY_LORA, RWKV_WIDTH), 0.1 * DECAY_LORA ** -0.5),
        'rwkv_a0': nrm(ks[10], (L, RWKV_WIDTH), 0.1),
        'rwkv_a2': nrm(ks[11], (L, AAA_LORA, RWKV_WIDTH), AAA_LORA ** -0.5),
        'rwkv_g2': nrm(ks[12], (L, GATE_LORA, RWKV_WIDTH), GATE_LORA ** -0.5),
        'rwkv_k_k': gain(ks[13], (L, RWKV_WIDTH)),
        'rwkv_k_a': gain(ks[14], (L, RWKV_WIDTH)),
        'rwkv_r_k': nrm(ks[15], (L, RWKV_HEADS, HEAD_DIM), 0.1),
        'rwkv_ln_w': gain(ks[16], (L, RWKV_WIDTH)),
        'rwkv_ln_b': nrm(ks[17], (L, RWKV_WIDTH), 0.01),
        'attn_q_norm': gain(ks[18], (L, HEAD_DIM)),
        'attn_k_norm': gain(ks[19], (L, HEAD_DIM)),
        'w_proj_rwkv': nrm(ks[20], (L, RWKV_WIDTH, D), RWKV_WIDTH ** -0.5),
        'w_proj_attn': nrm(ks[21], (L, ATTN_WIDTH, D), ATTN_WIDTH ** -0.5),
        'w_out': nrm(ks[22], (L, D, D), D ** -0.5),
        'ffn2_norm': gain(ks[23], (L, D)),
        'ffn2_w_in': nrm(ks[24], (L, D, 2 * D_FF), D ** -0.5),
        'ffn2_w_out': nrm(ks[25], (L, D_FF, D), D_FF ** -0.5),
    }


def reference(x, ffn1_norm, ffn1_w_in, ffn1_w_out, mix_norm, w_in, b_gate, rwkv_mu,
              rwkv_w0, rwkv_w2, rwkv_a0, rwkv_a2, rwkv_g2, rwkv_k_k, rwkv_k_a, rwkv_r_k,
              rwkv_ln_w, rwkv_ln_b, attn_q_norm, attn_k_norm, w_proj_rwkv, w_proj_attn,
              w_out, ffn2_norm, ffn2_w_in, ffn2_w_out):
    for l in range(DEPTH):
        x = x + 0.5 * swiglu(rms_norm(x, ffn1_norm[l]), ffn1_w_in[l], ffn1_w_out[l])
        h = rms_norm(x, mix_norm[l])
        p_rwkv, p_q, p_k, p_v, p_gate = _split_last(
            h @ w_in[l], (RWKV_COLS, ATTN_WIDTH, ATTN_WIDTH, ATTN_WIDTH, N_BRANCHES * D_MODEL))
        y_a = rwkv7_time_mix(p_rwkv, rwkv_mu[l], rwkv_w0[l], rwkv_w2[l], rwkv_a0[l],
                             rwkv_a2[l], rwkv_g2[l], rwkv_k_k[l], rwkv_k_a[l], rwkv_r_k[l],
                             rwkv_ln_w[l], rwkv_ln_b[l]).astype(x.dtype)
        y_b = dilated_attention(p_q, p_k, p_v, attn_q_norm[l], attn_k_norm[l]).astype(x.dtype)
        g_a, g_b = _split_last(jax.nn.sigmoid(p_gate + b_gate[l]), (D_MODEL, D_MODEL))
        merged = g_a * (y_a @ w_proj_rwkv[l]) + g_b * (y_b @ w_proj_attn[l])
        x = x + merged @ w_out[l]
        x = x + 0.5 * swiglu(rms_norm(x, ffn2_norm[l]), ffn2_w_in[l], ffn2_w_out[l])
    return x
```

```python
import numpy as np
from contextlib import ExitStack
import concourse.bass as bass
import concourse.mybir as mybir
from concourse.bass_utils import run_bass_kernel_spmd
from concourse.alu_op_type import AluOpType as ALU

F32 = mybir.dt.float32
BF16 = mybir.dt.bfloat16
AF = mybir.ActivationFunctionType
AX = mybir.AxisListType

S = 4096
D = 1024
DFF = 2816
NCORES = 8
RMS_EPS = 1e-6

ENGS = ['pe', 'act', 'dve', 'pool', 'sp']
MAXOPS = [0]
SEM_LIM = 30000
DMA_LIM = 1800


class Prog:
    def __init__(self, nc):
        self.nc = nc
        self.ops = []
        self.eng_ops = {e: [] for e in ENGS}
        self.last_w = {}
        self.readers = {}
        self.dma_cnt = {}
        self.barrier = {e: None for e in ENGS}

    def op(self, eng, fn, reads=(), writes=(), dma_key=None):
        mo = MAXOPS[0]
        if mo and len(self.ops) >= mo and fn is not None:
            return None
        if mo and len(self.ops) == mo - 1 and fn is not None:
            print("LAST OP:", eng, fn.__code__.co_firstlineno, reads, writes)
        oid = len(self.ops)
        deps = set()
        dma_deps = {}
        writes = list(writes) + [r for r in reads if (r.startswith('pb') or r.startswith('ps')) and r not in writes]

        def add(o):
            od = self.ops[o]
            if od['dma_key'] is not None:
                k = od['dma_key']
                dma_deps[k] = self.dma_cnt[k]
            else:
                deps.add(o)
        for r in reads:
            if r in self.last_w:
                add(self.last_w[r])
        for w in writes:
            if w in self.last_w:
                add(self.last_w[w])
            for rd in self.readers.get(w, {}).values():
                add(rd)
        if self.barrier[eng] is not None:
            bd, bdma = self.barrier[eng]
            for o in bd:
                deps.add(o)
            for k, v in bdma.items():
                dma_deps[k] = max(dma_deps.get(k, 0), v)
            self.barrier[eng] = None
        cnt = None
        if dma_key is not None:
            self.dma_cnt[dma_key] = self.dma_cnt.get(dma_key, 0) + 1
            cnt = self.dma_cnt[dma_key]
        o = dict(id=oid, eng=eng, fn=fn, deps=deps, dma_deps=dma_deps, dma_key=dma_key,
                 dma_cnt=cnt, idx=len(self.eng_ops[eng]), sig=False)
        self.ops.append(o)
        self.eng_ops[eng].append(o)
        ch = eng if dma_key is None else 'dma:' + dma_key
        for r in reads:
            self.readers.setdefault(r, {})[ch] = oid
        for w in writes:
            self.last_w[w] = oid
            self.readers[w] = {}
        return oid

    def sync_all(self):
        bd = set()
        for e in ENGS:
            for o in reversed(self.eng_ops[e]):
                if o['dma_key'] is None and o['fn'] is not None:
                    bd.add(o['id'])
                    break
        bdma = dict(self.dma_cnt)
        for e in ENGS:
            self.barrier[e] = (set(bd), dict(bdma))

    def finalize_and_emit(self, stack):
        nc = self.nc
        for o in self.ops:
            per = {}
            for d in o['deps']:
                od = self.ops[d]
                if od['eng'] == 'pe' and o['eng'] == 'pe':
                    continue
                e = od['eng']
                if e not in per or self.ops[per[e]]['idx'] < od['idx']:
                    per[e] = d
            o['cdeps'] = per
            for d in per.values():
                self.ops[d]['sig'] = True
        sems = {}

        def get_sem(name):
            return sems[name]
        for e in ENGS:
            c = 0
            for o in self.eng_ops[e]:
                if o['dma_key'] is None and o['sig']:
                    c += 1
                    o['sigval'] = c
        for o in self.ops:
            waits = {}
            for e, d in o['cdeps'].items():
                v = self.ops[d]['sigval']
                key = ('c_%s_%d' % (e, (v - 1) // SEM_LIM))
                val = (v - 1) % SEM_LIM + 1
                waits[key] = max(waits.get(key, 0), val)
            for k, n in o['dma_deps'].items():
                key = ('d_%s_%d' % (k, (n - 1) // DMA_LIM))
                val = 16 * ((n - 1) % DMA_LIM + 1)
                waits[key] = max(waits.get(key, 0), val)
            o['waits'] = waits
        names = set()
        for o in self.ops:
            names.update(o['waits'].keys())
            if o['dma_key'] is not None:
                names.add('d_%s_%d' % (o['dma_key'], (o['dma_cnt'] - 1) // DMA_LIM))
            elif o['sig']:
                names.add('c_%s_%d' % (o['eng'], (o['sigval'] - 1) // SEM_LIM))
        for nm in sorted(names):
            sems[nm] = stack.enter_context(nc.semaphore(nm))
        print("n_sems", len(names), "n_ops", len(self.ops), {e: len(v) for e, v in self.eng_ops.items()})
        block = stack.enter_context(nc.Block())
        decos = {'pe': block.tensor, 'act': block.scalar, 'dve': block.vector,
                 'pool': block.gpsimd, 'sp': block.sync}
        for e in ENGS:
            ops = self.eng_ops[e]

            def body(eng, ops=ops, e=e):
                waited = {}
                for o in ops:
                    for key, val in o['waits'].items():
                        if waited.get(key, 0) >= val:
                            continue
                        waited[key] = val
                        eng.wait_ge(get_sem(key), val)
                    if o['fn'] is None:
                        continue
                    ins = o['fn'](eng)
                    if o['dma_key'] is not None:
                        n = o['dma_cnt']
                        ins.then_inc(get_sem('d_%s_%d' % (o['dma_key'], (n - 1) // DMA_LIM)), 16)
                    elif o['sig']:
                        v = o['sigval']
                        ins.then_inc(get_sem('c_%s_%d' % (e, (v - 1) // SEM_LIM)), 1)
            decos[e](body)


class Arena:
    def __init__(self, tensor, nbytes):
        self.t = tensor
        self.nbytes = nbytes
        self.off = 0

    def alloc(self, shape, dtype, parts=128):
        n = int(np.prod(shape))
        esz = 4 if dtype == F32 else 2
        nb = n * esz
        nb_al = (nb + 63) // 64 * 64
        assert self.off + nb_al <= self.nbytes, ("SBUF arena overflow", self.off, nb_al)
        ap = self.t[0:parts, self.off // 2:(self.off + nb) // 2]
        self.off += nb_al
        if dtype == F32:
            ap = ap.bitcast(F32)
        if len(shape) == 2:
            ap = ap.rearrange("p (a b) -> p a b", a=shape[0], b=shape[1])
        elif len(shape) == 3:
            ap = ap.rearrange("p (a b c) -> p a b c", a=shape[0], b=shape[1], c=shape[2])
        return ap


def build_program(debug=False, NPAIRS_DBG=8, stages=('ffn1', 'proj', 'rwkv', 'attn', 'merge', 'ffn2'), NB_DBG=None,
                  JS_DBG=range(4)):
    nc = bass.Bass("TRN2", target_bir_lowering=False)
    P = Prog(nc)

    def din(name, shape):
        return nc.dram_tensor(name, list(shape), F32, kind="ExternalInput").ap()
    x = din("x", [S, D])
    ffn_norm = [din("ffn1_norm", [D]), din("ffn2_norm", [D])]
    ffn_win = [din("ffn1_w_in", [D, 2 * DFF]), din("ffn2_w_in", [D, 2 * DFF])]
    ffn_wout = [din("ffn1_w_out", [DFF, D]), din("ffn2_w_out", [DFF, D])]
    out = nc.dram_tensor("out", [S, D], F32, kind="ExternalOutput").ap()
    x1_d = nc.dram_tensor("x1_scr", [S, D], F32, kind="ExternalOutput" if debug else "Internal").ap()

    stack = ExitStack()
    ARENA_BYTES = 200 * 1024
    arena_t = stack.enter_context(nc.sbuf_tensor("arena", [128, ARENA_BYTES // 2], BF16))
    A = Arena(arena_t, ARENA_BYTES)
    psum = stack.enter_context(nc.psum_tensor("psum", [128, 4096], F32))

    def bank(b, n=512, off=0):
        return psum[:, b * 512 + off:b * 512 + off + n]

    ident_f = A.alloc([128], F32)
    ident = A.alloc([128], BF16)
    ones_col = A.alloc([1], F32)

    P.op('pool', lambda e: e.memset(ident_f, 0.0), writes=['ident_f'])
    P.op('pool', lambda e: e.affine_select(out=ident_f, in_=ident_f, pattern=[[-1, 128]],
                                           compare_op=ALU.not_equal, fill=1.0, base=0, channel_multiplier=1),
         reads=['ident_f'], writes=['ident_f'])
    P.op('dve', lambda e: e.tensor_copy(out=ident, in_=ident_f), reads=['ident_f'], writes=['ident'])

    const_mark = A.off

    TT = 256
    NSUB = TT // 128
    NT = S // TT
    KC = D // 128
    FC = DFF // 128

    def ffn_stage(si, src, dst):
        A.off = const_mark
        W1 = A.alloc([KC, 2 * DFF], BF16)
        W2 = A.alloc([FC, D], BF16)
        gb = A.alloc([D], F32)
        xt = [A.alloc([NSUB, D], F32) for _ in range(2)]
        hb = [A.alloc([D], BF16) for _ in range(2)]
        hT = [A.alloc([KC, TT], BF16) for _ in range(2)]
        actT = A.alloc([FC, TT], BF16)
        sg = [A.alloc([TT], F32) for _ in range(2)]
        junk = A.alloc([D], BF16)
        ss = A.alloc([8], F32)
        pre = 's%d_' % si
        w1v = ffn_win[si].rearrange("(kc p) f -> p kc f", p=128)
        CH = 1408
        for kc in range(KC):
            for c in range(2 * DFF // CH):
                P.op('pool', lambda e, kc=kc, c=c: e.dma_start(out=W1[:, kc, c * CH:(c + 1) * CH],
                                                              in_=w1v[:, kc, c * CH:(c + 1) * CH]),
                     writes=[pre + 'W1'], dma_key=pre + 'W1')
        w2v = ffn_wout[si].rearrange("(fc p) d -> p fc d", p=128)
        for fc in range(FC):
            P.op('pool', lambda e, fc=fc: e.dma_start(out=W2[:, fc, :], in_=w2v[:, fc, :]),
                 writes=[pre + 'W2'], dma_key=pre + 'W2')
        P.op('sp', lambda e: e.dma_start(out=gb, in_=ffn_norm[si].partition_broadcast(128)),
             writes=[pre + 'gb'], dma_key=pre + 'gb')
        srcv = src.rearrange("(n s p) d -> n p s d", p=128, s=NSUB)
        dstv = dst.rearrange("(n s p) d -> n p s d", p=128, s=NSUB)
        for it in range(NT):
            sl = it % 2
            X = xt[sl]
            xr = pre + 'xt%d' % sl
            P.op('sp', lambda e, it=it, X=X: e.dma_start(out=X, in_=srcv[it]),
                 writes=[xr], dma_key=xr)
            HT = hT[sl]
            for s in range(NSUB):
                hs = (it * NSUB + s) % 2
                H = hb[hs]
                hr = pre + 'hb%d' % hs
                P.op('act', lambda e, X=X, s=s: e.activation(out=junk, in_=X[:, s, :], func=AF.Square,
                                                             accum_out=ss[:, 0:1]),
                     reads=[xr], writes=[pre + 'junk', pre + 'ss'])
                P.op('act', lambda e: e.activation(out=ss[:, 1:2], in_=ss[:, 0:1], func=AF.Sqrt,
                                                   scale=1.0 / D, bias=RMS_EPS),
                     reads=[pre + 'ss'], writes=[pre + 'ss1'])
                P.op('dve', lambda e: e.reciprocal(out=ss[:, 2:3], in_=ss[:, 1:2]),
                     reads=[pre + 'ss1'], writes=[pre + 'ss2'])
                P.op('dve', lambda e, X=X, s=s, H=H: e.scalar_tensor_tensor(
                    out=H, in0=X[:, s, :], scalar=ss[:, 2:3], in1=gb, op0=ALU.mult, op1=ALU.mult),
                    reads=[xr, pre + 'ss2', pre + 'gb'], writes=[hr])
                pT = bank(0).bitcast(BF16)
                for kc in range(KC):
                    P.op('pe', lambda e, kc=kc, H=H, pT=pT: e.transpose(
                        out=pT[:, kc * 128:(kc + 1) * 128], in_=H[:, kc * 128:(kc + 1) * 128], identity=ident),
                        reads=[hr, 'ident'], writes=['psT'])
                P.op('act', lambda e, HT=HT, s=s, pT=pT: e.copy(
                    out=HT[:, :, s * 128:(s + 1) * 128], in_=pT.rearrange("p (k t) -> p k t", k=KC)),
                    reads=['psT'], writes=[pre + 'hT%d' % sl])
            for fc in range(FC):
                b = 1 + fc % 2
                pg = bank(b)
                for half in range(2):
                    col = half * DFF + fc * 128
                    for kc in range(KC):
                        P.op('pe', lambda e, kc=kc, col=col, half=half, pg=pg, HT=HT: e.matmul(
                            pg[:, half * TT:(half + 1) * TT], lhsT=W1[:, kc, col:col + 128], rhs=HT[:, kc, :],
                            start=(kc == 0), stop=(kc == KC - 1)),
                            reads=[pre + 'W1', pre + 'hT%d' % sl], writes=['psG%d' % b])
                SG = sg[fc % 2]
                P.op('act', lambda e, pg=pg, SG=SG: e.activation(out=SG, in_=pg[:, 0:TT], func=AF.Silu),
                     reads=['psG%d' % b], writes=[pre + 'sg%d' % (fc % 2)])
                P.op('dve', lambda e, pg=pg, SG=SG, fc=fc: e.tensor_tensor(
                    out=actT[:, fc, :], in0=SG, in1=pg[:, TT:2 * TT], op=ALU.mult),
                    reads=['psG%d' % b, pre + 'sg%d' % (fc % 2)], writes=[pre + 'actT'])
            for s in range(NSUB):
                for dh in range(2):
                    b = 3 + (s * 2 + dh) % 2
                    pd = bank(b)
                    for fc in range(FC):
                        P.op('pe', lambda e, fc=fc, s=s, dh=dh, pd=pd: e.matmul(
                            pd, lhsT=actT[:, fc, s * 128:(s + 1) * 128], rhs=W2[:, fc, dh * 512:(dh + 1) * 512],
                            start=(fc == 0), stop=(fc == FC - 1)),
                            reads=[pre + 'actT', pre + 'W2'], writes=['psD%d' % b])
                    P.op('dve', lambda e, X=X, s=s, dh=dh, pd=pd: e.scalar_tensor_tensor(
                        out=X[:, s, dh * 512:(dh + 1) * 512], in0=pd, scalar=0.5,
                        in1=X[:, s, dh * 512:(dh + 1) * 512], op0=ALU.mult, op1=ALU.add),
                        reads=['psD%d' % b, xr], writes=[xr])
            P.op('sp', lambda e, it=it, X=X: e.dma_start(out=dstv[it], in_=X),
                 reads=[xr], writes=[pre + 'dst'], dma_key=pre + 'st%d' % sl)

    NCOL = 7712
    w_in = din("w_in", [D, NCOL])
    mix_norm = din("mix_norm", [D])
    rwkv_mu = din("rwkv_mu", [3360])
    b_gate = din("b_gate", [2048])
    qn = din("attn_q_norm", [64])
    kn = din("attn_k_norm", [64])
    kscr = "ExternalOutput" if debug else "Internal"
    if 'proj' not in stages:
        kscr = "ExternalInput"
    rkv_d = nc.dram_tensor("rkv_scr", [24, 128, S], F32, kind=kscr).ap()
    ta_d = nc.dram_tensor("ta_scr", [128, S], BF16, kind=kscr).ap()
    tg_d = nc.dram_tensor("tg_scr", [160, S], BF16, kind=kscr).ap()
    qk_d = nc.dram_tensor("qk_scr", [12, 128, S], BF16, kind=kscr).ap()
    v_d = nc.dram_tensor("v_scr", [S, 768], BF16, kind=kscr).ap()
    gate_d = nc.dram_tensor("gate_scr", [16, 128, S], BF16, kind=kscr).ap()

    def proj_stage():
        A.off = const_mark
        pre = 'p_'
        W = A.alloc([KC, NCOL], BF16)
        gb = A.alloc([D], F32)
        X = A.alloc([NSUB, D], F32)
        hb = [A.alloc([D], BF16) for _ in range(2)]
        hT = [A.alloc([KC, TT], BF16) for _ in range(2)]
        ss = A.alloc([8], F32)
        mu_t = A.alloc([27], F32)
        bg_t = A.alloc([16], F32)
        qg_t = A.alloc([2], F32)
        carry = A.alloc([27], F32)
        psb = [A.alloc([TT + 1], F32) for _ in range(2)]
        tmp = [A.alloc([TT], F32) for _ in range(2)]
        sq = [A.alloc([TT], BF16) for _ in range(2)]
        lnb = [A.alloc([TT], F32) for _ in range(2)]
        rkv_st = A.alloc([24, TT], F32)
        ta_st = A.alloc([TT], BF16)
        tg_st = A.alloc([2, TT], BF16)
        qk_st = A.alloc([12, TT], BF16)
        gate_st = A.alloc([16, TT], BF16)
        v_st = A.alloc([NSUB, 768], BF16)
        bones = A.alloc([128], BF16)
        print("proj_stage arena", A.off)
        wv = w_in.rearrange("(kc p) f -> p kc f", p=128)
        CH = 964
        for kc in range(KC):
            for c in range(NCOL // CH):
                P.op('pool', lambda e, kc=kc, c=c: e.dma_start(out=W[:, kc, c * CH:(c + 1) * CH],
                                                              in_=wv[:, kc, c * CH:(c + 1) * CH]),
                     writes=[pre + 'W'], dma_key=pre + 'W')
        P.op('sp', lambda e: e.dma_start(out=gb, in_=mix_norm.partition_broadcast(128)),
             writes=[pre + 'gb'], dma_key=pre + 'par')
        P.op('sp', lambda e: e.dma_start(out=mu_t[:, 0:26], in_=rwkv_mu[0:3328].rearrange("(b p) -> p b", p=128),
                                         allow_slow_non_contiguous=True), writes=[pre + 'mu'], dma_key=pre + 'par')
        P.op('sp', lambda e: e.dma_start(out=mu_t[0:32, 26:27], in_=rwkv_mu[3328:3360].rearrange("(p o) -> p o", o=1)),
             writes=[pre + 'mu'], dma_key=pre + 'par')
        P.op('sp', lambda e: e.dma_start(out=bg_t, in_=b_gate.rearrange("(b p) -> p b", p=128),
                                         allow_slow_non_contiguous=True), writes=[pre + 'bg'], dma_key=pre + 'par')
        for hh in range(2):
            P.op('sp', lambda e, hh=hh: e.dma_start(out=qg_t[hh * 64:(hh + 1) * 64, 0:1],
                                                     in_=qn.rearrange("(p o) -> p o", o=1)),
                 writes=[pre + 'qg'], dma_key=pre + 'par')
            P.op('sp', lambda e, hh=hh: e.dma_start(out=qg_t[hh * 64:(hh + 1) * 64, 1:2],
                                                     in_=kn.rearrange("(p o) -> p o", o=1)),
                 writes=[pre + 'qg'], dma_key=pre + 'par')
        P.op('pool', lambda e: e.tensor_scalar(out=qg_t[:, 0:1], in0=qg_t[:, 0:1], scalar1=0.125, scalar2=None,
                                               op0=ALU.mult), reads=[pre + 'qg'], writes=[pre + 'qg'])
        P.op('pool', lambda e: e.memset(carry, 0.0), writes=[pre + 'carry%d' % i for i in range(27)])
        P.op('pool', lambda e: e.memset(bones, 0.0), writes=[pre + 'bones'])
        P.op('pool', lambda e: e.memset(bones[0:64, 0:64], 1.0), reads=[pre + 'bones'], writes=[pre + 'bones'])
        P.op('pool', lambda e: e.memset(bones[64:128, 64:128], 1.0), reads=[pre + 'bones'], writes=[pre + 'bones'])

        blocks = []
        for b in range(24):
            blocks.append((b * 128, 128, 'rkv', b))
        blocks.append((3072, 128, 'ta', 24))
        blocks.append((3200, 128, 'tg0', 25))
        blocks.append((3328, 32, 'tg1', 26))
        for b in range(6):
            blocks.append((3360 + b * 128, 128, 'q', b))
        for b in range(6):
            blocks.append((4128 + b * 128, 128, 'k', 6 + b))
        for b in range(16):
            blocks.append((5664 + b * 128, 128, 'gate', b))

        srcv = x1_d.rearrange("(n s p) d -> n p s d", p=128, s=NSUB)
        xr = pre + 'X'
        for it in range(NT):
            t0 = it * TT
            sl = it % 2
            P.op('sp', lambda e, it=it: e.dma_start(out=X, in_=srcv[it]), writes=[xr], dma_key=xr)
            HT = hT[sl]
            htr = pre + 'hT%d' % sl
            for s in range(NSUB):
                hs = (it * NSUB + s) % 2
                H = hb[hs]
                hr = pre + 'hb%d' % hs
                P.op('act', lambda e, s=s, H=H: e.activation(out=H, in_=X[:, s, :], func=AF.Square,
                                                             accum_out=ss[:, 0:1]),
                     reads=[xr], writes=[hr, pre + 'ss'])
                P.op('act', lambda e: e.activation(out=ss[:, 1:2], in_=ss[:, 0:1], func=AF.Sqrt,
                                                   scale=1.0 / D, bias=RMS_EPS),
                     reads=[pre + 'ss'], writes=[pre + 'ss1'])
                P.op('dve', lambda e: e.reciprocal(out=ss[:, 2:3], in_=ss[:, 1:2]),
                     reads=[pre + 'ss1'], writes=[pre + 'ss2'])
                P.op('dve', lambda e, s=s, H=H: e.scalar_tensor_tensor(
                    out=H, in0=X[:, s, :], scalar=ss[:, 2:3], in1=gb, op0=ALU.mult, op1=ALU.mult),
                    reads=[xr, pre + 'ss2', pre + 'gb'], writes=[hr])
                pT = bank(0).bitcast(BF16)
                for kc in range(KC):
                    P.op('pe', lambda e, kc=kc, H=H, pT=pT: e.transpose(
                        out=pT[:, kc * 128:(kc + 1) * 128], in_=H[:, kc * 128:(kc + 1) * 128], identity=ident),
                        reads=[hr, 'ident'], writes=['psT'])
                P.op('act', lambda e, HT=HT, s=s, pT=pT: e.copy(
                    out=HT[:, :, s * 128:(s + 1) * 128], in_=pT.rearrange("p (k t) -> p k t", k=KC)),
                    reads=['psT'], writes=[htr])

            pending = [None]

            def flush():
                if pending[0] is None:
                    return
                pg, pgr, j, kind, idx = pending[0]
                pending[0] = None
                pss = bank(5)[:, 0:TT]
                P.op('pe', lambda e, j=j, pss=pss: e.matmul(pss, lhsT=bones, rhs=sq[j], start=True, stop=True),
                     reads=[pre + 'sq%d' % j, pre + 'bones'], writes=['pss'])
                P.op('act', lambda e, j=j, pss=pss: e.activation(out=lnb[j], in_=pss, func=AF.Ln,
                                                                 scale=1.0 / 64, bias=RMS_EPS),
                     reads=['pss'], writes=[pre + 'lnb%d' % j])
                P.op('act', lambda e, j=j: e.activation(out=lnb[j], in_=lnb[j], func=AF.Exp, scale=-0.5),
                     reads=[pre + 'lnb%d' % j], writes=[pre + 'lnb%d' % j])
                c = 0 if kind == 'q' else 1
                P.op('dve', lambda e, j=j, pg=pg, idx=idx, c=c: e.scalar_tensor_tensor(
                    out=qk_st[:, idx, :], in0=pg, scalar=qg_t[:, c:c + 1], in1=lnb[j], op0=ALU.mult, op1=ALU.mult),
                    reads=[pgr, pre + 'lnb%d' % j, pre + 'qg'], writes=[pre + 'qk_st'])

            for bi, (col0, M, kind, idx) in enumerate(blocks):
                b = 1 + bi % 4
                pgr = 'psG%d' % b
                pg = bank(b)[0:M, 0:TT]
                j = bi % 2
                for kc in range(KC):
                    P.op('pe', lambda e, kc=kc, col0=col0, M=M, pg=pg, HT=HT: e.matmul(
                        pg, lhsT=W[:, kc, col0:col0 + M], rhs=HT[:, kc, :], start=(kc == 0), stop=(kc == KC - 1)),
                        reads=[pre + 'W', htr], writes=[pgr])
                flush()
                if kind in ('rkv', 'ta', 'tg0', 'tg1'):
                    cr = pre + 'carry%d' % idx
                    pbr = pre + 'psb%d' % j
                    tr = pre + 'tmp%d' % j
                    PS = psb[j][0:M]
                    TM = tmp[j][0:M]
                    P.op('pool', lambda e, PS=PS, idx=idx, M=M: e.tensor_copy(out=PS[:, 0:1], in_=carry[0:M, idx:idx + 1]),
                         reads=[cr], writes=[pbr])
                    P.op('act', lambda e, PS=PS, pg=pg: e.copy(out=PS[:, 1:TT + 1], in_=pg),
                         reads=[pgr, pbr], writes=[pbr])
                    P.op('dve', lambda e, PS=PS, TM=TM: e.tensor_tensor(out=TM, in0=PS[:, 0:TT], in1=PS[:, 1:TT + 1],
                                                                        op=ALU.subtract),
                         reads=[pbr], writes=[tr])
                    P.op('pool', lambda e, PS=PS, idx=idx, M=M: e.tensor_copy(out=carry[0:M, idx:idx + 1],
                                                                              in_=PS[:, TT:TT + 1]),
                         reads=[pbr], writes=[cr])
                    if kind == 'rkv':
                        P.op('dve', lambda e, PS=PS, TM=TM, idx=idx: e.scalar_tensor_tensor(
                            out=rkv_st[:, idx, :], in0=TM, scalar=mu_t[:, idx:idx + 1], in1=PS[:, 1:TT + 1],
                            op0=ALU.mult, op1=ALU.add),
                            reads=[tr, pbr, pre + 'mu'], writes=[pre + 'rkv_st%d' % (idx // 8)])
                    else:
                        P.op('dve', lambda e, PS=PS, TM=TM, idx=idx, M=M: e.scalar_tensor_tensor(
                            out=TM, in0=TM, scalar=mu_t[0:M, idx:idx + 1], in1=PS[:, 1:TT + 1],
                            op0=ALU.mult, op1=ALU.add),
                            reads=[tr, pbr, pre + 'mu'], writes=[tr])
                        if kind == 'ta':
                            P.op('act', lambda e, TM=TM: e.activation(out=ta_st[0:64], in_=TM[0:64], func=AF.Tanh),
                                 reads=[tr], writes=[pre + 'ta_st'])
                            P.op('act', lambda e, TM=TM: e.copy(out=ta_st[64:128], in_=TM[64:128]),
                                 reads=[tr], writes=[pre + 'ta_st'])
                        elif kind == 'tg0':
                            P.op('act', lambda e, TM=TM: e.activation(out=tg_st[:, 0, :], in_=TM, func=AF.Sigmoid),
                                 reads=[tr], writes=[pre + 'tg_st'])
                        else:
                            P.op('act', lambda e, TM=TM: e.activation(out=tg_st[0:32, 1, :], in_=TM, func=AF.Sigmoid),
                                 reads=[tr], writes=[pre + 'tg_st'])
                elif kind in ('q', 'k'):
                    P.op('act', lambda e, pg=pg, j=j: e.activation(out=sq[j], in_=pg, func=AF.Square),
                         reads=[pgr], writes=[pre + 'sq%d' % j])
                    pending[0] = (pg, pgr, j, kind, idx)
                else:
                    P.op('act', lambda e, pg=pg, idx=idx: e.activation(out=gate_st[:, idx, :], in_=pg, func=AF.Sigmoid,
                                                                       bias=bg_t[:, idx:idx + 1]),
                         reads=[pgr, pre + 'bg'], writes=[pre + 'gate_st'])
            flush()
            for s in range(NSUB):
                for (c0, n, b) in ((4896, 512, 6), (5408, 256, 7)):
                    pv = bank(b)[:, 0:n]
                    for kc in range(KC):
                        P.op('pe', lambda e, kc=kc, s=s, c0=c0, n=n, pv=pv, HT=HT: e.matmul(
                            pv, lhsT=HT[:, kc, s * 128:(s + 1) * 128], rhs=W[:, kc, c0:c0 + n],
                            start=(kc == 0), stop=(kc == KC - 1)),
                            reads=[pre + 'W', htr], writes=['psV%d' % b])
                P.op('act', lambda e, s=s: e.copy(out=v_st[:, s, 0:512], in_=bank(6)),
                     reads=['psV6'], writes=[pre + 'v_st'])
                P.op('dve', lambda e, s=s: e.tensor_copy(out=v_st[:, s, 512:768], in_=bank(7)[:, 0:256]),
                     reads=['psV7'], writes=[pre + 'v_st'])
            rv = rkv_d.rearrange("b p t -> p b t")
            for g in range(3):
                P.op('sp', lambda e, g=g, t0=t0: e.dma_start(out=rv[:, g * 8:(g + 1) * 8, t0:t0 + TT],
                                                             in_=rkv_st[:, g * 8:(g + 1) * 8, :]),
                     reads=[pre + 'rkv_st%d' % g], writes=['d_rkv'], dma_key=pre + 'rkv_st%d' % g)
            P.op('sp', lambda e, t0=t0: e.dma_start(out=ta_d[:, t0:t0 + TT], in_=ta_st),
                 reads=[pre + 'ta_st'], writes=['d_ta'], dma_key=pre + 'ta_st')
            P.op('sp', lambda e, t0=t0: e.dma_start(out=tg_d[0:128, t0:t0 + TT], in_=tg_st[:, 0, :]),
                 reads=[pre + 'tg_st'], writes=['d_tg'], dma_key=pre + 'tg_st')
            P.op('sp', lambda e, t0=t0: e.dma_start(out=tg_d[128:160, t0:t0 + TT], in_=tg_st[0:32, 1, :]),
                 reads=[pre + 'tg_st'], writes=['d_tg'], dma_key=pre + 'tg_st')
            P.op('sp', lambda e, t0=t0: e.dma_start(out=qk_d.rearrange("b p t -> p b t")[:, :, t0:t0 + TT], in_=qk_st),
                 reads=[pre + 'qk_st'], writes=['d_qk'], dma_key=pre + 'qk_st')
            P.op('sp', lambda e, t0=t0: e.dma_start(out=gate_d.rearrange("b p t -> p b t")[:, :, t0:t0 + TT],
                                                    in_=gate_st),
                 reads=[pre + 'gate_st'], writes=['d_gate'], dma_key=pre + 'gate_st')
            P.op('sp', lambda e, t0=t0: e.dma_start(
                out=v_d[t0:t0 + TT, :].rearrange("(s p) c -> p s c", p=128), in_=v_st),
                reads=[pre + 'v_st'], writes=['d_v'], dma_key=pre + 'v_st')

    w2_d = din("rwkv_w2", [64, 1024])
    a2_d = din("rwkv_a2", [64, 1024])
    g2_d = din("rwkv_g2", [160, 1024])
    prm_names = ['rwkv_w0', 'rwkv_a0', 'rwkv_k_k', 'rwkv_k_a', 'rwkv_r_k', 'rwkv_ln_w', 'rwkv_ln_b']
    prm_d = [din(n, [1024]) for n in prm_names]
    ya_d = nc.dram_tensor("ya_scr", [8, 128, S], BF16, kind="ExternalOutput" if debug else "Internal").ap()
    TB = 1024
    NCH = TB // 128
    NB = S // TB if NB_DBG is None else NB_DBG
    C0 = float(np.exp(-0.5))
    GN_EPS = 64e-5

    def rwkv_stage(pairs=range(8)):
        A.off = const_mark
        pre = 'r_'
        WA = A.alloc([1024], BF16)
        G2a = A.alloc([1024], BF16)
        G2b = A.alloc([1024], BF16)
        prm = A.alloc([7, 8], F32)
        bones = A.alloc([128], BF16)
        mk4 = A.alloc([512], F32)
        mkL = A.alloc([2, 128], F32)
        E2 = A.alloc([64], F32)
        mrow = A.alloc([TB], F32)
        f32names = ['R', 'K', 'V', 'SG', 'AA', 'GG', 'KK', 'KM', 'T1', 'BVEC', 'CS', 'T2', 'T3', 'EP', 'EN', 'EPM',
                    'EC', 'BV', 'Y32', 'DD']
        T = {n: A.alloc([TB], F32) for n in f32names}
        bfnames = ['TA', 'TG0', 'TG1', 'TQ', 'BT', 'KT', 'BH', 'KH', 'VT', 'YB', 'YO']
        for n in bfnames:
            T[n] = A.alloc([TB], BF16)
        AR = A.alloc([NCH, 2, 128], BF16)
        TM4 = A.alloc([NCH, 4, 128], BF16)
        PC = A.alloc([NCH], F32)
        SC = [A.alloc([512], BF16) for _ in range(2)]
        LZ = [A.alloc([2, 256], BF16) for _ in range(2)]
        LT = [A.alloc([2, 128], BF16) for _ in range(2)]
        MCz = A.alloc([2, 64], BF16)
        QT = A.alloc([128], BF16)
        STz = A.alloc([2, 64], BF16)
        print("rwkv_stage arena", A.off)

        def R_(n):
            return pre + n
        P.op('pool', lambda e: e.dma_start(out=WA[0:64, :], in_=w2_d), writes=[R_('WA')], dma_key=R_('w'))
        P.op('pool', lambda e: e.dma_start(out=WA[64:128, :], in_=a2_d), writes=[R_('WA')], dma_key=R_('w'))
        P.op('pool', lambda e: e.dma_start(out=G2a, in_=g2_d[0:128, :]), writes=[R_('G2')], dma_key=R_('w'))
        P.op('pool', lambda e: e.dma_start(out=G2b[0:32, :], in_=g2_d[128:160, :]), writes=[R_('G2')], dma_key=R_('w'))
        for i in range(7):
            P.op('sp', lambda e, i=i: e.dma_start(out=prm[:, i, :], in_=prm_d[i].rearrange("(b p) -> p b", p=128),
                                                   allow_slow_non_contiguous=True),
                 writes=[R_('prm')], dma_key=R_('par'))
        P.op('pool', lambda e: e.memset(bones, 0.0), writes=[R_('bones')])
        P.op('pool', lambda e: e.memset(bones[0:64, 0:64], 1.0), reads=[R_('bones')], writes=[R_('bones')])
        P.op('pool', lambda e: e.memset(bones[64:128, 64:128], 1.0), reads=[R_('bones')], writes=[R_('bones')])
        P.op('pool', lambda e: e.memset(mk4, 1.0), writes=[R_('mk4')])
        for q in range(4):
            base = -1 if q % 2 == 0 else 0
            P.op('pool', lambda e, q=q, base=base: e.affine_select(
                out=mk4[:, q * 128:(q + 1) * 128], in_=mk4[:, q * 128:(q + 1) * 128], pattern=[[1, 128]],
                compare_op=ALU.is_ge, fill=0.0, base=base, channel_multiplier=-1),
                reads=[R_('mk4')], writes=[R_('mk4')])
        P.op('pool', lambda e: e.memset(mkL, 1.0), writes=[R_('mkL')])
        P.op('pool', lambda e: e.affine_select(out=mkL, in_=mkL, pattern=[[0, 2], [-1, 128]], compare_op=ALU.is_ge,
                                               fill=0.0, base=-1, channel_multiplier=1),
             reads=[R_('mkL')], writes=[R_('mkL')])
        P.op('pool', lambda e: e.tensor_copy(out=E2[0:64, :], in_=ident_f[0:64, 0:64]), reads=['ident_f'], writes=[R_('E2')])
        P.op('pool', lambda e: e.tensor_copy(out=E2[64:128, :], in_=ident_f[64:128, 64:128]), reads=['ident_f'],
             writes=[R_('E2')])
        P.op('pool', lambda e: e.memset(mrow, 1.0), writes=[R_('mrow')])
        P.op('pool', lambda e: e.memset(mrow.rearrange("p (c t) -> p c t", t=128)[:, :, 0:1], 0.0),
             reads=[R_('mrow')], writes=[R_('mrow')])

        def ch3(ap):
            return ap.rearrange("p (c t) -> p c t", t=128)

        def nm(x):
            if x.startswith('pb') or x in ('ident', 'ident_f'):
                return x
            return R_(x)

        def ew(eng, fn, reads, writes):
            P.op(eng, fn, reads=[nm(x) for x in reads], writes=[nm(x) for x in writes])

        for hp in pairs:
            cols = slice(hp * 128, (hp + 1) * 128)
            ew('pool', lambda e: e.memset(STz, 0.0), [], ['ST'])
            ew('pool', lambda e: e.memset(MCz, 0.0), [], ['MC'])
            for tb in range(NB):
                t0 = tb * TB
                tsl = slice(t0, t0 + TB)
                for i, n in enumerate(['R', 'K', 'V']):
                    P.op('sp', lambda e, i=i, n=n, hp=hp, tsl=tsl: e.dma_start(out=T[n], in_=rkv_d[i * 8 + hp, :, tsl]),
                         writes=[R_(n)], dma_key=R_('ld' + n))
                P.op('sp', lambda e, tsl=tsl: e.dma_start(out=T['TA'], in_=ta_d[:, tsl]), writes=[R_('TA')],
                     dma_key=R_('ldTA'))
                P.op('sp', lambda e, tsl=tsl: e.dma_start(out=T['TG0'], in_=tg_d[0:128, tsl]), writes=[R_('TG0')],
                     dma_key=R_('ldTG0'))
                P.op('sp', lambda e, tsl=tsl: e.dma_start(out=T['TG1'][0:32], in_=tg_d[128:160, tsl]),
                     writes=[R_('TG1')], dma_key=R_('ldTG1'))
                for hf in range(2):
                    hs = slice(hf * 512, (hf + 1) * 512)
                    ew('pe', lambda e, hs=hs, cols=cols: e.matmul(bank(1), lhsT=WA[0:64, cols], rhs=T['TA'][0:64, hs],
                                                                  start=True, stop=True), ['WA', 'TA'], ['pb1'])
                    ew('act', lambda e, hs=hs, hp=hp: e.activation(out=T['SG'][:, hs], in_=bank(1), func=AF.Sigmoid,
                                                                   bias=prm[:, 0, hp:hp + 1]), ['pb1', 'prm'], ['SG'])
                    ew('pe', lambda e, hs=hs, cols=cols: e.matmul(bank(2), lhsT=WA[64:128, cols], rhs=T['TA'][64:128, hs],
                                                                  start=True, stop=True), ['WA', 'TA'], ['pb2'])
                    ew('act', lambda e, hs=hs, hp=hp: e.activation(out=T['AA'][:, hs], in_=bank(2), func=AF.Sigmoid,
                                                                   bias=prm[:, 1, hp:hp + 1]), ['pb2', 'prm'], ['AA'])
                    ew('pe', lambda e, hs=hs, cols=cols: e.matmul(bank(3), lhsT=G2a[:, cols], rhs=T['TG0'][:, hs],
                                                                  start=True, stop=False), ['G2', 'TG0'], ['pb3', 'pb3'])
                    ew('pe', lambda e, hs=hs, cols=cols: e.matmul(bank(3), lhsT=G2b[0:32, cols], rhs=T['TG1'][0:32, hs],
                                                                  start=False, stop=True), ['G2', 'TG1'], ['pb3', 'pb3'])
                    ew('act', lambda e, hs=hs: e.copy(out=T['GG'][:, hs], in_=bank(3)), ['pb3', 'pb3'], ['GG'])
                ew('dve', lambda e, hp=hp: e.tensor_scalar(out=T['KK'], in0=T['K'], scalar1=prm[:, 2, hp:hp + 1],
                                                           scalar2=None, op0=ALU.mult), ['K', 'prm'], ['KK'])
                ew('act', lambda e: e.activation(out=T['TQ'], in_=T['KK'], func=AF.Square), ['KK'], ['TQ'])
                for hf in range(2):
                    hs = slice(hf * 512, (hf + 1) * 512)
                    ew('pe', lambda e, hs=hs: e.matmul(bank(4), lhsT=bones, rhs=T['TQ'][:, hs], start=True, stop=True),
                       ['bones', 'TQ'], ['pb4'])
                    ew('dve', lambda e, hs=hs: e.tensor_scalar(out=T['T1'][:, hs], in0=bank(4), scalar1=1e-19,
                                                               scalar2=None, op0=ALU.max), ['pb4'], ['T1'])
                ew('act', lambda e: e.activation(out=T['T1'], in_=T['T1'], func=AF.Ln), ['T1'], ['T1'])
                ew('act', lambda e: e.activation(out=T['T1'], in_=T['T1'], func=AF.Exp, scale=-0.5), ['T1'], ['T1'])
                ew('dve', lambda e: e.tensor_tensor(out=T['KK'], in0=T['KK'], in1=T['T1'], op=ALU.mult),
                   ['KK', 'T1'], ['KK'])
                ew('dve', lambda e, hp=hp: e.tensor_scalar(out=T['T1'], in0=T['AA'], scalar1=-1.0,
                                                           scalar2=prm[:, 3, hp:hp + 1], op0=ALU.add, op1=ALU.mult),
                   ['AA', 'prm'], ['T1'])
                ew('dve', lambda e: e.scalar_tensor_tensor(out=T['KM'], in0=T['T1'], scalar=1.0, in1=T['K'],
                                                           op0=ALU.add, op1=ALU.mult), ['T1', 'K'], ['KM'])
                ew('pool', lambda e: e.tensor_tensor(out=T['T1'], in0=T['R'], in1=T['KM'], op=ALU.mult),
                   ['R', 'KM'], ['T1'])
                ew('pool', lambda e, hp=hp: e.tensor_scalar(out=T['TQ'], in0=T['T1'], scalar1=prm[:, 4, hp:hp + 1],
                                                            scalar2=None, op0=ALU.mult), ['T1', 'prm'], ['TQ'])
                for hf in range(2):
                    hs = slice(hf * 512, (hf + 1) * 512)
                    ew('pe', lambda e, hs=hs: e.matmul(bank(5), lhsT=bones, rhs=T['TQ'][:, hs], start=True, stop=True),
                       ['bones', 'TQ'], ['pb5'])
                    ew('dve', lambda e, hs=hs: e.tensor_tensor(out=T['BV'][:, hs], in0=T['V'][:, hs], in1=bank(5),
                                                               op=ALU.mult), ['pb5', 'V'], ['BV'])
                ew('pool', lambda e: e.tensor_tensor(out=T['BVEC'], in0=T['KK'], in1=T['AA'], op=ALU.mult),
                   ['KK', 'AA'], ['BVEC'])
                ew('dve', lambda e: e.tensor_tensor_scan(out=T['CS'], data0=mrow, data1=T['SG'], initial=0.0,
                                                         op0=ALU.mult, op1=ALU.add), ['mrow', 'SG'], ['CS'])
                ew('pool', lambda e: e.tensor_tensor(out=T['T2'], in0=T['CS'], in1=T['SG'], op=ALU.subtract),
                   ['CS', 'SG'], ['T2'])
                ew('act', lambda e: e.activation(out=T['EP'], in_=T['CS'], func=AF.Exp, scale=-C0), ['CS'], ['EP'])
                ew('act', lambda e: e.activation(out=T['EN'], in_=T['CS'], func=AF.Exp, scale=C0), ['CS'], ['EN'])
                ew('act', lambda e: e.activation(out=T['EPM'], in_=T['T2'], func=AF.Exp, scale=-C0), ['T2'], ['EPM'])
                ew('dve', lambda e: e.tensor_tensor(
                    out=ch3(T['T3']), in0=ch3(T['CS']), in1=ch3(T['CS'])[:, :, 127:128].to_broadcast([128, NCH, 128]),
                    op=ALU.subtract), ['CS'], ['T3'])
                ew('act', lambda e: e.activation(out=T['EC'], in_=T['T3'], func=AF.Exp, scale=C0), ['T3'], ['EC'])
                ew('act', lambda e: e.activation(out=PC.rearrange("p (c o) -> p c o", o=1),
                                                 in_=ch3(T['CS'])[:, :, 127:128], func=AF.Exp, scale=-C0),
                   ['CS'], ['PC'])
                ew('dve', lambda e: e.scalar_tensor_tensor(out=AR[:, :, 0, :], in0=ch3(T['EPM']), scalar=-1.0,
                                                           in1=ch3(T['KK']), op0=ALU.mult, op1=ALU.mult),
                   ['EPM', 'KK'], ['AR0'])
                ew('pool', lambda e: e.tensor_tensor(out=AR[:, :, 1, :], in0=ch3(T['EP']), in1=ch3(T['R']), op=ALU.mult),
                   ['EP', 'R'], ['AR1'])
                ew('dve', lambda e: e.tensor_tensor(out=T['BT'], in0=T['EN'], in1=T['BVEC'], op=ALU.mult),
                   ['EN', 'BVEC'], ['BT'])
                ew('pool', lambda e: e.tensor_tensor(out=T['KT'], in0=T['EN'], in1=T['KM'], op=ALU.mult),
                   ['EN', 'KM'], ['KT'])
                ew('dve', lambda e: e.tensor_tensor(out=T['BH'], in0=T['EC'], in1=T['BVEC'], op=ALU.mult),
                   ['EC', 'BVEC'], ['BH'])
                ew('pool', lambda e: e.tensor_tensor(out=T['KH'], in0=T['EC'], in1=T['KM'], op=ALU.mult),
                   ['EC', 'KM'], ['KH'])
                ew('act', lambda e: e.copy(out=T['VT'], in_=T['V']), ['V'], ['VT'])
                pT = bank(0).bitcast(BF16)
                for c in range(NCH):
                    cs_ = slice(c * 128, (c + 1) * 128)
                    srcs = [(AR[:, c, 0, :], 'AR0'), (T['VT'][:, cs_], 'VT'), (T['BH'][:, cs_], 'BH'),
                            (T['KH'][:, cs_], 'KH')]
                    for q, (sap, sr) in enumerate(srcs):
                        ew('pe', lambda e, q=q, sap=sap: e.transpose(out=pT[:, q * 128:(q + 1) * 128], in_=sap,
                                                                     identity=ident), [sr, 'ident'], ['pb0'])
                    ew('act', lambda e, c=c: e.copy(out=TM4[:, c, :, :], in_=pT[:, 0:512].rearrange("p (q t) -> p q t", q=4)),
                       ['pb0'], ['TM4_%d' % c])
                for c in range(NCH):
                    cs_ = slice(c * 128, (c + 1) * 128)
                    tm = 'TM4_%d' % c
                    for h2 in range(2):
                        pb = 64 * h2
                        psl = slice(pb, pb + 64)
                        ps1 = bank(1 + h2)
                        arc = AR[:, c, :, :].rearrange("p a t -> p (a t)")
                        ew('pe', lambda e, ps1=ps1, psl=psl, cs_=cs_, arc=arc: e.matmul(
                            ps1[:, 0:256], lhsT=T['BT'][psl, cs_], rhs=arc[psl, :], start=True, stop=True),
                            ['BT', 'AR0', 'AR1'], ['pb%d' % (1 + h2)])
                        ew('pe', lambda e, ps1=ps1, psl=psl, cs_=cs_, arc=arc: e.matmul(
                            ps1[:, 256:512], lhsT=T['KT'][psl, cs_], rhs=arc[psl, :], start=True, stop=True),
                            ['KT', 'AR0', 'AR1'], ['pb%d' % (1 + h2)])
                        ew('dve', lambda e, ps1=ps1, h2=h2: e.tensor_tensor(out=SC[h2], in0=mk4, in1=ps1, op=ALU.mult),
                           ['pb%d' % (1 + h2), 'mk4'], ['SC%d' % h2])
                        ew('pe', lambda e, h2=h2, psl=psl, cs_=cs_, c=c: e.matmul(
                            bank(4 + h2)[:, 384:512], lhsT=AR[psl, c, 0, :], rhs=T['BT'][psl, cs_],
                            start=True, stop=True), ['AR0', 'BT'], ['pb45'])
                    for h2 in range(2):
                        ew('dve', lambda e, h2=h2: e.tensor_tensor(out=LZ[0][:, h2, 0:128], in0=mkL[:, h2, :],
                                                                   in1=bank(4 + h2)[:, 384:512], op=ALU.mult),
                           ['pb45', 'mkL'], ['LZ0'])
                    for h2 in range(2):
                        pb = 64 * h2
                        ew('pe', lambda e, h2=h2, pb=pb, c=c: e.matmul(
                            bank(3)[:, 256 + h2 * 64:256 + (h2 + 1) * 64], lhsT=SC[h2][:, 256:384],
                            rhs=TM4[:, c, 1, pb:pb + 64], start=True, stop=True), ['SC%d' % h2, tm], ['pb3'])
                    ew('pool', lambda e, c=c: e.tensor_copy(out=LZ[0][:, :, 128:192],
                                                            in_=TM4[:, c, 0, :].rearrange("p (h k) -> p h k", h=2)),
                       [tm], ['LZ0'])
                    for h2 in range(2):
                        ew('act', lambda e, h2=h2: e.copy(out=LZ[0][:, h2, 192:256],
                                                          in_=bank(3)[:, 256 + h2 * 64:256 + (h2 + 1) * 64]),
                           ['pb3'], ['LZ0'])
                    psv = psum[:, 4 * 512:6 * 512].rearrange("p (h f) -> p h f", h=2)
                    for n in range(7):
                        pp = n % 2
                        for h2 in range(2):
                            ps2 = bank(4 + h2)
                            ltn = SC[h2][:, 0:128] if n == 0 else LT[pp][:, h2, :]
                            ltr = ('SC%d' % h2) if n == 0 else ('LT%d' % pp)
                            if n < 6:
                                ew('pe', lambda e, ps2=ps2, ltn=ltn, pp=pp, h2=h2: e.matmul(
                                    ps2[:, 0:256], lhsT=ltn, rhs=LZ[pp][:, h2, 0:256], start=True, stop=True),
                                    [ltr, 'LZ%d' % pp], ['pb45'])
                                ew('pe', lambda e, ps2=ps2, ltn=ltn, pp=pp, h2=h2: e.matmul(
                                    ps2[:, 256:384], lhsT=LZ[pp][:, h2, 0:128], rhs=ltn, start=True, stop=True),
                                    [ltr, 'LZ%d' % pp], ['pb45'])
                            else:
                                ew('pe', lambda e, ps2=ps2, ltn=ltn, pp=pp, h2=h2: e.matmul(
                                    ps2[:, 128:256], lhsT=ltn, rhs=LZ[pp][:, h2, 128:256], start=True, stop=True),
                                    [ltr, 'LZ%d' % pp], ['pb45'])
                        for h2 in range(2):
                            ps2 = bank(4 + h2)
                            if n < 6:
                                ew('act', lambda e, pp=pp, h2=h2, ps2=ps2: e.copy(out=LZ[1 - pp][:, h2, 0:128],
                                                                                  in_=ps2[:, 0:128]),
                                   ['pb45'], ['LZ%d' % (1 - pp)])
                                ew('act', lambda e, pp=pp, h2=h2, ps2=ps2: e.copy(out=LT[1 - pp][:, h2, :],
                                                                                  in_=ps2[:, 256:384]),
                                   ['pb45'], ['LT%d' % (1 - pp)])
                            ew('dve', lambda e, pp=pp, h2=h2, ps2=ps2: e.tensor_tensor(
                                out=LZ[1 - pp][:, h2, 128:256], in0=LZ[pp][:, h2, 128:256], in1=ps2[:, 128:256],
                                op=ALU.add), ['pb45', 'LZ%d' % pp], ['LZ%d' % (1 - pp)])
                    ZF = LZ[1]
                    for h2 in range(2):
                        pb = 64 * h2
                        psl = slice(pb, pb + 64)
                        ew('pe', lambda e, h2=h2, pb=pb, psl=psl, c=c: e.matmul(
                            bank(6)[psl, 0:64], lhsT=ZF[:, h2, 128:192], rhs=TM4[:, c, 2, pb:pb + 64],
                            start=True, stop=True, tile_position=(0, pb)), ['LZ1', tm], ['pb6'])
                        ew('pe', lambda e, h2=h2, pb=pb, psl=psl: e.matmul(
                            bank(6)[psl, 64:192], lhsT=ZF[:, h2, 128:192], rhs=SC[h2][:, 128:256],
                            start=True, stop=True, tile_position=(0, pb)), ['LZ1', 'SC%d' % h2], ['pb6'])
                    for h2 in range(2):
                        psl = slice(64 * h2, 64 * h2 + 64)
                        ew('dve', lambda e, c=c, psl=psl, h2=h2: e.scalar_tensor_tensor(
                            out=MCz[psl, h2, :], in0=E2[psl, :], scalar=PC[psl, c:c + 1], in1=bank(6)[psl, 0:64],
                            op0=ALU.mult, op1=ALU.add), ['E2', 'PC', 'pb6'], ['MC'])
                    ew('dve', lambda e, c=c: e.tensor_tensor(out=QT, in0=AR[:, c, 1, :], in1=bank(6)[:, 64:192],
                                                             op=ALU.add), ['pb6', 'AR1'], ['QT'])
                    for h2 in range(2):
                        pb = 64 * h2
                        psl = slice(pb, pb + 64)
                        sb_ = 3 if h2 == 0 else 0
                        sr_ = 'pb%d' % sb_
                        psY = bank(sb_)[:, 0:128]
                        psS = bank(sb_)[:, 128:192]
                        UU = ZF[:, h2, 192:256]
                        ew('pe', lambda e, psl=psl, pb=pb, h2=h2, UU=UU, psY=psY: e.matmul(
                            psY[psl, :], lhsT=UU, rhs=SC[h2][:, 128:256], start=True, stop=False,
                            tile_position=(0, pb)), ['LZ1', 'SC%d' % h2], [sr_])
                        ew('pe', lambda e, psl=psl, pb=pb, h2=h2, c=c, psY=psY: e.matmul(
                            psY[psl, :], lhsT=TM4[:, c, 1, pb:pb + 64], rhs=SC[h2][:, 384:512], start=False, stop=False,
                            tile_position=(0, pb)), [tm, 'SC%d' % h2], [sr_])
                        ew('pe', lambda e, psl=psl, pb=pb, psY=psY, h2=h2: e.matmul(
                            psY[psl, :], lhsT=STz[:, h2, :], rhs=QT, start=False, stop=True,
                            tile_position=(0, pb)), ['ST', 'QT'], [sr_])
                        ew('pe', lambda e, psl=psl, pb=pb, psS=psS, h2=h2: e.matmul(
                            psS[psl, :], lhsT=MCz[:, h2, :], rhs=STz[:, h2, :], start=True, stop=False,
                            tile_position=(0, pb)), ['MC', 'ST'], [sr_])
                        ew('pe', lambda e, psl=psl, pb=pb, c=c, UU=UU, psS=psS: e.matmul(
                            psS[psl, :], lhsT=TM4[:, c, 2, pb:pb + 64], rhs=UU, start=False, stop=False,
                            tile_position=(0, pb)), [tm, 'LZ1'], [sr_])
                        ew('pe', lambda e, psl=psl, pb=pb, c=c, psS=psS: e.matmul(
                            psS[psl, :], lhsT=TM4[:, c, 3, pb:pb + 64], rhs=TM4[:, c, 1, pb:pb + 64], start=False,
                            stop=True, tile_position=(0, pb)), [tm], [sr_])
                    for h2 in range(2):
                        pb = 64 * h2
                        psl = slice(pb, pb + 64)
                        sb_ = 3 if h2 == 0 else 0
                        sr_ = 'pb%d' % sb_
                        ew('act', lambda e, cs_=cs_, psl=psl, sb_=sb_: e.copy(out=T['Y32'][psl, cs_],
                                                                             in_=bank(sb_)[psl, 0:128]),
                           [sr_], ['Y32'])
                        ew('dve', lambda e, psl=psl, sb_=sb_, h2=h2: e.tensor_copy(out=STz[psl, h2, :],
                                                                                   in_=bank(sb_)[psl, 128:192]),
                           [sr_], ['ST'])
                ew('pool', lambda e: e.tensor_copy(out=T['YB'], in_=T['Y32']), ['Y32'], ['YB'])
                for hf in range(2):
                    hs = slice(hf * 512, (hf + 1) * 512)
                    ew('pe', lambda e, hs=hs: e.matmul(bank(1), lhsT=bones, rhs=T['YB'][:, hs], start=True, stop=True),
                       ['bones', 'YB'], ['pb1'])
                    ew('dve', lambda e, hs=hs: e.scalar_tensor_tensor(out=T['DD'][:, hs], in0=bank(1), scalar=-1.0 / 64,
                                                                      in1=T['Y32'][:, hs], op0=ALU.mult, op1=ALU.add),
                       ['pb1', 'Y32'], ['DD'])
                ew('act', lambda e: e.activation(out=T['TQ'], in_=T['DD'], func=AF.Square), ['DD'], ['TQ'])
                for hf in range(2):
                    hs = slice(hf * 512, (hf + 1) * 512)
                    ew('pe', lambda e, hs=hs: e.matmul(bank(2), lhsT=bones, rhs=T['TQ'][:, hs], start=True, stop=True),
                       ['bones', 'TQ'], ['pb2'])
                    ew('act', lambda e, hs=hs: e.activation(out=T['T1'][:, hs], in_=bank(2), func=AF.Ln, scale=1.0 / 64,
                                                            bias=GN_EPS), ['pb2'], ['T1'])
                ew('act', lambda e: e.activation(out=T['T1'], in_=T['T1'], func=AF.Exp, scale=-0.5), ['T1'], ['T1'])
                ew('dve', lambda e: e.tensor_tensor(out=T['DD'], in0=T['DD'], in1=T['T1'], op=ALU.mult),
                   ['DD', 'T1'], ['DD'])
                ew('dve', lambda e, hp=hp: e.tensor_scalar(out=T['DD'], in0=T['DD'], scalar1=prm[:, 5, hp:hp + 1],
                                                           scalar2=prm[:, 6, hp:hp + 1], op0=ALU.mult, op1=ALU.add),
                   ['DD', 'prm'], ['DD'])
                ew('pool', lambda e: e.tensor_tensor(out=T['DD'], in0=T['DD'], in1=T['BV'], op=ALU.add),
                   ['DD', 'BV'], ['DD'])
                ew('dve', lambda e: e.tensor_tensor(out=T['YO'], in0=T['DD'], in1=T['GG'], op=ALU.mult),
                   ['DD', 'GG'], ['YO'])
                P.op('sp', lambda e, hp=hp, tsl=tsl: e.dma_start(out=ya_d[hp, :, tsl], in_=T['YO']),
                     reads=[R_('YO')], writes=['d_ya'], dma_key=R_('stYO'))

    yb_d = nc.dram_tensor("yb_scr", [6, 128, S], BF16, kind="ExternalOutput" if debug else "Internal").ap()
    DIL = (1, 4, 16)

    def attn_stage_full(js=range(4)):
        A.off = const_mark
        pre = 'a_'

        def R_(n):
            return pre + n

        def nm(x):
            if x.startswith('pb') or x in ('ident', 'ident_f'):
                return x
            return R_(x)

        def ew(eng, fn, reads, writes):
            P.op(eng, fn, reads=[nm(x) for x in reads], writes=[nm(x) for x in writes])
        QH = A.alloc([S], BF16)
        KH = A.alloc([S], BF16)
        VX = A.alloc([32, 64], BF16)
        ONES = A.alloc([64], BF16)
        OT = [A.alloc([S], F32) for _ in range(3)]
        DEN = [A.alloc([S], F32) for _ in range(3)]
        PT = [A.alloc([256], BF16) for _ in range(4)]
        mask2 = A.alloc([256], BF16)
        RD = [A.alloc([512], F32) for _ in range(2)]
        YBS = [A.alloc([512], BF16) for _ in range(2)]
        print("attn_stage arena", A.off)
        P.op('pool', lambda e: e.memset(mask2, 1.0), writes=[R_('mask')])
        P.op('pool', lambda e: e.affine_select(out=mask2[:, 0:128], in_=mask2[:, 0:128], pattern=[[1, 128]],
                                               compare_op=ALU.is_ge, fill=0.0, base=0, channel_multiplier=-1),
             reads=[R_('mask')], writes=[R_('mask')])
        P.op('pool', lambda e: e.affine_select(out=mask2[:, 128:256], in_=mask2[:, 128:256], pattern=[[-1, 128]],
                                               compare_op=ALU.is_ge, fill=0.0, base=0, channel_multiplier=1),
             reads=[R_('mask')], writes=[R_('mask')])
        P.op('pool', lambda e: e.memset(ONES, 1.0), writes=[R_('ONES')])

        tcount = [0]
        ccount = [0]
        for j in js:
            for g in range(3):
                d = DIL[g]
                nb = S // d // 128
                h = 4 * g + j
                pair = h // 2
                pb = 64 * (h % 2)
                vv = v_d.rearrange("(m d) c -> d m c", d=d)
                for r in range(d):
                    for n0 in range(0, nb, 8):
                        n1 = min(nb, n0 + 8)
                        P.op('sp', lambda e, r=r, h=h, nb=nb, vv=vv, n0=n0, n1=n1: e.dma_start(
                            out=VX[:, r * nb + n0:r * nb + n1, :],
                            in_=vv[r, n0 * 128:n1 * 128, h * 64:(h + 1) * 64].rearrange("(n i) c -> i n c", i=128)),
                            writes=[R_('VX')], dma_key=R_('ldV'))
                P.op('sp', lambda e, pair=pair, pb=pb: e.dma_start(out=QH[0:64, :], in_=qk_d[pair, pb:pb + 64, :]),
                     writes=[R_('QH')], dma_key=R_('ldQ'))
                P.op('sp', lambda e, pair=pair, pb=pb: e.dma_start(out=KH[0:64, :], in_=qk_d[6 + pair, pb:pb + 64, :]),
                     writes=[R_('KH')], dma_key=R_('ldK'))
                qv = QH.rearrange("p (m d) -> p d m", d=d)
                kv = KH.rearrange("p (m d) -> p d m", d=d)
                otr = 'OT%d' % g
                otv = OT[g].rearrange("p (m d) -> p d m", d=d)
                dnv = DEN[g].rearrange("p (m d) -> p d m", d=d)
                for r in range(d):
                    prev = None
                    for n in range(nb):
                        ti = tcount[0]
                        tcount[0] += 1
                        nq = 256 if n + 1 < nb else 128
                        sbk = 1 + ti % 2
                        ps = bank(sbk)[:, 0:nq]
                        pt = PT[ti % 4]
                        ptr = 'PT%d' % (ti % 4)
                        ew('pe', lambda e, ps=ps, kv=kv, qv=qv, r=r, n=n, nq=nq: e.matmul(
                            ps, lhsT=kv[0:64, r, 128 * n:128 * n + 128], rhs=qv[0:64, r, 128 * n:128 * n + nq],
                            start=True, stop=True), ['KH', 'QH'], ['pb%d' % sbk])
                        ew('act', lambda e, ps=ps, pt=pt, nq=nq: e.activation(out=pt[:, 0:nq], in_=ps, func=AF.Exp),
                           ['pb%d' % sbk], [ptr])
                        ew('pool', lambda e, pt=pt, nq=nq: e.tensor_tensor(out=pt[:, 0:nq], in0=pt[:, 0:nq],
                                                                          in1=mask2[:, 0:nq], op=ALU.mult),
                           [ptr, 'mask'], [ptr])
                        obk = 3 + ti % 2
                        po = bank(obk)[0:64, 0:128]
                        pdn = bank(obk)[0:64, 128:256]
                        vt = r * nb + n
                        if prev is not None:
                            ppt, pptr, pvt = prev
                            ew('pe', lambda e, po=po, ppt=ppt, pvt=pvt: e.matmul(
                                po, lhsT=VX[:, pvt, :], rhs=ppt[:, 128:256], start=True, stop=False),
                                ['VX', pptr], ['pb%d' % obk])
                        ew('pe', lambda e, po=po, pt=pt, vt=vt, first=(prev is None): e.matmul(
                            po, lhsT=VX[:, vt, :], rhs=pt[:, 0:128], start=first, stop=True),
                            ['VX', ptr], ['pb%d' % obk])
                        if prev is not None:
                            ew('pe', lambda e, pdn=pdn, ppt=ppt: e.matmul(
                                pdn, lhsT=ONES, rhs=ppt[:, 128:256], start=True, stop=False),
                                ['ONES', pptr], ['pb%d' % obk])
                        ew('pe', lambda e, pdn=pdn, pt=pt, first=(prev is None): e.matmul(
                            pdn, lhsT=ONES, rhs=pt[:, 0:128], start=first, stop=True),
                            ['ONES', ptr], ['pb%d' % obk])
                        ew('dve', lambda e, po=po, otv=otv, r=r, n=n: e.tensor_copy(
                            out=otv[0:64, r, 128 * n:128 * n + 128], in_=po), ['pb%d' % obk], [otr])
                        ew('dve', lambda e, pdn=pdn, dnv=dnv, r=r, n=n: e.tensor_copy(
                            out=dnv[0:64, r, 128 * n:128 * n + 128], in_=pdn), ['pb%d' % obk], ['DEN%d' % g])
                        prev = (pt, ptr, vt)
            for ck in range(S // 512):
                csl = slice(ck * 512, (ck + 1) * 512)
                cc = ccount[0]
                ccount[0] += 1
                rd = RD[cc % 2]
                ew('pool', lambda e, rd=rd, csl=csl: e.tensor_tensor(out=rd[0:64, :], in0=DEN[0][0:64, csl],
                                                                     in1=DEN[1][0:64, csl], op=ALU.add),
                   ['DEN0', 'DEN1'], ['RD%d' % (cc % 2)])
                ew('pool', lambda e, rd=rd, csl=csl: e.tensor_tensor(out=rd[0:64, :], in0=rd[0:64, :],
                                                                     in1=DEN[2][0:64, csl], op=ALU.add),
                   ['RD%d' % (cc % 2), 'DEN2'], ['RD%d' % (cc % 2)])
                ew('dve', lambda e, rd=rd: e.reciprocal(out=rd[0:64, :], in_=rd[0:64, :]), ['RD%d' % (cc % 2)],
                   ['RD%d' % (cc % 2)])
                for g in range(3):
                    h = 4 * g + j
                    pair = h // 2
                    pb = 64 * (h % 2)
                    yi = (cc * 3 + g) % 2
                    ys = YBS[yi]
                    ew('pool', lambda e, g=g, csl=csl, rd=rd, ys=ys: e.tensor_tensor(
                        out=ys[0:64, :], in0=OT[g][0:64, csl], in1=rd[0:64, :], op=ALU.mult),
                        ['OT%d' % g, 'RD%d' % (cc % 2)], ['YBS%d' % yi])
                    P.op('sp', lambda e, pair=pair, pb=pb, csl=csl, ys=ys: e.dma_start(
                        out=yb_d[pair, pb:pb + 64, csl], in_=ys[0:64, :]),
                        reads=[R_('YBS%d' % yi)], writes=['d_yb'], dma_key=R_('stYB%d' % yi))

    wpr_d = din("w_proj_rwkv", [1024, 1024])
    wpa_d = din("w_proj_attn", [768, 1024])
    wo_d = din("w_out", [1024, 1024])
    x2_d = nc.dram_tensor("x2_scr", [S, D], F32, kind="ExternalOutput" if debug else "Internal").ap()

    def merge_stage():
        A.off = const_mark
        pre = 'm_'

        def R_(n):
            return pre + n

        def nm(x):
            if x.startswith('pb'):
                return x
            return R_(x)

        def ew(eng, fn, reads, writes):
            P.op(eng, fn, reads=[nm(x) for x in reads], writes=[nm(x) for x in writes])
        Wr = A.alloc([8, 1024], BF16)
        Wa = A.alloc([6, 1024], BF16)
        Wo = A.alloc([8, 1024], BF16)
        X = [A.alloc([NSUB, D], F32) for _ in range(2)]
        YA = [A.alloc([8, TT], BF16) for _ in range(2)]
        YB = [A.alloc([6, TT], BF16) for _ in range(2)]
        G = [A.alloc([16, TT], BF16) for _ in range(2)]
        MT = A.alloc([8, TT], BF16)
        t1 = [A.alloc([TT], F32) for _ in range(2)]
        t2 = [A.alloc([TT], F32) for _ in range(2)]
        print("merge_stage arena", A.off)
        for kc in range(8):
            P.op('pool', lambda e, kc=kc: e.dma_start(out=Wr[:, kc, :], in_=wpr_d[kc * 128:(kc + 1) * 128, :]),
                 writes=[R_('Wr')], dma_key=R_('w'))
            P.op('pool', lambda e, kc=kc: e.dma_start(out=Wo[:, kc, :], in_=wo_d[kc * 128:(kc + 1) * 128, :]),
                 writes=[R_('Wo')], dma_key=R_('w'))
        for kc in range(6):
            P.op('pool', lambda e, kc=kc: e.dma_start(out=Wa[:, kc, :], in_=wpa_d[kc * 128:(kc + 1) * 128, :]),
                 writes=[R_('Wa')], dma_key=R_('w'))
        srcv = x1_d.rearrange("(n s p) d -> n p s d", p=128, s=NSUB)
        dstv = x2_d.rearrange("(n s p) d -> n p s d", p=128, s=NSUB)
        for it in range(NT):
            sl = it % 2
            tsl = slice(it * TT, (it + 1) * TT)
            sfx = '%d' % sl
            P.op('sp', lambda e, it=it, sl=sl: e.dma_start(out=X[sl], in_=srcv[it]), writes=[R_('X' + sfx)],
                 dma_key=R_('ldX' + sfx))
            P.op('sp', lambda e, sl=sl, tsl=tsl: e.dma_start(out=YA[sl], in_=ya_d.rearrange("b p t -> p b t")[:, :, tsl]),
                 writes=[R_('YA' + sfx)], dma_key=R_('ldYA' + sfx))
            P.op('sp', lambda e, sl=sl, tsl=tsl: e.dma_start(out=YB[sl], in_=yb_d.rearrange("b p t -> p b t")[:, :, tsl]),
                 writes=[R_('YB' + sfx)], dma_key=R_('ldYB' + sfx))
            P.op('sp', lambda e, sl=sl, tsl=tsl: e.dma_start(out=G[sl], in_=gate_d.rearrange("b p t -> p b t")[:, :, tsl]),
                 writes=[R_('G' + sfx)], dma_key=R_('ldG' + sfx))
            for c in range(8):
                bk = 1 + c % 2
                pg = bank(bk)
                for kc in range(8):
                    ew('pe', lambda e, c=c, kc=kc, pg=pg, sl=sl: e.matmul(
                        pg[:, 0:TT], lhsT=Wr[:, kc, c * 128:(c + 1) * 128], rhs=YA[sl][:, kc, :],
                        start=(kc == 0), stop=(kc == 7)), ['Wr', 'YA' + sfx], ['pb%d' % bk])
                for kc in range(6):
                    ew('pe', lambda e, c=c, kc=kc, pg=pg, sl=sl: e.matmul(
                        pg[:, TT:2 * TT], lhsT=Wa[:, kc, c * 128:(c + 1) * 128], rhs=YB[sl][:, kc, :],
                        start=(kc == 0), stop=(kc == 5)), ['Wa', 'YB' + sfx], ['pb%d' % bk])
                q = c % 2
                ew('dve', lambda e, c=c, pg=pg, sl=sl, q=q: e.tensor_tensor(out=t1[q], in0=G[sl][:, c, :],
                                                                           in1=pg[:, 0:TT], op=ALU.mult),
                   ['G' + sfx, 'pb%d' % bk], ['t1%d' % q])
                ew('dve', lambda e, c=c, pg=pg, sl=sl, q=q: e.tensor_tensor(out=t2[q], in0=G[sl][:, 8 + c, :],
                                                                           in1=pg[:, TT:2 * TT], op=ALU.mult),
                   ['G' + sfx, 'pb%d' % bk], ['t2%d' % q])
                ew('pool', lambda e, c=c, q=q: e.tensor_tensor(out=MT[:, c, :], in0=t1[q], in1=t2[q], op=ALU.add),
                   ['t1%d' % q, 't2%d' % q], ['MT'])
            for s in range(NSUB):
                for dh in range(2):
                    bk = 3 + (s * 2 + dh) % 2
                    pd = bank(bk)
                    for c in range(8):
                        ew('pe', lambda e, c=c, s=s, dh=dh, pd=pd: e.matmul(
                            pd, lhsT=MT[:, c, s * 128:(s + 1) * 128], rhs=Wo[:, c, dh * 512:(dh + 1) * 512],
                            start=(c == 0), stop=(c == 7)), ['MT', 'Wo'], ['pb%d' % bk])
                    ew('dve', lambda e, s=s, dh=dh, pd=pd, sl=sl: e.tensor_tensor(
                        out=X[sl][:, s, dh * 512:(dh + 1) * 512], in0=X[sl][:, s, dh * 512:(dh + 1) * 512], in1=pd,
                        op=ALU.add), ['pb%d' % bk, 'X' + sfx], ['X' + sfx])
            P.op('sp', lambda e, it=it, sl=sl: e.dma_start(out=dstv[it], in_=X[sl]),
                 reads=[R_('X' + sfx)], writes=['d_x2'], dma_key=R_('stX' + sfx))

    if 'ffn1' in stages:
        ffn_stage(0, x, x1_d)
        P.sync_all()
    if 'proj' in stages:
        proj_stage()
        P.sync_all()
    if 'rwkv' in stages:
        rwkv_stage(range(NPAIRS_DBG))
        P.sync_all()
    if 'attn' in stages:
        attn_stage_full(JS_DBG)
        P.sync_all()
    if 'merge' in stages:
        merge_stage()
        P.sync_all()
    if 'ffn2' in stages:
        ffn_stage(1, x2_d if 'merge' in stages else x1_d, out)
    P.sync_all()
    P.op('sp', None)
    P.finalize_and_emit(stack)
    stack.close()
    return nc


_CACHE = {}


SHARED_KEYS = ['ffn1_norm', 'ffn1_w_in', 'ffn1_w_out', 'ffn2_norm', 'ffn2_w_in', 'ffn2_w_out',
               'w_in', 'mix_norm', 'rwkv_mu', 'b_gate', 'attn_q_norm', 'attn_k_norm',
               'w_proj_rwkv', 'w_proj_attn', 'w_out', 'rwkv_w2', 'rwkv_a2', 'rwkv_g2', 'rwkv_w0', 'rwkv_a0', 'rwkv_k_k', 'rwkv_k_a', 'rwkv_r_k', 'rwkv_ln_w', 'rwkv_ln_b']


def make_shared(inputs):
    shared = {}
    for k in SHARED_KEYS:
        v = np.asarray(inputs[k], dtype=np.float32)
        v = v.reshape(v.shape[1:])
        if k == 'rwkv_r_k':
            v = v.reshape(-1)
        shared[k] = np.ascontiguousarray(v)
    return shared


def kernel(**inputs):
    if 'nc' not in _CACHE:
        _CACHE['nc'] = build_program()
    nc = _CACHE['nc']
    x = np.ascontiguousarray(inputs['x'], dtype=np.float32)
    shared = make_shared(inputs)
    in_maps = []
    for c in range(NCORES):
        m = dict(shared)
        m['x'] = x[c]
        in_maps.append(m)
    res = run_bass_kernel_spmd(nc, in_maps, core_ids=list(range(NCORES)))
    return np.stack([np.asarray(r['out']) for r in res.results], axis=0)
```

```python
import numpy as np
from contextlib import ExitStack
import concourse.bass as bass
import concourse.mybir as mybir
from concourse.bass_utils import run_bass_kernel_spmd
from concourse.alu_op_type import AluOpType as ALU

F32 = mybir.dt.float32
BF16 = mybir.dt.bfloat16
AF = mybir.ActivationFunctionType
AX = mybir.AxisListType

S = 4096
D = 1024
DFF = 2816
NCORES = 8
RMS_EPS = 1e-6

ENGS = ['pe', 'act', 'dve', 'pool', 'sp']
MAXOPS = [0]
SEM_LIM = 30000
DMA_LIM = 1800


class Prog:
    def __init__(self, nc):
        self.nc = nc
        self.ops = []
        self.eng_ops = {e: [] for e in ENGS}
        self.last_w = {}
        self.readers = {}
        self.dma_cnt = {}
        self.barrier = {e: None for e in ENGS}

    def op(self, eng, fn, reads=(), writes=(), dma_key=None):
        mo = MAXOPS[0]
        if mo and len(self.ops) >= mo and fn is not None:
            return None
        if mo and len(self.ops) == mo - 1 and fn is not None:
            print("LAST OP:", eng, fn.__code__.co_firstlineno, reads, writes)
        oid = len(self.ops)
        deps = set()
        dma_deps = {}
        writes = list(writes) + [r for r in reads if (r.startswith('pb') or r.startswith('ps')) and r not in writes]

        def add(o):
            od = self.ops[o]
            if od['dma_key'] is not None:
                k = od['dma_key']
                dma_deps[k] = self.dma_cnt[k]
            else:
                deps.add(o)
        for r in reads:
            if r in self.last_w:
                add(self.last_w[r])
        for w in writes:
            if w in self.last_w:
                add(self.last_w[w])
            for rd in self.readers.get(w, {}).values():
                add(rd)
        if self.barrier[eng] is not None:
            bd, bdma = self.barrier[eng]
            for o in bd:
                deps.add(o)
            for k, v in bdma.items():
                dma_deps[k] = max(dma_deps.get(k, 0), v)
            self.barrier[eng] = None
        cnt = None
        if dma_key is not None:
            self.dma_cnt[dma_key] = self.dma_cnt.get(dma_key, 0) + 1
            cnt = self.dma_cnt[dma_key]
        o = dict(id=oid, eng=eng, fn=fn, deps=deps, dma_deps=dma_deps, dma_key=dma_key,
                 dma_cnt=cnt, idx=len(self.eng_ops[eng]), sig=False)
        self.ops.append(o)
        self.eng_ops[eng].append(o)
        ch = eng if dma_key is None else 'dma:' + dma_key
        for r in reads:
            self.readers.setdefault(r, {})[ch] = oid
        for w in writes:
            self.last_w[w] = oid
            self.readers[w] = {}
        return oid

    def sync_all(self):
        bd = set()
        for e in ENGS:
            for o in reversed(self.eng_ops[e]):
                if o['dma_key'] is None and o['fn'] is not None:
                    bd.add(o['id'])
                    break
        bdma = dict(self.dma_cnt)
        for e in ENGS:
            self.barrier[e] = (set(bd), dict(bdma))

    def finalize_and_emit(self, stack):
        nc = self.nc
        for o in self.ops:
            per = {}
            for d in o['deps']:
                od = self.ops[d]
                if od['eng'] == 'pe' and o['eng'] == 'pe':
                    continue
                e = od['eng']
                if e not in per or self.ops[per[e]]['idx'] < od['idx']:
                    per[e] = d
            o['cdeps'] = per
            for d in per.values():
                self.ops[d]['sig'] = True
        sems = {}

        def get_sem(name):
            return sems[name]
        for e in ENGS:
            c = 0
            for o in self.eng_ops[e]:
                if o['dma_key'] is None and o['sig']:
                    c += 1
                    o['sigval'] = c
        for o in self.ops:
            waits = {}
            for e, d in o['cdeps'].items():
                v = self.ops[d]['sigval']
                key = ('c_%s_%d' % (e, (v - 1) // SEM_LIM))
                val = (v - 1) % SEM_LIM + 1
                waits[key] = max(waits.get(key, 0), val)
            for k, n in o['dma_deps'].items():
                key = ('d_%s_%d' % (k, (n - 1) // DMA_LIM))
                val = 16 * ((n - 1) % DMA_LIM + 1)
                waits[key] = max(waits.get(key, 0), val)
            o['waits'] = waits
        names = set()
        for o in self.ops:
            names.update(o['waits'].keys())
            if o['dma_key'] is not None:
                names.add('d_%s_%d' % (o['dma_key'], (o['dma_cnt'] - 1) // DMA_LIM))
            elif o['sig']:
                names.add('c_%s_%d' % (o['eng'], (o['sigval'] - 1) // SEM_LIM))
        for nm in sorted(names):
            sems[nm] = stack.enter_context(nc.semaphore(nm))
        print("n_sems", len(names), "n_ops", len(self.ops), {e: len(v) for e, v in self.eng_ops.items()})
        block = stack.enter_context(nc.Block())
        decos = {'pe': block.tensor, 'act': block.scalar, 'dve': block.vector,
                 'pool': block.gpsimd, 'sp': block.sync}
        for e in ENGS:
            ops = self.eng_ops[e]

            def body(eng, ops=ops, e=e):
                waited = {}
                for o in ops:
                    for key, val in o['waits'].items():
                        if waited.get(key, 0) >= val:
                            continue
                        waited[key] = val
                        eng.wait_ge(get_sem(key), val)
                    if o['fn'] is None:
                        continue
                    ins = o['fn'](eng)
                    if o['dma_key'] is not None:
                        n = o['dma_cnt']
                        ins.then_inc(get_sem('d_%s_%d' % (o['dma_key'], (n - 1) // DMA_LIM)), 16)
                    elif o['sig']:
                        v = o['sigval']
                        ins.then_inc(get_sem('c_%s_%d' % (e, (v - 1) // SEM_LIM)), 1)
            decos[e](body)


class Arena:
    def __init__(self, tensor, nbytes):
        self.t = tensor
        self.nbytes = nbytes
        self.off = 0

    def alloc(self, shape, dtype, parts=128):
        n = int(np.prod(shape))
        esz = 4 if dtype == F32 else 2
        nb = n * esz
        nb_al = (nb + 63) // 64 * 64
        assert self.off + nb_al <= self.nbytes, ("SBUF arena overflow", self.off, nb_al)
        ap = self.t[0:parts, self.off // 2:(self.off + nb) // 2]
        self.off += nb_al
        if dtype == F32:
            ap = ap.bitcast(F32)
        if len(shape) == 2:
            ap = ap.rearrange("p (a b) -> p a b", a=shape[0], b=shape[1])
        elif len(shape) == 3:
            ap = ap.rearrange("p (a b c) -> p a b c", a=shape[0], b=shape[1], c=shape[2])
        return ap


def build_program(debug=False, NPAIRS_DBG=8, stages=('ffn1', 'proj', 'rwkv', 'attn', 'merge', 'ffn2'), NB_DBG=None,
                  JS_DBG=range(4)):
    nc = bass.Bass("TRN2", target_bir_lowering=False)
    P = Prog(nc)

    def din(name, shape):
        return nc.dram_tensor(name, list(shape), F32, kind="ExternalInput").ap()
    x = din("x", [S, D])
    ffn_norm = [din("ffn1_norm", [D]), din("ffn2_norm", [D])]
    ffn_win = [din("ffn1_w_in", [D, 2 * DFF]), din("ffn2_w_in", [D, 2 * DFF])]
    ffn_wout = [din("ffn1_w_out", [DFF, D]), din("ffn2_w_out", [DFF, D])]
    out = nc.dram_tensor("out", [S, D], F32, kind="ExternalOutput").ap()
    x1_d = nc.dram_tensor("x1_scr", [S, D], F32, kind="ExternalOutput" if debug else "Internal").ap()

    stack = ExitStack()
    ARENA_BYTES = 200 * 1024
    arena_t = stack.enter_context(nc.sbuf_tensor("arena", [128, ARENA_BYTES // 2], BF16))
    A = Arena(arena_t, ARENA_BYTES)
    psum = stack.enter_context(nc.psum_tensor("psum", [128, 4096], F32))

    def bank(b, n=512, off=0):
        return psum[:, b * 512 + off:b * 512 + off + n]

    ident_f = A.alloc([128], F32)
    ident = A.alloc([128], BF16)
    ones_col = A.alloc([1], F32)

    P.op('pool', lambda e: e.memset(ident_f, 0.0), writes=['ident_f'])
    P.op('pool', lambda e: e.affine_select(out=ident_f, in_=ident_f, pattern=[[-1, 128]],
                                           compare_op=ALU.not_equal, fill=1.0, base=0, channel_multiplier=1),
         reads=['ident_f'], writes=['ident_f'])
    P.op('dve', lambda e: e.tensor_copy(out=ident, in_=ident_f), reads=['ident_f'], writes=['ident'])

    const_mark = A.off

    TT = 256
    NSUB = TT // 128
    NT = S // TT
    KC = D // 128
    FC = DFF // 128

    def ffn_stage(si, src, dst):
        A.off = const_mark
        W1 = A.alloc([KC, 2 * DFF], BF16)
        W2 = A.alloc([FC, D], BF16)
        gb = A.alloc([D], F32)
        xt = [A.alloc([NSUB, D], F32) for _ in range(2)]
        hb = [A.alloc([D], BF16) for _ in range(2)]
        hT = [A.alloc([KC, TT], BF16) for _ in range(2)]
        actT = A.alloc([FC, TT], BF16)
        sg = [A.alloc([TT], F32) for _ in range(2)]
        junk = A.alloc([D], BF16)
        ss = A.alloc([8], F32)
        pre = 's%d_' % si
        w1v = ffn_win[si].rearrange("(kc p) f -> p kc f", p=128)
        CH = 1408
        for kc in range(KC):
            for c in range(2 * DFF // CH):
                P.op('pool', lambda e, kc=kc, c=c: e.dma_start(out=W1[:, kc, c * CH:(c + 1) * CH],
                                                              in_=w1v[:, kc, c * CH:(c + 1) * CH]),
                     writes=[pre + 'W1'], dma_key=pre + 'W1')
        w2v = ffn_wout[si].rearrange("(fc p) d -> p fc d", p=128)
        for fc in range(FC):
            P.op('pool', lambda e, fc=fc: e.dma_start(out=W2[:, fc, :], in_=w2v[:, fc, :]),
                 writes=[pre + 'W2'], dma_key=pre + 'W2')
        P.op('sp', lambda e: e.dma_start(out=gb, in_=ffn_norm[si].partition_broadcast(128)),
             writes=[pre + 'gb'], dma_key=pre + 'gb')
        srcv = src.rearrange("(n s p) d -> n p s d", p=128, s=NSUB)
        dstv = dst.rearrange("(n s p) d -> n p s d", p=128, s=NSUB)
        for it in range(NT):
            sl = it % 2
            X = xt[sl]
            xr = pre + 'xt%d' % sl
            P.op('sp', lambda e, it=it, X=X: e.dma_start(out=X, in_=srcv[it]),
                 writes=[xr], dma_key=xr)
            HT = hT[sl]
            for s in range(NSUB):
                hs = (it * NSUB + s) % 2
                H = hb[hs]
                hr = pre + 'hb%d' % hs
                P.op('act', lambda e, X=X, s=s: e.activation(out=junk, in_=X[:, s, :], func=AF.Square,
                                                             accum_out=ss[:, 0:1]),
                     reads=[xr], writes=[pre + 'junk', pre + 'ss'])
                P.op('act', lambda e: e.activation(out=ss[:, 1:2], in_=ss[:, 0:1], func=AF.Sqrt,
                                                   scale=1.0 / D, bias=RMS_EPS),
                     reads=[pre + 'ss'], writes=[pre + 'ss1'])
                P.op('dve', lambda e: e.reciprocal(out=ss[:, 2:3], in_=ss[:, 1:2]),
                     reads=[pre + 'ss1'], writes=[pre + 'ss2'])
                P.op('dve', lambda e, X=X, s=s, H=H: e.scalar_tensor_tensor(
                    out=H, in0=X[:, s, :], scalar=ss[:, 2:3], in1=gb, op0=ALU.mult, op1=ALU.mult),
                    reads=[xr, pre + 'ss2', pre + 'gb'], writes=[hr])
                pT = bank(0).bitcast(BF16)
                for kc in range(KC):
                    P.op('pe', lambda e, kc=kc, H=H, pT=pT: e.transpose(
                        out=pT[:, kc * 128:(kc + 1) * 128], in_=H[:, kc * 128:(kc + 1) * 128], identity=ident),
                        reads=[hr, 'ident'], writes=['psT'])
                P.op('act', lambda e, HT=HT, s=s, pT=pT: e.copy(
                    out=HT[:, :, s * 128:(s + 1) * 128], in_=pT.rearrange("p (k t) -> p k t", k=KC)),
                    reads=['psT'], writes=[pre + 'hT%d' % sl])
            for fc in range(FC):
                b = 1 + fc % 2
                pg = bank(b)
                for half in range(2):
                    col = half * DFF + fc * 128
                    for kc in range(KC):
                        P.op('pe', lambda e, kc=kc, col=col, half=half, pg=pg, HT=HT: e.matmul(
                            pg[:, half * TT:(half + 1) * TT], lhsT=W1[:, kc, col:col + 128], rhs=HT[:, kc, :],
                            start=(kc == 0), stop=(kc == KC - 1)),
                            reads=[pre + 'W1', pre + 'hT%d' % sl], writes=['psG%d' % b])
                SG = sg[fc % 2]
                P.op('act', lambda e, pg=pg, SG=SG: e.activation(out=SG, in_=pg[:, 0:TT], func=AF.Silu),
                     reads=['psG%d' % b], writes=[pre + 'sg%d' % (fc % 2)])
                P.op('dve', lambda e, pg=pg, SG=SG, fc=fc: e.tensor_tensor(
                    out=actT[:, fc, :], in0=SG, in1=pg[:, TT:2 * TT], op=ALU.mult),
                    reads=['psG%d' % b, pre + 'sg%d' % (fc % 2)], writes=[pre + 'actT'])
            for s in range(NSUB):
                for dh in range(2):
                    b = 3 + (s * 2 + dh) % 2
                    pd = bank(b)
                    for fc in range(FC):
                        P.op('pe', lambda e, fc=fc, s=s, dh=dh, pd=pd: e.matmul(
                            pd, lhsT=actT[:, fc, s * 128:(s + 1) * 128], rhs=W2[:, fc, dh * 512:(dh + 1) * 512],
                            start=(fc == 0), stop=(fc == FC - 1)),
                            reads=[pre + 'actT', pre + 'W2'], writes=['psD%d' % b])
                    P.op('dve', lambda e, X=X, s=s, dh=dh, pd=pd: e.scalar_tensor_tensor(
                        out=X[:, s, dh * 512:(dh + 1) * 512], in0=pd, scalar=0.5,
                        in1=X[:, s, dh * 512:(dh + 1) * 512], op0=ALU.mult, op1=ALU.add),
                        reads=['psD%d' % b, xr], writes=[xr])
            P.op('sp', lambda e, it=it, X=X: e.dma_start(out=dstv[it], in_=X),
                 reads=[xr], writes=[pre + 'dst'], dma_key=pre + 'st%d' % sl)

    NCOL = 7712
    w_in = din("w_in", [D, NCOL])
    mix_norm = din("mix_norm", [D])
    rwkv_mu = din("rwkv_mu", [3360])
    b_gate = din("b_gate", [2048])
    qn = din("attn_q_norm", [64])
    kn = din("attn_k_norm", [64])
    kscr = "ExternalOutput" if debug else "Internal"
    if 'proj' not in stages:
        kscr = "ExternalInput"
    rkv_d = nc.dram_tensor("rkv_scr", [24, 128, S], F32, kind=kscr).ap()
    ta_d = nc.dram_tensor("ta_scr", [128, S], BF16, kind=kscr).ap()
    tg_d = nc.dram_tensor("tg_scr", [160, S], BF16, kind=kscr).ap()
    qk_d = nc.dram_tensor("qk_scr", [12, 128, S], BF16, kind=kscr).ap()
    v_d = nc.dram_tensor("v_scr", [S, 768], BF16, kind=kscr).ap()
    gate_d = nc.dram_tensor("gate_scr", [16, 128, S], BF16, kind=kscr).ap()

    def proj_stage():
        A.off = const_mark
        pre = 'p_'
        W = A.alloc([KC, NCOL], BF16)
        gb = A.alloc([D], F32)
        X = A.alloc([NSUB, D], F32)
        hb = [A.alloc([D], BF16) for _ in range(2)]
        hT = [A.alloc([KC, TT], BF16) for _ in range(2)]
        ss = A.alloc([8], F32)
        mu_t = A.alloc([27], F32)
        bg_t = A.alloc([16], F32)
        qg_t = A.alloc([2], F32)
        carry = A.alloc([27], F32)
        psb = [A.alloc([TT + 1], F32) for _ in range(2)]
        tmp = [A.alloc([TT], F32) for _ in range(2)]
        sq = [A.alloc([TT], BF16) for _ in range(2)]
        lnb = [A.alloc([TT], F32) for _ in range(2)]
        rkv_st = A.alloc([24, TT], F32)
        ta_st = A.alloc([TT], BF16)
        tg_st = A.alloc([2, TT], BF16)
        qk_st = A.alloc([12, TT], BF16)
        gate_st = A.alloc([16, TT], BF16)
        v_st = A.alloc([NSUB, 768], BF16)
        bones = A.alloc([128], BF16)
        print("proj_stage arena", A.off)
        wv = w_in.rearrange("(kc p) f -> p kc f", p=128)
        CH = 964
        for kc in range(KC):
            for c in range(NCOL // CH):
                P.op('pool', lambda e, kc=kc, c=c: e.dma_start(out=W[:, kc, c * CH:(c + 1) * CH],
                                                              in_=wv[:, kc, c * CH:(c + 1) * CH]),
                     writes=[pre + 'W'], dma_key=pre + 'W')
        P.op('sp', lambda e: e.dma_start(out=gb, in_=mix_norm.partition_broadcast(128)),
             writes=[pre + 'gb'], dma_key=pre + 'par')
        P.op('sp', lambda e: e.dma_start(out=mu_t[:, 0:26], in_=rwkv_mu[0:3328].rearrange("(b p) -> p b", p=128),
                                         allow_slow_non_contiguous=True), writes=[pre + 'mu'], dma_key=pre + 'par')
        P.op('sp', lambda e: e.dma_start(out=mu_t[0:32, 26:27], in_=rwkv_mu[3328:3360].rearrange("(p o) -> p o", o=1)),
             writes=[pre + 'mu'], dma_key=pre + 'par')
        P.op('sp', lambda e: e.dma_start(out=bg_t, in_=b_gate.rearrange("(b p) -> p b", p=128),
                                         allow_slow_non_contiguous=True), writes=[pre + 'bg'], dma_key=pre + 'par')
        for hh in range(2):
            P.op('sp', lambda e, hh=hh: e.dma_start(out=qg_t[hh * 64:(hh + 1) * 64, 0:1],
                                                     in_=qn.rearrange("(p o) -> p o", o=1)),
                 writes=[pre + 'qg'], dma_key=pre + 'par')
            P.op('sp', lambda e, hh=hh: e.dma_start(out=qg_t[hh * 64:(hh + 1) * 64, 1:2],
                                                     in_=kn.rearrange("(p o) -> p o", o=1)),
                 writes=[pre + 'qg'], dma_key=pre + 'par')
        P.op('pool', lambda e: e.tensor_scalar(out=qg_t[:, 0:1], in0=qg_t[:, 0:1], scalar1=0.125, scalar2=None,
                                               op0=ALU.mult), reads=[pre + 'qg'], writes=[pre + 'qg'])
        P.op('pool', lambda e: e.memset(carry, 0.0), writes=[pre + 'carry%d' % i for i in range(27)])
        P.op('pool', lambda e: e.memset(bones, 0.0), writes=[pre + 'bones'])
        P.op('pool', lambda e: e.memset(bones[0:64, 0:64], 1.0), reads=[pre + 'bones'], writes=[pre + 'bones'])
        P.op('pool', lambda e: e.memset(bones[64:128, 64:128], 1.0), reads=[pre + 'bones'], writes=[pre + 'bones'])

        blocks = []
        for b in range(24):
            blocks.append((b * 128, 128, 'rkv', b))
        blocks.append((3072, 128, 'ta', 24))
        blocks.append((3200, 128, 'tg0', 25))
        blocks.append((3328, 32, 'tg1', 26))
        for b in range(6):
            blocks.append((3360 + b * 128, 128, 'q', b))
        for b in range(6):
            blocks.append((4128 + b * 128, 128, 'k', 6 + b))
        for b in range(16):
            blocks.append((5664 + b * 128, 128, 'gate', b))

        srcv = x1_d.rearrange("(n s p) d -> n p s d", p=128, s=NSUB)
        xr = pre + 'X'
        for it in range(NT):
            t0 = it * TT
            sl = it % 2
            P.op('sp', lambda e, it=it: e.dma_start(out=X, in_=srcv[it]), writes=[xr], dma_key=xr)
            HT = hT[sl]
            htr = pre + 'hT%d' % sl
            for s in range(NSUB):
                hs = (it * NSUB + s) % 2
                H = hb[hs]
                hr = pre + 'hb%d' % hs
                P.op('act', lambda e, s=s, H=H: e.activation(out=H, in_=X[:, s, :], func=AF.Square,
                                                             accum_out=ss[:, 0:1]),
                     reads=[xr], writes=[hr, pre + 'ss'])
                P.op('act', lambda e: e.activation(out=ss[:, 1:2], in_=ss[:, 0:1], func=AF.Sqrt,
                                                   scale=1.0 / D, bias=RMS_EPS),
                     reads=[pre + 'ss'], writes=[pre + 'ss1'])
                P.op('dve', lambda e: e.reciprocal(out=ss[:, 2:3], in_=ss[:, 1:2]),
                     reads=[pre + 'ss1'], writes=[pre + 'ss2'])
                P.op('dve', lambda e, s=s, H=H: e.scalar_tensor_tensor(
                    out=H, in0=X[:, s, :], scalar=ss[:, 2:3], in1=gb, op0=ALU.mult, op1=ALU.mult),
                    reads=[xr, pre + 'ss2', pre + 'gb'], writes=[hr])
                pT = bank(0).bitcast(BF16)
                for kc in range(KC):
                    P.op('pe', lambda e, kc=kc, H=H, pT=pT: e.transpose(
                        out=pT[:, kc * 128:(kc + 1) * 128], in_=H[:, kc * 128:(kc + 1) * 128], identity=ident),
                        reads=[hr, 'ident'], writes=['psT'])
                P.op('act', lambda e, HT=HT, s=s, pT=pT: e.copy(
                    out=HT[:, :, s * 128:(s + 1) * 128], in_=pT.rearrange("p (k t) -> p k t", k=KC)),
                    reads=['psT'], writes=[htr])

            pending = [None]

            def flush():
                if pending[0] is None:
                    return
                pg, pgr, j, kind, idx = pending[0]
                pending[0] = None
                pss = bank(5)[:, 0:TT]
                P.op('pe', lambda e, j=j, pss=pss: e.matmul(pss, lhsT=bones, rhs=sq[j], start=True, stop=True),
                     reads=[pre + 'sq%d' % j, pre + 'bones'], writes=['pss'])
                P.op('act', lambda e, j=j, pss=pss: e.activation(out=lnb[j], in_=pss, func=AF.Ln,
                                                                 scale=1.0 / 64, bias=RMS_EPS),
                     reads=['pss'], writes=[pre + 'lnb%d' % j])
                P.op('act', lambda e, j=j: e.activation(out=lnb[j], in_=lnb[j], func=AF.Exp, scale=-0.5),
                     reads=[pre + 'lnb%d' % j], writes=[pre + 'lnb%d' % j])
                c = 0 if kind == 'q' else 1
                P.op('dve', lambda e, j=j, pg=pg, idx=idx, c=c: e.scalar_tensor_tensor(
                    out=qk_st[:, idx, :], in0=pg, scalar=qg_t[:, c:c + 1], in1=lnb[j], op0=ALU.mult, op1=ALU.mult),
                    reads=[pgr, pre + 'lnb%d' % j, pre + 'qg'], writes=[pre + 'qk_st'])

            for bi, (col0, M, kind, idx) in enumerate(blocks):
                b = 1 + bi % 4
                pgr = 'psG%d' % b
                pg = bank(b)[0:M, 0:TT]
                j = bi % 2
                for kc in range(KC):
                    P.op('pe', lambda e, kc=kc, col0=col0, M=M, pg=pg, HT=HT: e.matmul(
                        pg, lhsT=W[:, kc, col0:col0 + M], rhs=HT[:, kc, :], start=(kc == 0), stop=(kc == KC - 1)),
                        reads=[pre + 'W', htr], writes=[pgr])
                flush()
                if kind in ('rkv', 'ta', 'tg0', 'tg1'):
                    cr = pre + 'carry%d' % idx
                    pbr = pre + 'psb%d' % j
                    tr = pre + 'tmp%d' % j
                    PS = psb[j][0:M]
                    TM = tmp[j][0:M]
                    P.op('pool', lambda e, PS=PS, idx=idx, M=M: e.tensor_copy(out=PS[:, 0:1], in_=carry[0:M, idx:idx + 1]),
                         reads=[cr], writes=[pbr])
                    P.op('act', lambda e, PS=PS, pg=pg: e.copy(out=PS[:, 1:TT + 1], in_=pg),
                         reads=[pgr, pbr], writes=[pbr])
                    P.op('dve', lambda e, PS=PS, TM=TM: e.tensor_tensor(out=TM, in0=PS[:, 0:TT], in1=PS[:, 1:TT + 1],
                                                                        op=ALU.subtract),
                         reads=[pbr], writes=[tr])
                    P.op('pool', lambda e, PS=PS, idx=idx, M=M: e.tensor_copy(out=carry[0:M, idx:idx + 1],
                                                                              in_=PS[:, TT:TT + 1]),
                         reads=[pbr], writes=[cr])
                    if kind == 'rkv':
                        P.op('dve', lambda e, PS=PS, TM=TM, idx=idx: e.scalar_tensor_tensor(
                            out=rkv_st[:, idx, :], in0=TM, scalar=mu_t[:, idx:idx + 1], in1=PS[:, 1:TT + 1],
                            op0=ALU.mult, op1=ALU.add),
                            reads=[tr, pbr, pre + 'mu'], writes=[pre + 'rkv_st%d' % (idx // 8)])
                    else:
                        P.op('dve', lambda e, PS=PS, TM=TM, idx=idx, M=M: e.scalar_tensor_tensor(
                            out=TM, in0=TM, scalar=mu_t[0:M, idx:idx + 1], in1=PS[:, 1:TT + 1],
                            op0=ALU.mult, op1=ALU.add),
                            reads=[tr, pbr, pre + 'mu'], writes=[tr])
                        if kind == 'ta':
                            P.op('act', lambda e, TM=TM: e.activation(out=ta_st[0:64], in_=TM[0:64], func=AF.Tanh),
                                 reads=[tr], writes=[pre + 'ta_st'])
                            P.op('act', lambda e, TM=TM: e.copy(out=ta_st[64:128], in_=TM[64:128]),
                                 reads=[tr], writes=[pre + 'ta_st'])
                        elif kind == 'tg0':
                            P.op('act', lambda e, TM=TM: e.activation(out=tg_st[:, 0, :], in_=TM, func=AF.Sigmoid),
                                 reads=[tr], writes=[pre + 'tg_st'])
                        else:
                            P.op('act', lambda e, TM=TM: e.activation(out=tg_st[0:32, 1, :], in_=TM, func=AF.Sigmoid),
                                 reads=[tr], writes=[pre + 'tg_st'])
                elif kind in ('q', 'k'):
                    P.op('act', lambda e, pg=pg, j=j: e.activation(out=sq[j], in_=pg, func=AF.Square),
                         reads=[pgr], writes=[pre + 'sq%d' % j])
                    pending[0] = (pg, pgr, j, kind, idx)
                else:
                    P.op('act', lambda e, pg=pg, idx=idx: e.activation(out=gate_st[:, idx, :], in_=pg, func=AF.Sigmoid,
                                                                       bias=bg_t[:, idx:idx + 1]),
                         reads=[pgr, pre + 'bg'], writes=[pre + 'gate_st'])
            flush()
            for s in range(NSUB):
                for (c0, n, b) in ((4896, 512, 6), (5408, 256, 7)):
                    pv = bank(b)[:, 0:n]
                    for kc in range(KC):
                        P.op('pe', lambda e, kc=kc, s=s, c0=c0, n=n, pv=pv, HT=HT: e.matmul(
                            pv, lhsT=HT[:, kc, s * 128:(s + 1) * 128], rhs=W[:, kc, c0:c0 + n],
                            start=(kc == 0), stop=(kc == KC - 1)),
                            reads=[pre + 'W', htr], writes=['psV%d' % b])
                P.op('act', lambda e, s=s: e.copy(out=v_st[:, s, 0:512], in_=bank(6)),
                     reads=['psV6'], writes=[pre + 'v_st'])
                P.op('dve', lambda e, s=s: e.tensor_copy(out=v_st[:, s, 512:768], in_=bank(7)[:, 0:256]),
                     reads=['psV7'], writes=[pre + 'v_st'])
            rv = rkv_d.rearrange("b p t -> p b t")
            for g in range(3):
                P.op('sp', lambda e, g=g, t0=t0: e.dma_start(out=rv[:, g * 8:(g + 1) * 8, t0:t0 + TT],
                                                             in_=rkv_st[:, g * 8:(g + 1) * 8, :]),
                     reads=[pre + 'rkv_st%d' % g], writes=['d_rkv'], dma_key=pre + 'rkv_st%d' % g)
            P.op('sp', lambda e, t0=t0: e.dma_start(out=ta_d[:, t0:t0 + TT], in_=ta_st),
                 reads=[pre + 'ta_st'], writes=['d_ta'], dma_key=pre + 'ta_st')
            P.op('sp', lambda e, t0=t0: e.dma_start(out=tg_d[0:128, t0:t0 + TT], in_=tg_st[:, 0, :]),
                 reads=[pre + 'tg_st'], writes=['d_tg'], dma_key=pre + 'tg_st')
            P.op('sp', lambda e, t0=t0: e.dma_start(out=tg_d[128:160, t0:t0 + TT], in_=tg_st[0:32, 1, :]),
                 reads=[pre + 'tg_st'], writes=['d_tg'], dma_key=pre + 'tg_st')
            P.op('sp', lambda e, t0=t0: e.dma_start(out=qk_d.rearrange("b p t -> p b t")[:, :, t0:t0 + TT], in_=qk_st),
                 reads=[pre + 'qk_st'], writes=['d_qk'], dma_key=pre + 'qk_st')
            P.op('sp', lambda e, t0=t0: e.dma_start(out=gate_d.rearrange("b p t -> p b t")[:, :, t0:t0 + TT],
                                                    in_=gate_st),
                 reads=[pre + 'gate_st'], writes=['d_gate'], dma_key=pre + 'gate_st')
            P.op('sp', lambda e, t0=t0: e.dma_start(
                out=v_d[t0:t0 + TT, :].rearrange("(s p) c -> p s c", p=128), in_=v_st),
                reads=[pre + 'v_st'], writes=['d_v'], dma_key=pre + 'v_st')

    w2_d = din("rwkv_w2", [64, 1024])
    a2_d = din("rwkv_a2", [64, 1024])
    g2_d = din("rwkv_g2", [160, 1024])
    prm_names = ['rwkv_w0', 'rwkv_a0', 'rwkv_k_k', 'rwkv_k_a', 'rwkv_r_k', 'rwkv_ln_w', 'rwkv_ln_b']
    prm_d = [din(n, [1024]) for n in prm_names]
    ya_d = nc.dram_tensor("ya_scr", [8, 128, S], BF16, kind="ExternalOutput" if debug else "Internal").ap()
    TB = 1024
    NCH = TB // 128
    NB = S // TB if NB_DBG is None else NB_DBG
    C0 = float(np.exp(-0.5))
    GN_EPS = 64e-5

    def rwkv_stage(pairs=range(8)):
        A.off = const_mark
        pre = 'r_'
        WA = A.alloc([1024], BF16)
        G2a = A.alloc([1024], BF16)
        G2b = A.alloc([1024], BF16)
        prm = A.alloc([7, 8], F32)
        bones = A.alloc([128], BF16)
        mk4 = A.alloc([512], F32)
        mkL = A.alloc([2, 128], F32)
        E2 = A.alloc([64], F32)
        mrow = A.alloc([TB], F32)
        f32names = ['R', 'K', 'V', 'SG', 'AA', 'GG', 'KK', 'KM', 'T1', 'BVEC', 'CS', 'T2', 'T3', 'EP', 'EN', 'EPM',
                    'EC', 'BV', 'Y32', 'DD']
        T = {n: A.alloc([TB], F32) for n in f32names}
        bfnames = ['TA', 'TG0', 'TG1', 'TQ', 'BT', 'KT', 'BH', 'KH', 'VT', 'YB', 'YO']
        for n in bfnames:
            T[n] = A.alloc([TB], BF16)
        AR = A.alloc([NCH, 2, 128], BF16)
        TM4 = A.alloc([NCH, 4, 128], BF16)
        PC = A.alloc([NCH], F32)
        SC = [A.alloc([512], BF16) for _ in range(2)]
        LZ = [A.alloc([2, 384], BF16) for _ in range(2)]
        MCz = A.alloc([2, 64], BF16)
        QT = A.alloc([128], BF16)
        STz = A.alloc([2, 64], BF16)
        print("rwkv_stage arena", A.off)

        def R_(n):
            return pre + n
        P.op('pool', lambda e: e.dma_start(out=WA[0:64, :], in_=w2_d), writes=[R_('WA')], dma_key=R_('w'))
        P.op('pool', lambda e: e.dma_start(out=WA[64:128, :], in_=a2_d), writes=[R_('WA')], dma_key=R_('w'))
        P.op('pool', lambda e: e.dma_start(out=G2a, in_=g2_d[0:128, :]), writes=[R_('G2')], dma_key=R_('w'))
        P.op('pool', lambda e: e.dma_start(out=G2b[0:32, :], in_=g2_d[128:160, :]), writes=[R_('G2')], dma_key=R_('w'))
        for i in range(7):
            P.op('sp', lambda e, i=i: e.dma_start(out=prm[:, i, :], in_=prm_d[i].rearrange("(b p) -> p b", p=128),
                                                   allow_slow_non_contiguous=True),
                 writes=[R_('prm')], dma_key=R_('par'))
        P.op('pool', lambda e: e.memset(bones, 0.0), writes=[R_('bones')])
        P.op('pool', lambda e: e.memset(bones[0:64, 0:64], 1.0), reads=[R_('bones')], writes=[R_('bones')])
        P.op('pool', lambda e: e.memset(bones[64:128, 64:128], 1.0), reads=[R_('bones')], writes=[R_('bones')])
        P.op('pool', lambda e: e.memset(mk4, 1.0), writes=[R_('mk4')])
        for q in range(4):
            base = -1 if q % 2 == 0 else 0
            P.op('pool', lambda e, q=q, base=base: e.affine_select(
                out=mk4[:, q * 128:(q + 1) * 128], in_=mk4[:, q * 128:(q + 1) * 128], pattern=[[1, 128]],
                compare_op=ALU.is_ge, fill=0.0, base=base, channel_multiplier=-1),
                reads=[R_('mk4')], writes=[R_('mk4')])
        P.op('pool', lambda e: e.memset(mkL, 1.0), writes=[R_('mkL')])
        P.op('pool', lambda e: e.affine_select(out=mkL, in_=mkL, pattern=[[0, 2], [-1, 128]], compare_op=ALU.is_ge,
                                               fill=0.0, base=-1, channel_multiplier=1),
             reads=[R_('mkL')], writes=[R_('mkL')])
        P.op('pool', lambda e: e.tensor_copy(out=E2[0:64, :], in_=ident_f[0:64, 0:64]), reads=['ident_f'], writes=[R_('E2')])
        P.op('pool', lambda e: e.tensor_copy(out=E2[64:128, :], in_=ident_f[64:128, 64:128]), reads=['ident_f'],
             writes=[R_('E2')])
        P.op('pool', lambda e: e.memset(mrow, 1.0), writes=[R_('mrow')])
        P.op('pool', lambda e: e.memset(mrow.rearrange("p (c t) -> p c t", t=128)[:, :, 0:1], 0.0),
             reads=[R_('mrow')], writes=[R_('mrow')])

        def ch3(ap):
            return ap.rearrange("p (c t) -> p c t", t=128)

        def nm(x):
            if x.startswith('pb') or x in ('ident', 'ident_f'):
                return x
            return R_(x)

        def ew(eng, fn, reads, writes):
            P.op(eng, fn, reads=[nm(x) for x in reads], writes=[nm(x) for x in writes])

        for hp in pairs:
            cols = slice(hp * 128, (hp + 1) * 128)
            ew('pool', lambda e: e.memset(STz, 0.0), [], ['ST'])
            ew('pool', lambda e: e.memset(MCz, 0.0), [], ['MC'])
            for tb in range(NB):
                t0 = tb * TB
                tsl = slice(t0, t0 + TB)
                for i, n in enumerate(['R', 'K', 'V']):
                    P.op('sp', lambda e, i=i, n=n, hp=hp, tsl=tsl: e.dma_start(out=T[n], in_=rkv_d[i * 8 + hp, :, tsl]),
                         writes=[R_(n)], dma_key=R_('ld' + n))
                P.op('sp', lambda e, tsl=tsl: e.dma_start(out=T['TA'], in_=ta_d[:, tsl]), writes=[R_('TA')],
                     dma_key=R_('ldTA'))
                P.op('sp', lambda e, tsl=tsl: e.dma_start(out=T['TG0'], in_=tg_d[0:128, tsl]), writes=[R_('TG0')],
                     dma_key=R_('ldTG0'))
                P.op('sp', lambda e, tsl=tsl: e.dma_start(out=T['TG1'][0:32], in_=tg_d[128:160, tsl]),
                     writes=[R_('TG1')], dma_key=R_('ldTG1'))
                for hf in range(2):
                    hs = slice(hf * 512, (hf + 1) * 512)
                    ew('pe', lambda e, hs=hs, cols=cols: e.matmul(bank(1), lhsT=WA[0:64, cols], rhs=T['TA'][0:64, hs],
                                                                  start=True, stop=True), ['WA', 'TA'], ['pb1'])
                    ew('act', lambda e, hs=hs, hp=hp: e.activation(out=T['SG'][:, hs], in_=bank(1), func=AF.Sigmoid,
                                                                   bias=prm[:, 0, hp:hp + 1]), ['pb1', 'prm'], ['SG'])
                    ew('pe', lambda e, hs=hs, cols=cols: e.matmul(bank(2), lhsT=WA[64:128, cols], rhs=T['TA'][64:128, hs],
                                                                  start=True, stop=True), ['WA', 'TA'], ['pb2'])
                    ew('act', lambda e, hs=hs, hp=hp: e.activation(out=T['AA'][:, hs], in_=bank(2), func=AF.Sigmoid,
                                                                   bias=prm[:, 1, hp:hp + 1]), ['pb2', 'prm'], ['AA'])
                    ew('pe', lambda e, hs=hs, cols=cols: e.matmul(bank(3), lhsT=G2a[:, cols], rhs=T['TG0'][:, hs],
                                                                  start=True, stop=False), ['G2', 'TG0'], ['pb3', 'pb3'])
                    ew('pe', lambda e, hs=hs, cols=cols: e.matmul(bank(3), lhsT=G2b[0:32, cols], rhs=T['TG1'][0:32, hs],
                                                                  start=False, stop=True), ['G2', 'TG1'], ['pb3', 'pb3'])
                    ew('act', lambda e, hs=hs: e.copy(out=T['GG'][:, hs], in_=bank(3)), ['pb3', 'pb3'], ['GG'])
                ew('dve', lambda e, hp=hp: e.tensor_scalar(out=T['KK'], in0=T['K'], scalar1=prm[:, 2, hp:hp + 1],
                                                           scalar2=None, op0=ALU.mult), ['K', 'prm'], ['KK'])
                ew('act', lambda e: e.activation(out=T['TQ'], in_=T['KK'], func=AF.Square), ['KK'], ['TQ'])
                for hf in range(2):
                    hs = slice(hf * 512, (hf + 1) * 512)
                    ew('pe', lambda e, hs=hs: e.matmul(bank(4), lhsT=bones, rhs=T['TQ'][:, hs], start=True, stop=True),
                       ['bones', 'TQ'], ['pb4'])
                    ew('dve', lambda e, hs=hs: e.tensor_scalar(out=T['T1'][:, hs], in0=bank(4), scalar1=1e-19,
                                                               scalar2=None, op0=ALU.max), ['pb4'], ['T1'])
                ew('act', lambda e: e.activation(out=T['T1'], in_=T['T1'], func=AF.Ln), ['T1'], ['T1'])
                ew('act', lambda e: e.activation(out=T['T1'], in_=T['T1'], func=AF.Exp, scale=-0.5), ['T1'], ['T1'])
                ew('dve', lambda e: e.tensor_tensor(out=T['KK'], in0=T['KK'], in1=T['T1'], op=ALU.mult),
                   ['KK', 'T1'], ['KK'])
                ew('dve', lambda e, hp=hp: e.tensor_scalar(out=T['T1'], in0=T['AA'], scalar1=-1.0,
                                                           scalar2=prm[:, 3, hp:hp + 1], op0=ALU.add, op1=ALU.mult),
                   ['AA', 'prm'], ['T1'])
                ew('dve', lambda e: e.scalar_tensor_tensor(out=T['KM'], in0=T['T1'], scalar=1.0, in1=T['K'],
                                                           op0=ALU.add, op1=ALU.mult), ['T1', 'K'], ['KM'])
                ew('pool', lambda e: e.tensor_tensor(out=T['T1'], in0=T['R'], in1=T['KM'], op=ALU.mult),
                   ['R', 'KM'], ['T1'])
                ew('pool', lambda e, hp=hp: e.tensor_scalar(out=T['TQ'], in0=T['T1'], scalar1=prm[:, 4, hp:hp + 1],
                                                            scalar2=None, op0=ALU.mult), ['T1', 'prm'], ['TQ'])
                for hf in range(2):
                    hs = slice(hf * 512, (hf + 1) * 512)
                    ew('pe', lambda e, hs=hs: e.matmul(bank(5), lhsT=bones, rhs=T['TQ'][:, hs], start=True, stop=True),
                       ['bones', 'TQ'], ['pb5'])
                    ew('dve', lambda e, hs=hs: e.tensor_tensor(out=T['BV'][:, hs], in0=T['V'][:, hs], in1=bank(5),
                                                               op=ALU.mult), ['pb5', 'V'], ['BV'])
                ew('pool', lambda e: e.tensor_tensor(out=T['BVEC'], in0=T['KK'], in1=T['AA'], op=ALU.mult),
                   ['KK', 'AA'], ['BVEC'])
                ew('dve', lambda e: e.tensor_tensor_scan(out=T['CS'], data0=mrow, data1=T['SG'], initial=0.0,
                                                         op0=ALU.mult, op1=ALU.add), ['mrow', 'SG'], ['CS'])
                ew('pool', lambda e: e.tensor_tensor(out=T['T2'], in0=T['CS'], in1=T['SG'], op=ALU.subtract),
                   ['CS', 'SG'], ['T2'])
                ew('act', lambda e: e.activation(out=T['EP'], in_=T['CS'], func=AF.Exp, scale=-C0), ['CS'], ['EP'])
                ew('act', lambda e: e.activation(out=T['EN'], in_=T['CS'], func=AF.Exp, scale=C0), ['CS'], ['EN'])
                ew('act', lambda e: e.activation(out=T['EPM'], in_=T['T2'], func=AF.Exp, scale=-C0), ['T2'], ['EPM'])
                ew('dve', lambda e: e.tensor_tensor(
                    out=ch3(T['T3']), in0=ch3(T['CS']), in1=ch3(T['CS'])[:, :, 127:128].to_broadcast([128, NCH, 128]),
                    op=ALU.subtract), ['CS'], ['T3'])
                ew('act', lambda e: e.activation(out=T['EC'], in_=T['T3'], func=AF.Exp, scale=C0), ['T3'], ['EC'])
                ew('act', lambda e: e.activation(out=PC.rearrange("p (c o) -> p c o", o=1),
                                                 in_=ch3(T['CS'])[:, :, 127:128], func=AF.Exp, scale=-C0),
                   ['CS'], ['PC'])
                ew('dve', lambda e: e.scalar_tensor_tensor(out=AR[:, :, 0, :], in0=ch3(T['EPM']), scalar=-1.0,
                                                           in1=ch3(T['KK']), op0=ALU.mult, op1=ALU.mult),
                   ['EPM', 'KK'], ['AR0'])
                ew('pool', lambda e: e.tensor_tensor(out=AR[:, :, 1, :], in0=ch3(T['EP']), in1=ch3(T['R']), op=ALU.mult),
                   ['EP', 'R'], ['AR1'])
                ew('dve', lambda e: e.tensor_tensor(out=T['BT'], in0=T['EN'], in1=T['BVEC'], op=ALU.mult),
                   ['EN', 'BVEC'], ['BT'])
                ew('pool', lambda e: e.tensor_tensor(out=T['KT'], in0=T['EN'], in1=T['KM'], op=ALU.mult),
                   ['EN', 'KM'], ['KT'])
                ew('dve', lambda e: e.tensor_tensor(out=T['BH'], in0=T['EC'], in1=T['BVEC'], op=ALU.mult),
                   ['EC', 'BVEC'], ['BH'])
                ew('pool', lambda e: e.tensor_tensor(out=T['KH'], in0=T['EC'], in1=T['KM'], op=ALU.mult),
                   ['EC', 'KM'], ['KH'])
                ew('act', lambda e: e.copy(out=T['VT'], in_=T['V']), ['V'], ['VT'])
                pT = bank(0).bitcast(BF16)
                for c in range(NCH):
                    cs_ = slice(c * 128, (c + 1) * 128)
                    srcs = [(AR[:, c, 0, :], 'AR0'), (T['VT'][:, cs_], 'VT'), (T['BH'][:, cs_], 'BH'),
                            (T['KH'][:, cs_], 'KH')]
                    for q, (sap, sr) in enumerate(srcs):
                        ew('pe', lambda e, q=q, sap=sap: e.transpose(out=pT[:, q * 128:(q + 1) * 128], in_=sap,
                                                                     identity=ident), [sr, 'ident'], ['pb0'])
                    ew('act', lambda e, c=c: e.copy(out=TM4[:, c, :, :], in_=pT[:, 0:512].rearrange("p (q t) -> p q t", q=4)),
                       ['pb0'], ['TM4_%d' % c])
                for c in range(NCH):
                    cs_ = slice(c * 128, (c + 1) * 128)
                    tm = 'TM4_%d' % c
                    for h2 in range(2):
                        pb = 64 * h2
                        psl = slice(pb, pb + 64)
                        ps1 = bank(1 + h2)
                        arc = AR[:, c, :, :].rearrange("p a t -> p (a t)")
                        ew('pe', lambda e, ps1=ps1, psl=psl, cs_=cs_, arc=arc: e.matmul(
                            ps1[:, 0:256], lhsT=T['BT'][psl, cs_], rhs=arc[psl, :], start=True, stop=True),
                            ['BT', 'AR0', 'AR1'], ['pb%d' % (1 + h2)])
                        ew('pe', lambda e, ps1=ps1, psl=psl, cs_=cs_, arc=arc: e.matmul(
                            ps1[:, 256:512], lhsT=T['KT'][psl, cs_], rhs=arc[psl, :], start=True, stop=True),
                            ['KT', 'AR0', 'AR1'], ['pb%d' % (1 + h2)])
                        ew('dve', lambda e, ps1=ps1, h2=h2: e.tensor_tensor(out=SC[h2], in0=mk4, in1=ps1, op=ALU.mult),
                           ['pb%d' % (1 + h2), 'mk4'], ['SC%d' % h2])
                        ew('pe', lambda e, h2=h2, psl=psl, cs_=cs_, c=c: e.matmul(
                            bank(4 + h2)[:, 384:512], lhsT=AR[psl, c, 0, :], rhs=T['BT'][psl, cs_],
                            start=True, stop=True), ['AR0', 'BT'], ['pb%d' % (4 + h2)])
                    for h2 in range(2):
                        ew('dve', lambda e, h2=h2: e.tensor_tensor(out=LZ[0][:, h2, 0:128], in0=mkL[:, h2, :],
                                                                   in1=bank(4 + h2)[:, 384:512], op=ALU.mult),
                           ['pb%d' % (4 + h2), 'mkL'], ['LL0_%d' % h2])
                    for h2 in range(2):
                        pb = 64 * h2
                        ew('pe', lambda e, h2=h2, pb=pb, c=c: e.matmul(
                            bank(3)[:, 256 + h2 * 64:256 + (h2 + 1) * 64], lhsT=SC[h2][:, 256:384],
                            rhs=TM4[:, c, 1, pb:pb + 64], start=True, stop=True), ['SC%d' % h2, tm], ['pb3'])
                    ew('pool', lambda e, c=c: e.tensor_copy(out=LZ[0][:, :, 128:192],
                                                            in_=TM4[:, c, 0, :].rearrange("p (h k) -> p h k", h=2)),
                       [tm], ['ZZ0_0', 'ZZ0_1'])
                    for h2 in range(2):
                        ew('act', lambda e, h2=h2: e.copy(out=LZ[0][:, h2, 192:256],
                                                          in_=bank(3)[:, 256 + h2 * 64:256 + (h2 + 1) * 64]),
                           ['pb3'], ['ZZ0_%d' % h2])
                    for n in range(7):
                        pp = n % 2
                        for h2 in range(2):
                            ps2 = bank(4 + h2)
                            pbr = 'pb%d' % (4 + h2)
                            ltn = SC[h2][:, 0:128] if n == 0 else LZ[pp][:, h2, 256:384]
                            rds = ['LL%d_%d' % (pp, h2), 'ZZ%d_%d' % (pp, h2)] + (['SC%d' % h2] if n == 0 else [])
                            if n < 6:
                                ew('pe', lambda e, ps2=ps2, ltn=ltn, pp=pp, h2=h2: e.matmul(
                                    ps2[:, 0:256], lhsT=ltn, rhs=LZ[pp][:, h2, 0:256], start=True, stop=True),
                                    rds, [pbr])
                                ew('pe', lambda e, ps2=ps2, ltn=ltn, pp=pp, h2=h2: e.matmul(
                                    ps2[:, 256:384], lhsT=LZ[pp][:, h2, 0:128], rhs=ltn, start=True, stop=True),
                                    rds, [pbr])
                            else:
                                ew('pe', lambda e, ps2=ps2, ltn=ltn, pp=pp, h2=h2: e.matmul(
                                    ps2[:, 128:256], lhsT=ltn, rhs=LZ[pp][:, h2, 128:256], start=True, stop=True),
                                    rds, [pbr])

                        def cp(h2, pp=pp):
                            ps2 = bank(4 + h2)
                            ew('act', lambda e, pp=pp, h2=h2, ps2=ps2: e.copy(
                                out=LZ[1 - pp][:, h2, :].rearrange("p (s t) -> p s t", s=3)[:, 0:3:2, :],
                                in_=ps2[:, 0:384].rearrange("p (s t) -> p s t", s=3)[:, 0:3:2, :]),
                                ['pb%d' % (4 + h2)], ['LL%d_%d' % (1 - pp, h2)])

                        def ad(h2, pp=pp):
                            ps2 = bank(4 + h2)
                            ew('dve', lambda e, pp=pp, h2=h2, ps2=ps2: e.tensor_tensor(
                                out=LZ[1 - pp][:, h2, 128:256], in0=LZ[pp][:, h2, 128:256], in1=ps2[:, 128:256],
                                op=ALU.add), ['pb%d' % (4 + h2), 'ZZ%d_%d' % (pp, h2)],
                                ['ZZ%d_%d' % (1 - pp, h2)])
                        if n < 6:
                            cp(0)
                            ad(1)
                            cp(1)
                            ad(0)
                        else:
                            ad(0)
                            ad(1)
                    ZF = LZ[1]
                    for h2 in range(2):
                        pb = 64 * h2
                        psl = slice(pb, pb + 64)
                        ew('pe', lambda e, h2=h2, pb=pb, psl=psl, c=c: e.matmul(
                            bank(6)[psl, 0:64], lhsT=ZF[:, h2, 128:192], rhs=TM4[:, c, 2, pb:pb + 64],
                            start=True, stop=True, tile_position=(0, pb)), ['ZZ1_%d' % h2, tm], ['pb6'])
                        ew('pe', lambda e, h2=h2, pb=pb, psl=psl: e.matmul(
                            bank(6)[psl, 64:192], lhsT=ZF[:, h2, 128:192], rhs=SC[h2][:, 128:256],
                            start=True, stop=True, tile_position=(0, pb)), ['ZZ1_%d' % h2, 'SC%d' % h2], ['pb6'])
                    for h2 in range(2):
                        psl = slice(64 * h2, 64 * h2 + 64)
                        ew('dve', lambda e, c=c, psl=psl, h2=h2: e.scalar_tensor_tensor(
                            out=MCz[psl, h2, :], in0=E2[psl, :], scalar=PC[psl, c:c + 1], in1=bank(6)[psl, 0:64],
                            op0=ALU.mult, op1=ALU.add), ['E2', 'PC', 'pb6'], ['MC'])
                    ew('dve', lambda e, c=c: e.tensor_tensor(out=QT, in0=AR[:, c, 1, :], in1=bank(6)[:, 64:192],
                                                             op=ALU.add), ['pb6', 'AR1'], ['QT'])
                    for h2 in range(2):
                        pb = 64 * h2
                        psl = slice(pb, pb + 64)
                        sb_ = 3 if h2 == 0 else 0
                        sr_ = 'pb%d' % sb_
                        psY = bank(sb_)[:, 0:128]
                        psS = bank(sb_)[:, 128:192]
                        UU = ZF[:, h2, 192:256]
                        ew('pe', lambda e, psl=psl, pb=pb, h2=h2, UU=UU, psY=psY: e.matmul(
                            psY[psl, :], lhsT=UU, rhs=SC[h2][:, 128:256], start=True, stop=False,
                            tile_position=(0, pb)), ['ZZ1_%d' % h2, 'SC%d' % h2], [sr_])
                        ew('pe', lambda e, psl=psl, pb=pb, h2=h2, c=c, psY=psY: e.matmul(
                            psY[psl, :], lhsT=TM4[:, c, 1, pb:pb + 64], rhs=SC[h2][:, 384:512], start=False, stop=False,
                            tile_position=(0, pb)), [tm, 'SC%d' % h2], [sr_])
                        ew('pe', lambda e, psl=psl, pb=pb, psY=psY, h2=h2: e.matmul(
                            psY[psl, :], lhsT=STz[:, h2, :], rhs=QT, start=False, stop=True,
                            tile_position=(0, pb)), ['ST', 'QT'], [sr_])
                        ew('pe', lambda e, psl=psl, pb=pb, psS=psS, h2=h2: e.matmul(
                            psS[psl, :], lhsT=MCz[:, h2, :], rhs=STz[:, h2, :], start=True, stop=False,
                            tile_position=(0, pb)), ['MC', 'ST'], [sr_])
                        ew('pe', lambda e, psl=psl, pb=pb, c=c, UU=UU, psS=psS: e.matmul(
                            psS[psl, :], lhsT=TM4[:, c, 2, pb:pb + 64], rhs=UU, start=False, stop=False,
                            tile_position=(0, pb)), [tm, 'ZZ1_%d' % h2], [sr_])
                        ew('pe', lambda e, psl=psl, pb=pb, c=c, psS=psS: e.matmul(
                            psS[psl, :], lhsT=TM4[:, c, 3, pb:pb + 64], rhs=TM4[:, c, 1, pb:pb + 64], start=False,
                            stop=True, tile_position=(0, pb)), [tm], [sr_])
                    for h2 in range(2):
                        pb = 64 * h2
                        psl = slice(pb, pb + 64)
                        sb_ = 3 if h2 == 0 else 0
                        sr_ = 'pb%d' % sb_
                        ew('act', lambda e, cs_=cs_, psl=psl, sb_=sb_: e.copy(out=T['Y32'][psl, cs_],
                                                                             in_=bank(sb_)[psl, 0:128]),
                           [sr_], ['Y32'])
                        ew('dve', lambda e, psl=psl, sb_=sb_, h2=h2: e.tensor_copy(out=STz[psl, h2, :],
                                                                                   in_=bank(sb_)[psl, 128:192]),
                           [sr_], ['ST'])
                ew('pool', lambda e: e.tensor_copy(out=T['YB'], in_=T['Y32']), ['Y32'], ['YB'])
                for hf in range(2):
                    hs = slice(hf * 512, (hf + 1) * 512)
                    ew('pe', lambda e, hs=hs: e.matmul(bank(1), lhsT=bones, rhs=T['YB'][:, hs], start=True, stop=True),
                       ['bones', 'YB'], ['pb1'])
                    ew('dve', lambda e, hs=hs: e.scalar_tensor_tensor(out=T['DD'][:, hs], in0=bank(1), scalar=-1.0 / 64,
                                                                      in1=T['Y32'][:, hs], op0=ALU.mult, op1=ALU.add),
                       ['pb1', 'Y32'], ['DD'])
                ew('act', lambda e: e.activation(out=T['TQ'], in_=T['DD'], func=AF.Square), ['DD'], ['TQ'])
                for hf in range(2):
                    hs = slice(hf * 512, (hf + 1) * 512)
                    ew('pe', lambda e, hs=hs: e.matmul(bank(2), lhsT=bones, rhs=T['TQ'][:, hs], start=True, stop=True),
                       ['bones', 'TQ'], ['pb2'])
                    ew('act', lambda e, hs=hs: e.activation(out=T['T1'][:, hs], in_=bank(2), func=AF.Ln, scale=1.0 / 64,
                                                            bias=GN_EPS), ['pb2'], ['T1'])
                ew('act', lambda e: e.activation(out=T['T1'], in_=T['T1'], func=AF.Exp, scale=-0.5), ['T1'], ['T1'])
                ew('dve', lambda e: e.tensor_tensor(out=T['DD'], in0=T['DD'], in1=T['T1'], op=ALU.mult),
                   ['DD', 'T1'], ['DD'])
                ew('dve', lambda e, hp=hp: e.tensor_scalar(out=T['DD'], in0=T['DD'], scalar1=prm[:, 5, hp:hp + 1],
                                                           scalar2=prm[:, 6, hp:hp + 1], op0=ALU.mult, op1=ALU.add),
                   ['DD', 'prm'], ['DD'])
                ew('pool', lambda e: e.tensor_tensor(out=T['DD'], in0=T['DD'], in1=T['BV'], op=ALU.add),
                   ['DD', 'BV'], ['DD'])
                ew('dve', lambda e: e.tensor_tensor(out=T['YO'], in0=T['DD'], in1=T['GG'], op=ALU.mult),
                   ['DD', 'GG'], ['YO'])
                P.op('sp', lambda e, hp=hp, tsl=tsl: e.dma_start(out=ya_d[hp, :, tsl], in_=T['YO']),
                     reads=[R_('YO')], writes=['d_ya'], dma_key=R_('stYO'))

    yb_d = nc.dram_tensor("yb_scr", [6, 128, S], BF16, kind="ExternalOutput" if debug else "Internal").ap()
    DIL = (1, 4, 16)

    def attn_stage_full(js=range(4)):
        A.off = const_mark
        pre = 'a_'

        def R_(n):
            return pre + n

        def nm(x):
            if x.startswith('pb') or x in ('ident', 'ident_f'):
                return x
            return R_(x)

        def ew(eng, fn, reads, writes):
            P.op(eng, fn, reads=[nm(x) for x in reads], writes=[nm(x) for x in writes])
        QH = A.alloc([S], BF16)
        KH = A.alloc([S], BF16)
        VX = A.alloc([32, 64], BF16)
        ONES = A.alloc([64], BF16)
        OT = [A.alloc([S], F32) for _ in range(3)]
        DEN = [A.alloc([S], F32) for _ in range(3)]
        PT = [A.alloc([256], BF16) for _ in range(4)]
        mask2 = A.alloc([256], BF16)
        RD = [A.alloc([512], F32) for _ in range(2)]
        YBS = [A.alloc([512], BF16) for _ in range(2)]
        print("attn_stage arena", A.off)
        P.op('pool', lambda e: e.memset(mask2, 1.0), writes=[R_('mask')])
        P.op('pool', lambda e: e.affine_select(out=mask2[:, 0:128], in_=mask2[:, 0:128], pattern=[[1, 128]],
                                               compare_op=ALU.is_ge, fill=0.0, base=0, channel_multiplier=-1),
             reads=[R_('mask')], writes=[R_('mask')])
        P.op('pool', lambda e: e.affine_select(out=mask2[:, 128:256], in_=mask2[:, 128:256], pattern=[[-1, 128]],
                                               compare_op=ALU.is_ge, fill=0.0, base=0, channel_multiplier=1),
             reads=[R_('mask')], writes=[R_('mask')])
        P.op('pool', lambda e: e.memset(ONES, 1.0), writes=[R_('ONES')])

        tcount = [0]
        ccount = [0]
        for j in js:
            for g in range(3):
                d = DIL[g]
                nb = S // d // 128
                h = 4 * g + j
                pair = h // 2
                pb = 64 * (h % 2)
                vv = v_d.rearrange("(m d) c -> d m c", d=d)
                for r in range(d):
                    for n0 in range(0, nb, 8):
                        n1 = min(nb, n0 + 8)
                        P.op('sp', lambda e, r=r, h=h, nb=nb, vv=vv, n0=n0, n1=n1: e.dma_start(
                            out=VX[:, r * nb + n0:r * nb + n1, :],
                            in_=vv[r, n0 * 128:n1 * 128, h * 64:(h + 1) * 64].rearrange("(n i) c -> i n c", i=128)),
                            writes=[R_('VX')], dma_key=R_('ldV'))
                P.op('sp', lambda e, pair=pair, pb=pb: e.dma_start(out=QH[0:64, :], in_=qk_d[pair, pb:pb + 64, :]),
                     writes=[R_('QH')], dma_key=R_('ldQ'))
                P.op('sp', lambda e, pair=pair, pb=pb: e.dma_start(out=KH[0:64, :], in_=qk_d[6 + pair, pb:pb + 64, :]),
                     writes=[R_('KH')], dma_key=R_('ldK'))
                qv = QH.rearrange("p (m d) -> p d m", d=d)
                kv = KH.rearrange("p (m d) -> p d m", d=d)
                otr = 'OT%d' % g
                otv = OT[g].rearrange("p (m d) -> p d m", d=d)
                dnv = DEN[g].rearrange("p (m d) -> p d m", d=d)
                for r in range(d):
                    prev = None
                    for n in range(nb):
                        ti = tcount[0]
                        tcount[0] += 1
                        nq = 256 if n + 1 < nb else 128
                        sbk = 1 + ti % 2
                        ps = bank(sbk)[:, 0:nq]
                        pt = PT[ti % 4]
                        ptr = 'PT%d' % (ti % 4)
                        ew('pe', lambda e, ps=ps, kv=kv, qv=qv, r=r, n=n, nq=nq: e.matmul(
                            ps, lhsT=kv[0:64, r, 128 * n:128 * n + 128], rhs=qv[0:64, r, 128 * n:128 * n + nq],
                            start=True, stop=True), ['KH', 'QH'], ['pb%d' % sbk])
                        ew('act', lambda e, ps=ps, pt=pt, nq=nq: e.activation(out=pt[:, 0:nq], in_=ps, func=AF.Exp),
                           ['pb%d' % sbk], [ptr])
                        ew('pool', lambda e, pt=pt, nq=nq: e.tensor_tensor(out=pt[:, 0:nq], in0=pt[:, 0:nq],
                                                                          in1=mask2[:, 0:nq], op=ALU.mult),
                           [ptr, 'mask'], [ptr])
                        obk = 3 + ti % 2
                        po = bank(obk)[0:64, 0:128]
                        pdn = bank(obk)[0:64, 128:256]
                        vt = r * nb + n
                        if prev is not None:
                            ppt, pptr, pvt = prev
                            ew('pe', lambda e, po=po, ppt=ppt, pvt=pvt: e.matmul(
                                po, lhsT=VX[:, pvt, :], rhs=ppt[:, 128:256], start=True, stop=False),
                                ['VX', pptr], ['pb%d' % obk])
                        ew('pe', lambda e, po=po, pt=pt, vt=vt, first=(prev is None): e.matmul(
                            po, lhsT=VX[:, vt, :], rhs=pt[:, 0:128], start=first, stop=True),
                            ['VX', ptr], ['pb%d' % obk])
                        if prev is not None:
                            ew('pe', lambda e, pdn=pdn, ppt=ppt: e.matmul(
                                pdn, lhsT=ONES, rhs=ppt[:, 128:256], start=True, stop=False),
                                ['ONES', pptr], ['pb%d' % obk])
                        ew('pe', lambda e, pdn=pdn, pt=pt, first=(prev is None): e.matmul(
                            pdn, lhsT=ONES, rhs=pt[:, 0:128], start=first, stop=True),
                            ['ONES', ptr], ['pb%d' % obk])
                        ew('dve', lambda e, po=po, otv=otv, r=r, n=n: e.tensor_copy(
                            out=otv[0:64, r, 128 * n:128 * n + 128], in_=po), ['pb%d' % obk], [otr])
                        ew('dve', lambda e, pdn=pdn, dnv=dnv, r=r, n=n: e.tensor_copy(
                            out=dnv[0:64, r, 128 * n:128 * n + 128], in_=pdn), ['pb%d' % obk], ['DEN%d' % g])
                        prev = (pt, ptr, vt)
            for ck in range(S // 512):
                csl = slice(ck * 512, (ck + 1) * 512)
                cc = ccount[0]
                ccount[0] += 1
                rd = RD[cc % 2]
                ew('pool', lambda e, rd=rd, csl=csl: e.tensor_tensor(out=rd[0:64, :], in0=DEN[0][0:64, csl],
                                                                     in1=DEN[1][0:64, csl], op=ALU.add),
                   ['DEN0', 'DEN1'], ['RD%d' % (cc % 2)])
                ew('pool', lambda e, rd=rd, csl=csl: e.tensor_tensor(out=rd[0:64, :], in0=rd[0:64, :],
                                                                     in1=DEN[2][0:64, csl], op=ALU.add),
                   ['RD%d' % (cc % 2), 'DEN2'], ['RD%d' % (cc % 2)])
                ew('dve', lambda e, rd=rd: e.reciprocal(out=rd[0:64, :], in_=rd[0:64, :]), ['RD%d' % (cc % 2)],
                   ['RD%d' % (cc % 2)])
                for g in range(3):
                    h = 4 * g + j
                    pair = h // 2
                    pb = 64 * (h % 2)
                    yi = (cc * 3 + g) % 2
                    ys = YBS[yi]
                    ew('pool', lambda e, g=g, csl=csl, rd=rd, ys=ys: e.tensor_tensor(
                        out=ys[0:64, :], in0=OT[g][0:64, csl], in1=rd[0:64, :], op=ALU.mult),
                        ['OT%d' % g, 'RD%d' % (cc % 2)], ['YBS%d' % yi])
                    P.op('sp', lambda e, pair=pair, pb=pb, csl=csl, ys=ys: e.dma_start(
                        out=yb_d[pair, pb:pb + 64, csl], in_=ys[0:64, :]),
                        reads=[R_('YBS%d' % yi)], writes=['d_yb'], dma_key=R_('stYB%d' % yi))

    wpr_d = din("w_proj_rwkv", [1024, 1024])
    wpa_d = din("w_proj_attn", [768, 1024])
    wo_d = din("w_out", [1024, 1024])
    x2_d = nc.dram_tensor("x2_scr", [S, D], F32, kind="ExternalOutput" if debug else "Internal").ap()

    def merge_stage():
        A.off = const_mark
        pre = 'm_'

        def R_(n):
            return pre + n

        def nm(x):
            if x.startswith('pb'):
                return x
            return R_(x)

        def ew(eng, fn, reads, writes):
            P.op(eng, fn, reads=[nm(x) for x in reads], writes=[nm(x) for x in writes])
        Wr = A.alloc([8, 1024], BF16)
        Wa = A.alloc([6, 1024], BF16)
        Wo = A.alloc([8, 1024], BF16)
        X = [A.alloc([NSUB, D], F32) for _ in range(2)]
        YA = [A.alloc([8, TT], BF16) for _ in range(2)]
        YB = [A.alloc([6, TT], BF16) for _ in range(2)]
        G = [A.alloc([16, TT], BF16) for _ in range(2)]
        MT = A.alloc([8, TT], BF16)
        t1 = [A.alloc([TT], F32) for _ in range(2)]
        t2 = [A.alloc([TT], F32) for _ in range(2)]
        print("merge_stage arena", A.off)
        for kc in range(8):
            P.op('pool', lambda e, kc=kc: e.dma_start(out=Wr[:, kc, :], in_=wpr_d[kc * 128:(kc + 1) * 128, :]),
                 writes=[R_('Wr')], dma_key=R_('w'))
            P.op('pool', lambda e, kc=kc: e.dma_start(out=Wo[:, kc, :], in_=wo_d[kc * 128:(kc + 1) * 128, :]),
                 writes=[R_('Wo')], dma_key=R_('w'))
        for kc in range(6):
            P.op('pool', lambda e, kc=kc: e.dma_start(out=Wa[:, kc, :], in_=wpa_d[kc * 128:(kc + 1) * 128, :]),
                 writes=[R_('Wa')], dma_key=R_('w'))
        srcv = x1_d.rearrange("(n s p) d -> n p s d", p=128, s=NSUB)
        dstv = x2_d.rearrange("(n s p) d -> n p s d", p=128, s=NSUB)
        for it in range(NT):
            sl = it % 2
            tsl = slice(it * TT, (it + 1) * TT)
            sfx = '%d' % sl
            P.op('sp', lambda e, it=it, sl=sl: e.dma_start(out=X[sl], in_=srcv[it]), writes=[R_('X' + sfx)],
                 dma_key=R_('ldX' + sfx))
            P.op('sp', lambda e, sl=sl, tsl=tsl: e.dma_start(out=YA[sl], in_=ya_d.rearrange("b p t -> p b t")[:, :, tsl]),
                 writes=[R_('YA' + sfx)], dma_key=R_('ldYA' + sfx))
            P.op('sp', lambda e, sl=sl, tsl=tsl: e.dma_start(out=YB[sl], in_=yb_d.rearrange("b p t -> p b t")[:, :, tsl]),
                 writes=[R_('YB' + sfx)], dma_key=R_('ldYB' + sfx))
            P.op('sp', lambda e, sl=sl, tsl=tsl: e.dma_start(out=G[sl], in_=gate_d.rearrange("b p t -> p b t")[:, :, tsl]),
                 writes=[R_('G' + sfx)], dma_key=R_('ldG' + sfx))
            for c in range(8):
                bk = 1 + c % 2
                pg = bank(bk)
                for kc in range(8):
                    ew('pe', lambda e, c=c, kc=kc, pg=pg, sl=sl: e.matmul(
                        pg[:, 0:TT], lhsT=Wr[:, kc, c * 128:(c + 1) * 128], rhs=YA[sl][:, kc, :],
                        start=(kc == 0), stop=(kc == 7)), ['Wr', 'YA' + sfx], ['pb%d' % bk])
                for kc in range(6):
                    ew('pe', lambda e, c=c, kc=kc, pg=pg, sl=sl: e.matmul(
                        pg[:, TT:2 * TT], lhsT=Wa[:, kc, c * 128:(c + 1) * 128], rhs=YB[sl][:, kc, :],
                        start=(kc == 0), stop=(kc == 5)), ['Wa', 'YB' + sfx], ['pb%d' % bk])
                q = c % 2
                ew('dve', lambda e, c=c, pg=pg, sl=sl, q=q: e.tensor_tensor(out=t1[q], in0=G[sl][:, c, :],
                                                                           in1=pg[:, 0:TT], op=ALU.mult),
                   ['G' + sfx, 'pb%d' % bk], ['t1%d' % q])
                ew('dve', lambda e, c=c, pg=pg, sl=sl, q=q: e.tensor_tensor(out=t2[q], in0=G[sl][:, 8 + c, :],
                                                                           in1=pg[:, TT:2 * TT], op=ALU.mult),
                   ['G' + sfx, 'pb%d' % bk], ['t2%d' % q])
                ew('pool', lambda e, c=c, q=q: e.tensor_tensor(out=MT[:, c, :], in0=t1[q], in1=t2[q], op=ALU.add),
                   ['t1%d' % q, 't2%d' % q], ['MT'])
            for s in range(NSUB):
                for dh in range(2):
                    bk = 3 + (s * 2 + dh) % 2
                    pd = bank(bk)
                    for c in range(8):
                        ew('pe', lambda e, c=c, s=s, dh=dh, pd=pd: e.matmul(
                            pd, lhsT=MT[:, c, s * 128:(s + 1) * 128], rhs=Wo[:, c, dh * 512:(dh + 1) * 512],
                            start=(c == 0), stop=(c == 7)), ['MT', 'Wo'], ['pb%d' % bk])
                    ew('dve', lambda e, s=s, dh=dh, pd=pd, sl=sl: e.tensor_tensor(
                        out=X[sl][:, s, dh * 512:(dh + 1) * 512], in0=X[sl][:, s, dh * 512:(dh + 1) * 512], in1=pd,
                        op=ALU.add), ['pb%d' % bk, 'X' + sfx], ['X' + sfx])
            P.op('sp', lambda e, it=it, sl=sl: e.dma_start(out=dstv[it], in_=X[sl]),
                 reads=[R_('X' + sfx)], writes=['d_x2'], dma_key=R_('stX' + sfx))

    if 'ffn1' in stages:
        ffn_stage(0, x, x1_d)
        P.sync_all()
    if 'proj' in stages:
        proj_stage()
        P.sync_all()
    if 'rwkv' in stages:
        rwkv_stage(range(NPAIRS_DBG))
        P.sync_all()
    if 'attn' in stages:
        attn_stage_full(JS_DBG)
        P.sync_all()
    if 'merge' in stages:
        merge_stage()
        P.sync_all()
    if 'ffn2' in stages:
        ffn_stage(1, x2_d if 'merge' in stages else x1_d, out)
    P.sync_all()
    P.op('sp', None)
    P.finalize_and_emit(stack)
    stack.close()
    return nc


_CACHE = {}


SHARED_KEYS = ['ffn1_norm', 'ffn1_w_in', 'ffn1_w_out', 'ffn2_norm', 'ffn2_w_in', 'ffn2_w_out',
               'w_in', 'mix_norm', 'rwkv_mu', 'b_gate', 'attn_q_norm', 'attn_k_norm',
               'w_proj_rwkv', 'w_proj_attn', 'w_out', 'rwkv_w2', 'rwkv_a2', 'rwkv_g2', 'rwkv_w0', 'rwkv_a0', 'rwkv_k_k', 'rwkv_k_a', 'rwkv_r_k', 'rwkv_ln_w', 'rwkv_ln_b']


def make_shared(inputs):
    shared = {}
    for k in SHARED_KEYS:
        v = np.asarray(inputs[k], dtype=np.float32)
        v = v.reshape(v.shape[1:])
        if k == 'rwkv_r_k':
            v = v.reshape(-1)
        shared[k] = np.ascontiguousarray(v)
    return shared


def kernel(**inputs):
    if 'nc' not in _CACHE:
        _CACHE['nc'] = build_program()
    nc = _CACHE['nc']
    x = np.ascontiguousarray(inputs['x'], dtype=np.float32)
    shared = make_shared(inputs)
    in_maps = []
    for c in range(NCORES):
        m = dict(shared)
        m['x'] = x[c]
        in_maps.append(m)
    res = run_bass_kernel_spmd(nc, in_maps, core_ids=list(range(NCORES)))
    return np.stack([np.asarray(r['out']) for r in res.results], axis=0)
```

```python
import numpy as np
from contextlib import ExitStack
import concourse.bass as bass
import concourse.mybir as mybir
from concourse.bass_utils import run_bass_kernel_spmd
from concourse.alu_op_type import AluOpType as ALU

F32 = mybir.dt.float32
BF16 = mybir.dt.bfloat16
AF = mybir.ActivationFunctionType
AX = mybir.AxisListType

S = 4096
D = 1024
DFF = 2816
NCORES = 8
RMS_EPS = 1e-6

ENGS = ['pe', 'act', 'dve', 'pool', 'sp']
MAXOPS = [0]
SEM_LIM = 30000
DMA_LIM = 1800


class Prog:
    def __init__(self, nc):
        self.nc = nc
        self.ops = []
        self.eng_ops = {e: [] for e in ENGS}
        self.last_w = {}
        self.readers = {}
        self.dma_cnt = {}
        self.barrier = {e: None for e in ENGS}

    def op(self, eng, fn, reads=(), writes=(), dma_key=None):
        mo = MAXOPS[0]
        if mo and len(self.ops) >= mo and fn is not None:
            return None
        if mo and len(self.ops) == mo - 1 and fn is not None:
            print("LAST OP:", eng, fn.__code__.co_firstlineno, reads, writes)
        oid = len(self.ops)
        deps = set()
        dma_deps = {}
        writes = list(writes) + [r for r in reads if (r.startswith('pb') or r.startswith('ps')) and r not in writes]

        def add(o):
            od = self.ops[o]
            if od['dma_key'] is not None:
                k = od['dma_key']
                dma_deps[k] = self.dma_cnt[k]
            else:
                deps.add(o)
        for r in reads:
            if r in self.last_w:
                add(self.last_w[r])
        for w in writes:
            if w in self.last_w:
                add(self.last_w[w])
            for rd in self.readers.get(w, {}).values():
                add(rd)
        if self.barrier[eng] is not None:
            bd, bdma = self.barrier[eng]
            for o in bd:
                deps.add(o)
            for k, v in bdma.items():
                dma_deps[k] = max(dma_deps.get(k, 0), v)
            self.barrier[eng] = None
        cnt = None
        if dma_key is not None:
            self.dma_cnt[dma_key] = self.dma_cnt.get(dma_key, 0) + 1
            cnt = self.dma_cnt[dma_key]
        o = dict(id=oid, eng=eng, fn=fn, deps=deps, dma_deps=dma_deps, dma_key=dma_key,
                 dma_cnt=cnt, idx=len(self.eng_ops[eng]), sig=False)
        self.ops.append(o)
        self.eng_ops[eng].append(o)
        ch = eng if dma_key is None else 'dma:' + dma_key
        for r in reads:
            self.readers.setdefault(r, {})[ch] = oid
        for w in writes:
            self.last_w[w] = oid
            self.readers[w] = {}
        return oid

    def sync_all(self):
        bd = set()
        for e in ENGS:
            for o in reversed(self.eng_ops[e]):
                if o['dma_key'] is None and o['fn'] is not None:
                    bd.add(o['id'])
                    break
        bdma = dict(self.dma_cnt)
        for e in ENGS:
            self.barrier[e] = (set(bd), dict(bdma))

    def finalize_and_emit(self, stack):
        nc = self.nc
        for o in self.ops:
            per = {}
            for d in o['deps']:
                od = self.ops[d]
                if od['eng'] == 'pe' and o['eng'] == 'pe':
                    continue
                e = od['eng']
                if e not in per or self.ops[per[e]]['idx'] < od['idx']:
                    per[e] = d
            o['cdeps'] = per
            for d in per.values():
                self.ops[d]['sig'] = True
        sems = {}

        def get_sem(name):
            return sems[name]
        for e in ENGS:
            c = 0
            for o in self.eng_ops[e]:
                if o['dma_key'] is None and o['sig']:
                    c += 1
                    o['sigval'] = c
        for o in self.ops:
            waits = {}
            for e, d in o['cdeps'].items():
                v = self.ops[d]['sigval']
                key = ('c_%s_%d' % (e, (v - 1) // SEM_LIM))
                val = (v - 1) % SEM_LIM + 1
                waits[key] = max(waits.get(key, 0), val)
            for k, n in o['dma_deps'].items():
                key = ('d_%s_%d' % (k, (n - 1) // DMA_LIM))
                val = 16 * ((n - 1) % DMA_LIM + 1)
                waits[key] = max(waits.get(key, 0), val)
            o['waits'] = waits
        names = set()
        for o in self.ops:
            names.update(o['waits'].keys())
            if o['dma_key'] is not None:
                names.add('d_%s_%d' % (o['dma_key'], (o['dma_cnt'] - 1) // DMA_LIM))
            elif o['sig']:
                names.add('c_%s_%d' % (o['eng'], (o['sigval'] - 1) // SEM_LIM))
        for nm in sorted(names):
            sems[nm] = stack.enter_context(nc.semaphore(nm))
        print("n_sems", len(names), "n_ops", len(self.ops), {e: len(v) for e, v in self.eng_ops.items()})
        block = stack.enter_context(nc.Block())
        decos = {'pe': block.tensor, 'act': block.scalar, 'dve': block.vector,
                 'pool': block.gpsimd, 'sp': block.sync}
        for e in ENGS:
            ops = self.eng_ops[e]

            def body(eng, ops=ops, e=e):
                waited = {}
                for o in ops:
                    for key, val in o['waits'].items():
                        if waited.get(key, 0) >= val:
                            continue
                        waited[key] = val
                        eng.wait_ge(get_sem(key), val)
                    if o['fn'] is None:
                        continue
                    ins = o['fn'](eng)
                    if o['dma_key'] is not None:
                        n = o['dma_cnt']
                        ins.then_inc(get_sem('d_%s_%d' % (o['dma_key'], (n - 1) // DMA_LIM)), 16)
                    elif o['sig']:
                        v = o['sigval']
                        ins.then_inc(get_sem('c_%s_%d' % (e, (v - 1) // SEM_LIM)), 1)
            decos[e](body)


class Arena:
    def __init__(self, tensor, nbytes):
        self.t = tensor
        self.nbytes = nbytes
        self.off = 0

    def alloc(self, shape, dtype, parts=128):
        n = int(np.prod(shape))
        esz = 4 if dtype == F32 else 2
        nb = n * esz
        nb_al = (nb + 63) // 64 * 64
        assert self.off + nb_al <= self.nbytes, ("SBUF arena overflow", self.off, nb_al)
        ap = self.t[0:parts, self.off // 2:(self.off + nb) // 2]
        self.off += nb_al
        if dtype == F32:
            ap = ap.bitcast(F32)
        if len(shape) == 2:
            ap = ap.rearrange("p (a b) -> p a b", a=shape[0], b=shape[1])
        elif len(shape) == 3:
            ap = ap.rearrange("p (a b c) -> p a b c", a=shape[0], b=shape[1], c=shape[2])
        return ap


def build_program(debug=False, NPAIRS_DBG=8, stages=('ffn1', 'proj', 'rwkv', 'attn', 'merge', 'ffn2'), NB_DBG=None,
                  JS_DBG=range(4)):
    nc = bass.Bass("TRN2", target_bir_lowering=False)
    P = Prog(nc)

    def din(name, shape):
        return nc.dram_tensor(name, list(shape), F32, kind="ExternalInput").ap()
    x = din("x", [S, D])
    ffn_norm = [din("ffn1_norm", [D]), din("ffn2_norm", [D])]
    ffn_win = [din("ffn1_w_in", [D, 2 * DFF]), din("ffn2_w_in", [D, 2 * DFF])]
    ffn_wout = [din("ffn1_w_out", [DFF, D]), din("ffn2_w_out", [DFF, D])]
    out = nc.dram_tensor("out", [S, D], F32, kind="ExternalOutput").ap()
    x1_d = nc.dram_tensor("x1_scr", [S, D], F32, kind="ExternalOutput" if debug else "Internal").ap()

    stack = ExitStack()
    ARENA_BYTES = 200 * 1024
    arena_t = stack.enter_context(nc.sbuf_tensor("arena", [128, ARENA_BYTES // 2], BF16))
    A = Arena(arena_t, ARENA_BYTES)
    psum = stack.enter_context(nc.psum_tensor("psum", [128, 4096], F32))

    def bank(b, n=512, off=0):
        return psum[:, b * 512 + off:b * 512 + off + n]

    ident_f = A.alloc([128], F32)
    ident = A.alloc([128], BF16)
    ones_col = A.alloc([1], F32)

    P.op('pool', lambda e: e.memset(ident_f, 0.0), writes=['ident_f'])
    P.op('pool', lambda e: e.affine_select(out=ident_f, in_=ident_f, pattern=[[-1, 128]],
                                           compare_op=ALU.not_equal, fill=1.0, base=0, channel_multiplier=1),
         reads=['ident_f'], writes=['ident_f'])
    P.op('dve', lambda e: e.tensor_copy(out=ident, in_=ident_f), reads=['ident_f'], writes=['ident'])

    const_mark = A.off

    TT = 256
    NSUB = TT // 128
    NT = S // TT
    KC = D // 128
    FC = DFF // 128

    def ffn_stage(si, src, dst):
        A.off = const_mark
        W1 = A.alloc([KC, 2 * DFF], BF16)
        W2 = A.alloc([FC, D], BF16)
        gb = A.alloc([D], F32)
        xt = [A.alloc([NSUB, D], F32) for _ in range(2)]
        hb = [A.alloc([D], BF16) for _ in range(2)]
        hT = [A.alloc([KC, TT], BF16) for _ in range(2)]
        actT = A.alloc([FC, TT], BF16)
        sg = [A.alloc([TT], F32) for _ in range(2)]
        junk = A.alloc([D], BF16)
        ss = A.alloc([8], F32)
        pre = 's%d_' % si
        w1v = ffn_win[si].rearrange("(kc p) f -> p kc f", p=128)
        CH = 1408
        for kc in range(KC):
            for c in range(2 * DFF // CH):
                P.op('pool', lambda e, kc=kc, c=c: e.dma_start(out=W1[:, kc, c * CH:(c + 1) * CH],
                                                              in_=w1v[:, kc, c * CH:(c + 1) * CH]),
                     writes=[pre + 'W1'], dma_key=pre + 'W1')
        w2v = ffn_wout[si].rearrange("(fc p) d -> p fc d", p=128)
        for fc in range(FC):
            P.op('pool', lambda e, fc=fc: e.dma_start(out=W2[:, fc, :], in_=w2v[:, fc, :]),
                 writes=[pre + 'W2'], dma_key=pre + 'W2')
        P.op('sp', lambda e: e.dma_start(out=gb, in_=ffn_norm[si].partition_broadcast(128)),
             writes=[pre + 'gb'], dma_key=pre + 'gb')
        srcv = src.rearrange("(n s p) d -> n p s d", p=128, s=NSUB)
        dstv = dst.rearrange("(n s p) d -> n p s d", p=128, s=NSUB)
        for it in range(NT):
            sl = it % 2
            X = xt[sl]
            xr = pre + 'xt%d' % sl
            P.op('sp', lambda e, it=it, X=X: e.dma_start(out=X, in_=srcv[it]),
                 writes=[xr], dma_key=xr)
            HT = hT[sl]
            for s in range(NSUB):
                hs = (it * NSUB + s) % 2
                H = hb[hs]
                hr = pre + 'hb%d' % hs
                P.op('act', lambda e, X=X, s=s: e.activation(out=junk, in_=X[:, s, :], func=AF.Square,
                                                             accum_out=ss[:, 0:1]),
                     reads=[xr], writes=[pre + 'junk', pre + 'ss'])
                P.op('act', lambda e: e.activation(out=ss[:, 1:2], in_=ss[:, 0:1], func=AF.Sqrt,
                                                   scale=1.0 / D, bias=RMS_EPS),
                     reads=[pre + 'ss'], writes=[pre + 'ss1'])
                P.op('dve', lambda e: e.reciprocal(out=ss[:, 2:3], in_=ss[:, 1:2]),
                     reads=[pre + 'ss1'], writes=[pre + 'ss2'])
                P.op('dve', lambda e, X=X, s=s, H=H: e.scalar_tensor_tensor(
                    out=H, in0=X[:, s, :], scalar=ss[:, 2:3], in1=gb, op0=ALU.mult, op1=ALU.mult),
                    reads=[xr, pre + 'ss2', pre + 'gb'], writes=[hr])
                pT = bank(0).bitcast(BF16)
                for kc in range(KC):
                    P.op('pe', lambda e, kc=kc, H=H, pT=pT: e.transpose(
                        out=pT[:, kc * 128:(kc + 1) * 128], in_=H[:, kc * 128:(kc + 1) * 128], identity=ident),
                        reads=[hr, 'ident'], writes=['psT'])
                P.op('act', lambda e, HT=HT, s=s, pT=pT: e.copy(
                    out=HT[:, :, s * 128:(s + 1) * 128], in_=pT.rearrange("p (k t) -> p k t", k=KC)),
                    reads=['psT'], writes=[pre + 'hT%d' % sl])
            for fc in range(FC):
                b = 1 + fc % 2
                pg = bank(b)
                for half in range(2):
                    col = half * DFF + fc * 128
                    for kc in range(KC):
                        P.op('pe', lambda e, kc=kc, col=col, half=half, pg=pg, HT=HT: e.matmul(
                            pg[:, half * TT:(half + 1) * TT], lhsT=W1[:, kc, col:col + 128], rhs=HT[:, kc, :],
                            start=(kc == 0), stop=(kc == KC - 1)),
                            reads=[pre + 'W1', pre + 'hT%d' % sl], writes=['psG%d' % b])
                SG = sg[fc % 2]
                P.op('act', lambda e, pg=pg, SG=SG: e.activation(out=SG, in_=pg[:, 0:TT], func=AF.Silu),
                     reads=['psG%d' % b], writes=[pre + 'sg%d' % (fc % 2)])
                P.op('dve', lambda e, pg=pg, SG=SG, fc=fc: e.tensor_tensor(
                    out=actT[:, fc, :], in0=SG, in1=pg[:, TT:2 * TT], op=ALU.mult),
                    reads=['psG%d' % b, pre + 'sg%d' % (fc % 2)], writes=[pre + 'actT'])
            for s in range(NSUB):
                for dh in range(2):
                    b = 3 + (s * 2 + dh) % 2
                    pd = bank(b)
                    for fc in range(FC):
                        P.op('pe', lambda e, fc=fc, s=s, dh=dh, pd=pd: e.matmul(
                            pd, lhsT=actT[:, fc, s * 128:(s + 1) * 128], rhs=W2[:, fc, dh * 512:(dh + 1) * 512],
                            start=(fc == 0), stop=(fc == FC - 1)),
                            reads=[pre + 'actT', pre + 'W2'], writes=['psD%d' % b])
                    P.op('dve', lambda e, X=X, s=s, dh=dh, pd=pd: e.scalar_tensor_tensor(
                        out=X[:, s, dh * 512:(dh + 1) * 512], in0=pd, scalar=0.5,
                        in1=X[:, s, dh * 512:(dh + 1) * 512], op0=ALU.mult, op1=ALU.add),
                        reads=['psD%d' % b, xr], writes=[xr])
            P.op('sp', lambda e, it=it, X=X: e.dma_start(out=dstv[it], in_=X),
                 reads=[xr], writes=[pre + 'dst'], dma_key=pre + 'st%d' % sl)

    NCOL = 7712
    w_in = din("w_in", [D, NCOL])
    mix_norm = din("mix_norm", [D])
    rwkv_mu = din("rwkv_mu", [3360])
    b_gate = din("b_gate", [2048])
    qn = din("attn_q_norm", [64])
    kn = din("attn_k_norm", [64])
    kscr = "ExternalOutput" if debug else "Internal"
    if 'proj' not in stages:
        kscr = "ExternalInput"
    rkv_d = nc.dram_tensor("rkv_scr", [24, 128, S], F32, kind=kscr).ap()
    ta_d = nc.dram_tensor("ta_scr", [128, S], BF16, kind=kscr).ap()
    tg_d = nc.dram_tensor("tg_scr", [160, S], BF16, kind=kscr).ap()
    qk_d = nc.dram_tensor("qk_scr", [12, 128, S], BF16, kind=kscr).ap()
    v_d = nc.dram_tensor("v_scr", [S, 768], BF16, kind=kscr).ap()
    gate_d = nc.dram_tensor("gate_scr", [16, 128, S], BF16, kind=kscr).ap()

    def proj_stage():
        A.off = const_mark
        pre = 'p_'
        W = A.alloc([KC, NCOL], BF16)
        gb = A.alloc([D], F32)
        X = A.alloc([NSUB, D], F32)
        hb = [A.alloc([D], BF16) for _ in range(2)]
        hT = [A.alloc([KC, TT], BF16) for _ in range(2)]
        ss = A.alloc([8], F32)
        mu_t = A.alloc([27], F32)
        bg_t = A.alloc([16], F32)
        qg_t = A.alloc([2], F32)
        carry = A.alloc([27], F32)
        psb = [A.alloc([TT + 1], F32) for _ in range(2)]
        tmp = [A.alloc([TT], F32) for _ in range(2)]
        sq = [A.alloc([TT], BF16) for _ in range(2)]
        lnb = [A.alloc([TT], F32) for _ in range(2)]
        rkv_st = A.alloc([24, TT], F32)
        ta_st = A.alloc([TT], BF16)
        tg_st = A.alloc([2, TT], BF16)
        qk_st = A.alloc([12, TT], BF16)
        gate_st = A.alloc([16, TT], BF16)
        v_st = A.alloc([NSUB, 768], BF16)
        bones = A.alloc([128], BF16)
        print("proj_stage arena", A.off)
        wv = w_in.rearrange("(kc p) f -> p kc f", p=128)
        CH = 964
        for kc in range(KC):
            for c in range(NCOL // CH):
                P.op('pool', lambda e, kc=kc, c=c: e.dma_start(out=W[:, kc, c * CH:(c + 1) * CH],
                                                              in_=wv[:, kc, c * CH:(c + 1) * CH]),
                     writes=[pre + 'W'], dma_key=pre + 'W')
        P.op('sp', lambda e: e.dma_start(out=gb, in_=mix_norm.partition_broadcast(128)),
             writes=[pre + 'gb'], dma_key=pre + 'par')
        P.op('sp', lambda e: e.dma_start(out=mu_t[:, 0:26], in_=rwkv_mu[0:3328].rearrange("(b p) -> p b", p=128),
                                         allow_slow_non_contiguous=True), writes=[pre + 'mu'], dma_key=pre + 'par')
        P.op('sp', lambda e: e.dma_start(out=mu_t[0:32, 26:27], in_=rwkv_mu[3328:3360].rearrange("(p o) -> p o", o=1)),
             writes=[pre + 'mu'], dma_key=pre + 'par')
        P.op('sp', lambda e: e.dma_start(out=bg_t, in_=b_gate.rearrange("(b p) -> p b", p=128),
                                         allow_slow_non_contiguous=True), writes=[pre + 'bg'], dma_key=pre + 'par')
        for hh in range(2):
            P.op('sp', lambda e, hh=hh: e.dma_start(out=qg_t[hh * 64:(hh + 1) * 64, 0:1],
                                                     in_=qn.rearrange("(p o) -> p o", o=1)),
                 writes=[pre + 'qg'], dma_key=pre + 'par')
            P.op('sp', lambda e, hh=hh: e.dma_start(out=qg_t[hh * 64:(hh + 1) * 64, 1:2],
                                                     in_=kn.rearrange("(p o) -> p o", o=1)),
                 writes=[pre + 'qg'], dma_key=pre + 'par')
        P.op('pool', lambda e: e.tensor_scalar(out=qg_t[:, 0:1], in0=qg_t[:, 0:1], scalar1=0.125, scalar2=None,
                                               op0=ALU.mult), reads=[pre + 'qg'], writes=[pre + 'qg'])
        P.op('pool', lambda e: e.memset(carry, 0.0), writes=[pre + 'carry%d' % i for i in range(27)])
        P.op('pool', lambda e: e.memset(bones, 0.0), writes=[pre + 'bones'])
        P.op('pool', lambda e: e.memset(bones[0:64, 0:64], 1.0), reads=[pre + 'bones'], writes=[pre + 'bones'])
        P.op('pool', lambda e: e.memset(bones[64:128, 64:128], 1.0), reads=[pre + 'bones'], writes=[pre + 'bones'])

        blocks = []
        for b in range(24):
            blocks.append((b * 128, 128, 'rkv', b))
        blocks.append((3072, 128, 'ta', 24))
        blocks.append((3200, 128, 'tg0', 25))
        blocks.append((3328, 32, 'tg1', 26))
        for b in range(6):
            blocks.append((3360 + b * 128, 128, 'q', b))
        for b in range(6):
            blocks.append((4128 + b * 128, 128, 'k', 6 + b))
        for b in range(16):
            blocks.append((5664 + b * 128, 128, 'gate', b))

        srcv = x1_d.rearrange("(n s p) d -> n p s d", p=128, s=NSUB)
        xr = pre + 'X'
        for it in range(NT):
            t0 = it * TT
            sl = it % 2
            P.op('sp', lambda e, it=it: e.dma_start(out=X, in_=srcv[it]), writes=[xr], dma_key=xr)
            HT = hT[sl]
            htr = pre + 'hT%d' % sl
            for s in range(NSUB):
                hs = (it * NSUB + s) % 2
                H = hb[hs]
                hr = pre + 'hb%d' % hs
                P.op('act', lambda e, s=s, H=H: e.activation(out=H, in_=X[:, s, :], func=AF.Square,
                                                             accum_out=ss[:, 0:1]),
                     reads=[xr], writes=[hr, pre + 'ss'])
                P.op('act', lambda e: e.activation(out=ss[:, 1:2], in_=ss[:, 0:1], func=AF.Sqrt,
                                                   scale=1.0 / D, bias=RMS_EPS),
                     reads=[pre + 'ss'], writes=[pre + 'ss1'])
                P.op('dve', lambda e: e.reciprocal(out=ss[:, 2:3], in_=ss[:, 1:2]),
                     reads=[pre + 'ss1'], writes=[pre + 'ss2'])
                P.op('dve', lambda e, s=s, H=H: e.scalar_tensor_tensor(
                    out=H, in0=X[:, s, :], scalar=ss[:, 2:3], in1=gb, op0=ALU.mult, op1=ALU.mult),
                    reads=[xr, pre + 'ss2', pre + 'gb'], writes=[hr])
                pT = bank(0).bitcast(BF16)
                for kc in range(KC):
                    P.op('pe', lambda e, kc=kc, H=H, pT=pT: e.transpose(
                        out=pT[:, kc * 128:(kc + 1) * 128], in_=H[:, kc * 128:(kc + 1) * 128], identity=ident),
                        reads=[hr, 'ident'], writes=['psT'])
                P.op('act', lambda e, HT=HT, s=s, pT=pT: e.copy(
                    out=HT[:, :, s * 128:(s + 1) * 128], in_=pT.rearrange("p (k t) -> p k t", k=KC)),
                    reads=['psT'], writes=[htr])

            pending = [None]

            def flush():
                if pending[0] is None:
                    return
                pg, pgr, j, kind, idx = pending[0]
                pending[0] = None
                pss = bank(5)[:, 0:TT]
                P.op('pe', lambda e, j=j, pss=pss: e.matmul(pss, lhsT=bones, rhs=sq[j], start=True, stop=True),
                     reads=[pre + 'sq%d' % j, pre + 'bones'], writes=['pss'])
                P.op('act', lambda e, j=j, pss=pss: e.activation(out=lnb[j], in_=pss, func=AF.Ln,
                                                                 scale=1.0 / 64, bias=RMS_EPS),
                     reads=['pss'], writes=[pre + 'lnb%d' % j])
                P.op('act', lambda e, j=j: e.activation(out=lnb[j], in_=lnb[j], func=AF.Exp, scale=-0.5),
                     reads=[pre + 'lnb%d' % j], writes=[pre + 'lnb%d' % j])
                c = 0 if kind == 'q' else 1
                P.op('dve', lambda e, j=j, pg=pg, idx=idx, c=c: e.scalar_tensor_tensor(
                    out=qk_st[:, idx, :], in0=pg, scalar=qg_t[:, c:c + 1], in1=lnb[j], op0=ALU.mult, op1=ALU.mult),
                    reads=[pgr, pre + 'lnb%d' % j, pre + 'qg'], writes=[pre + 'qk_st'])

            for bi, (col0, M, kind, idx) in enumerate(blocks):
                b = 1 + bi % 4
                pgr = 'psG%d' % b
                pg = bank(b)[0:M, 0:TT]
                j = bi % 2
                for kc in range(KC):
                    P.op('pe', lambda e, kc=kc, col0=col0, M=M, pg=pg, HT=HT: e.matmul(
                        pg, lhsT=W[:, kc, col0:col0 + M], rhs=HT[:, kc, :], start=(kc == 0), stop=(kc == KC - 1)),
                        reads=[pre + 'W', htr], writes=[pgr])
                flush()
                if kind in ('rkv', 'ta', 'tg0', 'tg1'):
                    cr = pre + 'carry%d' % idx
                    pbr = pre + 'psb%d' % j
                    tr = pre + 'tmp%d' % j
                    PS = psb[j][0:M]
                    TM = tmp[j][0:M]
                    P.op('pool', lambda e, PS=PS, idx=idx, M=M: e.tensor_copy(out=PS[:, 0:1], in_=carry[0:M, idx:idx + 1]),
                         reads=[cr], writes=[pbr])
                    P.op('act', lambda e, PS=PS, pg=pg: e.copy(out=PS[:, 1:TT + 1], in_=pg),
                         reads=[pgr, pbr], writes=[pbr])
                    P.op('dve', lambda e, PS=PS, TM=TM: e.tensor_tensor(out=TM, in0=PS[:, 0:TT], in1=PS[:, 1:TT + 1],
                                                                        op=ALU.subtract),
                         reads=[pbr], writes=[tr])
                    P.op('pool', lambda e, PS=PS, idx=idx, M=M: e.tensor_copy(out=carry[0:M, idx:idx + 1],
                                                                              in_=PS[:, TT:TT + 1]),
                         reads=[pbr], writes=[cr])
                    if kind == 'rkv':
                        P.op('dve', lambda e, PS=PS, TM=TM, idx=idx: e.scalar_tensor_tensor(
                            out=rkv_st[:, idx, :], in0=TM, scalar=mu_t[:, idx:idx + 1], in1=PS[:, 1:TT + 1],
                            op0=ALU.mult, op1=ALU.add),
                            reads=[tr, pbr, pre + 'mu'], writes=[pre + 'rkv_st%d' % (idx // 8)])
                    else:
                        P.op('dve', lambda e, PS=PS, TM=TM, idx=idx, M=M: e.scalar_tensor_tensor(
                            out=TM, in0=TM, scalar=mu_t[0:M, idx:idx + 1], in1=PS[:, 1:TT + 1],
                            op0=ALU.mult, op1=ALU.add),
                            reads=[tr, pbr, pre + 'mu'], writes=[tr])
                        if kind == 'ta':
                            P.op('act', lambda e, TM=TM: e.activation(out=ta_st[0:64], in_=TM[0:64], func=AF.Tanh),
                                 reads=[tr], writes=[pre + 'ta_st'])
                            P.op('act', lambda e, TM=TM: e.copy(out=ta_st[64:128], in_=TM[64:128]),
                                 reads=[tr], writes=[pre + 'ta_st'])
                        elif kind == 'tg0':
                            P.op('act', lambda e, TM=TM: e.activation(out=tg_st[:, 0, :], in_=TM, func=AF.Sigmoid),
                                 reads=[tr], writes=[pre + 'tg_st'])
                        else:
                            P.op('act', lambda e, TM=TM: e.activation(out=tg_st[0:32, 1, :], in_=TM, func=AF.Sigmoid),
                                 reads=[tr], writes=[pre + 'tg_st'])
                elif kind in ('q', 'k'):
                    P.op('act', lambda e, pg=pg, j=j: e.activation(out=sq[j], in_=pg, func=AF.Square),
                         reads=[pgr], writes=[pre + 'sq%d' % j])
                    pending[0] = (pg, pgr, j, kind, idx)
                else:
                    P.op('act', lambda e, pg=pg, idx=idx: e.activation(out=gate_st[:, idx, :], in_=pg, func=AF.Sigmoid,
                                                                       bias=bg_t[:, idx:idx + 1]),
                         reads=[pgr, pre + 'bg'], writes=[pre + 'gate_st'])
            flush()
            for s in range(NSUB):
                for (c0, n, b) in ((4896, 512, 6), (5408, 256, 7)):
                    pv = bank(b)[:, 0:n]
                    for kc in range(KC):
                        P.op('pe', lambda e, kc=kc, s=s, c0=c0, n=n, pv=pv, HT=HT: e.matmul(
                            pv, lhsT=HT[:, kc, s * 128:(s + 1) * 128], rhs=W[:, kc, c0:c0 + n],
                            start=(kc == 0), stop=(kc == KC - 1)),
                            reads=[pre + 'W', htr], writes=['psV%d' % b])
                P.op('act', lambda e, s=s: e.copy(out=v_st[:, s, 0:512], in_=bank(6)),
                     reads=['psV6'], writes=[pre + 'v_st'])
                P.op('dve', lambda e, s=s: e.tensor_copy(out=v_st[:, s, 512:768], in_=bank(7)[:, 0:256]),
                     reads=['psV7'], writes=[pre + 'v_st'])
            rv = rkv_d.rearrange("b p t -> p b t")
            for g in range(3):
                P.op('sp', lambda e, g=g, t0=t0: e.dma_start(out=rv[:, g * 8:(g + 1) * 8, t0:t0 + TT],
                                                             in_=rkv_st[:, g * 8:(g + 1) * 8, :]),
                     reads=[pre + 'rkv_st%d' % g], writes=['d_rkv'], dma_key=pre + 'rkv_st%d' % g)
            P.op('sp', lambda e, t0=t0: e.dma_start(out=ta_d[:, t0:t0 + TT], in_=ta_st),
                 reads=[pre + 'ta_st'], writes=['d_ta'], dma_key=pre + 'ta_st')
            P.op('sp', lambda e, t0=t0: e.dma_start(out=tg_d[0:128, t0:t0 + TT], in_=tg_st[:, 0, :]),
                 reads=[pre + 'tg_st'], writes=['d_tg'], dma_key=pre + 'tg_st')
            P.op('sp', lambda e, t0=t0: e.dma_start(out=tg_d[128:160, t0:t0 + TT], in_=tg_st[0:32, 1, :]),
                 reads=[pre + 'tg_st'], writes=['d_tg'], dma_key=pre + 'tg_st')
            P.op('sp', lambda e, t0=t0: e.dma_start(out=qk_d.rearrange("b p t -> p b t")[:, :, t0:t0 + TT], in_=qk_st),
                 reads=[pre + 'qk_st'], writes=['d_qk'], dma_key=pre + 'qk_st')
            P.op('sp', lambda e, t0=t0: e.dma_start(out=gate_d.rearrange("b p t -> p b t")[:, :, t0:t0 + TT],
                                                    in_=gate_st),
                 reads=[pre + 'gate_st'], writes=['d_gate'], dma_key=pre + 'gate_st')
            P.op('sp', lambda e, t0=t0: e.dma_start(
                out=v_d[t0:t0 + TT, :].rearrange("(s p) c -> p s c", p=128), in_=v_st),
                reads=[pre + 'v_st'], writes=['d_v'], dma_key=pre + 'v_st')

    w2_d = din("rwkv_w2", [64, 1024])
    a2_d = din("rwkv_a2", [64, 1024])
    g2_d = din("rwkv_g2", [160, 1024])
    prm_names = ['rwkv_w0', 'rwkv_a0', 'rwkv_k_k', 'rwkv_k_a', 'rwkv_r_k', 'rwkv_ln_w', 'rwkv_ln_b']
    prm_d = [din(n, [1024]) for n in prm_names]
    ya_d = nc.dram_tensor("ya_scr", [8, 128, S], BF16, kind="ExternalOutput" if debug else "Internal").ap()
    TB = 1024
    NCH = TB // 128
    NB = S // TB if NB_DBG is None else NB_DBG
    C0 = float(np.exp(-0.5))
    GN_EPS = 64e-5

    def rwkv_stage(pairs=range(8)):
        A.off = const_mark
        pre = 'r_'
        WA = A.alloc([1024], BF16)
        G2a = A.alloc([1024], BF16)
        G2b = A.alloc([1024], BF16)
        prm = A.alloc([7, 8], F32)
        bones = A.alloc([128], BF16)
        mk4 = A.alloc([512], F32)
        mkL = A.alloc([2, 128], F32)
        E2 = A.alloc([64], F32)
        mrow = A.alloc([TB], F32)
        f32names = ['R', 'K', 'V', 'SG', 'AA', 'GG', 'KK', 'KM', 'T1', 'BVEC', 'CS', 'T2', 'T3', 'EP', 'EN', 'EPM',
                    'EC', 'BV', 'Y32', 'DD']
        T = {n: A.alloc([TB], F32) for n in f32names}
        bfnames = ['TA', 'TG0', 'TG1', 'TQ', 'BT', 'KT', 'BH', 'KH', 'VT', 'YB', 'YO']
        for n in bfnames:
            T[n] = A.alloc([TB], BF16)
        AR = A.alloc([NCH, 2, 128], BF16)
        TM4 = A.alloc([NCH, 4, 128], BF16)
        PC = A.alloc([NCH], F32)
        SC = [[A.alloc([512], BF16) for _ in range(2)] for _ in range(2)]
        LZ = [[A.alloc([2, 384], BF16) for _ in range(2)] for _ in range(2)]
        MCz = [A.alloc([2, 64], BF16) for _ in range(2)]
        QT = [A.alloc([128], BF16) for _ in range(2)]
        STz = A.alloc([2, 64], BF16)
        print("rwkv_stage arena", A.off)

        def R_(n):
            return pre + n
        P.op('pool', lambda e: e.dma_start(out=WA[0:64, :], in_=w2_d), writes=[R_('WA')], dma_key=R_('w'))
        P.op('pool', lambda e: e.dma_start(out=WA[64:128, :], in_=a2_d), writes=[R_('WA')], dma_key=R_('w'))
        P.op('pool', lambda e: e.dma_start(out=G2a, in_=g2_d[0:128, :]), writes=[R_('G2')], dma_key=R_('w'))
        P.op('pool', lambda e: e.dma_start(out=G2b[0:32, :], in_=g2_d[128:160, :]), writes=[R_('G2')], dma_key=R_('w'))
        for i in range(7):
            P.op('sp', lambda e, i=i: e.dma_start(out=prm[:, i, :], in_=prm_d[i].rearrange("(b p) -> p b", p=128),
                                                   allow_slow_non_contiguous=True),
                 writes=[R_('prm')], dma_key=R_('par'))
        P.op('pool', lambda e: e.memset(bones, 0.0), writes=[R_('bones')])
        P.op('pool', lambda e: e.memset(bones[0:64, 0:64], 1.0), reads=[R_('bones')], writes=[R_('bones')])
        P.op('pool', lambda e: e.memset(bones[64:128, 64:128], 1.0), reads=[R_('bones')], writes=[R_('bones')])
        P.op('pool', lambda e: e.memset(mk4, 1.0), writes=[R_('mk4')])
        for q in range(4):
            base = -1 if q % 2 == 0 else 0
            P.op('pool', lambda e, q=q, base=base: e.affine_select(
                out=mk4[:, q * 128:(q + 1) * 128], in_=mk4[:, q * 128:(q + 1) * 128], pattern=[[1, 128]],
                compare_op=ALU.is_ge, fill=0.0, base=base, channel_multiplier=-1),
                reads=[R_('mk4')], writes=[R_('mk4')])
        P.op('pool', lambda e: e.memset(mkL, 1.0), writes=[R_('mkL')])
        P.op('pool', lambda e: e.affine_select(out=mkL, in_=mkL, pattern=[[0, 2], [-1, 128]], compare_op=ALU.is_ge,
                                               fill=0.0, base=-1, channel_multiplier=1),
             reads=[R_('mkL')], writes=[R_('mkL')])
        P.op('pool', lambda e: e.tensor_copy(out=E2[0:64, :], in_=ident_f[0:64, 0:64]), reads=['ident_f'], writes=[R_('E2')])
        P.op('pool', lambda e: e.tensor_copy(out=E2[64:128, :], in_=ident_f[64:128, 64:128]), reads=['ident_f'],
             writes=[R_('E2')])
        P.op('pool', lambda e: e.memset(mrow, 1.0), writes=[R_('mrow')])
        P.op('pool', lambda e: e.memset(mrow.rearrange("p (c t) -> p c t", t=128)[:, :, 0:1], 0.0),
             reads=[R_('mrow')], writes=[R_('mrow')])

        def ch3(ap):
            return ap.rearrange("p (c t) -> p c t", t=128)

        def nm(x):
            if x.startswith('pb') or x in ('ident', 'ident_f'):
                return x
            return R_(x)

        def ew(eng, fn, reads, writes):
            P.op(eng, fn, reads=[nm(x) for x in reads], writes=[nm(x) for x in writes])

        for hp in pairs:
            cols = slice(hp * 128, (hp + 1) * 128)
            ew('pool', lambda e: e.memset(STz, 0.0), [], ['ST'])
            for sl_ in range(2):
                ew('pool', lambda e, sl_=sl_: e.memset(MCz[sl_], 0.0), [], ['MC%d' % sl_])
            for tb in range(NB):
                t0 = tb * TB
                tsl = slice(t0, t0 + TB)
                for i, n in enumerate(['R', 'K', 'V']):
                    P.op('sp', lambda e, i=i, n=n, hp=hp, tsl=tsl: e.dma_start(out=T[n], in_=rkv_d[i * 8 + hp, :, tsl]),
                         writes=[R_(n)], dma_key=R_('ld' + n))
                P.op('sp', lambda e, tsl=tsl: e.dma_start(out=T['TA'], in_=ta_d[:, tsl]), writes=[R_('TA')],
                     dma_key=R_('ldTA'))
                P.op('sp', lambda e, tsl=tsl: e.dma_start(out=T['TG0'], in_=tg_d[0:128, tsl]), writes=[R_('TG0')],
                     dma_key=R_('ldTG0'))
                P.op('sp', lambda e, tsl=tsl: e.dma_start(out=T['TG1'][0:32], in_=tg_d[128:160, tsl]),
                     writes=[R_('TG1')], dma_key=R_('ldTG1'))
                for hf in range(2):
                    hs = slice(hf * 512, (hf + 1) * 512)
                    ew('pe', lambda e, hs=hs, cols=cols: e.matmul(bank(1), lhsT=WA[0:64, cols], rhs=T['TA'][0:64, hs],
                                                                  start=True, stop=True), ['WA', 'TA'], ['pb1'])
                    ew('act', lambda e, hs=hs, hp=hp: e.activation(out=T['SG'][:, hs], in_=bank(1), func=AF.Sigmoid,
                                                                   bias=prm[:, 0, hp:hp + 1]), ['pb1', 'prm'], ['SG'])
                    ew('pe', lambda e, hs=hs, cols=cols: e.matmul(bank(2), lhsT=WA[64:128, cols], rhs=T['TA'][64:128, hs],
                                                                  start=True, stop=True), ['WA', 'TA'], ['pb2'])
                    ew('act', lambda e, hs=hs, hp=hp: e.activation(out=T['AA'][:, hs], in_=bank(2), func=AF.Sigmoid,
                                                                   bias=prm[:, 1, hp:hp + 1]), ['pb2', 'prm'], ['AA'])
                    ew('pe', lambda e, hs=hs, cols=cols: e.matmul(bank(3), lhsT=G2a[:, cols], rhs=T['TG0'][:, hs],
                                                                  start=True, stop=False), ['G2', 'TG0'], ['pb3', 'pb3'])
                    ew('pe', lambda e, hs=hs, cols=cols: e.matmul(bank(3), lhsT=G2b[0:32, cols], rhs=T['TG1'][0:32, hs],
                                                                  start=False, stop=True), ['G2', 'TG1'], ['pb3', 'pb3'])
                    ew('act', lambda e, hs=hs: e.copy(out=T['GG'][:, hs], in_=bank(3)), ['pb3', 'pb3'], ['GG'])
                ew('dve', lambda e, hp=hp: e.tensor_scalar(out=T['KK'], in0=T['K'], scalar1=prm[:, 2, hp:hp + 1],
                                                           scalar2=None, op0=ALU.mult), ['K', 'prm'], ['KK'])
                ew('act', lambda e: e.activation(out=T['TQ'], in_=T['KK'], func=AF.Square), ['KK'], ['TQ'])
                for hf in range(2):
                    hs = slice(hf * 512, (hf + 1) * 512)
                    ew('pe', lambda e, hs=hs: e.matmul(bank(4), lhsT=bones, rhs=T['TQ'][:, hs], start=True, stop=True),
                       ['bones', 'TQ'], ['pb4'])
                    ew('dve', lambda e, hs=hs: e.tensor_scalar(out=T['T1'][:, hs], in0=bank(4), scalar1=1e-19,
                                                               scalar2=None, op0=ALU.max), ['pb4'], ['T1'])
                ew('act', lambda e: e.activation(out=T['T1'], in_=T['T1'], func=AF.Ln), ['T1'], ['T1'])
                ew('act', lambda e: e.activation(out=T['T1'], in_=T['T1'], func=AF.Exp, scale=-0.5), ['T1'], ['T1'])
                ew('dve', lambda e: e.tensor_tensor(out=T['KK'], in0=T['KK'], in1=T['T1'], op=ALU.mult),
                   ['KK', 'T1'], ['KK'])
                ew('dve', lambda e, hp=hp: e.tensor_scalar(out=T['T1'], in0=T['AA'], scalar1=-1.0,
                                                           scalar2=prm[:, 3, hp:hp + 1], op0=ALU.add, op1=ALU.mult),
                   ['AA', 'prm'], ['T1'])
                ew('dve', lambda e: e.scalar_tensor_tensor(out=T['KM'], in0=T['T1'], scalar=1.0, in1=T['K'],
                                                           op0=ALU.add, op1=ALU.mult), ['T1', 'K'], ['KM'])
                ew('pool', lambda e: e.tensor_tensor(out=T['T1'], in0=T['R'], in1=T['KM'], op=ALU.mult),
                   ['R', 'KM'], ['T1'])
                ew('pool', lambda e, hp=hp: e.tensor_scalar(out=T['TQ'], in0=T['T1'], scalar1=prm[:, 4, hp:hp + 1],
                                                            scalar2=None, op0=ALU.mult), ['T1', 'prm'], ['TQ'])
                for hf in range(2):
                    hs = slice(hf * 512, (hf + 1) * 512)
                    ew('pe', lambda e, hs=hs: e.matmul(bank(5), lhsT=bones, rhs=T['TQ'][:, hs], start=True, stop=True),
                       ['bones', 'TQ'], ['pb5'])
                    ew('dve', lambda e, hs=hs: e.tensor_tensor(out=T['BV'][:, hs], in0=T['V'][:, hs], in1=bank(5),
                                                               op=ALU.mult), ['pb5', 'V'], ['BV'])
                ew('pool', lambda e: e.tensor_tensor(out=T['BVEC'], in0=T['KK'], in1=T['AA'], op=ALU.mult),
                   ['KK', 'AA'], ['BVEC'])
                ew('dve', lambda e: e.tensor_tensor_scan(out=T['CS'], data0=mrow, data1=T['SG'], initial=0.0,
                                                         op0=ALU.mult, op1=ALU.add), ['mrow', 'SG'], ['CS'])
                ew('pool', lambda e: e.tensor_tensor(out=T['T2'], in0=T['CS'], in1=T['SG'], op=ALU.subtract),
                   ['CS', 'SG'], ['T2'])
                ew('act', lambda e: e.activation(out=T['EP'], in_=T['CS'], func=AF.Exp, scale=-C0), ['CS'], ['EP'])
                ew('act', lambda e: e.activation(out=T['EN'], in_=T['CS'], func=AF.Exp, scale=C0), ['CS'], ['EN'])
                ew('act', lambda e: e.activation(out=T['EPM'], in_=T['T2'], func=AF.Exp, scale=-C0), ['T2'], ['EPM'])
                ew('dve', lambda e: e.tensor_tensor(
                    out=ch3(T['T3']), in0=ch3(T['CS']), in1=ch3(T['CS'])[:, :, 127:128].to_broadcast([128, NCH, 128]),
                    op=ALU.subtract), ['CS'], ['T3'])
                ew('act', lambda e: e.activation(out=T['EC'], in_=T['T3'], func=AF.Exp, scale=C0), ['T3'], ['EC'])
                ew('act', lambda e: e.activation(out=PC.rearrange("p (c o) -> p c o", o=1),
                                                 in_=ch3(T['CS'])[:, :, 127:128], func=AF.Exp, scale=-C0),
                   ['CS'], ['PC'])
                ew('dve', lambda e: e.scalar_tensor_tensor(out=AR[:, :, 0, :], in0=ch3(T['EPM']), scalar=-1.0,
                                                           in1=ch3(T['KK']), op0=ALU.mult, op1=ALU.mult),
                   ['EPM', 'KK'], ['AR0'])
                ew('pool', lambda e: e.tensor_tensor(out=AR[:, :, 1, :], in0=ch3(T['EP']), in1=ch3(T['R']), op=ALU.mult),
                   ['EP', 'R'], ['AR1'])
                ew('dve', lambda e: e.tensor_tensor(out=T['BT'], in0=T['EN'], in1=T['BVEC'], op=ALU.mult),
                   ['EN', 'BVEC'], ['BT'])
                ew('pool', lambda e: e.tensor_tensor(out=T['KT'], in0=T['EN'], in1=T['KM'], op=ALU.mult),
                   ['EN', 'KM'], ['KT'])
                ew('dve', lambda e: e.tensor_tensor(out=T['BH'], in0=T['EC'], in1=T['BVEC'], op=ALU.mult),
                   ['EC', 'BVEC'], ['BH'])
                ew('pool', lambda e: e.tensor_tensor(out=T['KH'], in0=T['EC'], in1=T['KM'], op=ALU.mult),
                   ['EC', 'KM'], ['KH'])
                ew('act', lambda e: e.copy(out=T['VT'], in_=T['V']), ['V'], ['VT'])
                pT = bank(0).bitcast(BF16)
                for c in range(NCH):
                    cs_ = slice(c * 128, (c + 1) * 128)
                    srcs = [(AR[:, c, 0, :], 'AR0'), (T['VT'][:, cs_], 'VT'), (T['BH'][:, cs_], 'BH'),
                            (T['KH'][:, cs_], 'KH')]
                    for q, (sap, sr) in enumerate(srcs):
                        ew('pe', lambda e, q=q, sap=sap: e.transpose(out=pT[:, q * 128:(q + 1) * 128], in_=sap,
                                                                     identity=ident), [sr, 'ident'], ['pb0'])
                    ew('act', lambda e, c=c: e.copy(out=TM4[:, c, :, :], in_=pT[:, 0:512].rearrange("p (q t) -> p q t", q=4)),
                       ['pb0'], ['TM4_%d' % c])
                P1 = (1, 2)
                P2 = ((4, 5), (6, 7))

                def partA(c, sl):
                    cs_ = slice(c * 128, (c + 1) * 128)
                    tm = 'TM4_%d' % c
                    arc = AR[:, c, :, :].rearrange("p a t -> p (a t)")
                    b1 = P1[sl]
                    ps1 = bank(b1)
                    for h2 in range(2):
                        psl = slice(64 * h2, 64 * h2 + 64)
                        scr = 'SC%d_%d' % (sl, h2)
                        ew('pe', lambda e, ps1=ps1, psl=psl, cs_=cs_, arc=arc: e.matmul(
                            ps1[:, 0:256], lhsT=T['BT'][psl, cs_], rhs=arc[psl, :], start=True, stop=True),
                            ['BT', 'AR0', 'AR1'], ['pb%d' % b1])
                        ew('pe', lambda e, ps1=ps1, psl=psl, cs_=cs_, arc=arc: e.matmul(
                            ps1[:, 256:512], lhsT=T['KT'][psl, cs_], rhs=arc[psl, :], start=True, stop=True),
                            ['KT', 'AR0', 'AR1'], ['pb%d' % b1])
                        ew('dve', lambda e, ps1=ps1, h2=h2, sl=sl: e.tensor_tensor(out=SC[sl][h2], in0=mk4, in1=ps1,
                                                                                   op=ALU.mult),
                           ['pb%d' % b1, 'mk4'], [scr])
                        b2 = P2[sl][h2]
                        ew('pe', lambda e, b2=b2, psl=psl, cs_=cs_, c=c: e.matmul(
                            bank(b2)[:, 384:512], lhsT=AR[psl, c, 0, :], rhs=T['BT'][psl, cs_],
                            start=True, stop=True), ['AR0', 'BT'], ['pb%d' % b2])
                    for h2 in range(2):
                        b2 = P2[sl][h2]
                        ew('dve', lambda e, h2=h2, b2=b2, sl=sl: e.tensor_tensor(
                            out=LZ[sl][0][:, h2, 0:128], in0=mkL[:, h2, :], in1=bank(b2)[:, 384:512], op=ALU.mult),
                            ['pb%d' % b2, 'mkL'], ['LL%d_0_%d' % (sl, h2)])
                    for h2 in range(2):
                        pb = 64 * h2
                        ew('pe', lambda e, h2=h2, pb=pb, c=c, sl=sl: e.matmul(
                            bank(3)[:, 256 + h2 * 64:256 + (h2 + 1) * 64], lhsT=SC[sl][h2][:, 256:384],
                            rhs=TM4[:, c, 1, pb:pb + 64], start=True, stop=True), ['SC%d_%d' % (sl, h2), tm], ['pb3'])
                    ew('pool', lambda e, c=c, sl=sl: e.tensor_copy(
                        out=LZ[sl][0][:, :, 128:192], in_=TM4[:, c, 0, :].rearrange("p (h k) -> p h k", h=2)),
                        [tm], ['ZZ%d_0_0' % sl, 'ZZ%d_0_1' % sl])
                    for h2 in range(2):
                        ew('act', lambda e, h2=h2, sl=sl: e.copy(out=LZ[sl][0][:, h2, 192:256],
                                                                 in_=bank(3)[:, 256 + h2 * 64:256 + (h2 + 1) * 64]),
                           ['pb3'], ['ZZ%d_0_%d' % (sl, h2)])

                def partB(c, sl, n):
                    pp = n % 2
                    for h2 in range(2):
                        b2 = P2[sl][h2]
                        ps2 = bank(b2)
                        pbr = 'pb%d' % b2
                        ltn = SC[sl][h2][:, 0:128] if n == 0 else LZ[sl][pp][:, h2, 256:384]
                        rds = ['LL%d_%d_%d' % (sl, pp, h2), 'ZZ%d_%d_%d' % (sl, pp, h2)] + \
                            (['SC%d_%d' % (sl, h2)] if n == 0 else [])
                        if n < 6:
                            ew('pe', lambda e, ps2=ps2, ltn=ltn, pp=pp, h2=h2, sl=sl: e.matmul(
                                ps2[:, 0:256], lhsT=ltn, rhs=LZ[sl][pp][:, h2, 0:256], start=True, stop=True),
                                rds, [pbr])
                            ew('pe', lambda e, ps2=ps2, ltn=ltn, pp=pp, h2=h2, sl=sl: e.matmul(
                                ps2[:, 256:384], lhsT=LZ[sl][pp][:, h2, 0:128], rhs=ltn, start=True, stop=True),
                                rds, [pbr])
                        else:
                            ew('pe', lambda e, ps2=ps2, ltn=ltn, pp=pp, h2=h2, sl=sl: e.matmul(
                                ps2[:, 128:256], lhsT=ltn, rhs=LZ[sl][pp][:, h2, 128:256], start=True, stop=True),
                                rds, [pbr])

                    def cp(h2):
                        b2 = P2[sl][h2]
                        ew('act', lambda e, h2=h2, b2=b2: e.copy(
                            out=LZ[sl][1 - pp][:, h2, :].rearrange("p (s t) -> p s t", s=3)[:, 0:3:2, :],
                            in_=bank(b2)[:, 0:384].rearrange("p (s t) -> p s t", s=3)[:, 0:3:2, :]),
                            ['pb%d' % b2], ['LL%d_%d_%d' % (sl, 1 - pp, h2)])

                    def ad(h2):
                        b2 = P2[sl][h2]
                        ew('dve', lambda e, h2=h2, b2=b2: e.tensor_tensor(
                            out=LZ[sl][1 - pp][:, h2, 128:256], in0=LZ[sl][pp][:, h2, 128:256],
                            in1=bank(b2)[:, 128:256], op=ALU.add),
                            ['pb%d' % b2, 'ZZ%d_%d_%d' % (sl, pp, h2)], ['ZZ%d_%d_%d' % (sl, 1 - pp, h2)])
                    if n < 6:
                        cp(0)
                        ad(1)
                        cp(1)
                        ad(0)
                    else:
                        ad(0)
                        ad(1)

                def partC(c, sl):
                    tm = 'TM4_%d' % c
                    ZF = LZ[sl][1]
                    p3 = bank(0)[:, 192:384]
                    for h2 in range(2):
                        pb = 64 * h2
                        psl = slice(pb, pb + 64)
                        zr = 'ZZ%d_1_%d' % (sl, h2)
                        ew('pe', lambda e, h2=h2, pb=pb, psl=psl, c=c: e.matmul(
                            p3[psl, 0:64], lhsT=ZF[:, h2, 128:192], rhs=TM4[:, c, 2, pb:pb + 64],
                            start=True, stop=True, tile_position=(0, pb)), [zr, tm], ['pb0'])
                        ew('pe', lambda e, h2=h2, pb=pb, psl=psl: e.matmul(
                            p3[psl, 64:192], lhsT=ZF[:, h2, 128:192], rhs=SC[sl][h2][:, 128:256],
                            start=True, stop=True, tile_position=(0, pb)), [zr, 'SC%d_%d' % (sl, h2)], ['pb0'])
                    for h2 in range(2):
                        psl = slice(64 * h2, 64 * h2 + 64)
                        ew('dve', lambda e, c=c, psl=psl, h2=h2: e.scalar_tensor_tensor(
                            out=MCz[sl][psl, h2, :], in0=E2[psl, :], scalar=PC[psl, c:c + 1], in1=p3[psl, 0:64],
                            op0=ALU.mult, op1=ALU.add), ['E2', 'PC', 'pb0'], ['MC%d' % sl])
                    ew('dve', lambda e, c=c: e.tensor_tensor(out=QT[sl], in0=AR[:, c, 1, :], in1=p3[:, 64:192],
                                                             op=ALU.add), ['pb0', 'AR1'], ['QT%d' % sl])

                def partD(c, sl):
                    cs_ = slice(c * 128, (c + 1) * 128)
                    tm = 'TM4_%d' % c
                    ZF = LZ[sl][1]
                    for h2 in range(2):
                        pb = 64 * h2
                        psl = slice(pb, pb + 64)
                        sb_ = 3 if h2 == 0 else 0
                        sr_ = 'pb%d' % sb_
                        psY = bank(sb_)[:, 0:128]
                        psS = bank(sb_)[:, 128:192]
                        UU = ZF[:, h2, 192:256]
                        zr = 'ZZ%d_1_%d' % (sl, h2)
                        scr = 'SC%d_%d' % (sl, h2)
                        ew('pe', lambda e, psl=psl, pb=pb, h2=h2, UU=UU, psY=psY: e.matmul(
                            psY[psl, :], lhsT=UU, rhs=SC[sl][h2][:, 128:256], start=True, stop=False,
                            tile_position=(0, pb)), [zr, scr], [sr_])
                        ew('pe', lambda e, psl=psl, pb=pb, h2=h2, c=c, psY=psY: e.matmul(
                            psY[psl, :], lhsT=TM4[:, c, 1, pb:pb + 64], rhs=SC[sl][h2][:, 384:512], start=False,
                            stop=False, tile_position=(0, pb)), [tm, scr], [sr_])
                        ew('pe', lambda e, psl=psl, pb=pb, psY=psY, h2=h2: e.matmul(
                            psY[psl, :], lhsT=STz[:, h2, :], rhs=QT[sl], start=False, stop=True,
                            tile_position=(0, pb)), ['ST', 'QT%d' % sl], [sr_])
                        ew('pe', lambda e, psl=psl, pb=pb, psS=psS, h2=h2: e.matmul(
                            psS[psl, :], lhsT=MCz[sl][:, h2, :], rhs=STz[:, h2, :], start=True, stop=False,
                            tile_position=(0, pb)), ['MC%d' % sl, 'ST'], [sr_])
                        ew('pe', lambda e, psl=psl, pb=pb, c=c, UU=UU, psS=psS: e.matmul(
                            psS[psl, :], lhsT=TM4[:, c, 2, pb:pb + 64], rhs=UU, start=False, stop=False,
                            tile_position=(0, pb)), [tm, zr], [sr_])
                        ew('pe', lambda e, psl=psl, pb=pb, c=c, psS=psS: e.matmul(
                            psS[psl, :], lhsT=TM4[:, c, 3, pb:pb + 64], rhs=TM4[:, c, 1, pb:pb + 64], start=False,
                            stop=True, tile_position=(0, pb)), [tm], [sr_])
                    for h2 in range(2):
                        pb = 64 * h2
                        psl = slice(pb, pb + 64)
                        sb_ = 3 if h2 == 0 else 0
                        sr_ = 'pb%d' % sb_
                        ew('act', lambda e, cs_=cs_, psl=psl, sb_=sb_: e.copy(out=T['Y32'][psl, cs_],
                                                                             in_=bank(sb_)[psl, 0:128]),
                           [sr_], ['Y32'])
                        ew('dve', lambda e, psl=psl, sb_=sb_, h2=h2: e.tensor_copy(out=STz[psl, h2, :],
                                                                                   in_=bank(sb_)[psl, 128:192]),
                           [sr_], ['ST'])

                for c0 in range(0, NCH, 2):
                    for sl in range(2):
                        partA(c0 + sl, sl)
                    for n in range(7):
                        for sl in range(2):
                            partB(c0 + sl, sl, n)
                    for sl in range(2):
                        partC(c0 + sl, sl)
                    for sl in range(2):
                        partD(c0 + sl, sl)
                ew('pool', lambda e: e.tensor_copy(out=T['YB'], in_=T['Y32']), ['Y32'], ['YB'])
                for hf in range(2):
                    hs = slice(hf * 512, (hf + 1) * 512)
                    ew('pe', lambda e, hs=hs: e.matmul(bank(1), lhsT=bones, rhs=T['YB'][:, hs], start=True, stop=True),
                       ['bones', 'YB'], ['pb1'])
                    ew('dve', lambda e, hs=hs: e.scalar_tensor_tensor(out=T['DD'][:, hs], in0=bank(1), scalar=-1.0 / 64,
                                                                      in1=T['Y32'][:, hs], op0=ALU.mult, op1=ALU.add),
                       ['pb1', 'Y32'], ['DD'])
                ew('act', lambda e: e.activation(out=T['TQ'], in_=T['DD'], func=AF.Square), ['DD'], ['TQ'])
                for hf in range(2):
                    hs = slice(hf * 512, (hf + 1) * 512)
                    ew('pe', lambda e, hs=hs: e.matmul(bank(2), lhsT=bones, rhs=T['TQ'][:, hs], start=True, stop=True),
                       ['bones', 'TQ'], ['pb2'])
                    ew('act', lambda e, hs=hs: e.activation(out=T['T1'][:, hs], in_=bank(2), func=AF.Ln, scale=1.0 / 64,
                                                            bias=GN_EPS), ['pb2'], ['T1'])
                ew('act', lambda e: e.activation(out=T['T1'], in_=T['T1'], func=AF.Exp, scale=-0.5), ['T1'], ['T1'])
                ew('dve', lambda e: e.tensor_tensor(out=T['DD'], in0=T['DD'], in1=T['T1'], op=ALU.mult),
                   ['DD', 'T1'], ['DD'])
                ew('dve', lambda e, hp=hp: e.tensor_scalar(out=T['DD'], in0=T['DD'], scalar1=prm[:, 5, hp:hp + 1],
                                                           scalar2=prm[:, 6, hp:hp + 1], op0=ALU.mult, op1=ALU.add),
                   ['DD', 'prm'], ['DD'])
                ew('pool', lambda e: e.tensor_tensor(out=T['DD'], in0=T['DD'], in1=T['BV'], op=ALU.add),
                   ['DD', 'BV'], ['DD'])
                ew('dve', lambda e: e.tensor_tensor(out=T['YO'], in0=T['DD'], in1=T['GG'], op=ALU.mult),
                   ['DD', 'GG'], ['YO'])
                P.op('sp', lambda e, hp=hp, tsl=tsl: e.dma_start(out=ya_d[hp, :, tsl], in_=T['YO']),
                     reads=[R_('YO')], writes=['d_ya'], dma_key=R_('stYO'))

    yb_d = nc.dram_tensor("yb_scr", [6, 128, S], BF16, kind="ExternalOutput" if debug else "Internal").ap()
    DIL = (1, 4, 16)

    def attn_stage_full(js=range(4)):
        A.off = const_mark
        pre = 'a_'

        def R_(n):
            return pre + n

        def nm(x):
            if x.startswith('pb') or x in ('ident', 'ident_f'):
                return x
            return R_(x)

        def ew(eng, fn, reads, writes):
            P.op(eng, fn, reads=[nm(x) for x in reads], writes=[nm(x) for x in writes])
        QH = A.alloc([S], BF16)
        KH = A.alloc([S], BF16)
        VX = A.alloc([32, 64], BF16)
        ONES = A.alloc([64], BF16)
        OT = [A.alloc([S], F32) for _ in range(3)]
        DEN = [A.alloc([S], F32) for _ in range(3)]
        PT = [A.alloc([256], BF16) for _ in range(4)]
        mask2 = A.alloc([256], BF16)
        RD = [A.alloc([512], F32) for _ in range(2)]
        YBS = [A.alloc([512], BF16) for _ in range(2)]
        print("attn_stage arena", A.off)
        P.op('pool', lambda e: e.memset(mask2, 1.0), writes=[R_('mask')])
        P.op('pool', lambda e: e.affine_select(out=mask2[:, 0:128], in_=mask2[:, 0:128], pattern=[[1, 128]],
                                               compare_op=ALU.is_ge, fill=0.0, base=0, channel_multiplier=-1),
             reads=[R_('mask')], writes=[R_('mask')])
        P.op('pool', lambda e: e.affine_select(out=mask2[:, 128:256], in_=mask2[:, 128:256], pattern=[[-1, 128]],
                                               compare_op=ALU.is_ge, fill=0.0, base=0, channel_multiplier=1),
             reads=[R_('mask')], writes=[R_('mask')])
        P.op('pool', lambda e: e.memset(ONES, 1.0), writes=[R_('ONES')])

        tcount = [0]
        ccount = [0]
        for j in js:
            for g in range(3):
                d = DIL[g]
                nb = S // d // 128
                h = 4 * g + j
                pair = h // 2
                pb = 64 * (h % 2)
                vv = v_d.rearrange("(m d) c -> d m c", d=d)
                for r in range(d):
                    for n0 in range(0, nb, 8):
                        n1 = min(nb, n0 + 8)
                        P.op('sp', lambda e, r=r, h=h, nb=nb, vv=vv, n0=n0, n1=n1: e.dma_start(
                            out=VX[:, r * nb + n0:r * nb + n1, :],
                            in_=vv[r, n0 * 128:n1 * 128, h * 64:(h + 1) * 64].rearrange("(n i) c -> i n c", i=128)),
                            writes=[R_('VX')], dma_key=R_('ldV'))
                P.op('sp', lambda e, pair=pair, pb=pb: e.dma_start(out=QH[0:64, :], in_=qk_d[pair, pb:pb + 64, :]),
                     writes=[R_('QH')], dma_key=R_('ldQ'))
                P.op('sp', lambda e, pair=pair, pb=pb: e.dma_start(out=KH[0:64, :], in_=qk_d[6 + pair, pb:pb + 64, :]),
                     writes=[R_('KH')], dma_key=R_('ldK'))
                qv = QH.rearrange("p (m d) -> p d m", d=d)
                kv = KH.rearrange("p (m d) -> p d m", d=d)
                otr = 'OT%d' % g
                otv = OT[g].rearrange("p (m d) -> p d m", d=d)
                dnv = DEN[g].rearrange("p (m d) -> p d m", d=d)
                for r in range(d):
                    prev = None
                    for n in range(nb):
                        ti = tcount[0]
                        tcount[0] += 1
                        nq = 256 if n + 1 < nb else 128
                        sbk = 1 + ti % 2
                        ps = bank(sbk)[:, 0:nq]
                        pt = PT[ti % 4]
                        ptr = 'PT%d' % (ti % 4)
                        ew('pe', lambda e, ps=ps, kv=kv, qv=qv, r=r, n=n, nq=nq: e.matmul(
                            ps, lhsT=kv[0:64, r, 128 * n:128 * n + 128], rhs=qv[0:64, r, 128 * n:128 * n + nq],
                            start=True, stop=True), ['KH', 'QH'], ['pb%d' % sbk])
                        ew('act', lambda e, ps=ps, pt=pt, nq=nq: e.activation(out=pt[:, 0:nq], in_=ps, func=AF.Exp),
                           ['pb%d' % sbk], [ptr])
                        ew('pool', lambda e, pt=pt, nq=nq: e.tensor_tensor(out=pt[:, 0:nq], in0=pt[:, 0:nq],
                                                                          in1=mask2[:, 0:nq], op=ALU.mult),
                           [ptr, 'mask'], [ptr])
                        obk = 3 + ti % 2
                        po = bank(obk)[0:64, 0:128]
                        pdn = bank(obk)[0:64, 128:256]
                        vt = r * nb + n
                        if prev is not None:
                            ppt, pptr, pvt = prev
                            ew('pe', lambda e, po=po, ppt=ppt, pvt=pvt: e.matmul(
                                po, lhsT=VX[:, pvt, :], rhs=ppt[:, 128:256], start=True, stop=False),
                                ['VX', pptr], ['pb%d' % obk])
                        ew('pe', lambda e, po=po, pt=pt, vt=vt, first=(prev is None): e.matmul(
                            po, lhsT=VX[:, vt, :], rhs=pt[:, 0:128], start=first, stop=True),
                            ['VX', ptr], ['pb%d' % obk])
                        if prev is not None:
                            ew('pe', lambda e, pdn=pdn, ppt=ppt: e.matmul(
                                pdn, lhsT=ONES, rhs=ppt[:, 128:256], start=True, stop=False),
                                ['ONES', pptr], ['pb%d' % obk])
                        ew('pe', lambda e, pdn=pdn, pt=pt, first=(prev is None): e.matmul(
                            pdn, lhsT=ONES, rhs=pt[:, 0:128], start=first, stop=True),
                            ['ONES', ptr], ['pb%d' % obk])
                        ew('dve', lambda e, po=po, otv=otv, r=r, n=n: e.tensor_copy(
                            out=otv[0:64, r, 128 * n:128 * n + 128], in_=po), ['pb%d' % obk], [otr])
                        ew('dve', lambda e, pdn=pdn, dnv=dnv, r=r, n=n: e.tensor_copy(
                            out=dnv[0:64, r, 128 * n:128 * n + 128], in_=pdn), ['pb%d' % obk], ['DEN%d' % g])
                        prev = (pt, ptr, vt)
            for ck in range(S // 512):
                csl = slice(ck * 512, (ck + 1) * 512)
                cc = ccount[0]
                ccount[0] += 1
                rd = RD[cc % 2]
                ew('pool', lambda e, rd=rd, csl=csl: e.tensor_tensor(out=rd[0:64, :], in0=DEN[0][0:64, csl],
                                                                     in1=DEN[1][0:64, csl], op=ALU.add),
                   ['DEN0', 'DEN1'], ['RD%d' % (cc % 2)])
                ew('pool', lambda e, rd=rd, csl=csl: e.tensor_tensor(out=rd[0:64, :], in0=rd[0:64, :],
                                                                     in1=DEN[2][0:64, csl], op=ALU.add),
                   ['RD%d' % (cc % 2), 'DEN2'], ['RD%d' % (cc % 2)])
                ew('dve', lambda e, rd=rd: e.reciprocal(out=rd[0:64, :], in_=rd[0:64, :]), ['RD%d' % (cc % 2)],
                   ['RD%d' % (cc % 2)])
                for g in range(3):
                    h = 4 * g + j
                    pair = h // 2
                    pb = 64 * (h % 2)
                    yi = (cc * 3 + g) % 2
                    ys = YBS[yi]
                    ew('pool', lambda e, g=g, csl=csl, rd=rd, ys=ys: e.tensor_tensor(
                        out=ys[0:64, :], in0=OT[g][0:64, csl], in1=rd[0:64, :], op=ALU.mult),
                        ['OT%d' % g, 'RD%d' % (cc % 2)], ['YBS%d' % yi])
                    P.op('sp', lambda e, pair=pair, pb=pb, csl=csl, ys=ys: e.dma_start(
                        out=yb_d[pair, pb:pb + 64, csl], in_=ys[0:64, :]),
                        reads=[R_('YBS%d' % yi)], writes=['d_yb'], dma_key=R_('stYB%d' % yi))

    wpr_d = din("w_proj_rwkv", [1024, 1024])
    wpa_d = din("w_proj_attn", [768, 1024])
    wo_d = din("w_out", [1024, 1024])
    x2_d = nc.dram_tensor("x2_scr", [S, D], F32, kind="ExternalOutput" if debug else "Internal").ap()

    def merge_stage():
        A.off = const_mark
        pre = 'm_'

        def R_(n):
            return pre + n

        def nm(x):
            if x.startswith('pb'):
                return x
            return R_(x)

        def ew(eng, fn, reads, writes):
            P.op(eng, fn, reads=[nm(x) for x in reads], writes=[nm(x) for x in writes])
        Wr = A.alloc([8, 1024], BF16)
        Wa = A.alloc([6, 1024], BF16)
        Wo = A.alloc([8, 1024], BF16)
        X = [A.alloc([NSUB, D], F32) for _ in range(2)]
        YA = [A.alloc([8, TT], BF16) for _ in range(2)]
        YB = [A.alloc([6, TT], BF16) for _ in range(2)]
        G = [A.alloc([16, TT], BF16) for _ in range(2)]
        MT = A.alloc([8, TT], BF16)
        t1 = [A.alloc([TT], F32) for _ in range(2)]
        t2 = [A.alloc([TT], F32) for _ in range(2)]
        print("merge_stage arena", A.off)
        for kc in range(8):
            P.op('pool', lambda e, kc=kc: e.dma_start(out=Wr[:, kc, :], in_=wpr_d[kc * 128:(kc + 1) * 128, :]),
                 writes=[R_('Wr')], dma_key=R_('w'))
            P.op('pool', lambda e, kc=kc: e.dma_start(out=Wo[:, kc, :], in_=wo_d[kc * 128:(kc + 1) * 128, :]),
                 writes=[R_('Wo')], dma_key=R_('w'))
        for kc in range(6):
            P.op('pool', lambda e, kc=kc: e.dma_start(out=Wa[:, kc, :], in_=wpa_d[kc * 128:(kc + 1) * 128, :]),
                 writes=[R_('Wa')], dma_key=R_('w'))
        srcv = x1_d.rearrange("(n s p) d -> n p s d", p=128, s=NSUB)
        dstv = x2_d.rearrange("(n s p) d -> n p s d", p=128, s=NSUB)
        for it in range(NT):
            sl = it % 2
            tsl = slice(it * TT, (it + 1) * TT)
            sfx = '%d' % sl
            P.op('sp', lambda e, it=it, sl=sl: e.dma_start(out=X[sl], in_=srcv[it]), writes=[R_('X' + sfx)],
                 dma_key=R_('ldX' + sfx))
            P.op('sp', lambda e, sl=sl, tsl=tsl: e.dma_start(out=YA[sl], in_=ya_d.rearrange("b p t -> p b t")[:, :, tsl]),
                 writes=[R_('YA' + sfx)], dma_key=R_('ldYA' + sfx))
            P.op('sp', lambda e, sl=sl, tsl=tsl: e.dma_start(out=YB[sl], in_=yb_d.rearrange("b p t -> p b t")[:, :, tsl]),
                 writes=[R_('YB' + sfx)], dma_key=R_('ldYB' + sfx))
            P.op('sp', lambda e, sl=sl, tsl=tsl: e.dma_start(out=G[sl], in_=gate_d.rearrange("b p t -> p b t")[:, :, tsl]),
                 writes=[R_('G' + sfx)], dma_key=R_('ldG' + sfx))
            for c in range(8):
                bk = 1 + c % 2
                pg = bank(bk)
                for kc in range(8):
                    ew('pe', lambda e, c=c, kc=kc, pg=pg, sl=sl: e.matmul(
                        pg[:, 0:TT], lhsT=Wr[:, kc, c * 128:(c + 1) * 128], rhs=YA[sl][:, kc, :],
                        start=(kc == 0), stop=(kc == 7)), ['Wr', 'YA' + sfx], ['pb%d' % bk])
                for kc in range(6):
                    ew('pe', lambda e, c=c, kc=kc, pg=pg, sl=sl: e.matmul(
                        pg[:, TT:2 * TT], lhsT=Wa[:, kc, c * 128:(c + 1) * 128], rhs=YB[sl][:, kc, :],
                        start=(kc == 0), stop=(kc == 5)), ['Wa', 'YB' + sfx], ['pb%d' % bk])
                q = c % 2
                ew('dve', lambda e, c=c, pg=pg, sl=sl, q=q: e.tensor_tensor(out=t1[q], in0=G[sl][:, c, :],
                                                                           in1=pg[:, 0:TT], op=ALU.mult),
                   ['G' + sfx, 'pb%d' % bk], ['t1%d' % q])
                ew('dve', lambda e, c=c, pg=pg, sl=sl, q=q: e.tensor_tensor(out=t2[q], in0=G[sl][:, 8 + c, :],
                                                                           in1=pg[:, TT:2 * TT], op=ALU.mult),
                   ['G' + sfx, 'pb%d' % bk], ['t2%d' % q])
                ew('pool', lambda e, c=c, q=q: e.tensor_tensor(out=MT[:, c, :], in0=t1[q], in1=t2[q], op=ALU.add),
                   ['t1%d' % q, 't2%d' % q], ['MT'])
            for s in range(NSUB):
                for dh in range(2):
                    bk = 3 + (s * 2 + dh) % 2
                    pd = bank(bk)
                    for c in range(8):
                        ew('pe', lambda e, c=c, s=s, dh=dh, pd=pd: e.matmul(
                            pd, lhsT=MT[:, c, s * 128:(s + 1) * 128], rhs=Wo[:, c, dh * 512:(dh + 1) * 512],
                            start=(c == 0), stop=(c == 7)), ['MT', 'Wo'], ['pb%d' % bk])
                    ew('dve', lambda e, s=s, dh=dh, pd=pd, sl=sl: e.tensor_tensor(
                        out=X[sl][:, s, dh * 512:(dh + 1) * 512], in0=X[sl][:, s, dh * 512:(dh + 1) * 512], in1=pd,
                        op=ALU.add), ['pb%d' % bk, 'X' + sfx], ['X' + sfx])
            P.op('sp', lambda e, it=it, sl=sl: e.dma_start(out=dstv[it], in_=X[sl]),
                 reads=[R_('X' + sfx)], writes=['d_x2'], dma_key=R_('stX' + sfx))

    if 'ffn1' in stages:
        ffn_stage(0, x, x1_d)
        P.sync_all()
    if 'proj' in stages:
        proj_stage()
        P.sync_all()
    if 'rwkv' in stages:
        rwkv_stage(range(NPAIRS_DBG))
        P.sync_all()
    if 'attn' in stages:
        attn_stage_full(JS_DBG)
        P.sync_all()
    if 'merge' in stages:
        merge_stage()
        P.sync_all()
    if 'ffn2' in stages:
        ffn_stage(1, x2_d if 'merge' in stages else x1_d, out)
    P.sync_all()
    P.op('sp', None)
    P.finalize_and_emit(stack)
    stack.close()
    return nc


_CACHE = {}


SHARED_KEYS = ['ffn1_norm', 'ffn1_w_in', 'ffn1_w_out', 'ffn2_norm', 'ffn2_w_in', 'ffn2_w_out',
               'w_in', 'mix_norm', 'rwkv_mu', 'b_gate', 'attn_q_norm', 'attn_k_norm',
               'w_proj_rwkv', 'w_proj_attn', 'w_out', 'rwkv_w2', 'rwkv_a2', 'rwkv_g2', 'rwkv_w0', 'rwkv_a0', 'rwkv_k_k', 'rwkv_k_a', 'rwkv_r_k', 'rwkv_ln_w', 'rwkv_ln_b']


def make_shared(inputs):
    shared = {}
    for k in SHARED_KEYS:
        v = np.asarray(inputs[k], dtype=np.float32)
        v = v.reshape(v.shape[1:])
        if k == 'rwkv_r_k':
            v = v.reshape(-1)
        shared[k] = np.ascontiguousarray(v)
    return shared


def kernel(**inputs):
    if 'nc' not in _CACHE:
        _CACHE['nc'] = build_program()
    nc = _CACHE['nc']
    x = np.ascontiguousarray(inputs['x'], dtype=np.float32)
    shared = make_shared(inputs)
    in_maps = []
    for c in range(NCORES):
        m = dict(shared)
        m['x'] = x[c]
        in_maps.append(m)
    res = run_bass_kernel_spmd(nc, in_maps, core_ids=list(range(NCORES)))
    return np.stack([np.asarray(r['out']) for r in res.results], axis=0)
```

```python
import numpy as np
from contextlib import ExitStack
import concourse.bass as bass
import concourse.mybir as mybir
from concourse.bass_utils import run_bass_kernel_spmd
from concourse.alu_op_type import AluOpType as ALU

F32 = mybir.dt.float32
BF16 = mybir.dt.bfloat16
AF = mybir.ActivationFunctionType
AX = mybir.AxisListType

S = 4096
D = 1024
DFF = 2816
NCORES = 8
RMS_EPS = 1e-6

ENGS = ['pe', 'act', 'dve', 'pool', 'sp']
MAXOPS = [0]
SEM_LIM = 30000
DMA_LIM = 1800


class Prog:
    def __init__(self, nc):
        self.nc = nc
        self.ops = []
        self.eng_ops = {e: [] for e in ENGS}
        self.last_w = {}
        self.readers = {}
        self.dma_cnt = {}
        self.barrier = {e: None for e in ENGS}

    def op(self, eng, fn, reads=(), writes=(), dma_key=None):
        mo = MAXOPS[0]
        if mo and len(self.ops) >= mo and fn is not None:
            return None
        if mo and len(self.ops) == mo - 1 and fn is not None:
            print("LAST OP:", eng, fn.__code__.co_firstlineno, reads, writes)
        oid = len(self.ops)
        deps = set()
        dma_deps = {}
        writes = list(writes) + [r for r in reads if (r.startswith('pb') or r.startswith('ps')) and r not in writes]

        def add(o):
            od = self.ops[o]
            if od['dma_key'] is not None:
                k = od['dma_key']
                dma_deps[k] = self.dma_cnt[k]
            else:
                deps.add(o)
        for r in reads:
            if r in self.last_w:
                add(self.last_w[r])
        for w in writes:
            if w in self.last_w:
                add(self.last_w[w])
            for rd in self.readers.get(w, {}).values():
                add(rd)
        if self.barrier[eng] is not None:
            bd, bdma = self.barrier[eng]
            for o in bd:
                deps.add(o)
            for k, v in bdma.items():
                dma_deps[k] = max(dma_deps.get(k, 0), v)
            self.barrier[eng] = None
        cnt = None
        if dma_key is not None:
            self.dma_cnt[dma_key] = self.dma_cnt.get(dma_key, 0) + 1
            cnt = self.dma_cnt[dma_key]
        o = dict(id=oid, eng=eng, fn=fn, deps=deps, dma_deps=dma_deps, dma_key=dma_key,
                 dma_cnt=cnt, idx=len(self.eng_ops[eng]), sig=False)
        self.ops.append(o)
        self.eng_ops[eng].append(o)
        ch = eng if dma_key is None else 'dma:' + dma_key
        for r in reads:
            self.readers.setdefault(r, {})[ch] = oid
        for w in writes:
            self.last_w[w] = oid
            self.readers[w] = {}
        return oid

    def sync_all(self):
        bd = set()
        for e in ENGS:
            for o in reversed(self.eng_ops[e]):
                if o['dma_key'] is None and o['fn'] is not None:
                    bd.add(o['id'])
                    break
        bdma = dict(self.dma_cnt)
        for e in ENGS:
            self.barrier[e] = (set(bd), dict(bdma))

    def finalize_and_emit(self, stack):
        nc = self.nc
        for o in self.ops:
            per = {}
            for d in o['deps']:
                od = self.ops[d]
                if od['eng'] == 'pe' and o['eng'] == 'pe':
                    continue
                e = od['eng']
                if e not in per or self.ops[per[e]]['idx'] < od['idx']:
                    per[e] = d
            o['cdeps'] = per
            for d in per.values():
                self.ops[d]['sig'] = True
        sems = {}

        def get_sem(name):
            return sems[name]
        for e in ENGS:
            c = 0
            for o in self.eng_ops[e]:
                if o['dma_key'] is None and o['sig']:
                    c += 1
                    o['sigval'] = c
        for o in self.ops:
            waits = {}
            for e, d in o['cdeps'].items():
                v = self.ops[d]['sigval']
                key = ('c_%s_%d' % (e, (v - 1) // SEM_LIM))
                val = (v - 1) % SEM_LIM + 1
                waits[key] = max(waits.get(key, 0), val)
            for k, n in o['dma_deps'].items():
                key = ('d_%s_%d' % (k, (n - 1) // DMA_LIM))
                val = 16 * ((n - 1) % DMA_LIM + 1)
                waits[key] = max(waits.get(key, 0), val)
            o['waits'] = waits
        names = set()
        for o in self.ops:
            names.update(o['waits'].keys())
            if o['dma_key'] is not None:
                names.add('d_%s_%d' % (o['dma_key'], (o['dma_cnt'] - 1) // DMA_LIM))
            elif o['sig']:
                names.add('c_%s_%d' % (o['eng'], (o['sigval'] - 1) // SEM_LIM))
        for nm in sorted(names):
            sems[nm] = stack.enter_context(nc.semaphore(nm))
        print("n_sems", len(names), "n_ops", len(self.ops), {e: len(v) for e, v in self.eng_ops.items()})
        block = stack.enter_context(nc.Block())
        decos = {'pe': block.tensor, 'act': block.scalar, 'dve': block.vector,
                 'pool': block.gpsimd, 'sp': block.sync}
        for e in ENGS:
            ops = self.eng_ops[e]

            def body(eng, ops=ops, e=e):
                waited = {}
                for o in ops:
                    for key, val in o['waits'].items():
                        if waited.get(key, 0) >= val:
                            continue
                        waited[key] = val
                        eng.wait_ge(get_sem(key), val)
                    if o['fn'] is None:
                        continue
                    ins = o['fn'](eng)
                    if o['dma_key'] is not None:
                        n = o['dma_cnt']
                        ins.then_inc(get_sem('d_%s_%d' % (o['dma_key'], (n - 1) // DMA_LIM)), 16)
                    elif o['sig']:
                        v = o['sigval']
                        ins.then_inc(get_sem('c_%s_%d' % (e, (v - 1) // SEM_LIM)), 1)
            decos[e](body)


class Arena:
    def __init__(self, tensor, nbytes):
        self.t = tensor
        self.nbytes = nbytes
        self.off = 0

    def alloc(self, shape, dtype, parts=128):
        n = int(np.prod(shape))
        esz = 4 if dtype == F32 else 2
        nb = n * esz
        nb_al = (nb + 63) // 64 * 64
        assert self.off + nb_al <= self.nbytes, ("SBUF arena overflow", self.off, nb_al)
        ap = self.t[0:parts, self.off // 2:(self.off + nb) // 2]
        self.off += nb_al
        if dtype == F32:
            ap = ap.bitcast(F32)
        if len(shape) == 2:
            ap = ap.rearrange("p (a b) -> p a b", a=shape[0], b=shape[1])
        elif len(shape) == 3:
            ap = ap.rearrange("p (a b c) -> p a b c", a=shape[0], b=shape[1], c=shape[2])
        return ap


def build_program(debug=False, NPAIRS_DBG=8, stages=('ffn1', 'proj', 'rwkv', 'attn', 'merge', 'ffn2'), NB_DBG=None,
                  JS_DBG=range(4)):
    nc = bass.Bass("TRN2", target_bir_lowering=False)
    P = Prog(nc)

    def din(name, shape):
        return nc.dram_tensor(name, list(shape), F32, kind="ExternalInput").ap()
    x = din("x", [S, D])
    ffn_norm = [din("ffn1_norm", [D]), din("ffn2_norm", [D])]
    ffn_win = [din("ffn1_w_in", [D, 2 * DFF]), din("ffn2_w_in", [D, 2 * DFF])]
    ffn_wout = [din("ffn1_w_out", [DFF, D]), din("ffn2_w_out", [DFF, D])]
    out = nc.dram_tensor("out", [S, D], F32, kind="ExternalOutput").ap()
    x1_d = nc.dram_tensor("x1_scr", [S, D], F32, kind="ExternalOutput" if debug else "Internal").ap()

    stack = ExitStack()
    ARENA_BYTES = 200 * 1024
    arena_t = stack.enter_context(nc.sbuf_tensor("arena", [128, ARENA_BYTES // 2], BF16))
    A = Arena(arena_t, ARENA_BYTES)
    psum = stack.enter_context(nc.psum_tensor("psum", [128, 4096], F32))

    def bank(b, n=512, off=0):
        return psum[:, b * 512 + off:b * 512 + off + n]

    ident_f = A.alloc([128], F32)
    ident = A.alloc([128], BF16)
    ones_col = A.alloc([1], F32)

    P.op('pool', lambda e: e.memset(ident_f, 0.0), writes=['ident_f'])
    P.op('pool', lambda e: e.affine_select(out=ident_f, in_=ident_f, pattern=[[-1, 128]],
                                           compare_op=ALU.not_equal, fill=1.0, base=0, channel_multiplier=1),
         reads=['ident_f'], writes=['ident_f'])
    P.op('dve', lambda e: e.tensor_copy(out=ident, in_=ident_f), reads=['ident_f'], writes=['ident'])

    const_mark = A.off

    TT = 256
    NSUB = TT // 128
    NT = S // TT
    KC = D // 128
    FC = DFF // 128

    def ffn_stage(si, src, dst):
        A.off = const_mark
        W1 = A.alloc([KC, 2 * DFF], BF16)
        W2 = A.alloc([FC, D], BF16)
        gb = A.alloc([D], F32)
        xt = [A.alloc([NSUB, D], F32) for _ in range(2)]
        hb = [A.alloc([D], BF16) for _ in range(2)]
        hT = [A.alloc([KC, TT], BF16) for _ in range(2)]
        actT = A.alloc([FC, TT], BF16)
        sg = [A.alloc([TT], F32) for _ in range(2)]
        junk = A.alloc([D], BF16)
        ss = A.alloc([8], F32)
        pre = 's%d_' % si
        w1v = ffn_win[si].rearrange("(kc p) f -> p kc f", p=128)
        CH = 1408
        for kc in range(KC):
            for c in range(2 * DFF // CH):
                P.op('pool', lambda e, kc=kc, c=c: e.dma_start(out=W1[:, kc, c * CH:(c + 1) * CH],
                                                              in_=w1v[:, kc, c * CH:(c + 1) * CH]),
                     writes=[pre + 'W1'], dma_key=pre + 'W1')
        w2v = ffn_wout[si].rearrange("(fc p) d -> p fc d", p=128)
        for fc in range(FC):
            P.op('pool', lambda e, fc=fc: e.dma_start(out=W2[:, fc, :], in_=w2v[:, fc, :]),
                 writes=[pre + 'W2'], dma_key=pre + 'W2')
        P.op('sp', lambda e: e.dma_start(out=gb, in_=ffn_norm[si].partition_broadcast(128)),
             writes=[pre + 'gb'], dma_key=pre + 'gb')
        srcv = src.rearrange("(n s p) d -> n p s d", p=128, s=NSUB)
        dstv = dst.rearrange("(n s p) d -> n p s d", p=128, s=NSUB)
        for it in range(NT):
            sl = it % 2
            X = xt[sl]
            xr = pre + 'xt%d' % sl
            P.op('sp', lambda e, it=it, X=X: e.dma_start(out=X, in_=srcv[it]),
                 writes=[xr], dma_key=xr)
            HT = hT[sl]
            for s in range(NSUB):
                hs = (it * NSUB + s) % 2
                H = hb[hs]
                hr = pre + 'hb%d' % hs
                P.op('act', lambda e, X=X, s=s: e.activation(out=junk, in_=X[:, s, :], func=AF.Square,
                                                             accum_out=ss[:, 0:1]),
                     reads=[xr], writes=[pre + 'junk', pre + 'ss'])
                P.op('act', lambda e: e.activation(out=ss[:, 1:2], in_=ss[:, 0:1], func=AF.Sqrt,
                                                   scale=1.0 / D, bias=RMS_EPS),
                     reads=[pre + 'ss'], writes=[pre + 'ss1'])
                P.op('dve', lambda e: e.reciprocal(out=ss[:, 2:3], in_=ss[:, 1:2]),
                     reads=[pre + 'ss1'], writes=[pre + 'ss2'])
                P.op('dve', lambda e, X=X, s=s, H=H: e.scalar_tensor_tensor(
                    out=H, in0=X[:, s, :], scalar=ss[:, 2:3], in1=gb, op0=ALU.mult, op1=ALU.mult),
                    reads=[xr, pre + 'ss2', pre + 'gb'], writes=[hr])
                pT = bank(0).bitcast(BF16)
                for kc in range(KC):
                    P.op('pe', lambda e, kc=kc, H=H, pT=pT: e.transpose(
                        out=pT[:, kc * 128:(kc + 1) * 128], in_=H[:, kc * 128:(kc + 1) * 128], identity=ident),
                        reads=[hr, 'ident'], writes=['psT'])
                P.op('act', lambda e, HT=HT, s=s, pT=pT: e.copy(
                    out=HT[:, :, s * 128:(s + 1) * 128], in_=pT.rearrange("p (k t) -> p k t", k=KC)),
                    reads=['psT'], writes=[pre + 'hT%d' % sl])
            for fc in range(FC):
                b = 1 + fc % 2
                pg = bank(b)
                for half in range(2):
                    col = half * DFF + fc * 128
                    for kc in range(KC):
                        P.op('pe', lambda e, kc=kc, col=col, half=half, pg=pg, HT=HT: e.matmul(
                            pg[:, half * TT:(half + 1) * TT], lhsT=W1[:, kc, col:col + 128], rhs=HT[:, kc, :],
                            start=(kc == 0), stop=(kc == KC - 1)),
                            reads=[pre + 'W1', pre + 'hT%d' % sl], writes=['psG%d' % b])
                SG = sg[fc % 2]
                P.op('act', lambda e, pg=pg, SG=SG: e.activation(out=SG, in_=pg[:, 0:TT], func=AF.Silu),
                     reads=['psG%d' % b], writes=[pre + 'sg%d' % (fc % 2)])
                P.op('dve', lambda e, pg=pg, SG=SG, fc=fc: e.tensor_tensor(
                    out=actT[:, fc, :], in0=SG, in1=pg[:, TT:2 * TT], op=ALU.mult),
                    reads=['psG%d' % b, pre + 'sg%d' % (fc % 2)], writes=[pre + 'actT'])
            for s in range(NSUB):
                for dh in range(2):
                    b = 3 + (s * 2 + dh) % 2
                    pd = bank(b)
                    for fc in range(FC):
                        P.op('pe', lambda e, fc=fc, s=s, dh=dh, pd=pd: e.matmul(
                            pd, lhsT=actT[:, fc, s * 128:(s + 1) * 128], rhs=W2[:, fc, dh * 512:(dh + 1) * 512],
                            start=(fc == 0), stop=(fc == FC - 1)),
                            reads=[pre + 'actT', pre + 'W2'], writes=['psD%d' % b])
                    P.op('dve', lambda e, X=X, s=s, dh=dh, pd=pd: e.scalar_tensor_tensor(
                        out=X[:, s, dh * 512:(dh + 1) * 512], in0=pd, scalar=0.5,
                        in1=X[:, s, dh * 512:(dh + 1) * 512], op0=ALU.mult, op1=ALU.add),
                        reads=['psD%d' % b, xr], writes=[xr])
            P.op('sp', lambda e, it=it, X=X: e.dma_start(out=dstv[it], in_=X),
                 reads=[xr], writes=[pre + 'dst'], dma_key=pre + 'st%d' % sl)

    NCOL = 7712
    w_in = din("w_in", [D, NCOL])
    mix_norm = din("mix_norm", [D])
    rwkv_mu = din("rwkv_mu", [3360])
    b_gate = din("b_gate", [2048])
    qn = din("attn_q_norm", [64])
    kn = din("attn_k_norm", [64])
    kscr = "ExternalOutput" if debug else "Internal"
    if 'proj' not in stages:
        kscr = "ExternalInput"
    rkv_d = nc.dram_tensor("rkv_scr", [24, 128, S], F32, kind=kscr).ap()
    ta_d = nc.dram_tensor("ta_scr", [128, S], BF16, kind=kscr).ap()
    tg_d = nc.dram_tensor("tg_scr", [160, S], BF16, kind=kscr).ap()
    qk_d = nc.dram_tensor("qk_scr", [12, 128, S], BF16, kind=kscr).ap()
    v_d = nc.dram_tensor("v_scr", [S, 768], BF16, kind=kscr).ap()
    gate_d = nc.dram_tensor("gate_scr", [16, 128, S], BF16, kind=kscr).ap()

    def proj_stage():
        A.off = const_mark
        pre = 'p_'
        W = A.alloc([KC, NCOL], BF16)
        gb = A.alloc([D], F32)
        X = A.alloc([NSUB, D], F32)
        hb = [A.alloc([D], BF16) for _ in range(2)]
        hT = [A.alloc([KC, TT], BF16) for _ in range(2)]
        ss = A.alloc([8], F32)
        mu_t = A.alloc([27], F32)
        bg_t = A.alloc([16], F32)
        qg_t = A.alloc([2], F32)
        carry = A.alloc([27], F32)
        psb = [A.alloc([TT + 1], F32) for _ in range(2)]
        tmp = [A.alloc([TT], F32) for _ in range(2)]
        sq = [A.alloc([TT], BF16) for _ in range(2)]
        lnb = [A.alloc([TT], F32) for _ in range(2)]
        rkv_st = A.alloc([24, TT], F32)
        ta_st = A.alloc([TT], BF16)
        tg_st = A.alloc([2, TT], BF16)
        qk_st = A.alloc([12, TT], BF16)
        gate_st = A.alloc([16, TT], BF16)
        v_st = A.alloc([NSUB, 768], BF16)
        bones = A.alloc([128], BF16)
        print("proj_stage arena", A.off)
        wv = w_in.rearrange("(kc p) f -> p kc f", p=128)
        CH = 964
        for kc in range(KC):
            for c in range(NCOL // CH):
                P.op('pool', lambda e, kc=kc, c=c: e.dma_start(out=W[:, kc, c * CH:(c + 1) * CH],
                                                              in_=wv[:, kc, c * CH:(c + 1) * CH]),
                     writes=[pre + 'W'], dma_key=pre + 'W')
        P.op('sp', lambda e: e.dma_start(out=gb, in_=mix_norm.partition_broadcast(128)),
             writes=[pre + 'gb'], dma_key=pre + 'par')
        P.op('sp', lambda e: e.dma_start(out=mu_t[:, 0:26], in_=rwkv_mu[0:3328].rearrange("(b p) -> p b", p=128),
                                         allow_slow_non_contiguous=True), writes=[pre + 'mu'], dma_key=pre + 'par')
        P.op('sp', lambda e: e.dma_start(out=mu_t[0:32, 26:27], in_=rwkv_mu[3328:3360].rearrange("(p o) -> p o", o=1)),
             writes=[pre + 'mu'], dma_key=pre + 'par')
        P.op('sp', lambda e: e.dma_start(out=bg_t, in_=b_gate.rearrange("(b p) -> p b", p=128),
                                         allow_slow_non_contiguous=True), writes=[pre + 'bg'], dma_key=pre + 'par')
        for hh in range(2):
            P.op('sp', lambda e, hh=hh: e.dma_start(out=qg_t[hh * 64:(hh + 1) * 64, 0:1],
                                                     in_=qn.rearrange("(p o) -> p o", o=1)),
                 writes=[pre + 'qg'], dma_key=pre + 'par')
            P.op('sp', lambda e, hh=hh: e.dma_start(out=qg_t[hh * 64:(hh + 1) * 64, 1:2],
                                                     in_=kn.rearrange("(p o) -> p o", o=1)),
                 writes=[pre + 'qg'], dma_key=pre + 'par')
        P.op('pool', lambda e: e.tensor_scalar(out=qg_t[:, 0:1], in0=qg_t[:, 0:1], scalar1=0.125, scalar2=None,
                                               op0=ALU.mult), reads=[pre + 'qg'], writes=[pre + 'qg'])
        P.op('pool', lambda e: e.memset(carry, 0.0), writes=[pre + 'carry%d' % i for i in range(27)])
        P.op('pool', lambda e: e.memset(bones, 0.0), writes=[pre + 'bones'])
        P.op('pool', lambda e: e.memset(bones[0:64, 0:64], 1.0), reads=[pre + 'bones'], writes=[pre + 'bones'])
        P.op('pool', lambda e: e.memset(bones[64:128, 64:128], 1.0), reads=[pre + 'bones'], writes=[pre + 'bones'])

        blocks = []
        for b in range(24):
            blocks.append((b * 128, 128, 'rkv', b))
        blocks.append((3072, 128, 'ta', 24))
        blocks.append((3200, 128, 'tg0', 25))
        blocks.append((3328, 32, 'tg1', 26))
        for b in range(6):
            blocks.append((3360 + b * 128, 128, 'q', b))
        for b in range(6):
            blocks.append((4128 + b * 128, 128, 'k', 6 + b))
        for b in range(16):
            blocks.append((5664 + b * 128, 128, 'gate', b))

        srcv = x1_d.rearrange("(n s p) d -> n p s d", p=128, s=NSUB)
        xr = pre + 'X'
        for it in range(NT):
            t0 = it * TT
            sl = it % 2
            P.op('sp', lambda e, it=it: e.dma_start(out=X, in_=srcv[it]), writes=[xr], dma_key=xr)
            HT = hT[sl]
            htr = pre + 'hT%d' % sl
            for s in range(NSUB):
                hs = (it * NSUB + s) % 2
                H = hb[hs]
                hr = pre + 'hb%d' % hs
                P.op('act', lambda e, s=s, H=H: e.activation(out=H, in_=X[:, s, :], func=AF.Square,
                                                             accum_out=ss[:, 0:1]),
                     reads=[xr], writes=[hr, pre + 'ss'])
                P.op('act', lambda e: e.activation(out=ss[:, 1:2], in_=ss[:, 0:1], func=AF.Sqrt,
                                                   scale=1.0 / D, bias=RMS_EPS),
                     reads=[pre + 'ss'], writes=[pre + 'ss1'])
                P.op('dve', lambda e: e.reciprocal(out=ss[:, 2:3], in_=ss[:, 1:2]),
                     reads=[pre + 'ss1'], writes=[pre + 'ss2'])
                P.op('dve', lambda e, s=s, H=H: e.scalar_tensor_tensor(
                    out=H, in0=X[:, s, :], scalar=ss[:, 2:3], in1=gb, op0=ALU.mult, op1=ALU.mult),
                    reads=[xr, pre + 'ss2', pre + 'gb'], writes=[hr])
                pT = bank(0).bitcast(BF16)
                for kc in range(KC):
                    P.op('pe', lambda e, kc=kc, H=H, pT=pT: e.transpose(
                        out=pT[:, kc * 128:(kc + 1) * 128], in_=H[:, kc * 128:(kc + 1) * 128], identity=ident),
                        reads=[hr, 'ident'], writes=['psT'])
                P.op('act', lambda e, HT=HT, s=s, pT=pT: e.copy(
                    out=HT[:, :, s * 128:(s + 1) * 128], in_=pT.rearrange("p (k t) -> p k t", k=KC)),
                    reads=['psT'], writes=[htr])

            pending = [None]

            def flush():
                if pending[0] is None:
                    return
                pg, pgr, j, kind, idx = pending[0]
                pending[0] = None
                pss = bank(5)[:, 0:TT]
                P.op('pe', lambda e, j=j, pss=pss: e.matmul(pss, lhsT=bones, rhs=sq[j], start=True, stop=True),
                     reads=[pre + 'sq%d' % j, pre + 'bones'], writes=['pss'])
                P.op('act', lambda e, j=j, pss=pss: e.activation(out=lnb[j], in_=pss, func=AF.Ln,
                                                                 scale=1.0 / 64, bias=RMS_EPS),
                     reads=['pss'], writes=[pre + 'lnb%d' % j])
                P.op('act', lambda e, j=j: e.activation(out=lnb[j], in_=lnb[j], func=AF.Exp, scale=-0.5),
                     reads=[pre + 'lnb%d' % j], writes=[pre + 'lnb%d' % j])
                c = 0 if kind == 'q' else 1
                P.op('dve', lambda e, j=j, pg=pg, idx=idx, c=c: e.scalar_tensor_tensor(
                    out=qk_st[:, idx, :], in0=pg, scalar=qg_t[:, c:c + 1], in1=lnb[j], op0=ALU.mult, op1=ALU.mult),
                    reads=[pgr, pre + 'lnb%d' % j, pre + 'qg'], writes=[pre + 'qk_st'])

            for bi, (col0, M, kind, idx) in enumerate(blocks):
                b = 1 + bi % 4
                pgr = 'psG%d' % b
                pg = bank(b)[0:M, 0:TT]
                j = bi % 2
                for kc in range(KC):
                    P.op('pe', lambda e, kc=kc, col0=col0, M=M, pg=pg, HT=HT: e.matmul(
                        pg, lhsT=W[:, kc, col0:col0 + M], rhs=HT[:, kc, :], start=(kc == 0), stop=(kc == KC - 1)),
                        reads=[pre + 'W', htr], writes=[pgr])
                flush()
                if kind in ('rkv', 'ta', 'tg0', 'tg1'):
                    cr = pre + 'carry%d' % idx
                    pbr = pre + 'psb%d' % j
                    tr = pre + 'tmp%d' % j
                    PS = psb[j][0:M]
                    TM = tmp[j][0:M]
                    P.op('pool', lambda e, PS=PS, idx=idx, M=M: e.tensor_copy(out=PS[:, 0:1], in_=carry[0:M, idx:idx + 1]),
                         reads=[cr], writes=[pbr])
                    P.op('act', lambda e, PS=PS, pg=pg: e.copy(out=PS[:, 1:TT + 1], in_=pg),
                         reads=[pgr, pbr], writes=[pbr])
                    P.op('dve', lambda e, PS=PS, TM=TM: e.tensor_tensor(out=TM, in0=PS[:, 0:TT], in1=PS[:, 1:TT + 1],
                                                                        op=ALU.subtract),
                         reads=[pbr], writes=[tr])
                    P.op('pool', lambda e, PS=PS, idx=idx, M=M: e.tensor_copy(out=carry[0:M, idx:idx + 1],
                                                                              in_=PS[:, TT:TT + 1]),
                         reads=[pbr], writes=[cr])
                    if kind == 'rkv':
                        P.op('dve', lambda e, PS=PS, TM=TM, idx=idx: e.scalar_tensor_tensor(
                            out=rkv_st[:, idx, :], in0=TM, scalar=mu_t[:, idx:idx + 1], in1=PS[:, 1:TT + 1],
                            op0=ALU.mult, op1=ALU.add),
                            reads=[tr, pbr, pre + 'mu'], writes=[pre + 'rkv_st%d' % (idx // 8)])
                    else:
                        P.op('dve', lambda e, PS=PS, TM=TM, idx=idx, M=M: e.scalar_tensor_tensor(
                            out=TM, in0=TM, scalar=mu_t[0:M, idx:idx + 1], in1=PS[:, 1:TT + 1],
                            op0=ALU.mult, op1=ALU.add),
                            reads=[tr, pbr, pre + 'mu'], writes=[tr])
                        if kind == 'ta':
                            P.op('act', lambda e, TM=TM: e.activation(out=ta_st[0:64], in_=TM[0:64], func=AF.Tanh),
                                 reads=[tr], writes=[pre + 'ta_st'])
                            P.op('act', lambda e, TM=TM: e.copy(out=ta_st[64:128], in_=TM[64:128]),
                                 reads=[tr], writes=[pre + 'ta_st'])
                        elif kind == 'tg0':
                            P.op('act', lambda e, TM=TM: e.activation(out=tg_st[:, 0, :], in_=TM, func=AF.Sigmoid),
                                 reads=[tr], writes=[pre + 'tg_st'])
                        else:
                            P.op('act', lambda e, TM=TM: e.activation(out=tg_st[0:32, 1, :], in_=TM, func=AF.Sigmoid),
                                 reads=[tr], writes=[pre + 'tg_st'])
                elif kind in ('q', 'k'):
                    P.op('act', lambda e, pg=pg, j=j: e.activation(out=sq[j], in_=pg, func=AF.Square),
                         reads=[pgr], writes=[pre + 'sq%d' % j])
                    pending[0] = (pg, pgr, j, kind, idx)
                else:
                    P.op('act', lambda e, pg=pg, idx=idx: e.activation(out=gate_st[:, idx, :], in_=pg, func=AF.Sigmoid,
                                                                       bias=bg_t[:, idx:idx + 1]),
                         reads=[pgr, pre + 'bg'], writes=[pre + 'gate_st'])
            flush()
            for s in range(NSUB):
                for (c0, n, b) in ((4896, 512, 6), (5408, 256, 7)):
                    pv = bank(b)[:, 0:n]
                    for kc in range(KC):
                        P.op('pe', lambda e, kc=kc, s=s, c0=c0, n=n, pv=pv, HT=HT: e.matmul(
                            pv, lhsT=HT[:, kc, s * 128:(s + 1) * 128], rhs=W[:, kc, c0:c0 + n],
                            start=(kc == 0), stop=(kc == KC - 1)),
                            reads=[pre + 'W', htr], writes=['psV%d' % b])
                P.op('act', lambda e, s=s: e.copy(out=v_st[:, s, 0:512], in_=bank(6)),
                     reads=['psV6'], writes=[pre + 'v_st'])
                P.op('dve', lambda e, s=s: e.tensor_copy(out=v_st[:, s, 512:768], in_=bank(7)[:, 0:256]),
                     reads=['psV7'], writes=[pre + 'v_st'])
            rv = rkv_d.rearrange("b p t -> p b t")
            for g in range(3):
                P.op('sp', lambda e, g=g, t0=t0: e.dma_start(out=rv[:, g * 8:(g + 1) * 8, t0:t0 + TT],
                                                             in_=rkv_st[:, g * 8:(g + 1) * 8, :]),
                     reads=[pre + 'rkv_st%d' % g], writes=['d_rkv'], dma_key=pre + 'rkv_st%d' % g)
            P.op('sp', lambda e, t0=t0: e.dma_start(out=ta_d[:, t0:t0 + TT], in_=ta_st),
                 reads=[pre + 'ta_st'], writes=['d_ta'], dma_key=pre + 'ta_st')
            P.op('sp', lambda e, t0=t0: e.dma_start(out=tg_d[0:128, t0:t0 + TT], in_=tg_st[:, 0, :]),
                 reads=[pre + 'tg_st'], writes=['d_tg'], dma_key=pre + 'tg_st')
            P.op('sp', lambda e, t0=t0: e.dma_start(out=tg_d[128:160, t0:t0 + TT], in_=tg_st[0:32, 1, :]),
                 reads=[pre + 'tg_st'], writes=['d_tg'], dma_key=pre + 'tg_st')
            P.op('sp', lambda e, t0=t0: e.dma_start(out=qk_d.rearrange("b p t -> p b t")[:, :, t0:t0 + TT], in_=qk_st),
                 reads=[pre + 'qk_st'], writes=['d_qk'], dma_key=pre + 'qk_st')
            P.op('sp', lambda e, t0=t0: e.dma_start(out=gate_d.rearrange("b p t -> p b t")[:, :, t0:t0 + TT],
                                                    in_=gate_st),
                 reads=[pre + 'gate_st'], writes=['d_gate'], dma_key=pre + 'gate_st')
            P.op('sp', lambda e, t0=t0: e.dma_start(
                out=v_d[t0:t0 + TT, :].rearrange("(s p) c -> p s c", p=128), in_=v_st),
                reads=[pre + 'v_st'], writes=['d_v'], dma_key=pre + 'v_st')

    w2_d = din("rwkv_w2", [64, 1024])
    a2_d = din("rwkv_a2", [64, 1024])
    g2_d = din("rwkv_g2", [160, 1024])
    prm_names = ['rwkv_w0', 'rwkv_a0', 'rwkv_k_k', 'rwkv_k_a', 'rwkv_r_k', 'rwkv_ln_w', 'rwkv_ln_b']
    prm_d = [din(n, [1024]) for n in prm_names]
    ya_d = nc.dram_tensor("ya_scr", [8, 128, S], BF16, kind="ExternalOutput" if debug else "Internal").ap()
    TB = 1024
    NCH = TB // 128
    NB = S // TB if NB_DBG is None else NB_DBG
    C0 = float(np.exp(-0.5))
    GN_EPS = 64e-5

    def rwkv_stage(pairs=range(8)):
        A.off = const_mark
        pre = 'r_'
        WA = A.alloc([1024], BF16)
        G2a = A.alloc([1024], BF16)
        G2b = A.alloc([1024], BF16)
        prm = A.alloc([7, 8], F32)
        bones = A.alloc([128], BF16)
        mk4 = A.alloc([512], F32)
        mkL = A.alloc([2, 128], F32)
        E2 = A.alloc([64], F32)
        mrow = A.alloc([TB], F32)
        f32names = ['R', 'K', 'V', 'SG', 'AA', 'GG', 'KK', 'KM', 'T1', 'BVEC', 'CS', 'T2', 'T3', 'EP', 'EN', 'EPM',
                    'EC', 'BV', 'Y32', 'DD']
        T = {n: A.alloc([TB], F32) for n in f32names}
        bfnames = ['TA', 'TG0', 'TG1', 'TQ', 'BT', 'KT', 'BH', 'KH', 'VT', 'YB', 'YO']
        for n in bfnames:
            T[n] = A.alloc([TB], BF16)
        AR = A.alloc([NCH, 2, 128], BF16)
        TM4 = A.alloc([NCH, 4, 128], BF16)
        PC = A.alloc([NCH], F32)
        SC = [[A.alloc([512], BF16) for _ in range(2)] for _ in range(2)]
        LZ = [[A.alloc([2, 384], BF16) for _ in range(2)] for _ in range(2)]
        MCz = [A.alloc([2, 64], BF16) for _ in range(2)]
        QT = [A.alloc([128], BF16) for _ in range(2)]
        STz = A.alloc([2, 64], BF16)
        print("rwkv_stage arena", A.off)

        def R_(n):
            return pre + n
        P.op('pool', lambda e: e.dma_start(out=WA[0:64, :], in_=w2_d), writes=[R_('WA')], dma_key=R_('w'))
        P.op('pool', lambda e: e.dma_start(out=WA[64:128, :], in_=a2_d), writes=[R_('WA')], dma_key=R_('w'))
        P.op('pool', lambda e: e.dma_start(out=G2a, in_=g2_d[0:128, :]), writes=[R_('G2')], dma_key=R_('w'))
        P.op('pool', lambda e: e.dma_start(out=G2b[0:32, :], in_=g2_d[128:160, :]), writes=[R_('G2')], dma_key=R_('w'))
        for i in range(7):
            P.op('sp', lambda e, i=i: e.dma_start(out=prm[:, i, :], in_=prm_d[i].rearrange("(b p) -> p b", p=128),
                                                   allow_slow_non_contiguous=True),
                 writes=[R_('prm')], dma_key=R_('par'))
        P.op('pool', lambda e: e.memset(bones, 0.0), writes=[R_('bones')])
        P.op('pool', lambda e: e.memset(bones[0:64, 0:64], 1.0), reads=[R_('bones')], writes=[R_('bones')])
        P.op('pool', lambda e: e.memset(bones[64:128, 64:128], 1.0), reads=[R_('bones')], writes=[R_('bones')])
        P.op('pool', lambda e: e.memset(mk4, 1.0), writes=[R_('mk4')])
        for q in range(4):
            base = -1 if q % 2 == 0 else 0
            P.op('pool', lambda e, q=q, base=base: e.affine_select(
                out=mk4[:, q * 128:(q + 1) * 128], in_=mk4[:, q * 128:(q + 1) * 128], pattern=[[1, 128]],
                compare_op=ALU.is_ge, fill=0.0, base=base, channel_multiplier=-1),
                reads=[R_('mk4')], writes=[R_('mk4')])
        P.op('pool', lambda e: e.memset(mkL, 1.0), writes=[R_('mkL')])
        P.op('pool', lambda e: e.affine_select(out=mkL, in_=mkL, pattern=[[0, 2], [-1, 128]], compare_op=ALU.is_ge,
                                               fill=0.0, base=-1, channel_multiplier=1),
             reads=[R_('mkL')], writes=[R_('mkL')])
        P.op('pool', lambda e: e.tensor_copy(out=E2[0:64, :], in_=ident_f[0:64, 0:64]), reads=['ident_f'], writes=[R_('E2')])
        P.op('pool', lambda e: e.tensor_copy(out=E2[64:128, :], in_=ident_f[64:128, 64:128]), reads=['ident_f'],
             writes=[R_('E2')])
        P.op('pool', lambda e: e.memset(mrow, 1.0), writes=[R_('mrow')])
        P.op('pool', lambda e: e.memset(mrow.rearrange("p (c t) -> p c t", t=128)[:, :, 0:1], 0.0),
             reads=[R_('mrow')], writes=[R_('mrow')])

        def ch3(ap):
            return ap.rearrange("p (c t) -> p c t", t=128)

        def nm(x):
            if x.startswith('pb') or x in ('ident', 'ident_f'):
                return x
            return R_(x)

        def ew(eng, fn, reads, writes):
            P.op(eng, fn, reads=[nm(x) for x in reads], writes=[nm(x) for x in writes])

        for hp in pairs:
            cols = slice(hp * 128, (hp + 1) * 128)
            ew('pool', lambda e: e.memset(STz, 0.0), [], ['ST'])
            for sl_ in range(2):
                ew('pool', lambda e, sl_=sl_: e.memset(MCz[sl_], 0.0), [], ['MC%d' % sl_])
            for tb in range(NB):
                t0 = tb * TB
                tsl = slice(t0, t0 + TB)
                for i, n in enumerate(['R', 'K', 'V']):
                    P.op('sp', lambda e, i=i, n=n, hp=hp, tsl=tsl: e.dma_start(out=T[n], in_=rkv_d[i * 8 + hp, :, tsl]),
                         writes=[R_(n)], dma_key=R_('ld' + n))
                P.op('sp', lambda e, tsl=tsl: e.dma_start(out=T['TA'], in_=ta_d[:, tsl]), writes=[R_('TA')],
                     dma_key=R_('ldTA'))
                P.op('sp', lambda e, tsl=tsl: e.dma_start(out=T['TG0'], in_=tg_d[0:128, tsl]), writes=[R_('TG0')],
                     dma_key=R_('ldTG0'))
                P.op('sp', lambda e, tsl=tsl: e.dma_start(out=T['TG1'][0:32], in_=tg_d[128:160, tsl]),
                     writes=[R_('TG1')], dma_key=R_('ldTG1'))
                for hf in range(2):
                    hs = slice(hf * 512, (hf + 1) * 512)
                    ew('pe', lambda e, hs=hs, cols=cols: e.matmul(bank(1), lhsT=WA[0:64, cols], rhs=T['TA'][0:64, hs],
                                                                  start=True, stop=True), ['WA', 'TA'], ['pb1'])
                    ew('act', lambda e, hs=hs, hp=hp: e.activation(out=T['SG'][:, hs], in_=bank(1), func=AF.Sigmoid,
                                                                   bias=prm[:, 0, hp:hp + 1]), ['pb1', 'prm'], ['SG'])
                    ew('pe', lambda e, hs=hs, cols=cols: e.matmul(bank(2), lhsT=WA[64:128, cols], rhs=T['TA'][64:128, hs],
                                                                  start=True, stop=True), ['WA', 'TA'], ['pb2'])
                    ew('act', lambda e, hs=hs, hp=hp: e.activation(out=T['AA'][:, hs], in_=bank(2), func=AF.Sigmoid,
                                                                   bias=prm[:, 1, hp:hp + 1]), ['pb2', 'prm'], ['AA'])
                    ew('pe', lambda e, hs=hs, cols=cols: e.matmul(bank(3), lhsT=G2a[:, cols], rhs=T['TG0'][:, hs],
                                                                  start=True, stop=False), ['G2', 'TG0'], ['pb3', 'pb3'])
                    ew('pe', lambda e, hs=hs, cols=cols: e.matmul(bank(3), lhsT=G2b[0:32, cols], rhs=T['TG1'][0:32, hs],
                                                                  start=False, stop=True), ['G2', 'TG1'], ['pb3', 'pb3'])
                    ew('act', lambda e, hs=hs: e.copy(out=T['GG'][:, hs], in_=bank(3)), ['pb3', 'pb3'], ['GG'])
                ew('dve', lambda e, hp=hp: e.tensor_scalar(out=T['KK'], in0=T['K'], scalar1=prm[:, 2, hp:hp + 1],
                                                           scalar2=None, op0=ALU.mult), ['K', 'prm'], ['KK'])
                ew('act', lambda e: e.activation(out=T['TQ'], in_=T['KK'], func=AF.Square), ['KK'], ['TQ'])
                for hf in range(2):
                    hs = slice(hf * 512, (hf + 1) * 512)
                    ew('pe', lambda e, hs=hs: e.matmul(bank(4), lhsT=bones, rhs=T['TQ'][:, hs], start=True, stop=True),
                       ['bones', 'TQ'], ['pb4'])
                    ew('dve', lambda e, hs=hs: e.tensor_scalar(out=T['T1'][:, hs], in0=bank(4), scalar1=1e-19,
                                                               scalar2=None, op0=ALU.max), ['pb4'], ['T1'])
                ew('act', lambda e: e.activation(out=T['T1'], in_=T['T1'], func=AF.Ln), ['T1'], ['T1'])
                ew('act', lambda e: e.activation(out=T['T1'], in_=T['T1'], func=AF.Exp, scale=-0.5), ['T1'], ['T1'])
                ew('dve', lambda e: e.tensor_tensor(out=T['KK'], in0=T['KK'], in1=T['T1'], op=ALU.mult),
                   ['KK', 'T1'], ['KK'])
                ew('dve', lambda e, hp=hp: e.tensor_scalar(out=T['T1'], in0=T['AA'], scalar1=-1.0,
                                                           scalar2=prm[:, 3, hp:hp + 1], op0=ALU.add, op1=ALU.mult),
                   ['AA', 'prm'], ['T1'])
                ew('dve', lambda e: e.scalar_tensor_tensor(out=T['KM'], in0=T['T1'], scalar=1.0, in1=T['K'],
                                                           op0=ALU.add, op1=ALU.mult), ['T1', 'K'], ['KM'])
                ew('pool', lambda e: e.tensor_tensor(out=T['T1'], in0=T['R'], in1=T['KM'], op=ALU.mult),
                   ['R', 'KM'], ['T1'])
                ew('pool', lambda e, hp=hp: e.tensor_scalar(out=T['TQ'], in0=T['T1'], scalar1=prm[:, 4, hp:hp + 1],
                                                            scalar2=None, op0=ALU.mult), ['T1', 'prm'], ['TQ'])
                for hf in range(2):
                    hs = slice(hf * 512, (hf + 1) * 512)
                    ew('pe', lambda e, hs=hs: e.matmul(bank(5), lhsT=bones, rhs=T['TQ'][:, hs], start=True, stop=True),
                       ['bones', 'TQ'], ['pb5'])
                    ew('dve', lambda e, hs=hs: e.tensor_tensor(out=T['BV'][:, hs], in0=T['V'][:, hs], in1=bank(5),
                                                               op=ALU.mult), ['pb5', 'V'], ['BV'])
                ew('pool', lambda e: e.tensor_tensor(out=T['BVEC'], in0=T['KK'], in1=T['AA'], op=ALU.mult),
                   ['KK', 'AA'], ['BVEC'])
                ew('dve', lambda e: e.tensor_tensor_scan(out=T['CS'], data0=mrow, data1=T['SG'], initial=0.0,
                                                         op0=ALU.mult, op1=ALU.add), ['mrow', 'SG'], ['CS'])
                ew('pool', lambda e: e.tensor_tensor(out=T['T2'], in0=T['CS'], in1=T['SG'], op=ALU.subtract),
                   ['CS', 'SG'], ['T2'])
                ew('act', lambda e: e.activation(out=T['EP'], in_=T['CS'], func=AF.Exp, scale=-C0), ['CS'], ['EP'])
                ew('act', lambda e: e.activation(out=T['EN'], in_=T['CS'], func=AF.Exp, scale=C0), ['CS'], ['EN'])
                ew('act', lambda e: e.activation(out=T['EPM'], in_=T['T2'], func=AF.Exp, scale=-C0), ['T2'], ['EPM'])
                ew('dve', lambda e: e.tensor_tensor(
                    out=ch3(T['T3']), in0=ch3(T['CS']), in1=ch3(T['CS'])[:, :, 127:128].to_broadcast([128, NCH, 128]),
                    op=ALU.subtract), ['CS'], ['T3'])
                ew('act', lambda e: e.activation(out=T['EC'], in_=T['T3'], func=AF.Exp, scale=C0), ['T3'], ['EC'])
                ew('act', lambda e: e.activation(out=PC.rearrange("p (c o) -> p c o", o=1),
                                                 in_=ch3(T['CS'])[:, :, 127:128], func=AF.Exp, scale=-C0),
                   ['CS'], ['PC'])
                ew('dve', lambda e: e.scalar_tensor_tensor(out=AR[:, :, 0, :], in0=ch3(T['EPM']), scalar=-1.0,
                                                           in1=ch3(T['KK']), op0=ALU.mult, op1=ALU.mult),
                   ['EPM', 'KK'], ['AR0'])
                ew('pool', lambda e: e.tensor_tensor(out=AR[:, :, 1, :], in0=ch3(T['EP']), in1=ch3(T['R']), op=ALU.mult),
                   ['EP', 'R'], ['AR1'])
                ew('dve', lambda e: e.tensor_tensor(out=T['BT'], in0=T['EN'], in1=T['BVEC'], op=ALU.mult),
                   ['EN', 'BVEC'], ['BT'])
                ew('pool', lambda e: e.tensor_tensor(out=T['KT'], in0=T['EN'], in1=T['KM'], op=ALU.mult),
                   ['EN', 'KM'], ['KT'])
                ew('dve', lambda e: e.tensor_tensor(out=T['BH'], in0=T['EC'], in1=T['BVEC'], op=ALU.mult),
                   ['EC', 'BVEC'], ['BH'])
                ew('pool', lambda e: e.tensor_tensor(out=T['KH'], in0=T['EC'], in1=T['KM'], op=ALU.mult),
                   ['EC', 'KM'], ['KH'])
                ew('act', lambda e: e.copy(out=T['VT'], in_=T['V']), ['V'], ['VT'])
                pT = bank(0).bitcast(BF16)
                for c in range(NCH):
                    cs_ = slice(c * 128, (c + 1) * 128)
                    srcs = [(AR[:, c, 0, :], 'AR0'), (T['VT'][:, cs_], 'VT'), (T['BH'][:, cs_], 'BH'),
                            (T['KH'][:, cs_], 'KH')]
                    for q, (sap, sr) in enumerate(srcs):
                        ew('pe', lambda e, q=q, sap=sap: e.transpose(out=pT[:, q * 128:(q + 1) * 128], in_=sap,
                                                                     identity=ident), [sr, 'ident'], ['pb0'])
                    ew('act', lambda e, c=c: e.copy(out=TM4[:, c, :, :], in_=pT[:, 0:512].rearrange("p (q t) -> p q t", q=4)),
                       ['pb0'], ['TM4_%d' % c])
                P1 = (1, 2)
                P2 = ((4, 5), (6, 7))

                def partA(c, sl):
                    cs_ = slice(c * 128, (c + 1) * 128)
                    tm = 'TM4_%d' % c
                    arc = AR[:, c, :, :].rearrange("p a t -> p (a t)")
                    b1 = P1[sl]
                    ps1 = bank(b1)
                    for h2 in range(2):
                        psl = slice(64 * h2, 64 * h2 + 64)
                        scr = 'SC%d_%d' % (sl, h2)
                        ew('pe', lambda e, ps1=ps1, psl=psl, cs_=cs_, arc=arc: e.matmul(
                            ps1[:, 0:256], lhsT=T['BT'][psl, cs_], rhs=arc[psl, :], start=True, stop=True),
                            ['BT', 'AR0', 'AR1'], ['pb%d' % b1])
                        ew('pe', lambda e, ps1=ps1, psl=psl, cs_=cs_, arc=arc: e.matmul(
                            ps1[:, 256:512], lhsT=T['KT'][psl, cs_], rhs=arc[psl, :], start=True, stop=True),
                            ['KT', 'AR0', 'AR1'], ['pb%d' % b1])
                        ew('dve', lambda e, ps1=ps1, h2=h2, sl=sl: e.tensor_tensor(out=SC[sl][h2], in0=mk4, in1=ps1,
                                                                                   op=ALU.mult),
                           ['pb%d' % b1, 'mk4'], [scr])
                        b2 = P2[sl][h2]
                        ew('pe', lambda e, b2=b2, psl=psl, cs_=cs_, c=c: e.matmul(
                            bank(b2)[:, 384:512], lhsT=AR[psl, c, 0, :], rhs=T['BT'][psl, cs_],
                            start=True, stop=True), ['AR0', 'BT'], ['pb%d' % b2])
                    for h2 in range(2):
                        b2 = P2[sl][h2]
                        ew('dve', lambda e, h2=h2, b2=b2, sl=sl: e.tensor_tensor(
                            out=LZ[sl][0][:, h2, 0:128], in0=mkL[:, h2, :], in1=bank(b2)[:, 384:512], op=ALU.mult),
                            ['pb%d' % b2, 'mkL'], ['LL%d_0_%d' % (sl, h2)])
                    for h2 in range(2):
                        pb = 64 * h2
                        ew('pe', lambda e, h2=h2, pb=pb, c=c, sl=sl: e.matmul(
                            bank(3)[:, 256 + h2 * 64:256 + (h2 + 1) * 64], lhsT=SC[sl][h2][:, 256:384],
                            rhs=TM4[:, c, 1, pb:pb + 64], start=True, stop=True), ['SC%d_%d' % (sl, h2), tm], ['pb3'])
                    ew('pool', lambda e, c=c, sl=sl: e.tensor_copy(
                        out=LZ[sl][0][:, :, 128:192], in_=TM4[:, c, 0, :].rearrange("p (h k) -> p h k", h=2)),
                        [tm], ['ZZ%d_0_0' % sl, 'ZZ%d_0_1' % sl])
                    for h2 in range(2):
                        ew('act', lambda e, h2=h2, sl=sl: e.copy(out=LZ[sl][0][:, h2, 192:256],
                                                                 in_=bank(3)[:, 256 + h2 * 64:256 + (h2 + 1) * 64]),
                           ['pb3'], ['ZZ%d_0_%d' % (sl, h2)])

                def partB(c, sl, n):
                    pp = n % 2
                    for h2 in range(2):
                        b2 = P2[sl][h2]
                        ps2 = bank(b2)
                        pbr = 'pb%d' % b2
                        ltn = SC[sl][h2][:, 0:128] if n == 0 else LZ[sl][pp][:, h2, 256:384]
                        rds = ['LL%d_%d_%d' % (sl, pp, h2), 'ZZ%d_%d_%d' % (sl, pp, h2)] + \
                            (['SC%d_%d' % (sl, h2)] if n == 0 else [])
                        if n < 6:
                            ew('pe', lambda e, ps2=ps2, ltn=ltn, pp=pp, h2=h2, sl=sl: e.matmul(
                                ps2[:, 0:256], lhsT=ltn, rhs=LZ[sl][pp][:, h2, 0:256], start=True, stop=True),
                                rds, [pbr])
                            ew('pe', lambda e, ps2=ps2, ltn=ltn, pp=pp, h2=h2, sl=sl: e.matmul(
                                ps2[:, 256:384], lhsT=LZ[sl][pp][:, h2, 0:128], rhs=ltn, start=True, stop=True),
                                rds, [pbr])
                        else:
                            ew('pe', lambda e, ps2=ps2, ltn=ltn, pp=pp, h2=h2, sl=sl: e.matmul(
                                ps2[:, 128:256], lhsT=ltn, rhs=LZ[sl][pp][:, h2, 128:256], start=True, stop=True),
                                rds, [pbr])

                    def cp(h2):
                        b2 = P2[sl][h2]
                        ew('act', lambda e, h2=h2, b2=b2: e.copy(
                            out=LZ[sl][1 - pp][:, h2, :].rearrange("p (s t) -> p s t", s=3)[:, 0:3:2, :],
                            in_=bank(b2)[:, 0:384].rearrange("p (s t) -> p s t", s=3)[:, 0:3:2, :]),
                            ['pb%d' % b2], ['LL%d_%d_%d' % (sl, 1 - pp, h2)])

                    def ad(h2):
                        b2 = P2[sl][h2]
                        ew('dve', lambda e, h2=h2, b2=b2: e.tensor_tensor(
                            out=LZ[sl][1 - pp][:, h2, 128:256], in0=LZ[sl][pp][:, h2, 128:256],
                            in1=bank(b2)[:, 128:256], op=ALU.add),
                            ['pb%d' % b2, 'ZZ%d_%d_%d' % (sl, pp, h2)], ['ZZ%d_%d_%d' % (sl, 1 - pp, h2)])
                    if n < 6:
                        cp(0)
                        ad(1)
                        cp(1)
                        ad(0)
                    else:
                        ad(0)
                        ad(1)

                def partC(c, sl):
                    tm = 'TM4_%d' % c
                    ZF = LZ[sl][1]
                    p3 = bank(0)[:, 192:384]
                    for h2 in range(2):
                        pb = 64 * h2
                        psl = slice(pb, pb + 64)
                        zr = 'ZZ%d_1_%d' % (sl, h2)
                        ew('pe', lambda e, h2=h2, pb=pb, psl=psl, c=c: e.matmul(
                            p3[psl, 0:64], lhsT=ZF[:, h2, 128:192], rhs=TM4[:, c, 2, pb:pb + 64],
                            start=True, stop=True, tile_position=(0, pb)), [zr, tm], ['pb0'])
                        ew('pe', lambda e, h2=h2, pb=pb, psl=psl: e.matmul(
                            p3[psl, 64:192], lhsT=ZF[:, h2, 128:192], rhs=SC[sl][h2][:, 128:256],
                            start=True, stop=True, tile_position=(0, pb)), [zr, 'SC%d_%d' % (sl, h2)], ['pb0'])
                    for h2 in range(2):
                        psl = slice(64 * h2, 64 * h2 + 64)
                        ew('dve', lambda e, c=c, psl=psl, h2=h2: e.scalar_tensor_tensor(
                            out=MCz[sl][psl, h2, :], in0=E2[psl, :], scalar=PC[psl, c:c + 1], in1=p3[psl, 0:64],
                            op0=ALU.mult, op1=ALU.add), ['E2', 'PC', 'pb0'], ['MC%d' % sl])
                    ew('dve', lambda e, c=c: e.tensor_tensor(out=QT[sl], in0=AR[:, c, 1, :], in1=p3[:, 64:192],
                                                             op=ALU.add), ['pb0', 'AR1'], ['QT%d' % sl])

                def partD(c, sl):
                    cs_ = slice(c * 128, (c + 1) * 128)
                    tm = 'TM4_%d' % c
                    ZF = LZ[sl][1]
                    for h2 in range(2):
                        pb = 64 * h2
                        psl = slice(pb, pb + 64)
                        sb_ = 3 if h2 == 0 else 0
                        sr_ = 'pb%d' % sb_
                        psY = bank(sb_)[:, 0:128]
                        psS = bank(sb_)[:, 128:192]
                        UU = ZF[:, h2, 192:256]
                        zr = 'ZZ%d_1_%d' % (sl, h2)
                        scr = 'SC%d_%d' % (sl, h2)
                        ew('pe', lambda e, psl=psl, pb=pb, h2=h2, UU=UU, psY=psY: e.matmul(
                            psY[psl, :], lhsT=UU, rhs=SC[sl][h2][:, 128:256], start=True, stop=False,
                            tile_position=(0, pb)), [zr, scr], [sr_])
                        ew('pe', lambda e, psl=psl, pb=pb, h2=h2, c=c, psY=psY: e.matmul(
                            psY[psl, :], lhsT=TM4[:, c, 1, pb:pb + 64], rhs=SC[sl][h2][:, 384:512], start=False,
                            stop=False, tile_position=(0, pb)), [tm, scr], [sr_])
                        ew('pe', lambda e, psl=psl, pb=pb, psY=psY, h2=h2: e.matmul(
                            psY[psl, :], lhsT=STz[:, h2, :], rhs=QT[sl], start=False, stop=True,
                            tile_position=(0, pb)), ['ST', 'QT%d' % sl], [sr_])
                        ew('pe', lambda e, psl=psl, pb=pb, psS=psS, h2=h2: e.matmul(
                            psS[psl, :], lhsT=MCz[sl][:, h2, :], rhs=STz[:, h2, :], start=True, stop=False,
                            tile_position=(0, pb)), ['MC%d' % sl, 'ST'], [sr_])
                        ew('pe', lambda e, psl=psl, pb=pb, c=c, UU=UU, psS=psS: e.matmul(
                            psS[psl, :], lhsT=TM4[:, c, 2, pb:pb + 64], rhs=UU, start=False, stop=False,
                            tile_position=(0, pb)), [tm, zr], [sr_])
                        ew('pe', lambda e, psl=psl, pb=pb, c=c, psS=psS: e.matmul(
                            psS[psl, :], lhsT=TM4[:, c, 3, pb:pb + 64], rhs=TM4[:, c, 1, pb:pb + 64], start=False,
                            stop=True, tile_position=(0, pb)), [tm], [sr_])
                    for h2 in range(2):
                        pb = 64 * h2
                        psl = slice(pb, pb + 64)
                        sb_ = 3 if h2 == 0 else 0
                        sr_ = 'pb%d' % sb_
                        ew('act', lambda e, cs_=cs_, psl=psl, sb_=sb_: e.copy(out=T['Y32'][psl, cs_],
                                                                             in_=bank(sb_)[psl, 0:128]),
                           [sr_], ['Y32'])
                        ew('dve', lambda e, psl=psl, sb_=sb_, h2=h2: e.tensor_copy(out=STz[psl, h2, :],
                                                                                   in_=bank(sb_)[psl, 128:192]),
                           [sr_], ['ST'])

                for c0 in range(0, NCH, 2):
                    for sl in range(2):
                        partA(c0 + sl, sl)
                    for n in range(7):
                        for sl in range(2):
                            partB(c0 + sl, sl, n)
                    for sl in range(2):
                        partC(c0 + sl, sl)
                    for sl in range(2):
                        partD(c0 + sl, sl)
                ew('pool', lambda e: e.tensor_copy(out=T['YB'], in_=T['Y32']), ['Y32'], ['YB'])
                for hf in range(2):
                    hs = slice(hf * 512, (hf + 1) * 512)
                    ew('pe', lambda e, hs=hs: e.matmul(bank(1), lhsT=bones, rhs=T['YB'][:, hs], start=True, stop=True),
                       ['bones', 'YB'], ['pb1'])
                    ew('dve', lambda e, hs=hs: e.scalar_tensor_tensor(out=T['DD'][:, hs], in0=bank(1), scalar=-1.0 / 64,
                                                                      in1=T['Y32'][:, hs], op0=ALU.mult, op1=ALU.add),
                       ['pb1', 'Y32'], ['DD'])
                ew('act', lambda e: e.activation(out=T['TQ'], in_=T['DD'], func=AF.Square), ['DD'], ['TQ'])
                for hf in range(2):
                    hs = slice(hf * 512, (hf + 1) * 512)
                    ew('pe', lambda e, hs=hs: e.matmul(bank(2), lhsT=bones, rhs=T['TQ'][:, hs], start=True, stop=True),
                       ['bones', 'TQ'], ['pb2'])
                    ew('act', lambda e, hs=hs: e.activation(out=T['T1'][:, hs], in_=bank(2), func=AF.Ln, scale=1.0 / 64,
                                                            bias=GN_EPS), ['pb2'], ['T1'])
                ew('act', lambda e: e.activation(out=T['T1'], in_=T['T1'], func=AF.Exp, scale=-0.5), ['T1'], ['T1'])
                ew('dve', lambda e: e.tensor_tensor(out=T['DD'], in0=T['DD'], in1=T['T1'], op=ALU.mult),
                   ['DD', 'T1'], ['DD'])
                ew('dve', lambda e, hp=hp: e.tensor_scalar(out=T['DD'], in0=T['DD'], scalar1=prm[:, 5, hp:hp + 1],
                                                           scalar2=prm[:, 6, hp:hp + 1], op0=ALU.mult, op1=ALU.add),
                   ['DD', 'prm'], ['DD'])
                ew('pool', lambda e: e.tensor_tensor(out=T['DD'], in0=T['DD'], in1=T['BV'], op=ALU.add),
                   ['DD', 'BV'], ['DD'])
                ew('dve', lambda e: e.tensor_tensor(out=T['YO'], in0=T['DD'], in1=T['GG'], op=ALU.mult),
                   ['DD', 'GG'], ['YO'])
                P.op('sp', lambda e, hp=hp, tsl=tsl: e.dma_start(out=ya_d[hp, :, tsl], in_=T['YO']),
                     reads=[R_('YO')], writes=['d_ya'], dma_key=R_('stYO'))

    yb_d = nc.dram_tensor("yb_scr", [6, 128, S], BF16, kind="ExternalOutput" if debug else "Internal").ap()
    DIL = (1, 4, 16)

    def attn_stage_full(js=range(4)):
        A.off = const_mark
        pre = 'a_'

        def R_(n):
            return pre + n

        def nm(x):
            if x.startswith('pb') or x in ('ident', 'ident_f'):
                return x
            return R_(x)

        def ew(eng, fn, reads, writes):
            P.op(eng, fn, reads=[nm(x) for x in reads], writes=[nm(x) for x in writes])
        QH = A.alloc([S], BF16)
        KH = A.alloc([S], BF16)
        VX = A.alloc([32, 64], BF16)
        ONES = A.alloc([64], BF16)
        OT = [A.alloc([S], F32) for _ in range(3)]
        DEN = [A.alloc([S], F32) for _ in range(3)]
        PT = [A.alloc([256], BF16) for _ in range(4)]
        mask2 = A.alloc([256], BF16)
        RD = [A.alloc([512], F32) for _ in range(2)]
        YBS = [A.alloc([512], BF16) for _ in range(2)]
        print("attn_stage arena", A.off)
        P.op('pool', lambda e: e.memset(mask2, 1.0), writes=[R_('mask')])
        P.op('pool', lambda e: e.affine_select(out=mask2[:, 0:128], in_=mask2[:, 0:128], pattern=[[1, 128]],
                                               compare_op=ALU.is_ge, fill=0.0, base=0, channel_multiplier=-1),
             reads=[R_('mask')], writes=[R_('mask')])
        P.op('pool', lambda e: e.affine_select(out=mask2[:, 128:256], in_=mask2[:, 128:256], pattern=[[-1, 128]],
                                               compare_op=ALU.is_ge, fill=0.0, base=0, channel_multiplier=1),
             reads=[R_('mask')], writes=[R_('mask')])
        P.op('pool', lambda e: e.memset(ONES, 1.0), writes=[R_('ONES')])

        tcount = [0]
        ccount = [0]
        for j in js:
            for g in range(3):
                d = DIL[g]
                nb = S // d // 128
                h = 4 * g + j
                pair = h // 2
                pb = 64 * (h % 2)
                vv = v_d.rearrange("(m d) c -> d m c", d=d)
                for r in range(d):
                    for n0 in range(0, nb, 8):
                        n1 = min(nb, n0 + 8)
                        P.op('sp', lambda e, r=r, h=h, nb=nb, vv=vv, n0=n0, n1=n1: e.dma_start(
                            out=VX[:, r * nb + n0:r * nb + n1, :],
                            in_=vv[r, n0 * 128:n1 * 128, h * 64:(h + 1) * 64].rearrange("(n i) c -> i n c", i=128)),
                            writes=[R_('VX')], dma_key=R_('ldV'))
                P.op('sp', lambda e, pair=pair, pb=pb: e.dma_start(out=QH[0:64, :], in_=qk_d[pair, pb:pb + 64, :]),
                     writes=[R_('QH')], dma_key=R_('ldQ'))
                P.op('sp', lambda e, pair=pair, pb=pb: e.dma_start(out=KH[0:64, :], in_=qk_d[6 + pair, pb:pb + 64, :]),
                     writes=[R_('KH')], dma_key=R_('ldK'))
                qv = QH.rearrange("p (m d) -> p d m", d=d)
                kv = KH.rearrange("p (m d) -> p d m", d=d)
                otr = 'OT%d' % g
                otv = OT[g].rearrange("p (m d) -> p d m", d=d)
                dnv = DEN[g].rearrange("p (m d) -> p d m", d=d)
                tiles = []
                for r in range(d):
                    for n in range(nb):
                        tiles.append((r, n, tcount[0]))
                        tcount[0] += 1

                def emit_score(r, n, ti, kv=kv, qv=qv, nb=nb):
                    nq = 256 if n + 1 < nb else 128
                    sbk = 1 + ti % 2
                    ps = bank(sbk)[:, 0:nq]
                    pt = PT[ti % 4]
                    ptr = 'PT%d' % (ti % 4)
                    ew('pe', lambda e, ps=ps, r=r, n=n, nq=nq: e.matmul(
                        ps, lhsT=kv[0:64, r, 128 * n:128 * n + 128], rhs=qv[0:64, r, 128 * n:128 * n + nq],
                        start=True, stop=True), ['KH', 'QH'], ['pb%d' % sbk])
                    ew('act', lambda e, ps=ps, pt=pt, nq=nq: e.activation(out=pt[:, 0:nq], in_=ps, func=AF.Exp),
                       ['pb%d' % sbk], [ptr])
                    ew('pool', lambda e, pt=pt, nq=nq: e.tensor_tensor(out=pt[:, 0:nq], in0=pt[:, 0:nq],
                                                                      in1=mask2[:, 0:nq], op=ALU.mult),
                       [ptr, 'mask'], [ptr])

                def emit_pv(r, n, ti, otv=otv, dnv=dnv, nb=nb, g=g, otr=otr):
                    pt = PT[ti % 4]
                    ptr = 'PT%d' % (ti % 4)
                    obk = 3 + ti % 2
                    po = bank(obk)[0:64, 0:128]
                    pdn = bank(obk)[0:64, 128:256]
                    vt = r * nb + n
                    has_prev = n > 0
                    if has_prev:
                        ppt = PT[(ti - 1) % 4]
                        pptr = 'PT%d' % ((ti - 1) % 4)
                        ew('pe', lambda e, ppt=ppt, vt=vt: e.matmul(
                            po, lhsT=VX[:, vt - 1, :], rhs=ppt[:, 128:256], start=True, stop=False),
                            ['VX', pptr], ['pb%d' % obk])
                    ew('pe', lambda e, pt=pt, vt=vt: e.matmul(
                        po, lhsT=VX[:, vt, :], rhs=pt[:, 0:128], start=(not has_prev), stop=True),
                        ['VX', ptr], ['pb%d' % obk])
                    if has_prev:
                        ew('pe', lambda e, ppt=ppt: e.matmul(
                            pdn, lhsT=ONES, rhs=ppt[:, 128:256], start=True, stop=False),
                            ['ONES', pptr], ['pb%d' % obk])
                    ew('pe', lambda e, pt=pt: e.matmul(
                        pdn, lhsT=ONES, rhs=pt[:, 0:128], start=(not has_prev), stop=True),
                        ['ONES', ptr], ['pb%d' % obk])
                    ew('dve', lambda e, r=r, n=n: e.tensor_copy(
                        out=otv[0:64, r, 128 * n:128 * n + 128], in_=po), ['pb%d' % obk], [otr])
                    ew('dve', lambda e, r=r, n=n: e.tensor_copy(
                        out=dnv[0:64, r, 128 * n:128 * n + 128], in_=pdn), ['pb%d' % obk], ['DEN%d' % g])

                for idx, (r, n, ti) in enumerate(tiles):
                    emit_score(r, n, ti)
                    if idx >= 1:
                        emit_pv(*tiles[idx - 1])
                emit_pv(*tiles[-1])
            for ck in range(S // 512):
                csl = slice(ck * 512, (ck + 1) * 512)
                cc = ccount[0]
                ccount[0] += 1
                rd = RD[cc % 2]
                ew('pool', lambda e, rd=rd, csl=csl: e.tensor_tensor(out=rd[0:64, :], in0=DEN[0][0:64, csl],
                                                                     in1=DEN[1][0:64, csl], op=ALU.add),
                   ['DEN0', 'DEN1'], ['RD%d' % (cc % 2)])
                ew('pool', lambda e, rd=rd, csl=csl: e.tensor_tensor(out=rd[0:64, :], in0=rd[0:64, :],
                                                                     in1=DEN[2][0:64, csl], op=ALU.add),
                   ['RD%d' % (cc % 2), 'DEN2'], ['RD%d' % (cc % 2)])
                ew('dve', lambda e, rd=rd: e.reciprocal(out=rd[0:64, :], in_=rd[0:64, :]), ['RD%d' % (cc % 2)],
                   ['RD%d' % (cc % 2)])
                for g in range(3):
                    h = 4 * g + j
                    pair = h // 2
                    pb = 64 * (h % 2)
                    yi = (cc * 3 + g) % 2
                    ys = YBS[yi]
                    ew('pool', lambda e, g=g, csl=csl, rd=rd, ys=ys: e.tensor_tensor(
                        out=ys[0:64, :], in0=OT[g][0:64, csl], in1=rd[0:64, :], op=ALU.mult),
                        ['OT%d' % g, 'RD%d' % (cc % 2)], ['YBS%d' % yi])
                    P.op('sp', lambda e, pair=pair, pb=pb, csl=csl, ys=ys: e.dma_start(
                        out=yb_d[pair, pb:pb + 64, csl], in_=ys[0:64, :]),
                        reads=[R_('YBS%d' % yi)], writes=['d_yb'], dma_key=R_('stYB%d' % yi))

    wpr_d = din("w_proj_rwkv", [1024, 1024])
    wpa_d = din("w_proj_attn", [768, 1024])
    wo_d = din("w_out", [1024, 1024])
    x2_d = nc.dram_tensor("x2_scr", [S, D], F32, kind="ExternalOutput" if debug else "Internal").ap()

    def merge_stage():
        A.off = const_mark
        pre = 'm_'

        def R_(n):
            return pre + n

        def nm(x):
            if x.startswith('pb'):
                return x
            return R_(x)

        def ew(eng, fn, reads, writes):
            P.op(eng, fn, reads=[nm(x) for x in reads], writes=[nm(x) for x in writes])
        Wr = A.alloc([8, 1024], BF16)
        Wa = A.alloc([6, 1024], BF16)
        Wo = A.alloc([8, 1024], BF16)
        X = [A.alloc([NSUB, D], F32) for _ in range(2)]
        YA = [A.alloc([8, TT], BF16) for _ in range(2)]
        YB = [A.alloc([6, TT], BF16) for _ in range(2)]
        G = [A.alloc([16, TT], BF16) for _ in range(2)]
        MT = A.alloc([8, TT], BF16)
        t1 = [A.alloc([TT], F32) for _ in range(2)]
        t2 = [A.alloc([TT], F32) for _ in range(2)]
        print("merge_stage arena", A.off)
        for kc in range(8):
            P.op('pool', lambda e, kc=kc: e.dma_start(out=Wr[:, kc, :], in_=wpr_d[kc * 128:(kc + 1) * 128, :]),
                 writes=[R_('Wr')], dma_key=R_('w'))
            P.op('pool', lambda e, kc=kc: e.dma_start(out=Wo[:, kc, :], in_=wo_d[kc * 128:(kc + 1) * 128, :]),
                 writes=[R_('Wo')], dma_key=R_('w'))
        for kc in range(6):
            P.op('pool', lambda e, kc=kc: e.dma_start(out=Wa[:, kc, :], in_=wpa_d[kc * 128:(kc + 1) * 128, :]),
                 writes=[R_('Wa')], dma_key=R_('w'))
        srcv = x1_d.rearrange("(n s p) d -> n p s d", p=128, s=NSUB)
        dstv = x2_d.rearrange("(n s p) d -> n p s d", p=128, s=NSUB)
        for it in range(NT):
            sl = it % 2
            tsl = slice(it * TT, (it + 1) * TT)
            sfx = '%d' % sl
            P.op('sp', lambda e, it=it, sl=sl: e.dma_start(out=X[sl], in_=srcv[it]), writes=[R_('X' + sfx)],
                 dma_key=R_('ldX' + sfx))
            P.op('sp', lambda e, sl=sl, tsl=tsl: e.dma_start(out=YA[sl], in_=ya_d.rearrange("b p t -> p b t")[:, :, tsl]),
                 writes=[R_('YA' + sfx)], dma_key=R_('ldYA' + sfx))
            P.op('sp', lambda e, sl=sl, tsl=tsl: e.dma_start(out=YB[sl], in_=yb_d.rearrange("b p t -> p b t")[:, :, tsl]),
                 writes=[R_('YB' + sfx)], dma_key=R_('ldYB' + sfx))
            P.op('sp', lambda e, sl=sl, tsl=tsl: e.dma_start(out=G[sl], in_=gate_d.rearrange("b p t -> p b t")[:, :, tsl]),
                 writes=[R_('G' + sfx)], dma_key=R_('ldG' + sfx))
            for c in range(8):
                bk = 1 + c % 2
                pg = bank(bk)
                for kc in range(8):
                    ew('pe', lambda e, c=c, kc=kc, pg=pg, sl=sl: e.matmul(
                        pg[:, 0:TT], lhsT=Wr[:, kc, c * 128:(c + 1) * 128], rhs=YA[sl][:, kc, :],
                        start=(kc == 0), stop=(kc == 7)), ['Wr', 'YA' + sfx], ['pb%d' % bk])
                for kc in range(6):
                    ew('pe', lambda e, c=c, kc=kc, pg=pg, sl=sl: e.matmul(
                        pg[:, TT:2 * TT], lhsT=Wa[:, kc, c * 128:(c + 1) * 128], rhs=YB[sl][:, kc, :],
                        start=(kc == 0), stop=(kc == 5)), ['Wa', 'YB' + sfx], ['pb%d' % bk])
                q = c % 2
                ew('dve', lambda e, c=c, pg=pg, sl=sl, q=q: e.tensor_tensor(out=t1[q], in0=G[sl][:, c, :],
                                                                           in1=pg[:, 0:TT], op=ALU.mult),
                   ['G' + sfx, 'pb%d' % bk], ['t1%d' % q])
                ew('dve', lambda e, c=c, pg=pg, sl=sl, q=q: e.tensor_tensor(out=t2[q], in0=G[sl][:, 8 + c, :],
                                                                           in1=pg[:, TT:2 * TT], op=ALU.mult),
                   ['G' + sfx, 'pb%d' % bk], ['t2%d' % q])
                ew('pool', lambda e, c=c, q=q: e.tensor_tensor(out=MT[:, c, :], in0=t1[q], in1=t2[q], op=ALU.add),
                   ['t1%d' % q, 't2%d' % q], ['MT'])
            for s in range(NSUB):
                for dh in range(2):
                    bk = 3 + (s * 2 + dh) % 2
                    pd = bank(bk)
                    for c in range(8):
                        ew('pe', lambda e, c=c, s=s, dh=dh, pd=pd: e.matmul(
                            pd, lhsT=MT[:, c, s * 128:(s + 1) * 128], rhs=Wo[:, c, dh * 512:(dh + 1) * 512],
                            start=(c == 0), stop=(c == 7)), ['MT', 'Wo'], ['pb%d' % bk])
                    ew('dve', lambda e, s=s, dh=dh, pd=pd, sl=sl: e.tensor_tensor(
                        out=X[sl][:, s, dh * 512:(dh + 1) * 512], in0=X[sl][:, s, dh * 512:(dh + 1) * 512], in1=pd,
                        op=ALU.add), ['pb%d' % bk, 'X' + sfx], ['X' + sfx])
            P.op('sp', lambda e, it=it, sl=sl: e.dma_start(out=dstv[it], in_=X[sl]),
                 reads=[R_('X' + sfx)], writes=['d_x2'], dma_key=R_('stX' + sfx))

    if 'ffn1' in stages:
        ffn_stage(0, x, x1_d)
        P.sync_all()
    if 'proj' in stages:
        proj_stage()
        P.sync_all()
    if 'rwkv' in stages:
        rwkv_stage(range(NPAIRS_DBG))
        P.sync_all()
    if 'attn' in stages:
        attn_stage_full(JS_DBG)
        P.sync_all()
    if 'merge' in stages:
        merge_stage()
        P.sync_all()
    if 'ffn2' in stages:
        ffn_stage(1, x2_d if 'merge' in stages else x1_d, out)
    P.sync_all()
    P.op('sp', None)
    P.finalize_and_emit(stack)
    stack.close()
    return nc


_CACHE = {}


SHARED_KEYS = ['ffn1_norm', 'ffn1_w_in', 'ffn1_w_out', 'ffn2_norm', 'ffn2_w_in', 'ffn2_w_out',
               'w_in', 'mix_norm', 'rwkv_mu', 'b_gate', 'attn_q_norm', 'attn_k_norm',
               'w_proj_rwkv', 'w_proj_attn', 'w_out', 'rwkv_w2', 'rwkv_a2', 'rwkv_g2', 'rwkv_w0', 'rwkv_a0', 'rwkv_k_k', 'rwkv_k_a', 'rwkv_r_k', 'rwkv_ln_w', 'rwkv_ln_b']


def make_shared(inputs):
    shared = {}
    for k in SHARED_KEYS:
        v = np.asarray(inputs[k], dtype=np.float32)
        v = v.reshape(v.shape[1:])
        if k == 'rwkv_r_k':
            v = v.reshape(-1)
        shared[k] = np.ascontiguousarray(v)
    return shared


def kernel(**inputs):
    if 'nc' not in _CACHE:
        _CACHE['nc'] = build_program()
    nc = _CACHE['nc']
    x = np.ascontiguousarray(inputs['x'], dtype=np.float32)
    shared = make_shared(inputs)
    in_maps = []
    for c in range(NCORES):
        m = dict(shared)
        m['x'] = x[c]
        in_maps.append(m)
    res = run_bass_kernel_spmd(nc, in_maps, core_ids=list(range(NCORES)))
    return np.stack([np.asarray(r['out']) for r in res.results], axis=0)
```

```python
import numpy as np
from contextlib import ExitStack
import concourse.bass as bass
import concourse.mybir as mybir
from concourse.bass_utils import run_bass_kernel_spmd
from concourse.alu_op_type import AluOpType as ALU

F32 = mybir.dt.float32
BF16 = mybir.dt.bfloat16
AF = mybir.ActivationFunctionType
AX = mybir.AxisListType

S = 4096
D = 1024
DFF = 2816
NCORES = 8
RMS_EPS = 1e-6

ENGS = ['pe', 'act', 'dve', 'pool', 'sp']
MAXOPS = [0]
SEM_LIM = 30000
DMA_LIM = 1800


class Prog:
    def __init__(self, nc):
        self.nc = nc
        self.ops = []
        self.eng_ops = {e: [] for e in ENGS}
        self.last_w = {}
        self.readers = {}
        self.dma_cnt = {}
        self.barrier = {e: None for e in ENGS}

    def op(self, eng, fn, reads=(), writes=(), dma_key=None):
        mo = MAXOPS[0]
        if mo and len(self.ops) >= mo and fn is not None:
            return None
        if mo and len(self.ops) == mo - 1 and fn is not None:
            print("LAST OP:", eng, fn.__code__.co_firstlineno, reads, writes)
        oid = len(self.ops)
        deps = set()
        dma_deps = {}
        writes = list(writes) + [r for r in reads if (r.startswith('pb') or r.startswith('ps')) and r not in writes]

        def add(o):
            od = self.ops[o]
            if od['dma_key'] is not None:
                k = od['dma_key']
                dma_deps[k] = self.dma_cnt[k]
            else:
                deps.add(o)
        for r in reads:
            if r in self.last_w:
                add(self.last_w[r])
        for w in writes:
            if w in self.last_w:
                add(self.last_w[w])
            for rd in self.readers.get(w, {}).values():
                add(rd)
        if self.barrier[eng] is not None:
            bd, bdma = self.barrier[eng]
            for o in bd:
                deps.add(o)
            for k, v in bdma.items():
                dma_deps[k] = max(dma_deps.get(k, 0), v)
            self.barrier[eng] = None
        cnt = None
        if dma_key is not None:
            self.dma_cnt[dma_key] = self.dma_cnt.get(dma_key, 0) + 1
            cnt = self.dma_cnt[dma_key]
        o = dict(id=oid, eng=eng, fn=fn, deps=deps, dma_deps=dma_deps, dma_key=dma_key,
                 dma_cnt=cnt, idx=len(self.eng_ops[eng]), sig=False)
        self.ops.append(o)
        self.eng_ops[eng].append(o)
        ch = eng if dma_key is None else 'dma:' + dma_key
        for r in reads:
            self.readers.setdefault(r, {})[ch] = oid
        for w in writes:
            self.last_w[w] = oid
            self.readers[w] = {}
        return oid

    def sync_all(self):
        bd = set()
        for e in ENGS:
            for o in reversed(self.eng_ops[e]):
                if o['dma_key'] is None and o['fn'] is not None:
                    bd.add(o['id'])
                    break
        bdma = dict(self.dma_cnt)
        for e in ENGS:
            self.barrier[e] = (set(bd), dict(bdma))

    def finalize_and_emit(self, stack):
        nc = self.nc
        for o in self.ops:
            per = {}
            for d in o['deps']:
                od = self.ops[d]
                if od['eng'] == 'pe' and o['eng'] == 'pe':
                    continue
                e = od['eng']
                if e not in per or self.ops[per[e]]['idx'] < od['idx']:
                    per[e] = d
            o['cdeps'] = per
            for d in per.values():
                self.ops[d]['sig'] = True
        sems = {}

        def get_sem(name):
            return sems[name]
        for e in ENGS:
            c = 0
            for o in self.eng_ops[e]:
                if o['dma_key'] is None and o['sig']:
                    c += 1
                    o['sigval'] = c
        for o in self.ops:
            waits = {}
            for e, d in o['cdeps'].items():
                v = self.ops[d]['sigval']
                key = ('c_%s_%d' % (e, (v - 1) // SEM_LIM))
                val = (v - 1) % SEM_LIM + 1
                waits[key] = max(waits.get(key, 0), val)
            for k, n in o['dma_deps'].items():
                key = ('d_%s_%d' % (k, (n - 1) // DMA_LIM))
                val = 16 * ((n - 1) % DMA_LIM + 1)
                waits[key] = max(waits.get(key, 0), val)
            o['waits'] = waits
        names = set()
        for o in self.ops:
            names.update(o['waits'].keys())
            if o['dma_key'] is not None:
                names.add('d_%s_%d' % (o['dma_key'], (o['dma_cnt'] - 1) // DMA_LIM))
            elif o['sig']:
                names.add('c_%s_%d' % (o['eng'], (o['sigval'] - 1) // SEM_LIM))
        for nm in sorted(names):
            sems[nm] = stack.enter_context(nc.semaphore(nm))
        print("n_sems", len(names), "n_ops", len(self.ops), {e: len(v) for e, v in self.eng_ops.items()})
        block = stack.enter_context(nc.Block())
        decos = {'pe': block.tensor, 'act': block.scalar, 'dve': block.vector,
                 'pool': block.gpsimd, 'sp': block.sync}
        for e in ENGS:
            ops = self.eng_ops[e]

            def body(eng, ops=ops, e=e):
                waited = {}
                for o in ops:
                    for key, val in o['waits'].items():
                        if waited.get(key, 0) >= val:
                            continue
                        waited[key] = val
                        eng.wait_ge(get_sem(key), val)
                    if o['fn'] is None:
                        continue
                    ins = o['fn'](eng)
                    if o['dma_key'] is not None:
                        n = o['dma_cnt']
                        ins.then_inc(get_sem('d_%s_%d' % (o['dma_key'], (n - 1) // DMA_LIM)), 16)
                    elif o['sig']:
                        v = o['sigval']
                        ins.then_inc(get_sem('c_%s_%d' % (e, (v - 1) // SEM_LIM)), 1)
            decos[e](body)


class Arena:
    def __init__(self, tensor, nbytes):
        self.t = tensor
        self.nbytes = nbytes
        self.off = 0

    def alloc(self, shape, dtype, parts=128):
        n = int(np.prod(shape))
        esz = 4 if dtype == F32 else 2
        nb = n * esz
        nb_al = (nb + 63) // 64 * 64
        assert self.off + nb_al <= self.nbytes, ("SBUF arena overflow", self.off, nb_al)
        ap = self.t[0:parts, self.off // 2:(self.off + nb) // 2]
        self.off += nb_al
        if dtype == F32:
            ap = ap.bitcast(F32)
        if len(shape) == 2:
            ap = ap.rearrange("p (a b) -> p a b", a=shape[0], b=shape[1])
        elif len(shape) == 3:
            ap = ap.rearrange("p (a b c) -> p a b c", a=shape[0], b=shape[1], c=shape[2])
        return ap


def build_program(debug=False, NPAIRS_DBG=8, stages=('ffn1', 'proj', 'rwkv', 'attn', 'merge', 'ffn2'), NB_DBG=None,
                  JS_DBG=range(4), NT_DBG=None):
    nc = bass.Bass("TRN2", target_bir_lowering=False)
    P = Prog(nc)

    def din(name, shape):
        return nc.dram_tensor(name, list(shape), F32, kind="ExternalInput").ap()
    x = din("x", [S, D])
    ffn_norm = [din("ffn1_norm", [D]), din("ffn2_norm", [D])]
    ffn_win = [din("ffn1_w_in", [D, 2 * DFF]), din("ffn2_w_in", [D, 2 * DFF])]
    ffn_wout = [din("ffn1_w_out", [DFF, D]), din("ffn2_w_out", [DFF, D])]
    out = nc.dram_tensor("out", [S, D], F32, kind="ExternalOutput").ap()
    x1_d = nc.dram_tensor("x1_scr", [S, D], F32, kind="ExternalOutput" if debug else "Internal").ap()

    stack = ExitStack()
    ARENA_BYTES = 206 * 1024
    arena_t = stack.enter_context(nc.sbuf_tensor("arena", [128, ARENA_BYTES // 2], BF16))
    A = Arena(arena_t, ARENA_BYTES)
    psum = stack.enter_context(nc.psum_tensor("psum", [128, 4096], F32))

    def bank(b, n=512, off=0):
        return psum[:, b * 512 + off:b * 512 + off + n]

    ident_f = A.alloc([128], F32)
    ident = A.alloc([128], BF16)
    ones_col = A.alloc([1], F32)

    P.op('pool', lambda e: e.memset(ident_f, 0.0), writes=['ident_f'])
    P.op('pool', lambda e: e.affine_select(out=ident_f, in_=ident_f, pattern=[[-1, 128]],
                                           compare_op=ALU.not_equal, fill=1.0, base=0, channel_multiplier=1),
         reads=['ident_f'], writes=['ident_f'])
    P.op('dve', lambda e: e.tensor_copy(out=ident, in_=ident_f), reads=['ident_f'], writes=['ident'])

    const_mark = A.off

    TT = 256
    NSUB = TT // 128
    NT = S // TT if NT_DBG is None else NT_DBG
    KC = D // 128
    FC = DFF // 128

    def ffn_stage(si, src, dst):
        A.off = const_mark
        TT = 512
        NSUB = TT // 128
        NT = (S // TT) if NT_DBG is None else NT_DBG
        W1 = A.alloc([KC, 2 * DFF], BF16)
        W2 = A.alloc([FC, D], BF16)
        gb = A.alloc([D], F32)
        xt = [A.alloc([NSUB, D], F32)] * 2
        hb = [A.alloc([D], BF16) for _ in range(2)]
        hT = [A.alloc([KC, TT], BF16) for _ in range(2)]
        actT = A.alloc([FC, TT], BF16)
        sg = [A.alloc([TT], F32) for _ in range(2)]
        junk = A.alloc([D], BF16)
        ss = A.alloc([8], F32)
        pre = 's%d_' % si
        w1v = ffn_win[si].rearrange("(kc p) f -> p kc f", p=128)
        CH = 1408
        for kc in range(KC):
            for c in range(2 * DFF // CH):
                P.op('pool', lambda e, kc=kc, c=c: e.dma_start(out=W1[:, kc, c * CH:(c + 1) * CH],
                                                              in_=w1v[:, kc, c * CH:(c + 1) * CH]),
                     writes=[pre + 'W1'], dma_key=pre + 'W1')
        w2v = ffn_wout[si].rearrange("(fc p) d -> p fc d", p=128)
        for fc in range(FC):
            P.op('pool', lambda e, fc=fc: e.dma_start(out=W2[:, fc, :], in_=w2v[:, fc, :]),
                 writes=[pre + 'W2'], dma_key=pre + 'W2')
        P.op('sp', lambda e: e.dma_start(out=gb, in_=ffn_norm[si].partition_broadcast(128)),
             writes=[pre + 'gb'], dma_key=pre + 'gb')
        srcv = src.rearrange("(n s p) d -> n p s d", p=128, s=NSUB)
        dstv = dst.rearrange("(n s p) d -> n p s d", p=128, s=NSUB)
        for it in range(NT):
            sl = it % 2
            X = xt[sl]
            xr = pre + 'xt'
            P.op('sp', lambda e, it=it, X=X: e.dma_start(out=X, in_=srcv[it]),
                 writes=[xr], dma_key=xr)
            HT = hT[sl]
            for s in range(NSUB):
                hs = (it * NSUB + s) % 2
                H = hb[hs]
                hr = pre + 'hb%d' % hs
                P.op('act', lambda e, X=X, s=s: e.activation(out=junk, in_=X[:, s, :], func=AF.Square,
                                                             accum_out=ss[:, 0:1]),
                     reads=[xr], writes=[pre + 'junk', pre + 'ss'])
                P.op('act', lambda e: e.activation(out=ss[:, 1:2], in_=ss[:, 0:1], func=AF.Sqrt,
                                                   scale=1.0 / D, bias=RMS_EPS),
                     reads=[pre + 'ss'], writes=[pre + 'ss1'])
                P.op('dve', lambda e: e.reciprocal(out=ss[:, 2:3], in_=ss[:, 1:2]),
                     reads=[pre + 'ss1'], writes=[pre + 'ss2'])
                P.op('dve', lambda e, X=X, s=s, H=H: e.scalar_tensor_tensor(
                    out=H, in0=X[:, s, :], scalar=ss[:, 2:3], in1=gb, op0=ALU.mult, op1=ALU.mult),
                    reads=[xr, pre + 'ss2', pre + 'gb'], writes=[hr])
                pT = bank(0).bitcast(BF16)
                for kc in range(KC):
                    P.op('pe', lambda e, kc=kc, H=H, pT=pT: e.transpose(
                        out=pT[:, kc * 128:(kc + 1) * 128], in_=H[:, kc * 128:(kc + 1) * 128], identity=ident),
                        reads=[hr, 'ident'], writes=['psT'])
                P.op('act', lambda e, HT=HT, s=s, pT=pT: e.copy(
                    out=HT[:, :, s * 128:(s + 1) * 128], in_=pT.rearrange("p (k t) -> p k t", k=KC)),
                    reads=['psT'], writes=[pre + 'hT%d' % sl])
            for fc in range(FC):
                bg = 1 + 2 * (fc % 2)
                bu = bg + 1
                pgate = bank(bg)
                pup = bank(bu)
                for half, pdst, br in ((0, pgate, bg), (1, pup, bu)):
                    col = half * DFF + fc * 128
                    for kc in range(KC):
                        P.op('pe', lambda e, kc=kc, col=col, pdst=pdst, HT=HT: e.matmul(
                            pdst, lhsT=W1[:, kc, col:col + 128], rhs=HT[:, kc, :],
                            start=(kc == 0), stop=(kc == KC - 1)),
                            reads=[pre + 'W1', pre + 'hT%d' % sl], writes=['psG%d' % br])
                SG = sg[fc % 2]
                P.op('act', lambda e, pgate=pgate, SG=SG: e.activation(out=SG, in_=pgate, func=AF.Silu),
                     reads=['psG%d' % bg], writes=[pre + 'sg%d' % (fc % 2)])
                P.op('dve', lambda e, pup=pup, SG=SG, fc=fc: e.tensor_tensor(
                    out=actT[:, fc, :], in0=SG, in1=pup, op=ALU.mult),
                    reads=['psG%d' % bu, pre + 'sg%d' % (fc % 2)], writes=[pre + 'actT'])
            for s in range(NSUB):
                for dh in range(2):
                    b = 5 + (s * 2 + dh) % 2
                    pd = bank(b)
                    for fc in range(FC):
                        P.op('pe', lambda e, fc=fc, s=s, dh=dh, pd=pd: e.matmul(
                            pd, lhsT=actT[:, fc, s * 128:(s + 1) * 128], rhs=W2[:, fc, dh * 512:(dh + 1) * 512],
                            start=(fc == 0), stop=(fc == FC - 1)),
                            reads=[pre + 'actT', pre + 'W2'], writes=['psD%d' % b])
                    P.op('dve', lambda e, X=X, s=s, dh=dh, pd=pd: e.scalar_tensor_tensor(
                        out=X[:, s, dh * 512:(dh + 1) * 512], in0=pd, scalar=0.5,
                        in1=X[:, s, dh * 512:(dh + 1) * 512], op0=ALU.mult, op1=ALU.add),
                        reads=['psD%d' % b, xr], writes=[xr])
            P.op('sp', lambda e, it=it, X=X: e.dma_start(out=dstv[it], in_=X),
                 reads=[xr], writes=[pre + 'dst'], dma_key=pre + 'st')

    NCOL = 7712
    w_in = din("w_in", [D, NCOL])
    mix_norm = din("mix_norm", [D])
    rwkv_mu = din("rwkv_mu", [3360])
    b_gate = din("b_gate", [2048])
    qn = din("attn_q_norm", [64])
    kn = din("attn_k_norm", [64])
    kscr = "ExternalOutput" if debug else "Internal"
    if 'proj' not in stages:
        kscr = "ExternalInput"
    rkv_d = nc.dram_tensor("rkv_scr", [24, 128, S], F32, kind=kscr).ap()
    ta_d = nc.dram_tensor("ta_scr", [128, S], BF16, kind=kscr).ap()
    tg_d = nc.dram_tensor("tg_scr", [160, S], BF16, kind=kscr).ap()
    qk_d = nc.dram_tensor("qk_scr", [12, 128, S], BF16, kind=kscr).ap()
    v_d = nc.dram_tensor("v_scr", [S, 768], BF16, kind=kscr).ap()
    gate_d = nc.dram_tensor("gate_scr", [16, 128, S], BF16, kind=kscr).ap()

    def proj_stage():
        A.off = const_mark
        pre = 'p_'
        W = A.alloc([KC, NCOL], BF16)
        gb = A.alloc([D], F32)
        X = A.alloc([NSUB, D], F32)
        hb = [A.alloc([D], BF16) for _ in range(2)]
        hT = [A.alloc([KC, TT], BF16) for _ in range(2)]
        ss = A.alloc([8], F32)
        mu_t = A.alloc([27], F32)
        bg_t = A.alloc([16], F32)
        qg_t = A.alloc([2], F32)
        carry = A.alloc([27], F32)
        psb = [A.alloc([TT + 1], F32) for _ in range(2)]
        tmp = [A.alloc([TT], F32) for _ in range(2)]
        sq = [A.alloc([TT], BF16) for _ in range(2)]
        lnb = [A.alloc([TT], F32) for _ in range(2)]
        rkv_st = A.alloc([24, TT], F32)
        ta_st = A.alloc([TT], BF16)
        tg_st = A.alloc([2, TT], BF16)
        qk_st = A.alloc([12, TT], BF16)
        gate_st = A.alloc([16, TT], BF16)
        v_st = A.alloc([NSUB, 768], BF16)
        bones = A.alloc([128], BF16)
        print("proj_stage arena", A.off)
        wv = w_in.rearrange("(kc p) f -> p kc f", p=128)
        CH = 964
        for kc in range(KC):
            for c in range(NCOL // CH):
                P.op('pool', lambda e, kc=kc, c=c: e.dma_start(out=W[:, kc, c * CH:(c + 1) * CH],
                                                              in_=wv[:, kc, c * CH:(c + 1) * CH]),
                     writes=[pre + 'W'], dma_key=pre + 'W')
        P.op('sp', lambda e: e.dma_start(out=gb, in_=mix_norm.partition_broadcast(128)),
             writes=[pre + 'gb'], dma_key=pre + 'par')
        P.op('sp', lambda e: e.dma_start(out=mu_t[:, 0:26], in_=rwkv_mu[0:3328].rearrange("(b p) -> p b", p=128),
                                         allow_slow_non_contiguous=True), writes=[pre + 'mu'], dma_key=pre + 'par')
        P.op('sp', lambda e: e.dma_start(out=mu_t[0:32, 26:27], in_=rwkv_mu[3328:3360].rearrange("(p o) -> p o", o=1)),
             writes=[pre + 'mu'], dma_key=pre + 'par')
        P.op('sp', lambda e: e.dma_start(out=bg_t, in_=b_gate.rearrange("(b p) -> p b", p=128),
                                         allow_slow_non_contiguous=True), writes=[pre + 'bg'], dma_key=pre + 'par')
        for hh in range(2):
            P.op('sp', lambda e, hh=hh: e.dma_start(out=qg_t[hh * 64:(hh + 1) * 64, 0:1],
                                                     in_=qn.rearrange("(p o) -> p o", o=1)),
                 writes=[pre + 'qg'], dma_key=pre + 'par')
            P.op('sp', lambda e, hh=hh: e.dma_start(out=qg_t[hh * 64:(hh + 1) * 64, 1:2],
                                                     in_=kn.rearrange("(p o) -> p o", o=1)),
                 writes=[pre + 'qg'], dma_key=pre + 'par')
        P.op('pool', lambda e: e.tensor_scalar(out=qg_t[:, 0:1], in0=qg_t[:, 0:1], scalar1=0.125, scalar2=None,
                                               op0=ALU.mult), reads=[pre + 'qg'], writes=[pre + 'qg'])
        P.op('pool', lambda e: e.memset(carry, 0.0), writes=[pre + 'carry%d' % i for i in range(27)])
        P.op('pool', lambda e: e.memset(bones, 0.0), writes=[pre + 'bones'])
        P.op('pool', lambda e: e.memset(bones[0:64, 0:64], 1.0), reads=[pre + 'bones'], writes=[pre + 'bones'])
        P.op('pool', lambda e: e.memset(bones[64:128, 64:128], 1.0), reads=[pre + 'bones'], writes=[pre + 'bones'])

        blocks = []
        for b in range(24):
            blocks.append((b * 128, 128, 'rkv', b))
        blocks.append((3072, 128, 'ta', 24))
        blocks.append((3200, 128, 'tg0', 25))
        blocks.append((3328, 32, 'tg1', 26))
        for b in range(6):
            blocks.append((3360 + b * 128, 128, 'q', b))
        for b in range(6):
            blocks.append((4128 + b * 128, 128, 'k', 6 + b))
        for b in range(16):
            blocks.append((5664 + b * 128, 128, 'gate', b))

        srcv = x1_d.rearrange("(n s p) d -> n p s d", p=128, s=NSUB)
        xr = pre + 'X'
        for it in range(NT):
            t0 = it * TT
            sl = it % 2
            P.op('sp', lambda e, it=it: e.dma_start(out=X, in_=srcv[it]), writes=[xr], dma_key=xr)
            HT = hT[sl]
            htr = pre + 'hT%d' % sl
            for s in range(NSUB):
                hs = (it * NSUB + s) % 2
                H = hb[hs]
                hr = pre + 'hb%d' % hs
                P.op('act', lambda e, s=s, H=H: e.activation(out=H, in_=X[:, s, :], func=AF.Square,
                                                             accum_out=ss[:, 0:1]),
                     reads=[xr], writes=[hr, pre + 'ss'])
                P.op('act', lambda e: e.activation(out=ss[:, 1:2], in_=ss[:, 0:1], func=AF.Sqrt,
                                                   scale=1.0 / D, bias=RMS_EPS),
                     reads=[pre + 'ss'], writes=[pre + 'ss1'])
                P.op('dve', lambda e: e.reciprocal(out=ss[:, 2:3], in_=ss[:, 1:2]),
                     reads=[pre + 'ss1'], writes=[pre + 'ss2'])
                P.op('dve', lambda e, s=s, H=H: e.scalar_tensor_tensor(
                    out=H, in0=X[:, s, :], scalar=ss[:, 2:3], in1=gb, op0=ALU.mult, op1=ALU.mult),
                    reads=[xr, pre + 'ss2', pre + 'gb'], writes=[hr])
                pT = bank(0).bitcast(BF16)
                for kc in range(KC):
                    P.op('pe', lambda e, kc=kc, H=H, pT=pT: e.transpose(
                        out=pT[:, kc * 128:(kc + 1) * 128], in_=H[:, kc * 128:(kc + 1) * 128], identity=ident),
                        reads=[hr, 'ident'], writes=['psT'])
                P.op('act', lambda e, HT=HT, s=s, pT=pT: e.copy(
                    out=HT[:, :, s * 128:(s + 1) * 128], in_=pT.rearrange("p (k t) -> p k t", k=KC)),
                    reads=['psT'], writes=[htr])

            pending = [None]

            def flush():
                if pending[0] is None:
                    return
                pg, pgr, j, kind, idx = pending[0]
                pending[0] = None
                pss = bank(5)[:, 0:TT]
                P.op('pe', lambda e, j=j, pss=pss: e.matmul(pss, lhsT=bones, rhs=sq[j], start=True, stop=True),
                     reads=[pre + 'sq%d' % j, pre + 'bones'], writes=['pss'])
                P.op('act', lambda e, j=j, pss=pss: e.activation(out=lnb[j], in_=pss, func=AF.Ln,
                                                                 scale=1.0 / 64, bias=RMS_EPS),
                     reads=['pss'], writes=[pre + 'lnb%d' % j])
                P.op('act', lambda e, j=j: e.activation(out=lnb[j], in_=lnb[j], func=AF.Exp, scale=-0.5),
                     reads=[pre + 'lnb%d' % j], writes=[pre + 'lnb%d' % j])
                c = 0 if kind == 'q' else 1
                P.op('dve', lambda e, j=j, pg=pg, idx=idx, c=c: e.scalar_tensor_tensor(
                    out=qk_st[:, idx, :], in0=pg, scalar=qg_t[:, c:c + 1], in1=lnb[j], op0=ALU.mult, op1=ALU.mult),
                    reads=[pgr, pre + 'lnb%d' % j, pre + 'qg'], writes=[pre + 'qk_st'])

            for bi, (col0, M, kind, idx) in enumerate(blocks):
                b = 1 + bi % 4
                pgr = 'psG%d' % b
                pg = bank(b)[0:M, 0:TT]
                j = bi % 2
                for kc in range(KC):
                    P.op('pe', lambda e, kc=kc, col0=col0, M=M, pg=pg, HT=HT: e.matmul(
                        pg, lhsT=W[:, kc, col0:col0 + M], rhs=HT[:, kc, :], start=(kc == 0), stop=(kc == KC - 1)),
                        reads=[pre + 'W', htr], writes=[pgr])
                flush()
                if kind in ('rkv', 'ta', 'tg0', 'tg1'):
                    cr = pre + 'carry%d' % idx
                    pbr = pre + 'psb%d' % j
                    tr = pre + 'tmp%d' % j
                    PS = psb[j][0:M]
                    TM = tmp[j][0:M]
                    P.op('pool', lambda e, PS=PS, idx=idx, M=M: e.tensor_copy(out=PS[:, 0:1], in_=carry[0:M, idx:idx + 1]),
                         reads=[cr], writes=[pbr])
                    P.op('act', lambda e, PS=PS, pg=pg: e.copy(out=PS[:, 1:TT + 1], in_=pg),
                         reads=[pgr, pbr], writes=[pbr])
                    P.op('dve', lambda e, PS=PS, TM=TM: e.tensor_tensor(out=TM, in0=PS[:, 0:TT], in1=PS[:, 1:TT + 1],
                                                                        op=ALU.subtract),
                         reads=[pbr], writes=[tr])
                    P.op('pool', lambda e, PS=PS, idx=idx, M=M: e.tensor_copy(out=carry[0:M, idx:idx + 1],
                                                                              in_=PS[:, TT:TT + 1]),
                         reads=[pbr], writes=[cr])
                    if kind == 'rkv':
                        P.op('dve', lambda e, PS=PS, TM=TM, idx=idx: e.scalar_tensor_tensor(
                            out=rkv_st[:, idx, :], in0=TM, scalar=mu_t[:, idx:idx + 1], in1=PS[:, 1:TT + 1],
                            op0=ALU.mult, op1=ALU.add),
                            reads=[tr, pbr, pre + 'mu'], writes=[pre + 'rkv_st%d' % (idx // 8)])
                    else:
                        P.op('dve', lambda e, PS=PS, TM=TM, idx=idx, M=M: e.scalar_tensor_tensor(
                            out=TM, in0=TM, scalar=mu_t[0:M, idx:idx + 1], in1=PS[:, 1:TT + 1],
                            op0=ALU.mult, op1=ALU.add),
                            reads=[tr, pbr, pre + 'mu'], writes=[tr])
                        if kind == 'ta':
                            P.op('act', lambda e, TM=TM: e.activation(out=ta_st[0:64], in_=TM[0:64], func=AF.Tanh),
                                 reads=[tr], writes=[pre + 'ta_st'])
                            P.op('act', lambda e, TM=TM: e.copy(out=ta_st[64:128], in_=TM[64:128]),
                                 reads=[tr], writes=[pre + 'ta_st'])
                        elif kind == 'tg0':
                            P.op('act', lambda e, TM=TM: e.activation(out=tg_st[:, 0, :], in_=TM, func=AF.Sigmoid),
                                 reads=[tr], writes=[pre + 'tg_st'])
                        else:
                            P.op('act', lambda e, TM=TM: e.activation(out=tg_st[0:32, 1, :], in_=TM, func=AF.Sigmoid),
                                 reads=[tr], writes=[pre + 'tg_st'])
                elif kind in ('q', 'k'):
                    P.op('act', lambda e, pg=pg, j=j: e.activation(out=sq[j], in_=pg, func=AF.Square),
                         reads=[pgr], writes=[pre + 'sq%d' % j])
                    pending[0] = (pg, pgr, j, kind, idx)
                else:
                    P.op('act', lambda e, pg=pg, idx=idx: e.activation(out=gate_st[:, idx, :], in_=pg, func=AF.Sigmoid,
                                                                       bias=bg_t[:, idx:idx + 1]),
                         reads=[pgr, pre + 'bg'], writes=[pre + 'gate_st'])
            flush()
            for s in range(NSUB):
                for (c0, n, b) in ((4896, 512, 6), (5408, 256, 7)):
                    pv = bank(b)[:, 0:n]
                    for kc in range(KC):
                        P.op('pe', lambda e, kc=kc, s=s, c0=c0, n=n, pv=pv, HT=HT: e.matmul(
                            pv, lhsT=HT[:, kc, s * 128:(s + 1) * 128], rhs=W[:, kc, c0:c0 + n],
                            start=(kc == 0), stop=(kc == KC - 1)),
                            reads=[pre + 'W', htr], writes=['psV%d' % b])
                P.op('act', lambda e, s=s: e.copy(out=v_st[:, s, 0:512], in_=bank(6)),
                     reads=['psV6'], writes=[pre + 'v_st'])
                P.op('dve', lambda e, s=s: e.tensor_copy(out=v_st[:, s, 512:768], in_=bank(7)[:, 0:256]),
                     reads=['psV7'], writes=[pre + 'v_st'])
            rv = rkv_d.rearrange("b p t -> p b t")
            for g in range(3):
                P.op('sp', lambda e, g=g, t0=t0: e.dma_start(out=rv[:, g * 8:(g + 1) * 8, t0:t0 + TT],
                                                             in_=rkv_st[:, g * 8:(g + 1) * 8, :]),
                     reads=[pre + 'rkv_st%d' % g], writes=['d_rkv'], dma_key=pre + 'rkv_st%d' % g)
            P.op('sp', lambda e, t0=t0: e.dma_start(out=ta_d[:, t0:t0 + TT], in_=ta_st),
                 reads=[pre + 'ta_st'], writes=['d_ta'], dma_key=pre + 'ta_st')
            P.op('sp', lambda e, t0=t0: e.dma_start(out=tg_d[0:128, t0:t0 + TT], in_=tg_st[:, 0, :]),
                 reads=[pre + 'tg_st'], writes=['d_tg'], dma_key=pre + 'tg_st')
            P.op('sp', lambda e, t0=t0: e.dma_start(out=tg_d[128:160, t0:t0 + TT], in_=tg_st[0:32, 1, :]),
                 reads=[pre + 'tg_st'], writes=['d_tg'], dma_key=pre + 'tg_st')
            P.op('sp', lambda e, t0=t0: e.dma_start(out=qk_d.rearrange("b p t -> p b t")[:, :, t0:t0 + TT], in_=qk_st),
                 reads=[pre + 'qk_st'], writes=['d_qk'], dma_key=pre + 'qk_st')
            P.op('sp', lambda e, t0=t0: e.dma_start(out=gate_d.rearrange("b p t -> p b t")[:, :, t0:t0 + TT],
                                                    in_=gate_st),
                 reads=[pre + 'gate_st'], writes=['d_gate'], dma_key=pre + 'gate_st')
            P.op('sp', lambda e, t0=t0: e.dma_start(
                out=v_d[t0:t0 + TT, :].rearrange("(s p) c -> p s c", p=128), in_=v_st),
                reads=[pre + 'v_st'], writes=['d_v'], dma_key=pre + 'v_st')

    w2_d = din("rwkv_w2", [64, 1024])
    a2_d = din("rwkv_a2", [64, 1024])
    g2_d = din("rwkv_g2", [160, 1024])
    prm_names = ['rwkv_w0', 'rwkv_a0', 'rwkv_k_k', 'rwkv_k_a', 'rwkv_r_k', 'rwkv_ln_w', 'rwkv_ln_b']
    prm_d = [din(n, [1024]) for n in prm_names]
    ya_d = nc.dram_tensor("ya_scr", [8, 128, S], BF16, kind="ExternalOutput" if debug else "Internal").ap()
    TB = 1024
    NCH = TB // 128
    NB = S // TB if NB_DBG is None else NB_DBG
    C0 = float(np.exp(-0.5))
    GN_EPS = 64e-5

    def rwkv_stage(pairs=range(8)):
        A.off = const_mark
        pre = 'r_'
        WA = A.alloc([1024], BF16)
        G2a = A.alloc([1024], BF16)
        G2b = A.alloc([1024], BF16)
        prm = A.alloc([7, 8], F32)
        bones = A.alloc([128], BF16)
        mk4 = A.alloc([512], F32)
        mkL = A.alloc([2, 128], F32)
        E2 = A.alloc([64], F32)
        mrow = A.alloc([TB], F32)
        f32names = ['R', 'K', 'V', 'SG', 'AA', 'GG', 'KK', 'KM', 'T1', 'BVEC', 'CS', 'T2', 'T3', 'EP', 'EN', 'EPM',
                    'EC', 'BV', 'Y32', 'DD']
        T = {n: A.alloc([TB], F32) for n in f32names}
        bfnames = ['TA', 'TG0', 'TG1', 'TQ', 'BT', 'KT', 'BH', 'KH', 'VT', 'YB', 'YO']
        for n in bfnames:
            T[n] = A.alloc([TB], BF16)
        AR = A.alloc([NCH, 2, 128], BF16)
        TM4 = A.alloc([NCH, 4, 128], BF16)
        PC = A.alloc([NCH], F32)
        SC = [[A.alloc([512], BF16) for _ in range(2)] for _ in range(2)]
        LZ = [[A.alloc([2, 384], BF16) for _ in range(2)] for _ in range(2)]
        MCz = [A.alloc([2, 64], BF16) for _ in range(2)]
        QT = [A.alloc([128], BF16) for _ in range(2)]
        STz = A.alloc([2, 64], BF16)
        print("rwkv_stage arena", A.off)

        def R_(n):
            return pre + n
        P.op('pool', lambda e: e.dma_start(out=WA[0:64, :], in_=w2_d), writes=[R_('WA')], dma_key=R_('w'))
        P.op('pool', lambda e: e.dma_start(out=WA[64:128, :], in_=a2_d), writes=[R_('WA')], dma_key=R_('w'))
        P.op('pool', lambda e: e.dma_start(out=G2a, in_=g2_d[0:128, :]), writes=[R_('G2')], dma_key=R_('w'))
        P.op('pool', lambda e: e.dma_start(out=G2b[0:32, :], in_=g2_d[128:160, :]), writes=[R_('G2')], dma_key=R_('w'))
        for i in range(7):
            P.op('sp', lambda e, i=i: e.dma_start(out=prm[:, i, :], in_=prm_d[i].rearrange("(b p) -> p b", p=128),
                                                   allow_slow_non_contiguous=True),
                 writes=[R_('prm')], dma_key=R_('par'))
        P.op('pool', lambda e: e.memset(bones, 0.0), writes=[R_('bones')])
        P.op('pool', lambda e: e.memset(bones[0:64, 0:64], 1.0), reads=[R_('bones')], writes=[R_('bones')])
        P.op('pool', lambda e: e.memset(bones[64:128, 64:128], 1.0), reads=[R_('bones')], writes=[R_('bones')])
        P.op('pool', lambda e: e.memset(mk4, 1.0), writes=[R_('mk4')])
        for q in range(4):
            base = -1 if q % 2 == 0 else 0
            P.op('pool', lambda e, q=q, base=base: e.affine_select(
                out=mk4[:, q * 128:(q + 1) * 128], in_=mk4[:, q * 128:(q + 1) * 128], pattern=[[1, 128]],
                compare_op=ALU.is_ge, fill=0.0, base=base, channel_multiplier=-1),
                reads=[R_('mk4')], writes=[R_('mk4')])
        P.op('pool', lambda e: e.memset(mkL, 1.0), writes=[R_('mkL')])
        P.op('pool', lambda e: e.affine_select(out=mkL, in_=mkL, pattern=[[0, 2], [-1, 128]], compare_op=ALU.is_ge,
                                               fill=0.0, base=-1, channel_multiplier=1),
             reads=[R_('mkL')], writes=[R_('mkL')])
        P.op('pool', lambda e: e.tensor_copy(out=E2[0:64, :], in_=ident_f[0:64, 0:64]), reads=['ident_f'], writes=[R_('E2')])
        P.op('pool', lambda e: e.tensor_copy(out=E2[64:128, :], in_=ident_f[64:128, 64:128]), reads=['ident_f'],
             writes=[R_('E2')])
        P.op('pool', lambda e: e.memset(mrow, 1.0), writes=[R_('mrow')])
        P.op('pool', lambda e: e.memset(mrow.rearrange("p (c t) -> p c t", t=128)[:, :, 0:1], 0.0),
             reads=[R_('mrow')], writes=[R_('mrow')])

        def ch3(ap):
            return ap.rearrange("p (c t) -> p c t", t=128)

        def nm(x):
            if x.startswith('pb') or x in ('ident', 'ident_f'):
                return x
            return R_(x)

        def ew(eng, fn, reads, writes):
            P.op(eng, fn, reads=[nm(x) for x in reads], writes=[nm(x) for x in writes])

        for hp in pairs:
            cols = slice(hp * 128, (hp + 1) * 128)
            ew('pool', lambda e: e.memset(STz, 0.0), [], ['ST'])
            for sl_ in range(2):
                ew('pool', lambda e, sl_=sl_: e.memset(MCz[sl_], 0.0), [], ['MC%d' % sl_])
            for tb in range(NB):
                t0 = tb * TB
                tsl = slice(t0, t0 + TB)
                for i, n in enumerate(['R', 'K', 'V']):
                    P.op('sp', lambda e, i=i, n=n, hp=hp, tsl=tsl: e.dma_start(out=T[n], in_=rkv_d[i * 8 + hp, :, tsl]),
                         writes=[R_(n)], dma_key=R_('ld' + n))
                P.op('sp', lambda e, tsl=tsl: e.dma_start(out=T['TA'], in_=ta_d[:, tsl]), writes=[R_('TA')],
                     dma_key=R_('ldTA'))
                P.op('sp', lambda e, tsl=tsl: e.dma_start(out=T['TG0'], in_=tg_d[0:128, tsl]), writes=[R_('TG0')],
                     dma_key=R_('ldTG0'))
                P.op('sp', lambda e, tsl=tsl: e.dma_start(out=T['TG1'][0:32], in_=tg_d[128:160, tsl]),
                     writes=[R_('TG1')], dma_key=R_('ldTG1'))
                for hf in range(2):
                    hs = slice(hf * 512, (hf + 1) * 512)
                    ew('pe', lambda e, hs=hs, cols=cols: e.matmul(bank(1), lhsT=WA[0:64, cols], rhs=T['TA'][0:64, hs],
                                                                  start=True, stop=True), ['WA', 'TA'], ['pb1'])
                    ew('act', lambda e, hs=hs, hp=hp: e.activation(out=T['SG'][:, hs], in_=bank(1), func=AF.Sigmoid,
                                                                   bias=prm[:, 0, hp:hp + 1]), ['pb1', 'prm'], ['SG'])
                    ew('pe', lambda e, hs=hs, cols=cols: e.matmul(bank(2), lhsT=WA[64:128, cols], rhs=T['TA'][64:128, hs],
                                                                  start=True, stop=True), ['WA', 'TA'], ['pb2'])
                    ew('act', lambda e, hs=hs, hp=hp: e.activation(out=T['AA'][:, hs], in_=bank(2), func=AF.Sigmoid,
                                                                   bias=prm[:, 1, hp:hp + 1]), ['pb2', 'prm'], ['AA'])
                    ew('pe', lambda e, hs=hs, cols=cols: e.matmul(bank(3), lhsT=G2a[:, cols], rhs=T['TG0'][:, hs],
                                                                  start=True, stop=False), ['G2', 'TG0'], ['pb3', 'pb3'])
                    ew('pe', lambda e, hs=hs, cols=cols: e.matmul(bank(3), lhsT=G2b[0:32, cols], rhs=T['TG1'][0:32, hs],
                                                                  start=False, stop=True), ['G2', 'TG1'], ['pb3', 'pb3'])
                    ew('act', lambda e, hs=hs: e.copy(out=T['GG'][:, hs], in_=bank(3)), ['pb3', 'pb3'], ['GG'])
                ew('dve', lambda e, hp=hp: e.tensor_scalar(out=T['KK'], in0=T['K'], scalar1=prm[:, 2, hp:hp + 1],
                                                           scalar2=None, op0=ALU.mult), ['K', 'prm'], ['KK'])
                ew('act', lambda e: e.activation(out=T['TQ'], in_=T['KK'], func=AF.Square), ['KK'], ['TQ'])
                for hf in range(2):
                    hs = slice(hf * 512, (hf + 1) * 512)
                    ew('pe', lambda e, hs=hs: e.matmul(bank(4), lhsT=bones, rhs=T['TQ'][:, hs], start=True, stop=True),
                       ['bones', 'TQ'], ['pb4'])
                    ew('dve', lambda e, hs=hs: e.tensor_scalar(out=T['T1'][:, hs], in0=bank(4), scalar1=1e-19,
                                                               scalar2=None, op0=ALU.max), ['pb4'], ['T1'])
                ew('act', lambda e: e.activation(out=T['T1'], in_=T['T1'], func=AF.Ln), ['T1'], ['T1'])
                ew('act', lambda e: e.activation(out=T['T1'], in_=T['T1'], func=AF.Exp, scale=-0.5), ['T1'], ['T1'])
                ew('dve', lambda e: e.tensor_tensor(out=T['KK'], in0=T['KK'], in1=T['T1'], op=ALU.mult),
                   ['KK', 'T1'], ['KK'])
                ew('dve', lambda e, hp=hp: e.tensor_scalar(out=T['T1'], in0=T['AA'], scalar1=-1.0,
                                                           scalar2=prm[:, 3, hp:hp + 1], op0=ALU.add, op1=ALU.mult),
                   ['AA', 'prm'], ['T1'])
                ew('dve', lambda e: e.scalar_tensor_tensor(out=T['KM'], in0=T['T1'], scalar=1.0, in1=T['K'],
                                                           op0=ALU.add, op1=ALU.mult), ['T1', 'K'], ['KM'])
                ew('pool', lambda e: e.tensor_tensor(out=T['T1'], in0=T['R'], in1=T['KM'], op=ALU.mult),
                   ['R', 'KM'], ['T1'])
                ew('pool', lambda e, hp=hp: e.tensor_scalar(out=T['TQ'], in0=T['T1'], scalar1=prm[:, 4, hp:hp + 1],
                                                            scalar2=None, op0=ALU.mult), ['T1', 'prm'], ['TQ'])
                for hf in range(2):
                    hs = slice(hf * 512, (hf + 1) * 512)
                    ew('pe', lambda e, hs=hs: e.matmul(bank(5), lhsT=bones, rhs=T['TQ'][:, hs], start=True, stop=True),
                       ['bones', 'TQ'], ['pb5'])
                    ew('dve', lambda e, hs=hs: e.tensor_tensor(out=T['BV'][:, hs], in0=T['V'][:, hs], in1=bank(5),
                                                               op=ALU.mult), ['pb5', 'V'], ['BV'])
                ew('pool', lambda e: e.tensor_tensor(out=T['BVEC'], in0=T['KK'], in1=T['AA'], op=ALU.mult),
                   ['KK', 'AA'], ['BVEC'])
                ew('dve', lambda e: e.tensor_tensor_scan(out=T['CS'], data0=mrow, data1=T['SG'], initial=0.0,
                                                         op0=ALU.mult, op1=ALU.add), ['mrow', 'SG'], ['CS'])
                ew('pool', lambda e: e.tensor_tensor(out=T['T2'], in0=T['CS'], in1=T['SG'], op=ALU.subtract),
                   ['CS', 'SG'], ['T2'])
                ew('act', lambda e: e.activation(out=T['EP'], in_=T['CS'], func=AF.Exp, scale=-C0), ['CS'], ['EP'])
                ew('act', lambda e: e.activation(out=T['EN'], in_=T['CS'], func=AF.Exp, scale=C0), ['CS'], ['EN'])
                ew('act', lambda e: e.activation(out=T['EPM'], in_=T['T2'], func=AF.Exp, scale=-C0), ['T2'], ['EPM'])
                ew('dve', lambda e: e.tensor_tensor(
                    out=ch3(T['T3']), in0=ch3(T['CS']), in1=ch3(T['CS'])[:, :, 127:128].to_broadcast([128, NCH, 128]),
                    op=ALU.subtract), ['CS'], ['T3'])
                ew('act', lambda e: e.activation(out=T['EC'], in_=T['T3'], func=AF.Exp, scale=C0), ['T3'], ['EC'])
                ew('act', lambda e: e.activation(out=PC.rearrange("p (c o) -> p c o", o=1),
                                                 in_=ch3(T['CS'])[:, :, 127:128], func=AF.Exp, scale=-C0),
                   ['CS'], ['PC'])
                ew('dve', lambda e: e.scalar_tensor_tensor(out=AR[:, :, 0, :], in0=ch3(T['EPM']), scalar=-1.0,
                                                           in1=ch3(T['KK']), op0=ALU.mult, op1=ALU.mult),
                   ['EPM', 'KK'], ['AR0'])
                ew('pool', lambda e: e.tensor_tensor(out=AR[:, :, 1, :], in0=ch3(T['EP']), in1=ch3(T['R']), op=ALU.mult),
                   ['EP', 'R'], ['AR1'])
                ew('dve', lambda e: e.tensor_tensor(out=T['BT'], in0=T['EN'], in1=T['BVEC'], op=ALU.mult),
                   ['EN', 'BVEC'], ['BT'])
                ew('pool', lambda e: e.tensor_tensor(out=T['KT'], in0=T['EN'], in1=T['KM'], op=ALU.mult),
                   ['EN', 'KM'], ['KT'])
                ew('dve', lambda e: e.tensor_tensor(out=T['BH'], in0=T['EC'], in1=T['BVEC'], op=ALU.mult),
                   ['EC', 'BVEC'], ['BH'])
                ew('pool', lambda e: e.tensor_tensor(out=T['KH'], in0=T['EC'], in1=T['KM'], op=ALU.mult),
                   ['EC', 'KM'], ['KH'])
                ew('act', lambda e: e.copy(out=T['VT'], in_=T['V']), ['V'], ['VT'])
                pT = bank(0).bitcast(BF16)
                for c in range(NCH):
                    cs_ = slice(c * 128, (c + 1) * 128)
                    srcs = [(AR[:, c, 0, :], 'AR0'), (T['VT'][:, cs_], 'VT'), (T['BH'][:, cs_], 'BH'),
                            (T['KH'][:, cs_], 'KH')]
                    for q, (sap, sr) in enumerate(srcs):
                        ew('pe', lambda e, q=q, sap=sap: e.transpose(out=pT[:, q * 128:(q + 1) * 128], in_=sap,
                                                                     identity=ident), [sr, 'ident'], ['pb0'])
                    ew('act', lambda e, c=c: e.copy(out=TM4[:, c, :, :], in_=pT[:, 0:512].rearrange("p (q t) -> p q t", q=4)),
                       ['pb0'], ['TM4_%d' % c])
                P1 = (1, 2)
                P2 = ((4, 5), (6, 7))

                def partA(c, sl):
                    cs_ = slice(c * 128, (c + 1) * 128)
                    tm = 'TM4_%d' % c
                    arc = AR[:, c, :, :].rearrange("p a t -> p (a t)")
                    b1 = P1[sl]
                    ps1 = bank(b1)
                    for h2 in range(2):
                        psl = slice(64 * h2, 64 * h2 + 64)
                        scr = 'SC%d_%d' % (sl, h2)
                        ew('pe', lambda e, ps1=ps1, psl=psl, cs_=cs_, arc=arc: e.matmul(
                            ps1[:, 0:256], lhsT=T['BT'][psl, cs_], rhs=arc[psl, :], start=True, stop=True),
                            ['BT', 'AR0', 'AR1'], ['pb%d' % b1])
                        ew('pe', lambda e, ps1=ps1, psl=psl, cs_=cs_, arc=arc: e.matmul(
                            ps1[:, 256:512], lhsT=T['KT'][psl, cs_], rhs=arc[psl, :], start=True, stop=True),
                            ['KT', 'AR0', 'AR1'], ['pb%d' % b1])
                        ew('dve', lambda e, ps1=ps1, h2=h2, sl=sl: e.tensor_tensor(out=SC[sl][h2], in0=mk4, in1=ps1,
                                                                                   op=ALU.mult),
                           ['pb%d' % b1, 'mk4'], [scr])
                        b2 = P2[sl][h2]
                        ew('pe', lambda e, b2=b2, psl=psl, cs_=cs_, c=c: e.matmul(
                            bank(b2)[:, 384:512], lhsT=AR[psl, c, 0, :], rhs=T['BT'][psl, cs_],
                            start=True, stop=True), ['AR0', 'BT'], ['pb%d' % b2])
                    for h2 in range(2):
                        b2 = P2[sl][h2]
                        ew('dve', lambda e, h2=h2, b2=b2, sl=sl: e.tensor_tensor(
                            out=LZ[sl][0][:, h2, 0:128], in0=mkL[:, h2, :], in1=bank(b2)[:, 384:512], op=ALU.mult),
                            ['pb%d' % b2, 'mkL'], ['LL%d_0_%d' % (sl, h2)])
                    for h2 in range(2):
                        pb = 64 * h2
                        ew('pe', lambda e, h2=h2, pb=pb, c=c, sl=sl: e.matmul(
                            bank(3)[:, 256 + h2 * 64:256 + (h2 + 1) * 64], lhsT=SC[sl][h2][:, 256:384],
                            rhs=TM4[:, c, 1, pb:pb + 64], start=True, stop=True), ['SC%d_%d' % (sl, h2), tm], ['pb3'])
                    ew('pool', lambda e, c=c, sl=sl: e.tensor_copy(
                        out=LZ[sl][0][:, :, 128:192], in_=TM4[:, c, 0, :].rearrange("p (h k) -> p h k", h=2)),
                        [tm], ['ZZ%d_0_0' % sl, 'ZZ%d_0_1' % sl])
                    for h2 in range(2):
                        ew('act', lambda e, h2=h2, sl=sl: e.copy(out=LZ[sl][0][:, h2, 192:256],
                                                                 in_=bank(3)[:, 256 + h2 * 64:256 + (h2 + 1) * 64]),
                           ['pb3'], ['ZZ%d_0_%d' % (sl, h2)])

                def partB(c, sl, n):
                    pp = n % 2
                    for h2 in range(2):
                        b2 = P2[sl][h2]
                        ps2 = bank(b2)
                        pbr = 'pb%d' % b2
                        ltn = SC[sl][h2][:, 0:128] if n == 0 else LZ[sl][pp][:, h2, 256:384]
                        rds = ['LL%d_%d_%d' % (sl, pp, h2), 'ZZ%d_%d_%d' % (sl, pp, h2)] + \
                            (['SC%d_%d' % (sl, h2)] if n == 0 else [])
                        if n < 6:
                            ew('pe', lambda e, ps2=ps2, ltn=ltn, pp=pp, h2=h2, sl=sl: e.matmul(
                                ps2[:, 0:256], lhsT=ltn, rhs=LZ[sl][pp][:, h2, 0:256], start=True, stop=True),
                                rds, [pbr])
                            ew('pe', lambda e, ps2=ps2, ltn=ltn, pp=pp, h2=h2, sl=sl: e.matmul(
                                ps2[:, 256:384], lhsT=LZ[sl][pp][:, h2, 0:128], rhs=ltn, start=True, stop=True),
                                rds, [pbr])
                        else:
                            ew('pe', lambda e, ps2=ps2, ltn=ltn, pp=pp, h2=h2, sl=sl: e.matmul(
                                ps2[:, 128:256], lhsT=ltn, rhs=LZ[sl][pp][:, h2, 128:256], start=True, stop=True),
                                rds, [pbr])

                    def cp(h2):
                        b2 = P2[sl][h2]
                        ew('act', lambda e, h2=h2, b2=b2: e.copy(
                            out=LZ[sl][1 - pp][:, h2, :].rearrange("p (s t) -> p s t", s=3)[:, 0:3:2, :],
                            in_=bank(b2)[:, 0:384].rearrange("p (s t) -> p s t", s=3)[:, 0:3:2, :]),
                            ['pb%d' % b2], ['LL%d_%d_%d' % (sl, 1 - pp, h2)])

                    def ad(h2):
                        b2 = P2[sl][h2]
                        ew('dve', lambda e, h2=h2, b2=b2: e.tensor_tensor(
                            out=LZ[sl][1 - pp][:, h2, 128:256], in0=LZ[sl][pp][:, h2, 128:256],
                            in1=bank(b2)[:, 128:256], op=ALU.add),
                            ['pb%d' % b2, 'ZZ%d_%d_%d' % (sl, pp, h2)], ['ZZ%d_%d_%d' % (sl, 1 - pp, h2)])
                    if n < 6:
                        cp(0)
                        ad(1)
                        cp(1)
                        ad(0)
                    else:
                        ad(0)
                        ad(1)

                def partC(c, sl):
                    tm = 'TM4_%d' % c
                    ZF = LZ[sl][1]
                    p3 = bank(0)[:, 192:384]
                    for h2 in range(2):
                        pb = 64 * h2
                        psl = slice(pb, pb + 64)
                        zr = 'ZZ%d_1_%d' % (sl, h2)
                        ew('pe', lambda e, h2=h2, pb=pb, psl=psl, c=c: e.matmul(
                            p3[psl, 0:64], lhsT=ZF[:, h2, 128:192], rhs=TM4[:, c, 2, pb:pb + 64],
                            start=True, stop=True, tile_position=(0, pb)), [zr, tm], ['pb0'])
                        ew('pe', lambda e, h2=h2, pb=pb, psl=psl: e.matmul(
                            p3[psl, 64:192], lhsT=ZF[:, h2, 128:192], rhs=SC[sl][h2][:, 128:256],
                            start=True, stop=True, tile_position=(0, pb)), [zr, 'SC%d_%d' % (sl, h2)], ['pb0'])
                    for h2 in range(2):
                        psl = slice(64 * h2, 64 * h2 + 64)
                        ew('dve', lambda e, c=c, psl=psl, h2=h2: e.scalar_tensor_tensor(
                            out=MCz[sl][psl, h2, :], in0=E2[psl, :], scalar=PC[psl, c:c + 1], in1=p3[psl, 0:64],
                            op0=ALU.mult, op1=ALU.add), ['E2', 'PC', 'pb0'], ['MC%d' % sl])
                    ew('dve', lambda e, c=c: e.tensor_tensor(out=QT[sl], in0=AR[:, c, 1, :], in1=p3[:, 64:192],
                                                             op=ALU.add), ['pb0', 'AR1'], ['QT%d' % sl])

                def partD(c, sl):
                    cs_ = slice(c * 128, (c + 1) * 128)
                    tm = 'TM4_%d' % c
                    ZF = LZ[sl][1]
                    for h2 in range(2):
                        pb = 64 * h2
                        psl = slice(pb, pb + 64)
                        sb_ = 3 if h2 == 0 else 0
                        sr_ = 'pb%d' % sb_
                        psY = bank(sb_)[:, 0:128]
                        psS = bank(sb_)[:, 128:192]
                        UU = ZF[:, h2, 192:256]
                        zr = 'ZZ%d_1_%d' % (sl, h2)
                        scr = 'SC%d_%d' % (sl, h2)
                        ew('pe', lambda e, psl=psl, pb=pb, h2=h2, UU=UU, psY=psY: e.matmul(
                            psY[psl, :], lhsT=UU, rhs=SC[sl][h2][:, 128:256], start=True, stop=False,
                            tile_position=(0, pb)), [zr, scr], [sr_])
                        ew('pe', lambda e, psl=psl, pb=pb, h2=h2, c=c, psY=psY: e.matmul(
                            psY[psl, :], lhsT=TM4[:, c, 1, pb:pb + 64], rhs=SC[sl][h2][:, 384:512], start=False,
                            stop=False, tile_position=(0, pb)), [tm, scr], [sr_])
                        ew('pe', lambda e, psl=psl, pb=pb, psY=psY, h2=h2: e.matmul(
                            psY[psl, :], lhsT=STz[:, h2, :], rhs=QT[sl], start=False, stop=True,
                            tile_position=(0, pb)), ['ST', 'QT%d' % sl], [sr_])
                        ew('pe', lambda e, psl=psl, pb=pb, psS=psS, h2=h2: e.matmul(
                            psS[psl, :], lhsT=MCz[sl][:, h2, :], rhs=STz[:, h2, :], start=True, stop=False,
                            tile_position=(0, pb)), ['MC%d' % sl, 'ST'], [sr_])
                        ew('pe', lambda e, psl=psl, pb=pb, c=c, UU=UU, psS=psS: e.matmul(
                            psS[psl, :], lhsT=TM4[:, c, 2, pb:pb + 64], rhs=UU, start=False, stop=False,
                            tile_position=(0, pb)), [tm, zr], [sr_])
                        ew('pe', lambda e, psl=psl, pb=pb, c=c, psS=psS: e.matmul(
                            psS[psl, :], lhsT=TM4[:, c, 3, pb:pb + 64], rhs=TM4[:, c, 1, pb:pb + 64], start=False,
                            stop=True, tile_position=(0, pb)), [tm], [sr_])
                    for h2 in range(2):
                        pb = 64 * h2
                        psl = slice(pb, pb + 64)
                        sb_ = 3 if h2 == 0 else 0
                        sr_ = 'pb%d' % sb_
                        ew('act', lambda e, cs_=cs_, psl=psl, sb_=sb_: e.copy(out=T['Y32'][psl, cs_],
                                                                             in_=bank(sb_)[psl, 0:128]),
                           [sr_], ['Y32'])
                        ew('dve', lambda e, psl=psl, sb_=sb_, h2=h2: e.tensor_copy(out=STz[psl, h2, :],
                                                                                   in_=bank(sb_)[psl, 128:192]),
                           [sr_], ['ST'])

                for c0 in range(0, NCH, 2):
                    for sl in range(2):
                        partA(c0 + sl, sl)
                    for n in range(7):
                        for sl in range(2):
                            partB(c0 + sl, sl, n)
                    for sl in range(2):
                        partC(c0 + sl, sl)
                    for sl in range(2):
                        partD(c0 + sl, sl)
                ew('pool', lambda e: e.tensor_copy(out=T['YB'], in_=T['Y32']), ['Y32'], ['YB'])
                for hf in range(2):
                    hs = slice(hf * 512, (hf + 1) * 512)
                    ew('pe', lambda e, hs=hs: e.matmul(bank(1), lhsT=bones, rhs=T['YB'][:, hs], start=True, stop=True),
                       ['bones', 'YB'], ['pb1'])
                    ew('dve', lambda e, hs=hs: e.scalar_tensor_tensor(out=T['DD'][:, hs], in0=bank(1), scalar=-1.0 / 64,
                                                                      in1=T['Y32'][:, hs], op0=ALU.mult, op1=ALU.add),
                       ['pb1', 'Y32'], ['DD'])
                ew('act', lambda e: e.activation(out=T['TQ'], in_=T['DD'], func=AF.Square), ['DD'], ['TQ'])
                for hf in range(2):
                    hs = slice(hf * 512, (hf + 1) * 512)
                    ew('pe', lambda e, hs=hs: e.matmul(bank(2), lhsT=bones, rhs=T['TQ'][:, hs], start=True, stop=True),
                       ['bones', 'TQ'], ['pb2'])
                    ew('act', lambda e, hs=hs: e.activation(out=T['T1'][:, hs], in_=bank(2), func=AF.Ln, scale=1.0 / 64,
                                                            bias=GN_EPS), ['pb2'], ['T1'])
                ew('act', lambda e: e.activation(out=T['T1'], in_=T['T1'], func=AF.Exp, scale=-0.5), ['T1'], ['T1'])
                ew('dve', lambda e: e.tensor_tensor(out=T['DD'], in0=T['DD'], in1=T['T1'], op=ALU.mult),
                   ['DD', 'T1'], ['DD'])
                ew('dve', lambda e, hp=hp: e.tensor_scalar(out=T['DD'], in0=T['DD'], scalar1=prm[:, 5, hp:hp + 1],
                                                           scalar2=prm[:, 6, hp:hp + 1], op0=ALU.mult, op1=ALU.add),
                   ['DD', 'prm'], ['DD'])
                ew('pool', lambda e: e.tensor_tensor(out=T['DD'], in0=T['DD'], in1=T['BV'], op=ALU.add),
                   ['DD', 'BV'], ['DD'])
                ew('dve', lambda e: e.tensor_tensor(out=T['YO'], in0=T['DD'], in1=T['GG'], op=ALU.mult),
                   ['DD', 'GG'], ['YO'])
                P.op('sp', lambda e, hp=hp, tsl=tsl: e.dma_start(out=ya_d[hp, :, tsl], in_=T['YO']),
                     reads=[R_('YO')], writes=['d_ya'], dma_key=R_('stYO'))

    yb_d = nc.dram_tensor("yb_scr", [6, 128, S], BF16, kind="ExternalOutput" if debug else "Internal").ap()
    DIL = (1, 4, 16)

    def attn_stage_full(js=range(4)):
        A.off = const_mark
        pre = 'a_'

        def R_(n):
            return pre + n

        def nm(x):
            if x.startswith('pb') or x in ('ident', 'ident_f'):
                return x
            return R_(x)

        def ew(eng, fn, reads, writes):
            P.op(eng, fn, reads=[nm(x) for x in reads], writes=[nm(x) for x in writes])
        QH = A.alloc([S], BF16)
        KH = A.alloc([S], BF16)
        VX = A.alloc([32, 64], BF16)
        ONES = A.alloc([64], BF16)
        OT = [A.alloc([S], F32) for _ in range(3)]
        DEN = [A.alloc([S], F32) for _ in range(3)]
        PT = [A.alloc([256], BF16) for _ in range(4)]
        mask2 = A.alloc([256], BF16)
        RD = [A.alloc([512], F32) for _ in range(2)]
        YBS = [A.alloc([512], BF16) for _ in range(2)]
        print("attn_stage arena", A.off)
        P.op('pool', lambda e: e.memset(mask2, 1.0), writes=[R_('mask')])
        P.op('pool', lambda e: e.affine_select(out=mask2[:, 0:128], in_=mask2[:, 0:128], pattern=[[1, 128]],
                                               compare_op=ALU.is_ge, fill=0.0, base=0, channel_multiplier=-1),
             reads=[R_('mask')], writes=[R_('mask')])
        P.op('pool', lambda e: e.affine_select(out=mask2[:, 128:256], in_=mask2[:, 128:256], pattern=[[-1, 128]],
                                               compare_op=ALU.is_ge, fill=0.0, base=0, channel_multiplier=1),
             reads=[R_('mask')], writes=[R_('mask')])
        P.op('pool', lambda e: e.memset(ONES, 1.0), writes=[R_('ONES')])

        tcount = [0]
        ccount = [0]
        for j in js:
            for g in range(3):
                d = DIL[g]
                nb = S // d // 128
                h = 4 * g + j
                pair = h // 2
                pb = 64 * (h % 2)
                vv = v_d.rearrange("(m d) c -> d m c", d=d)
                for r in range(d):
                    for n0 in range(0, nb, 8):
                        n1 = min(nb, n0 + 8)
                        P.op('sp', lambda e, r=r, h=h, nb=nb, vv=vv, n0=n0, n1=n1: e.dma_start(
                            out=VX[:, r * nb + n0:r * nb + n1, :],
                            in_=vv[r, n0 * 128:n1 * 128, h * 64:(h + 1) * 64].rearrange("(n i) c -> i n c", i=128)),
                            writes=[R_('VX')], dma_key=R_('ldV'))
                P.op('sp', lambda e, pair=pair, pb=pb: e.dma_start(out=QH[0:64, :], in_=qk_d[pair, pb:pb + 64, :]),
                     writes=[R_('QH')], dma_key=R_('ldQ'))
                P.op('sp', lambda e, pair=pair, pb=pb: e.dma_start(out=KH[0:64, :], in_=qk_d[6 + pair, pb:pb + 64, :]),
                     writes=[R_('KH')], dma_key=R_('ldK'))
                qv = QH.rearrange("p (m d) -> p d m", d=d)
                kv = KH.rearrange("p (m d) -> p d m", d=d)
                otr = 'OT%d' % g
                otv = OT[g].rearrange("p (m d) -> p d m", d=d)
                dnv = DEN[g].rearrange("p (m d) -> p d m", d=d)
                tiles = []
                for r in range(d):
                    for n in range(nb):
                        tiles.append((r, n, tcount[0]))
                        tcount[0] += 1

                def emit_score(r, n, ti, kv=kv, qv=qv, nb=nb):
                    nq = 256 if n + 1 < nb else 128
                    sbk = 1 + ti % 2
                    ps = bank(sbk)[:, 0:nq]
                    pt = PT[ti % 4]
                    ptr = 'PT%d' % (ti % 4)
                    ew('pe', lambda e, ps=ps, r=r, n=n, nq=nq: e.matmul(
                        ps, lhsT=kv[0:64, r, 128 * n:128 * n + 128], rhs=qv[0:64, r, 128 * n:128 * n + nq],
                        start=True, stop=True), ['KH', 'QH'], ['pb%d' % sbk])
                    ew('act', lambda e, ps=ps, pt=pt, nq=nq: e.activation(out=pt[:, 0:nq], in_=ps, func=AF.Exp),
                       ['pb%d' % sbk], [ptr])
                    ew('pool', lambda e, pt=pt, nq=nq: e.tensor_tensor(out=pt[:, 0:nq], in0=pt[:, 0:nq],
                                                                      in1=mask2[:, 0:nq], op=ALU.mult),
                       [ptr, 'mask'], [ptr])

                def emit_pv(r, n, ti, otv=otv, dnv=dnv, nb=nb, g=g, otr=otr):
                    pt = PT[ti % 4]
                    ptr = 'PT%d' % (ti % 4)
                    obk = 3 + ti % 2
                    po = bank(obk)[0:64, 0:128]
                    pdn = bank(obk)[0:64, 128:256]
                    vt = r * nb + n
                    has_prev = n > 0
                    if has_prev:
                        ppt = PT[(ti - 1) % 4]
                        pptr = 'PT%d' % ((ti - 1) % 4)
                        ew('pe', lambda e, ppt=ppt, vt=vt: e.matmul(
                            po, lhsT=VX[:, vt - 1, :], rhs=ppt[:, 128:256], start=True, stop=False),
                            ['VX', pptr], ['pb%d' % obk])
                    ew('pe', lambda e, pt=pt, vt=vt: e.matmul(
                        po, lhsT=VX[:, vt, :], rhs=pt[:, 0:128], start=(not has_prev), stop=True),
                        ['VX', ptr], ['pb%d' % obk])
                    if has_prev:
                        ew('pe', lambda e, ppt=ppt: e.matmul(
                            pdn, lhsT=ONES, rhs=ppt[:, 128:256], start=True, stop=False),
                            ['ONES', pptr], ['pb%d' % obk])
                    ew('pe', lambda e, pt=pt: e.matmul(
                        pdn, lhsT=ONES, rhs=pt[:, 0:128], start=(not has_prev), stop=True),
                        ['ONES', ptr], ['pb%d' % obk])
                    ew('dve', lambda e, r=r, n=n: e.tensor_copy(
                        out=otv[0:64, r, 128 * n:128 * n + 128], in_=po), ['pb%d' % obk], [otr])
                    ew('dve', lambda e, r=r, n=n: e.tensor_copy(
                        out=dnv[0:64, r, 128 * n:128 * n + 128], in_=pdn), ['pb%d' % obk], ['DEN%d' % g])

                for idx, (r, n, ti) in enumerate(tiles):
                    emit_score(r, n, ti)
                    if idx >= 1:
                        emit_pv(*tiles[idx - 1])
                emit_pv(*tiles[-1])
            for ck in range(S // 512):
                csl = slice(ck * 512, (ck + 1) * 512)
                cc = ccount[0]
                ccount[0] += 1
                rd = RD[cc % 2]
                ew('pool', lambda e, rd=rd, csl=csl: e.tensor_tensor(out=rd[0:64, :], in0=DEN[0][0:64, csl],
                                                                     in1=DEN[1][0:64, csl], op=ALU.add),
                   ['DEN0', 'DEN1'], ['RD%d' % (cc % 2)])
                ew('pool', lambda e, rd=rd, csl=csl: e.tensor_tensor(out=rd[0:64, :], in0=rd[0:64, :],
                                                                     in1=DEN[2][0:64, csl], op=ALU.add),
                   ['RD%d' % (cc % 2), 'DEN2'], ['RD%d' % (cc % 2)])
                ew('dve', lambda e, rd=rd: e.reciprocal(out=rd[0:64, :], in_=rd[0:64, :]), ['RD%d' % (cc % 2)],
                   ['RD%d' % (cc % 2)])
                for g in range(3):
                    h = 4 * g + j
                    pair = h // 2
                    pb = 64 * (h % 2)
                    yi = (cc * 3 + g) % 2
                    ys = YBS[yi]
                    ew('pool', lambda e, g=g, csl=csl, rd=rd, ys=ys: e.tensor_tensor(
                        out=ys[0:64, :], in0=OT[g][0:64, csl], in1=rd[0:64, :], op=ALU.mult),
                        ['OT%d' % g, 'RD%d' % (cc % 2)], ['YBS%d' % yi])
                    P.op('sp', lambda e, pair=pair, pb=pb, csl=csl, ys=ys: e.dma_start(
                        out=yb_d[pair, pb:pb + 64, csl], in_=ys[0:64, :]),
                        reads=[R_('YBS%d' % yi)], writes=['d_yb'], dma_key=R_('stYB%d' % yi))

    wpr_d = din("w_proj_rwkv", [1024, 1024])
    wpa_d = din("w_proj_attn", [768, 1024])
    wo_d = din("w_out", [1024, 1024])
    x2_d = nc.dram_tensor("x2_scr", [S, D], F32, kind="ExternalOutput" if debug else "Internal").ap()

    def merge_stage():
        A.off = const_mark
        pre = 'm_'

        def R_(n):
            return pre + n

        def nm(x):
            if x.startswith('pb'):
                return x
            return R_(x)

        def ew(eng, fn, reads, writes):
            P.op(eng, fn, reads=[nm(x) for x in reads], writes=[nm(x) for x in writes])
        Wr = A.alloc([8, 1024], BF16)
        Wa = A.alloc([6, 1024], BF16)
        Wo = A.alloc([8, 1024], BF16)
        X = [A.alloc([NSUB, D], F32) for _ in range(2)]
        YA = [A.alloc([8, TT], BF16) for _ in range(2)]
        YB = [A.alloc([6, TT], BF16) for _ in range(2)]
        G = [A.alloc([16, TT], BF16) for _ in range(2)]
        MT = A.alloc([8, TT], BF16)
        t1 = [A.alloc([TT], F32) for _ in range(2)]
        t2 = [A.alloc([TT], F32) for _ in range(2)]
        print("merge_stage arena", A.off)
        for kc in range(8):
            P.op('pool', lambda e, kc=kc: e.dma_start(out=Wr[:, kc, :], in_=wpr_d[kc * 128:(kc + 1) * 128, :]),
                 writes=[R_('Wr')], dma_key=R_('w'))
            P.op('pool', lambda e, kc=kc: e.dma_start(out=Wo[:, kc, :], in_=wo_d[kc * 128:(kc + 1) * 128, :]),
                 writes=[R_('Wo')], dma_key=R_('w'))
        for kc in range(6):
            P.op('pool', lambda e, kc=kc: e.dma_start(out=Wa[:, kc, :], in_=wpa_d[kc * 128:(kc + 1) * 128, :]),
                 writes=[R_('Wa')], dma_key=R_('w'))
        srcv = x1_d.rearrange("(n s p) d -> n p s d", p=128, s=NSUB)
        dstv = x2_d.rearrange("(n s p) d -> n p s d", p=128, s=NSUB)
        for it in range(NT):
            sl = it % 2
            tsl = slice(it * TT, (it + 1) * TT)
            sfx = '%d' % sl
            P.op('sp', lambda e, it=it, sl=sl: e.dma_start(out=X[sl], in_=srcv[it]), writes=[R_('X' + sfx)],
                 dma_key=R_('ldX' + sfx))
            P.op('sp', lambda e, sl=sl, tsl=tsl: e.dma_start(out=YA[sl], in_=ya_d.rearrange("b p t -> p b t")[:, :, tsl]),
                 writes=[R_('YA' + sfx)], dma_key=R_('ldYA' + sfx))
            P.op('sp', lambda e, sl=sl, tsl=tsl: e.dma_start(out=YB[sl], in_=yb_d.rearrange("b p t -> p b t")[:, :, tsl]),
                 writes=[R_('YB' + sfx)], dma_key=R_('ldYB' + sfx))
            P.op('sp', lambda e, sl=sl, tsl=tsl: e.dma_start(out=G[sl], in_=gate_d.rearrange("b p t -> p b t")[:, :, tsl]),
                 writes=[R_('G' + sfx)], dma_key=R_('ldG' + sfx))
            for c in range(8):
                bk = 1 + c % 2
                pg = bank(bk)
                for kc in range(8):
                    ew('pe', lambda e, c=c, kc=kc, pg=pg, sl=sl: e.matmul(
                        pg[:, 0:TT], lhsT=Wr[:, kc, c * 128:(c + 1) * 128], rhs=YA[sl][:, kc, :],
                        start=(kc == 0), stop=(kc == 7)), ['Wr', 'YA' + sfx], ['pb%d' % bk])
                for kc in range(6):
                    ew('pe', lambda e, c=c, kc=kc, pg=pg, sl=sl: e.matmul(
                        pg[:, TT:2 * TT], lhsT=Wa[:, kc, c * 128:(c + 1) * 128], rhs=YB[sl][:, kc, :],
                        start=(kc == 0), stop=(kc == 5)), ['Wa', 'YB' + sfx], ['pb%d' % bk])
                q = c % 2
                ew('dve', lambda e, c=c, pg=pg, sl=sl, q=q: e.tensor_tensor(out=t1[q], in0=G[sl][:, c, :],
                                                                           in1=pg[:, 0:TT], op=ALU.mult),
                   ['G' + sfx, 'pb%d' % bk], ['t1%d' % q])
                ew('dve', lambda e, c=c, pg=pg, sl=sl, q=q: e.tensor_tensor(out=t2[q], in0=G[sl][:, 8 + c, :],
                                                                           in1=pg[:, TT:2 * TT], op=ALU.mult),
                   ['G' + sfx, 'pb%d' % bk], ['t2%d' % q])
                ew('pool', lambda e, c=c, q=q: e.tensor_tensor(out=MT[:, c, :], in0=t1[q], in1=t2[q], op=ALU.add),
                   ['t1%d' % q, 't2%d' % q], ['MT'])
            for s in range(NSUB):
                for dh in range(2):
                    bk = 3 + (s * 2 + dh) % 2
                    pd = bank(bk)
                    for c in range(8):
                        ew('pe', lambda e, c=c, s=s, dh=dh, pd=pd: e.matmul(
                            pd, lhsT=MT[:, c, s * 128:(s + 1) * 128], rhs=Wo[:, c, dh * 512:(dh + 1) * 512],
                            start=(c == 0), stop=(c == 7)), ['MT', 'Wo'], ['pb%d' % bk])
                    ew('dve', lambda e, s=s, dh=dh, pd=pd, sl=sl: e.tensor_tensor(
                        out=X[sl][:, s, dh * 512:(dh + 1) * 512], in0=X[sl][:, s, dh * 512:(dh + 1) * 512], in1=pd,
                        op=ALU.add), ['pb%d' % bk, 'X' + sfx], ['X' + sfx])
            P.op('sp', lambda e, it=it, sl=sl: e.dma_start(out=dstv[it], in_=X[sl]),
                 reads=[R_('X' + sfx)], writes=['d_x2'], dma_key=R_('stX' + sfx))

    if 'ffn1' in stages:
        ffn_stage(0, x, x1_d)
        P.sync_all()
    if 'proj' in stages:
        proj_stage()
        P.sync_all()
    if 'rwkv' in stages:
        rwkv_stage(range(NPAIRS_DBG))
        P.sync_all()
    if 'attn' in stages:
        attn_stage_full(JS_DBG)
        P.sync_all()
    if 'merge' in stages:
        merge_stage()
        P.sync_all()
    if 'ffn2' in stages:
        ffn_stage(1, x2_d if 'merge' in stages else x1_d, out)
    P.sync_all()
    P.op('sp', None)
    P.finalize_and_emit(stack)
    stack.close()
    return nc


_CACHE = {}


SHARED_KEYS = ['ffn1_norm', 'ffn1_w_in', 'ffn1_w_out', 'ffn2_norm', 'ffn2_w_in', 'ffn2_w_out',
               'w_in', 'mix_norm', 'rwkv_mu', 'b_gate', 'attn_q_norm', 'attn_k_norm',
               'w_proj_rwkv', 'w_proj_attn', 'w_out', 'rwkv_w2', 'rwkv_a2', 'rwkv_g2', 'rwkv_w0', 'rwkv_a0', 'rwkv_k_k', 'rwkv_k_a', 'rwkv_r_k', 'rwkv_ln_w', 'rwkv_ln_b']


def make_shared(inputs):
    shared = {}
    for k in SHARED_KEYS:
        v = np.asarray(inputs[k], dtype=np.float32)
        v = v.reshape(v.shape[1:])
        if k == 'rwkv_r_k':
            v = v.reshape(-1)
        shared[k] = np.ascontiguousarray(v)
    return shared


def kernel(**inputs):
    if 'nc' not in _CACHE:
        _CACHE['nc'] = build_program()
    nc = _CACHE['nc']
    x = np.ascontiguousarray(inputs['x'], dtype=np.float32)
    shared = make_shared(inputs)
    in_maps = []
    for c in range(NCORES):
        m = dict(shared)
        m['x'] = x[c]
        in_maps.append(m)
    res = run_bass_kernel_spmd(nc, in_maps, core_ids=list(range(NCORES)))
    return np.stack([np.asarray(r['out']) for r in res.results], axis=0)
```

```python
import numpy as np
from contextlib import ExitStack
import concourse.bass as bass
import concourse.mybir as mybir
from concourse.bass_utils import run_bass_kernel_spmd
from concourse.alu_op_type import AluOpType as ALU

F32 = mybir.dt.float32
BF16 = mybir.dt.bfloat16
AF = mybir.ActivationFunctionType
AX = mybir.AxisListType

S = 4096
D = 1024
DFF = 2816
NCORES = 8
RMS_EPS = 1e-6

ENGS = ['pe', 'act', 'dve', 'pool', 'sp']
MAXOPS = [0]
SEM_LIM = 30000
DMA_LIM = 1800


class Prog:
    def __init__(self, nc):
        self.nc = nc
        self.ops = []
        self.eng_ops = {e: [] for e in ENGS}
        self.last_w = {}
        self.readers = {}
        self.dma_cnt = {}
        self.barrier = {e: None for e in ENGS}

    def op(self, eng, fn, reads=(), writes=(), dma_key=None):
        mo = MAXOPS[0]
        if mo and len(self.ops) >= mo and fn is not None:
            return None
        if mo and len(self.ops) == mo - 1 and fn is not None:
            print("LAST OP:", eng, fn.__code__.co_firstlineno, reads, writes)
        oid = len(self.ops)
        deps = set()
        dma_deps = {}
        writes = list(writes) + [r for r in reads if (r.startswith('pb') or r.startswith('ps')) and r not in writes]

        def add(o):
            od = self.ops[o]
            if od['dma_key'] is not None:
                k = od['dma_key']
                dma_deps[k] = self.dma_cnt[k]
            else:
                deps.add(o)
        for r in reads:
            if r in self.last_w:
                add(self.last_w[r])
        for w in writes:
            if w in self.last_w:
                add(self.last_w[w])
            for rd in self.readers.get(w, {}).values():
                add(rd)
        if self.barrier[eng] is not None:
            bd, bdma = self.barrier[eng]
            for o in bd:
                deps.add(o)
            for k, v in bdma.items():
                dma_deps[k] = max(dma_deps.get(k, 0), v)
            self.barrier[eng] = None
        cnt = None
        if dma_key is not None:
            self.dma_cnt[dma_key] = self.dma_cnt.get(dma_key, 0) + 1
            cnt = self.dma_cnt[dma_key]
        o = dict(id=oid, eng=eng, fn=fn, deps=deps, dma_deps=dma_deps, dma_key=dma_key,
                 dma_cnt=cnt, idx=len(self.eng_ops[eng]), sig=False)
        self.ops.append(o)
        self.eng_ops[eng].append(o)
        ch = eng if dma_key is None else 'dma:' + dma_key
        for r in reads:
            self.readers.setdefault(r, {})[ch] = oid
        for w in writes:
            self.last_w[w] = oid
            self.readers[w] = {}
        return oid

    def sync_all(self):
        bd = set()
        for e in ENGS:
            for o in reversed(self.eng_ops[e]):
                if o['dma_key'] is None and o['fn'] is not None:
                    bd.add(o['id'])
                    break
        bdma = dict(self.dma_cnt)
        for e in ENGS:
            self.barrier[e] = (set(bd), dict(bdma))

    def finalize_and_emit(self, stack):
        nc = self.nc
        for o in self.ops:
            per = {}
            for d in o['deps']:
                od = self.ops[d]
                if od['eng'] == 'pe' and o['eng'] == 'pe':
                    continue
                e = od['eng']
                if e not in per or self.ops[per[e]]['idx'] < od['idx']:
                    per[e] = d
            o['cdeps'] = per
            for d in per.values():
                self.ops[d]['sig'] = True
        sems = {}

        def get_sem(name):
            return sems[name]
        for e in ENGS:
            c = 0
            for o in self.eng_ops[e]:
                if o['dma_key'] is None and o['sig']:
                    c += 1
                    o['sigval'] = c
        for o in self.ops:
            waits = {}
            for e, d in o['cdeps'].items():
                v = self.ops[d]['sigval']
                key = ('c_%s_%d' % (e, (v - 1) // SEM_LIM))
                val = (v - 1) % SEM_LIM + 1
                waits[key] = max(waits.get(key, 0), val)
            for k, n in o['dma_deps'].items():
                key = ('d_%s_%d' % (k, (n - 1) // DMA_LIM))
                val = 16 * ((n - 1) % DMA_LIM + 1)
                waits[key] = max(waits.get(key, 0), val)
            o['waits'] = waits
        names = set()
        for o in self.ops:
            names.update(o['waits'].keys())
            if o['dma_key'] is not None:
                names.add('d_%s_%d' % (o['dma_key'], (o['dma_cnt'] - 1) // DMA_LIM))
            elif o['sig']:
                names.add('c_%s_%d' % (o['eng'], (o['sigval'] - 1) // SEM_LIM))
        for nm in sorted(names):
            sems[nm] = stack.enter_context(nc.semaphore(nm))
        print("n_sems", len(names), "n_ops", len(self.ops), {e: len(v) for e, v in self.eng_ops.items()})
        block = stack.enter_context(nc.Block())
        decos = {'pe': block.tensor, 'act': block.scalar, 'dve': block.vector,
                 'pool': block.gpsimd, 'sp': block.sync}
        for e in ENGS:
            ops = self.eng_ops[e]

            def body(eng, ops=ops, e=e):
                waited = {}
                for o in ops:
                    for key, val in o['waits'].items():
                        if waited.get(key, 0) >= val:
                            continue
                        waited[key] = val
                        eng.wait_ge(get_sem(key), val)
                    if o['fn'] is None:
                        continue
                    ins = o['fn'](eng)
                    if o['dma_key'] is not None:
                        n = o['dma_cnt']
                        ins.then_inc(get_sem('d_%s_%d' % (o['dma_key'], (n - 1) // DMA_LIM)), 16)
                    elif o['sig']:
                        v = o['sigval']
                        ins.then_inc(get_sem('c_%s_%d' % (e, (v - 1) // SEM_LIM)), 1)
            decos[e](body)


class Arena:
    def __init__(self, tensor, nbytes):
        self.t = tensor
        self.nbytes = nbytes
        self.off = 0

    def alloc(self, shape, dtype, parts=128):
        n = int(np.prod(shape))
        esz = 4 if dtype == F32 else 2
        nb = n * esz
        nb_al = (nb + 63) // 64 * 64
        assert self.off + nb_al <= self.nbytes, ("SBUF arena overflow", self.off, nb_al)
        ap = self.t[0:parts, self.off // 2:(self.off + nb) // 2]
        self.off += nb_al
        if dtype == F32:
            ap = ap.bitcast(F32)
        if len(shape) == 2:
            ap = ap.rearrange("p (a b) -> p a b", a=shape[0], b=shape[1])
        elif len(shape) == 3:
            ap = ap.rearrange("p (a b c) -> p a b c", a=shape[0], b=shape[1], c=shape[2])
        return ap


def build_program(debug=False, NPAIRS_DBG=8, stages=('ffn1', 'proj', 'rwkv', 'attn', 'merge', 'ffn2'), NB_DBG=None,
                  JS_DBG=range(4), NT_DBG=None):
    nc = bass.Bass("TRN2", target_bir_lowering=False)
    P = Prog(nc)

    def din(name, shape):
        return nc.dram_tensor(name, list(shape), F32, kind="ExternalInput").ap()
    x = din("x", [S, D])
    ffn_norm = [din("ffn1_norm", [D]), din("ffn2_norm", [D])]
    ffn_win = [din("ffn1_w_in", [D, 2 * DFF]), din("ffn2_w_in", [D, 2 * DFF])]
    ffn_wout = [din("ffn1_w_out", [DFF, D]), din("ffn2_w_out", [DFF, D])]
    out = nc.dram_tensor("out", [S, D], F32, kind="ExternalOutput").ap()
    x1_d = nc.dram_tensor("x1_scr", [S, D], F32, kind="ExternalOutput" if debug else "Internal").ap()

    stack = ExitStack()
    ARENA_BYTES = 206 * 1024
    arena_t = stack.enter_context(nc.sbuf_tensor("arena", [128, ARENA_BYTES // 2], BF16))
    A = Arena(arena_t, ARENA_BYTES)
    psum = stack.enter_context(nc.psum_tensor("psum", [128, 4096], F32))

    def bank(b, n=512, off=0):
        return psum[:, b * 512 + off:b * 512 + off + n]

    ident_f = A.alloc([128], F32)
    ident = A.alloc([128], BF16)
    ones_col = A.alloc([1], F32)

    P.op('pool', lambda e: e.memset(ident_f, 0.0), writes=['ident_f'])
    P.op('pool', lambda e: e.affine_select(out=ident_f, in_=ident_f, pattern=[[-1, 128]],
                                           compare_op=ALU.not_equal, fill=1.0, base=0, channel_multiplier=1),
         reads=['ident_f'], writes=['ident_f'])
    P.op('dve', lambda e: e.tensor_copy(out=ident, in_=ident_f), reads=['ident_f'], writes=['ident'])

    const_mark = A.off

    TT = 256
    NSUB = TT // 128
    NT = S // TT if NT_DBG is None else NT_DBG
    KC = D // 128
    FC = DFF // 128

    def ffn_stage(si, src, dst):
        A.off = const_mark
        TT = 512
        NSUB = TT // 128
        NT = (S // TT) if NT_DBG is None else NT_DBG
        W1 = A.alloc([KC, 2 * DFF], BF16)
        W2 = A.alloc([FC, D], BF16)
        gb = A.alloc([D], F32)
        xt = [A.alloc([NSUB, D], F32)] * 2
        hb = [A.alloc([D], BF16) for _ in range(2)]
        hT = [A.alloc([KC, TT], BF16) for _ in range(2)]
        actT = A.alloc([FC, TT], BF16)
        sg = [A.alloc([TT], F32) for _ in range(2)]
        junk = A.alloc([D], BF16)
        ss = A.alloc([8], F32)
        pre = 's%d_' % si
        w1v = ffn_win[si].rearrange("(kc p) f -> p kc f", p=128)
        CH = 1408
        for kc in range(KC):
            for c in range(2 * DFF // CH):
                P.op('pool', lambda e, kc=kc, c=c: e.dma_start(out=W1[:, kc, c * CH:(c + 1) * CH],
                                                              in_=w1v[:, kc, c * CH:(c + 1) * CH]),
                     writes=[pre + 'W1'], dma_key=pre + 'W1')
        w2v = ffn_wout[si].rearrange("(fc p) d -> p fc d", p=128)
        for fc in range(FC):
            P.op('pool', lambda e, fc=fc: e.dma_start(out=W2[:, fc, :], in_=w2v[:, fc, :]),
                 writes=[pre + 'W2'], dma_key=pre + 'W2')
        P.op('sp', lambda e: e.dma_start(out=gb, in_=ffn_norm[si].partition_broadcast(128)),
             writes=[pre + 'gb'], dma_key=pre + 'gb')
        srcv = src.rearrange("(n s p) d -> n p s d", p=128, s=NSUB)
        dstv = dst.rearrange("(n s p) d -> n p s d", p=128, s=NSUB)
        for it in range(NT):
            sl = it % 2
            X = xt[sl]
            xr = pre + 'xt'
            P.op('sp', lambda e, it=it, X=X: e.dma_start(out=X, in_=srcv[it]),
                 writes=[xr], dma_key=xr)
            HT = hT[sl]
            for s in range(NSUB):
                hs = (it * NSUB + s) % 2
                H = hb[hs]
                hr = pre + 'hb%d' % hs
                P.op('act', lambda e, X=X, s=s: e.activation(out=junk, in_=X[:, s, :], func=AF.Square,
                                                             accum_out=ss[:, 0:1]),
                     reads=[xr], writes=[pre + 'junk', pre + 'ss'])
                P.op('act', lambda e: e.activation(out=ss[:, 1:2], in_=ss[:, 0:1], func=AF.Sqrt,
                                                   scale=1.0 / D, bias=RMS_EPS),
                     reads=[pre + 'ss'], writes=[pre + 'ss1'])
                P.op('dve', lambda e: e.reciprocal(out=ss[:, 2:3], in_=ss[:, 1:2]),
                     reads=[pre + 'ss1'], writes=[pre + 'ss2'])
                P.op('dve', lambda e, X=X, s=s, H=H: e.scalar_tensor_tensor(
                    out=H, in0=X[:, s, :], scalar=ss[:, 2:3], in1=gb, op0=ALU.mult, op1=ALU.mult),
                    reads=[xr, pre + 'ss2', pre + 'gb'], writes=[hr])
                pT = bank(0).bitcast(BF16)
                for kc in range(KC):
                    P.op('pe', lambda e, kc=kc, H=H, pT=pT: e.transpose(
                        out=pT[:, kc * 128:(kc + 1) * 128], in_=H[:, kc * 128:(kc + 1) * 128], identity=ident),
                        reads=[hr, 'ident'], writes=['psT'])
                P.op('act', lambda e, HT=HT, s=s, pT=pT: e.copy(
                    out=HT[:, :, s * 128:(s + 1) * 128], in_=pT.rearrange("p (k t) -> p k t", k=KC)),
                    reads=['psT'], writes=[pre + 'hT%d' % sl])
            for fc in range(FC):
                bg = 1 + 2 * (fc % 2)
                bu = bg + 1
                pgate = bank(bg)
                pup = bank(bu)
                for half, pdst, br in ((0, pgate, bg), (1, pup, bu)):
                    col = half * DFF + fc * 128
                    for kc in range(KC):
                        P.op('pe', lambda e, kc=kc, col=col, pdst=pdst, HT=HT: e.matmul(
                            pdst, lhsT=W1[:, kc, col:col + 128], rhs=HT[:, kc, :],
                            start=(kc == 0), stop=(kc == KC - 1)),
                            reads=[pre + 'W1', pre + 'hT%d' % sl], writes=['psG%d' % br])
                SG = sg[fc % 2]
                P.op('act', lambda e, pgate=pgate, SG=SG: e.activation(out=SG, in_=pgate, func=AF.Silu),
                     reads=['psG%d' % bg], writes=[pre + 'sg%d' % (fc % 2)])
                P.op('dve', lambda e, pup=pup, SG=SG, fc=fc: e.tensor_tensor(
                    out=actT[:, fc, :], in0=SG, in1=pup, op=ALU.mult),
                    reads=['psG%d' % bu, pre + 'sg%d' % (fc % 2)], writes=[pre + 'actT'])
            for s in range(NSUB):
                for dh in range(2):
                    b = 5 + (s * 2 + dh) % 2
                    pd = bank(b)
                    for fc in range(FC):
                        P.op('pe', lambda e, fc=fc, s=s, dh=dh, pd=pd: e.matmul(
                            pd, lhsT=actT[:, fc, s * 128:(s + 1) * 128], rhs=W2[:, fc, dh * 512:(dh + 1) * 512],
                            start=(fc == 0), stop=(fc == FC - 1)),
                            reads=[pre + 'actT', pre + 'W2'], writes=['psD%d' % b])
                    P.op('dve', lambda e, X=X, s=s, dh=dh, pd=pd: e.scalar_tensor_tensor(
                        out=X[:, s, dh * 512:(dh + 1) * 512], in0=pd, scalar=0.5,
                        in1=X[:, s, dh * 512:(dh + 1) * 512], op0=ALU.mult, op1=ALU.add),
                        reads=['psD%d' % b, xr], writes=[xr])
            P.op('sp', lambda e, it=it, X=X: e.dma_start(out=dstv[it], in_=X),
                 reads=[xr], writes=[pre + 'dst'], dma_key=pre + 'st')

    NCOL = 7712
    w_in = din("w_in", [D, NCOL])
    mix_norm = din("mix_norm", [D])
    rwkv_mu = din("rwkv_mu", [3360])
    b_gate = din("b_gate", [2048])
    qn = din("attn_q_norm", [64])
    kn = din("attn_k_norm", [64])
    kscr = "ExternalOutput" if debug else "Internal"
    if 'proj' not in stages:
        kscr = "ExternalInput"
    rkv_d = nc.dram_tensor("rkv_scr", [24, 128, S], F32, kind=kscr).ap()
    ta_d = nc.dram_tensor("ta_scr", [128, S], BF16, kind=kscr).ap()
    tg_d = nc.dram_tensor("tg_scr", [160, S], BF16, kind=kscr).ap()
    qk_d = nc.dram_tensor("qk_scr", [12, 128, S], BF16, kind=kscr).ap()
    v_d = nc.dram_tensor("v_scr", [S, 768], BF16, kind=kscr).ap()
    gate_d = nc.dram_tensor("gate_scr", [16, 128, S], BF16, kind=kscr).ap()

    def proj_stage():
        A.off = const_mark
        pre = 'p_'
        W = A.alloc([KC, NCOL], BF16)
        gb = A.alloc([D], F32)
        X = A.alloc([NSUB, D], F32)
        hb = [A.alloc([D], BF16) for _ in range(2)]
        hT = [A.alloc([KC, TT], BF16) for _ in range(2)]
        ss = A.alloc([8], F32)
        mu_t = A.alloc([27], F32)
        bg_t = A.alloc([16], F32)
        qg_t = A.alloc([2], F32)
        carry = A.alloc([27], F32)
        psb = [A.alloc([TT + 1], F32) for _ in range(2)]
        tmp = [A.alloc([TT], F32) for _ in range(2)]
        sq = [A.alloc([TT], BF16) for _ in range(2)]
        lnb = [A.alloc([TT], F32) for _ in range(2)]
        rkv_st = A.alloc([24, TT], F32)
        ta_st = A.alloc([TT], BF16)
        tg_st = A.alloc([2, TT], BF16)
        qk_st = A.alloc([12, TT], BF16)
        gate_st = A.alloc([16, TT], BF16)
        v_st = A.alloc([NSUB, 768], BF16)
        bones = A.alloc([128], BF16)
        print("proj_stage arena", A.off)
        wv = w_in.rearrange("(kc p) f -> p kc f", p=128)
        CH = 964
        for kc in range(KC):
            for c in range(NCOL // CH):
                P.op('pool', lambda e, kc=kc, c=c: e.dma_start(out=W[:, kc, c * CH:(c + 1) * CH],
                                                              in_=wv[:, kc, c * CH:(c + 1) * CH]),
                     writes=[pre + 'W'], dma_key=pre + 'W')
        P.op('sp', lambda e: e.dma_start(out=gb, in_=mix_norm.partition_broadcast(128)),
             writes=[pre + 'gb'], dma_key=pre + 'par')
        P.op('sp', lambda e: e.dma_start(out=mu_t[:, 0:26], in_=rwkv_mu[0:3328].rearrange("(b p) -> p b", p=128),
                                         allow_slow_non_contiguous=True), writes=[pre + 'mu'], dma_key=pre + 'par')
        P.op('sp', lambda e: e.dma_start(out=mu_t[0:32, 26:27], in_=rwkv_mu[3328:3360].rearrange("(p o) -> p o", o=1)),
             writes=[pre + 'mu'], dma_key=pre + 'par')
        P.op('sp', lambda e: e.dma_start(out=bg_t, in_=b_gate.rearrange("(b p) -> p b", p=128),
                                         allow_slow_non_contiguous=True), writes=[pre + 'bg'], dma_key=pre + 'par')
        for hh in range(2):
            P.op('sp', lambda e, hh=hh: e.dma_start(out=qg_t[hh * 64:(hh + 1) * 64, 0:1],
                                                     in_=qn.rearrange("(p o) -> p o", o=1)),
                 writes=[pre + 'qg'], dma_key=pre + 'par')
            P.op('sp', lambda e, hh=hh: e.dma_start(out=qg_t[hh * 64:(hh + 1) * 64, 1:2],
                                                     in_=kn.rearrange("(p o) -> p o", o=1)),
                 writes=[pre + 'qg'], dma_key=pre + 'par')
        P.op('pool', lambda e: e.tensor_scalar(out=qg_t[:, 0:1], in0=qg_t[:, 0:1], scalar1=0.125, scalar2=None,
                                               op0=ALU.mult), reads=[pre + 'qg'], writes=[pre + 'qg'])
        P.op('pool', lambda e: e.memset(carry, 0.0), writes=[pre + 'carry%d' % i for i in range(27)])
        P.op('pool', lambda e: e.memset(bones, 0.0), writes=[pre + 'bones'])
        P.op('pool', lambda e: e.memset(bones[0:64, 0:64], 1.0), reads=[pre + 'bones'], writes=[pre + 'bones'])
        P.op('pool', lambda e: e.memset(bones[64:128, 64:128], 1.0), reads=[pre + 'bones'], writes=[pre + 'bones'])

        blocks = []
        for b in range(24):
            blocks.append((b * 128, 128, 'rkv', b))
        blocks.append((3072, 128, 'ta', 24))
        blocks.append((3200, 128, 'tg0', 25))
        blocks.append((3328, 32, 'tg1', 26))
        for b in range(6):
            blocks.append((3360 + b * 128, 128, 'q', b))
        for b in range(6):
            blocks.append((4128 + b * 128, 128, 'k', 6 + b))
        for b in range(16):
            blocks.append((5664 + b * 128, 128, 'gate', b))

        srcv = x1_d.rearrange("(n s p) d -> n p s d", p=128, s=NSUB)
        xr = pre + 'X'
        for it in range(NT):
            t0 = it * TT
            sl = it % 2
            P.op('sp', lambda e, it=it: e.dma_start(out=X, in_=srcv[it]), writes=[xr], dma_key=xr)
            HT = hT[sl]
            htr = pre + 'hT%d' % sl
            for s in range(NSUB):
                hs = (it * NSUB + s) % 2
                H = hb[hs]
                hr = pre + 'hb%d' % hs
                P.op('act', lambda e, s=s, H=H: e.activation(out=H, in_=X[:, s, :], func=AF.Square,
                                                             accum_out=ss[:, 0:1]),
                     reads=[xr], writes=[hr, pre + 'ss'])
                P.op('act', lambda e: e.activation(out=ss[:, 1:2], in_=ss[:, 0:1], func=AF.Sqrt,
                                                   scale=1.0 / D, bias=RMS_EPS),
                     reads=[pre + 'ss'], writes=[pre + 'ss1'])
                P.op('dve', lambda e: e.reciprocal(out=ss[:, 2:3], in_=ss[:, 1:2]),
                     reads=[pre + 'ss1'], writes=[pre + 'ss2'])
                P.op('dve', lambda e, s=s, H=H: e.scalar_tensor_tensor(
                    out=H, in0=X[:, s, :], scalar=ss[:, 2:3], in1=gb, op0=ALU.mult, op1=ALU.mult),
                    reads=[xr, pre + 'ss2', pre + 'gb'], writes=[hr])
                pT = bank(0).bitcast(BF16)
                for kc in range(KC):
                    P.op('pe', lambda e, kc=kc, H=H, pT=pT: e.transpose(
                        out=pT[:, kc * 128:(kc + 1) * 128], in_=H[:, kc * 128:(kc + 1) * 128], identity=ident),
                        reads=[hr, 'ident'], writes=['psT'])
                P.op('act', lambda e, HT=HT, s=s, pT=pT: e.copy(
                    out=HT[:, :, s * 128:(s + 1) * 128], in_=pT.rearrange("p (k t) -> p k t", k=KC)),
                    reads=['psT'], writes=[htr])

            pending = [None]

            def flush():
                if pending[0] is None:
                    return
                pg, pgr, j, kind, idx = pending[0]
                pending[0] = None
                pss = bank(5)[:, 0:TT]
                P.op('pe', lambda e, j=j, pss=pss: e.matmul(pss, lhsT=bones, rhs=sq[j], start=True, stop=True),
                     reads=[pre + 'sq%d' % j, pre + 'bones'], writes=['pss'])
                P.op('act', lambda e, j=j, pss=pss: e.activation(out=lnb[j], in_=pss, func=AF.Ln,
                                                                 scale=1.0 / 64, bias=RMS_EPS),
                     reads=['pss'], writes=[pre + 'lnb%d' % j])
                P.op('act', lambda e, j=j: e.activation(out=lnb[j], in_=lnb[j], func=AF.Exp, scale=-0.5),
                     reads=[pre + 'lnb%d' % j], writes=[pre + 'lnb%d' % j])
                c = 0 if kind == 'q' else 1
                P.op('dve', lambda e, j=j, pg=pg, idx=idx, c=c: e.scalar_tensor_tensor(
                    out=qk_st[:, idx, :], in0=pg, scalar=qg_t[:, c:c + 1], in1=lnb[j], op0=ALU.mult, op1=ALU.mult),
                    reads=[pgr, pre + 'lnb%d' % j, pre + 'qg'], writes=[pre + 'qk_st'])

            for bi, (col0, M, kind, idx) in enumerate(blocks):
                b = 1 + bi % 4
                pgr = 'psG%d' % b
                pg = bank(b)[0:M, 0:TT]
                j = bi % 2
                for kc in range(KC):
                    P.op('pe', lambda e, kc=kc, col0=col0, M=M, pg=pg, HT=HT: e.matmul(
                        pg, lhsT=W[:, kc, col0:col0 + M], rhs=HT[:, kc, :], start=(kc == 0), stop=(kc == KC - 1)),
                        reads=[pre + 'W', htr], writes=[pgr])
                flush()
                if kind in ('rkv', 'ta', 'tg0', 'tg1'):
                    cr = pre + 'carry%d' % idx
                    pbr = pre + 'psb%d' % j
                    tr = pre + 'tmp%d' % j
                    PS = psb[j][0:M]
                    TM = tmp[j][0:M]
                    P.op('pool', lambda e, PS=PS, idx=idx, M=M: e.tensor_copy(out=PS[:, 0:1], in_=carry[0:M, idx:idx + 1]),
                         reads=[cr], writes=[pbr])
                    P.op('act', lambda e, PS=PS, pg=pg: e.copy(out=PS[:, 1:TT + 1], in_=pg),
                         reads=[pgr, pbr], writes=[pbr])
                    P.op('dve', lambda e, PS=PS, TM=TM: e.tensor_tensor(out=TM, in0=PS[:, 0:TT], in1=PS[:, 1:TT + 1],
                                                                        op=ALU.subtract),
                         reads=[pbr], writes=[tr])
                    P.op('pool', lambda e, PS=PS, idx=idx, M=M: e.tensor_copy(out=carry[0:M, idx:idx + 1],
                                                                              in_=PS[:, TT:TT + 1]),
                         reads=[pbr], writes=[cr])
                    if kind == 'rkv':
                        P.op('dve', lambda e, PS=PS, TM=TM, idx=idx: e.scalar_tensor_tensor(
                            out=rkv_st[:, idx, :], in0=TM, scalar=mu_t[:, idx:idx + 1], in1=PS[:, 1:TT + 1],
                            op0=ALU.mult, op1=ALU.add),
                            reads=[tr, pbr, pre + 'mu'], writes=[pre + 'rkv_st%d' % (idx // 8)])
                    else:
                        P.op('dve', lambda e, PS=PS, TM=TM, idx=idx, M=M: e.scalar_tensor_tensor(
                            out=TM, in0=TM, scalar=mu_t[0:M, idx:idx + 1], in1=PS[:, 1:TT + 1],
                            op0=ALU.mult, op1=ALU.add),
                            reads=[tr, pbr, pre + 'mu'], writes=[tr])
                        if kind == 'ta':
                            P.op('act', lambda e, TM=TM: e.activation(out=ta_st[0:64], in_=TM[0:64], func=AF.Tanh),
                                 reads=[tr], writes=[pre + 'ta_st'])
                            P.op('act', lambda e, TM=TM: e.copy(out=ta_st[64:128], in_=TM[64:128]),
                                 reads=[tr], writes=[pre + 'ta_st'])
                        elif kind == 'tg0':
                            P.op('act', lambda e, TM=TM: e.activation(out=tg_st[:, 0, :], in_=TM, func=AF.Sigmoid),
                                 reads=[tr], writes=[pre + 'tg_st'])
                        else:
                            P.op('act', lambda e, TM=TM: e.activation(out=tg_st[0:32, 1, :], in_=TM, func=AF.Sigmoid),
                                 reads=[tr], writes=[pre + 'tg_st'])
                elif kind in ('q', 'k'):
                    P.op('act', lambda e, pg=pg, j=j: e.activation(out=sq[j], in_=pg, func=AF.Square),
                         reads=[pgr], writes=[pre + 'sq%d' % j])
                    pending[0] = (pg, pgr, j, kind, idx)
                else:
                    P.op('act', lambda e, pg=pg, idx=idx: e.activation(out=gate_st[:, idx, :], in_=pg, func=AF.Sigmoid,
                                                                       bias=bg_t[:, idx:idx + 1]),
                         reads=[pgr, pre + 'bg'], writes=[pre + 'gate_st'])
            flush()
            for s in range(NSUB):
                for (c0, n, b) in ((4896, 512, 6), (5408, 256, 7)):
                    pv = bank(b)[:, 0:n]
                    for kc in range(KC):
                        P.op('pe', lambda e, kc=kc, s=s, c0=c0, n=n, pv=pv, HT=HT: e.matmul(
                            pv, lhsT=HT[:, kc, s * 128:(s + 1) * 128], rhs=W[:, kc, c0:c0 + n],
                            start=(kc == 0), stop=(kc == KC - 1)),
                            reads=[pre + 'W', htr], writes=['psV%d' % b])
                P.op('act', lambda e, s=s: e.copy(out=v_st[:, s, 0:512], in_=bank(6)),
                     reads=['psV6'], writes=[pre + 'v_st'])
                P.op('dve', lambda e, s=s: e.tensor_copy(out=v_st[:, s, 512:768], in_=bank(7)[:, 0:256]),
                     reads=['psV7'], writes=[pre + 'v_st'])
            rv = rkv_d.rearrange("b p t -> p b t")
            for g in range(3):
                P.op('sp', lambda e, g=g, t0=t0: e.dma_start(out=rv[:, g * 8:(g + 1) * 8, t0:t0 + TT],
                                                             in_=rkv_st[:, g * 8:(g + 1) * 8, :]),
                     reads=[pre + 'rkv_st%d' % g], writes=['d_rkv'], dma_key=pre + 'rkv_st%d' % g)
            P.op('sp', lambda e, t0=t0: e.dma_start(out=ta_d[:, t0:t0 + TT], in_=ta_st),
                 reads=[pre + 'ta_st'], writes=['d_ta'], dma_key=pre + 'ta_st')
            P.op('sp', lambda e, t0=t0: e.dma_start(out=tg_d[0:128, t0:t0 + TT], in_=tg_st[:, 0, :]),
                 reads=[pre + 'tg_st'], writes=['d_tg'], dma_key=pre + 'tg_st')
            P.op('sp', lambda e, t0=t0: e.dma_start(out=tg_d[128:160, t0:t0 + TT], in_=tg_st[0:32, 1, :]),
                 reads=[pre + 'tg_st'], writes=['d_tg'], dma_key=pre + 'tg_st')
            P.op('sp', lambda e, t0=t0: e.dma_start(out=qk_d.rearrange("b p t -> p b t")[:, :, t0:t0 + TT], in_=qk_st),
                 reads=[pre + 'qk_st'], writes=['d_qk'], dma_key=pre + 'qk_st')
            P.op('sp', lambda e, t0=t0: e.dma_start(out=gate_d.rearrange("b p t -> p b t")[:, :, t0:t0 + TT],
                                                    in_=gate_st),
                 reads=[pre + 'gate_st'], writes=['d_gate'], dma_key=pre + 'gate_st')
            P.op('sp', lambda e, t0=t0: e.dma_start(
                out=v_d[t0:t0 + TT, :].rearrange("(s p) c -> p s c", p=128), in_=v_st),
                reads=[pre + 'v_st'], writes=['d_v'], dma_key=pre + 'v_st')

    w2_d = din("rwkv_w2", [64, 1024])
    a2_d = din("rwkv_a2", [64, 1024])
    g2_d = din("rwkv_g2", [160, 1024])
    prm_names = ['rwkv_w0', 'rwkv_a0', 'rwkv_k_k', 'rwkv_k_a', 'rwkv_r_k', 'rwkv_ln_w', 'rwkv_ln_b']
    prm_d = [din(n, [1024]) for n in prm_names]
    ya_d = nc.dram_tensor("ya_scr", [8, 128, S], BF16, kind="ExternalOutput" if debug else "Internal").ap()
    TB = 1024
    NCH = TB // 128
    NB = S // TB if NB_DBG is None else NB_DBG
    C0 = float(np.exp(-0.5))
    GN_EPS = 64e-5

    def rwkv_stage(pairs=range(8)):
        A.off = const_mark
        pre = 'r_'
        WA = A.alloc([1024], BF16)
        G2a = A.alloc([1024], BF16)
        G2b = A.alloc([1024], BF16)
        prm = A.alloc([7, 8], F32)
        bones = A.alloc([128], BF16)
        mk4 = A.alloc([512], F32)
        mkL = A.alloc([2, 128], F32)
        E2 = A.alloc([64], F32)
        mrow = A.alloc([TB], F32)
        f32names = ['R', 'K', 'V', 'SG', 'AA', 'GG', 'KK', 'KM', 'T1', 'BVEC', 'CS', 'T2', 'T3', 'EP', 'EN', 'EPM',
                    'EC', 'BV', 'Y32', 'DD']
        T = {n: A.alloc([TB], F32) for n in f32names}
        bfnames = ['TA', 'TG0', 'TG1', 'TQ', 'BT', 'KT', 'BH', 'KH', 'VT', 'YB', 'YO']
        for n in bfnames:
            T[n] = A.alloc([TB], BF16)
        AR = A.alloc([NCH, 2, 128], BF16)
        TM4 = A.alloc([NCH, 4, 128], BF16)
        PC = A.alloc([NCH], F32)
        SC = [[A.alloc([512], BF16) for _ in range(2)] for _ in range(2)]
        LZ = [[A.alloc([2, 384], BF16) for _ in range(2)] for _ in range(2)]
        MCz = [A.alloc([2, 64], BF16) for _ in range(2)]
        QT = [A.alloc([128], BF16) for _ in range(2)]
        STz = A.alloc([2, 64], BF16)
        DBT = ['BT', 'KT', 'GG', 'BV', 'Y32']
        T2 = {n: [T[n], A.alloc([TB], BF16 if n in ('BT', 'KT') else F32)] for n in DBT}
        AR2 = [AR, A.alloc([NCH, 2, 128], BF16)]
        TM42 = [TM4, A.alloc([NCH, 4, 128], BF16)]
        PC2 = [PC, A.alloc([NCH], F32)]
        par = [0]
        coll = [None]

        class TP(dict):
            def __getitem__(self, k):
                if k in T2:
                    return T2[k][par[0]]
                return dict.__getitem__(self, k)

        class BP(object):
            def __init__(self, bufs):
                self.bufs = bufs

            def __getitem__(self, idx):
                return self.bufs[par[0]][idx]

            def rearrange(self, *a_, **k_):
                return self.bufs[par[0]].rearrange(*a_, **k_)
        T = TP(T)
        AR = BP(AR2)
        TM4 = BP(TM42)
        PC = BP(PC2)
        DBNAMES = set(DBT) | {'AR0', 'AR1', 'PC'} | {'TM4_%d' % c_ for c_ in range(NCH)}
        print("rwkv_stage arena", A.off)

        def R_(n):
            return pre + n
        P.op('pool', lambda e: e.dma_start(out=WA[0:64, :], in_=w2_d), writes=[R_('WA')], dma_key=R_('w'))
        P.op('pool', lambda e: e.dma_start(out=WA[64:128, :], in_=a2_d), writes=[R_('WA')], dma_key=R_('w'))
        P.op('pool', lambda e: e.dma_start(out=G2a, in_=g2_d[0:128, :]), writes=[R_('G2')], dma_key=R_('w'))
        P.op('pool', lambda e: e.dma_start(out=G2b[0:32, :], in_=g2_d[128:160, :]), writes=[R_('G2')], dma_key=R_('w'))
        for i in range(7):
            P.op('sp', lambda e, i=i: e.dma_start(out=prm[:, i, :], in_=prm_d[i].rearrange("(b p) -> p b", p=128),
                                                   allow_slow_non_contiguous=True),
                 writes=[R_('prm')], dma_key=R_('par'))
        P.op('pool', lambda e: e.memset(bones, 0.0), writes=[R_('bones')])
        P.op('pool', lambda e: e.memset(bones[0:64, 0:64], 1.0), reads=[R_('bones')], writes=[R_('bones')])
        P.op('pool', lambda e: e.memset(bones[64:128, 64:128], 1.0), reads=[R_('bones')], writes=[R_('bones')])
        P.op('pool', lambda e: e.memset(mk4, 1.0), writes=[R_('mk4')])
        for q in range(4):
            base = -1 if q % 2 == 0 else 0
            P.op('pool', lambda e, q=q, base=base: e.affine_select(
                out=mk4[:, q * 128:(q + 1) * 128], in_=mk4[:, q * 128:(q + 1) * 128], pattern=[[1, 128]],
                compare_op=ALU.is_ge, fill=0.0, base=base, channel_multiplier=-1),
                reads=[R_('mk4')], writes=[R_('mk4')])
        P.op('pool', lambda e: e.memset(mkL, 1.0), writes=[R_('mkL')])
        P.op('pool', lambda e: e.affine_select(out=mkL, in_=mkL, pattern=[[0, 2], [-1, 128]], compare_op=ALU.is_ge,
                                               fill=0.0, base=-1, channel_multiplier=1),
             reads=[R_('mkL')], writes=[R_('mkL')])
        P.op('pool', lambda e: e.tensor_copy(out=E2[0:64, :], in_=ident_f[0:64, 0:64]), reads=['ident_f'], writes=[R_('E2')])
        P.op('pool', lambda e: e.tensor_copy(out=E2[64:128, :], in_=ident_f[64:128, 64:128]), reads=['ident_f'],
             writes=[R_('E2')])
        P.op('pool', lambda e: e.memset(mrow, 1.0), writes=[R_('mrow')])
        P.op('pool', lambda e: e.memset(mrow.rearrange("p (c t) -> p c t", t=128)[:, :, 0:1], 0.0),
             reads=[R_('mrow')], writes=[R_('mrow')])

        def ch3(ap):
            return ap.rearrange("p (c t) -> p c t", t=128)

        def nm(x):
            if x.startswith('pb') or x in ('ident', 'ident_f'):
                return x
            if x in DBNAMES:
                return R_(x) + '@%d' % par[0]
            return R_(x)

        def pdma(eng, fn, reads=(), writes=(), dma_key=None):
            p_ = par[0]

            def fn2(e, fn=fn, p_=p_):
                par[0] = p_
                return fn(e)
            args = (eng, fn2, list(reads), list(writes), dma_key)
            if coll[0] is not None:
                coll[0].append(args)
            else:
                P.op(args[0], args[1], reads=args[2], writes=args[3], dma_key=args[4])

        def ew(eng, fn, reads, writes):
            pdma(eng, fn, [nm(x) for x in reads], [nm(x) for x in writes])

        for hp in pairs:
            cols = slice(hp * 128, (hp + 1) * 128)
            ew('pool', lambda e: e.memset(STz, 0.0), [], ['ST'])
            for sl_ in range(2):
                ew('pool', lambda e, sl_=sl_: e.memset(MCz[sl_], 0.0), [], ['MC%d' % sl_])
            def prep(tb, hp=hp, cols=cols):
                t0 = tb * TB
                tsl = slice(t0, t0 + TB)
                for i, n in enumerate(['R', 'K', 'V']):
                    pdma('sp', lambda e, i=i, n=n, hp=hp, tsl=tsl: e.dma_start(out=T[n], in_=rkv_d[i * 8 + hp, :, tsl]),
                         writes=[R_(n)], dma_key=R_('ld' + n))
                pdma('sp', lambda e, tsl=tsl: e.dma_start(out=T['TA'], in_=ta_d[:, tsl]), writes=[R_('TA')],
                     dma_key=R_('ldTA'))
                pdma('sp', lambda e, tsl=tsl: e.dma_start(out=T['TG0'], in_=tg_d[0:128, tsl]), writes=[R_('TG0')],
                     dma_key=R_('ldTG0'))
                pdma('sp', lambda e, tsl=tsl: e.dma_start(out=T['TG1'][0:32], in_=tg_d[128:160, tsl]),
                     writes=[R_('TG1')], dma_key=R_('ldTG1'))
                for hf in range(2):
                    hs = slice(hf * 512, (hf + 1) * 512)
                    ew('pe', lambda e, hs=hs, cols=cols: e.matmul(bank(1), lhsT=WA[0:64, cols], rhs=T['TA'][0:64, hs],
                                                                  start=True, stop=True), ['WA', 'TA'], ['pb1'])
                    ew('act', lambda e, hs=hs, hp=hp: e.activation(out=T['SG'][:, hs], in_=bank(1), func=AF.Sigmoid,
                                                                   bias=prm[:, 0, hp:hp + 1]), ['pb1', 'prm'], ['SG'])
                    ew('pe', lambda e, hs=hs, cols=cols: e.matmul(bank(2), lhsT=WA[64:128, cols], rhs=T['TA'][64:128, hs],
                                                                  start=True, stop=True), ['WA', 'TA'], ['pb2'])
                    ew('act', lambda e, hs=hs, hp=hp: e.activation(out=T['AA'][:, hs], in_=bank(2), func=AF.Sigmoid,
                                                                   bias=prm[:, 1, hp:hp + 1]), ['pb2', 'prm'], ['AA'])
                    ew('pe', lambda e, hs=hs, cols=cols: e.matmul(bank(3), lhsT=G2a[:, cols], rhs=T['TG0'][:, hs],
                                                                  start=True, stop=False), ['G2', 'TG0'], ['pb3', 'pb3'])
                    ew('pe', lambda e, hs=hs, cols=cols: e.matmul(bank(3), lhsT=G2b[0:32, cols], rhs=T['TG1'][0:32, hs],
                                                                  start=False, stop=True), ['G2', 'TG1'], ['pb3', 'pb3'])
                    ew('act', lambda e, hs=hs: e.copy(out=T['GG'][:, hs], in_=bank(3)), ['pb3', 'pb3'], ['GG'])
                ew('dve', lambda e, hp=hp: e.tensor_scalar(out=T['KK'], in0=T['K'], scalar1=prm[:, 2, hp:hp + 1],
                                                           scalar2=None, op0=ALU.mult), ['K', 'prm'], ['KK'])
                ew('act', lambda e: e.activation(out=T['TQ'], in_=T['KK'], func=AF.Square), ['KK'], ['TQ'])
                for hf in range(2):
                    hs = slice(hf * 512, (hf + 1) * 512)
                    ew('pe', lambda e, hs=hs: e.matmul(bank(4), lhsT=bones, rhs=T['TQ'][:, hs], start=True, stop=True),
                       ['bones', 'TQ'], ['pb4'])
                    ew('dve', lambda e, hs=hs: e.tensor_scalar(out=T['T1'][:, hs], in0=bank(4), scalar1=1e-19,
                                                               scalar2=None, op0=ALU.max), ['pb4'], ['T1'])
                ew('act', lambda e: e.activation(out=T['T1'], in_=T['T1'], func=AF.Ln), ['T1'], ['T1'])
                ew('act', lambda e: e.activation(out=T['T1'], in_=T['T1'], func=AF.Exp, scale=-0.5), ['T1'], ['T1'])
                ew('dve', lambda e: e.tensor_tensor(out=T['KK'], in0=T['KK'], in1=T['T1'], op=ALU.mult),
                   ['KK', 'T1'], ['KK'])
                ew('dve', lambda e, hp=hp: e.tensor_scalar(out=T['T1'], in0=T['AA'], scalar1=-1.0,
                                                           scalar2=prm[:, 3, hp:hp + 1], op0=ALU.add, op1=ALU.mult),
                   ['AA', 'prm'], ['T1'])
                ew('dve', lambda e: e.scalar_tensor_tensor(out=T['KM'], in0=T['T1'], scalar=1.0, in1=T['K'],
                                                           op0=ALU.add, op1=ALU.mult), ['T1', 'K'], ['KM'])
                ew('pool', lambda e: e.tensor_tensor(out=T['T1'], in0=T['R'], in1=T['KM'], op=ALU.mult),
                   ['R', 'KM'], ['T1'])
                ew('pool', lambda e, hp=hp: e.tensor_scalar(out=T['TQ'], in0=T['T1'], scalar1=prm[:, 4, hp:hp + 1],
                                                            scalar2=None, op0=ALU.mult), ['T1', 'prm'], ['TQ'])
                for hf in range(2):
                    hs = slice(hf * 512, (hf + 1) * 512)
                    ew('pe', lambda e, hs=hs: e.matmul(bank(5), lhsT=bones, rhs=T['TQ'][:, hs], start=True, stop=True),
                       ['bones', 'TQ'], ['pb5'])
                    ew('dve', lambda e, hs=hs: e.tensor_tensor(out=T['BV'][:, hs], in0=T['V'][:, hs], in1=bank(5),
                                                               op=ALU.mult), ['pb5', 'V'], ['BV'])
                ew('pool', lambda e: e.tensor_tensor(out=T['BVEC'], in0=T['KK'], in1=T['AA'], op=ALU.mult),
                   ['KK', 'AA'], ['BVEC'])
                ew('dve', lambda e: e.tensor_tensor_scan(out=T['CS'], data0=mrow, data1=T['SG'], initial=0.0,
                                                         op0=ALU.mult, op1=ALU.add), ['mrow', 'SG'], ['CS'])
                ew('pool', lambda e: e.tensor_tensor(out=T['T2'], in0=T['CS'], in1=T['SG'], op=ALU.subtract),
                   ['CS', 'SG'], ['T2'])
                ew('act', lambda e: e.activation(out=T['EP'], in_=T['CS'], func=AF.Exp, scale=-C0), ['CS'], ['EP'])
                ew('act', lambda e: e.activation(out=T['EN'], in_=T['CS'], func=AF.Exp, scale=C0), ['CS'], ['EN'])
                ew('act', lambda e: e.activation(out=T['EPM'], in_=T['T2'], func=AF.Exp, scale=-C0), ['T2'], ['EPM'])
                ew('dve', lambda e: e.tensor_tensor(
                    out=ch3(T['T3']), in0=ch3(T['CS']), in1=ch3(T['CS'])[:, :, 127:128].to_broadcast([128, NCH, 128]),
                    op=ALU.subtract), ['CS'], ['T3'])
                ew('act', lambda e: e.activation(out=T['EC'], in_=T['T3'], func=AF.Exp, scale=C0), ['T3'], ['EC'])
                ew('act', lambda e: e.activation(out=PC.rearrange("p (c o) -> p c o", o=1),
                                                 in_=ch3(T['CS'])[:, :, 127:128], func=AF.Exp, scale=-C0),
                   ['CS'], ['PC'])
                ew('dve', lambda e: e.scalar_tensor_tensor(out=AR[:, :, 0, :], in0=ch3(T['EPM']), scalar=-1.0,
                                                           in1=ch3(T['KK']), op0=ALU.mult, op1=ALU.mult),
                   ['EPM', 'KK'], ['AR0'])
                ew('pool', lambda e: e.tensor_tensor(out=AR[:, :, 1, :], in0=ch3(T['EP']), in1=ch3(T['R']), op=ALU.mult),
                   ['EP', 'R'], ['AR1'])
                ew('dve', lambda e: e.tensor_tensor(out=T['BT'], in0=T['EN'], in1=T['BVEC'], op=ALU.mult),
                   ['EN', 'BVEC'], ['BT'])
                ew('pool', lambda e: e.tensor_tensor(out=T['KT'], in0=T['EN'], in1=T['KM'], op=ALU.mult),
                   ['EN', 'KM'], ['KT'])
                ew('dve', lambda e: e.tensor_tensor(out=T['BH'], in0=T['EC'], in1=T['BVEC'], op=ALU.mult),
                   ['EC', 'BVEC'], ['BH'])
                ew('pool', lambda e: e.tensor_tensor(out=T['KH'], in0=T['EC'], in1=T['KM'], op=ALU.mult),
                   ['EC', 'KM'], ['KH'])
                ew('act', lambda e: e.copy(out=T['VT'], in_=T['V']), ['V'], ['VT'])
                pT = bank(0).bitcast(BF16)
                for c in range(NCH):
                    cs_ = slice(c * 128, (c + 1) * 128)
                    srcs = [(AR[:, c, 0, :], 'AR0'), (T['VT'][:, cs_], 'VT'), (T['BH'][:, cs_], 'BH'),
                            (T['KH'][:, cs_], 'KH')]
                    for q, (sap, sr) in enumerate(srcs):
                        ew('pe', lambda e, q=q, sap=sap: e.transpose(out=pT[:, q * 128:(q + 1) * 128], in_=sap,
                                                                     identity=ident), [sr, 'ident'], ['pb0'])
                    ew('act', lambda e, c=c: e.copy(out=TM4[:, c, :, :], in_=pT[:, 0:512].rearrange("p (q t) -> p q t", q=4)),
                       ['pb0'], ['TM4_%d' % c])

            def chunks(tb, inter):
                P1 = (1, 2)
                P2 = ((4, 5), (6, 7))

                def partA(c, sl):
                    cs_ = slice(c * 128, (c + 1) * 128)
                    tm = 'TM4_%d' % c
                    arc = AR[:, c, :, :].rearrange("p a t -> p (a t)")
                    b1 = P1[sl]
                    ps1 = bank(b1)
                    for h2 in range(2):
                        psl = slice(64 * h2, 64 * h2 + 64)
                        scr = 'SC%d_%d' % (sl, h2)
                        ew('pe', lambda e, ps1=ps1, psl=psl, cs_=cs_, arc=arc: e.matmul(
                            ps1[:, 0:256], lhsT=T['BT'][psl, cs_], rhs=arc[psl, :], start=True, stop=True),
                            ['BT', 'AR0', 'AR1'], ['pb%d' % b1])
                        ew('pe', lambda e, ps1=ps1, psl=psl, cs_=cs_, arc=arc: e.matmul(
                            ps1[:, 256:512], lhsT=T['KT'][psl, cs_], rhs=arc[psl, :], start=True, stop=True),
                            ['KT', 'AR0', 'AR1'], ['pb%d' % b1])
                        ew('dve', lambda e, ps1=ps1, h2=h2, sl=sl: e.tensor_tensor(out=SC[sl][h2], in0=mk4, in1=ps1,
                                                                                   op=ALU.mult),
                           ['pb%d' % b1, 'mk4'], [scr])
                        b2 = P2[sl][h2]
                        ew('pe', lambda e, b2=b2, psl=psl, cs_=cs_, c=c: e.matmul(
                            bank(b2)[:, 384:512], lhsT=AR[psl, c, 0, :], rhs=T['BT'][psl, cs_],
                            start=True, stop=True), ['AR0', 'BT'], ['pb%d' % b2])
                    for h2 in range(2):
                        b2 = P2[sl][h2]
                        ew('dve', lambda e, h2=h2, b2=b2, sl=sl: e.tensor_tensor(
                            out=LZ[sl][0][:, h2, 0:128], in0=mkL[:, h2, :], in1=bank(b2)[:, 384:512], op=ALU.mult),
                            ['pb%d' % b2, 'mkL'], ['LL%d_0_%d' % (sl, h2)])
                    for h2 in range(2):
                        pb = 64 * h2
                        ew('pe', lambda e, h2=h2, pb=pb, c=c, sl=sl: e.matmul(
                            bank(3)[:, 256 + h2 * 64:256 + (h2 + 1) * 64], lhsT=SC[sl][h2][:, 256:384],
                            rhs=TM4[:, c, 1, pb:pb + 64], start=True, stop=True), ['SC%d_%d' % (sl, h2), tm], ['pb3'])
                    ew('pool', lambda e, c=c, sl=sl: e.tensor_copy(
                        out=LZ[sl][0][:, :, 128:192], in_=TM4[:, c, 0, :].rearrange("p (h k) -> p h k", h=2)),
                        [tm], ['ZZ%d_0_0' % sl, 'ZZ%d_0_1' % sl])
                    for h2 in range(2):
                        ew('act', lambda e, h2=h2, sl=sl: e.copy(out=LZ[sl][0][:, h2, 192:256],
                                                                 in_=bank(3)[:, 256 + h2 * 64:256 + (h2 + 1) * 64]),
                           ['pb3'], ['ZZ%d_0_%d' % (sl, h2)])

                def partB(c, sl, n):
                    pp = n % 2
                    for h2 in range(2):
                        b2 = P2[sl][h2]
                        ps2 = bank(b2)
                        pbr = 'pb%d' % b2
                        ltn = SC[sl][h2][:, 0:128] if n == 0 else LZ[sl][pp][:, h2, 256:384]
                        rds = ['LL%d_%d_%d' % (sl, pp, h2), 'ZZ%d_%d_%d' % (sl, pp, h2)] + \
                            (['SC%d_%d' % (sl, h2)] if n == 0 else [])
                        if n < 6:
                            ew('pe', lambda e, ps2=ps2, ltn=ltn, pp=pp, h2=h2, sl=sl: e.matmul(
                                ps2[:, 0:256], lhsT=ltn, rhs=LZ[sl][pp][:, h2, 0:256], start=True, stop=True),
                                rds, [pbr])
                            ew('pe', lambda e, ps2=ps2, ltn=ltn, pp=pp, h2=h2, sl=sl: e.matmul(
                                ps2[:, 256:384], lhsT=LZ[sl][pp][:, h2, 0:128], rhs=ltn, start=True, stop=True),
                                rds, [pbr])
                        else:
                            ew('pe', lambda e, ps2=ps2, ltn=ltn, pp=pp, h2=h2, sl=sl: e.matmul(
                                ps2[:, 128:256], lhsT=ltn, rhs=LZ[sl][pp][:, h2, 128:256], start=True, stop=True),
                                rds, [pbr])

                    def cp(h2):
                        b2 = P2[sl][h2]
                        ew('act', lambda e, h2=h2, b2=b2: e.copy(
                            out=LZ[sl][1 - pp][:, h2, :].rearrange("p (s t) -> p s t", s=3)[:, 0:3:2, :],
                            in_=bank(b2)[:, 0:384].rearrange("p (s t) -> p s t", s=3)[:, 0:3:2, :]),
                            ['pb%d' % b2], ['LL%d_%d_%d' % (sl, 1 - pp, h2)])

                    def ad(h2):
                        b2 = P2[sl][h2]
                        ew('dve', lambda e, h2=h2, b2=b2: e.tensor_tensor(
                            out=LZ[sl][1 - pp][:, h2, 128:256], in0=LZ[sl][pp][:, h2, 128:256],
                            in1=bank(b2)[:, 128:256], op=ALU.add),
                            ['pb%d' % b2, 'ZZ%d_%d_%d' % (sl, pp, h2)], ['ZZ%d_%d_%d' % (sl, 1 - pp, h2)])
                    if n < 6:
                        cp(0)
                        ad(1)
                        cp(1)
                        ad(0)
                    else:
                        ad(0)
                        ad(1)

                def partC(c, sl):
                    tm = 'TM4_%d' % c
                    ZF = LZ[sl][1]
                    p3 = bank(0)[:, 192:384]
                    for h2 in range(2):
                        pb = 64 * h2
                        psl = slice(pb, pb + 64)
                        zr = 'ZZ%d_1_%d' % (sl, h2)
                        ew('pe', lambda e, h2=h2, pb=pb, psl=psl, c=c: e.matmul(
                            p3[psl, 0:64], lhsT=ZF[:, h2, 128:192], rhs=TM4[:, c, 2, pb:pb + 64],
                            start=True, stop=True, tile_position=(0, pb)), [zr, tm], ['pb0'])
                        ew('pe', lambda e, h2=h2, pb=pb, psl=psl: e.matmul(
                            p3[psl, 64:192], lhsT=ZF[:, h2, 128:192], rhs=SC[sl][h2][:, 128:256],
                            start=True, stop=True, tile_position=(0, pb)), [zr, 'SC%d_%d' % (sl, h2)], ['pb0'])
                    for h2 in range(2):
                        psl = slice(64 * h2, 64 * h2 + 64)
                        ew('dve', lambda e, c=c, psl=psl, h2=h2: e.scalar_tensor_tensor(
                            out=MCz[sl][psl, h2, :], in0=E2[psl, :], scalar=PC[psl, c:c + 1], in1=p3[psl, 0:64],
                            op0=ALU.mult, op1=ALU.add), ['E2', 'PC', 'pb0'], ['MC%d' % sl])
                    ew('dve', lambda e, c=c: e.tensor_tensor(out=QT[sl], in0=AR[:, c, 1, :], in1=p3[:, 64:192],
                                                             op=ALU.add), ['pb0', 'AR1'], ['QT%d' % sl])

                def partD(c, sl):
                    cs_ = slice(c * 128, (c + 1) * 128)
                    tm = 'TM4_%d' % c
                    ZF = LZ[sl][1]
                    for h2 in range(2):
                        pb = 64 * h2
                        psl = slice(pb, pb + 64)
                        sb_ = 3 if h2 == 0 else 0
                        sr_ = 'pb%d' % sb_
                        psY = bank(sb_)[:, 0:128]
                        psS = bank(sb_)[:, 128:192]
                        UU = ZF[:, h2, 192:256]
                        zr = 'ZZ%d_1_%d' % (sl, h2)
                        scr = 'SC%d_%d' % (sl, h2)
                        ew('pe', lambda e, psl=psl, pb=pb, h2=h2, UU=UU, psY=psY: e.matmul(
                            psY[psl, :], lhsT=UU, rhs=SC[sl][h2][:, 128:256], start=True, stop=False,
                            tile_position=(0, pb)), [zr, scr], [sr_])
                        ew('pe', lambda e, psl=psl, pb=pb, h2=h2, c=c, psY=psY: e.matmul(
                            psY[psl, :], lhsT=TM4[:, c, 1, pb:pb + 64], rhs=SC[sl][h2][:, 384:512], start=False,
                            stop=False, tile_position=(0, pb)), [tm, scr], [sr_])
                        ew('pe', lambda e, psl=psl, pb=pb, psY=psY, h2=h2: e.matmul(
                            psY[psl, :], lhsT=STz[:, h2, :], rhs=QT[sl], start=False, stop=True,
                            tile_position=(0, pb)), ['ST', 'QT%d' % sl], [sr_])
                        ew('pe', lambda e, psl=psl, pb=pb, psS=psS, h2=h2: e.matmul(
                            psS[psl, :], lhsT=MCz[sl][:, h2, :], rhs=STz[:, h2, :], start=True, stop=False,
                            tile_position=(0, pb)), ['MC%d' % sl, 'ST'], [sr_])
                        ew('pe', lambda e, psl=psl, pb=pb, c=c, UU=UU, psS=psS: e.matmul(
                            psS[psl, :], lhsT=TM4[:, c, 2, pb:pb + 64], rhs=UU, start=False, stop=False,
                            tile_position=(0, pb)), [tm, zr], [sr_])
                        ew('pe', lambda e, psl=psl, pb=pb, c=c, psS=psS: e.matmul(
                            psS[psl, :], lhsT=TM4[:, c, 3, pb:pb + 64], rhs=TM4[:, c, 1, pb:pb + 64], start=False,
                            stop=True, tile_position=(0, pb)), [tm], [sr_])
                    for h2 in range(2):
                        pb = 64 * h2
                        psl = slice(pb, pb + 64)
                        sb_ = 3 if h2 == 0 else 0
                        sr_ = 'pb%d' % sb_
                        ew('act', lambda e, cs_=cs_, psl=psl, sb_=sb_: e.copy(out=T['Y32'][psl, cs_],
                                                                             in_=bank(sb_)[psl, 0:128]),
                           [sr_], ['Y32'])
                        ew('dve', lambda e, psl=psl, sb_=sb_, h2=h2: e.tensor_copy(out=STz[psl, h2, :],
                                                                                   in_=bank(sb_)[psl, 128:192]),
                           [sr_], ['ST'])

                for c0 in range(0, NCH, 2):
                    for sl in range(2):
                        partA(c0 + sl, sl)
                    for n in range(7):
                        for sl in range(2):
                            partB(c0 + sl, sl, n)
                        if n in (1, 3, 5):
                            inter()
                    for sl in range(2):
                        partC(c0 + sl, sl)
                    for sl in range(2):
                        partD(c0 + sl, sl)
                    inter()

            def outst(tb, hp=hp):
                t0 = tb * TB
                tsl = slice(t0, t0 + TB)
                ew('pool', lambda e: e.tensor_copy(out=T['YB'], in_=T['Y32']), ['Y32'], ['YB'])
                for hf in range(2):
                    hs = slice(hf * 512, (hf + 1) * 512)
                    ew('pe', lambda e, hs=hs: e.matmul(bank(1), lhsT=bones, rhs=T['YB'][:, hs], start=True, stop=True),
                       ['bones', 'YB'], ['pb1'])
                    ew('dve', lambda e, hs=hs: e.scalar_tensor_tensor(out=T['DD'][:, hs], in0=bank(1), scalar=-1.0 / 64,
                                                                      in1=T['Y32'][:, hs], op0=ALU.mult, op1=ALU.add),
                       ['pb1', 'Y32'], ['DD'])
                ew('act', lambda e: e.activation(out=T['TQ'], in_=T['DD'], func=AF.Square), ['DD'], ['TQ'])
                for hf in range(2):
                    hs = slice(hf * 512, (hf + 1) * 512)
                    ew('pe', lambda e, hs=hs: e.matmul(bank(2), lhsT=bones, rhs=T['TQ'][:, hs], start=True, stop=True),
                       ['bones', 'TQ'], ['pb2'])
                    ew('act', lambda e, hs=hs: e.activation(out=T['T1'][:, hs], in_=bank(2), func=AF.Ln, scale=1.0 / 64,
                                                            bias=GN_EPS), ['pb2'], ['T1'])
                ew('act', lambda e: e.activation(out=T['T1'], in_=T['T1'], func=AF.Exp, scale=-0.5), ['T1'], ['T1'])
                ew('dve', lambda e: e.tensor_tensor(out=T['DD'], in0=T['DD'], in1=T['T1'], op=ALU.mult),
                   ['DD', 'T1'], ['DD'])
                ew('dve', lambda e, hp=hp: e.tensor_scalar(out=T['DD'], in0=T['DD'], scalar1=prm[:, 5, hp:hp + 1],
                                                           scalar2=prm[:, 6, hp:hp + 1], op0=ALU.mult, op1=ALU.add),
                   ['DD', 'prm'], ['DD'])
                ew('pool', lambda e: e.tensor_tensor(out=T['DD'], in0=T['DD'], in1=T['BV'], op=ALU.add),
                   ['DD', 'BV'], ['DD'])
                ew('dve', lambda e: e.tensor_tensor(out=T['YO'], in0=T['DD'], in1=T['GG'], op=ALU.mult),
                   ['DD', 'GG'], ['YO'])
                pdma('sp', lambda e, hp=hp, tsl=tsl: e.dma_start(out=ya_d[hp, :, tsl], in_=T['YO']),
                     reads=[R_('YO')], writes=['d_ya'], dma_key=R_('stYO'))


            par[0] = 0
            prep(0)
            for tb in range(NB):
                pend = []
                if tb + 1 < NB:
                    par[0] = (tb + 1) % 2
                    coll[0] = pend
                    prep(tb + 1)
                    coll[0] = None
                par[0] = tb % 2
                nsl = 4 * (NCH // 2)
                step = (len(pend) + nsl - 1) // nsl if pend else 0
                pos = [0]

                def inter(pend=pend, step=step, pos=pos):
                    open_banks = set()
                    cnt = 0
                    while pos[0] < len(pend) and (cnt < step or open_banks):
                        a_ = pend[pos[0]]
                        pos[0] += 1
                        cnt += 1
                        if a_[0] == 'pe':
                            open_banks.update(w_ for w_ in a_[3] if w_.startswith('pb'))
                        else:
                            open_banks.difference_update(r_ for r_ in a_[2] if r_.startswith('pb'))
                        P.op(a_[0], a_[1], reads=a_[2], writes=a_[3], dma_key=a_[4])
                chunks(tb, inter)
                for a_ in pend[pos[0]:]:
                    P.op(a_[0], a_[1], reads=a_[2], writes=a_[3], dma_key=a_[4])
                par[0] = tb % 2
                outst(tb)
    yb_d = nc.dram_tensor("yb_scr", [6, 128, S], BF16, kind="ExternalOutput" if debug else "Internal").ap()
    DIL = (1, 4, 16)

    def attn_stage_full(js=range(4)):
        A.off = const_mark
        pre = 'a_'

        def R_(n):
            return pre + n

        def nm(x):
            if x.startswith('pb') or x in ('ident', 'ident_f'):
                return x
            return R_(x)

        def ew(eng, fn, reads, writes):
            P.op(eng, fn, reads=[nm(x) for x in reads], writes=[nm(x) for x in writes])
        QH = A.alloc([S], BF16)
        KH = A.alloc([S], BF16)
        VX = A.alloc([32, 64], BF16)
        ONES = A.alloc([64], BF16)
        OT = [A.alloc([S], F32) for _ in range(3)]
        DEN = [A.alloc([S], F32) for _ in range(3)]
        PT = [A.alloc([256], BF16) for _ in range(4)]
        mask2 = A.alloc([256], BF16)
        RD = [A.alloc([512], F32) for _ in range(2)]
        YBS = [A.alloc([512], BF16) for _ in range(2)]
        print("attn_stage arena", A.off)
        P.op('pool', lambda e: e.memset(mask2, 1.0), writes=[R_('mask')])
        P.op('pool', lambda e: e.affine_select(out=mask2[:, 0:128], in_=mask2[:, 0:128], pattern=[[1, 128]],
                                               compare_op=ALU.is_ge, fill=0.0, base=0, channel_multiplier=-1),
             reads=[R_('mask')], writes=[R_('mask')])
        P.op('pool', lambda e: e.affine_select(out=mask2[:, 128:256], in_=mask2[:, 128:256], pattern=[[-1, 128]],
                                               compare_op=ALU.is_ge, fill=0.0, base=0, channel_multiplier=1),
             reads=[R_('mask')], writes=[R_('mask')])
        P.op('pool', lambda e: e.memset(ONES, 1.0), writes=[R_('ONES')])

        tcount = [0]
        ccount = [0]
        for j in js:
            for g in range(3):
                d = DIL[g]
                nb = S // d // 128
                h = 4 * g + j
                pair = h // 2
                pb = 64 * (h % 2)
                vv = v_d.rearrange("(m d) c -> d m c", d=d)
                for r in range(d):
                    for n0 in range(0, nb, 8):
                        n1 = min(nb, n0 + 8)
                        P.op('sp', lambda e, r=r, h=h, nb=nb, vv=vv, n0=n0, n1=n1: e.dma_start(
                            out=VX[:, r * nb + n0:r * nb + n1, :],
                            in_=vv[r, n0 * 128:n1 * 128, h * 64:(h + 1) * 64].rearrange("(n i) c -> i n c", i=128)),
                            writes=[R_('VX')], dma_key=R_('ldV'))
                P.op('sp', lambda e, pair=pair, pb=pb: e.dma_start(out=QH[0:64, :], in_=qk_d[pair, pb:pb + 64, :]),
                     writes=[R_('QH')], dma_key=R_('ldQ'))
                P.op('sp', lambda e, pair=pair, pb=pb: e.dma_start(out=KH[0:64, :], in_=qk_d[6 + pair, pb:pb + 64, :]),
                     writes=[R_('KH')], dma_key=R_('ldK'))
                qv = QH.rearrange("p (m d) -> p d m", d=d)
                kv = KH.rearrange("p (m d) -> p d m", d=d)
                otr = 'OT%d' % g
                otv = OT[g].rearrange("p (m d) -> p d m", d=d)
                dnv = DEN[g].rearrange("p (m d) -> p d m", d=d)
                tiles = []
                for r in range(d):
                    for n in range(nb):
                        tiles.append((r, n, tcount[0]))
                        tcount[0] += 1

                def emit_score(r, n, ti, kv=kv, qv=qv, nb=nb):
                    nq = 256 if n + 1 < nb else 128
                    sbk = 1 + ti % 2
                    ps = bank(sbk)[:, 0:nq]
                    pt = PT[ti % 4]
                    ptr = 'PT%d' % (ti % 4)
                    ew('pe', lambda e, ps=ps, r=r, n=n, nq=nq: e.matmul(
                        ps, lhsT=kv[0:64, r, 128 * n:128 * n + 128], rhs=qv[0:64, r, 128 * n:128 * n + nq],
                        start=True, stop=True), ['KH', 'QH'], ['pb%d' % sbk])
                    ew('act', lambda e, ps=ps, pt=pt, nq=nq: e.activation(out=pt[:, 0:nq], in_=ps, func=AF.Exp),
                       ['pb%d' % sbk], [ptr])
                    ew('pool', lambda e, pt=pt, nq=nq: e.tensor_tensor(out=pt[:, 0:nq], in0=pt[:, 0:nq],
                                                                      in1=mask2[:, 0:nq], op=ALU.mult),
                       [ptr, 'mask'], [ptr])

                def emit_pv(r, n, ti, otv=otv, dnv=dnv, nb=nb, g=g, otr=otr):
                    pt = PT[ti % 4]
                    ptr = 'PT%d' % (ti % 4)
                    obk = 3 + ti % 2
                    po = bank(obk)[0:64, 0:128]
                    pdn = bank(obk)[0:64, 128:256]
                    vt = r * nb + n
                    has_prev = n > 0
                    if has_prev:
                        ppt = PT[(ti - 1) % 4]
                        pptr = 'PT%d' % ((ti - 1) % 4)
                        ew('pe', lambda e, ppt=ppt, vt=vt: e.matmul(
                            po, lhsT=VX[:, vt - 1, :], rhs=ppt[:, 128:256], start=True, stop=False),
                            ['VX', pptr], ['pb%d' % obk])
                    ew('pe', lambda e, pt=pt, vt=vt: e.matmul(
                        po, lhsT=VX[:, vt, :], rhs=pt[:, 0:128], start=(not has_prev), stop=True),
                        ['VX', ptr], ['pb%d' % obk])
                    if has_prev:
                        ew('pe', lambda e, ppt=ppt: e.matmul(
                            pdn, lhsT=ONES, rhs=ppt[:, 128:256], start=True, stop=False),
                            ['ONES', pptr], ['pb%d' % obk])
                    ew('pe', lambda e, pt=pt: e.matmul(
                        pdn, lhsT=ONES, rhs=pt[:, 0:128], start=(not has_prev), stop=True),
                        ['ONES', ptr], ['pb%d' % obk])
                    ew('dve', lambda e, r=r, n=n: e.tensor_copy(
                        out=otv[0:64, r, 128 * n:128 * n + 128], in_=po), ['pb%d' % obk], [otr])
                    ew('dve', lambda e, r=r, n=n: e.tensor_copy(
                        out=dnv[0:64, r, 128 * n:128 * n + 128], in_=pdn), ['pb%d' % obk], ['DEN%d' % g])

                for idx, (r, n, ti) in enumerate(tiles):
                    emit_score(r, n, ti)
                    if idx >= 1:
                        emit_pv(*tiles[idx - 1])
                emit_pv(*tiles[-1])
            for ck in range(S // 512):
                csl = slice(ck * 512, (ck + 1) * 512)
                cc = ccount[0]
                ccount[0] += 1
                rd = RD[cc % 2]
                ew('pool', lambda e, rd=rd, csl=csl: e.tensor_tensor(out=rd[0:64, :], in0=DEN[0][0:64, csl],
                                                                     in1=DEN[1][0:64, csl], op=ALU.add),
                   ['DEN0', 'DEN1'], ['RD%d' % (cc % 2)])
                ew('pool', lambda e, rd=rd, csl=csl: e.tensor_tensor(out=rd[0:64, :], in0=rd[0:64, :],
                                                                     in1=DEN[2][0:64, csl], op=ALU.add),
                   ['RD%d' % (cc % 2), 'DEN2'], ['RD%d' % (cc % 2)])
                ew('dve', lambda e, rd=rd: e.reciprocal(out=rd[0:64, :], in_=rd[0:64, :]), ['RD%d' % (cc % 2)],
                   ['RD%d' % (cc % 2)])
                for g in range(3):
                    h = 4 * g + j
                    pair = h // 2
                    pb = 64 * (h % 2)
                    yi = (cc * 3 + g) % 2
                    ys = YBS[yi]
                    ew('pool', lambda e, g=g, csl=csl, rd=rd, ys=ys: e.tensor_tensor(
                        out=ys[0:64, :], in0=OT[g][0:64, csl], in1=rd[0:64, :], op=ALU.mult),
                        ['OT%d' % g, 'RD%d' % (cc % 2)], ['YBS%d' % yi])
                    P.op('sp', lambda e, pair=pair, pb=pb, csl=csl, ys=ys: e.dma_start(
                        out=yb_d[pair, pb:pb + 64, csl], in_=ys[0:64, :]),
                        reads=[R_('YBS%d' % yi)], writes=['d_yb'], dma_key=R_('stYB%d' % yi))

    wpr_d = din("w_proj_rwkv", [1024, 1024])
    wpa_d = din("w_proj_attn", [768, 1024])
    wo_d = din("w_out", [1024, 1024])
    x2_d = nc.dram_tensor("x2_scr", [S, D], F32, kind="ExternalOutput" if debug else "Internal").ap()

    def merge_stage():
        A.off = const_mark
        pre = 'm_'

        def R_(n):
            return pre + n

        def nm(x):
            if x.startswith('pb'):
                return x
            return R_(x)

        def ew(eng, fn, reads, writes):
            P.op(eng, fn, reads=[nm(x) for x in reads], writes=[nm(x) for x in writes])
        Wr = A.alloc([8, 1024], BF16)
        Wa = A.alloc([6, 1024], BF16)
        Wo = A.alloc([8, 1024], BF16)
        X = [A.alloc([NSUB, D], F32) for _ in range(2)]
        YA = [A.alloc([8, TT], BF16) for _ in range(2)]
        YB = [A.alloc([6, TT], BF16) for _ in range(2)]
        G = [A.alloc([16, TT], BF16) for _ in range(2)]
        MT = A.alloc([8, TT], BF16)
        t1 = [A.alloc([TT], F32) for _ in range(2)]
        t2 = [A.alloc([TT], F32) for _ in range(2)]
        print("merge_stage arena", A.off)
        for kc in range(8):
            P.op('pool', lambda e, kc=kc: e.dma_start(out=Wr[:, kc, :], in_=wpr_d[kc * 128:(kc + 1) * 128, :]),
                 writes=[R_('Wr')], dma_key=R_('w'))
            P.op('pool', lambda e, kc=kc: e.dma_start(out=Wo[:, kc, :], in_=wo_d[kc * 128:(kc + 1) * 128, :]),
                 writes=[R_('Wo')], dma_key=R_('w'))
        for kc in range(6):
            P.op('pool', lambda e, kc=kc: e.dma_start(out=Wa[:, kc, :], in_=wpa_d[kc * 128:(kc + 1) * 128, :]),
                 writes=[R_('Wa')], dma_key=R_('w'))
        srcv = x1_d.rearrange("(n s p) d -> n p s d", p=128, s=NSUB)
        dstv = x2_d.rearrange("(n s p) d -> n p s d", p=128, s=NSUB)
        for it in range(NT):
            sl = it % 2
            tsl = slice(it * TT, (it + 1) * TT)
            sfx = '%d' % sl
            P.op('sp', lambda e, it=it, sl=sl: e.dma_start(out=X[sl], in_=srcv[it]), writes=[R_('X' + sfx)],
                 dma_key=R_('ldX' + sfx))
            P.op('sp', lambda e, sl=sl, tsl=tsl: e.dma_start(out=YA[sl], in_=ya_d.rearrange("b p t -> p b t")[:, :, tsl]),
                 writes=[R_('YA' + sfx)], dma_key=R_('ldYA' + sfx))
            P.op('sp', lambda e, sl=sl, tsl=tsl: e.dma_start(out=YB[sl], in_=yb_d.rearrange("b p t -> p b t")[:, :, tsl]),
                 writes=[R_('YB' + sfx)], dma_key=R_('ldYB' + sfx))
            P.op('sp', lambda e, sl=sl, tsl=tsl: e.dma_start(out=G[sl], in_=gate_d.rearrange("b p t -> p b t")[:, :, tsl]),
                 writes=[R_('G' + sfx)], dma_key=R_('ldG' + sfx))
            for c in range(8):
                bk = 1 + c % 2
                pg = bank(bk)
                for kc in range(8):
                    ew('pe', lambda e, c=c, kc=kc, pg=pg, sl=sl: e.matmul(
                        pg[:, 0:TT], lhsT=Wr[:, kc, c * 128:(c + 1) * 128], rhs=YA[sl][:, kc, :],
                        start=(kc == 0), stop=(kc == 7)), ['Wr', 'YA' + sfx], ['pb%d' % bk])
                for kc in range(6):
                    ew('pe', lambda e, c=c, kc=kc, pg=pg, sl=sl: e.matmul(
                        pg[:, TT:2 * TT], lhsT=Wa[:, kc, c * 128:(c + 1) * 128], rhs=YB[sl][:, kc, :],
                        start=(kc == 0), stop=(kc == 5)), ['Wa', 'YB' + sfx], ['pb%d' % bk])
                q = c % 2
                ew('dve', lambda e, c=c, pg=pg, sl=sl, q=q: e.tensor_tensor(out=t1[q], in0=G[sl][:, c, :],
                                                                           in1=pg[:, 0:TT], op=ALU.mult),
                   ['G' + sfx, 'pb%d' % bk], ['t1%d' % q])
                ew('dve', lambda e, c=c, pg=pg, sl=sl, q=q: e.tensor_tensor(out=t2[q], in0=G[sl][:, 8 + c, :],
                                                                           in1=pg[:, TT:2 * TT], op=ALU.mult),
                   ['G' + sfx, 'pb%d' % bk], ['t2%d' % q])
                ew('pool', lambda e, c=c, q=q: e.tensor_tensor(out=MT[:, c, :], in0=t1[q], in1=t2[q], op=ALU.add),
                   ['t1%d' % q, 't2%d' % q], ['MT'])
            for s in range(NSUB):
                for dh in range(2):
                    bk = 3 + (s * 2 + dh) % 2
                    pd = bank(bk)
                    for c in range(8):
                        ew('pe', lambda e, c=c, s=s, dh=dh, pd=pd: e.matmul(
                            pd, lhsT=MT[:, c, s * 128:(s + 1) * 128], rhs=Wo[:, c, dh * 512:(dh + 1) * 512],
                            start=(c == 0), stop=(c == 7)), ['MT', 'Wo'], ['pb%d' % bk])
                    ew('dve', lambda e, s=s, dh=dh, pd=pd, sl=sl: e.tensor_tensor(
                        out=X[sl][:, s, dh * 512:(dh + 1) * 512], in0=X[sl][:, s, dh * 512:(dh + 1) * 512], in1=pd,
                        op=ALU.add), ['pb%d' % bk, 'X' + sfx], ['X' + sfx])
            P.op('sp', lambda e, it=it, sl=sl: e.dma_start(out=dstv[it], in_=X[sl]),
                 reads=[R_('X' + sfx)], writes=['d_x2'], dma_key=R_('stX' + sfx))

    if 'ffn1' in stages:
        ffn_stage(0, x, x1_d)
        P.sync_all()
    if 'proj' in stages:
        proj_stage()
        P.sync_all()
    if 'rwkv' in stages:
        rwkv_stage(range(NPAIRS_DBG))
        P.sync_all()
    if 'attn' in stages:
        attn_stage_full(JS_DBG)
        P.sync_all()
    if 'merge' in stages:
        merge_stage()
        P.sync_all()
    if 'ffn2' in stages:
        ffn_stage(1, x2_d if 'merge' in stages else x1_d, out)
    P.sync_all()
    P.op('sp', None)
    P.finalize_and_emit(stack)
    stack.close()
    return nc


_CACHE = {}


SHARED_KEYS = ['ffn1_norm', 'ffn1_w_in', 'ffn1_w_out', 'ffn2_norm', 'ffn2_w_in', 'ffn2_w_out',
               'w_in', 'mix_norm', 'rwkv_mu', 'b_gate', 'attn_q_norm', 'attn_k_norm',
               'w_proj_rwkv', 'w_proj_attn', 'w_out', 'rwkv_w2', 'rwkv_a2', 'rwkv_g2', 'rwkv_w0', 'rwkv_a0', 'rwkv_k_k', 'rwkv_k_a', 'rwkv_r_k', 'rwkv_ln_w', 'rwkv_ln_b']


def make_shared(inputs):
    shared = {}
    for k in SHARED_KEYS:
        v = np.asarray(inputs[k], dtype=np.float32)
        v = v.reshape(v.shape[1:])
        if k == 'rwkv_r_k':
            v = v.reshape(-1)
        shared[k] = np.ascontiguousarray(v)
    return shared


def kernel(**inputs):
    if 'nc' not in _CACHE:
        _CACHE['nc'] = build_program()
    nc = _CACHE['nc']
    x = np.ascontiguousarray(inputs['x'], dtype=np.float32)
    shared = make_shared(inputs)
    in_maps = []
    for c in range(NCORES):
        m = dict(shared)
        m['x'] = x[c]
        in_maps.append(m)
    res = run_bass_kernel_spmd(nc, in_maps, core_ids=list(range(NCORES)))
    return np.stack([np.asarray(r['out']) for r in res.results], axis=0)
```

```python
import numpy as np
from contextlib import ExitStack
import concourse.bass as bass
import concourse.mybir as mybir
from concourse.bass_utils import run_bass_kernel_spmd
from concourse.alu_op_type import AluOpType as ALU

F32 = mybir.dt.float32
BF16 = mybir.dt.bfloat16
AF = mybir.ActivationFunctionType
AX = mybir.AxisListType

S = 4096
D = 1024
DFF = 2816
NCORES = 8
RMS_EPS = 1e-6

ENGS = ['pe', 'act', 'dve', 'pool', 'sp']
MAXOPS = [0]
SEM_LIM = 30000
DMA_LIM = 1800


class Prog:
    def __init__(self, nc):
        self.nc = nc
        self.ops = []
        self.eng_ops = {e: [] for e in ENGS}
        self.last_w = {}
        self.readers = {}
        self.dma_cnt = {}
        self.barrier = {e: None for e in ENGS}

    def op(self, eng, fn, reads=(), writes=(), dma_key=None):
        mo = MAXOPS[0]
        if mo and len(self.ops) >= mo and fn is not None:
            return None
        if mo and len(self.ops) == mo - 1 and fn is not None:
            print("LAST OP:", eng, fn.__code__.co_firstlineno, reads, writes)
        oid = len(self.ops)
        deps = set()
        dma_deps = {}
        writes = list(writes) + [r for r in reads if (r.startswith('pb') or r.startswith('ps')) and r not in writes]

        def add(o):
            od = self.ops[o]
            if od['dma_key'] is not None:
                k = od['dma_key']
                dma_deps[k] = self.dma_cnt[k]
            else:
                deps.add(o)
        for r in reads:
            if r in self.last_w:
                add(self.last_w[r])
        for w in writes:
            if w in self.last_w:
                add(self.last_w[w])
            for rd in self.readers.get(w, {}).values():
                add(rd)
        if self.barrier[eng] is not None:
            bd, bdma = self.barrier[eng]
            for o in bd:
                deps.add(o)
            for k, v in bdma.items():
                dma_deps[k] = max(dma_deps.get(k, 0), v)
            self.barrier[eng] = None
        cnt = None
        if dma_key is not None:
            self.dma_cnt[dma_key] = self.dma_cnt.get(dma_key, 0) + 1
            cnt = self.dma_cnt[dma_key]
        o = dict(id=oid, eng=eng, fn=fn, deps=deps, dma_deps=dma_deps, dma_key=dma_key,
                 dma_cnt=cnt, idx=len(self.eng_ops[eng]), sig=False)
        self.ops.append(o)
        self.eng_ops[eng].append(o)
        ch = eng if dma_key is None else 'dma:' + dma_key
        for r in reads:
            self.readers.setdefault(r, {})[ch] = oid
        for w in writes:
            self.last_w[w] = oid
            self.readers[w] = {}
        return oid

    def sync_all(self):
        bd = set()
        for e in ENGS:
            for o in reversed(self.eng_ops[e]):
                if o['dma_key'] is None and o['fn'] is not None:
                    bd.add(o['id'])
                    break
        bdma = dict(self.dma_cnt)
        for e in ENGS:
            self.barrier[e] = (set(bd), dict(bdma))

    def finalize_and_emit(self, stack):
        nc = self.nc
        for o in self.ops:
            per = {}
            for d in o['deps']:
                od = self.ops[d]
                if od['eng'] == 'pe' and o['eng'] == 'pe':
                    continue
                e = od['eng']
                if e not in per or self.ops[per[e]]['idx'] < od['idx']:
                    per[e] = d
            o['cdeps'] = per
            for d in per.values():
                self.ops[d]['sig'] = True
        sems = {}

        def get_sem(name):
            return sems[name]
        for e in ENGS:
            c = 0
            for o in self.eng_ops[e]:
                if o['dma_key'] is None and o['sig']:
                    c += 1
                    o['sigval'] = c
        for o in self.ops:
            waits = {}
            for e, d in o['cdeps'].items():
                v = self.ops[d]['sigval']
                key = ('c_%s_%d' % (e, (v - 1) // SEM_LIM))
                val = (v - 1) % SEM_LIM + 1
                waits[key] = max(waits.get(key, 0), val)
            for k, n in o['dma_deps'].items():
                key = ('d_%s_%d' % (k, (n - 1) // DMA_LIM))
                val = 16 * ((n - 1) % DMA_LIM + 1)
                waits[key] = max(waits.get(key, 0), val)
            o['waits'] = waits
        names = set()
        for o in self.ops:
            names.update(o['waits'].keys())
            if o['dma_key'] is not None:
                names.add('d_%s_%d' % (o['dma_key'], (o['dma_cnt'] - 1) // DMA_LIM))
            elif o['sig']:
                names.add('c_%s_%d' % (o['eng'], (o['sigval'] - 1) // SEM_LIM))
        for nm in sorted(names):
            sems[nm] = stack.enter_context(nc.semaphore(nm))
        print("n_sems", len(names), "n_ops", len(self.ops), {e: len(v) for e, v in self.eng_ops.items()})
        block = stack.enter_context(nc.Block())
        decos = {'pe': block.tensor, 'act': block.scalar, 'dve': block.vector,
                 'pool': block.gpsimd, 'sp': block.sync}
        for e in ENGS:
            ops = self.eng_ops[e]

            def body(eng, ops=ops, e=e):
                waited = {}
                for o in ops:
                    for key, val in o['waits'].items():
                        if waited.get(key, 0) >= val:
                            continue
                        waited[key] = val
                        eng.wait_ge(get_sem(key), val)
                    if o['fn'] is None:
                        continue
                    ins = o['fn'](eng)
                    if o['dma_key'] is not None:
                        n = o['dma_cnt']
                        ins.then_inc(get_sem('d_%s_%d' % (o['dma_key'], (n - 1) // DMA_LIM)), 16)
                    elif o['sig']:
                        v = o['sigval']
                        ins.then_inc(get_sem('c_%s_%d' % (e, (v - 1) // SEM_LIM)), 1)
            decos[e](body)


class Arena:
    def __init__(self, tensor, nbytes):
        self.t = tensor
        self.nbytes = nbytes
        self.off = 0

    def alloc(self, shape, dtype, parts=128):
        n = int(np.prod(shape))
        esz = 4 if dtype == F32 else 2
        nb = n * esz
        nb_al = (nb + 63) // 64 * 64
        assert self.off + nb_al <= self.nbytes, ("SBUF arena overflow", self.off, nb_al)
        ap = self.t[0:parts, self.off // 2:(self.off + nb) // 2]
        self.off += nb_al
        if dtype == F32:
            ap = ap.bitcast(F32)
        if len(shape) == 2:
            ap = ap.rearrange("p (a b) -> p a b", a=shape[0], b=shape[1])
        elif len(shape) == 3:
            ap = ap.rearrange("p (a b c) -> p a b c", a=shape[0], b=shape[1], c=shape[2])
        return ap


def build_program(debug=False, NPAIRS_DBG=8, stages=('ffn1', 'proj', 'rwkv', 'attn', 'merge', 'ffn2'), NB_DBG=None,
                  JS_DBG=range(4), NT_DBG=None):
    nc = bass.Bass("TRN2", target_bir_lowering=False)
    P = Prog(nc)

    def din(name, shape):
        return nc.dram_tensor(name, list(shape), F32, kind="ExternalInput").ap()
    x = din("x", [S, D])
    ffn_norm = [din("ffn1_norm", [D]), din("ffn2_norm", [D])]
    ffn_win = [din("ffn1_w_in", [D, 2 * DFF]), din("ffn2_w_in", [D, 2 * DFF])]
    ffn_wout = [din("ffn1_w_out", [DFF, D]), din("ffn2_w_out", [DFF, D])]
    out = nc.dram_tensor("out", [S, D], F32, kind="ExternalOutput").ap()
    x1_d = nc.dram_tensor("x1_scr", [S, D], F32, kind="ExternalOutput" if debug else "Internal").ap()

    stack = ExitStack()
    ARENA_BYTES = 206 * 1024
    arena_t = stack.enter_context(nc.sbuf_tensor("arena", [128, ARENA_BYTES // 2], BF16))
    A = Arena(arena_t, ARENA_BYTES)
    psum = stack.enter_context(nc.psum_tensor("psum", [128, 4096], F32))

    def bank(b, n=512, off=0):
        return psum[:, b * 512 + off:b * 512 + off + n]

    ident_f = A.alloc([128], F32)
    ident = A.alloc([128], BF16)
    ones_col = A.alloc([1], F32)

    P.op('pool', lambda e: e.memset(ident_f, 0.0), writes=['ident_f'])
    P.op('pool', lambda e: e.affine_select(out=ident_f, in_=ident_f, pattern=[[-1, 128]],
                                           compare_op=ALU.not_equal, fill=1.0, base=0, channel_multiplier=1),
         reads=['ident_f'], writes=['ident_f'])
    P.op('dve', lambda e: e.tensor_copy(out=ident, in_=ident_f), reads=['ident_f'], writes=['ident'])

    const_mark = A.off

    TT = 256
    NSUB = TT // 128
    NT = S // TT if NT_DBG is None else NT_DBG
    KC = D // 128
    FC = DFF // 128

    def ffn_stage(si, src, dst):
        A.off = const_mark
        TT = 512
        NSUB = TT // 128
        NT = (S // TT) if NT_DBG is None else NT_DBG
        W1 = A.alloc([KC, 2 * DFF], BF16)
        W2 = A.alloc([FC, D], BF16)
        gb = A.alloc([D], F32)
        xt = [A.alloc([NSUB, D], F32)] * 2
        hb = [A.alloc([D], BF16) for _ in range(2)]
        hT = [A.alloc([KC, TT], BF16) for _ in range(2)]
        actT = A.alloc([FC, TT], BF16)
        sg = [A.alloc([TT], F32) for _ in range(2)]
        junk = A.alloc([D], BF16)
        ss = A.alloc([8], F32)
        pre = 's%d_' % si
        w1v = ffn_win[si].rearrange("(kc p) f -> p kc f", p=128)
        CH = 1408
        for kc in range(KC):
            for c in range(2 * DFF // CH):
                P.op('pool', lambda e, kc=kc, c=c: e.dma_start(out=W1[:, kc, c * CH:(c + 1) * CH],
                                                              in_=w1v[:, kc, c * CH:(c + 1) * CH]),
                     writes=[pre + 'W1'], dma_key=pre + 'W1')
        w2v = ffn_wout[si].rearrange("(fc p) d -> p fc d", p=128)
        for fc in range(FC):
            P.op('pool', lambda e, fc=fc: e.dma_start(out=W2[:, fc, :], in_=w2v[:, fc, :]),
                 writes=[pre + 'W2'], dma_key=pre + 'W2')
        P.op('sp', lambda e: e.dma_start(out=gb, in_=ffn_norm[si].partition_broadcast(128)),
             writes=[pre + 'gb'], dma_key=pre + 'gb')
        srcv = src.rearrange("(n s p) d -> n p s d", p=128, s=NSUB)
        dstv = dst.rearrange("(n s p) d -> n p s d", p=128, s=NSUB)
        for it in range(NT):
            sl = it % 2
            X = xt[sl]
            xr = pre + 'xt'
            P.op('sp', lambda e, it=it, X=X: e.dma_start(out=X, in_=srcv[it]),
                 writes=[xr], dma_key=xr)
            HT = hT[sl]
            for s in range(NSUB):
                hs = (it * NSUB + s) % 2
                H = hb[hs]
                hr = pre + 'hb%d' % hs
                P.op('act', lambda e, X=X, s=s: e.activation(out=junk, in_=X[:, s, :], func=AF.Square,
                                                             accum_out=ss[:, 0:1]),
                     reads=[xr], writes=[pre + 'junk', pre + 'ss'])
                P.op('act', lambda e: e.activation(out=ss[:, 1:2], in_=ss[:, 0:1], func=AF.Sqrt,
                                                   scale=1.0 / D, bias=RMS_EPS),
                     reads=[pre + 'ss'], writes=[pre + 'ss1'])
                P.op('dve', lambda e: e.reciprocal(out=ss[:, 2:3], in_=ss[:, 1:2]),
                     reads=[pre + 'ss1'], writes=[pre + 'ss2'])
                P.op('dve', lambda e, X=X, s=s, H=H: e.scalar_tensor_tensor(
                    out=H, in0=X[:, s, :], scalar=ss[:, 2:3], in1=gb, op0=ALU.mult, op1=ALU.mult),
                    reads=[xr, pre + 'ss2', pre + 'gb'], writes=[hr])
                pT = bank(0).bitcast(BF16)
                for kc in range(KC):
                    P.op('pe', lambda e, kc=kc, H=H, pT=pT: e.transpose(
                        out=pT[:, kc * 128:(kc + 1) * 128], in_=H[:, kc * 128:(kc + 1) * 128], identity=ident),
                        reads=[hr, 'ident'], writes=['psT'])
                P.op('act', lambda e, HT=HT, s=s, pT=pT: e.copy(
                    out=HT[:, :, s * 128:(s + 1) * 128], in_=pT.rearrange("p (k t) -> p k t", k=KC)),
                    reads=['psT'], writes=[pre + 'hT%d' % sl])
            for fc in range(FC):
                bg = 1 + 2 * (fc % 2)
                bu = bg + 1
                pgate = bank(bg)
                pup = bank(bu)
                for half, pdst, br in ((0, pgate, bg), (1, pup, bu)):
                    col = half * DFF + fc * 128
                    for kc in range(KC):
                        P.op('pe', lambda e, kc=kc, col=col, pdst=pdst, HT=HT: e.matmul(
                            pdst, lhsT=W1[:, kc, col:col + 128], rhs=HT[:, kc, :],
                            start=(kc == 0), stop=(kc == KC - 1)),
                            reads=[pre + 'W1', pre + 'hT%d' % sl], writes=['psG%d' % br])
                SG = sg[fc % 2]
                P.op('act', lambda e, pgate=pgate, SG=SG: e.activation(out=SG, in_=pgate, func=AF.Silu),
                     reads=['psG%d' % bg], writes=[pre + 'sg%d' % (fc % 2)])
                P.op('dve', lambda e, pup=pup, SG=SG, fc=fc: e.tensor_tensor(
                    out=actT[:, fc, :], in0=SG, in1=pup, op=ALU.mult),
                    reads=['psG%d' % bu, pre + 'sg%d' % (fc % 2)], writes=[pre + 'actT'])
            for s in range(NSUB):
                for dh in range(2):
                    b = 5 + (s * 2 + dh) % 2
                    pd = bank(b)
                    for fc in range(FC):
                        P.op('pe', lambda e, fc=fc, s=s, dh=dh, pd=pd: e.matmul(
                            pd, lhsT=actT[:, fc, s * 128:(s + 1) * 128], rhs=W2[:, fc, dh * 512:(dh + 1) * 512],
                            start=(fc == 0), stop=(fc == FC - 1)),
                            reads=[pre + 'actT', pre + 'W2'], writes=['psD%d' % b])
                    P.op('dve', lambda e, X=X, s=s, dh=dh, pd=pd: e.scalar_tensor_tensor(
                        out=X[:, s, dh * 512:(dh + 1) * 512], in0=pd, scalar=0.5,
                        in1=X[:, s, dh * 512:(dh + 1) * 512], op0=ALU.mult, op1=ALU.add),
                        reads=['psD%d' % b, xr], writes=[xr])
            P.op('sp', lambda e, it=it, X=X: e.dma_start(out=dstv[it], in_=X),
                 reads=[xr], writes=[pre + 'dst'], dma_key=pre + 'st')

    NCOL = 7712
    w_in = din("w_in", [D, NCOL])
    mix_norm = din("mix_norm", [D])
    rwkv_mu = din("rwkv_mu", [3360])
    b_gate = din("b_gate", [2048])
    qn = din("attn_q_norm", [64])
    kn = din("attn_k_norm", [64])
    kscr = "ExternalOutput" if debug else "Internal"
    if 'proj' not in stages:
        kscr = "ExternalInput"
    rkv_d = nc.dram_tensor("rkv_scr", [24, 128, S], F32, kind=kscr).ap()
    ta_d = nc.dram_tensor("ta_scr", [128, S], BF16, kind=kscr).ap()
    tg_d = nc.dram_tensor("tg_scr", [160, S], BF16, kind=kscr).ap()
    qk_d = nc.dram_tensor("qk_scr", [12, 128, S], BF16, kind=kscr).ap()
    v_d = nc.dram_tensor("v_scr", [S, 768], BF16, kind=kscr).ap()
    gate_d = nc.dram_tensor("gate_scr", [16, 128, S], BF16, kind=kscr).ap()

    def proj_stage():
        A.off = const_mark
        pre = 'p_'
        W = A.alloc([KC, NCOL], BF16)
        gb = A.alloc([D], F32)
        X = A.alloc([NSUB, D], F32)
        hb = [A.alloc([D], BF16) for _ in range(2)]
        hT = [A.alloc([KC, TT], BF16) for _ in range(2)]
        ss = A.alloc([8], F32)
        mu_t = A.alloc([27], F32)
        bg_t = A.alloc([16], F32)
        qg_t = A.alloc([2], F32)
        carry = A.alloc([27], F32)
        psb = [A.alloc([TT + 1], F32) for _ in range(2)]
        tmp = [A.alloc([TT], F32) for _ in range(2)]
        sq = [A.alloc([TT], BF16) for _ in range(2)]
        lnb = [A.alloc([TT], F32) for _ in range(2)]
        rkv_st = A.alloc([24, TT], F32)
        ta_st = A.alloc([TT], BF16)
        tg_st = A.alloc([2, TT], BF16)
        qk_st = A.alloc([12, TT], BF16)
        gate_st = A.alloc([16, TT], BF16)
        v_st = A.alloc([NSUB, 768], BF16)
        bones = A.alloc([128], BF16)
        print("proj_stage arena", A.off)
        wv = w_in.rearrange("(kc p) f -> p kc f", p=128)
        CH = 964
        for kc in range(KC):
            for c in range(NCOL // CH):
                P.op('pool', lambda e, kc=kc, c=c: e.dma_start(out=W[:, kc, c * CH:(c + 1) * CH],
                                                              in_=wv[:, kc, c * CH:(c + 1) * CH]),
                     writes=[pre + 'W'], dma_key=pre + 'W')
        P.op('sp', lambda e: e.dma_start(out=gb, in_=mix_norm.partition_broadcast(128)),
             writes=[pre + 'gb'], dma_key=pre + 'par')
        P.op('sp', lambda e: e.dma_start(out=mu_t[:, 0:26], in_=rwkv_mu[0:3328].rearrange("(b p) -> p b", p=128),
                                         allow_slow_non_contiguous=True), writes=[pre + 'mu'], dma_key=pre + 'par')
        P.op('sp', lambda e: e.dma_start(out=mu_t[0:32, 26:27], in_=rwkv_mu[3328:3360].rearrange("(p o) -> p o", o=1)),
             writes=[pre + 'mu'], dma_key=pre + 'par')
        P.op('sp', lambda e: e.dma_start(out=bg_t, in_=b_gate.rearrange("(b p) -> p b", p=128),
                                         allow_slow_non_contiguous=True), writes=[pre + 'bg'], dma_key=pre + 'par')
        for hh in range(2):
            P.op('sp', lambda e, hh=hh: e.dma_start(out=qg_t[hh * 64:(hh + 1) * 64, 0:1],
                                                     in_=qn.rearrange("(p o) -> p o", o=1)),
                 writes=[pre + 'qg'], dma_key=pre + 'par')
            P.op('sp', lambda e, hh=hh: e.dma_start(out=qg_t[hh * 64:(hh + 1) * 64, 1:2],
                                                     in_=kn.rearrange("(p o) -> p o", o=1)),
                 writes=[pre + 'qg'], dma_key=pre + 'par')
        P.op('pool', lambda e: e.tensor_scalar(out=qg_t[:, 0:1], in0=qg_t[:, 0:1], scalar1=0.125, scalar2=None,
                                               op0=ALU.mult), reads=[pre + 'qg'], writes=[pre + 'qg'])
        P.op('pool', lambda e: e.memset(carry, 0.0), writes=[pre + 'carry%d' % i for i in range(27)])
        P.op('pool', lambda e: e.memset(bones, 0.0), writes=[pre + 'bones'])
        P.op('pool', lambda e: e.memset(bones[0:64, 0:64], 1.0), reads=[pre + 'bones'], writes=[pre + 'bones'])
        P.op('pool', lambda e: e.memset(bones[64:128, 64:128], 1.0), reads=[pre + 'bones'], writes=[pre + 'bones'])

        blocks = []
        for b in range(24):
            blocks.append((b * 128, 128, 'rkv', b))
        blocks.append((3072, 128, 'ta', 24))
        blocks.append((3200, 128, 'tg0', 25))
        blocks.append((3328, 32, 'tg1', 26))
        for b in range(6):
            blocks.append((3360 + b * 128, 128, 'q', b))
        for b in range(6):
            blocks.append((4128 + b * 128, 128, 'k', 6 + b))
        for b in range(16):
            blocks.append((5664 + b * 128, 128, 'gate', b))

        srcv = x1_d.rearrange("(n s p) d -> n p s d", p=128, s=NSUB)
        xr = pre + 'X'
        for it in range(NT):
            t0 = it * TT
            sl = it % 2
            P.op('sp', lambda e, it=it: e.dma_start(out=X, in_=srcv[it]), writes=[xr], dma_key=xr)
            HT = hT[sl]
            htr = pre + 'hT%d' % sl
            for s in range(NSUB):
                hs = (it * NSUB + s) % 2
                H = hb[hs]
                hr = pre + 'hb%d' % hs
                P.op('act', lambda e, s=s, H=H: e.activation(out=H, in_=X[:, s, :], func=AF.Square,
                                                             accum_out=ss[:, 0:1]),
                     reads=[xr], writes=[hr, pre + 'ss'])
                P.op('act', lambda e: e.activation(out=ss[:, 1:2], in_=ss[:, 0:1], func=AF.Sqrt,
                                                   scale=1.0 / D, bias=RMS_EPS),
                     reads=[pre + 'ss'], writes=[pre + 'ss1'])
                P.op('dve', lambda e: e.reciprocal(out=ss[:, 2:3], in_=ss[:, 1:2]),
                     reads=[pre + 'ss1'], writes=[pre + 'ss2'])
                P.op('dve', lambda e, s=s, H=H: e.scalar_tensor_tensor(
                    out=H, in0=X[:, s, :], scalar=ss[:, 2:3], in1=gb, op0=ALU.mult, op1=ALU.mult),
                    reads=[xr, pre + 'ss2', pre + 'gb'], writes=[hr])
                pT = bank(0).bitcast(BF16)
                for kc in range(KC):
                    P.op('pe', lambda e, kc=kc, H=H, pT=pT: e.transpose(
                        out=pT[:, kc * 128:(kc + 1) * 128], in_=H[:, kc * 128:(kc + 1) * 128], identity=ident),
                        reads=[hr, 'ident'], writes=['psT'])
                P.op('act', lambda e, HT=HT, s=s, pT=pT: e.copy(
                    out=HT[:, :, s * 128:(s + 1) * 128], in_=pT.rearrange("p (k t) -> p k t", k=KC)),
                    reads=['psT'], writes=[htr])

            pending = [None]

            def flush():
                if pending[0] is None:
                    return
                pg, pgr, j, kind, idx = pending[0]
                pending[0] = None
                pss = bank(5)[:, 0:TT]
                P.op('pe', lambda e, j=j, pss=pss: e.matmul(pss, lhsT=bones, rhs=sq[j], start=True, stop=True),
                     reads=[pre + 'sq%d' % j, pre + 'bones'], writes=['pss'])
                P.op('act', lambda e, j=j, pss=pss: e.activation(out=lnb[j], in_=pss, func=AF.Ln,
                                                                 scale=1.0 / 64, bias=RMS_EPS),
                     reads=['pss'], writes=[pre + 'lnb%d' % j])
                P.op('act', lambda e, j=j: e.activation(out=lnb[j], in_=lnb[j], func=AF.Exp, scale=-0.5),
                     reads=[pre + 'lnb%d' % j], writes=[pre + 'lnb%d' % j])
                c = 0 if kind == 'q' else 1
                P.op('dve', lambda e, j=j, pg=pg, idx=idx, c=c: e.scalar_tensor_tensor(
                    out=qk_st[:, idx, :], in0=pg, scalar=qg_t[:, c:c + 1], in1=lnb[j], op0=ALU.mult, op1=ALU.mult),
                    reads=[pgr, pre + 'lnb%d' % j, pre + 'qg'], writes=[pre + 'qk_st'])

            for bi, (col0, M, kind, idx) in enumerate(blocks):
                b = 1 + bi % 4
                pgr = 'psG%d' % b
                pg = bank(b)[0:M, 0:TT]
                j = bi % 2
                for kc in range(KC):
                    P.op('pe', lambda e, kc=kc, col0=col0, M=M, pg=pg, HT=HT: e.matmul(
                        pg, lhsT=W[:, kc, col0:col0 + M], rhs=HT[:, kc, :], start=(kc == 0), stop=(kc == KC - 1)),
                        reads=[pre + 'W', htr], writes=[pgr])
                flush()
                if kind in ('rkv', 'ta', 'tg0', 'tg1'):
                    cr = pre + 'carry%d' % idx
                    pbr = pre + 'psb%d' % j
                    tr = pre + 'tmp%d' % j
                    PS = psb[j][0:M]
                    TM = tmp[j][0:M]
                    P.op('pool', lambda e, PS=PS, idx=idx, M=M: e.tensor_copy(out=PS[:, 0:1], in_=carry[0:M, idx:idx + 1]),
                         reads=[cr], writes=[pbr])
                    P.op('act', lambda e, PS=PS, pg=pg: e.copy(out=PS[:, 1:TT + 1], in_=pg),
                         reads=[pgr, pbr], writes=[pbr])
                    P.op('dve', lambda e, PS=PS, TM=TM: e.tensor_tensor(out=TM, in0=PS[:, 0:TT], in1=PS[:, 1:TT + 1],
                                                                        op=ALU.subtract),
                         reads=[pbr], writes=[tr])
                    P.op('pool', lambda e, PS=PS, idx=idx, M=M: e.tensor_copy(out=carry[0:M, idx:idx + 1],
                                                                              in_=PS[:, TT:TT + 1]),
                         reads=[pbr], writes=[cr])
                    if kind == 'rkv':
                        P.op('dve', lambda e, PS=PS, TM=TM, idx=idx: e.scalar_tensor_tensor(
                            out=rkv_st[:, idx, :], in0=TM, scalar=mu_t[:, idx:idx + 1], in1=PS[:, 1:TT + 1],
                            op0=ALU.mult, op1=ALU.add),
                            reads=[tr, pbr, pre + 'mu'], writes=[pre + 'rkv_st%d' % (idx // 8)])
                    else:
                        P.op('dve', lambda e, PS=PS, TM=TM, idx=idx, M=M: e.scalar_tensor_tensor(
                            out=TM, in0=TM, scalar=mu_t[0:M, idx:idx + 1], in1=PS[:, 1:TT + 1],
                            op0=ALU.mult, op1=ALU.add),
                            reads=[tr, pbr, pre + 'mu'], writes=[tr])
                        if kind == 'ta':
                            P.op('act', lambda e, TM=TM: e.activation(out=ta_st[0:64], in_=TM[0:64], func=AF.Tanh),
                                 reads=[tr], writes=[pre + 'ta_st'])
                            P.op('act', lambda e, TM=TM: e.copy(out=ta_st[64:128], in_=TM[64:128]),
                                 reads=[tr], writes=[pre + 'ta_st'])
                        elif kind == 'tg0':
                            P.op('act', lambda e, TM=TM: e.activation(out=tg_st[:, 0, :], in_=TM, func=AF.Sigmoid),
                                 reads=[tr], writes=[pre + 'tg_st'])
                        else:
                            P.op('act', lambda e, TM=TM: e.activation(out=tg_st[0:32, 1, :], in_=TM, func=AF.Sigmoid),
                                 reads=[tr], writes=[pre + 'tg_st'])
                elif kind in ('q', 'k'):
                    P.op('act', lambda e, pg=pg, j=j: e.activation(out=sq[j], in_=pg, func=AF.Square),
                         reads=[pgr], writes=[pre + 'sq%d' % j])
                    pending[0] = (pg, pgr, j, kind, idx)
                else:
                    P.op('act', lambda e, pg=pg, idx=idx: e.activation(out=gate_st[:, idx, :], in_=pg, func=AF.Sigmoid,
                                                                       bias=bg_t[:, idx:idx + 1]),
                         reads=[pgr, pre + 'bg'], writes=[pre + 'gate_st'])
            flush()
            for s in range(NSUB):
                for (c0, n, b) in ((4896, 512, 6), (5408, 256, 7)):
                    pv = bank(b)[:, 0:n]
                    for kc in range(KC):
                        P.op('pe', lambda e, kc=kc, s=s, c0=c0, n=n, pv=pv, HT=HT: e.matmul(
                            pv, lhsT=HT[:, kc, s * 128:(s + 1) * 128], rhs=W[:, kc, c0:c0 + n],
                            start=(kc == 0), stop=(kc == KC - 1)),
                            reads=[pre + 'W', htr], writes=['psV%d' % b])
                P.op('act', lambda e, s=s: e.copy(out=v_st[:, s, 0:512], in_=bank(6)),
                     reads=['psV6'], writes=[pre + 'v_st'])
                P.op('dve', lambda e, s=s: e.tensor_copy(out=v_st[:, s, 512:768], in_=bank(7)[:, 0:256]),
                     reads=['psV7'], writes=[pre + 'v_st'])
            rv = rkv_d.rearrange("b p t -> p b t")
            for g in range(3):
                P.op('sp', lambda e, g=g, t0=t0: e.dma_start(out=rv[:, g * 8:(g + 1) * 8, t0:t0 + TT],
                                                             in_=rkv_st[:, g * 8:(g + 1) * 8, :]),
                     reads=[pre + 'rkv_st%d' % g], writes=['d_rkv'], dma_key=pre + 'rkv_st%d' % g)
            P.op('sp', lambda e, t0=t0: e.dma_start(out=ta_d[:, t0:t0 + TT], in_=ta_st),
                 reads=[pre + 'ta_st'], writes=['d_ta'], dma_key=pre + 'ta_st')
            P.op('sp', lambda e, t0=t0: e.dma_start(out=tg_d[0:128, t0:t0 + TT], in_=tg_st[:, 0, :]),
                 reads=[pre + 'tg_st'], writes=['d_tg'], dma_key=pre + 'tg_st')
            P.op('sp', lambda e, t0=t0: e.dma_start(out=tg_d[128:160, t0:t0 + TT], in_=tg_st[0:32, 1, :]),
                 reads=[pre + 'tg_st'], writes=['d_tg'], dma_key=pre + 'tg_st')
            P.op('sp', lambda e, t0=t0: e.dma_start(out=qk_d.rearrange("b p t -> p b t")[:, :, t0:t0 + TT], in_=qk_st),
                 reads=[pre + 'qk_st'], writes=['d_qk'], dma_key=pre + 'qk_st')
            P.op('sp', lambda e, t0=t0: e.dma_start(out=gate_d.rearrange("b p t -> p b t")[:, :, t0:t0 + TT],
                                                    in_=gate_st),
                 reads=[pre + 'gate_st'], writes=['d_gate'], dma_key=pre + 'gate_st')
            P.op('sp', lambda e, t0=t0: e.dma_start(
                out=v_d[t0:t0 + TT, :].rearrange("(s p) c -> p s c", p=128), in_=v_st),
                reads=[pre + 'v_st'], writes=['d_v'], dma_key=pre + 'v_st')

    w2_d = din("rwkv_w2", [64, 1024])
    a2_d = din("rwkv_a2", [64, 1024])
    g2_d = din("rwkv_g2", [160, 1024])
    prm_names = ['rwkv_w0', 'rwkv_a0', 'rwkv_k_k', 'rwkv_k_a', 'rwkv_r_k', 'rwkv_ln_w', 'rwkv_ln_b']
    prm_d = [din(n, [1024]) for n in prm_names]
    ya_d = nc.dram_tensor("ya_scr", [8, 128, S], BF16, kind="ExternalOutput" if debug else "Internal").ap()
    TB = 1024
    NCH = TB // 128
    NB = S // TB if NB_DBG is None else NB_DBG
    C0 = float(np.exp(-0.5))
    GN_EPS = 64e-5

    def rwkv_stage(pairs=range(8)):
        A.off = const_mark
        pre = 'r_'
        WA = A.alloc([1024], BF16)
        G2a = A.alloc([1024], BF16)
        G2b = A.alloc([1024], BF16)
        prm = A.alloc([7, 8], F32)
        bones = A.alloc([128], BF16)
        mk4 = A.alloc([512], F32)
        mkL = A.alloc([2, 128], F32)
        E2 = A.alloc([64], F32)
        mrow = A.alloc([TB], F32)
        f32names = ['R', 'K', 'V', 'SG', 'AA', 'GG', 'KK', 'KM', 'T1', 'BVEC', 'CS', 'T2', 'T3', 'EP', 'EN', 'EPM',
                    'EC', 'BV', 'Y32', 'DD']
        T = {n: A.alloc([TB], F32) for n in f32names}
        bfnames = ['TA', 'TG0', 'TG1', 'TQ', 'BT', 'KT', 'BH', 'KH', 'VT', 'YB', 'YO']
        for n in bfnames:
            T[n] = A.alloc([TB], BF16)
        AR = A.alloc([NCH, 2, 128], BF16)
        TM4 = A.alloc([NCH, 4, 128], BF16)
        PC = A.alloc([NCH], F32)
        SC = [[A.alloc([512], BF16) for _ in range(2)] for _ in range(2)]
        LZ = [[A.alloc([2, 384], BF16) for _ in range(2)] for _ in range(2)]
        MCz = [A.alloc([2, 64], BF16) for _ in range(2)]
        QT = [A.alloc([128], BF16) for _ in range(2)]
        STz = A.alloc([2, 64], BF16)
        DBT = ['BT', 'KT', 'GG', 'BV', 'Y32']
        T2 = {n: [T[n], A.alloc([TB], BF16 if n in ('BT', 'KT') else F32)] for n in DBT}
        AR2 = [AR, A.alloc([NCH, 2, 128], BF16)]
        TM42 = [TM4, A.alloc([NCH, 4, 128], BF16)]
        PC2 = [PC, A.alloc([NCH], F32)]
        par = [0]
        coll = [None]

        class TP(dict):
            def __getitem__(self, k):
                if k in T2:
                    return T2[k][par[0]]
                return dict.__getitem__(self, k)

        class BP(object):
            def __init__(self, bufs):
                self.bufs = bufs

            def __getitem__(self, idx):
                return self.bufs[par[0]][idx]

            def rearrange(self, *a_, **k_):
                return self.bufs[par[0]].rearrange(*a_, **k_)
        T = TP(T)
        AR = BP(AR2)
        TM4 = BP(TM42)
        PC = BP(PC2)
        DBNAMES = set(DBT) | {'AR0', 'AR1', 'PC'} | {'TM4_%d' % c_ for c_ in range(NCH)}
        print("rwkv_stage arena", A.off)

        def R_(n):
            return pre + n
        P.op('pool', lambda e: e.dma_start(out=WA[0:64, :], in_=w2_d), writes=[R_('WA')], dma_key=R_('w'))
        P.op('pool', lambda e: e.dma_start(out=WA[64:128, :], in_=a2_d), writes=[R_('WA')], dma_key=R_('w'))
        P.op('pool', lambda e: e.dma_start(out=G2a, in_=g2_d[0:128, :]), writes=[R_('G2')], dma_key=R_('w'))
        P.op('pool', lambda e: e.dma_start(out=G2b[0:32, :], in_=g2_d[128:160, :]), writes=[R_('G2')], dma_key=R_('w'))
        for i in range(7):
            P.op('sp', lambda e, i=i: e.dma_start(out=prm[:, i, :], in_=prm_d[i].rearrange("(b p) -> p b", p=128),
                                                   allow_slow_non_contiguous=True),
                 writes=[R_('prm')], dma_key=R_('par'))
        P.op('pool', lambda e: e.memset(bones, 0.0), writes=[R_('bones')])
        P.op('pool', lambda e: e.memset(bones[0:64, 0:64], 1.0), reads=[R_('bones')], writes=[R_('bones')])
        P.op('pool', lambda e: e.memset(bones[64:128, 64:128], 1.0), reads=[R_('bones')], writes=[R_('bones')])
        P.op('pool', lambda e: e.memset(mk4, 1.0), writes=[R_('mk4')])
        for q in range(4):
            base = -1 if q % 2 == 0 else 0
            P.op('pool', lambda e, q=q, base=base: e.affine_select(
                out=mk4[:, q * 128:(q + 1) * 128], in_=mk4[:, q * 128:(q + 1) * 128], pattern=[[1, 128]],
                compare_op=ALU.is_ge, fill=0.0, base=base, channel_multiplier=-1),
                reads=[R_('mk4')], writes=[R_('mk4')])
        P.op('pool', lambda e: e.memset(mkL, 1.0), writes=[R_('mkL')])
        P.op('pool', lambda e: e.affine_select(out=mkL, in_=mkL, pattern=[[0, 2], [-1, 128]], compare_op=ALU.is_ge,
                                               fill=0.0, base=-1, channel_multiplier=1),
             reads=[R_('mkL')], writes=[R_('mkL')])
        P.op('pool', lambda e: e.tensor_copy(out=E2[0:64, :], in_=ident_f[0:64, 0:64]), reads=['ident_f'], writes=[R_('E2')])
        P.op('pool', lambda e: e.tensor_copy(out=E2[64:128, :], in_=ident_f[64:128, 64:128]), reads=['ident_f'],
             writes=[R_('E2')])
        P.op('pool', lambda e: e.memset(mrow, 1.0), writes=[R_('mrow')])
        P.op('pool', lambda e: e.memset(mrow.rearrange("p (c t) -> p c t", t=128)[:, :, 0:1], 0.0),
             reads=[R_('mrow')], writes=[R_('mrow')])

        def ch3(ap):
            return ap.rearrange("p (c t) -> p c t", t=128)

        def nm(x):
            if x.startswith('pb') or x in ('ident', 'ident_f'):
                return x
            if x in DBNAMES:
                return R_(x) + '@%d' % par[0]
            return R_(x)

        def pdma(eng, fn, reads=(), writes=(), dma_key=None):
            p_ = par[0]

            def fn2(e, fn=fn, p_=p_):
                par[0] = p_
                return fn(e)
            args = (eng, fn2, list(reads), list(writes), dma_key)
            if coll[0] is not None:
                coll[0].append(args)
            else:
                P.op(args[0], args[1], reads=args[2], writes=args[3], dma_key=args[4])

        def ew(eng, fn, reads, writes):
            pdma(eng, fn, [nm(x) for x in reads], [nm(x) for x in writes])

        for sl_ in range(2):
            ew('pool', lambda e, sl_=sl_: e.memset(MCz[sl_], 0.0), [], ['MC%d' % sl_])
        funcs = {}
        for hp in pairs:
            cols = slice(hp * 128, (hp + 1) * 128)
            def prep(tb, hp=hp, cols=cols):
                t0 = tb * TB
                tsl = slice(t0, t0 + TB)
                for i, n in enumerate(['R', 'K', 'V']):
                    pdma('sp', lambda e, i=i, n=n, hp=hp, tsl=tsl: e.dma_start(out=T[n], in_=rkv_d[i * 8 + hp, :, tsl]),
                         writes=[R_(n)], dma_key=R_('ld' + n))
                pdma('sp', lambda e, tsl=tsl: e.dma_start(out=T['TA'], in_=ta_d[:, tsl]), writes=[R_('TA')],
                     dma_key=R_('ldTA'))
                pdma('sp', lambda e, tsl=tsl: e.dma_start(out=T['TG0'], in_=tg_d[0:128, tsl]), writes=[R_('TG0')],
                     dma_key=R_('ldTG0'))
                pdma('sp', lambda e, tsl=tsl: e.dma_start(out=T['TG1'][0:32], in_=tg_d[128:160, tsl]),
                     writes=[R_('TG1')], dma_key=R_('ldTG1'))
                for hf in range(2):
                    hs = slice(hf * 512, (hf + 1) * 512)
                    ew('pe', lambda e, hs=hs, cols=cols: e.matmul(bank(1), lhsT=WA[0:64, cols], rhs=T['TA'][0:64, hs],
                                                                  start=True, stop=True), ['WA', 'TA'], ['pb1'])
                    ew('act', lambda e, hs=hs, hp=hp: e.activation(out=T['SG'][:, hs], in_=bank(1), func=AF.Sigmoid,
                                                                   bias=prm[:, 0, hp:hp + 1]), ['pb1', 'prm'], ['SG'])
                    ew('pe', lambda e, hs=hs, cols=cols: e.matmul(bank(2), lhsT=WA[64:128, cols], rhs=T['TA'][64:128, hs],
                                                                  start=True, stop=True), ['WA', 'TA'], ['pb2'])
                    ew('act', lambda e, hs=hs, hp=hp: e.activation(out=T['AA'][:, hs], in_=bank(2), func=AF.Sigmoid,
                                                                   bias=prm[:, 1, hp:hp + 1]), ['pb2', 'prm'], ['AA'])
                    ew('pe', lambda e, hs=hs, cols=cols: e.matmul(bank(3), lhsT=G2a[:, cols], rhs=T['TG0'][:, hs],
                                                                  start=True, stop=False), ['G2', 'TG0'], ['pb3', 'pb3'])
                    ew('pe', lambda e, hs=hs, cols=cols: e.matmul(bank(3), lhsT=G2b[0:32, cols], rhs=T['TG1'][0:32, hs],
                                                                  start=False, stop=True), ['G2', 'TG1'], ['pb3', 'pb3'])
                    ew('act', lambda e, hs=hs: e.copy(out=T['GG'][:, hs], in_=bank(3)), ['pb3', 'pb3'], ['GG'])
                ew('dve', lambda e, hp=hp: e.tensor_scalar(out=T['KK'], in0=T['K'], scalar1=prm[:, 2, hp:hp + 1],
                                                           scalar2=None, op0=ALU.mult), ['K', 'prm'], ['KK'])
                ew('act', lambda e: e.activation(out=T['TQ'], in_=T['KK'], func=AF.Square), ['KK'], ['TQ'])
                for hf in range(2):
                    hs = slice(hf * 512, (hf + 1) * 512)
                    ew('pe', lambda e, hs=hs: e.matmul(bank(4), lhsT=bones, rhs=T['TQ'][:, hs], start=True, stop=True),
                       ['bones', 'TQ'], ['pb4'])
                    ew('dve', lambda e, hs=hs: e.tensor_scalar(out=T['T1'][:, hs], in0=bank(4), scalar1=1e-19,
                                                               scalar2=None, op0=ALU.max), ['pb4'], ['T1'])
                ew('act', lambda e: e.activation(out=T['T1'], in_=T['T1'], func=AF.Ln), ['T1'], ['T1'])
                ew('act', lambda e: e.activation(out=T['T1'], in_=T['T1'], func=AF.Exp, scale=-0.5), ['T1'], ['T1'])
                ew('dve', lambda e: e.tensor_tensor(out=T['KK'], in0=T['KK'], in1=T['T1'], op=ALU.mult),
                   ['KK', 'T1'], ['KK'])
                ew('dve', lambda e, hp=hp: e.tensor_scalar(out=T['T1'], in0=T['AA'], scalar1=-1.0,
                                                           scalar2=prm[:, 3, hp:hp + 1], op0=ALU.add, op1=ALU.mult),
                   ['AA', 'prm'], ['T1'])
                ew('dve', lambda e: e.scalar_tensor_tensor(out=T['KM'], in0=T['T1'], scalar=1.0, in1=T['K'],
                                                           op0=ALU.add, op1=ALU.mult), ['T1', 'K'], ['KM'])
                ew('pool', lambda e: e.tensor_tensor(out=T['T1'], in0=T['R'], in1=T['KM'], op=ALU.mult),
                   ['R', 'KM'], ['T1'])
                ew('pool', lambda e, hp=hp: e.tensor_scalar(out=T['TQ'], in0=T['T1'], scalar1=prm[:, 4, hp:hp + 1],
                                                            scalar2=None, op0=ALU.mult), ['T1', 'prm'], ['TQ'])
                for hf in range(2):
                    hs = slice(hf * 512, (hf + 1) * 512)
                    ew('pe', lambda e, hs=hs: e.matmul(bank(5), lhsT=bones, rhs=T['TQ'][:, hs], start=True, stop=True),
                       ['bones', 'TQ'], ['pb5'])
                    ew('dve', lambda e, hs=hs: e.tensor_tensor(out=T['BV'][:, hs], in0=T['V'][:, hs], in1=bank(5),
                                                               op=ALU.mult), ['pb5', 'V'], ['BV'])
                ew('pool', lambda e: e.tensor_tensor(out=T['BVEC'], in0=T['KK'], in1=T['AA'], op=ALU.mult),
                   ['KK', 'AA'], ['BVEC'])
                ew('dve', lambda e: e.tensor_tensor_scan(out=T['CS'], data0=mrow, data1=T['SG'], initial=0.0,
                                                         op0=ALU.mult, op1=ALU.add), ['mrow', 'SG'], ['CS'])
                ew('pool', lambda e: e.tensor_tensor(out=T['T2'], in0=T['CS'], in1=T['SG'], op=ALU.subtract),
                   ['CS', 'SG'], ['T2'])
                ew('act', lambda e: e.activation(out=T['EP'], in_=T['CS'], func=AF.Exp, scale=-C0), ['CS'], ['EP'])
                ew('act', lambda e: e.activation(out=T['EN'], in_=T['CS'], func=AF.Exp, scale=C0), ['CS'], ['EN'])
                ew('act', lambda e: e.activation(out=T['EPM'], in_=T['T2'], func=AF.Exp, scale=-C0), ['T2'], ['EPM'])
                ew('dve', lambda e: e.tensor_tensor(
                    out=ch3(T['T3']), in0=ch3(T['CS']), in1=ch3(T['CS'])[:, :, 127:128].to_broadcast([128, NCH, 128]),
                    op=ALU.subtract), ['CS'], ['T3'])
                ew('act', lambda e: e.activation(out=T['EC'], in_=T['T3'], func=AF.Exp, scale=C0), ['T3'], ['EC'])
                ew('act', lambda e: e.activation(out=PC.rearrange("p (c o) -> p c o", o=1),
                                                 in_=ch3(T['CS'])[:, :, 127:128], func=AF.Exp, scale=-C0),
                   ['CS'], ['PC'])
                ew('dve', lambda e: e.scalar_tensor_tensor(out=AR[:, :, 0, :], in0=ch3(T['EPM']), scalar=-1.0,
                                                           in1=ch3(T['KK']), op0=ALU.mult, op1=ALU.mult),
                   ['EPM', 'KK'], ['AR0'])
                ew('pool', lambda e: e.tensor_tensor(out=AR[:, :, 1, :], in0=ch3(T['EP']), in1=ch3(T['R']), op=ALU.mult),
                   ['EP', 'R'], ['AR1'])
                ew('dve', lambda e: e.tensor_tensor(out=T['BT'], in0=T['EN'], in1=T['BVEC'], op=ALU.mult),
                   ['EN', 'BVEC'], ['BT'])
                ew('pool', lambda e: e.tensor_tensor(out=T['KT'], in0=T['EN'], in1=T['KM'], op=ALU.mult),
                   ['EN', 'KM'], ['KT'])
                ew('dve', lambda e: e.tensor_tensor(out=T['BH'], in0=T['EC'], in1=T['BVEC'], op=ALU.mult),
                   ['EC', 'BVEC'], ['BH'])
                ew('pool', lambda e: e.tensor_tensor(out=T['KH'], in0=T['EC'], in1=T['KM'], op=ALU.mult),
                   ['EC', 'KM'], ['KH'])
                ew('act', lambda e: e.copy(out=T['VT'], in_=T['V']), ['V'], ['VT'])
                pT = bank(0).bitcast(BF16)
                for c in range(NCH):
                    cs_ = slice(c * 128, (c + 1) * 128)
                    srcs = [(AR[:, c, 0, :], 'AR0'), (T['VT'][:, cs_], 'VT'), (T['BH'][:, cs_], 'BH'),
                            (T['KH'][:, cs_], 'KH')]
                    for q, (sap, sr) in enumerate(srcs):
                        ew('pe', lambda e, q=q, sap=sap: e.transpose(out=pT[:, q * 128:(q + 1) * 128], in_=sap,
                                                                     identity=ident), [sr, 'ident'], ['pb0'])
                    ew('act', lambda e, c=c: e.copy(out=TM4[:, c, :, :], in_=pT[:, 0:512].rearrange("p (q t) -> p q t", q=4)),
                       ['pb0'], ['TM4_%d' % c])

            def chunks(tb, inter):
                P1 = (1, 2)
                P2 = ((4, 5), (6, 7))

                def partA(c, sl):
                    cs_ = slice(c * 128, (c + 1) * 128)
                    tm = 'TM4_%d' % c
                    arc = AR[:, c, :, :].rearrange("p a t -> p (a t)")
                    b1 = P1[sl]
                    ps1 = bank(b1)
                    for h2 in range(2):
                        psl = slice(64 * h2, 64 * h2 + 64)
                        scr = 'SC%d_%d' % (sl, h2)
                        ew('pe', lambda e, ps1=ps1, psl=psl, cs_=cs_, arc=arc: e.matmul(
                            ps1[:, 0:256], lhsT=T['BT'][psl, cs_], rhs=arc[psl, :], start=True, stop=True),
                            ['BT', 'AR0', 'AR1'], ['pb%d' % b1])
                        ew('pe', lambda e, ps1=ps1, psl=psl, cs_=cs_, arc=arc: e.matmul(
                            ps1[:, 256:512], lhsT=T['KT'][psl, cs_], rhs=arc[psl, :], start=True, stop=True),
                            ['KT', 'AR0', 'AR1'], ['pb%d' % b1])
                        ew('dve', lambda e, ps1=ps1, h2=h2, sl=sl: e.tensor_tensor(out=SC[sl][h2], in0=mk4, in1=ps1,
                                                                                   op=ALU.mult),
                           ['pb%d' % b1, 'mk4'], [scr])
                        b2 = P2[sl][h2]
                        ew('pe', lambda e, b2=b2, psl=psl, cs_=cs_, c=c: e.matmul(
                            bank(b2)[:, 384:512], lhsT=AR[psl, c, 0, :], rhs=T['BT'][psl, cs_],
                            start=True, stop=True), ['AR0', 'BT'], ['pb%d' % b2])
                    for h2 in range(2):
                        b2 = P2[sl][h2]
                        ew('dve', lambda e, h2=h2, b2=b2, sl=sl: e.tensor_tensor(
                            out=LZ[sl][0][:, h2, 0:128], in0=mkL[:, h2, :], in1=bank(b2)[:, 384:512], op=ALU.mult),
                            ['pb%d' % b2, 'mkL'], ['LL%d_0_%d' % (sl, h2)])
                    for h2 in range(2):
                        pb = 64 * h2
                        ew('pe', lambda e, h2=h2, pb=pb, c=c, sl=sl: e.matmul(
                            bank(3)[:, 256 + h2 * 64:256 + (h2 + 1) * 64], lhsT=SC[sl][h2][:, 256:384],
                            rhs=TM4[:, c, 1, pb:pb + 64], start=True, stop=True), ['SC%d_%d' % (sl, h2), tm], ['pb3'])
                    ew('pool', lambda e, c=c, sl=sl: e.tensor_copy(
                        out=LZ[sl][0][:, :, 128:192], in_=TM4[:, c, 0, :].rearrange("p (h k) -> p h k", h=2)),
                        [tm], ['ZZ%d_0_0' % sl, 'ZZ%d_0_1' % sl])
                    for h2 in range(2):
                        ew('act', lambda e, h2=h2, sl=sl: e.copy(out=LZ[sl][0][:, h2, 192:256],
                                                                 in_=bank(3)[:, 256 + h2 * 64:256 + (h2 + 1) * 64]),
                           ['pb3'], ['ZZ%d_0_%d' % (sl, h2)])

                def partB(c, sl, n):
                    pp = n % 2
                    for h2 in range(2):
                        b2 = P2[sl][h2]
                        ps2 = bank(b2)
                        pbr = 'pb%d' % b2
                        ltn = SC[sl][h2][:, 0:128] if n == 0 else LZ[sl][pp][:, h2, 256:384]
                        rds = ['LL%d_%d_%d' % (sl, pp, h2), 'ZZ%d_%d_%d' % (sl, pp, h2)] + \
                            (['SC%d_%d' % (sl, h2)] if n == 0 else [])
                        if n < 6:
                            ew('pe', lambda e, ps2=ps2, ltn=ltn, pp=pp, h2=h2, sl=sl: e.matmul(
                                ps2[:, 0:256], lhsT=ltn, rhs=LZ[sl][pp][:, h2, 0:256], start=True, stop=True),
                                rds, [pbr])
                            ew('pe', lambda e, ps2=ps2, ltn=ltn, pp=pp, h2=h2, sl=sl: e.matmul(
                                ps2[:, 256:384], lhsT=LZ[sl][pp][:, h2, 0:128], rhs=ltn, start=True, stop=True),
                                rds, [pbr])
                        else:
                            ew('pe', lambda e, ps2=ps2, ltn=ltn, pp=pp, h2=h2, sl=sl: e.matmul(
                                ps2[:, 128:256], lhsT=ltn, rhs=LZ[sl][pp][:, h2, 128:256], start=True, stop=True),
                                rds, [pbr])

                    def cp(h2):
                        b2 = P2[sl][h2]
                        ew('act', lambda e, h2=h2, b2=b2: e.copy(
                            out=LZ[sl][1 - pp][:, h2, :].rearrange("p (s t) -> p s t", s=3)[:, 0:3:2, :],
                            in_=bank(b2)[:, 0:384].rearrange("p (s t) -> p s t", s=3)[:, 0:3:2, :]),
                            ['pb%d' % b2], ['LL%d_%d_%d' % (sl, 1 - pp, h2)])

                    def ad(h2):
                        b2 = P2[sl][h2]
                        ew('dve', lambda e, h2=h2, b2=b2: e.tensor_tensor(
                            out=LZ[sl][1 - pp][:, h2, 128:256], in0=LZ[sl][pp][:, h2, 128:256],
                            in1=bank(b2)[:, 128:256], op=ALU.add),
                            ['pb%d' % b2, 'ZZ%d_%d_%d' % (sl, pp, h2)], ['ZZ%d_%d_%d' % (sl, 1 - pp, h2)])
                    if n < 6:
                        cp(0)
                        ad(1)
                        cp(1)
                        ad(0)
                    else:
                        ad(0)
                        ad(1)

                def partC(c, sl):
                    tm = 'TM4_%d' % c
                    ZF = LZ[sl][1]
                    p3 = bank(0)[:, 192:384]
                    for h2 in range(2):
                        pb = 64 * h2
                        psl = slice(pb, pb + 64)
                        zr = 'ZZ%d_1_%d' % (sl, h2)
                        ew('pe', lambda e, h2=h2, pb=pb, psl=psl, c=c: e.matmul(
                            p3[psl, 0:64], lhsT=ZF[:, h2, 128:192], rhs=TM4[:, c, 2, pb:pb + 64],
                            start=True, stop=True, tile_position=(0, pb)), [zr, tm], ['pb0'])
                        ew('pe', lambda e, h2=h2, pb=pb, psl=psl: e.matmul(
                            p3[psl, 64:192], lhsT=ZF[:, h2, 128:192], rhs=SC[sl][h2][:, 128:256],
                            start=True, stop=True, tile_position=(0, pb)), [zr, 'SC%d_%d' % (sl, h2)], ['pb0'])
                    for h2 in range(2):
                        psl = slice(64 * h2, 64 * h2 + 64)
                        ew('dve', lambda e, c=c, psl=psl, h2=h2: e.scalar_tensor_tensor(
                            out=MCz[sl][psl, h2, :], in0=E2[psl, :], scalar=PC[psl, c:c + 1], in1=p3[psl, 0:64],
                            op0=ALU.mult, op1=ALU.add), ['E2', 'PC', 'pb0'], ['MC%d' % sl])
                    ew('dve', lambda e, c=c: e.tensor_tensor(out=QT[sl], in0=AR[:, c, 1, :], in1=p3[:, 64:192],
                                                             op=ALU.add), ['pb0', 'AR1'], ['QT%d' % sl])

                def partD(c, sl):
                    cs_ = slice(c * 128, (c + 1) * 128)
                    tm = 'TM4_%d' % c
                    ZF = LZ[sl][1]
                    for h2 in range(2):
                        pb = 64 * h2
                        psl = slice(pb, pb + 64)
                        sb_ = 3 if h2 == 0 else 0
                        sr_ = 'pb%d' % sb_
                        psY = bank(sb_)[:, 0:128]
                        psS = bank(sb_)[:, 128:192]
                        UU = ZF[:, h2, 192:256]
                        zr = 'ZZ%d_1_%d' % (sl, h2)
                        scr = 'SC%d_%d' % (sl, h2)
                        ew('pe', lambda e, psl=psl, pb=pb, h2=h2, UU=UU, psY=psY: e.matmul(
                            psY[psl, :], lhsT=UU, rhs=SC[sl][h2][:, 128:256], start=True, stop=False,
                            tile_position=(0, pb)), [zr, scr], [sr_])
                        ew('pe', lambda e, psl=psl, pb=pb, h2=h2, c=c, psY=psY: e.matmul(
                            psY[psl, :], lhsT=TM4[:, c, 1, pb:pb + 64], rhs=SC[sl][h2][:, 384:512], start=False,
                            stop=False, tile_position=(0, pb)), [tm, scr], [sr_])
                        ew('pe', lambda e, psl=psl, pb=pb, psY=psY, h2=h2: e.matmul(
                            psY[psl, :], lhsT=STz[:, h2, :], rhs=QT[sl], start=False, stop=True,
                            tile_position=(0, pb)), ['ST', 'QT%d' % sl], [sr_])
                        ew('pe', lambda e, psl=psl, pb=pb, psS=psS, h2=h2: e.matmul(
                            psS[psl, :], lhsT=MCz[sl][:, h2, :], rhs=STz[:, h2, :], start=True, stop=False,
                            tile_position=(0, pb)), ['MC%d' % sl, 'ST'], [sr_])
                        ew('pe', lambda e, psl=psl, pb=pb, c=c, UU=UU, psS=psS: e.matmul(
                            psS[psl, :], lhsT=TM4[:, c, 2, pb:pb + 64], rhs=UU, start=False, stop=False,
                            tile_position=(0, pb)), [tm, zr], [sr_])
                        ew('pe', lambda e, psl=psl, pb=pb, c=c, psS=psS: e.matmul(
                            psS[psl, :], lhsT=TM4[:, c, 3, pb:pb + 64], rhs=TM4[:, c, 1, pb:pb + 64], start=False,
                            stop=True, tile_position=(0, pb)), [tm], [sr_])
                    for h2 in range(2):
                        pb = 64 * h2
                        psl = slice(pb, pb + 64)
                        sb_ = 3 if h2 == 0 else 0
                        sr_ = 'pb%d' % sb_
                        ew('act', lambda e, cs_=cs_, psl=psl, sb_=sb_: e.copy(out=T['Y32'][psl, cs_],
                                                                             in_=bank(sb_)[psl, 0:128]),
                           [sr_], ['Y32'])
                        ew('dve', lambda e, psl=psl, sb_=sb_, h2=h2: e.tensor_copy(out=STz[psl, h2, :],
                                                                                   in_=bank(sb_)[psl, 128:192]),
                           [sr_], ['ST'])

                for c0 in range(0, NCH, 2):
                    for sl in range(2):
                        partA(c0 + sl, sl)
                    for n in range(7):
                        for sl in range(2):
                            partB(c0 + sl, sl, n)
                        if n in (1, 3, 5):
                            inter()
                    for sl in range(2):
                        partC(c0 + sl, sl)
                    for sl in range(2):
                        partD(c0 + sl, sl)
                    inter()

            def outst(tb, hp=hp):
                t0 = tb * TB
                tsl = slice(t0, t0 + TB)
                ew('pool', lambda e: e.tensor_copy(out=T['YB'], in_=T['Y32']), ['Y32'], ['YB'])
                for hf in range(2):
                    hs = slice(hf * 512, (hf + 1) * 512)
                    ew('pe', lambda e, hs=hs: e.matmul(bank(1), lhsT=bones, rhs=T['YB'][:, hs], start=True, stop=True),
                       ['bones', 'YB'], ['pb1'])
                    ew('dve', lambda e, hs=hs: e.scalar_tensor_tensor(out=T['DD'][:, hs], in0=bank(1), scalar=-1.0 / 64,
                                                                      in1=T['Y32'][:, hs], op0=ALU.mult, op1=ALU.add),
                       ['pb1', 'Y32'], ['DD'])
                ew('act', lambda e: e.activation(out=T['TQ'], in_=T['DD'], func=AF.Square), ['DD'], ['TQ'])
                for hf in range(2):
                    hs = slice(hf * 512, (hf + 1) * 512)
                    ew('pe', lambda e, hs=hs: e.matmul(bank(2), lhsT=bones, rhs=T['TQ'][:, hs], start=True, stop=True),
                       ['bones', 'TQ'], ['pb2'])
                    ew('act', lambda e, hs=hs: e.activation(out=T['T1'][:, hs], in_=bank(2), func=AF.Ln, scale=1.0 / 64,
                                                            bias=GN_EPS), ['pb2'], ['T1'])
                ew('act', lambda e: e.activation(out=T['T1'], in_=T['T1'], func=AF.Exp, scale=-0.5), ['T1'], ['T1'])
                ew('dve', lambda e: e.tensor_tensor(out=T['DD'], in0=T['DD'], in1=T['T1'], op=ALU.mult),
                   ['DD', 'T1'], ['DD'])
                ew('dve', lambda e, hp=hp: e.tensor_scalar(out=T['DD'], in0=T['DD'], scalar1=prm[:, 5, hp:hp + 1],
                                                           scalar2=prm[:, 6, hp:hp + 1], op0=ALU.mult, op1=ALU.add),
                   ['DD', 'prm'], ['DD'])
                ew('pool', lambda e: e.tensor_tensor(out=T['DD'], in0=T['DD'], in1=T['BV'], op=ALU.add),
                   ['DD', 'BV'], ['DD'])
                ew('dve', lambda e: e.tensor_tensor(out=T['YO'], in0=T['DD'], in1=T['GG'], op=ALU.mult),
                   ['DD', 'GG'], ['YO'])
                pdma('sp', lambda e, hp=hp, tsl=tsl: e.dma_start(out=ya_d[hp, :, tsl], in_=T['YO']),
                     reads=[R_('YO')], writes=['d_ya'], dma_key=R_('stYO'))


            funcs[hp] = (prep, chunks, outst)

        def emit_all(lst):
            for a_ in lst:
                P.op(a_[0], a_[1], reads=a_[2], writes=a_[3], dma_key=a_[4])
        blocks = [(hp_, tb_) for hp_ in pairs for tb_ in range(NB)]
        par[0] = 0
        funcs[blocks[0][0]][0](blocks[0][1])
        pend_out = []
        for bi, (hp_, tb_) in enumerate(blocks):
            prep_f, chunks_f, outst_f = funcs[hp_]
            pend = list(pend_out)
            if bi + 1 < len(blocks):
                nhp, ntb = blocks[bi + 1]
                par[0] = (bi + 1) % 2
                coll[0] = pend
                funcs[nhp][0](ntb)
                coll[0] = None
            par[0] = bi % 2
            if tb_ == 0:
                ew('pool', lambda e: e.memset(STz, 0.0), [], ['ST'])
            nsl = 4 * (NCH // 2)
            step = (len(pend) + nsl - 1) // nsl if pend else 0
            pos = [0]

            def inter(pend=pend, step=step, pos=pos):
                open_banks = set()
                cnt = 0
                while pos[0] < len(pend) and (cnt < step or open_banks):
                    a_ = pend[pos[0]]
                    pos[0] += 1
                    cnt += 1
                    if a_[0] == 'pe':
                        open_banks.update(w_ for w_ in a_[3] if w_.startswith('pb'))
                    else:
                        open_banks.difference_update(r_ for r_ in a_[2] if r_.startswith('pb'))
                    P.op(a_[0], a_[1], reads=a_[2], writes=a_[3], dma_key=a_[4])
            chunks_f(tb_, inter)
            emit_all(pend[pos[0]:])
            par[0] = bi % 2
            pend_out = []
            coll[0] = pend_out
            outst_f(tb_)
            coll[0] = None
        emit_all(pend_out)

    yb_d = nc.dram_tensor("yb_scr", [6, 128, S], BF16, kind="ExternalOutput" if debug else "Internal").ap()
    DIL = (1, 4, 16)

    def attn_stage_full(js=range(4)):
        A.off = const_mark
        pre = 'a_'

        def R_(n):
            return pre + n

        def nm(x):
            if x.startswith('pb') or x in ('ident', 'ident_f'):
                return x
            return R_(x)

        def ew(eng, fn, reads, writes):
            P.op(eng, fn, reads=[nm(x) for x in reads], writes=[nm(x) for x in writes])
        QH = A.alloc([S], BF16)
        KH = A.alloc([S], BF16)
        VX = A.alloc([32, 64], BF16)
        ONES = A.alloc([64], BF16)
        OT = [A.alloc([S], F32) for _ in range(3)]
        DEN = [A.alloc([S], F32) for _ in range(3)]
        PT = [A.alloc([256], BF16) for _ in range(4)]
        mask2 = A.alloc([256], BF16)
        RD = [A.alloc([512], F32) for _ in range(2)]
        YBS = [A.alloc([512], BF16) for _ in range(2)]
        print("attn_stage arena", A.off)
        P.op('pool', lambda e: e.memset(mask2, 1.0), writes=[R_('mask')])
        P.op('pool', lambda e: e.affine_select(out=mask2[:, 0:128], in_=mask2[:, 0:128], pattern=[[1, 128]],
                                               compare_op=ALU.is_ge, fill=0.0, base=0, channel_multiplier=-1),
             reads=[R_('mask')], writes=[R_('mask')])
        P.op('pool', lambda e: e.affine_select(out=mask2[:, 128:256], in_=mask2[:, 128:256], pattern=[[-1, 128]],
                                               compare_op=ALU.is_ge, fill=0.0, base=0, channel_multiplier=1),
             reads=[R_('mask')], writes=[R_('mask')])
        P.op('pool', lambda e: e.memset(ONES, 1.0), writes=[R_('ONES')])

        tcount = [0]
        ccount = [0]
        for j in js:
            for g in range(3):
                d = DIL[g]
                nb = S // d // 128
                h = 4 * g + j
                pair = h // 2
                pb = 64 * (h % 2)
                vv = v_d.rearrange("(m d) c -> d m c", d=d)
                for r in range(d):
                    for n0 in range(0, nb, 8):
                        n1 = min(nb, n0 + 8)
                        P.op('sp', lambda e, r=r, h=h, nb=nb, vv=vv, n0=n0, n1=n1: e.dma_start(
                            out=VX[:, r * nb + n0:r * nb + n1, :],
                            in_=vv[r, n0 * 128:n1 * 128, h * 64:(h + 1) * 64].rearrange("(n i) c -> i n c", i=128)),
                            writes=[R_('VX')], dma_key=R_('ldV'))
                P.op('sp', lambda e, pair=pair, pb=pb: e.dma_start(out=QH[0:64, :], in_=qk_d[pair, pb:pb + 64, :]),
                     writes=[R_('QH')], dma_key=R_('ldQ'))
                P.op('sp', lambda e, pair=pair, pb=pb: e.dma_start(out=KH[0:64, :], in_=qk_d[6 + pair, pb:pb + 64, :]),
                     writes=[R_('KH')], dma_key=R_('ldK'))
                qv = QH.rearrange("p (m d) -> p d m", d=d)
                kv = KH.rearrange("p (m d) -> p d m", d=d)
                otr = 'OT%d' % g
                otv = OT[g].rearrange("p (m d) -> p d m", d=d)
                dnv = DEN[g].rearrange("p (m d) -> p d m", d=d)
                tiles = []
                for r in range(d):
                    for n in range(nb):
                        tiles.append((r, n, tcount[0]))
                        tcount[0] += 1

                def emit_score(r, n, ti, kv=kv, qv=qv, nb=nb):
                    nq = 256 if n + 1 < nb else 128
                    sbk = 1 + ti % 2
                    ps = bank(sbk)[:, 0:nq]
                    pt = PT[ti % 4]
                    ptr = 'PT%d' % (ti % 4)
                    ew('pe', lambda e, ps=ps, r=r, n=n, nq=nq: e.matmul(
                        ps, lhsT=kv[0:64, r, 128 * n:128 * n + 128], rhs=qv[0:64, r, 128 * n:128 * n + nq],
                        start=True, stop=True), ['KH', 'QH'], ['pb%d' % sbk])
                    ew('act', lambda e, ps=ps, pt=pt, nq=nq: e.activation(out=pt[:, 0:nq], in_=ps, func=AF.Exp),
                       ['pb%d' % sbk], [ptr])
                    ew('pool', lambda e, pt=pt, nq=nq: e.tensor_tensor(out=pt[:, 0:nq], in0=pt[:, 0:nq],
                                                                      in1=mask2[:, 0:nq], op=ALU.mult),
                       [ptr, 'mask'], [ptr])

                def emit_pv(r, n, ti, otv=otv, dnv=dnv, nb=nb, g=g, otr=otr):
                    pt = PT[ti % 4]
                    ptr = 'PT%d' % (ti % 4)
                    obk = 3 + ti % 2
                    po = bank(obk)[0:64, 0:128]
                    pdn = bank(obk)[0:64, 128:256]
                    vt = r * nb + n
                    has_prev = n > 0
                    if has_prev:
                        ppt = PT[(ti - 1) % 4]
                        pptr = 'PT%d' % ((ti - 1) % 4)
                        ew('pe', lambda e, ppt=ppt, vt=vt: e.matmul(
                            po, lhsT=VX[:, vt - 1, :], rhs=ppt[:, 128:256], start=True, stop=False),
                            ['VX', pptr], ['pb%d' % obk])
                    ew('pe', lambda e, pt=pt, vt=vt: e.matmul(
                        po, lhsT=VX[:, vt, :], rhs=pt[:, 0:128], start=(not has_prev), stop=True),
                        ['VX', ptr], ['pb%d' % obk])
                    if has_prev:
                        ew('pe', lambda e, ppt=ppt: e.matmul(
                            pdn, lhsT=ONES, rhs=ppt[:, 128:256], start=True, stop=False),
                            ['ONES', pptr], ['pb%d' % obk])
                    ew('pe', lambda e, pt=pt: e.matmul(
                        pdn, lhsT=ONES, rhs=pt[:, 0:128], start=(not has_prev), stop=True),
                        ['ONES', ptr], ['pb%d' % obk])
                    ew('dve', lambda e, r=r, n=n: e.tensor_copy(
                        out=otv[0:64, r, 128 * n:128 * n + 128], in_=po), ['pb%d' % obk], [otr])
                    ew('dve', lambda e, r=r, n=n: e.tensor_copy(
                        out=dnv[0:64, r, 128 * n:128 * n + 128], in_=pdn), ['pb%d' % obk], ['DEN%d' % g])

                for idx, (r, n, ti) in enumerate(tiles):
                    emit_score(r, n, ti)
                    if idx >= 1:
                        emit_pv(*tiles[idx - 1])
                emit_pv(*tiles[-1])
            for ck in range(S // 512):
                csl = slice(ck * 512, (ck + 1) * 512)
                cc = ccount[0]
                ccount[0] += 1
                rd = RD[cc % 2]
                ew('pool', lambda e, rd=rd, csl=csl: e.tensor_tensor(out=rd[0:64, :], in0=DEN[0][0:64, csl],
                                                                     in1=DEN[1][0:64, csl], op=ALU.add),
                   ['DEN0', 'DEN1'], ['RD%d' % (cc % 2)])
                ew('pool', lambda e, rd=rd, csl=csl: e.tensor_tensor(out=rd[0:64, :], in0=rd[0:64, :],
                                                                     in1=DEN[2][0:64, csl], op=ALU.add),
                   ['RD%d' % (cc % 2), 'DEN2'], ['RD%d' % (cc % 2)])
                ew('dve', lambda e, rd=rd: e.reciprocal(out=rd[0:64, :], in_=rd[0:64, :]), ['RD%d' % (cc % 2)],
                   ['RD%d' % (cc % 2)])
                for g in range(3):
                    h = 4 * g + j
                    pair = h // 2
                    pb = 64 * (h % 2)
                    yi = (cc * 3 + g) % 2
                    ys = YBS[yi]
                    ew('pool', lambda e, g=g, csl=csl, rd=rd, ys=ys: e.tensor_tensor(
                        out=ys[0:64, :], in0=OT[g][0:64, csl], in1=rd[0:64, :], op=ALU.mult),
                        ['OT%d' % g, 'RD%d' % (cc % 2)], ['YBS%d' % yi])
                    P.op('sp', lambda e, pair=pair, pb=pb, csl=csl, ys=ys: e.dma_start(
                        out=yb_d[pair, pb:pb + 64, csl], in_=ys[0:64, :]),
                        reads=[R_('YBS%d' % yi)], writes=['d_yb'], dma_key=R_('stYB%d' % yi))

    wpr_d = din("w_proj_rwkv", [1024, 1024])
    wpa_d = din("w_proj_attn", [768, 1024])
    wo_d = din("w_out", [1024, 1024])
    x2_d = nc.dram_tensor("x2_scr", [S, D], F32, kind="ExternalOutput" if debug else "Internal").ap()

    def merge_stage():
        A.off = const_mark
        pre = 'm_'

        def R_(n):
            return pre + n

        def nm(x):
            if x.startswith('pb'):
                return x
            return R_(x)

        def ew(eng, fn, reads, writes):
            P.op(eng, fn, reads=[nm(x) for x in reads], writes=[nm(x) for x in writes])
        Wr = A.alloc([8, 1024], BF16)
        Wa = A.alloc([6, 1024], BF16)
        Wo = A.alloc([8, 1024], BF16)
        X = [A.alloc([NSUB, D], F32) for _ in range(2)]
        YA = [A.alloc([8, TT], BF16) for _ in range(2)]
        YB = [A.alloc([6, TT], BF16) for _ in range(2)]
        G = [A.alloc([16, TT], BF16) for _ in range(2)]
        MT = A.alloc([8, TT], BF16)
        t1 = [A.alloc([TT], F32) for _ in range(2)]
        t2 = [A.alloc([TT], F32) for _ in range(2)]
        print("merge_stage arena", A.off)
        for kc in range(8):
            P.op('pool', lambda e, kc=kc: e.dma_start(out=Wr[:, kc, :], in_=wpr_d[kc * 128:(kc + 1) * 128, :]),
                 writes=[R_('Wr')], dma_key=R_('w'))
            P.op('pool', lambda e, kc=kc: e.dma_start(out=Wo[:, kc, :], in_=wo_d[kc * 128:(kc + 1) * 128, :]),
                 writes=[R_('Wo')], dma_key=R_('w'))
        for kc in range(6):
            P.op('pool', lambda e, kc=kc: e.dma_start(out=Wa[:, kc, :], in_=wpa_d[kc * 128:(kc + 1) * 128, :]),
                 writes=[R_('Wa')], dma_key=R_('w'))
        srcv = x1_d.rearrange("(n s p) d -> n p s d", p=128, s=NSUB)
        dstv = x2_d.rearrange("(n s p) d -> n p s d", p=128, s=NSUB)
        for it in range(NT):
            sl = it % 2
            tsl = slice(it * TT, (it + 1) * TT)
            sfx = '%d' % sl
            P.op('sp', lambda e, it=it, sl=sl: e.dma_start(out=X[sl], in_=srcv[it]), writes=[R_('X' + sfx)],
                 dma_key=R_('ldX' + sfx))
            P.op('sp', lambda e, sl=sl, tsl=tsl: e.dma_start(out=YA[sl], in_=ya_d.rearrange("b p t -> p b t")[:, :, tsl]),
                 writes=[R_('YA' + sfx)], dma_key=R_('ldYA' + sfx))
            P.op('sp', lambda e, sl=sl, tsl=tsl: e.dma_start(out=YB[sl], in_=yb_d.rearrange("b p t -> p b t")[:, :, tsl]),
                 writes=[R_('YB' + sfx)], dma_key=R_('ldYB' + sfx))
            P.op('sp', lambda e, sl=sl, tsl=tsl: e.dma_start(out=G[sl], in_=gate_d.rearrange("b p t -> p b t")[:, :, tsl]),
                 writes=[R_('G' + sfx)], dma_key=R_('ldG' + sfx))
            for c in range(8):
                bk = 1 + c % 2
                pg = bank(bk)
                for kc in range(8):
                    ew('pe', lambda e, c=c, kc=kc, pg=pg, sl=sl: e.matmul(
                        pg[:, 0:TT], lhsT=Wr[:, kc, c * 128:(c + 1) * 128], rhs=YA[sl][:, kc, :],
                        start=(kc == 0), stop=(kc == 7)), ['Wr', 'YA' + sfx], ['pb%d' % bk])
                for kc in range(6):
                    ew('pe', lambda e, c=c, kc=kc, pg=pg, sl=sl: e.matmul(
                        pg[:, TT:2 * TT], lhsT=Wa[:, kc, c * 128:(c + 1) * 128], rhs=YB[sl][:, kc, :],
                        start=(kc == 0), stop=(kc == 5)), ['Wa', 'YB' + sfx], ['pb%d' % bk])
                q = c % 2
                ew('dve', lambda e, c=c, pg=pg, sl=sl, q=q: e.tensor_tensor(out=t1[q], in0=G[sl][:, c, :],
                                                                           in1=pg[:, 0:TT], op=ALU.mult),
                   ['G' + sfx, 'pb%d' % bk], ['t1%d' % q])
                ew('dve', lambda e, c=c, pg=pg, sl=sl, q=q: e.tensor_tensor(out=t2[q], in0=G[sl][:, 8 + c, :],
                                                                           in1=pg[:, TT:2 * TT], op=ALU.mult),
                   ['G' + sfx, 'pb%d' % bk], ['t2%d' % q])
                ew('pool', lambda e, c=c, q=q: e.tensor_tensor(out=MT[:, c, :], in0=t1[q], in1=t2[q], op=ALU.add),
                   ['t1%d' % q, 't2%d' % q], ['MT'])
            for s in range(NSUB):
                for dh in range(2):
                    bk = 3 + (s * 2 + dh) % 2
                    pd = bank(bk)
                    for c in range(8):
                        ew('pe', lambda e, c=c, s=s, dh=dh, pd=pd: e.matmul(
                            pd, lhsT=MT[:, c, s * 128:(s + 1) * 128], rhs=Wo[:, c, dh * 512:(dh + 1) * 512],
                            start=(c == 0), stop=(c == 7)), ['MT', 'Wo'], ['pb%d' % bk])
                    ew('dve', lambda e, s=s, dh=dh, pd=pd, sl=sl: e.tensor_tensor(
                        out=X[sl][:, s, dh * 512:(dh + 1) * 512], in0=X[sl][:, s, dh * 512:(dh + 1) * 512], in1=pd,
                        op=ALU.add), ['pb%d' % bk, 'X' + sfx], ['X' + sfx])
            P.op('sp', lambda e, it=it, sl=sl: e.dma_start(out=dstv[it], in_=X[sl]),
                 reads=[R_('X' + sfx)], writes=['d_x2'], dma_key=R_('stX' + sfx))

    if 'ffn1' in stages:
        ffn_stage(0, x, x1_d)
        P.sync_all()
    if 'proj' in stages:
        proj_stage()
        P.sync_all()
    if 'rwkv' in stages:
        rwkv_stage(range(NPAIRS_DBG))
        P.sync_all()
    if 'attn' in stages:
        attn_stage_full(JS_DBG)
        P.sync_all()
    if 'merge' in stages:
        merge_stage()
        P.sync_all()
    if 'ffn2' in stages:
        ffn_stage(1, x2_d if 'merge' in stages else x1_d, out)
    P.sync_all()
    P.op('sp', None)
    P.finalize_and_emit(stack)
    stack.close()
    return nc


_CACHE = {}


SHARED_KEYS = ['ffn1_norm', 'ffn1_w_in', 'ffn1_w_out', 'ffn2_norm', 'ffn2_w_in', 'ffn2_w_out',
               'w_in', 'mix_norm', 'rwkv_mu', 'b_gate', 'attn_q_norm', 'attn_k_norm',
               'w_proj_rwkv', 'w_proj_attn', 'w_out', 'rwkv_w2', 'rwkv_a2', 'rwkv_g2', 'rwkv_w0', 'rwkv_a0', 'rwkv_k_k', 'rwkv_k_a', 'rwkv_r_k', 'rwkv_ln_w', 'rwkv_ln_b']


def make_shared(inputs):
    shared = {}
    for k in SHARED_KEYS:
        v = np.asarray(inputs[k], dtype=np.float32)
        v = v.reshape(v.shape[1:])
        if k == 'rwkv_r_k':
            v = v.reshape(-1)
        shared[k] = np.ascontiguousarray(v)
    return shared


def kernel(**inputs):
    if 'nc' not in _CACHE:
        _CACHE['nc'] = build_program()
    nc = _CACHE['nc']
    x = np.ascontiguousarray(inputs['x'], dtype=np.float32)
    shared = make_shared(inputs)
    in_maps = []
    for c in range(NCORES):
        m = dict(shared)
        m['x'] = x[c]
        in_maps.append(m)
    res = run_bass_kernel_spmd(nc, in_maps, core_ids=list(range(NCORES)))
    return np.stack([np.asarray(r['out']) for r in res.results], axis=0)
```

```python
import numpy as np
from contextlib import ExitStack
import concourse.bass as bass
import concourse.mybir as mybir
from concourse.bass_utils import run_bass_kernel_spmd
from concourse.alu_op_type import AluOpType as ALU

F32 = mybir.dt.float32
BF16 = mybir.dt.bfloat16
AF = mybir.ActivationFunctionType
AX = mybir.AxisListType

S = 4096
D = 1024
DFF = 2816
NCORES = 8
RMS_EPS = 1e-6

ENGS = ['pe', 'act', 'dve', 'pool', 'sp']
MAXOPS = [0]
SEM_LIM = 30000
DMA_LIM = 1800


class Prog:
    def __init__(self, nc):
        self.nc = nc
        self.ops = []
        self.eng_ops = {e: [] for e in ENGS}
        self.last_w = {}
        self.readers = {}
        self.dma_cnt = {}
        self.barrier = {e: None for e in ENGS}

    def op(self, eng, fn, reads=(), writes=(), dma_key=None):
        mo = MAXOPS[0]
        if mo and len(self.ops) >= mo and fn is not None:
            return None
        if mo and len(self.ops) == mo - 1 and fn is not None:
            print("LAST OP:", eng, fn.__code__.co_firstlineno, reads, writes)
        oid = len(self.ops)
        deps = set()
        dma_deps = {}
        writes = list(writes) + [r for r in reads if (r.startswith('pb') or r.startswith('ps')) and r not in writes]

        def add(o):
            od = self.ops[o]
            if od['dma_key'] is not None:
                k = od['dma_key']
                dma_deps[k] = self.dma_cnt[k]
            else:
                deps.add(o)
        for r in reads:
            if r in self.last_w:
                add(self.last_w[r])
        for w in writes:
            if w in self.last_w:
                add(self.last_w[w])
            for rd in self.readers.get(w, {}).values():
                add(rd)
        if self.barrier[eng] is not None:
            bd, bdma = self.barrier[eng]
            for o in bd:
                deps.add(o)
            for k, v in bdma.items():
                dma_deps[k] = max(dma_deps.get(k, 0), v)
            self.barrier[eng] = None
        cnt = None
        if dma_key is not None:
            self.dma_cnt[dma_key] = self.dma_cnt.get(dma_key, 0) + 1
            cnt = self.dma_cnt[dma_key]
        o = dict(id=oid, eng=eng, fn=fn, deps=deps, dma_deps=dma_deps, dma_key=dma_key,
                 dma_cnt=cnt, idx=len(self.eng_ops[eng]), sig=False)
        self.ops.append(o)
        self.eng_ops[eng].append(o)
        ch = eng if dma_key is None else 'dma:' + dma_key
        for r in reads:
            self.readers.setdefault(r, {})[ch] = oid
        for w in writes:
            self.last_w[w] = oid
            self.readers[w] = {}
        return oid

    def sync_all(self):
        bd = set()
        for e in ENGS:
            for o in reversed(self.eng_ops[e]):
                if o['dma_key'] is None and o['fn'] is not None:
                    bd.add(o['id'])
                    break
        bdma = dict(self.dma_cnt)
        for e in ENGS:
            self.barrier[e] = (set(bd), dict(bdma))

    def finalize_and_emit(self, stack):
        nc = self.nc
        for o in self.ops:
            per = {}
            for d in o['deps']:
                od = self.ops[d]
                if od['eng'] == 'pe' and o['eng'] == 'pe':
                    continue
                e = od['eng']
                if e not in per or self.ops[per[e]]['idx'] < od['idx']:
                    per[e] = d
            o['cdeps'] = per
            for d in per.values():
                self.ops[d]['sig'] = True
        sems = {}

        def get_sem(name):
            return sems[name]
        for e in ENGS:
            c = 0
            for o in self.eng_ops[e]:
                if o['dma_key'] is None and o['sig']:
                    c += 1
                    o['sigval'] = c
        for o in self.ops:
            waits = {}
            for e, d in o['cdeps'].items():
                v = self.ops[d]['sigval']
                key = ('c_%s_%d' % (e, (v - 1) // SEM_LIM))
                val = (v - 1) % SEM_LIM + 1
                waits[key] = max(waits.get(key, 0), val)
            for k, n in o['dma_deps'].items():
                key = ('d_%s_%d' % (k, (n - 1) // DMA_LIM))
                val = 16 * ((n - 1) % DMA_LIM + 1)
                waits[key] = max(waits.get(key, 0), val)
            o['waits'] = waits
        names = set()
        for o in self.ops:
            names.update(o['waits'].keys())
            if o['dma_key'] is not None:
                names.add('d_%s_%d' % (o['dma_key'], (o['dma_cnt'] - 1) // DMA_LIM))
            elif o['sig']:
                names.add('c_%s_%d' % (o['eng'], (o['sigval'] - 1) // SEM_LIM))
        for nm in sorted(names):
            sems[nm] = stack.enter_context(nc.semaphore(nm))
        print("n_sems", len(names), "n_ops", len(self.ops), {e: len(v) for e, v in self.eng_ops.items()})
        block = stack.enter_context(nc.Block())
        decos = {'pe': block.tensor, 'act': block.scalar, 'dve': block.vector,
                 'pool': block.gpsimd, 'sp': block.sync}
        for e in ENGS:
            ops = self.eng_ops[e]

            def body(eng, ops=ops, e=e):
                waited = {}
                for o in ops:
                    for key, val in o['waits'].items():
                        if waited.get(key, 0) >= val:
                            continue
                        waited[key] = val
                        eng.wait_ge(get_sem(key), val)
                    if o['fn'] is None:
                        continue
                    ins = o['fn'](eng)
                    if o['dma_key'] is not None:
                        n = o['dma_cnt']
                        ins.then_inc(get_sem('d_%s_%d' % (o['dma_key'], (n - 1) // DMA_LIM)), 16)
                    elif o['sig']:
                        v = o['sigval']
                        ins.then_inc(get_sem('c_%s_%d' % (e, (v - 1) // SEM_LIM)), 1)
            decos[e](body)


class Arena:
    def __init__(self, tensor, nbytes):
        self.t = tensor
        self.nbytes = nbytes
        self.off = 0

    def alloc(self, shape, dtype, parts=128):
        n = int(np.prod(shape))
        esz = 4 if dtype == F32 else 2
        nb = n * esz
        nb_al = (nb + 63) // 64 * 64
        assert self.off + nb_al <= self.nbytes, ("SBUF arena overflow", self.off, nb_al)
        ap = self.t[0:parts, self.off // 2:(self.off + nb) // 2]
        self.off += nb_al
        if dtype == F32:
            ap = ap.bitcast(F32)
        if len(shape) == 2:
            ap = ap.rearrange("p (a b) -> p a b", a=shape[0], b=shape[1])
        elif len(shape) == 3:
            ap = ap.rearrange("p (a b c) -> p a b c", a=shape[0], b=shape[1], c=shape[2])
        return ap


def build_program(debug=False, NPAIRS_DBG=8, stages=('ffn1', 'proj', 'rwkv', 'attn', 'merge', 'ffn2'), NB_DBG=None,
                  JS_DBG=range(4), NT_DBG=None):
    nc = bass.Bass("TRN2", target_bir_lowering=False)
    P = Prog(nc)

    def din(name, shape):
        return nc.dram_tensor(name, list(shape), F32, kind="ExternalInput").ap()
    x = din("x", [S, D])
    ffn_norm = [din("ffn1_norm", [D]), din("ffn2_norm", [D])]
    ffn_win = [din("ffn1_w_in", [D, 2 * DFF]), din("ffn2_w_in", [D, 2 * DFF])]
    ffn_wout = [din("ffn1_w_out", [DFF, D]), din("ffn2_w_out", [DFF, D])]
    out = nc.dram_tensor("out", [S, D], F32, kind="ExternalOutput").ap()
    x1_d = nc.dram_tensor("x1_scr", [S, D], F32, kind="ExternalOutput" if debug else "Internal").ap()

    stack = ExitStack()
    ARENA_BYTES = 207 * 1024
    arena_t = stack.enter_context(nc.sbuf_tensor("arena", [128, ARENA_BYTES // 2], BF16))
    A = Arena(arena_t, ARENA_BYTES)
    psum = stack.enter_context(nc.psum_tensor("psum", [128, 4096], F32))

    def bank(b, n=512, off=0):
        return psum[:, b * 512 + off:b * 512 + off + n]

    ident_f = A.alloc([128], F32)
    ident = A.alloc([128], BF16)
    ones_col = A.alloc([1], F32)

    P.op('pool', lambda e: e.memset(ident_f, 0.0), writes=['ident_f'])
    P.op('pool', lambda e: e.affine_select(out=ident_f, in_=ident_f, pattern=[[-1, 128]],
                                           compare_op=ALU.not_equal, fill=1.0, base=0, channel_multiplier=1),
         reads=['ident_f'], writes=['ident_f'])
    P.op('dve', lambda e: e.tensor_copy(out=ident, in_=ident_f), reads=['ident_f'], writes=['ident'])

    const_mark = A.off

    TT = 256
    NSUB = TT // 128
    NT = S // TT if NT_DBG is None else NT_DBG
    KC = D // 128
    FC = DFF // 128

    def ffn_stage(si, src, dst):
        A.off = const_mark
        TT = 512
        NSUB = TT // 128
        NT = (S // TT) if NT_DBG is None else NT_DBG
        W1 = A.alloc([KC, 2 * DFF], BF16)
        W2 = A.alloc([FC, D], BF16)
        gb = A.alloc([D], F32)
        xt = [A.alloc([NSUB, D], F32)] * 2
        xc = [A.alloc([512], F32) for _ in range(3)]
        hb = [A.alloc([D], BF16) for _ in range(2)]
        hT = [A.alloc([KC, TT], BF16) for _ in range(2)]
        actT = A.alloc([FC, TT], BF16)
        sg = [A.alloc([TT], F32) for _ in range(2)]
        ss = A.alloc([8], F32)
        pre = 's%d_' % si
        w1v = ffn_win[si].rearrange("(kc p) f -> p kc f", p=128)
        CH = 1408
        for kc in range(KC):
            for c in range(2 * DFF // CH):
                P.op('pool', lambda e, kc=kc, c=c: e.dma_start(out=W1[:, kc, c * CH:(c + 1) * CH],
                                                              in_=w1v[:, kc, c * CH:(c + 1) * CH]),
                     writes=[pre + 'W1'], dma_key=pre + 'W1')
        w2v = ffn_wout[si].rearrange("(fc p) d -> p fc d", p=128)
        for fc in range(FC):
            P.op('pool', lambda e, fc=fc: e.dma_start(out=W2[:, fc, :], in_=w2v[:, fc, :]),
                 writes=[pre + 'W2'], dma_key=pre + 'W2')
        P.op('sp', lambda e: e.dma_start(out=gb, in_=ffn_norm[si].partition_broadcast(128)),
             writes=[pre + 'gb'], dma_key=pre + 'gb')
        srcv = src.rearrange("(n s p) d -> n p s d", p=128, s=NSUB)
        dstv = dst.rearrange("(n s p) d -> n p s d", p=128, s=NSUB)
        for it in range(NT):
            sl = it % 2
            X = xt[sl]
            xr = pre + 'xt'
            if it == 0:
                P.op('sp', lambda e, X=X: e.dma_start(out=X, in_=srcv[0]), writes=[xr], dma_key=xr)
            HT = hT[sl]
            for s in range(NSUB):
                hs = (it * NSUB + s) % 2
                H = hb[hs]
                hr = pre + 'hb%d' % hs
                P.op('act', lambda e, X=X, s=s, H=H: e.activation(out=H, in_=X[:, s, :], func=AF.Square,
                                                             accum_out=ss[:, 0:1]),
                     reads=[xr], writes=[hr, pre + 'ss'])
                P.op('act', lambda e: e.activation(out=ss[:, 1:2], in_=ss[:, 0:1], func=AF.Sqrt,
                                                   scale=1.0 / D, bias=RMS_EPS),
                     reads=[pre + 'ss'], writes=[pre + 'ss1'])
                P.op('dve', lambda e: e.reciprocal(out=ss[:, 2:3], in_=ss[:, 1:2]),
                     reads=[pre + 'ss1'], writes=[pre + 'ss2'])
                P.op('dve', lambda e, X=X, s=s, H=H: e.scalar_tensor_tensor(
                    out=H, in0=X[:, s, :], scalar=ss[:, 2:3], in1=gb, op0=ALU.mult, op1=ALU.mult),
                    reads=[xr, pre + 'ss2', pre + 'gb'], writes=[hr])
                pT = bank(0).bitcast(BF16)
                for kc in range(KC):
                    P.op('pe', lambda e, kc=kc, H=H, pT=pT: e.transpose(
                        out=pT[:, kc * 128:(kc + 1) * 128], in_=H[:, kc * 128:(kc + 1) * 128], identity=ident),
                        reads=[hr, 'ident'], writes=['psT'])
                P.op('act', lambda e, HT=HT, s=s, pT=pT: e.copy(
                    out=HT[:, :, s * 128:(s + 1) * 128], in_=pT.rearrange("p (k t) -> p k t", k=KC)),
                    reads=['psT'], writes=[pre + 'hT%d' % sl])
            if it + 1 < NT:
                P.op('sp', lambda e, it=it, X=X: e.dma_start(out=X, in_=srcv[it + 1]), writes=[xr], dma_key=xr)
            for fc in range(FC):
                bg = 1 + 2 * (fc % 2)
                bu = bg + 1
                pgate = bank(bg)
                pup = bank(bu)
                for half, pdst, br in ((0, pgate, bg), (1, pup, bu)):
                    col = half * DFF + fc * 128
                    for kc in range(KC):
                        P.op('pe', lambda e, kc=kc, col=col, pdst=pdst, HT=HT: e.matmul(
                            pdst, lhsT=W1[:, kc, col:col + 128], rhs=HT[:, kc, :],
                            start=(kc == 0), stop=(kc == KC - 1)),
                            reads=[pre + 'W1', pre + 'hT%d' % sl], writes=['psG%d' % br])
                SG = sg[fc % 2]
                P.op('act', lambda e, pgate=pgate, SG=SG: e.activation(out=SG, in_=pgate, func=AF.Silu),
                     reads=['psG%d' % bg], writes=[pre + 'sg%d' % (fc % 2)])
                P.op('dve', lambda e, pup=pup, SG=SG, fc=fc: e.tensor_tensor(
                    out=actT[:, fc, :], in0=SG, in1=pup, op=ALU.mult),
                    reads=['psG%d' % bu, pre + 'sg%d' % (fc % 2)], writes=[pre + 'actT'])
            for s in range(NSUB):
                for dh in range(2):
                    b = 5 + (s * 2 + dh) % 2
                    pd = bank(b)
                    for fc in range(FC):
                        P.op('pe', lambda e, fc=fc, s=s, dh=dh, pd=pd: e.matmul(
                            pd, lhsT=actT[:, fc, s * 128:(s + 1) * 128], rhs=W2[:, fc, dh * 512:(dh + 1) * 512],
                            start=(fc == 0), stop=(fc == FC - 1)),
                            reads=[pre + 'actT', pre + 'W2'], writes=['psD%d' % b])
                    k = (it * NSUB * 2 + s * 2 + dh) % 3
                    XC = xc[k]
                    xcr = pre + 'xc%d' % k
                    P.op('sp', lambda e, it=it, s=s, dh=dh, XC=XC: e.dma_start(
                        out=XC, in_=srcv[it][:, s, dh * 512:(dh + 1) * 512]), writes=[xcr], dma_key=xcr)
                    P.op('dve', lambda e, XC=XC, pd=pd: e.scalar_tensor_tensor(
                        out=XC, in0=pd, scalar=0.5, in1=XC, op0=ALU.mult, op1=ALU.add),
                        reads=['psD%d' % b, xcr], writes=[xcr])
                    P.op('sp', lambda e, it=it, s=s, dh=dh, XC=XC: e.dma_start(
                        out=dstv[it][:, s, dh * 512:(dh + 1) * 512], in_=XC),
                        reads=[xcr], writes=[pre + 'dst'], dma_key=xcr)

    NCOL = 7712
    w_in = din("w_in", [D, NCOL])
    mix_norm = din("mix_norm", [D])
    rwkv_mu = din("rwkv_mu", [3360])
    b_gate = din("b_gate", [2048])
    qn = din("attn_q_norm", [64])
    kn = din("attn_k_norm", [64])
    kscr = "ExternalOutput" if debug else "Internal"
    if 'proj' not in stages:
        kscr = "ExternalInput"
    rkv_d = nc.dram_tensor("rkv_scr", [24, 128, S], F32, kind=kscr).ap()
    ta_d = nc.dram_tensor("ta_scr", [128, S], BF16, kind=kscr).ap()
    tg_d = nc.dram_tensor("tg_scr", [160, S], BF16, kind=kscr).ap()
    qk_d = nc.dram_tensor("qk_scr", [12, 128, S], BF16, kind=kscr).ap()
    v_d = nc.dram_tensor("v_scr", [S, 768], BF16, kind=kscr).ap()
    gate_d = nc.dram_tensor("gate_scr", [16, 128, S], BF16, kind=kscr).ap()

    def proj_stage():
        A.off = const_mark
        pre = 'p_'
        W = A.alloc([KC, NCOL], BF16)
        gb = A.alloc([D], F32)
        X = A.alloc([NSUB, D], F32)
        hb = [A.alloc([D], BF16) for _ in range(2)]
        hT = [A.alloc([KC, TT], BF16) for _ in range(2)]
        ss = A.alloc([8], F32)
        mu_t = A.alloc([27], F32)
        bg_t = A.alloc([16], F32)
        qg_t = A.alloc([2], F32)
        carry = A.alloc([27], F32)
        psb = [A.alloc([TT + 1], F32) for _ in range(2)]
        tmp = [A.alloc([TT], F32) for _ in range(2)]
        sq = [A.alloc([TT], BF16) for _ in range(2)]
        lnb = [A.alloc([TT], F32) for _ in range(2)]
        rkv_st = A.alloc([24, TT], F32)
        ta_st = A.alloc([TT], BF16)
        tg_st = A.alloc([2, TT], BF16)
        qk_st = A.alloc([12, TT], BF16)
        gate_st = A.alloc([16, TT], BF16)
        v_st = A.alloc([NSUB, 768], BF16)
        bones = A.alloc([128], BF16)
        print("proj_stage arena", A.off)
        wv = w_in.rearrange("(kc p) f -> p kc f", p=128)
        CH = 964
        for kc in range(KC):
            for c in range(NCOL // CH):
                P.op('pool', lambda e, kc=kc, c=c: e.dma_start(out=W[:, kc, c * CH:(c + 1) * CH],
                                                              in_=wv[:, kc, c * CH:(c + 1) * CH]),
                     writes=[pre + 'W'], dma_key=pre + 'W')
        P.op('sp', lambda e: e.dma_start(out=gb, in_=mix_norm.partition_broadcast(128)),
             writes=[pre + 'gb'], dma_key=pre + 'par')
        P.op('sp', lambda e: e.dma_start(out=mu_t[:, 0:26], in_=rwkv_mu[0:3328].rearrange("(b p) -> p b", p=128),
                                         allow_slow_non_contiguous=True), writes=[pre + 'mu'], dma_key=pre + 'par')
        P.op('sp', lambda e: e.dma_start(out=mu_t[0:32, 26:27], in_=rwkv_mu[3328:3360].rearrange("(p o) -> p o", o=1)),
             writes=[pre + 'mu'], dma_key=pre + 'par')
        P.op('sp', lambda e: e.dma_start(out=bg_t, in_=b_gate.rearrange("(b p) -> p b", p=128),
                                         allow_slow_non_contiguous=True), writes=[pre + 'bg'], dma_key=pre + 'par')
        for hh in range(2):
            P.op('sp', lambda e, hh=hh: e.dma_start(out=qg_t[hh * 64:(hh + 1) * 64, 0:1],
                                                     in_=qn.rearrange("(p o) -> p o", o=1)),
                 writes=[pre + 'qg'], dma_key=pre + 'par')
            P.op('sp', lambda e, hh=hh: e.dma_start(out=qg_t[hh * 64:(hh + 1) * 64, 1:2],
                                                     in_=kn.rearrange("(p o) -> p o", o=1)),
                 writes=[pre + 'qg'], dma_key=pre + 'par')
        P.op('pool', lambda e: e.tensor_scalar(out=qg_t[:, 0:1], in0=qg_t[:, 0:1], scalar1=0.125, scalar2=None,
                                               op0=ALU.mult), reads=[pre + 'qg'], writes=[pre + 'qg'])
        P.op('pool', lambda e: e.memset(carry, 0.0), writes=[pre + 'carry%d' % i for i in range(27)])
        P.op('pool', lambda e: e.memset(bones, 0.0), writes=[pre + 'bones'])
        P.op('pool', lambda e: e.memset(bones[0:64, 0:64], 1.0), reads=[pre + 'bones'], writes=[pre + 'bones'])
        P.op('pool', lambda e: e.memset(bones[64:128, 64:128], 1.0), reads=[pre + 'bones'], writes=[pre + 'bones'])

        blocks = []
        for b in range(24):
            blocks.append((b * 128, 128, 'rkv', b))
        blocks.append((3072, 128, 'ta', 24))
        blocks.append((3200, 128, 'tg0', 25))
        blocks.append((3328, 32, 'tg1', 26))
        for b in range(6):
            blocks.append((3360 + b * 128, 128, 'q', b))
        for b in range(6):
            blocks.append((4128 + b * 128, 128, 'k', 6 + b))
        for b in range(16):
            blocks.append((5664 + b * 128, 128, 'gate', b))

        srcv = x1_d.rearrange("(n s p) d -> n p s d", p=128, s=NSUB)
        xr = pre + 'X'
        for it in range(NT):
            t0 = it * TT
            sl = it % 2
            P.op('sp', lambda e, it=it: e.dma_start(out=X, in_=srcv[it]), writes=[xr], dma_key=xr)
            HT = hT[sl]
            htr = pre + 'hT%d' % sl
            for s in range(NSUB):
                hs = (it * NSUB + s) % 2
                H = hb[hs]
                hr = pre + 'hb%d' % hs
                P.op('act', lambda e, s=s, H=H: e.activation(out=H, in_=X[:, s, :], func=AF.Square,
                                                             accum_out=ss[:, 0:1]),
                     reads=[xr], writes=[hr, pre + 'ss'])
                P.op('act', lambda e: e.activation(out=ss[:, 1:2], in_=ss[:, 0:1], func=AF.Sqrt,
                                                   scale=1.0 / D, bias=RMS_EPS),
                     reads=[pre + 'ss'], writes=[pre + 'ss1'])
                P.op('dve', lambda e: e.reciprocal(out=ss[:, 2:3], in_=ss[:, 1:2]),
                     reads=[pre + 'ss1'], writes=[pre + 'ss2'])
                P.op('dve', lambda e, s=s, H=H: e.scalar_tensor_tensor(
                    out=H, in0=X[:, s, :], scalar=ss[:, 2:3], in1=gb, op0=ALU.mult, op1=ALU.mult),
                    reads=[xr, pre + 'ss2', pre + 'gb'], writes=[hr])
                pT = bank(0).bitcast(BF16)
                for kc in range(KC):
                    P.op('pe', lambda e, kc=kc, H=H, pT=pT: e.transpose(
                        out=pT[:, kc * 128:(kc + 1) * 128], in_=H[:, kc * 128:(kc + 1) * 128], identity=ident),
                        reads=[hr, 'ident'], writes=['psT'])
                P.op('act', lambda e, HT=HT, s=s, pT=pT: e.copy(
                    out=HT[:, :, s * 128:(s + 1) * 128], in_=pT.rearrange("p (k t) -> p k t", k=KC)),
                    reads=['psT'], writes=[htr])

            pending = [None]

            def flush():
                if pending[0] is None:
                    return
                pg, pgr, j, kind, idx = pending[0]
                pending[0] = None
                pss = bank(5)[:, 0:TT]
                P.op('pe', lambda e, j=j, pss=pss: e.matmul(pss, lhsT=bones, rhs=sq[j], start=True, stop=True),
                     reads=[pre + 'sq%d' % j, pre + 'bones'], writes=['pss'])
                P.op('act', lambda e, j=j, pss=pss: e.activation(out=lnb[j], in_=pss, func=AF.Ln,
                                                                 scale=1.0 / 64, bias=RMS_EPS),
                     reads=['pss'], writes=[pre + 'lnb%d' % j])
                P.op('act', lambda e, j=j: e.activation(out=lnb[j], in_=lnb[j], func=AF.Exp, scale=-0.5),
                     reads=[pre + 'lnb%d' % j], writes=[pre + 'lnb%d' % j])
                c = 0 if kind == 'q' else 1
                P.op('dve', lambda e, j=j, pg=pg, idx=idx, c=c: e.scalar_tensor_tensor(
                    out=qk_st[:, idx, :], in0=pg, scalar=qg_t[:, c:c + 1], in1=lnb[j], op0=ALU.mult, op1=ALU.mult),
                    reads=[pgr, pre + 'lnb%d' % j, pre + 'qg'], writes=[pre + 'qk_st'])

            for bi, (col0, M, kind, idx) in enumerate(blocks):
                b = 1 + bi % 4
                pgr = 'psG%d' % b
                pg = bank(b)[0:M, 0:TT]
                j = bi % 2
                for kc in range(KC):
                    P.op('pe', lambda e, kc=kc, col0=col0, M=M, pg=pg, HT=HT: e.matmul(
                        pg, lhsT=W[:, kc, col0:col0 + M], rhs=HT[:, kc, :], start=(kc == 0), stop=(kc == KC - 1)),
                        reads=[pre + 'W', htr], writes=[pgr])
                flush()
                if kind in ('rkv', 'ta', 'tg0', 'tg1'):
                    cr = pre + 'carry%d' % idx
                    pbr = pre + 'psb%d' % j
                    tr = pre + 'tmp%d' % j
                    PS = psb[j][0:M]
                    TM = tmp[j][0:M]
                    P.op('pool', lambda e, PS=PS, idx=idx, M=M: e.tensor_copy(out=PS[:, 0:1], in_=carry[0:M, idx:idx + 1]),
                         reads=[cr], writes=[pbr])
                    P.op('act', lambda e, PS=PS, pg=pg: e.copy(out=PS[:, 1:TT + 1], in_=pg),
                         reads=[pgr, pbr], writes=[pbr])
                    P.op('dve', lambda e, PS=PS, TM=TM: e.tensor_tensor(out=TM, in0=PS[:, 0:TT], in1=PS[:, 1:TT + 1],
                                                                        op=ALU.subtract),
                         reads=[pbr], writes=[tr])
                    P.op('pool', lambda e, PS=PS, idx=idx, M=M: e.tensor_copy(out=carry[0:M, idx:idx + 1],
                                                                              in_=PS[:, TT:TT + 1]),
                         reads=[pbr], writes=[cr])
                    if kind == 'rkv':
                        P.op('dve', lambda e, PS=PS, TM=TM, idx=idx: e.scalar_tensor_tensor(
                            out=rkv_st[:, idx, :], in0=TM, scalar=mu_t[:, idx:idx + 1], in1=PS[:, 1:TT + 1],
                            op0=ALU.mult, op1=ALU.add),
                            reads=[tr, pbr, pre + 'mu'], writes=[pre + 'rkv_st%d' % (idx // 8)])
                    else:
                        P.op('dve', lambda e, PS=PS, TM=TM, idx=idx, M=M: e.scalar_tensor_tensor(
                            out=TM, in0=TM, scalar=mu_t[0:M, idx:idx + 1], in1=PS[:, 1:TT + 1],
                            op0=ALU.mult, op1=ALU.add),
                            reads=[tr, pbr, pre + 'mu'], writes=[tr])
                        if kind == 'ta':
                            P.op('act', lambda e, TM=TM: e.activation(out=ta_st[0:64], in_=TM[0:64], func=AF.Tanh),
                                 reads=[tr], writes=[pre + 'ta_st'])
                            P.op('act', lambda e, TM=TM: e.copy(out=ta_st[64:128], in_=TM[64:128]),
                                 reads=[tr], writes=[pre + 'ta_st'])
                        elif kind == 'tg0':
                            P.op('act', lambda e, TM=TM: e.activation(out=tg_st[:, 0, :], in_=TM, func=AF.Sigmoid),
                                 reads=[tr], writes=[pre + 'tg_st'])
                        else:
                            P.op('act', lambda e, TM=TM: e.activation(out=tg_st[0:32, 1, :], in_=TM, func=AF.Sigmoid),
                                 reads=[tr], writes=[pre + 'tg_st'])
                elif kind in ('q', 'k'):
                    P.op('act', lambda e, pg=pg, j=j: e.activation(out=sq[j], in_=pg, func=AF.Square),
                         reads=[pgr], writes=[pre + 'sq%d' % j])
                    pending[0] = (pg, pgr, j, kind, idx)
                else:
                    P.op('act', lambda e, pg=pg, idx=idx: e.activation(out=gate_st[:, idx, :], in_=pg, func=AF.Sigmoid,
                                                                       bias=bg_t[:, idx:idx + 1]),
                         reads=[pgr, pre + 'bg'], writes=[pre + 'gate_st'])
            flush()
            for s in range(NSUB):
                for (c0, n, b) in ((4896, 512, 6), (5408, 256, 7)):
                    pv = bank(b)[:, 0:n]
                    for kc in range(KC):
                        P.op('pe', lambda e, kc=kc, s=s, c0=c0, n=n, pv=pv, HT=HT: e.matmul(
                            pv, lhsT=HT[:, kc, s * 128:(s + 1) * 128], rhs=W[:, kc, c0:c0 + n],
                            start=(kc == 0), stop=(kc == KC - 1)),
                            reads=[pre + 'W', htr], writes=['psV%d' % b])
                P.op('act', lambda e, s=s: e.copy(out=v_st[:, s, 0:512], in_=bank(6)),
                     reads=['psV6'], writes=[pre + 'v_st'])
                P.op('dve', lambda e, s=s: e.tensor_copy(out=v_st[:, s, 512:768], in_=bank(7)[:, 0:256]),
                     reads=['psV7'], writes=[pre + 'v_st'])
            rv = rkv_d.rearrange("b p t -> p b t")
            for g in range(3):
                P.op('sp', lambda e, g=g, t0=t0: e.dma_start(out=rv[:, g * 8:(g + 1) * 8, t0:t0 + TT],
                                                             in_=rkv_st[:, g * 8:(g + 1) * 8, :]),
                     reads=[pre + 'rkv_st%d' % g], writes=['d_rkv'], dma_key=pre + 'rkv_st%d' % g)
            P.op('sp', lambda e, t0=t0: e.dma_start(out=ta_d[:, t0:t0 + TT], in_=ta_st),
                 reads=[pre + 'ta_st'], writes=['d_ta'], dma_key=pre + 'ta_st')
            P.op('sp', lambda e, t0=t0: e.dma_start(out=tg_d[0:128, t0:t0 + TT], in_=tg_st[:, 0, :]),
                 reads=[pre + 'tg_st'], writes=['d_tg'], dma_key=pre + 'tg_st')
            P.op('sp', lambda e, t0=t0: e.dma_start(out=tg_d[128:160, t0:t0 + TT], in_=tg_st[0:32, 1, :]),
                 reads=[pre + 'tg_st'], writes=['d_tg'], dma_key=pre + 'tg_st')
            P.op('sp', lambda e, t0=t0: e.dma_start(out=qk_d.rearrange("b p t -> p b t")[:, :, t0:t0 + TT], in_=qk_st),
                 reads=[pre + 'qk_st'], writes=['d_qk'], dma_key=pre + 'qk_st')
            P.op('sp', lambda e, t0=t0: e.dma_start(out=gate_d.rearrange("b p t -> p b t")[:, :, t0:t0 + TT],
                                                    in_=gate_st),
                 reads=[pre + 'gate_st'], writes=['d_gate'], dma_key=pre + 'gate_st')
            P.op('sp', lambda e, t0=t0: e.dma_start(
                out=v_d[t0:t0 + TT, :].rearrange("(s p) c -> p s c", p=128), in_=v_st),
                reads=[pre + 'v_st'], writes=['d_v'], dma_key=pre + 'v_st')

    w2_d = din("rwkv_w2", [64, 1024])
    a2_d = din("rwkv_a2", [64, 1024])
    g2_d = din("rwkv_g2", [160, 1024])
    prm_names = ['rwkv_w0', 'rwkv_a0', 'rwkv_k_k', 'rwkv_k_a', 'rwkv_r_k', 'rwkv_ln_w', 'rwkv_ln_b']
    prm_d = [din(n, [1024]) for n in prm_names]
    ya_d = nc.dram_tensor("ya_scr", [8, 128, S], BF16, kind="ExternalOutput" if debug else "Internal").ap()
    TB = 1024
    NCH = TB // 128
    NB = S // TB if NB_DBG is None else NB_DBG
    C0 = float(np.exp(-0.5))
    GN_EPS = 64e-5

    def rwkv_stage(pairs=range(8)):
        A.off = const_mark
        pre = 'r_'
        WA = A.alloc([1024], BF16)
        G2a = A.alloc([1024], BF16)
        G2b = A.alloc([1024], BF16)
        prm = A.alloc([7, 8], F32)
        bones = A.alloc([128], BF16)
        mk4 = A.alloc([512], F32)
        mkL = A.alloc([2, 128], F32)
        E2 = A.alloc([64], F32)
        mrow = A.alloc([TB], F32)
        f32names = ['R', 'K', 'V', 'SG', 'AA', 'GG', 'KK', 'KM', 'T1', 'BVEC', 'CS', 'T2', 'T3', 'EP', 'EN', 'EPM',
                    'EC', 'BV', 'Y32', 'DD']
        T = {n: A.alloc([TB], F32) for n in f32names}
        bfnames = ['TA', 'TG0', 'TG1', 'TQ', 'BT', 'KT', 'BH', 'KH', 'VT', 'YB', 'YO']
        for n in bfnames:
            T[n] = A.alloc([TB], BF16)
        AR = A.alloc([NCH, 2, 128], BF16)
        TM4 = A.alloc([NCH, 4, 128], BF16)
        PC = A.alloc([NCH], F32)
        SC = [[A.alloc([512], BF16) for _ in range(2)] for _ in range(2)]
        LZ = [[A.alloc([2, 384], BF16) for _ in range(2)] for _ in range(2)]
        MCz = [A.alloc([2, 64], BF16) for _ in range(2)]
        QT = [A.alloc([128], BF16) for _ in range(2)]
        STz = A.alloc([2, 64], BF16)
        DBT = ['BT', 'KT', 'GG', 'BV', 'Y32']
        T2 = {n: [T[n], A.alloc([TB], BF16 if n in ('BT', 'KT') else F32)] for n in DBT}
        AR2 = [AR, A.alloc([NCH, 2, 128], BF16)]
        TM42 = [TM4, A.alloc([NCH, 4, 128], BF16)]
        PC2 = [PC, A.alloc([NCH], F32)]
        par = [0]
        coll = [None]

        class TP(dict):
            def __getitem__(self, k):
                if k in T2:
                    return T2[k][par[0]]
                return dict.__getitem__(self, k)

        class BP(object):
            def __init__(self, bufs):
                self.bufs = bufs

            def __getitem__(self, idx):
                return self.bufs[par[0]][idx]

            def rearrange(self, *a_, **k_):
                return self.bufs[par[0]].rearrange(*a_, **k_)
        T = TP(T)
        AR = BP(AR2)
        TM4 = BP(TM42)
        PC = BP(PC2)
        DBNAMES = set(DBT) | {'AR0', 'AR1', 'PC'} | {'TM4_%d' % c_ for c_ in range(NCH)}
        print("rwkv_stage arena", A.off)

        def R_(n):
            return pre + n
        P.op('pool', lambda e: e.dma_start(out=WA[0:64, :], in_=w2_d), writes=[R_('WA')], dma_key=R_('w'))
        P.op('pool', lambda e: e.dma_start(out=WA[64:128, :], in_=a2_d), writes=[R_('WA')], dma_key=R_('w'))
        P.op('pool', lambda e: e.dma_start(out=G2a, in_=g2_d[0:128, :]), writes=[R_('G2')], dma_key=R_('w'))
        P.op('pool', lambda e: e.dma_start(out=G2b[0:32, :], in_=g2_d[128:160, :]), writes=[R_('G2')], dma_key=R_('w'))
        for i in range(7):
            P.op('sp', lambda e, i=i: e.dma_start(out=prm[:, i, :], in_=prm_d[i].rearrange("(b p) -> p b", p=128),
                                                   allow_slow_non_contiguous=True),
                 writes=[R_('prm')], dma_key=R_('par'))
        P.op('pool', lambda e: e.memset(bones, 0.0), writes=[R_('bones')])
        P.op('pool', lambda e: e.memset(bones[0:64, 0:64], 1.0), reads=[R_('bones')], writes=[R_('bones')])
        P.op('pool', lambda e: e.memset(bones[64:128, 64:128], 1.0), reads=[R_('bones')], writes=[R_('bones')])
        P.op('pool', lambda e: e.memset(mk4, 1.0), writes=[R_('mk4')])
        for q in range(4):
            base = -1 if q % 2 == 0 else 0
            P.op('pool', lambda e, q=q, base=base: e.affine_select(
                out=mk4[:, q * 128:(q + 1) * 128], in_=mk4[:, q * 128:(q + 1) * 128], pattern=[[1, 128]],
                compare_op=ALU.is_ge, fill=0.0, base=base, channel_multiplier=-1),
                reads=[R_('mk4')], writes=[R_('mk4')])
        P.op('pool', lambda e: e.memset(mkL, 1.0), writes=[R_('mkL')])
        P.op('pool', lambda e: e.affine_select(out=mkL, in_=mkL, pattern=[[0, 2], [-1, 128]], compare_op=ALU.is_ge,
                                               fill=0.0, base=-1, channel_multiplier=1),
             reads=[R_('mkL')], writes=[R_('mkL')])
        P.op('pool', lambda e: e.tensor_copy(out=E2[0:64, :], in_=ident_f[0:64, 0:64]), reads=['ident_f'], writes=[R_('E2')])
        P.op('pool', lambda e: e.tensor_copy(out=E2[64:128, :], in_=ident_f[64:128, 64:128]), reads=['ident_f'],
             writes=[R_('E2')])
        P.op('pool', lambda e: e.memset(mrow, 1.0), writes=[R_('mrow')])
        P.op('pool', lambda e: e.memset(mrow.rearrange("p (c t) -> p c t", t=128)[:, :, 0:1], 0.0),
             reads=[R_('mrow')], writes=[R_('mrow')])

        def ch3(ap):
            return ap.rearrange("p (c t) -> p c t", t=128)

        def nm(x):
            if x.startswith('pb') or x in ('ident', 'ident_f'):
                return x
            if x in DBNAMES:
                return R_(x) + '@%d' % par[0]
            return R_(x)

        def pdma(eng, fn, reads=(), writes=(), dma_key=None):
            p_ = par[0]

            def fn2(e, fn=fn, p_=p_):
                par[0] = p_
                return fn(e)
            args = (eng, fn2, list(reads), list(writes), dma_key)
            if coll[0] is not None:
                coll[0].append(args)
            else:
                P.op(args[0], args[1], reads=args[2], writes=args[3], dma_key=args[4])

        def ew(eng, fn, reads, writes):
            pdma(eng, fn, [nm(x) for x in reads], [nm(x) for x in writes])

        for sl_ in range(2):
            ew('pool', lambda e, sl_=sl_: e.memset(MCz[sl_], 0.0), [], ['MC%d' % sl_])
        funcs = {}
        for hp in pairs:
            cols = slice(hp * 128, (hp + 1) * 128)
            def prep(tb, hp=hp, cols=cols):
                t0 = tb * TB
                tsl = slice(t0, t0 + TB)
                for i, n in enumerate(['R', 'K', 'V']):
                    pdma('sp', lambda e, i=i, n=n, hp=hp, tsl=tsl: e.dma_start(out=T[n], in_=rkv_d[i * 8 + hp, :, tsl]),
                         writes=[R_(n)], dma_key=R_('ld' + n))
                pdma('sp', lambda e, tsl=tsl: e.dma_start(out=T['TA'], in_=ta_d[:, tsl]), writes=[R_('TA')],
                     dma_key=R_('ldTA'))
                pdma('sp', lambda e, tsl=tsl: e.dma_start(out=T['TG0'], in_=tg_d[0:128, tsl]), writes=[R_('TG0')],
                     dma_key=R_('ldTG0'))
                pdma('sp', lambda e, tsl=tsl: e.dma_start(out=T['TG1'][0:32], in_=tg_d[128:160, tsl]),
                     writes=[R_('TG1')], dma_key=R_('ldTG1'))
                for hf in range(2):
                    hs = slice(hf * 512, (hf + 1) * 512)
                    ew('pe', lambda e, hs=hs, cols=cols: e.matmul(bank(1), lhsT=WA[0:64, cols], rhs=T['TA'][0:64, hs],
                                                                  start=True, stop=True), ['WA', 'TA'], ['pb1'])
                    ew('act', lambda e, hs=hs, hp=hp: e.activation(out=T['SG'][:, hs], in_=bank(1), func=AF.Sigmoid,
                                                                   bias=prm[:, 0, hp:hp + 1]), ['pb1', 'prm'], ['SG'])
                    ew('pe', lambda e, hs=hs, cols=cols: e.matmul(bank(2), lhsT=WA[64:128, cols], rhs=T['TA'][64:128, hs],
                                                                  start=True, stop=True), ['WA', 'TA'], ['pb2'])
                    ew('act', lambda e, hs=hs, hp=hp: e.activation(out=T['AA'][:, hs], in_=bank(2), func=AF.Sigmoid,
                                                                   bias=prm[:, 1, hp:hp + 1]), ['pb2', 'prm'], ['AA'])
                    ew('pe', lambda e, hs=hs, cols=cols: e.matmul(bank(3), lhsT=G2a[:, cols], rhs=T['TG0'][:, hs],
                                                                  start=True, stop=False), ['G2', 'TG0'], ['pb3', 'pb3'])
                    ew('pe', lambda e, hs=hs, cols=cols: e.matmul(bank(3), lhsT=G2b[0:32, cols], rhs=T['TG1'][0:32, hs],
                                                                  start=False, stop=True), ['G2', 'TG1'], ['pb3', 'pb3'])
                    ew('act', lambda e, hs=hs: e.copy(out=T['GG'][:, hs], in_=bank(3)), ['pb3', 'pb3'], ['GG'])
                ew('dve', lambda e, hp=hp: e.tensor_scalar(out=T['KK'], in0=T['K'], scalar1=prm[:, 2, hp:hp + 1],
                                                           scalar2=None, op0=ALU.mult), ['K', 'prm'], ['KK'])
                ew('act', lambda e: e.activation(out=T['TQ'], in_=T['KK'], func=AF.Square), ['KK'], ['TQ'])
                for hf in range(2):
                    hs = slice(hf * 512, (hf + 1) * 512)
                    ew('pe', lambda e, hs=hs: e.matmul(bank(4), lhsT=bones, rhs=T['TQ'][:, hs], start=True, stop=True),
                       ['bones', 'TQ'], ['pb4'])
                    ew('dve', lambda e, hs=hs: e.tensor_scalar(out=T['T1'][:, hs], in0=bank(4), scalar1=1e-19,
                                                               scalar2=None, op0=ALU.max), ['pb4'], ['T1'])
                ew('act', lambda e: e.activation(out=T['T1'], in_=T['T1'], func=AF.Ln), ['T1'], ['T1'])
                ew('act', lambda e: e.activation(out=T['T1'], in_=T['T1'], func=AF.Exp, scale=-0.5), ['T1'], ['T1'])
                ew('dve', lambda e: e.tensor_tensor(out=T['KK'], in0=T['KK'], in1=T['T1'], op=ALU.mult),
                   ['KK', 'T1'], ['KK'])
                ew('dve', lambda e, hp=hp: e.tensor_scalar(out=T['T1'], in0=T['AA'], scalar1=-1.0,
                                                           scalar2=prm[:, 3, hp:hp + 1], op0=ALU.add, op1=ALU.mult),
                   ['AA', 'prm'], ['T1'])
                ew('dve', lambda e: e.scalar_tensor_tensor(out=T['KM'], in0=T['T1'], scalar=1.0, in1=T['K'],
                                                           op0=ALU.add, op1=ALU.mult), ['T1', 'K'], ['KM'])
                ew('pool', lambda e: e.tensor_tensor(out=T['T1'], in0=T['R'], in1=T['KM'], op=ALU.mult),
                   ['R', 'KM'], ['T1'])
                ew('pool', lambda e, hp=hp: e.tensor_scalar(out=T['TQ'], in0=T['T1'], scalar1=prm[:, 4, hp:hp + 1],
                                                            scalar2=None, op0=ALU.mult), ['T1', 'prm'], ['TQ'])
                for hf in range(2):
                    hs = slice(hf * 512, (hf + 1) * 512)
                    ew('pe', lambda e, hs=hs: e.matmul(bank(5), lhsT=bones, rhs=T['TQ'][:, hs], start=True, stop=True),
                       ['bones', 'TQ'], ['pb5'])
                    ew('dve', lambda e, hs=hs: e.tensor_tensor(out=T['BV'][:, hs], in0=T['V'][:, hs], in1=bank(5),
                                                               op=ALU.mult), ['pb5', 'V'], ['BV'])
                ew('pool', lambda e: e.tensor_tensor(out=T['BVEC'], in0=T['KK'], in1=T['AA'], op=ALU.mult),
                   ['KK', 'AA'], ['BVEC'])
                ew('dve', lambda e: e.tensor_tensor_scan(out=T['CS'], data0=mrow, data1=T['SG'], initial=0.0,
                                                         op0=ALU.mult, op1=ALU.add), ['mrow', 'SG'], ['CS'])
                ew('pool', lambda e: e.tensor_tensor(out=T['T2'], in0=T['CS'], in1=T['SG'], op=ALU.subtract),
                   ['CS', 'SG'], ['T2'])
                ew('act', lambda e: e.activation(out=T['EP'], in_=T['CS'], func=AF.Exp, scale=-C0), ['CS'], ['EP'])
                ew('act', lambda e: e.activation(out=T['EN'], in_=T['CS'], func=AF.Exp, scale=C0), ['CS'], ['EN'])
                ew('act', lambda e: e.activation(out=T['EPM'], in_=T['T2'], func=AF.Exp, scale=-C0), ['T2'], ['EPM'])
                ew('dve', lambda e: e.tensor_tensor(
                    out=ch3(T['T3']), in0=ch3(T['CS']), in1=ch3(T['CS'])[:, :, 127:128].to_broadcast([128, NCH, 128]),
                    op=ALU.subtract), ['CS'], ['T3'])
                ew('act', lambda e: e.activation(out=T['EC'], in_=T['T3'], func=AF.Exp, scale=C0), ['T3'], ['EC'])
                ew('act', lambda e: e.activation(out=PC.rearrange("p (c o) -> p c o", o=1),
                                                 in_=ch3(T['CS'])[:, :, 127:128], func=AF.Exp, scale=-C0),
                   ['CS'], ['PC'])
                ew('dve', lambda e: e.scalar_tensor_tensor(out=AR[:, :, 0, :], in0=ch3(T['EPM']), scalar=-1.0,
                                                           in1=ch3(T['KK']), op0=ALU.mult, op1=ALU.mult),
                   ['EPM', 'KK'], ['AR0'])
                ew('pool', lambda e: e.tensor_tensor(out=AR[:, :, 1, :], in0=ch3(T['EP']), in1=ch3(T['R']), op=ALU.mult),
                   ['EP', 'R'], ['AR1'])
                ew('dve', lambda e: e.tensor_tensor(out=T['BT'], in0=T['EN'], in1=T['BVEC'], op=ALU.mult),
                   ['EN', 'BVEC'], ['BT'])
                ew('pool', lambda e: e.tensor_tensor(out=T['KT'], in0=T['EN'], in1=T['KM'], op=ALU.mult),
                   ['EN', 'KM'], ['KT'])
                ew('dve', lambda e: e.tensor_tensor(out=T['BH'], in0=T['EC'], in1=T['BVEC'], op=ALU.mult),
                   ['EC', 'BVEC'], ['BH'])
                ew('pool', lambda e: e.tensor_tensor(out=T['KH'], in0=T['EC'], in1=T['KM'], op=ALU.mult),
                   ['EC', 'KM'], ['KH'])
                ew('act', lambda e: e.copy(out=T['VT'], in_=T['V']), ['V'], ['VT'])
                pT = bank(0).bitcast(BF16)
                for c in range(NCH):
                    cs_ = slice(c * 128, (c + 1) * 128)
                    srcs = [(AR[:, c, 0, :], 'AR0'), (T['VT'][:, cs_], 'VT'), (T['BH'][:, cs_], 'BH'),
                            (T['KH'][:, cs_], 'KH')]
                    for q, (sap, sr) in enumerate(srcs):
                        ew('pe', lambda e, q=q, sap=sap: e.transpose(out=pT[:, q * 128:(q + 1) * 128], in_=sap,
                                                                     identity=ident), [sr, 'ident'], ['pb0'])
                    ew('act', lambda e, c=c: e.copy(out=TM4[:, c, :, :], in_=pT[:, 0:512].rearrange("p (q t) -> p q t", q=4)),
                       ['pb0'], ['TM4_%d' % c])

            def chunks(tb, inter):
                P1 = (1, 2)
                P2 = ((4, 5), (6, 7))

                def partA(c, sl):
                    cs_ = slice(c * 128, (c + 1) * 128)
                    tm = 'TM4_%d' % c
                    arc = AR[:, c, :, :].rearrange("p a t -> p (a t)")
                    b1 = P1[sl]
                    ps1 = bank(b1)
                    for h2 in range(2):
                        psl = slice(64 * h2, 64 * h2 + 64)
                        scr = 'SC%d_%d' % (sl, h2)
                        ew('pe', lambda e, ps1=ps1, psl=psl, cs_=cs_, arc=arc: e.matmul(
                            ps1[:, 0:256], lhsT=T['BT'][psl, cs_], rhs=arc[psl, :], start=True, stop=True),
                            ['BT', 'AR0', 'AR1'], ['pb%d' % b1])
                        ew('pe', lambda e, ps1=ps1, psl=psl, cs_=cs_, arc=arc: e.matmul(
                            ps1[:, 256:512], lhsT=T['KT'][psl, cs_], rhs=arc[psl, :], start=True, stop=True),
                            ['KT', 'AR0', 'AR1'], ['pb%d' % b1])
                        ew('dve', lambda e, ps1=ps1, h2=h2, sl=sl: e.tensor_tensor(out=SC[sl][h2], in0=mk4, in1=ps1,
                                                                                   op=ALU.mult),
                           ['pb%d' % b1, 'mk4'], [scr])
                        b2 = P2[sl][h2]
                        ew('pe', lambda e, b2=b2, psl=psl, cs_=cs_, c=c: e.matmul(
                            bank(b2)[:, 384:512], lhsT=AR[psl, c, 0, :], rhs=T['BT'][psl, cs_],
                            start=True, stop=True), ['AR0', 'BT'], ['pb%d' % b2])
                    for h2 in range(2):
                        b2 = P2[sl][h2]
                        ew('dve', lambda e, h2=h2, b2=b2, sl=sl: e.tensor_tensor(
                            out=LZ[sl][0][:, h2, 0:128], in0=mkL[:, h2, :], in1=bank(b2)[:, 384:512], op=ALU.mult),
                            ['pb%d' % b2, 'mkL'], ['LL%d_0_%d' % (sl, h2)])
                    for h2 in range(2):
                        pb = 64 * h2
                        ew('pe', lambda e, h2=h2, pb=pb, c=c, sl=sl: e.matmul(
                            bank(3)[:, 256 + h2 * 64:256 + (h2 + 1) * 64], lhsT=SC[sl][h2][:, 256:384],
                            rhs=TM4[:, c, 1, pb:pb + 64], start=True, stop=True), ['SC%d_%d' % (sl, h2), tm], ['pb3'])
                    ew('pool', lambda e, c=c, sl=sl: e.tensor_copy(
                        out=LZ[sl][0][:, :, 128:192], in_=TM4[:, c, 0, :].rearrange("p (h k) -> p h k", h=2)),
                        [tm], ['ZZ%d_0_0' % sl, 'ZZ%d_0_1' % sl])
                    for h2 in range(2):
                        ew('act', lambda e, h2=h2, sl=sl: e.copy(out=LZ[sl][0][:, h2, 192:256],
                                                                 in_=bank(3)[:, 256 + h2 * 64:256 + (h2 + 1) * 64]),
                           ['pb3'], ['ZZ%d_0_%d' % (sl, h2)])

                def partB(c, sl, n):
                    pp = n % 2
                    for h2 in range(2):
                        b2 = P2[sl][h2]
                        ps2 = bank(b2)
                        pbr = 'pb%d' % b2
                        ltn = SC[sl][h2][:, 0:128] if n == 0 else LZ[sl][pp][:, h2, 256:384]
                        rds = ['LL%d_%d_%d' % (sl, pp, h2), 'ZZ%d_%d_%d' % (sl, pp, h2)] + \
                            (['SC%d_%d' % (sl, h2)] if n == 0 else [])
                        if n < 6:
                            ew('pe', lambda e, ps2=ps2, ltn=ltn, pp=pp, h2=h2, sl=sl: e.matmul(
                                ps2[:, 0:256], lhsT=ltn, rhs=LZ[sl][pp][:, h2, 0:256], start=True, stop=True),
                                rds, [pbr])
                            ew('pe', lambda e, ps2=ps2, ltn=ltn, pp=pp, h2=h2, sl=sl: e.matmul(
                                ps2[:, 256:384], lhsT=LZ[sl][pp][:, h2, 0:128], rhs=ltn, start=True, stop=True),
                                rds, [pbr])
                        else:
                            ew('pe', lambda e, ps2=ps2, ltn=ltn, pp=pp, h2=h2, sl=sl: e.matmul(
                                ps2[:, 128:256], lhsT=ltn, rhs=LZ[sl][pp][:, h2, 128:256], start=True, stop=True),
                                rds, [pbr])

                    def cp(h2):
                        b2 = P2[sl][h2]
                        ew('act', lambda e, h2=h2, b2=b2: e.copy(
                            out=LZ[sl][1 - pp][:, h2, :].rearrange("p (s t) -> p s t", s=3)[:, 0:3:2, :],
                            in_=bank(b2)[:, 0:384].rearrange("p (s t) -> p s t", s=3)[:, 0:3:2, :]),
                            ['pb%d' % b2], ['LL%d_%d_%d' % (sl, 1 - pp, h2)])

                    def ad(h2):
                        b2 = P2[sl][h2]
                        ew('dve', lambda e, h2=h2, b2=b2: e.tensor_tensor(
                            out=LZ[sl][1 - pp][:, h2, 128:256], in0=LZ[sl][pp][:, h2, 128:256],
                            in1=bank(b2)[:, 128:256], op=ALU.add),
                            ['pb%d' % b2, 'ZZ%d_%d_%d' % (sl, pp, h2)], ['ZZ%d_%d_%d' % (sl, 1 - pp, h2)])
                    if n < 6:
                        cp(0)
                        ad(1)
                        cp(1)
                        ad(0)
                    else:
                        ad(0)
                        ad(1)

                def partC(c, sl):
                    tm = 'TM4_%d' % c
                    ZF = LZ[sl][1]
                    p3 = bank(0)[:, 192:384]
                    for h2 in range(2):
                        pb = 64 * h2
                        psl = slice(pb, pb + 64)
                        zr = 'ZZ%d_1_%d' % (sl, h2)
                        ew('pe', lambda e, h2=h2, pb=pb, psl=psl, c=c: e.matmul(
                            p3[psl, 0:64], lhsT=ZF[:, h2, 128:192], rhs=TM4[:, c, 2, pb:pb + 64],
                            start=True, stop=True, tile_position=(0, pb)), [zr, tm], ['pb0'])
                        ew('pe', lambda e, h2=h2, pb=pb, psl=psl: e.matmul(
                            p3[psl, 64:192], lhsT=ZF[:, h2, 128:192], rhs=SC[sl][h2][:, 128:256],
                            start=True, stop=True, tile_position=(0, pb)), [zr, 'SC%d_%d' % (sl, h2)], ['pb0'])
                    for h2 in range(2):
                        psl = slice(64 * h2, 64 * h2 + 64)
                        ew('dve', lambda e, c=c, psl=psl, h2=h2: e.scalar_tensor_tensor(
                            out=MCz[sl][psl, h2, :], in0=E2[psl, :], scalar=PC[psl, c:c + 1], in1=p3[psl, 0:64],
                            op0=ALU.mult, op1=ALU.add), ['E2', 'PC', 'pb0'], ['MC%d' % sl])
                    ew('dve', lambda e, c=c: e.tensor_tensor(out=QT[sl], in0=AR[:, c, 1, :], in1=p3[:, 64:192],
                                                             op=ALU.add), ['pb0', 'AR1'], ['QT%d' % sl])

                def partD(c, sl):
                    cs_ = slice(c * 128, (c + 1) * 128)
                    tm = 'TM4_%d' % c
                    ZF = LZ[sl][1]
                    for h2 in range(2):
                        pb = 64 * h2
                        psl = slice(pb, pb + 64)
                        sb_ = 3 if h2 == 0 else 0
                        sr_ = 'pb%d' % sb_
                        psY = bank(sb_)[:, 0:128]
                        psS = bank(sb_)[:, 128:192]
                        UU = ZF[:, h2, 192:256]
                        zr = 'ZZ%d_1_%d' % (sl, h2)
                        scr = 'SC%d_%d' % (sl, h2)
                        ew('pe', lambda e, psl=psl, pb=pb, h2=h2, UU=UU, psY=psY: e.matmul(
                            psY[psl, :], lhsT=UU, rhs=SC[sl][h2][:, 128:256], start=True, stop=False,
                            tile_position=(0, pb)), [zr, scr], [sr_])
                        ew('pe', lambda e, psl=psl, pb=pb, h2=h2, c=c, psY=psY: e.matmul(
                            psY[psl, :], lhsT=TM4[:, c, 1, pb:pb + 64], rhs=SC[sl][h2][:, 384:512], start=False,
                            stop=False, tile_position=(0, pb)), [tm, scr], [sr_])
                        ew('pe', lambda e, psl=psl, pb=pb, psY=psY, h2=h2: e.matmul(
                            psY[psl, :], lhsT=STz[:, h2, :], rhs=QT[sl], start=False, stop=True,
                            tile_position=(0, pb)), ['ST', 'QT%d' % sl], [sr_])
                        ew('pe', lambda e, psl=psl, pb=pb, psS=psS, h2=h2: e.matmul(
                            psS[psl, :], lhsT=MCz[sl][:, h2, :], rhs=STz[:, h2, :], start=True, stop=False,
                            tile_position=(0, pb)), ['MC%d' % sl, 'ST'], [sr_])
                        ew('pe', lambda e, psl=psl, pb=pb, c=c, UU=UU, psS=psS: e.matmul(
                            psS[psl, :], lhsT=TM4[:, c, 2, pb:pb + 64], rhs=UU, start=False, stop=False,
                            tile_position=(0, pb)), [tm, zr], [sr_])
                        ew('pe', lambda e, psl=psl, pb=pb, c=c, psS=psS: e.matmul(
                            psS[psl, :], lhsT=TM4[:, c, 3, pb:pb + 64], rhs=TM4[:, c, 1, pb:pb + 64], start=False,
                            stop=True, tile_position=(0, pb)), [tm], [sr_])
                    for h2 in range(2):
                        pb = 64 * h2
                        psl = slice(pb, pb + 64)
                        sb_ = 3 if h2 == 0 else 0
                        sr_ = 'pb%d' % sb_
                        ew('act', lambda e, cs_=cs_, psl=psl, sb_=sb_: e.copy(out=T['Y32'][psl, cs_],
                                                                             in_=bank(sb_)[psl, 0:128]),
                           [sr_], ['Y32'])
                        ew('dve', lambda e, psl=psl, sb_=sb_, h2=h2: e.tensor_copy(out=STz[psl, h2, :],
                                                                                   in_=bank(sb_)[psl, 128:192]),
                           [sr_], ['ST'])

                for c0 in range(0, NCH, 2):
                    for sl in range(2):
                        partA(c0 + sl, sl)
                    for n in range(7):
                        for sl in range(2):
                            partB(c0 + sl, sl, n)
                        if n in (1, 3, 5):
                            inter()
                    for sl in range(2):
                        partC(c0 + sl, sl)
                    for sl in range(2):
                        partD(c0 + sl, sl)
                    inter()

            def outst(tb, hp=hp):
                t0 = tb * TB
                tsl = slice(t0, t0 + TB)
                ew('pool', lambda e: e.tensor_copy(out=T['YB'], in_=T['Y32']), ['Y32'], ['YB'])
                for hf in range(2):
                    hs = slice(hf * 512, (hf + 1) * 512)
                    ew('pe', lambda e, hs=hs: e.matmul(bank(1), lhsT=bones, rhs=T['YB'][:, hs], start=True, stop=True),
                       ['bones', 'YB'], ['pb1'])
                    ew('dve', lambda e, hs=hs: e.scalar_tensor_tensor(out=T['DD'][:, hs], in0=bank(1), scalar=-1.0 / 64,
                                                                      in1=T['Y32'][:, hs], op0=ALU.mult, op1=ALU.add),
                       ['pb1', 'Y32'], ['DD'])
                ew('act', lambda e: e.activation(out=T['TQ'], in_=T['DD'], func=AF.Square), ['DD'], ['TQ'])
                for hf in range(2):
                    hs = slice(hf * 512, (hf + 1) * 512)
                    ew('pe', lambda e, hs=hs: e.matmul(bank(2), lhsT=bones, rhs=T['TQ'][:, hs], start=True, stop=True),
                       ['bones', 'TQ'], ['pb2'])
                    ew('act', lambda e, hs=hs: e.activation(out=T['T1'][:, hs], in_=bank(2), func=AF.Ln, scale=1.0 / 64,
                                                            bias=GN_EPS), ['pb2'], ['T1'])
                ew('act', lambda e: e.activation(out=T['T1'], in_=T['T1'], func=AF.Exp, scale=-0.5), ['T1'], ['T1'])
                ew('dve', lambda e: e.tensor_tensor(out=T['DD'], in0=T['DD'], in1=T['T1'], op=ALU.mult),
                   ['DD', 'T1'], ['DD'])
                ew('dve', lambda e, hp=hp: e.tensor_scalar(out=T['DD'], in0=T['DD'], scalar1=prm[:, 5, hp:hp + 1],
                                                           scalar2=prm[:, 6, hp:hp + 1], op0=ALU.mult, op1=ALU.add),
                   ['DD', 'prm'], ['DD'])
                ew('pool', lambda e: e.tensor_tensor(out=T['DD'], in0=T['DD'], in1=T['BV'], op=ALU.add),
                   ['DD', 'BV'], ['DD'])
                ew('dve', lambda e: e.tensor_tensor(out=T['YO'], in0=T['DD'], in1=T['GG'], op=ALU.mult),
                   ['DD', 'GG'], ['YO'])
                pdma('sp', lambda e, hp=hp, tsl=tsl: e.dma_start(out=ya_d[hp, :, tsl], in_=T['YO']),
                     reads=[R_('YO')], writes=['d_ya'], dma_key=R_('stYO'))


            funcs[hp] = (prep, chunks, outst)

        def emit_all(lst):
            for a_ in lst:
                P.op(a_[0], a_[1], reads=a_[2], writes=a_[3], dma_key=a_[4])
        blocks = [(hp_, tb_) for hp_ in pairs for tb_ in range(NB)]
        par[0] = 0
        funcs[blocks[0][0]][0](blocks[0][1])
        pend_out = []
        for bi, (hp_, tb_) in enumerate(blocks):
            prep_f, chunks_f, outst_f = funcs[hp_]
            pend = list(pend_out)
            if bi + 1 < len(blocks):
                nhp, ntb = blocks[bi + 1]
                par[0] = (bi + 1) % 2
                coll[0] = pend
                funcs[nhp][0](ntb)
                coll[0] = None
            par[0] = bi % 2
            if tb_ == 0:
                ew('pool', lambda e: e.memset(STz, 0.0), [], ['ST'])
            nsl = 4 * (NCH // 2)
            step = (len(pend) + nsl - 1) // nsl if pend else 0
            pos = [0]

            def inter(pend=pend, step=step, pos=pos):
                open_banks = set()
                cnt = 0
                while pos[0] < len(pend) and (cnt < step or open_banks):
                    a_ = pend[pos[0]]
                    pos[0] += 1
                    cnt += 1
                    if a_[0] == 'pe':
                        open_banks.update(w_ for w_ in a_[3] if w_.startswith('pb'))
                    else:
                        open_banks.difference_update(r_ for r_ in a_[2] if r_.startswith('pb'))
                    P.op(a_[0], a_[1], reads=a_[2], writes=a_[3], dma_key=a_[4])
            chunks_f(tb_, inter)
            emit_all(pend[pos[0]:])
            par[0] = bi % 2
            pend_out = []
            coll[0] = pend_out
            outst_f(tb_)
            coll[0] = None
        emit_all(pend_out)

    yb_d = nc.dram_tensor("yb_scr", [6, 128, S], BF16, kind="ExternalOutput" if debug else "Internal").ap()
    DIL = (1, 4, 16)

    def attn_stage_full(js=range(4)):
        A.off = const_mark
        pre = 'a_'

        def R_(n):
            return pre + n

        def nm(x):
            if x.startswith('pb') or x in ('ident', 'ident_f'):
                return x
            return R_(x)

        def ew(eng, fn, reads, writes):
            P.op(eng, fn, reads=[nm(x) for x in reads], writes=[nm(x) for x in writes])
        QH = A.alloc([S], BF16)
        KH = A.alloc([S], BF16)
        VX = A.alloc([32, 64], BF16)
        ONES = A.alloc([64], BF16)
        OT = [A.alloc([S], F32) for _ in range(3)]
        DEN = [A.alloc([S], F32) for _ in range(3)]
        PT = [A.alloc([256], BF16) for _ in range(4)]
        mask2 = A.alloc([256], BF16)
        RD = [A.alloc([512], F32) for _ in range(2)]
        YBS = [A.alloc([512], BF16) for _ in range(2)]
        print("attn_stage arena", A.off)
        P.op('pool', lambda e: e.memset(mask2, 1.0), writes=[R_('mask')])
        P.op('pool', lambda e: e.affine_select(out=mask2[:, 0:128], in_=mask2[:, 0:128], pattern=[[1, 128]],
                                               compare_op=ALU.is_ge, fill=0.0, base=0, channel_multiplier=-1),
             reads=[R_('mask')], writes=[R_('mask')])
        P.op('pool', lambda e: e.affine_select(out=mask2[:, 128:256], in_=mask2[:, 128:256], pattern=[[-1, 128]],
                                               compare_op=ALU.is_ge, fill=0.0, base=0, channel_multiplier=1),
             reads=[R_('mask')], writes=[R_('mask')])
        P.op('pool', lambda e: e.memset(ONES, 1.0), writes=[R_('ONES')])

        tcount = [0]
        ccount = [0]
        for j in js:
            for g in range(3):
                d = DIL[g]
                nb = S // d // 128
                h = 4 * g + j
                pair = h // 2
                pb = 64 * (h % 2)
                vv = v_d.rearrange("(m d) c -> d m c", d=d)
                for r in range(d):
                    for n0 in range(0, nb, 8):
                        n1 = min(nb, n0 + 8)
                        P.op('sp', lambda e, r=r, h=h, nb=nb, vv=vv, n0=n0, n1=n1: e.dma_start(
                            out=VX[:, r * nb + n0:r * nb + n1, :],
                            in_=vv[r, n0 * 128:n1 * 128, h * 64:(h + 1) * 64].rearrange("(n i) c -> i n c", i=128)),
                            writes=[R_('VX')], dma_key=R_('ldV'))
                P.op('sp', lambda e, pair=pair, pb=pb: e.dma_start(out=QH[0:64, :], in_=qk_d[pair, pb:pb + 64, :]),
                     writes=[R_('QH')], dma_key=R_('ldQ'))
                P.op('sp', lambda e, pair=pair, pb=pb: e.dma_start(out=KH[0:64, :], in_=qk_d[6 + pair, pb:pb + 64, :]),
                     writes=[R_('KH')], dma_key=R_('ldK'))
                qv = QH.rearrange("p (m d) -> p d m", d=d)
                kv = KH.rearrange("p (m d) -> p d m", d=d)
                otr = 'OT%d' % g
                otv = OT[g].rearrange("p (m d) -> p d m", d=d)
                dnv = DEN[g].rearrange("p (m d) -> p d m", d=d)
                tiles = []
                for r in range(d):
                    for n in range(nb):
                        tiles.append((r, n, tcount[0]))
                        tcount[0] += 1

                def emit_score(r, n, ti, kv=kv, qv=qv, nb=nb):
                    nq = 256 if n + 1 < nb else 128
                    sbk = 1 + ti % 2
                    ps = bank(sbk)[:, 0:nq]
                    pt = PT[ti % 4]
                    ptr = 'PT%d' % (ti % 4)
                    ew('pe', lambda e, ps=ps, r=r, n=n, nq=nq: e.matmul(
                        ps, lhsT=kv[0:64, r, 128 * n:128 * n + 128], rhs=qv[0:64, r, 128 * n:128 * n + nq],
                        start=True, stop=True), ['KH', 'QH'], ['pb%d' % sbk])
                    ew('act', lambda e, ps=ps, pt=pt, nq=nq: e.activation(out=pt[:, 0:nq], in_=ps, func=AF.Exp),
                       ['pb%d' % sbk], [ptr])
                    ew('pool', lambda e, pt=pt, nq=nq: e.tensor_tensor(out=pt[:, 0:nq], in0=pt[:, 0:nq],
                                                                      in1=mask2[:, 0:nq], op=ALU.mult),
                       [ptr, 'mask'], [ptr])

                def emit_pv(r, n, ti, otv=otv, dnv=dnv, nb=nb, g=g, otr=otr):
                    pt = PT[ti % 4]
                    ptr = 'PT%d' % (ti % 4)
                    obk = 3 + ti % 2
                    po = bank(obk)[0:64, 0:128]
                    pdn = bank(obk)[0:64, 128:256]
                    vt = r * nb + n
                    has_prev = n > 0
                    if has_prev:
                        ppt = PT[(ti - 1) % 4]
                        pptr = 'PT%d' % ((ti - 1) % 4)
                        ew('pe', lambda e, ppt=ppt, vt=vt: e.matmul(
                            po, lhsT=VX[:, vt - 1, :], rhs=ppt[:, 128:256], start=True, stop=False),
                            ['VX', pptr], ['pb%d' % obk])
                    ew('pe', lambda e, pt=pt, vt=vt: e.matmul(
                        po, lhsT=VX[:, vt, :], rhs=pt[:, 0:128], start=(not has_prev), stop=True),
                        ['VX', ptr], ['pb%d' % obk])
                    if has_prev:
                        ew('pe', lambda e, ppt=ppt: e.matmul(
                            pdn, lhsT=ONES, rhs=ppt[:, 128:256], start=True, stop=False),
                            ['ONES', pptr], ['pb%d' % obk])
                    ew('pe', lambda e, pt=pt: e.matmul(
                        pdn, lhsT=ONES, rhs=pt[:, 0:128], start=(not has_prev), stop=True),
                        ['ONES', ptr], ['pb%d' % obk])
                    ew('dve', lambda e, r=r, n=n: e.tensor_copy(
                        out=otv[0:64, r, 128 * n:128 * n + 128], in_=po), ['pb%d' % obk], [otr])
                    ew('dve', lambda e, r=r, n=n: e.tensor_copy(
                        out=dnv[0:64, r, 128 * n:128 * n + 128], in_=pdn), ['pb%d' % obk], ['DEN%d' % g])

                for idx, (r, n, ti) in enumerate(tiles):
                    emit_score(r, n, ti)
                    if idx >= 1:
                        emit_pv(*tiles[idx - 1])
                emit_pv(*tiles[-1])
            for ck in range(S // 512):
                csl = slice(ck * 512, (ck + 1) * 512)
                cc = ccount[0]
                ccount[0] += 1
                rd = RD[cc % 2]
                ew('pool', lambda e, rd=rd, csl=csl: e.tensor_tensor(out=rd[0:64, :], in0=DEN[0][0:64, csl],
                                                                     in1=DEN[1][0:64, csl], op=ALU.add),
                   ['DEN0', 'DEN1'], ['RD%d' % (cc % 2)])
                ew('pool', lambda e, rd=rd, csl=csl: e.tensor_tensor(out=rd[0:64, :], in0=rd[0:64, :],
                                                                     in1=DEN[2][0:64, csl], op=ALU.add),
                   ['RD%d' % (cc % 2), 'DEN2'], ['RD%d' % (cc % 2)])
                ew('dve', lambda e, rd=rd: e.reciprocal(out=rd[0:64, :], in_=rd[0:64, :]), ['RD%d' % (cc % 2)],
                   ['RD%d' % (cc % 2)])
                for g in range(3):
                    h = 4 * g + j
                    pair = h // 2
                    pb = 64 * (h % 2)
                    yi = (cc * 3 + g) % 2
                    ys = YBS[yi]
                    ew('pool', lambda e, g=g, csl=csl, rd=rd, ys=ys: e.tensor_tensor(
                        out=ys[0:64, :], in0=OT[g][0:64, csl], in1=rd[0:64, :], op=ALU.mult),
                        ['OT%d' % g, 'RD%d' % (cc % 2)], ['YBS%d' % yi])
                    P.op('sp', lambda e, pair=pair, pb=pb, csl=csl, ys=ys: e.dma_start(
                        out=yb_d[pair, pb:pb + 64, csl], in_=ys[0:64, :]),
                        reads=[R_('YBS%d' % yi)], writes=['d_yb'], dma_key=R_('stYB%d' % yi))

    wpr_d = din("w_proj_rwkv", [1024, 1024])
    wpa_d = din("w_proj_attn", [768, 1024])
    wo_d = din("w_out", [1024, 1024])
    x2_d = nc.dram_tensor("x2_scr", [S, D], F32, kind="ExternalOutput" if debug else "Internal").ap()

    def merge_stage():
        A.off = const_mark
        pre = 'm_'

        def R_(n):
            return pre + n

        def nm(x):
            if x.startswith('pb'):
                return x
            return R_(x)

        def ew(eng, fn, reads, writes):
            P.op(eng, fn, reads=[nm(x) for x in reads], writes=[nm(x) for x in writes])
        Wr = A.alloc([8, 1024], BF16)
        Wa = A.alloc([6, 1024], BF16)
        Wo = A.alloc([8, 1024], BF16)
        X = [A.alloc([NSUB, D], F32) for _ in range(2)]
        YA = [A.alloc([8, TT], BF16) for _ in range(2)]
        YB = [A.alloc([6, TT], BF16) for _ in range(2)]
        G = [A.alloc([16, TT], BF16) for _ in range(2)]
        MT = A.alloc([8, TT], BF16)
        t1 = [A.alloc([TT], F32) for _ in range(2)]
        t2 = [A.alloc([TT], F32) for _ in range(2)]
        print("merge_stage arena", A.off)
        for kc in range(8):
            P.op('pool', lambda e, kc=kc: e.dma_start(out=Wr[:, kc, :], in_=wpr_d[kc * 128:(kc + 1) * 128, :]),
                 writes=[R_('Wr')], dma_key=R_('w'))
            P.op('pool', lambda e, kc=kc: e.dma_start(out=Wo[:, kc, :], in_=wo_d[kc * 128:(kc + 1) * 128, :]),
                 writes=[R_('Wo')], dma_key=R_('w'))
        for kc in range(6):
            P.op('pool', lambda e, kc=kc: e.dma_start(out=Wa[:, kc, :], in_=wpa_d[kc * 128:(kc + 1) * 128, :]),
                 writes=[R_('Wa')], dma_key=R_('w'))
        srcv = x1_d.rearrange("(n s p) d -> n p s d", p=128, s=NSUB)
        dstv = x2_d.rearrange("(n s p) d -> n p s d", p=128, s=NSUB)
        for it in range(NT):
            sl = it % 2
            tsl = slice(it * TT, (it + 1) * TT)
            sfx = '%d' % sl
            P.op('sp', lambda e, it=it, sl=sl: e.dma_start(out=X[sl], in_=srcv[it]), writes=[R_('X' + sfx)],
                 dma_key=R_('ldX' + sfx))
            P.op('sp', lambda e, sl=sl, tsl=tsl: e.dma_start(out=YA[sl], in_=ya_d.rearrange("b p t -> p b t")[:, :, tsl]),
                 writes=[R_('YA' + sfx)], dma_key=R_('ldYA' + sfx))
            P.op('sp', lambda e, sl=sl, tsl=tsl: e.dma_start(out=YB[sl], in_=yb_d.rearrange("b p t -> p b t")[:, :, tsl]),
                 writes=[R_('YB' + sfx)], dma_key=R_('ldYB' + sfx))
            P.op('sp', lambda e, sl=sl, tsl=tsl: e.dma_start(out=G[sl], in_=gate_d.rearrange("b p t -> p b t")[:, :, tsl]),
                 writes=[R_('G' + sfx)], dma_key=R_('ldG' + sfx))
            for c in range(8):
                bk = 1 + c % 2
                pg = bank(bk)
                for kc in range(8):
                    ew('pe', lambda e, c=c, kc=kc, pg=pg, sl=sl: e.matmul(
                        pg[:, 0:TT], lhsT=Wr[:, kc, c * 128:(c + 1) * 128], rhs=YA[sl][:, kc, :],
                        start=(kc == 0), stop=(kc == 7)), ['Wr', 'YA' + sfx], ['pb%d' % bk])
                for kc in range(6):
                    ew('pe', lambda e, c=c, kc=kc, pg=pg, sl=sl: e.matmul(
                        pg[:, TT:2 * TT], lhsT=Wa[:, kc, c * 128:(c + 1) * 128], rhs=YB[sl][:, kc, :],
                        start=(kc == 0), stop=(kc == 5)), ['Wa', 'YB' + sfx], ['pb%d' % bk])
                q = c % 2
                ew('dve', lambda e, c=c, pg=pg, sl=sl, q=q: e.tensor_tensor(out=t1[q], in0=G[sl][:, c, :],
                                                                           in1=pg[:, 0:TT], op=ALU.mult),
                   ['G' + sfx, 'pb%d' % bk], ['t1%d' % q])
                ew('dve', lambda e, c=c, pg=pg, sl=sl, q=q: e.tensor_tensor(out=t2[q], in0=G[sl][:, 8 + c, :],
                                                                           in1=pg[:, TT:2 * TT], op=ALU.mult),
                   ['G' + sfx, 'pb%d' % bk], ['t2%d' % q])
                ew('pool', lambda e, c=c, q=q: e.tensor_tensor(out=MT[:, c, :], in0=t1[q], in1=t2[q], op=ALU.add),
                   ['t1%d' % q, 't2%d' % q], ['MT'])
            for s in range(NSUB):
                for dh in range(2):
                    bk = 3 + (s * 2 + dh) % 2
                    pd = bank(bk)
                    for c in range(8):
                        ew('pe', lambda e, c=c, s=s, dh=dh, pd=pd: e.matmul(
                            pd, lhsT=MT[:, c, s * 128:(s + 1) * 128], rhs=Wo[:, c, dh * 512:(dh + 1) * 512],
                            start=(c == 0), stop=(c == 7)), ['MT', 'Wo'], ['pb%d' % bk])
                    ew('dve', lambda e, s=s, dh=dh, pd=pd, sl=sl: e.tensor_tensor(
                        out=X[sl][:, s, dh * 512:(dh + 1) * 512], in0=X[sl][:, s, dh * 512:(dh + 1) * 512], in1=pd,
                        op=ALU.add), ['pb%d' % bk, 'X' + sfx], ['X' + sfx])
            P.op('sp', lambda e, it=it, sl=sl: e.dma_start(out=dstv[it], in_=X[sl]),
                 reads=[R_('X' + sfx)], writes=['d_x2'], dma_key=R_('stX' + sfx))

    if 'ffn1' in stages:
        ffn_stage(0, x, x1_d)
        P.sync_all()
    if 'proj' in stages:
        proj_stage()
        P.sync_all()
    if 'rwkv' in stages:
        rwkv_stage(range(NPAIRS_DBG))
        P.sync_all()
    if 'attn' in stages:
        attn_stage_full(JS_DBG)
        P.sync_all()
    if 'merge' in stages:
        merge_stage()
        P.sync_all()
    if 'ffn2' in stages:
        ffn_stage(1, x2_d if 'merge' in stages else x1_d, out)
    P.sync_all()
    P.op('sp', None)
    P.finalize_and_emit(stack)
    stack.close()
    return nc


_CACHE = {}


SHARED_KEYS = ['ffn1_norm', 'ffn1_w_in', 'ffn1_w_out', 'ffn2_norm', 'ffn2_w_in', 'ffn2_w_out',
               'w_in', 'mix_norm', 'rwkv_mu', 'b_gate', 'attn_q_norm', 'attn_k_norm',
               'w_proj_rwkv', 'w_proj_attn', 'w_out', 'rwkv_w2', 'rwkv_a2', 'rwkv_g2', 'rwkv_w0', 'rwkv_a0', 'rwkv_k_k', 'rwkv_k_a', 'rwkv_r_k', 'rwkv_ln_w', 'rwkv_ln_b']


def make_shared(inputs):
    shared = {}
    for k in SHARED_KEYS:
        v = np.asarray(inputs[k], dtype=np.float32)
        v = v.reshape(v.shape[1:])
        if k == 'rwkv_r_k':
            v = v.reshape(-1)
        shared[k] = np.ascontiguousarray(v)
    return shared


def kernel(**inputs):
    if 'nc' not in _CACHE:
        _CACHE['nc'] = build_program()
    nc = _CACHE['nc']
    x = np.ascontiguousarray(inputs['x'], dtype=np.float32)
    shared = make_shared(inputs)
    in_maps = []
    for c in range(NCORES):
        m = dict(shared)
        m['x'] = x[c]
        in_maps.append(m)
    res = run_bass_kernel_spmd(nc, in_maps, core_ids=list(range(NCORES)))
    return np.stack([np.asarray(r['out']) for r in res.results], axis=0)
```

```python
import numpy as np
from contextlib import ExitStack
import concourse.bass as bass
import concourse.mybir as mybir
from concourse.bass_utils import run_bass_kernel_spmd
from concourse.alu_op_type import AluOpType as ALU

F32 = mybir.dt.float32
BF16 = mybir.dt.bfloat16
AF = mybir.ActivationFunctionType
AX = mybir.AxisListType

S = 4096
D = 1024
DFF = 2816
NCORES = 8
RMS_EPS = 1e-6

ENGS = ['pe', 'act', 'dve', 'pool', 'sp']
MAXOPS = [0]
SEM_LIM = 30000
DMA_LIM = 1800


class Prog:
    def __init__(self, nc):
        self.nc = nc
        self.ops = []
        self.eng_ops = {e: [] for e in ENGS}
        self.last_w = {}
        self.readers = {}
        self.dma_cnt = {}
        self.barrier = {e: None for e in ENGS}

    def op(self, eng, fn, reads=(), writes=(), dma_key=None):
        mo = MAXOPS[0]
        if mo and len(self.ops) >= mo and fn is not None:
            return None
        if mo and len(self.ops) == mo - 1 and fn is not None:
            print("LAST OP:", eng, fn.__code__.co_firstlineno, reads, writes)
        oid = len(self.ops)
        deps = set()
        dma_deps = {}
        writes = list(writes) + [r for r in reads if (r.startswith('pb') or r.startswith('ps')) and r not in writes]

        def add(o):
            od = self.ops[o]
            if od['dma_key'] is not None:
                k = od['dma_key']
                dma_deps[k] = self.dma_cnt[k]
            else:
                deps.add(o)
        for r in reads:
            if r in self.last_w:
                add(self.last_w[r])
        for w in writes:
            if w in self.last_w:
                add(self.last_w[w])
            for rd in self.readers.get(w, {}).values():
                add(rd)
        if self.barrier[eng] is not None:
            bd, bdma = self.barrier[eng]
            for o in bd:
                deps.add(o)
            for k, v in bdma.items():
                dma_deps[k] = max(dma_deps.get(k, 0), v)
            self.barrier[eng] = None
        cnt = None
        if dma_key is not None:
            self.dma_cnt[dma_key] = self.dma_cnt.get(dma_key, 0) + 1
            cnt = self.dma_cnt[dma_key]
        o = dict(id=oid, eng=eng, fn=fn, deps=deps, dma_deps=dma_deps, dma_key=dma_key,
                 dma_cnt=cnt, idx=len(self.eng_ops[eng]), sig=False)
        self.ops.append(o)
        self.eng_ops[eng].append(o)
        ch = eng if dma_key is None else 'dma:' + dma_key
        for r in reads:
            self.readers.setdefault(r, {})[ch] = oid
        for w in writes:
            self.last_w[w] = oid
            self.readers[w] = {}
        return oid

    def sync_all(self):
        bd = set()
        for e in ENGS:
            for o in reversed(self.eng_ops[e]):
                if o['dma_key'] is None and o['fn'] is not None:
                    bd.add(o['id'])
                    break
        bdma = dict(self.dma_cnt)
        for e in ENGS:
            self.barrier[e] = (set(bd), dict(bdma))

    def finalize_and_emit(self, stack):
        nc = self.nc
        for o in self.ops:
            per = {}
            for d in o['deps']:
                od = self.ops[d]
                if od['eng'] == 'pe' and o['eng'] == 'pe':
                    continue
                e = od['eng']
                if e not in per or self.ops[per[e]]['idx'] < od['idx']:
                    per[e] = d
            o['cdeps'] = per
            for d in per.values():
                self.ops[d]['sig'] = True
        sems = {}

        def get_sem(name):
            return sems[name]
        for e in ENGS:
            c = 0
            for o in self.eng_ops[e]:
                if o['dma_key'] is None and o['sig']:
                    c += 1
                    o['sigval'] = c
        for o in self.ops:
            waits = {}
            for e, d in o['cdeps'].items():
                v = self.ops[d]['sigval']
                key = ('c_%s_%d' % (e, (v - 1) // SEM_LIM))
                val = (v - 1) % SEM_LIM + 1
                waits[key] = max(waits.get(key, 0), val)
            for k, n in o['dma_deps'].items():
                key = ('d_%s_%d' % (k, (n - 1) // DMA_LIM))
                val = 16 * ((n - 1) % DMA_LIM + 1)
                waits[key] = max(waits.get(key, 0), val)
            o['waits'] = waits
        names = set()
        for o in self.ops:
            names.update(o['waits'].keys())
            if o['dma_key'] is not None:
                names.add('d_%s_%d' % (o['dma_key'], (o['dma_cnt'] - 1) // DMA_LIM))
            elif o['sig']:
                names.add('c_%s_%d' % (o['eng'], (o['sigval'] - 1) // SEM_LIM))
        for nm in sorted(names):
            sems[nm] = stack.enter_context(nc.semaphore(nm))
        print("n_sems", len(names), "n_ops", len(self.ops), {e: len(v) for e, v in self.eng_ops.items()})
        block = stack.enter_context(nc.Block())
        decos = {'pe': block.tensor, 'act': block.scalar, 'dve': block.vector,
                 'pool': block.gpsimd, 'sp': block.sync}
        for e in ENGS:
            ops = self.eng_ops[e]

            def body(eng, ops=ops, e=e):
                waited = {}
                for o in ops:
                    for key, val in o['waits'].items():
                        if waited.get(key, 0) >= val:
                            continue
                        waited[key] = val
                        eng.wait_ge(get_sem(key), val)
                    if o['fn'] is None:
                        continue
                    ins = o['fn'](eng)
                    if o['dma_key'] is not None:
                        n = o['dma_cnt']
                        ins.then_inc(get_sem('d_%s_%d' % (o['dma_key'], (n - 1) // DMA_LIM)), 16)
                    elif o['sig']:
                        v = o['sigval']
                        ins.then_inc(get_sem('c_%s_%d' % (e, (v - 1) // SEM_LIM)), 1)
            decos[e](body)


class Arena:
    def __init__(self, tensor, nbytes):
        self.t = tensor
        self.nbytes = nbytes
        self.off = 0

    def alloc(self, shape, dtype, parts=128):
        n = int(np.prod(shape))
        esz = 4 if dtype == F32 else 2
        nb = n * esz
        nb_al = (nb + 63) // 64 * 64
        assert self.off + nb_al <= self.nbytes, ("SBUF arena overflow", self.off, nb_al)
        ap = self.t[0:parts, self.off // 2:(self.off + nb) // 2]
        self.off += nb_al
        if dtype == F32:
            ap = ap.bitcast(F32)
        if len(shape) == 2:
            ap = ap.rearrange("p (a b) -> p a b", a=shape[0], b=shape[1])
        elif len(shape) == 3:
            ap = ap.rearrange("p (a b c) -> p a b c", a=shape[0], b=shape[1], c=shape[2])
        return ap


def build_program(debug=False, NPAIRS_DBG=8, stages=('ffn1', 'proj', 'rwkv', 'attn', 'merge', 'ffn2'), NB_DBG=None,
                  JS_DBG=range(4), NT_DBG=None):
    nc = bass.Bass("TRN2", target_bir_lowering=False)
    P = Prog(nc)

    def din(name, shape):
        return nc.dram_tensor(name, list(shape), F32, kind="ExternalInput").ap()
    x = din("x", [S, D])
    ffn_norm = [din("ffn1_norm", [D]), din("ffn2_norm", [D])]
    ffn_win = [din("ffn1_w_in", [D, 2 * DFF]), din("ffn2_w_in", [D, 2 * DFF])]
    ffn_wout = [din("ffn1_w_out", [DFF, D]), din("ffn2_w_out", [DFF, D])]
    out = nc.dram_tensor("out", [S, D], F32, kind="ExternalOutput").ap()
    x1_d = nc.dram_tensor("x1_scr", [S, D], F32, kind="ExternalOutput" if debug else "Internal").ap()

    stack = ExitStack()
    ARENA_BYTES = 207 * 1024
    arena_t = stack.enter_context(nc.sbuf_tensor("arena", [128, ARENA_BYTES // 2], BF16))
    A = Arena(arena_t, ARENA_BYTES)
    psum = stack.enter_context(nc.psum_tensor("psum", [128, 4096], F32))

    def bank(b, n=512, off=0):
        return psum[:, b * 512 + off:b * 512 + off + n]

    ident_f = A.alloc([128], F32)
    ident = A.alloc([128], BF16)
    ones_col = A.alloc([1], F32)

    P.op('pool', lambda e: e.memset(ident_f, 0.0), writes=['ident_f'])
    P.op('pool', lambda e: e.affine_select(out=ident_f, in_=ident_f, pattern=[[-1, 128]],
                                           compare_op=ALU.not_equal, fill=1.0, base=0, channel_multiplier=1),
         reads=['ident_f'], writes=['ident_f'])
    P.op('dve', lambda e: e.tensor_copy(out=ident, in_=ident_f), reads=['ident_f'], writes=['ident'])

    const_mark = A.off

    TT = 256
    NSUB = TT // 128
    NT = S // TT if NT_DBG is None else NT_DBG
    KC = D // 128
    FC = DFF // 128

    def ffn_stage(si, src, dst):
        A.off = const_mark
        TT = 512
        NSUB = TT // 128
        NT = (S // TT) if NT_DBG is None else NT_DBG
        W1 = A.alloc([KC, 2 * DFF], BF16)
        W2 = A.alloc([FC, D], BF16)
        gb = A.alloc([D], F32)
        xt = [A.alloc([NSUB, D], F32)] * 2
        xc = [A.alloc([512], F32) for _ in range(3)]
        hb = [A.alloc([D], BF16) for _ in range(2)]
        hT = [A.alloc([KC, TT], BF16) for _ in range(2)]
        actT = A.alloc([FC, TT], BF16)
        sg = [A.alloc([TT], F32) for _ in range(2)]
        ss = A.alloc([8], F32)
        pre = 's%d_' % si
        w1v = ffn_win[si].rearrange("(kc p) f -> p kc f", p=128)
        CH = 1408
        for kc in range(KC):
            for c in range(2 * DFF // CH):
                P.op('pool', lambda e, kc=kc, c=c: e.dma_start(out=W1[:, kc, c * CH:(c + 1) * CH],
                                                              in_=w1v[:, kc, c * CH:(c + 1) * CH]),
                     writes=[pre + 'W1'], dma_key=pre + 'W1')
        w2v = ffn_wout[si].rearrange("(fc p) d -> p fc d", p=128)
        for fc in range(FC):
            P.op('pool', lambda e, fc=fc: e.dma_start(out=W2[:, fc, :], in_=w2v[:, fc, :]),
                 writes=[pre + 'W2'], dma_key=pre + 'W2')
        P.op('sp', lambda e: e.dma_start(out=gb, in_=ffn_norm[si].partition_broadcast(128)),
             writes=[pre + 'gb'], dma_key=pre + 'gb')
        srcv = src.rearrange("(n s p) d -> n p s d", p=128, s=NSUB)
        dstv = dst.rearrange("(n s p) d -> n p s d", p=128, s=NSUB)
        for it in range(NT):
            sl = it % 2
            X = xt[sl]
            xr = pre + 'xt'
            if it == 0:
                P.op('sp', lambda e, X=X: e.dma_start(out=X, in_=srcv[0]), writes=[xr], dma_key=xr)
            HT = hT[sl]
            for s in range(NSUB):
                hs = (it * NSUB + s) % 2
                H = hb[hs]
                hr = pre + 'hb%d' % hs
                P.op('act', lambda e, X=X, s=s, H=H: e.activation(out=H, in_=X[:, s, :], func=AF.Square,
                                                             accum_out=ss[:, 0:1]),
                     reads=[xr], writes=[hr, pre + 'ss'])
                P.op('act', lambda e: e.activation(out=ss[:, 1:2], in_=ss[:, 0:1], func=AF.Sqrt,
                                                   scale=1.0 / D, bias=RMS_EPS),
                     reads=[pre + 'ss'], writes=[pre + 'ss1'])
                P.op('dve', lambda e: e.reciprocal(out=ss[:, 2:3], in_=ss[:, 1:2]),
                     reads=[pre + 'ss1'], writes=[pre + 'ss2'])
                P.op('dve', lambda e, X=X, s=s, H=H: e.scalar_tensor_tensor(
                    out=H, in0=X[:, s, :], scalar=ss[:, 2:3], in1=gb, op0=ALU.mult, op1=ALU.mult),
                    reads=[xr, pre + 'ss2', pre + 'gb'], writes=[hr])
                pT = bank(0).bitcast(BF16)
                for kc in range(KC):
                    P.op('pe', lambda e, kc=kc, H=H, pT=pT: e.transpose(
                        out=pT[:, kc * 128:(kc + 1) * 128], in_=H[:, kc * 128:(kc + 1) * 128], identity=ident),
                        reads=[hr, 'ident'], writes=['psT'])
                P.op('act', lambda e, HT=HT, s=s, pT=pT: e.copy(
                    out=HT[:, :, s * 128:(s + 1) * 128], in_=pT.rearrange("p (k t) -> p k t", k=KC)),
                    reads=['psT'], writes=[pre + 'hT%d' % sl])
            if it + 1 < NT:
                P.op('sp', lambda e, it=it, X=X: e.dma_start(out=X, in_=srcv[it + 1]), writes=[xr], dma_key=xr)
            for fc in range(FC):
                bg = 1 + 2 * (fc % 2)
                bu = bg + 1
                pgate = bank(bg)
                pup = bank(bu)
                for half, pdst, br in ((0, pgate, bg), (1, pup, bu)):
                    col = half * DFF + fc * 128
                    for kc in range(KC):
                        P.op('pe', lambda e, kc=kc, col=col, pdst=pdst, HT=HT: e.matmul(
                            pdst, lhsT=W1[:, kc, col:col + 128], rhs=HT[:, kc, :],
                            start=(kc == 0), stop=(kc == KC - 1)),
                            reads=[pre + 'W1', pre + 'hT%d' % sl], writes=['psG%d' % br])
                SG = sg[fc % 2]
                P.op('act', lambda e, pgate=pgate, SG=SG: e.activation(out=SG, in_=pgate, func=AF.Silu),
                     reads=['psG%d' % bg], writes=[pre + 'sg%d' % (fc % 2)])
                P.op('dve', lambda e, pup=pup, SG=SG, fc=fc: e.tensor_tensor(
                    out=actT[:, fc, :], in0=SG, in1=pup, op=ALU.mult),
                    reads=['psG%d' % bu, pre + 'sg%d' % (fc % 2)], writes=[pre + 'actT'])
            for s in range(NSUB):
                for dh in range(2):
                    b = 5 + (s * 2 + dh) % 2
                    pd = bank(b)
                    for fc in range(FC):
                        P.op('pe', lambda e, fc=fc, s=s, dh=dh, pd=pd: e.matmul(
                            pd, lhsT=actT[:, fc, s * 128:(s + 1) * 128], rhs=W2[:, fc, dh * 512:(dh + 1) * 512],
                            start=(fc == 0), stop=(fc == FC - 1)),
                            reads=[pre + 'actT', pre + 'W2'], writes=['psD%d' % b])
                    k = (it * NSUB * 2 + s * 2 + dh) % 3
                    XC = xc[k]
                    xcr = pre + 'xc%d' % k
                    P.op('sp', lambda e, it=it, s=s, dh=dh, XC=XC: e.dma_start(
                        out=XC, in_=srcv[it][:, s, dh * 512:(dh + 1) * 512]), writes=[xcr], dma_key=xcr)
                    P.op('dve', lambda e, XC=XC, pd=pd: e.scalar_tensor_tensor(
                        out=XC, in0=pd, scalar=0.5, in1=XC, op0=ALU.mult, op1=ALU.add),
                        reads=['psD%d' % b, xcr], writes=[xcr])
                    P.op('sp', lambda e, it=it, s=s, dh=dh, XC=XC: e.dma_start(
                        out=dstv[it][:, s, dh * 512:(dh + 1) * 512], in_=XC),
                        reads=[xcr], writes=[pre + 'dst'], dma_key=xcr)

    NCOL = 7712
    w_in = din("w_in", [D, NCOL])
    mix_norm = din("mix_norm", [D])
    rwkv_mu = din("rwkv_mu", [3360])
    b_gate = din("b_gate", [2048])
    qn = din("attn_q_norm", [64])
    kn = din("attn_k_norm", [64])
    kscr = "ExternalOutput" if debug else "Internal"
    if 'proj' not in stages:
        kscr = "ExternalInput"
    rkv_d = nc.dram_tensor("rkv_scr", [24, 128, S], F32, kind=kscr).ap()
    ta_d = nc.dram_tensor("ta_scr", [128, S], BF16, kind=kscr).ap()
    tg_d = nc.dram_tensor("tg_scr", [160, S], BF16, kind=kscr).ap()
    qk_d = nc.dram_tensor("qk_scr", [12, 128, S], BF16, kind=kscr).ap()
    v_d = nc.dram_tensor("v_scr", [S, 768], BF16, kind=kscr).ap()
    gate_d = nc.dram_tensor("gate_scr", [16, 128, S], BF16, kind=kscr).ap()

    def proj_stage():
        A.off = const_mark
        pre = 'p_'
        W = A.alloc([KC, NCOL], BF16)
        gb = A.alloc([D], F32)
        X = A.alloc([NSUB, D], F32)
        hb = [A.alloc([D], BF16) for _ in range(2)]
        hT = [A.alloc([KC, TT], BF16) for _ in range(2)]
        ss = A.alloc([8], F32)
        mu_t = A.alloc([27], F32)
        bg_t = A.alloc([16], F32)
        qg_t = A.alloc([2], F32)
        carry = A.alloc([27], F32)
        psb = [A.alloc([TT + 1], F32) for _ in range(2)]
        tmp = [A.alloc([TT], F32) for _ in range(2)]
        sq = [A.alloc([TT], BF16) for _ in range(2)]
        lnb = [A.alloc([TT], F32) for _ in range(2)]
        rkv_st = A.alloc([24, TT], F32)
        ta_st = A.alloc([TT], BF16)
        tg_st = A.alloc([2, TT], BF16)
        qk_st = A.alloc([12, TT], BF16)
        gate_st = A.alloc([16, TT], BF16)
        v_st = A.alloc([NSUB, 768], BF16)
        bones = A.alloc([128], BF16)
        print("proj_stage arena", A.off)
        wv = w_in.rearrange("(kc p) f -> p kc f", p=128)
        CH = 964
        for kc in range(KC):
            for c in range(NCOL // CH):
                P.op('pool', lambda e, kc=kc, c=c: e.dma_start(out=W[:, kc, c * CH:(c + 1) * CH],
                                                              in_=wv[:, kc, c * CH:(c + 1) * CH]),
                     writes=[pre + 'W'], dma_key=pre + 'W')
        P.op('sp', lambda e: e.dma_start(out=gb, in_=mix_norm.partition_broadcast(128)),
             writes=[pre + 'gb'], dma_key=pre + 'par')
        P.op('sp', lambda e: e.dma_start(out=mu_t[:, 0:26], in_=rwkv_mu[0:3328].rearrange("(b p) -> p b", p=128),
                                         allow_slow_non_contiguous=True), writes=[pre + 'mu'], dma_key=pre + 'par')
        P.op('sp', lambda e: e.dma_start(out=mu_t[0:32, 26:27], in_=rwkv_mu[3328:3360].rearrange("(p o) -> p o", o=1)),
             writes=[pre + 'mu'], dma_key=pre + 'par')
        P.op('sp', lambda e: e.dma_start(out=bg_t, in_=b_gate.rearrange("(b p) -> p b", p=128),
                                         allow_slow_non_contiguous=True), writes=[pre + 'bg'], dma_key=pre + 'par')
        for hh in range(2):
            P.op('sp', lambda e, hh=hh: e.dma_start(out=qg_t[hh * 64:(hh + 1) * 64, 0:1],
                                                     in_=qn.rearrange("(p o) -> p o", o=1)),
                 writes=[pre + 'qg'], dma_key=pre + 'par')
            P.op('sp', lambda e, hh=hh: e.dma_start(out=qg_t[hh * 64:(hh + 1) * 64, 1:2],
                                                     in_=kn.rearrange("(p o) -> p o", o=1)),
                 writes=[pre + 'qg'], dma_key=pre + 'par')
        P.op('pool', lambda e: e.tensor_scalar(out=qg_t[:, 0:1], in0=qg_t[:, 0:1], scalar1=0.125, scalar2=None,
                                               op0=ALU.mult), reads=[pre + 'qg'], writes=[pre + 'qg'])
        P.op('pool', lambda e: e.memset(carry, 0.0), writes=[pre + 'carry%d' % i for i in range(27)])
        P.op('pool', lambda e: e.memset(bones, 0.0), writes=[pre + 'bones'])
        P.op('pool', lambda e: e.memset(bones[0:64, 0:64], 1.0), reads=[pre + 'bones'], writes=[pre + 'bones'])
        P.op('pool', lambda e: e.memset(bones[64:128, 64:128], 1.0), reads=[pre + 'bones'], writes=[pre + 'bones'])

        blocks = []
        for b in range(24):
            blocks.append((b * 128, 128, 'rkv', b))
        blocks.append((3072, 128, 'ta', 24))
        blocks.append((3200, 128, 'tg0', 25))
        blocks.append((3328, 32, 'tg1', 26))
        for b in range(6):
            blocks.append((3360 + b * 128, 128, 'q', b))
        for b in range(6):
            blocks.append((4128 + b * 128, 128, 'k', 6 + b))
        for b in range(16):
            blocks.append((5664 + b * 128, 128, 'gate', b))

        srcv = x1_d.rearrange("(n s p) d -> n p s d", p=128, s=NSUB)
        xr = pre + 'X'
        for it in range(NT):
            t0 = it * TT
            sl = it % 2
            if it == 0:
                P.op('sp', lambda e: e.dma_start(out=X, in_=srcv[0]), writes=[xr], dma_key=xr)
            HT = hT[sl]
            htr = pre + 'hT%d' % sl
            for s in range(NSUB):
                hs = (it * NSUB + s) % 2
                H = hb[hs]
                hr = pre + 'hb%d' % hs
                P.op('act', lambda e, s=s, H=H: e.activation(out=H, in_=X[:, s, :], func=AF.Square,
                                                             accum_out=ss[:, 0:1]),
                     reads=[xr], writes=[hr, pre + 'ss'])
                P.op('act', lambda e: e.activation(out=ss[:, 1:2], in_=ss[:, 0:1], func=AF.Sqrt,
                                                   scale=1.0 / D, bias=RMS_EPS),
                     reads=[pre + 'ss'], writes=[pre + 'ss1'])
                P.op('dve', lambda e: e.reciprocal(out=ss[:, 2:3], in_=ss[:, 1:2]),
                     reads=[pre + 'ss1'], writes=[pre + 'ss2'])
                P.op('dve', lambda e, s=s, H=H: e.scalar_tensor_tensor(
                    out=H, in0=X[:, s, :], scalar=ss[:, 2:3], in1=gb, op0=ALU.mult, op1=ALU.mult),
                    reads=[xr, pre + 'ss2', pre + 'gb'], writes=[hr])
                pT = bank(0).bitcast(BF16)
                for kc in range(KC):
                    P.op('pe', lambda e, kc=kc, H=H, pT=pT: e.transpose(
                        out=pT[:, kc * 128:(kc + 1) * 128], in_=H[:, kc * 128:(kc + 1) * 128], identity=ident),
                        reads=[hr, 'ident'], writes=['psT'])
                P.op('act', lambda e, HT=HT, s=s, pT=pT: e.copy(
                    out=HT[:, :, s * 128:(s + 1) * 128], in_=pT.rearrange("p (k t) -> p k t", k=KC)),
                    reads=['psT'], writes=[htr])

            if it + 1 < NT:
                P.op('sp', lambda e, it=it: e.dma_start(out=X, in_=srcv[it + 1]), writes=[xr], dma_key=xr)
            pending = [None]

            def flush():
                if pending[0] is None:
                    return
                pg, pgr, j, kind, idx = pending[0]
                pending[0] = None
                pss = bank(5)[:, 0:TT]
                P.op('pe', lambda e, j=j, pss=pss: e.matmul(pss, lhsT=bones, rhs=sq[j], start=True, stop=True),
                     reads=[pre + 'sq%d' % j, pre + 'bones'], writes=['pss'])
                P.op('act', lambda e, j=j, pss=pss: e.activation(out=lnb[j], in_=pss, func=AF.Ln,
                                                                 scale=1.0 / 64, bias=RMS_EPS),
                     reads=['pss'], writes=[pre + 'lnb%d' % j])
                P.op('act', lambda e, j=j: e.activation(out=lnb[j], in_=lnb[j], func=AF.Exp, scale=-0.5),
                     reads=[pre + 'lnb%d' % j], writes=[pre + 'lnb%d' % j])
                c = 0 if kind == 'q' else 1
                P.op('dve', lambda e, j=j, pg=pg, idx=idx, c=c: e.scalar_tensor_tensor(
                    out=qk_st[:, idx, :], in0=pg, scalar=qg_t[:, c:c + 1], in1=lnb[j], op0=ALU.mult, op1=ALU.mult),
                    reads=[pgr, pre + 'lnb%d' % j, pre + 'qg'], writes=[pre + 'qk_st'])

            for bi, (col0, M, kind, idx) in enumerate(blocks):
                b = 1 + bi % 4
                pgr = 'psG%d' % b
                pg = bank(b)[0:M, 0:TT]
                j = bi % 2
                for kc in range(KC):
                    P.op('pe', lambda e, kc=kc, col0=col0, M=M, pg=pg, HT=HT: e.matmul(
                        pg, lhsT=W[:, kc, col0:col0 + M], rhs=HT[:, kc, :], start=(kc == 0), stop=(kc == KC - 1)),
                        reads=[pre + 'W', htr], writes=[pgr])
                flush()
                if kind in ('rkv', 'ta', 'tg0', 'tg1'):
                    cr = pre + 'carry%d' % idx
                    pbr = pre + 'psb%d' % j
                    tr = pre + 'tmp%d' % j
                    PS = psb[j][0:M]
                    TM = tmp[j][0:M]
                    P.op('pool', lambda e, PS=PS, idx=idx, M=M: e.tensor_copy(out=PS[:, 0:1], in_=carry[0:M, idx:idx + 1]),
                         reads=[cr], writes=[pbr])
                    P.op('act', lambda e, PS=PS, pg=pg: e.copy(out=PS[:, 1:TT + 1], in_=pg),
                         reads=[pgr, pbr], writes=[pbr])
                    P.op('dve', lambda e, PS=PS, TM=TM: e.tensor_tensor(out=TM, in0=PS[:, 0:TT], in1=PS[:, 1:TT + 1],
                                                                        op=ALU.subtract),
                         reads=[pbr], writes=[tr])
                    P.op('pool', lambda e, PS=PS, idx=idx, M=M: e.tensor_copy(out=carry[0:M, idx:idx + 1],
                                                                              in_=PS[:, TT:TT + 1]),
                         reads=[pbr], writes=[cr])
                    if kind == 'rkv':
                        P.op('dve', lambda e, PS=PS, TM=TM, idx=idx: e.scalar_tensor_tensor(
                            out=rkv_st[:, idx, :], in0=TM, scalar=mu_t[:, idx:idx + 1], in1=PS[:, 1:TT + 1],
                            op0=ALU.mult, op1=ALU.add),
                            reads=[tr, pbr, pre + 'mu'], writes=[pre + 'rkv_st%d' % (idx // 8)])
                    else:
                        P.op('dve', lambda e, PS=PS, TM=TM, idx=idx, M=M: e.scalar_tensor_tensor(
                            out=TM, in0=TM, scalar=mu_t[0:M, idx:idx + 1], in1=PS[:, 1:TT + 1],
                            op0=ALU.mult, op1=ALU.add),
                            reads=[tr, pbr, pre + 'mu'], writes=[tr])
                        if kind == 'ta':
                            P.op('act', lambda e, TM=TM: e.activation(out=ta_st[0:64], in_=TM[0:64], func=AF.Tanh),
                                 reads=[tr], writes=[pre + 'ta_st'])
                            P.op('act', lambda e, TM=TM: e.copy(out=ta_st[64:128], in_=TM[64:128]),
                                 reads=[tr], writes=[pre + 'ta_st'])
                        elif kind == 'tg0':
                            P.op('act', lambda e, TM=TM: e.activation(out=tg_st[:, 0, :], in_=TM, func=AF.Sigmoid),
                                 reads=[tr], writes=[pre + 'tg_st'])
                        else:
                            P.op('act', lambda e, TM=TM: e.activation(out=tg_st[0:32, 1, :], in_=TM, func=AF.Sigmoid),
                                 reads=[tr], writes=[pre + 'tg_st'])
                elif kind in ('q', 'k'):
                    P.op('act', lambda e, pg=pg, j=j: e.activation(out=sq[j], in_=pg, func=AF.Square),
                         reads=[pgr], writes=[pre + 'sq%d' % j])
                    pending[0] = (pg, pgr, j, kind, idx)
                else:
                    P.op('act', lambda e, pg=pg, idx=idx: e.activation(out=gate_st[:, idx, :], in_=pg, func=AF.Sigmoid,
                                                                       bias=bg_t[:, idx:idx + 1]),
                         reads=[pgr, pre + 'bg'], writes=[pre + 'gate_st'])
            flush()
            for s in range(NSUB):
                for (c0, n, b) in ((4896, 512, 6), (5408, 256, 7)):
                    pv = bank(b)[:, 0:n]
                    for kc in range(KC):
                        P.op('pe', lambda e, kc=kc, s=s, c0=c0, n=n, pv=pv, HT=HT: e.matmul(
                            pv, lhsT=HT[:, kc, s * 128:(s + 1) * 128], rhs=W[:, kc, c0:c0 + n],
                            start=(kc == 0), stop=(kc == KC - 1)),
                            reads=[pre + 'W', htr], writes=['psV%d' % b])
                P.op('act', lambda e, s=s: e.copy(out=v_st[:, s, 0:512], in_=bank(6)),
                     reads=['psV6'], writes=[pre + 'v_st'])
                P.op('dve', lambda e, s=s: e.tensor_copy(out=v_st[:, s, 512:768], in_=bank(7)[:, 0:256]),
                     reads=['psV7'], writes=[pre + 'v_st'])
            rv = rkv_d.rearrange("b p t -> p b t")
            for g in range(3):
                P.op('sp', lambda e, g=g, t0=t0: e.dma_start(out=rv[:, g * 8:(g + 1) * 8, t0:t0 + TT],
                                                             in_=rkv_st[:, g * 8:(g + 1) * 8, :]),
                     reads=[pre + 'rkv_st%d' % g], writes=['d_rkv'], dma_key=pre + 'rkv_st%d' % g)
            P.op('sp', lambda e, t0=t0: e.dma_start(out=ta_d[:, t0:t0 + TT], in_=ta_st),
                 reads=[pre + 'ta_st'], writes=['d_ta'], dma_key=pre + 'ta_st')
            P.op('sp', lambda e, t0=t0: e.dma_start(out=tg_d[0:128, t0:t0 + TT], in_=tg_st[:, 0, :]),
                 reads=[pre + 'tg_st'], writes=['d_tg'], dma_key=pre + 'tg_st')
            P.op('sp', lambda e, t0=t0: e.dma_start(out=tg_d[128:160, t0:t0 + TT], in_=tg_st[0:32, 1, :]),
                 reads=[pre + 'tg_st'], writes=['d_tg'], dma_key=pre + 'tg_st')
            P.op('sp', lambda e, t0=t0: e.dma_start(out=qk_d.rearrange("b p t -> p b t")[:, :, t0:t0 + TT], in_=qk_st),
                 reads=[pre + 'qk_st'], writes=['d_qk'], dma_key=pre + 'qk_st')
            P.op('sp', lambda e, t0=t0: e.dma_start(out=gate_d.rearrange("b p t -> p b t")[:, :, t0:t0 + TT],
                                                    in_=gate_st),
                 reads=[pre + 'gate_st'], writes=['d_gate'], dma_key=pre + 'gate_st')
            P.op('sp', lambda e, t0=t0: e.dma_start(
                out=v_d[t0:t0 + TT, :].rearrange("(s p) c -> p s c", p=128), in_=v_st),
                reads=[pre + 'v_st'], writes=['d_v'], dma_key=pre + 'v_st')

    w2_d = din("rwkv_w2", [64, 1024])
    a2_d = din("rwkv_a2", [64, 1024])
    g2_d = din("rwkv_g2", [160, 1024])
    prm_names = ['rwkv_w0', 'rwkv_a0', 'rwkv_k_k', 'rwkv_k_a', 'rwkv_r_k', 'rwkv_ln_w', 'rwkv_ln_b']
    prm_d = [din(n, [1024]) for n in prm_names]
    ya_d = nc.dram_tensor("ya_scr", [8, 128, S], BF16, kind="ExternalOutput" if debug else "Internal").ap()
    TB = 1024
    NCH = TB // 128
    NB = S // TB if NB_DBG is None else NB_DBG
    C0 = float(np.exp(-0.5))
    GN_EPS = 64e-5

    def rwkv_stage(pairs=range(8)):
        A.off = const_mark
        pre = 'r_'
        WA = A.alloc([1024], BF16)
        G2a = A.alloc([1024], BF16)
        G2b = A.alloc([1024], BF16)
        prm = A.alloc([7, 8], F32)
        bones = A.alloc([128], BF16)
        mk4 = A.alloc([512], F32)
        mkL = A.alloc([2, 128], F32)
        E2 = A.alloc([64], F32)
        mrow = A.alloc([TB], F32)
        f32names = ['R', 'K', 'V', 'SG', 'AA', 'GG', 'KK', 'KM', 'T1', 'BVEC', 'CS', 'T2', 'T3', 'EP', 'EN', 'EPM',
                    'EC', 'BV', 'Y32', 'DD']
        T = {n: A.alloc([TB], F32) for n in f32names}
        bfnames = ['TA', 'TG0', 'TG1', 'TQ', 'BT', 'KT', 'BH', 'KH', 'VT', 'YB', 'YO']
        for n in bfnames:
            T[n] = A.alloc([TB], BF16)
        AR = A.alloc([NCH, 2, 128], BF16)
        TM4 = A.alloc([NCH, 4, 128], BF16)
        PC = A.alloc([NCH], F32)
        SC = [[A.alloc([512], BF16) for _ in range(2)] for _ in range(2)]
        LZ = [[A.alloc([2, 384], BF16) for _ in range(2)] for _ in range(2)]
        MCz = [A.alloc([2, 64], BF16) for _ in range(2)]
        QT = [A.alloc([128], BF16) for _ in range(2)]
        STz = A.alloc([2, 64], BF16)
        DBT = ['BT', 'KT', 'GG', 'BV', 'Y32']
        T2 = {n: [T[n], A.alloc([TB], BF16 if n in ('BT', 'KT') else F32)] for n in DBT}
        AR2 = [AR, A.alloc([NCH, 2, 128], BF16)]
        TM42 = [TM4, A.alloc([NCH, 4, 128], BF16)]
        PC2 = [PC, A.alloc([NCH], F32)]
        par = [0]
        coll = [None]

        class TP(dict):
            def __getitem__(self, k):
                if k in T2:
                    return T2[k][par[0]]
                return dict.__getitem__(self, k)

        class BP(object):
            def __init__(self, bufs):
                self.bufs = bufs

            def __getitem__(self, idx):
                return self.bufs[par[0]][idx]

            def rearrange(self, *a_, **k_):
                return self.bufs[par[0]].rearrange(*a_, **k_)
        T = TP(T)
        AR = BP(AR2)
        TM4 = BP(TM42)
        PC = BP(PC2)
        DBNAMES = set(DBT) | {'AR0', 'AR1', 'PC'} | {'TM4_%d' % c_ for c_ in range(NCH)}
        print("rwkv_stage arena", A.off)

        def R_(n):
            return pre + n
        P.op('pool', lambda e: e.dma_start(out=WA[0:64, :], in_=w2_d), writes=[R_('WA')], dma_key=R_('w'))
        P.op('pool', lambda e: e.dma_start(out=WA[64:128, :], in_=a2_d), writes=[R_('WA')], dma_key=R_('w'))
        P.op('pool', lambda e: e.dma_start(out=G2a, in_=g2_d[0:128, :]), writes=[R_('G2')], dma_key=R_('w'))
        P.op('pool', lambda e: e.dma_start(out=G2b[0:32, :], in_=g2_d[128:160, :]), writes=[R_('G2')], dma_key=R_('w'))
        for i in range(7):
            P.op('sp', lambda e, i=i: e.dma_start(out=prm[:, i, :], in_=prm_d[i].rearrange("(b p) -> p b", p=128),
                                                   allow_slow_non_contiguous=True),
                 writes=[R_('prm')], dma_key=R_('par'))
        P.op('pool', lambda e: e.memset(bones, 0.0), writes=[R_('bones')])
        P.op('pool', lambda e: e.memset(bones[0:64, 0:64], 1.0), reads=[R_('bones')], writes=[R_('bones')])
        P.op('pool', lambda e: e.memset(bones[64:128, 64:128], 1.0), reads=[R_('bones')], writes=[R_('bones')])
        P.op('pool', lambda e: e.memset(mk4, 1.0), writes=[R_('mk4')])
        for q in range(4):
            base = -1 if q % 2 == 0 else 0
            P.op('pool', lambda e, q=q, base=base: e.affine_select(
                out=mk4[:, q * 128:(q + 1) * 128], in_=mk4[:, q * 128:(q + 1) * 128], pattern=[[1, 128]],
                compare_op=ALU.is_ge, fill=0.0, base=base, channel_multiplier=-1),
                reads=[R_('mk4')], writes=[R_('mk4')])
        P.op('pool', lambda e: e.memset(mkL, 1.0), writes=[R_('mkL')])
        P.op('pool', lambda e: e.affine_select(out=mkL, in_=mkL, pattern=[[0, 2], [-1, 128]], compare_op=ALU.is_ge,
                                               fill=0.0, base=-1, channel_multiplier=1),
             reads=[R_('mkL')], writes=[R_('mkL')])
        P.op('pool', lambda e: e.tensor_copy(out=E2[0:64, :], in_=ident_f[0:64, 0:64]), reads=['ident_f'], writes=[R_('E2')])
        P.op('pool', lambda e: e.tensor_copy(out=E2[64:128, :], in_=ident_f[64:128, 64:128]), reads=['ident_f'],
             writes=[R_('E2')])
        P.op('pool', lambda e: e.memset(mrow, 1.0), writes=[R_('mrow')])
        P.op('pool', lambda e: e.memset(mrow.rearrange("p (c t) -> p c t", t=128)[:, :, 0:1], 0.0),
             reads=[R_('mrow')], writes=[R_('mrow')])

        def ch3(ap):
            return ap.rearrange("p (c t) -> p c t", t=128)

        def nm(x):
            if x.startswith('pb') or x in ('ident', 'ident_f'):
                return x
            if x in DBNAMES:
                return R_(x) + '@%d' % par[0]
            return R_(x)

        def pdma(eng, fn, reads=(), writes=(), dma_key=None):
            p_ = par[0]

            def fn2(e, fn=fn, p_=p_):
                par[0] = p_
                return fn(e)
            args = (eng, fn2, list(reads), list(writes), dma_key)
            if coll[0] is not None:
                coll[0].append(args)
            else:
                P.op(args[0], args[1], reads=args[2], writes=args[3], dma_key=args[4])

        def ew(eng, fn, reads, writes):
            pdma(eng, fn, [nm(x) for x in reads], [nm(x) for x in writes])

        for sl_ in range(2):
            ew('pool', lambda e, sl_=sl_: e.memset(MCz[sl_], 0.0), [], ['MC%d' % sl_])
        funcs = {}
        for hp in pairs:
            cols = slice(hp * 128, (hp + 1) * 128)
            def prep(tb, hp=hp, cols=cols):
                t0 = tb * TB
                tsl = slice(t0, t0 + TB)
                for i, n in enumerate(['R', 'K', 'V']):
                    pdma('sp', lambda e, i=i, n=n, hp=hp, tsl=tsl: e.dma_start(out=T[n], in_=rkv_d[i * 8 + hp, :, tsl]),
                         writes=[R_(n)], dma_key=R_('ld' + n))
                pdma('sp', lambda e, tsl=tsl: e.dma_start(out=T['TA'], in_=ta_d[:, tsl]), writes=[R_('TA')],
                     dma_key=R_('ldTA'))
                pdma('sp', lambda e, tsl=tsl: e.dma_start(out=T['TG0'], in_=tg_d[0:128, tsl]), writes=[R_('TG0')],
                     dma_key=R_('ldTG0'))
                pdma('sp', lambda e, tsl=tsl: e.dma_start(out=T['TG1'][0:32], in_=tg_d[128:160, tsl]),
                     writes=[R_('TG1')], dma_key=R_('ldTG1'))
                for hf in range(2):
                    hs = slice(hf * 512, (hf + 1) * 512)
                    ew('pe', lambda e, hs=hs, cols=cols: e.matmul(bank(1), lhsT=WA[0:64, cols], rhs=T['TA'][0:64, hs],
                                                                  start=True, stop=True), ['WA', 'TA'], ['pb1'])
                    ew('act', lambda e, hs=hs, hp=hp: e.activation(out=T['SG'][:, hs], in_=bank(1), func=AF.Sigmoid,
                                                                   bias=prm[:, 0, hp:hp + 1]), ['pb1', 'prm'], ['SG'])
                    ew('pe', lambda e, hs=hs, cols=cols: e.matmul(bank(2), lhsT=WA[64:128, cols], rhs=T['TA'][64:128, hs],
                                                                  start=True, stop=True), ['WA', 'TA'], ['pb2'])
                    ew('act', lambda e, hs=hs, hp=hp: e.activation(out=T['AA'][:, hs], in_=bank(2), func=AF.Sigmoid,
                                                                   bias=prm[:, 1, hp:hp + 1]), ['pb2', 'prm'], ['AA'])
                    ew('pe', lambda e, hs=hs, cols=cols: e.matmul(bank(3), lhsT=G2a[:, cols], rhs=T['TG0'][:, hs],
                                                                  start=True, stop=False), ['G2', 'TG0'], ['pb3', 'pb3'])
                    ew('pe', lambda e, hs=hs, cols=cols: e.matmul(bank(3), lhsT=G2b[0:32, cols], rhs=T['TG1'][0:32, hs],
                                                                  start=False, stop=True), ['G2', 'TG1'], ['pb3', 'pb3'])
                    ew('act', lambda e, hs=hs: e.copy(out=T['GG'][:, hs], in_=bank(3)), ['pb3', 'pb3'], ['GG'])
                ew('dve', lambda e, hp=hp: e.tensor_scalar(out=T['KK'], in0=T['K'], scalar1=prm[:, 2, hp:hp + 1],
                                                           scalar2=None, op0=ALU.mult), ['K', 'prm'], ['KK'])
                ew('act', lambda e: e.activation(out=T['TQ'], in_=T['KK'], func=AF.Square), ['KK'], ['TQ'])
                for hf in range(2):
                    hs = slice(hf * 512, (hf + 1) * 512)
                    ew('pe', lambda e, hs=hs: e.matmul(bank(4), lhsT=bones, rhs=T['TQ'][:, hs], start=True, stop=True),
                       ['bones', 'TQ'], ['pb4'])
                    ew('dve', lambda e, hs=hs: e.tensor_scalar(out=T['T1'][:, hs], in0=bank(4), scalar1=1e-19,
                                                               scalar2=None, op0=ALU.max), ['pb4'], ['T1'])
                ew('act', lambda e: e.activation(out=T['T1'], in_=T['T1'], func=AF.Ln), ['T1'], ['T1'])
                ew('act', lambda e: e.activation(out=T['T1'], in_=T['T1'], func=AF.Exp, scale=-0.5), ['T1'], ['T1'])
                ew('dve', lambda e: e.tensor_tensor(out=T['KK'], in0=T['KK'], in1=T['T1'], op=ALU.mult),
                   ['KK', 'T1'], ['KK'])
                ew('dve', lambda e, hp=hp: e.tensor_scalar(out=T['T1'], in0=T['AA'], scalar1=-1.0,
                                                           scalar2=prm[:, 3, hp:hp + 1], op0=ALU.add, op1=ALU.mult),
                   ['AA', 'prm'], ['T1'])
                ew('dve', lambda e: e.scalar_tensor_tensor(out=T['KM'], in0=T['T1'], scalar=1.0, in1=T['K'],
                                                           op0=ALU.add, op1=ALU.mult), ['T1', 'K'], ['KM'])
                ew('pool', lambda e: e.tensor_tensor(out=T['T1'], in0=T['R'], in1=T['KM'], op=ALU.mult),
                   ['R', 'KM'], ['T1'])
                ew('pool', lambda e, hp=hp: e.tensor_scalar(out=T['TQ'], in0=T['T1'], scalar1=prm[:, 4, hp:hp + 1],
                                                            scalar2=None, op0=ALU.mult), ['T1', 'prm'], ['TQ'])
                for hf in range(2):
                    hs = slice(hf * 512, (hf + 1) * 512)
                    ew('pe', lambda e, hs=hs: e.matmul(bank(5), lhsT=bones, rhs=T['TQ'][:, hs], start=True, stop=True),
                       ['bones', 'TQ'], ['pb5'])
                    ew('dve', lambda e, hs=hs: e.tensor_tensor(out=T['BV'][:, hs], in0=T['V'][:, hs], in1=bank(5),
                                                               op=ALU.mult), ['pb5', 'V'], ['BV'])
                ew('pool', lambda e: e.tensor_tensor(out=T['BVEC'], in0=T['KK'], in1=T['AA'], op=ALU.mult),
                   ['KK', 'AA'], ['BVEC'])
                ew('dve', lambda e: e.tensor_tensor_scan(out=T['CS'], data0=mrow, data1=T['SG'], initial=0.0,
                                                         op0=ALU.mult, op1=ALU.add), ['mrow', 'SG'], ['CS'])
                ew('pool', lambda e: e.tensor_tensor(out=T['T2'], in0=T['CS'], in1=T['SG'], op=ALU.subtract),
                   ['CS', 'SG'], ['T2'])
                ew('act', lambda e: e.activation(out=T['EP'], in_=T['CS'], func=AF.Exp, scale=-C0), ['CS'], ['EP'])
                ew('act', lambda e: e.activation(out=T['EN'], in_=T['CS'], func=AF.Exp, scale=C0), ['CS'], ['EN'])
                ew('act', lambda e: e.activation(out=T['EPM'], in_=T['T2'], func=AF.Exp, scale=-C0), ['T2'], ['EPM'])
                ew('dve', lambda e: e.tensor_tensor(
                    out=ch3(T['T3']), in0=ch3(T['CS']), in1=ch3(T['CS'])[:, :, 127:128].to_broadcast([128, NCH, 128]),
                    op=ALU.subtract), ['CS'], ['T3'])
                ew('act', lambda e: e.activation(out=T['EC'], in_=T['T3'], func=AF.Exp, scale=C0), ['T3'], ['EC'])
                ew('act', lambda e: e.activation(out=PC.rearrange("p (c o) -> p c o", o=1),
                                                 in_=ch3(T['CS'])[:, :, 127:128], func=AF.Exp, scale=-C0),
                   ['CS'], ['PC'])
                ew('dve', lambda e: e.scalar_tensor_tensor(out=AR[:, :, 0, :], in0=ch3(T['EPM']), scalar=-1.0,
                                                           in1=ch3(T['KK']), op0=ALU.mult, op1=ALU.mult),
                   ['EPM', 'KK'], ['AR0'])
                ew('pool', lambda e: e.tensor_tensor(out=AR[:, :, 1, :], in0=ch3(T['EP']), in1=ch3(T['R']), op=ALU.mult),
                   ['EP', 'R'], ['AR1'])
                ew('dve', lambda e: e.tensor_tensor(out=T['BT'], in0=T['EN'], in1=T['BVEC'], op=ALU.mult),
                   ['EN', 'BVEC'], ['BT'])
                ew('pool', lambda e: e.tensor_tensor(out=T['KT'], in0=T['EN'], in1=T['KM'], op=ALU.mult),
                   ['EN', 'KM'], ['KT'])
                ew('dve', lambda e: e.tensor_tensor(out=T['BH'], in0=T['EC'], in1=T['BVEC'], op=ALU.mult),
                   ['EC', 'BVEC'], ['BH'])
                ew('pool', lambda e: e.tensor_tensor(out=T['KH'], in0=T['EC'], in1=T['KM'], op=ALU.mult),
                   ['EC', 'KM'], ['KH'])
                ew('act', lambda e: e.copy(out=T['VT'], in_=T['V']), ['V'], ['VT'])
                pT = bank(0).bitcast(BF16)
                for c in range(NCH):
                    cs_ = slice(c * 128, (c + 1) * 128)
                    srcs = [(AR[:, c, 0, :], 'AR0'), (T['VT'][:, cs_], 'VT'), (T['BH'][:, cs_], 'BH'),
                            (T['KH'][:, cs_], 'KH')]
                    for q, (sap, sr) in enumerate(srcs):
                        ew('pe', lambda e, q=q, sap=sap: e.transpose(out=pT[:, q * 128:(q + 1) * 128], in_=sap,
                                                                     identity=ident), [sr, 'ident'], ['pb0'])
                    ew('act', lambda e, c=c: e.copy(out=TM4[:, c, :, :], in_=pT[:, 0:512].rearrange("p (q t) -> p q t", q=4)),
                       ['pb0'], ['TM4_%d' % c])

            def chunks(tb, inter):
                P1 = (1, 2)
                P2 = ((4, 5), (6, 7))

                def partA(c, sl):
                    cs_ = slice(c * 128, (c + 1) * 128)
                    tm = 'TM4_%d' % c
                    arc = AR[:, c, :, :].rearrange("p a t -> p (a t)")
                    b1 = P1[sl]
                    ps1 = bank(b1)
                    for h2 in range(2):
                        psl = slice(64 * h2, 64 * h2 + 64)
                        scr = 'SC%d_%d' % (sl, h2)
                        ew('pe', lambda e, ps1=ps1, psl=psl, cs_=cs_, arc=arc: e.matmul(
                            ps1[:, 0:256], lhsT=T['BT'][psl, cs_], rhs=arc[psl, :], start=True, stop=True),
                            ['BT', 'AR0', 'AR1'], ['pb%d' % b1])
                        ew('pe', lambda e, ps1=ps1, psl=psl, cs_=cs_, arc=arc: e.matmul(
                            ps1[:, 256:512], lhsT=T['KT'][psl, cs_], rhs=arc[psl, :], start=True, stop=True),
                            ['KT', 'AR0', 'AR1'], ['pb%d' % b1])
                        ew('dve', lambda e, ps1=ps1, h2=h2, sl=sl: e.tensor_tensor(out=SC[sl][h2], in0=mk4, in1=ps1,
                                                                                   op=ALU.mult),
                           ['pb%d' % b1, 'mk4'], [scr])
                        b2 = P2[sl][h2]
                        ew('pe', lambda e, b2=b2, psl=psl, cs_=cs_, c=c: e.matmul(
                            bank(b2)[:, 384:512], lhsT=AR[psl, c, 0, :], rhs=T['BT'][psl, cs_],
                            start=True, stop=True), ['AR0', 'BT'], ['pb%d' % b2])
                    for h2 in range(2):
                        b2 = P2[sl][h2]
                        ew('dve', lambda e, h2=h2, b2=b2, sl=sl: e.tensor_tensor(
                            out=LZ[sl][0][:, h2, 0:128], in0=mkL[:, h2, :], in1=bank(b2)[:, 384:512], op=ALU.mult),
                            ['pb%d' % b2, 'mkL'], ['LL%d_0_%d' % (sl, h2)])
                    for h2 in range(2):
                        pb = 64 * h2
                        ew('pe', lambda e, h2=h2, pb=pb, c=c, sl=sl: e.matmul(
                            bank(3)[:, 256 + h2 * 64:256 + (h2 + 1) * 64], lhsT=SC[sl][h2][:, 256:384],
                            rhs=TM4[:, c, 1, pb:pb + 64], start=True, stop=True), ['SC%d_%d' % (sl, h2), tm], ['pb3'])
                    ew('pool', lambda e, c=c, sl=sl: e.tensor_copy(
                        out=LZ[sl][0][:, :, 128:192], in_=TM4[:, c, 0, :].rearrange("p (h k) -> p h k", h=2)),
                        [tm], ['ZZ%d_0_0' % sl, 'ZZ%d_0_1' % sl])
                    for h2 in range(2):
                        ew('act', lambda e, h2=h2, sl=sl: e.copy(out=LZ[sl][0][:, h2, 192:256],
                                                                 in_=bank(3)[:, 256 + h2 * 64:256 + (h2 + 1) * 64]),
                           ['pb3'], ['ZZ%d_0_%d' % (sl, h2)])

                def partB(c, sl, n):
                    pp = n % 2
                    for h2 in range(2):
                        b2 = P2[sl][h2]
                        ps2 = bank(b2)
                        pbr = 'pb%d' % b2
                        ltn = SC[sl][h2][:, 0:128] if n == 0 else LZ[sl][pp][:, h2, 256:384]
                        rds = ['LL%d_%d_%d' % (sl, pp, h2), 'ZZ%d_%d_%d' % (sl, pp, h2)] + \
                            (['SC%d_%d' % (sl, h2)] if n == 0 else [])
                        if n < 6:
                            ew('pe', lambda e, ps2=ps2, ltn=ltn, pp=pp, h2=h2, sl=sl: e.matmul(
                                ps2[:, 0:256], lhsT=ltn, rhs=LZ[sl][pp][:, h2, 0:256], start=True, stop=True),
                                rds, [pbr])
                            ew('pe', lambda e, ps2=ps2, ltn=ltn, pp=pp, h2=h2, sl=sl: e.matmul(
                                ps2[:, 256:384], lhsT=LZ[sl][pp][:, h2, 0:128], rhs=ltn, start=True, stop=True),
                                rds, [pbr])
                        else:
                            ew('pe', lambda e, ps2=ps2, ltn=ltn, pp=pp, h2=h2, sl=sl: e.matmul(
                                ps2[:, 128:256], lhsT=ltn, rhs=LZ[sl][pp][:, h2, 128:256], start=True, stop=True),
                                rds, [pbr])

                    def cp(h2):
                        b2 = P2[sl][h2]
                        ew('act', lambda e, h2=h2, b2=b2: e.copy(
                            out=LZ[sl][1 - pp][:, h2, :].rearrange("p (s t) -> p s t", s=3)[:, 0:3:2, :],
                            in_=bank(b2)[:, 0:384].rearrange("p (s t) -> p s t", s=3)[:, 0:3:2, :]),
                            ['pb%d' % b2], ['LL%d_%d_%d' % (sl, 1 - pp, h2)])

                    def ad(h2):
                        b2 = P2[sl][h2]
                        ew('dve', lambda e, h2=h2, b2=b2: e.tensor_tensor(
                            out=LZ[sl][1 - pp][:, h2, 128:256], in0=LZ[sl][pp][:, h2, 128:256],
                            in1=bank(b2)[:, 128:256], op=ALU.add),
                            ['pb%d' % b2, 'ZZ%d_%d_%d' % (sl, pp, h2)], ['ZZ%d_%d_%d' % (sl, 1 - pp, h2)])
                    if n < 6:
                        cp(0)
                        ad(1)
                        cp(1)
                        ad(0)
                    else:
                        ad(0)
                        ad(1)

                def partC(c, sl):
                    tm = 'TM4_%d' % c
                    ZF = LZ[sl][1]
                    p3 = bank(0)[:, 192:384]
                    for h2 in range(2):
                        pb = 64 * h2
                        psl = slice(pb, pb + 64)
                        zr = 'ZZ%d_1_%d' % (sl, h2)
                        ew('pe', lambda e, h2=h2, pb=pb, psl=psl, c=c: e.matmul(
                            p3[psl, 0:64], lhsT=ZF[:, h2, 128:192], rhs=TM4[:, c, 2, pb:pb + 64],
                            start=True, stop=True, tile_position=(0, pb)), [zr, tm], ['pb0'])
                        ew('pe', lambda e, h2=h2, pb=pb, psl=psl: e.matmul(
                            p3[psl, 64:192], lhsT=ZF[:, h2, 128:192], rhs=SC[sl][h2][:, 128:256],
                            start=True, stop=True, tile_position=(0, pb)), [zr, 'SC%d_%d' % (sl, h2)], ['pb0'])
                    for h2 in range(2):
                        psl = slice(64 * h2, 64 * h2 + 64)
                        ew('dve', lambda e, c=c, psl=psl, h2=h2: e.scalar_tensor_tensor(
                            out=MCz[sl][psl, h2, :], in0=E2[psl, :], scalar=PC[psl, c:c + 1], in1=p3[psl, 0:64],
                            op0=ALU.mult, op1=ALU.add), ['E2', 'PC', 'pb0'], ['MC%d' % sl])
                    ew('dve', lambda e, c=c: e.tensor_tensor(out=QT[sl], in0=AR[:, c, 1, :], in1=p3[:, 64:192],
                                                             op=ALU.add), ['pb0', 'AR1'], ['QT%d' % sl])

                def partD(c, sl):
                    cs_ = slice(c * 128, (c + 1) * 128)
                    tm = 'TM4_%d' % c
                    ZF = LZ[sl][1]
                    for h2 in range(2):
                        pb = 64 * h2
                        psl = slice(pb, pb + 64)
                        sb_ = 3 if h2 == 0 else 0
                        sr_ = 'pb%d' % sb_
                        psY = bank(sb_)[:, 0:128]
                        psS = bank(sb_)[:, 128:192]
                        UU = ZF[:, h2, 192:256]
                        zr = 'ZZ%d_1_%d' % (sl, h2)
                        scr = 'SC%d_%d' % (sl, h2)
                        ew('pe', lambda e, psl=psl, pb=pb, h2=h2, UU=UU, psY=psY: e.matmul(
                            psY[psl, :], lhsT=UU, rhs=SC[sl][h2][:, 128:256], start=True, stop=False,
                            tile_position=(0, pb)), [zr, scr], [sr_])
                        ew('pe', lambda e, psl=psl, pb=pb, h2=h2, c=c, psY=psY: e.matmul(
                            psY[psl, :], lhsT=TM4[:, c, 1, pb:pb + 64], rhs=SC[sl][h2][:, 384:512], start=False,
                            stop=False, tile_position=(0, pb)), [tm, scr], [sr_])
                        ew('pe', lambda e, psl=psl, pb=pb, psY=psY, h2=h2: e.matmul(
                            psY[psl, :], lhsT=STz[:, h2, :], rhs=QT[sl], start=False, stop=True,
                            tile_position=(0, pb)), ['ST', 'QT%d' % sl], [sr_])
                        ew('pe', lambda e, psl=psl, pb=pb, psS=psS, h2=h2: e.matmul(
                            psS[psl, :], lhsT=MCz[sl][:, h2, :], rhs=STz[:, h2, :], start=True, stop=False,
                            tile_position=(0, pb)), ['MC%d' % sl, 'ST'], [sr_])
                        ew('pe', lambda e, psl=psl, pb=pb, c=c, UU=UU, psS=psS: e.matmul(
                            psS[psl, :], lhsT=TM4[:, c, 2, pb:pb + 64], rhs=UU, start=False, stop=False,
                            tile_position=(0, pb)), [tm, zr], [sr_])
                        ew('pe', lambda e, psl=psl, pb=pb, c=c, psS=psS: e.matmul(
                            psS[psl, :], lhsT=TM4[:, c, 3, pb:pb + 64], rhs=TM4[:, c, 1, pb:pb + 64], start=False,
                            stop=True, tile_position=(0, pb)), [tm], [sr_])
                    for h2 in range(2):
                        pb = 64 * h2
                        psl = slice(pb, pb + 64)
                        sb_ = 3 if h2 == 0 else 0
                        sr_ = 'pb%d' % sb_
                        ew('act', lambda e, cs_=cs_, psl=psl, sb_=sb_: e.copy(out=T['Y32'][psl, cs_],
                                                                             in_=bank(sb_)[psl, 0:128]),
                           [sr_], ['Y32'])
                        ew('dve', lambda e, psl=psl, sb_=sb_, h2=h2: e.tensor_copy(out=STz[psl, h2, :],
                                                                                   in_=bank(sb_)[psl, 128:192]),
                           [sr_], ['ST'])

                for c0 in range(0, NCH, 2):
                    for sl in range(2):
                        partA(c0 + sl, sl)
                    for n in range(7):
                        for sl in range(2):
                            partB(c0 + sl, sl, n)
                        if n in (1, 3, 5):
                            inter()
                    for sl in range(2):
                        partC(c0 + sl, sl)
                    for sl in range(2):
                        partD(c0 + sl, sl)
                    inter()

            def outst(tb, hp=hp):
                t0 = tb * TB
                tsl = slice(t0, t0 + TB)
                ew('pool', lambda e: e.tensor_copy(out=T['YB'], in_=T['Y32']), ['Y32'], ['YB'])
                for hf in range(2):
                    hs = slice(hf * 512, (hf + 1) * 512)
                    ew('pe', lambda e, hs=hs: e.matmul(bank(1), lhsT=bones, rhs=T['YB'][:, hs], start=True, stop=True),
                       ['bones', 'YB'], ['pb1'])
                    ew('dve', lambda e, hs=hs: e.scalar_tensor_tensor(out=T['DD'][:, hs], in0=bank(1), scalar=-1.0 / 64,
                                                                      in1=T['Y32'][:, hs], op0=ALU.mult, op1=ALU.add),
                       ['pb1', 'Y32'], ['DD'])
                ew('act', lambda e: e.activation(out=T['TQ'], in_=T['DD'], func=AF.Square), ['DD'], ['TQ'])
                for hf in range(2):
                    hs = slice(hf * 512, (hf + 1) * 512)
                    ew('pe', lambda e, hs=hs: e.matmul(bank(2), lhsT=bones, rhs=T['TQ'][:, hs], start=True, stop=True),
                       ['bones', 'TQ'], ['pb2'])
                    ew('act', lambda e, hs=hs: e.activation(out=T['T1'][:, hs], in_=bank(2), func=AF.Ln, scale=1.0 / 64,
                                                            bias=GN_EPS), ['pb2'], ['T1'])
                ew('act', lambda e: e.activation(out=T['T1'], in_=T['T1'], func=AF.Exp, scale=-0.5), ['T1'], ['T1'])
                ew('dve', lambda e: e.tensor_tensor(out=T['DD'], in0=T['DD'], in1=T['T1'], op=ALU.mult),
                   ['DD', 'T1'], ['DD'])
                ew('dve', lambda e, hp=hp: e.tensor_scalar(out=T['DD'], in0=T['DD'], scalar1=prm[:, 5, hp:hp + 1],
                                                           scalar2=prm[:, 6, hp:hp + 1], op0=ALU.mult, op1=ALU.add),
                   ['DD', 'prm'], ['DD'])
                ew('pool', lambda e: e.tensor_tensor(out=T['DD'], in0=T['DD'], in1=T['BV'], op=ALU.add),
                   ['DD', 'BV'], ['DD'])
                ew('dve', lambda e: e.tensor_tensor(out=T['YO'], in0=T['DD'], in1=T['GG'], op=ALU.mult),
                   ['DD', 'GG'], ['YO'])
                pdma('sp', lambda e, hp=hp, tsl=tsl: e.dma_start(out=ya_d[hp, :, tsl], in_=T['YO']),
                     reads=[R_('YO')], writes=['d_ya'], dma_key=R_('stYO'))


            funcs[hp] = (prep, chunks, outst)

        def emit_all(lst):
            for a_ in lst:
                P.op(a_[0], a_[1], reads=a_[2], writes=a_[3], dma_key=a_[4])
        blocks = [(hp_, tb_) for hp_ in pairs for tb_ in range(NB)]
        par[0] = 0
        funcs[blocks[0][0]][0](blocks[0][1])
        pend_out = []
        for bi, (hp_, tb_) in enumerate(blocks):
            prep_f, chunks_f, outst_f = funcs[hp_]
            pend = list(pend_out)
            if bi + 1 < len(blocks):
                nhp, ntb = blocks[bi + 1]
                par[0] = (bi + 1) % 2
                coll[0] = pend
                funcs[nhp][0](ntb)
                coll[0] = None
            par[0] = bi % 2
            if tb_ == 0:
                ew('pool', lambda e: e.memset(STz, 0.0), [], ['ST'])
            nsl = 4 * (NCH // 2)
            step = (len(pend) + nsl - 1) // nsl if pend else 0
            pos = [0]

            def inter(pend=pend, step=step, pos=pos):
                open_banks = set()
                cnt = 0
                while pos[0] < len(pend) and (cnt < step or open_banks):
                    a_ = pend[pos[0]]
                    pos[0] += 1
                    cnt += 1
                    if a_[0] == 'pe':
                        open_banks.update(w_ for w_ in a_[3] if w_.startswith('pb'))
                    else:
                        open_banks.difference_update(r_ for r_ in a_[2] if r_.startswith('pb'))
                    P.op(a_[0], a_[1], reads=a_[2], writes=a_[3], dma_key=a_[4])
            chunks_f(tb_, inter)
            emit_all(pend[pos[0]:])
            par[0] = bi % 2
            pend_out = []
            coll[0] = pend_out
            outst_f(tb_)
            coll[0] = None
        emit_all(pend_out)

    yb_d = nc.dram_tensor("yb_scr", [6, 128, S], BF16, kind="ExternalOutput" if debug else "Internal").ap()
    DIL = (1, 4, 16)

    def attn_stage_full(js=range(4)):
        A.off = const_mark
        pre = 'a_'

        def R_(n):
            return pre + n

        def nm(x):
            if x.startswith('pb') or x in ('ident', 'ident_f'):
                return x
            return R_(x)

        def ew(eng, fn, reads, writes):
            P.op(eng, fn, reads=[nm(x) for x in reads], writes=[nm(x) for x in writes])
        QH = A.alloc([S], BF16)
        KH = A.alloc([S], BF16)
        VX = A.alloc([32, 64], BF16)
        ONES = A.alloc([64], BF16)
        OT = [A.alloc([S], F32) for _ in range(3)]
        DEN = [A.alloc([S], F32) for _ in range(3)]
        PT = [A.alloc([256], BF16) for _ in range(4)]
        mask2 = A.alloc([256], BF16)
        RD = [A.alloc([512], F32) for _ in range(2)]
        YBS = [A.alloc([512], BF16) for _ in range(2)]
        print("attn_stage arena", A.off)
        P.op('pool', lambda e: e.memset(mask2, 1.0), writes=[R_('mask')])
        P.op('pool', lambda e: e.affine_select(out=mask2[:, 0:128], in_=mask2[:, 0:128], pattern=[[1, 128]],
                                               compare_op=ALU.is_ge, fill=0.0, base=0, channel_multiplier=-1),
             reads=[R_('mask')], writes=[R_('mask')])
        P.op('pool', lambda e: e.affine_select(out=mask2[:, 128:256], in_=mask2[:, 128:256], pattern=[[-1, 128]],
                                               compare_op=ALU.is_ge, fill=0.0, base=0, channel_multiplier=1),
             reads=[R_('mask')], writes=[R_('mask')])
        P.op('pool', lambda e: e.memset(ONES, 1.0), writes=[R_('ONES')])

        tcount = [0]
        ccount = [0]
        for j in js:
            for g in range(3):
                d = DIL[g]
                nb = S // d // 128
                h = 4 * g + j
                pair = h // 2
                pb = 64 * (h % 2)
                vv = v_d.rearrange("(m d) c -> d m c", d=d)
                for r in range(d):
                    for n0 in range(0, nb, 8):
                        n1 = min(nb, n0 + 8)
                        P.op('sp', lambda e, r=r, h=h, nb=nb, vv=vv, n0=n0, n1=n1: e.dma_start(
                            out=VX[:, r * nb + n0:r * nb + n1, :],
                            in_=vv[r, n0 * 128:n1 * 128, h * 64:(h + 1) * 64].rearrange("(n i) c -> i n c", i=128)),
                            writes=[R_('VX')], dma_key=R_('ldV'))
                P.op('sp', lambda e, pair=pair, pb=pb: e.dma_start(out=QH[0:64, :], in_=qk_d[pair, pb:pb + 64, :]),
                     writes=[R_('QH')], dma_key=R_('ldQ'))
                P.op('sp', lambda e, pair=pair, pb=pb: e.dma_start(out=KH[0:64, :], in_=qk_d[6 + pair, pb:pb + 64, :]),
                     writes=[R_('KH')], dma_key=R_('ldK'))
                qv = QH.rearrange("p (m d) -> p d m", d=d)
                kv = KH.rearrange("p (m d) -> p d m", d=d)
                otr = 'OT%d' % g
                otv = OT[g].rearrange("p (m d) -> p d m", d=d)
                dnv = DEN[g].rearrange("p (m d) -> p d m", d=d)
                tiles = []
                for r in range(d):
                    for n in range(nb):
                        tiles.append((r, n, tcount[0]))
                        tcount[0] += 1

                def emit_score(r, n, ti, kv=kv, qv=qv, nb=nb):
                    nq = 256 if n + 1 < nb else 128
                    sbk = (1, 2, 6)[ti % 3]
                    ps = bank(sbk)[:, 0:nq]
                    pt = PT[ti % 4]
                    ptr = 'PT%d' % (ti % 4)
                    ew('pe', lambda e, ps=ps, r=r, n=n, nq=nq: e.matmul(
                        ps, lhsT=kv[0:64, r, 128 * n:128 * n + 128], rhs=qv[0:64, r, 128 * n:128 * n + nq],
                        start=True, stop=True), ['KH', 'QH'], ['pb%d' % sbk])
                    ew('act', lambda e, ps=ps, pt=pt, nq=nq: e.activation(out=pt[:, 0:nq], in_=ps, func=AF.Exp),
                       ['pb%d' % sbk], [ptr])
                    ew('pool', lambda e, pt=pt, nq=nq: e.tensor_tensor(out=pt[:, 0:nq], in0=pt[:, 0:nq],
                                                                      in1=mask2[:, 0:nq], op=ALU.mult),
                       [ptr, 'mask'], [ptr])

                def emit_pv(r, n, ti, otv=otv, dnv=dnv, nb=nb, g=g, otr=otr):
                    pt = PT[ti % 4]
                    ptr = 'PT%d' % (ti % 4)
                    obk = 3 + ti % 2
                    po = bank(obk)[0:64, 0:128]
                    pdn = bank(obk)[0:64, 128:256]
                    vt = r * nb + n
                    has_prev = n > 0
                    if has_prev:
                        ppt = PT[(ti - 1) % 4]
                        pptr = 'PT%d' % ((ti - 1) % 4)
                        ew('pe', lambda e, ppt=ppt, vt=vt: e.matmul(
                            po, lhsT=VX[:, vt - 1, :], rhs=ppt[:, 128:256], start=True, stop=False),
                            ['VX', pptr], ['pb%d' % obk])
                    ew('pe', lambda e, pt=pt, vt=vt: e.matmul(
                        po, lhsT=VX[:, vt, :], rhs=pt[:, 0:128], start=(not has_prev), stop=True),
                        ['VX', ptr], ['pb%d' % obk])
                    if has_prev:
                        ew('pe', lambda e, ppt=ppt: e.matmul(
                            pdn, lhsT=ONES, rhs=ppt[:, 128:256], start=True, stop=False),
                            ['ONES', pptr], ['pb%d' % obk])
                    ew('pe', lambda e, pt=pt: e.matmul(
                        pdn, lhsT=ONES, rhs=pt[:, 0:128], start=(not has_prev), stop=True),
                        ['ONES', ptr], ['pb%d' % obk])
                    ew('dve', lambda e, r=r, n=n: e.tensor_copy(
                        out=otv[0:64, r, 128 * n:128 * n + 128], in_=po), ['pb%d' % obk], [otr])
                    ew('dve', lambda e, r=r, n=n: e.tensor_copy(
                        out=dnv[0:64, r, 128 * n:128 * n + 128], in_=pdn), ['pb%d' % obk], ['DEN%d' % g])

                LAG = 2
                for idx, (r, n, ti) in enumerate(tiles):
                    emit_score(r, n, ti)
                    if idx >= LAG:
                        emit_pv(*tiles[idx - LAG])
                for idx in range(max(0, len(tiles) - LAG), len(tiles)):
                    emit_pv(*tiles[idx])
            for ck in range(S // 512):
                csl = slice(ck * 512, (ck + 1) * 512)
                cc = ccount[0]
                ccount[0] += 1
                rd = RD[cc % 2]
                ew('pool', lambda e, rd=rd, csl=csl: e.tensor_tensor(out=rd[0:64, :], in0=DEN[0][0:64, csl],
                                                                     in1=DEN[1][0:64, csl], op=ALU.add),
                   ['DEN0', 'DEN1'], ['RD%d' % (cc % 2)])
                ew('pool', lambda e, rd=rd, csl=csl: e.tensor_tensor(out=rd[0:64, :], in0=rd[0:64, :],
                                                                     in1=DEN[2][0:64, csl], op=ALU.add),
                   ['RD%d' % (cc % 2), 'DEN2'], ['RD%d' % (cc % 2)])
                ew('dve', lambda e, rd=rd: e.reciprocal(out=rd[0:64, :], in_=rd[0:64, :]), ['RD%d' % (cc % 2)],
                   ['RD%d' % (cc % 2)])
                for g in range(3):
                    h = 4 * g + j
                    pair = h // 2
                    pb = 64 * (h % 2)
                    yi = (cc * 3 + g) % 2
                    ys = YBS[yi]
                    ew('pool', lambda e, g=g, csl=csl, rd=rd, ys=ys: e.tensor_tensor(
                        out=ys[0:64, :], in0=OT[g][0:64, csl], in1=rd[0:64, :], op=ALU.mult),
                        ['OT%d' % g, 'RD%d' % (cc % 2)], ['YBS%d' % yi])
                    P.op('sp', lambda e, pair=pair, pb=pb, csl=csl, ys=ys: e.dma_start(
                        out=yb_d[pair, pb:pb + 64, csl], in_=ys[0:64, :]),
                        reads=[R_('YBS%d' % yi)], writes=['d_yb'], dma_key=R_('stYB%d' % yi))

    wpr_d = din("w_proj_rwkv", [1024, 1024])
    wpa_d = din("w_proj_attn", [768, 1024])
    wo_d = din("w_out", [1024, 1024])
    x2_d = nc.dram_tensor("x2_scr", [S, D], F32, kind="ExternalOutput" if debug else "Internal").ap()

    def merge_stage():
        A.off = const_mark
        pre = 'm_'

        def R_(n):
            return pre + n

        def nm(x):
            if x.startswith('pb'):
                return x
            return R_(x)

        def ew(eng, fn, reads, writes):
            P.op(eng, fn, reads=[nm(x) for x in reads], writes=[nm(x) for x in writes])
        Wr = A.alloc([8, 1024], BF16)
        Wa = A.alloc([6, 1024], BF16)
        Wo = A.alloc([8, 1024], BF16)
        X = [A.alloc([NSUB, D], F32) for _ in range(2)]
        YA = [A.alloc([8, TT], BF16) for _ in range(2)]
        YB = [A.alloc([6, TT], BF16) for _ in range(2)]
        G = [A.alloc([16, TT], BF16) for _ in range(2)]
        MT = A.alloc([8, TT], BF16)
        t1 = [A.alloc([TT], F32) for _ in range(2)]
        t2 = [A.alloc([TT], F32) for _ in range(2)]
        print("merge_stage arena", A.off)
        for kc in range(8):
            P.op('pool', lambda e, kc=kc: e.dma_start(out=Wr[:, kc, :], in_=wpr_d[kc * 128:(kc + 1) * 128, :]),
                 writes=[R_('Wr')], dma_key=R_('w'))
            P.op('pool', lambda e, kc=kc: e.dma_start(out=Wo[:, kc, :], in_=wo_d[kc * 128:(kc + 1) * 128, :]),
                 writes=[R_('Wo')], dma_key=R_('w'))
        for kc in range(6):
            P.op('pool', lambda e, kc=kc: e.dma_start(out=Wa[:, kc, :], in_=wpa_d[kc * 128:(kc + 1) * 128, :]),
                 writes=[R_('Wa')], dma_key=R_('w'))
        srcv = x1_d.rearrange("(n s p) d -> n p s d", p=128, s=NSUB)
        dstv = x2_d.rearrange("(n s p) d -> n p s d", p=128, s=NSUB)
        for it in range(NT):
            sl = it % 2
            tsl = slice(it * TT, (it + 1) * TT)
            sfx = '%d' % sl
            P.op('sp', lambda e, it=it, sl=sl: e.dma_start(out=X[sl], in_=srcv[it]), writes=[R_('X' + sfx)],
                 dma_key=R_('ldX' + sfx))
            P.op('sp', lambda e, sl=sl, tsl=tsl: e.dma_start(out=YA[sl], in_=ya_d.rearrange("b p t -> p b t")[:, :, tsl]),
                 writes=[R_('YA' + sfx)], dma_key=R_('ldYA' + sfx))
            P.op('sp', lambda e, sl=sl, tsl=tsl: e.dma_start(out=YB[sl], in_=yb_d.rearrange("b p t -> p b t")[:, :, tsl]),
                 writes=[R_('YB' + sfx)], dma_key=R_('ldYB' + sfx))
            P.op('sp', lambda e, sl=sl, tsl=tsl: e.dma_start(out=G[sl], in_=gate_d.rearrange("b p t -> p b t")[:, :, tsl]),
                 writes=[R_('G' + sfx)], dma_key=R_('ldG' + sfx))
            for c in range(8):
                bk = 1 + c % 2
                pg = bank(bk)
                for kc in range(8):
                    ew('pe', lambda e, c=c, kc=kc, pg=pg, sl=sl: e.matmul(
                        pg[:, 0:TT], lhsT=Wr[:, kc, c * 128:(c + 1) * 128], rhs=YA[sl][:, kc, :],
                        start=(kc == 0), stop=(kc == 7)), ['Wr', 'YA' + sfx], ['pb%d' % bk])
                for kc in range(6):
                    ew('pe', lambda e, c=c, kc=kc, pg=pg, sl=sl: e.matmul(
                        pg[:, TT:2 * TT], lhsT=Wa[:, kc, c * 128:(c + 1) * 128], rhs=YB[sl][:, kc, :],
                        start=(kc == 0), stop=(kc == 5)), ['Wa', 'YB' + sfx], ['pb%d' % bk])
                q = c % 2
                ew('dve', lambda e, c=c, pg=pg, sl=sl, q=q: e.tensor_tensor(out=t1[q], in0=G[sl][:, c, :],
                                                                           in1=pg[:, 0:TT], op=ALU.mult),
                   ['G' + sfx, 'pb%d' % bk], ['t1%d' % q])
                ew('dve', lambda e, c=c, pg=pg, sl=sl, q=q: e.tensor_tensor(out=t2[q], in0=G[sl][:, 8 + c, :],
                                                                           in1=pg[:, TT:2 * TT], op=ALU.mult),
                   ['G' + sfx, 'pb%d' % bk], ['t2%d' % q])
                ew('pool', lambda e, c=c, q=q: e.tensor_tensor(out=MT[:, c, :], in0=t1[q], in1=t2[q], op=ALU.add),
                   ['t1%d' % q, 't2%d' % q], ['MT'])
            for s in range(NSUB):
                for dh in range(2):
                    bk = 3 + (s * 2 + dh) % 2
                    pd = bank(bk)
                    for c in range(8):
                        ew('pe', lambda e, c=c, s=s, dh=dh, pd=pd: e.matmul(
                            pd, lhsT=MT[:, c, s * 128:(s + 1) * 128], rhs=Wo[:, c, dh * 512:(dh + 1) * 512],
                            start=(c == 0), stop=(c == 7)), ['MT', 'Wo'], ['pb%d' % bk])
                    ew('dve', lambda e, s=s, dh=dh, pd=pd, sl=sl: e.tensor_tensor(
                        out=X[sl][:, s, dh * 512:(dh + 1) * 512], in0=X[sl][:, s, dh * 512:(dh + 1) * 512], in1=pd,
                        op=ALU.add), ['pb%d' % bk, 'X' + sfx], ['X' + sfx])
            P.op('sp', lambda e, it=it, sl=sl: e.dma_start(out=dstv[it], in_=X[sl]),
                 reads=[R_('X' + sfx)], writes=['d_x2'], dma_key=R_('stX' + sfx))

    if 'ffn1' in stages:
        ffn_stage(0, x, x1_d)
        P.sync_all()
    if 'proj' in stages:
        proj_stage()
        P.sync_all()
    if 'rwkv' in stages:
        rwkv_stage(range(NPAIRS_DBG))
        P.sync_all()
    if 'attn' in stages:
        attn_stage_full(JS_DBG)
        P.sync_all()
    if 'merge' in stages:
        merge_stage()
        P.sync_all()
    if 'ffn2' in stages:
        ffn_stage(1, x2_d if 'merge' in stages else x1_d, out)
    P.sync_all()
    P.op('sp', None)
    P.finalize_and_emit(stack)
    stack.close()
    return nc


_CACHE = {}


SHARED_KEYS = ['ffn1_norm', 'ffn1_w_in', 'ffn1_w_out', 'ffn2_norm', 'ffn2_w_in', 'ffn2_w_out',
               'w_in', 'mix_norm', 'rwkv_mu', 'b_gate', 'attn_q_norm', 'attn_k_norm',
               'w_proj_rwkv', 'w_proj_attn', 'w_out', 'rwkv_w2', 'rwkv_a2', 'rwkv_g2', 'rwkv_w0', 'rwkv_a0', 'rwkv_k_k', 'rwkv_k_a', 'rwkv_r_k', 'rwkv_ln_w', 'rwkv_ln_b']


def make_shared(inputs):
    shared = {}
    for k in SHARED_KEYS:
        v = np.asarray(inputs[k], dtype=np.float32)
        v = v.reshape(v.shape[1:])
        if k == 'rwkv_r_k':
            v = v.reshape(-1)
        shared[k] = np.ascontiguousarray(v)
    return shared


def kernel(**inputs):
    if 'nc' not in _CACHE:
        _CACHE['nc'] = build_program()
    nc = _CACHE['nc']
    x = np.ascontiguousarray(inputs['x'], dtype=np.float32)
    shared = make_shared(inputs)
    in_maps = []
    for c in range(NCORES):
        m = dict(shared)
        m['x'] = x[c]
        in_maps.append(m)
    res = run_bass_kernel_spmd(nc, in_maps, core_ids=list(range(NCORES)))
    return np.stack([np.asarray(r['out']) for r in res.results], axis=0)
```

```python
import numpy as np
from contextlib import ExitStack
import concourse.bass as bass
import concourse.mybir as mybir
from concourse.bass_utils import run_bass_kernel_spmd
from concourse.alu_op_type import AluOpType as ALU

F32 = mybir.dt.float32
BF16 = mybir.dt.bfloat16
AF = mybir.ActivationFunctionType
AX = mybir.AxisListType

S = 4096
D = 1024
DFF = 2816
NCORES = 8
RMS_EPS = 1e-6

ENGS = ['pe', 'act', 'dve', 'pool', 'sp']
MAXOPS = [0]
SEM_LIM = 30000
DMA_LIM = 1800


class Prog:
    def __init__(self, nc):
        self.nc = nc
        self.ops = []
        self.eng_ops = {e: [] for e in ENGS}
        self.last_w = {}
        self.readers = {}
        self.dma_cnt = {}
        self.barrier = {e: None for e in ENGS}

    def op(self, eng, fn, reads=(), writes=(), dma_key=None):
        mo = MAXOPS[0]
        if mo and len(self.ops) >= mo and fn is not None:
            return None
        if mo and len(self.ops) == mo - 1 and fn is not None:
            print("LAST OP:", eng, fn.__code__.co_firstlineno, reads, writes)
        oid = len(self.ops)
        deps = set()
        dma_deps = {}
        writes = list(writes) + [r for r in reads if (r.startswith('pb') or r.startswith('ps')) and r not in writes]

        def add(o):
            od = self.ops[o]
            if od['dma_key'] is not None:
                k = od['dma_key']
                dma_deps[k] = self.dma_cnt[k]
            else:
                deps.add(o)
        for r in reads:
            if r in self.last_w:
                add(self.last_w[r])
        for w in writes:
            if w in self.last_w:
                add(self.last_w[w])
            for rd in self.readers.get(w, {}).values():
                add(rd)
        if self.barrier[eng] is not None:
            bd, bdma = self.barrier[eng]
            for o in bd:
                deps.add(o)
            for k, v in bdma.items():
                dma_deps[k] = max(dma_deps.get(k, 0), v)
            self.barrier[eng] = None
        cnt = None
        if dma_key is not None:
            self.dma_cnt[dma_key] = self.dma_cnt.get(dma_key, 0) + 1
            cnt = self.dma_cnt[dma_key]
        o = dict(id=oid, eng=eng, fn=fn, deps=deps, dma_deps=dma_deps, dma_key=dma_key,
                 dma_cnt=cnt, idx=len(self.eng_ops[eng]), sig=False)
        self.ops.append(o)
        self.eng_ops[eng].append(o)
        ch = eng if dma_key is None else 'dma:' + dma_key
        for r in reads:
            self.readers.setdefault(r, {})[ch] = oid
        for w in writes:
            self.last_w[w] = oid
            self.readers[w] = {}
        return oid

    def sync_all(self):
        bd = set()
        for e in ENGS:
            for o in reversed(self.eng_ops[e]):
                if o['dma_key'] is None and o['fn'] is not None:
                    bd.add(o['id'])
                    break
        bdma = dict(self.dma_cnt)
        for e in ENGS:
            self.barrier[e] = (set(bd), dict(bdma))

    def finalize_and_emit(self, stack):
        nc = self.nc
        for o in self.ops:
            per = {}
            for d in o['deps']:
                od = self.ops[d]
                if od['eng'] == 'pe' and o['eng'] == 'pe':
                    continue
                e = od['eng']
                if e not in per or self.ops[per[e]]['idx'] < od['idx']:
                    per[e] = d
            o['cdeps'] = per
            for d in per.values():
                self.ops[d]['sig'] = True
        sems = {}

        def get_sem(name):
            return sems[name]
        for e in ENGS:
            c = 0
            for o in self.eng_ops[e]:
                if o['dma_key'] is None and o['sig']:
                    c += 1
                    o['sigval'] = c
        for o in self.ops:
            waits = {}
            for e, d in o['cdeps'].items():
                v = self.ops[d]['sigval']
                key = ('c_%s_%d' % (e, (v - 1) // SEM_LIM))
                val = (v - 1) % SEM_LIM + 1
                waits[key] = max(waits.get(key, 0), val)
            for k, n in o['dma_deps'].items():
                key = ('d_%s_%d' % (k, (n - 1) // DMA_LIM))
                val = 16 * ((n - 1) % DMA_LIM + 1)
                waits[key] = max(waits.get(key, 0), val)
            o['waits'] = waits
        names = set()
        for o in self.ops:
            names.update(o['waits'].keys())
            if o['dma_key'] is not None:
                names.add('d_%s_%d' % (o['dma_key'], (o['dma_cnt'] - 1) // DMA_LIM))
            elif o['sig']:
                names.add('c_%s_%d' % (o['eng'], (o['sigval'] - 1) // SEM_LIM))
        for nm in sorted(names):
            sems[nm] = stack.enter_context(nc.semaphore(nm))
        print("n_sems", len(names), "n_ops", len(self.ops), {e: len(v) for e, v in self.eng_ops.items()})
        block = stack.enter_context(nc.Block())
        decos = {'pe': block.tensor, 'act': block.scalar, 'dve': block.vector,
                 'pool': block.gpsimd, 'sp': block.sync}
        for e in ENGS:
            ops = self.eng_ops[e]

            def body(eng, ops=ops, e=e):
                waited = {}
                for o in ops:
                    for key, val in o['waits'].items():
                        if waited.get(key, 0) >= val:
                            continue
                        waited[key] = val
                        eng.wait_ge(get_sem(key), val)
                    if o['fn'] is None:
                        continue
                    ins = o['fn'](eng)
                    if o['dma_key'] is not None:
                        n = o['dma_cnt']
                        ins.then_inc(get_sem('d_%s_%d' % (o['dma_key'], (n - 1) // DMA_LIM)), 16)
                    elif o['sig']:
                        v = o['sigval']
                        ins.then_inc(get_sem('c_%s_%d' % (e, (v - 1) // SEM_LIM)), 1)
            decos[e](body)


class Arena:
    def __init__(self, tensor, nbytes):
        self.t = tensor
        self.nbytes = nbytes
        self.off = 0

    def alloc(self, shape, dtype, parts=128):
        n = int(np.prod(shape))
        esz = 4 if dtype == F32 else 2
        nb = n * esz
        nb_al = (nb + 63) // 64 * 64
        assert self.off + nb_al <= self.nbytes, ("SBUF arena overflow", self.off, nb_al)
        ap = self.t[0:parts, self.off // 2:(self.off + nb) // 2]
        self.off += nb_al
        if dtype == F32:
            ap = ap.bitcast(F32)
        if len(shape) == 2:
            ap = ap.rearrange("p (a b) -> p a b", a=shape[0], b=shape[1])
        elif len(shape) == 3:
            ap = ap.rearrange("p (a b c) -> p a b c", a=shape[0], b=shape[1], c=shape[2])
        return ap


def build_program(debug=False, NPAIRS_DBG=8, stages=('ffn1', 'proj', 'rwkv', 'attn', 'merge', 'ffn2'), NB_DBG=None,
                  JS_DBG=range(4), NT_DBG=None):
    nc = bass.Bass("TRN2", target_bir_lowering=False)
    P = Prog(nc)

    def din(name, shape):
        return nc.dram_tensor(name, list(shape), F32, kind="ExternalInput").ap()
    x = din("x", [S, D])
    ffn_norm = [din("ffn1_norm", [D]), din("ffn2_norm", [D])]
    ffn_win = [din("ffn1_w_in", [D, 2 * DFF]), din("ffn2_w_in", [D, 2 * DFF])]
    ffn_wout = [din("ffn1_w_out", [DFF, D]), din("ffn2_w_out", [DFF, D])]
    out = nc.dram_tensor("out", [S, D], F32, kind="ExternalOutput").ap()
    x1_d = nc.dram_tensor("x1_scr", [S, D], F32, kind="ExternalOutput" if debug else "Internal").ap()

    stack = ExitStack()
    ARENA_BYTES = 207 * 1024
    arena_t = stack.enter_context(nc.sbuf_tensor("arena", [128, ARENA_BYTES // 2], BF16))
    A = Arena(arena_t, ARENA_BYTES)
    psum = stack.enter_context(nc.psum_tensor("psum", [128, 4096], F32))

    def bank(b, n=512, off=0):
        return psum[:, b * 512 + off:b * 512 + off + n]

    ident_f = A.alloc([128], F32)
    ident = A.alloc([128], BF16)
    ones_col = A.alloc([1], F32)

    P.op('pool', lambda e: e.memset(ident_f, 0.0), writes=['ident_f'])
    P.op('pool', lambda e: e.affine_select(out=ident_f, in_=ident_f, pattern=[[-1, 128]],
                                           compare_op=ALU.not_equal, fill=1.0, base=0, channel_multiplier=1),
         reads=['ident_f'], writes=['ident_f'])
    P.op('dve', lambda e: e.tensor_copy(out=ident, in_=ident_f), reads=['ident_f'], writes=['ident'])

    const_mark = A.off

    TT = 256
    NSUB = TT // 128
    NT = S // TT if NT_DBG is None else NT_DBG
    KC = D // 128
    FC = DFF // 128

    def ffn_stage(si, src, dst):
        A.off = const_mark
        TT = 512
        NSUB = TT // 128
        NT = (S // TT) if NT_DBG is None else NT_DBG
        W1 = A.alloc([KC, 2 * DFF], BF16)
        W2 = A.alloc([FC, D], BF16)
        gb = A.alloc([D], F32)
        xt = [A.alloc([NSUB, D], F32)] * 2
        xc = [A.alloc([512], F32) for _ in range(3)]
        hb = [A.alloc([D], BF16) for _ in range(2)]
        hT = [A.alloc([KC, TT], BF16) for _ in range(2)]
        actT = A.alloc([FC, TT], BF16)
        sg = [A.alloc([TT], F32) for _ in range(2)]
        ss = A.alloc([8], F32)
        pre = 's%d_' % si
        w1v = ffn_win[si].rearrange("(kc p) f -> p kc f", p=128)
        CH = 1408
        for kc in range(KC):
            for c in range(2 * DFF // CH):
                P.op('pool', lambda e, kc=kc, c=c: e.dma_start(out=W1[:, kc, c * CH:(c + 1) * CH],
                                                              in_=w1v[:, kc, c * CH:(c + 1) * CH]),
                     writes=[pre + 'W1'], dma_key=pre + 'W1')
        w2v = ffn_wout[si].rearrange("(fc p) d -> p fc d", p=128)
        for fc in range(FC):
            P.op('pool', lambda e, fc=fc: e.dma_start(out=W2[:, fc, :], in_=w2v[:, fc, :]),
                 writes=[pre + 'W2'], dma_key=pre + 'W2')
        P.op('sp', lambda e: e.dma_start(out=gb, in_=ffn_norm[si].partition_broadcast(128)),
             writes=[pre + 'gb'], dma_key=pre + 'gb')
        srcv = src.rearrange("(n s p) d -> n p s d", p=128, s=NSUB)
        dstv = dst.rearrange("(n s p) d -> n p s d", p=128, s=NSUB)
        for it in range(NT):
            sl = it % 2
            X = xt[sl]
            xr = pre + 'xt'
            if it == 0:
                P.op('sp', lambda e, X=X: e.dma_start(out=X, in_=srcv[0]), writes=[xr], dma_key=xr)
            HT = hT[sl]
            for s in range(NSUB):
                hs = (it * NSUB + s) % 2
                H = hb[hs]
                hr = pre + 'hb%d' % hs
                P.op('act', lambda e, X=X, s=s, H=H: e.activation(out=H, in_=X[:, s, :], func=AF.Square,
                                                             accum_out=ss[:, 0:1]),
                     reads=[xr], writes=[hr, pre + 'ss'])
                P.op('act', lambda e: e.activation(out=ss[:, 1:2], in_=ss[:, 0:1], func=AF.Sqrt,
                                                   scale=1.0 / D, bias=RMS_EPS),
                     reads=[pre + 'ss'], writes=[pre + 'ss1'])
                P.op('dve', lambda e: e.reciprocal(out=ss[:, 2:3], in_=ss[:, 1:2]),
                     reads=[pre + 'ss1'], writes=[pre + 'ss2'])
                P.op('dve', lambda e, X=X, s=s, H=H: e.scalar_tensor_tensor(
                    out=H, in0=X[:, s, :], scalar=ss[:, 2:3], in1=gb, op0=ALU.mult, op1=ALU.mult),
                    reads=[xr, pre + 'ss2', pre + 'gb'], writes=[hr])
                pT = bank(0).bitcast(BF16)
                for kc in range(KC):
                    P.op('pe', lambda e, kc=kc, H=H, pT=pT: e.transpose(
                        out=pT[:, kc * 128:(kc + 1) * 128], in_=H[:, kc * 128:(kc + 1) * 128], identity=ident),
                        reads=[hr, 'ident'], writes=['psT'])
                P.op('act', lambda e, HT=HT, s=s, pT=pT: e.copy(
                    out=HT[:, :, s * 128:(s + 1) * 128], in_=pT.rearrange("p (k t) -> p k t", k=KC)),
                    reads=['psT'], writes=[pre + 'hT%d' % sl])
            if it + 1 < NT:
                P.op('sp', lambda e, it=it, X=X: e.dma_start(out=X, in_=srcv[it + 1]), writes=[xr], dma_key=xr)
            for fc in range(FC):
                bg = 1 + 2 * (fc % 2)
                bu = bg + 1
                pgate = bank(bg)
                pup = bank(bu)
                for half, pdst, br in ((0, pgate, bg), (1, pup, bu)):
                    col = half * DFF + fc * 128
                    for kc in range(KC):
                        P.op('pe', lambda e, kc=kc, col=col, pdst=pdst, HT=HT: e.matmul(
                            pdst, lhsT=W1[:, kc, col:col + 128], rhs=HT[:, kc, :],
                            start=(kc == 0), stop=(kc == KC - 1)),
                            reads=[pre + 'W1', pre + 'hT%d' % sl], writes=['psG%d' % br])
                SG = sg[fc % 2]
                P.op('act', lambda e, pgate=pgate, SG=SG: e.activation(out=SG, in_=pgate, func=AF.Silu),
                     reads=['psG%d' % bg], writes=[pre + 'sg%d' % (fc % 2)])
                P.op('dve', lambda e, pup=pup, SG=SG, fc=fc: e.tensor_tensor(
                    out=actT[:, fc, :], in0=SG, in1=pup, op=ALU.mult),
                    reads=['psG%d' % bu, pre + 'sg%d' % (fc % 2)], writes=[pre + 'actT'])
            for s in range(NSUB):
                for dh in range(2):
                    b = 5 + (s * 2 + dh) % 2
                    pd = bank(b)
                    for fc in range(FC):
                        P.op('pe', lambda e, fc=fc, s=s, dh=dh, pd=pd: e.matmul(
                            pd, lhsT=actT[:, fc, s * 128:(s + 1) * 128], rhs=W2[:, fc, dh * 512:(dh + 1) * 512],
                            start=(fc == 0), stop=(fc == FC - 1)),
                            reads=[pre + 'actT', pre + 'W2'], writes=['psD%d' % b])
                    k = (it * NSUB * 2 + s * 2 + dh) % 3
                    XC = xc[k]
                    xcr = pre + 'xc%d' % k
                    P.op('sp', lambda e, it=it, s=s, dh=dh, XC=XC: e.dma_start(
                        out=XC, in_=srcv[it][:, s, dh * 512:(dh + 1) * 512]), writes=[xcr], dma_key=xcr)
                    P.op('dve', lambda e, XC=XC, pd=pd: e.scalar_tensor_tensor(
                        out=XC, in0=pd, scalar=0.5, in1=XC, op0=ALU.mult, op1=ALU.add),
                        reads=['psD%d' % b, xcr], writes=[xcr])
                    P.op('sp', lambda e, it=it, s=s, dh=dh, XC=XC: e.dma_start(
                        out=dstv[it][:, s, dh * 512:(dh + 1) * 512], in_=XC),
                        reads=[xcr], writes=[pre + 'dst'], dma_key=xcr)

    NCOL = 7712
    w_in = din("w_in", [D, NCOL])
    mix_norm = din("mix_norm", [D])
    rwkv_mu = din("rwkv_mu", [3360])
    b_gate = din("b_gate", [2048])
    qn = din("attn_q_norm", [64])
    kn = din("attn_k_norm", [64])
    kscr = "ExternalOutput" if debug else "Internal"
    if 'proj' not in stages:
        kscr = "ExternalInput"
    rkv_d = nc.dram_tensor("rkv_scr", [24, 128, S], F32, kind=kscr).ap()
    ta_d = nc.dram_tensor("ta_scr", [128, S], BF16, kind=kscr).ap()
    tg_d = nc.dram_tensor("tg_scr", [160, S], BF16, kind=kscr).ap()
    qk_d = nc.dram_tensor("qk_scr", [12, 128, S], BF16, kind=kscr).ap()
    v_d = nc.dram_tensor("v_scr", [S, 768], BF16, kind=kscr).ap()
    gate_d = nc.dram_tensor("gate_scr", [16, 128, S], BF16, kind=kscr).ap()

    def proj_stage():
        A.off = const_mark
        pre = 'p_'
        W = A.alloc([KC, NCOL], BF16)
        gb = A.alloc([D], F32)
        X = A.alloc([NSUB, D], F32)
        hb = [A.alloc([D], BF16) for _ in range(2)]
        hT = [A.alloc([KC, TT], BF16) for _ in range(2)]
        ss = A.alloc([8], F32)
        mu_t = A.alloc([27], F32)
        bg_t = A.alloc([16], F32)
        qg_t = A.alloc([2], F32)
        carry = A.alloc([27], F32)
        psb = [A.alloc([TT + 1], F32) for _ in range(2)]
        tmp = [A.alloc([TT], F32) for _ in range(2)]
        sq = [A.alloc([TT], BF16) for _ in range(2)]
        lnb = [A.alloc([TT], F32) for _ in range(2)]
        rkv_st = A.alloc([24, TT], F32)
        ta_st = A.alloc([TT], BF16)
        tg_st = A.alloc([2, TT], BF16)
        qk_st = A.alloc([12, TT], BF16)
        gate_st = A.alloc([16, TT], BF16)
        v_st = A.alloc([NSUB, 768], BF16)
        bones = A.alloc([128], BF16)
        print("proj_stage arena", A.off)
        wv = w_in.rearrange("(kc p) f -> p kc f", p=128)
        CH = 964
        for kc in range(KC):
            for c in range(NCOL // CH):
                P.op('pool', lambda e, kc=kc, c=c: e.dma_start(out=W[:, kc, c * CH:(c + 1) * CH],
                                                              in_=wv[:, kc, c * CH:(c + 1) * CH]),
                     writes=[pre + 'W'], dma_key=pre + 'W')
        P.op('sp', lambda e: e.dma_start(out=gb, in_=mix_norm.partition_broadcast(128)),
             writes=[pre + 'gb'], dma_key=pre + 'par')
        P.op('sp', lambda e: e.dma_start(out=mu_t[:, 0:26], in_=rwkv_mu[0:3328].rearrange("(b p) -> p b", p=128),
                                         allow_slow_non_contiguous=True), writes=[pre + 'mu'], dma_key=pre + 'par')
        P.op('sp', lambda e: e.dma_start(out=mu_t[0:32, 26:27], in_=rwkv_mu[3328:3360].rearrange("(p o) -> p o", o=1)),
             writes=[pre + 'mu'], dma_key=pre + 'par')
        P.op('sp', lambda e: e.dma_start(out=bg_t, in_=b_gate.rearrange("(b p) -> p b", p=128),
                                         allow_slow_non_contiguous=True), writes=[pre + 'bg'], dma_key=pre + 'par')
        for hh in range(2):
            P.op('sp', lambda e, hh=hh: e.dma_start(out=qg_t[hh * 64:(hh + 1) * 64, 0:1],
                                                     in_=qn.rearrange("(p o) -> p o", o=1)),
                 writes=[pre + 'qg'], dma_key=pre + 'par')
            P.op('sp', lambda e, hh=hh: e.dma_start(out=qg_t[hh * 64:(hh + 1) * 64, 1:2],
                                                     in_=kn.rearrange("(p o) -> p o", o=1)),
                 writes=[pre + 'qg'], dma_key=pre + 'par')
        P.op('pool', lambda e: e.tensor_scalar(out=qg_t[:, 0:1], in0=qg_t[:, 0:1], scalar1=0.125, scalar2=None,
                                               op0=ALU.mult), reads=[pre + 'qg'], writes=[pre + 'qg'])
        P.op('pool', lambda e: e.memset(carry, 0.0), writes=[pre + 'carry%d' % i for i in range(27)])
        P.op('pool', lambda e: e.memset(bones, 0.0), writes=[pre + 'bones'])
        P.op('pool', lambda e: e.memset(bones[0:64, 0:64], 1.0), reads=[pre + 'bones'], writes=[pre + 'bones'])
        P.op('pool', lambda e: e.memset(bones[64:128, 64:128], 1.0), reads=[pre + 'bones'], writes=[pre + 'bones'])

        blocks = []
        for b in range(24):
            blocks.append((b * 128, 128, 'rkv', b))
        blocks.append((3072, 128, 'ta', 24))
        blocks.append((3200, 128, 'tg0', 25))
        blocks.append((3328, 32, 'tg1', 26))
        for b in range(6):
            blocks.append((3360 + b * 128, 128, 'q', b))
        for b in range(6):
            blocks.append((4128 + b * 128, 128, 'k', 6 + b))
        for b in range(16):
            blocks.append((5664 + b * 128, 128, 'gate', b))

        srcv = x1_d.rearrange("(n s p) d -> n p s d", p=128, s=NSUB)
        xr = pre + 'X'
        for it in range(NT):
            t0 = it * TT
            sl = it % 2
            if it == 0:
                P.op('sp', lambda e: e.dma_start(out=X, in_=srcv[0]), writes=[xr], dma_key=xr)
            HT = hT[sl]
            htr = pre + 'hT%d' % sl
            for s in range(NSUB):
                hs = (it * NSUB + s) % 2
                H = hb[hs]
                hr = pre + 'hb%d' % hs
                P.op('act', lambda e, s=s, H=H: e.activation(out=H, in_=X[:, s, :], func=AF.Square,
                                                             accum_out=ss[:, 0:1]),
                     reads=[xr], writes=[hr, pre + 'ss'])
                P.op('act', lambda e: e.activation(out=ss[:, 1:2], in_=ss[:, 0:1], func=AF.Sqrt,
                                                   scale=1.0 / D, bias=RMS_EPS),
                     reads=[pre + 'ss'], writes=[pre + 'ss1'])
                P.op('dve', lambda e: e.reciprocal(out=ss[:, 2:3], in_=ss[:, 1:2]),
                     reads=[pre + 'ss1'], writes=[pre + 'ss2'])
                P.op('dve', lambda e, s=s, H=H: e.scalar_tensor_tensor(
                    out=H, in0=X[:, s, :], scalar=ss[:, 2:3], in1=gb, op0=ALU.mult, op1=ALU.mult),
                    reads=[xr, pre + 'ss2', pre + 'gb'], writes=[hr])
                pT = bank(0).bitcast(BF16)
                for kc in range(KC):
                    P.op('pe', lambda e, kc=kc, H=H, pT=pT: e.transpose(
                        out=pT[:, kc * 128:(kc + 1) * 128], in_=H[:, kc * 128:(kc + 1) * 128], identity=ident),
                        reads=[hr, 'ident'], writes=['psT'])
                P.op('act', lambda e, HT=HT, s=s, pT=pT: e.copy(
                    out=HT[:, :, s * 128:(s + 1) * 128], in_=pT.rearrange("p (k t) -> p k t", k=KC)),
                    reads=['psT'], writes=[htr])

            if it + 1 < NT:
                P.op('sp', lambda e, it=it: e.dma_start(out=X, in_=srcv[it + 1]), writes=[xr], dma_key=xr)
            pending = [None]

            def flush():
                if pending[0] is None:
                    return
                pg, pgr, j, kind, idx = pending[0]
                pending[0] = None
                pss = bank(5)[:, 0:TT]
                P.op('pe', lambda e, j=j, pss=pss: e.matmul(pss, lhsT=bones, rhs=sq[j], start=True, stop=True),
                     reads=[pre + 'sq%d' % j, pre + 'bones'], writes=['pss'])
                P.op('act', lambda e, j=j, pss=pss: e.activation(out=lnb[j], in_=pss, func=AF.Ln,
                                                                 scale=1.0 / 64, bias=RMS_EPS),
                     reads=['pss'], writes=[pre + 'lnb%d' % j])
                P.op('act', lambda e, j=j: e.activation(out=lnb[j], in_=lnb[j], func=AF.Exp, scale=-0.5),
                     reads=[pre + 'lnb%d' % j], writes=[pre + 'lnb%d' % j])
                c = 0 if kind == 'q' else 1
                P.op('dve', lambda e, j=j, pg=pg, idx=idx, c=c: e.scalar_tensor_tensor(
                    out=qk_st[:, idx, :], in0=pg, scalar=qg_t[:, c:c + 1], in1=lnb[j], op0=ALU.mult, op1=ALU.mult),
                    reads=[pgr, pre + 'lnb%d' % j, pre + 'qg'], writes=[pre + 'qk_st'])

            for bi, (col0, M, kind, idx) in enumerate(blocks):
                b = 1 + bi % 4
                pgr = 'psG%d' % b
                pg = bank(b)[0:M, 0:TT]
                j = bi % 2
                for kc in range(KC):
                    P.op('pe', lambda e, kc=kc, col0=col0, M=M, pg=pg, HT=HT: e.matmul(
                        pg, lhsT=W[:, kc, col0:col0 + M], rhs=HT[:, kc, :], start=(kc == 0), stop=(kc == KC - 1)),
                        reads=[pre + 'W', htr], writes=[pgr])
                flush()
                if kind in ('rkv', 'ta', 'tg0', 'tg1'):
                    cr = pre + 'carry%d' % idx
                    pbr = pre + 'psb%d' % j
                    tr = pre + 'tmp%d' % j
                    PS = psb[j][0:M]
                    TM = tmp[j][0:M]
                    P.op('pool', lambda e, PS=PS, idx=idx, M=M: e.tensor_copy(out=PS[:, 0:1], in_=carry[0:M, idx:idx + 1]),
                         reads=[cr], writes=[pbr])
                    P.op('act', lambda e, PS=PS, pg=pg: e.copy(out=PS[:, 1:TT + 1], in_=pg),
                         reads=[pgr, pbr], writes=[pbr])
                    P.op('dve', lambda e, PS=PS, TM=TM: e.tensor_tensor(out=TM, in0=PS[:, 0:TT], in1=PS[:, 1:TT + 1],
                                                                        op=ALU.subtract),
                         reads=[pbr], writes=[tr])
                    P.op('pool', lambda e, PS=PS, idx=idx, M=M: e.tensor_copy(out=carry[0:M, idx:idx + 1],
                                                                              in_=PS[:, TT:TT + 1]),
                         reads=[pbr], writes=[cr])
                    if kind == 'rkv':
                        P.op('dve', lambda e, PS=PS, TM=TM, idx=idx: e.scalar_tensor_tensor(
                            out=rkv_st[:, idx, :], in0=TM, scalar=mu_t[:, idx:idx + 1], in1=PS[:, 1:TT + 1],
                            op0=ALU.mult, op1=ALU.add),
                            reads=[tr, pbr, pre + 'mu'], writes=[pre + 'rkv_st%d' % (idx // 8)])
                    else:
                        P.op('dve', lambda e, PS=PS, TM=TM, idx=idx, M=M: e.scalar_tensor_tensor(
                            out=TM, in0=TM, scalar=mu_t[0:M, idx:idx + 1], in1=PS[:, 1:TT + 1],
                            op0=ALU.mult, op1=ALU.add),
                            reads=[tr, pbr, pre + 'mu'], writes=[tr])
                        if kind == 'ta':
                            P.op('act', lambda e, TM=TM: e.activation(out=ta_st[0:64], in_=TM[0:64], func=AF.Tanh),
                                 reads=[tr], writes=[pre + 'ta_st'])
                            P.op('act', lambda e, TM=TM: e.copy(out=ta_st[64:128], in_=TM[64:128]),
                                 reads=[tr], writes=[pre + 'ta_st'])
                        elif kind == 'tg0':
                            P.op('act', lambda e, TM=TM: e.activation(out=tg_st[:, 0, :], in_=TM, func=AF.Sigmoid),
                                 reads=[tr], writes=[pre + 'tg_st'])
                        else:
                            P.op('act', lambda e, TM=TM: e.activation(out=tg_st[0:32, 1, :], in_=TM, func=AF.Sigmoid),
                                 reads=[tr], writes=[pre + 'tg_st'])
                elif kind in ('q', 'k'):
                    P.op('act', lambda e, pg=pg, j=j: e.activation(out=sq[j], in_=pg, func=AF.Square),
                         reads=[pgr], writes=[pre + 'sq%d' % j])
                    pending[0] = (pg, pgr, j, kind, idx)
                else:
                    P.op('act', lambda e, pg=pg, idx=idx: e.activation(out=gate_st[:, idx, :], in_=pg, func=AF.Sigmoid,
                                                                       bias=bg_t[:, idx:idx + 1]),
                         reads=[pgr, pre + 'bg'], writes=[pre + 'gate_st'])
            flush()
            for s in range(NSUB):
                for (c0, n, b) in ((4896, 512, 6), (5408, 256, 7)):
                    pv = bank(b)[:, 0:n]
                    for kc in range(KC):
                        P.op('pe', lambda e, kc=kc, s=s, c0=c0, n=n, pv=pv, HT=HT: e.matmul(
                            pv, lhsT=HT[:, kc, s * 128:(s + 1) * 128], rhs=W[:, kc, c0:c0 + n],
                            start=(kc == 0), stop=(kc == KC - 1)),
                            reads=[pre + 'W', htr], writes=['psV%d' % b])
                P.op('act', lambda e, s=s: e.copy(out=v_st[:, s, 0:512], in_=bank(6)),
                     reads=['psV6'], writes=[pre + 'v_st'])
                P.op('dve', lambda e, s=s: e.tensor_copy(out=v_st[:, s, 512:768], in_=bank(7)[:, 0:256]),
                     reads=['psV7'], writes=[pre + 'v_st'])
            rv = rkv_d.rearrange("b p t -> p b t")
            for g in range(3):
                P.op('sp', lambda e, g=g, t0=t0: e.dma_start(out=rv[:, g * 8:(g + 1) * 8, t0:t0 + TT],
                                                             in_=rkv_st[:, g * 8:(g + 1) * 8, :]),
                     reads=[pre + 'rkv_st%d' % g], writes=['d_rkv'], dma_key=pre + 'rkv_st%d' % g)
            P.op('sp', lambda e, t0=t0: e.dma_start(out=ta_d[:, t0:t0 + TT], in_=ta_st),
                 reads=[pre + 'ta_st'], writes=['d_ta'], dma_key=pre + 'ta_st')
            P.op('sp', lambda e, t0=t0: e.dma_start(out=tg_d[0:128, t0:t0 + TT], in_=tg_st[:, 0, :]),
                 reads=[pre + 'tg_st'], writes=['d_tg'], dma_key=pre + 'tg_st')
            P.op('sp', lambda e, t0=t0: e.dma_start(out=tg_d[128:160, t0:t0 + TT], in_=tg_st[0:32, 1, :]),
                 reads=[pre + 'tg_st'], writes=['d_tg'], dma_key=pre + 'tg_st')
            P.op('sp', lambda e, t0=t0: e.dma_start(out=qk_d.rearrange("b p t -> p b t")[:, :, t0:t0 + TT], in_=qk_st),
                 reads=[pre + 'qk_st'], writes=['d_qk'], dma_key=pre + 'qk_st')
            P.op('sp', lambda e, t0=t0: e.dma_start(out=gate_d.rearrange("b p t -> p b t")[:, :, t0:t0 + TT],
                                                    in_=gate_st),
                 reads=[pre + 'gate_st'], writes=['d_gate'], dma_key=pre + 'gate_st')
            P.op('sp', lambda e, t0=t0: e.dma_start(
                out=v_d[t0:t0 + TT, :].rearrange("(s p) c -> p s c", p=128), in_=v_st),
                reads=[pre + 'v_st'], writes=['d_v'], dma_key=pre + 'v_st')

    w2_d = din("rwkv_w2", [64, 1024])
    a2_d = din("rwkv_a2", [64, 1024])
    g2_d = din("rwkv_g2", [160, 1024])
    prm_names = ['rwkv_w0', 'rwkv_a0', 'rwkv_k_k', 'rwkv_k_a', 'rwkv_r_k', 'rwkv_ln_w', 'rwkv_ln_b']
    prm_d = [din(n, [1024]) for n in prm_names]
    ya_d = nc.dram_tensor("ya_scr", [8, 128, S], BF16, kind="ExternalOutput" if debug else "Internal").ap()
    TB = 1024
    NCH = TB // 128
    NB = S // TB if NB_DBG is None else NB_DBG
    C0 = float(np.exp(-0.5))
    GN_EPS = 64e-5

    def rwkv_stage(pairs=range(8)):
        A.off = const_mark
        pre = 'r_'
        WA = A.alloc([1024], BF16)
        G2a = A.alloc([1024], BF16)
        G2b = A.alloc([1024], BF16)
        prm = A.alloc([7, 8], F32)
        bones = A.alloc([128], BF16)
        mk4 = A.alloc([512], F32)
        mkL = A.alloc([2, 128], F32)
        E2 = A.alloc([64], F32)
        mrow = A.alloc([TB], F32)
        f32names = ['R', 'K', 'V', 'SG', 'AA', 'GG', 'KK', 'KM', 'T1', 'BVEC', 'CS', 'T2', 'T3', 'EP', 'EN', 'EPM',
                    'EC', 'BV', 'Y32', 'DD']
        T = {n: A.alloc([TB], F32) for n in f32names}
        bfnames = ['TA', 'TG0', 'TG1', 'TQ', 'BT', 'KT', 'BH', 'KH', 'VT', 'YB', 'YO']
        for n in bfnames:
            T[n] = A.alloc([TB], BF16)
        AR = A.alloc([NCH, 2, 128], BF16)
        TM4 = A.alloc([NCH, 4, 128], BF16)
        PC = A.alloc([NCH], F32)
        SC = [[A.alloc([512], BF16) for _ in range(2)] for _ in range(2)]
        LZ = [[A.alloc([2, 384], BF16) for _ in range(2)] for _ in range(2)]
        MCz = [A.alloc([2, 64], BF16) for _ in range(2)]
        QT = [A.alloc([128], BF16) for _ in range(2)]
        STz = A.alloc([2, 64], BF16)
        DBT = ['BT', 'KT', 'GG', 'BV', 'Y32']
        T2 = {n: [T[n], A.alloc([TB], BF16 if n in ('BT', 'KT') else F32)] for n in DBT}
        AR2 = [AR, A.alloc([NCH, 2, 128], BF16)]
        TM42 = [TM4, A.alloc([NCH, 4, 128], BF16)]
        PC2 = [PC, A.alloc([NCH], F32)]
        par = [0]
        coll = [None]

        class TP(dict):
            def __getitem__(self, k):
                if k in T2:
                    return T2[k][par[0]]
                return dict.__getitem__(self, k)

        class BP(object):
            def __init__(self, bufs):
                self.bufs = bufs

            def __getitem__(self, idx):
                return self.bufs[par[0]][idx]

            def rearrange(self, *a_, **k_):
                return self.bufs[par[0]].rearrange(*a_, **k_)
        T = TP(T)
        AR = BP(AR2)
        TM4 = BP(TM42)
        PC = BP(PC2)
        DBNAMES = set(DBT) | {'AR0', 'AR1', 'PC'} | {'TM4_%d' % c_ for c_ in range(NCH)}
        print("rwkv_stage arena", A.off)

        def R_(n):
            return pre + n
        P.op('pool', lambda e: e.dma_start(out=WA[0:64, :], in_=w2_d), writes=[R_('WA')], dma_key=R_('w'))
        P.op('pool', lambda e: e.dma_start(out=WA[64:128, :], in_=a2_d), writes=[R_('WA')], dma_key=R_('w'))
        P.op('pool', lambda e: e.dma_start(out=G2a, in_=g2_d[0:128, :]), writes=[R_('G2')], dma_key=R_('w'))
        P.op('pool', lambda e: e.dma_start(out=G2b[0:32, :], in_=g2_d[128:160, :]), writes=[R_('G2')], dma_key=R_('w'))
        for i in range(7):
            P.op('sp', lambda e, i=i: e.dma_start(out=prm[:, i, :], in_=prm_d[i].rearrange("(b p) -> p b", p=128),
                                                   allow_slow_non_contiguous=True),
                 writes=[R_('prm')], dma_key=R_('par'))
        P.op('pool', lambda e: e.memset(bones, 0.0), writes=[R_('bones')])
        P.op('pool', lambda e: e.memset(bones[0:64, 0:64], 1.0), reads=[R_('bones')], writes=[R_('bones')])
        P.op('pool', lambda e: e.memset(bones[64:128, 64:128], 1.0), reads=[R_('bones')], writes=[R_('bones')])
        P.op('pool', lambda e: e.memset(mk4, 1.0), writes=[R_('mk4')])
        for q in range(4):
            base = -1 if q % 2 == 0 else 0
            P.op('pool', lambda e, q=q, base=base: e.affine_select(
                out=mk4[:, q * 128:(q + 1) * 128], in_=mk4[:, q * 128:(q + 1) * 128], pattern=[[1, 128]],
                compare_op=ALU.is_ge, fill=0.0, base=base, channel_multiplier=-1),
                reads=[R_('mk4')], writes=[R_('mk4')])
        P.op('pool', lambda e: e.memset(mkL, 1.0), writes=[R_('mkL')])
        P.op('pool', lambda e: e.affine_select(out=mkL, in_=mkL, pattern=[[0, 2], [-1, 128]], compare_op=ALU.is_ge,
                                               fill=0.0, base=-1, channel_multiplier=1),
             reads=[R_('mkL')], writes=[R_('mkL')])
        P.op('pool', lambda e: e.tensor_copy(out=E2[0:64, :], in_=ident_f[0:64, 0:64]), reads=['ident_f'], writes=[R_('E2')])
        P.op('pool', lambda e: e.tensor_copy(out=E2[64:128, :], in_=ident_f[64:128, 64:128]), reads=['ident_f'],
             writes=[R_('E2')])
        P.op('pool', lambda e: e.memset(mrow, 1.0), writes=[R_('mrow')])
        P.op('pool', lambda e: e.memset(mrow.rearrange("p (c t) -> p c t", t=128)[:, :, 0:1], 0.0),
             reads=[R_('mrow')], writes=[R_('mrow')])

        def ch3(ap):
            return ap.rearrange("p (c t) -> p c t", t=128)

        def nm(x):
            if x.startswith('pb') or x in ('ident', 'ident_f'):
                return x
            if x in DBNAMES:
                return R_(x) + '@%d' % par[0]
            return R_(x)

        def pdma(eng, fn, reads=(), writes=(), dma_key=None):
            p_ = par[0]

            def fn2(e, fn=fn, p_=p_):
                par[0] = p_
                return fn(e)
            args = (eng, fn2, list(reads), list(writes), dma_key)
            if coll[0] is not None:
                coll[0].append(args)
            else:
                P.op(args[0], args[1], reads=args[2], writes=args[3], dma_key=args[4])

        def ew(eng, fn, reads, writes):
            pdma(eng, fn, [nm(x) for x in reads], [nm(x) for x in writes])

        for sl_ in range(2):
            ew('pool', lambda e, sl_=sl_: e.memset(MCz[sl_], 0.0), [], ['MC%d' % sl_])
        funcs = {}
        for hp in pairs:
            cols = slice(hp * 128, (hp + 1) * 128)
            def prep(tb, hp=hp, cols=cols):
                t0 = tb * TB
                tsl = slice(t0, t0 + TB)
                for i, n in enumerate(['R', 'K', 'V']):
                    pdma('sp', lambda e, i=i, n=n, hp=hp, tsl=tsl: e.dma_start(out=T[n], in_=rkv_d[i * 8 + hp, :, tsl]),
                         writes=[R_(n)], dma_key=R_('ld' + n))
                pdma('sp', lambda e, tsl=tsl: e.dma_start(out=T['TA'], in_=ta_d[:, tsl]), writes=[R_('TA')],
                     dma_key=R_('ldTA'))
                pdma('sp', lambda e, tsl=tsl: e.dma_start(out=T['TG0'], in_=tg_d[0:128, tsl]), writes=[R_('TG0')],
                     dma_key=R_('ldTG0'))
                pdma('sp', lambda e, tsl=tsl: e.dma_start(out=T['TG1'][0:32], in_=tg_d[128:160, tsl]),
                     writes=[R_('TG1')], dma_key=R_('ldTG1'))
                for hf in range(2):
                    hs = slice(hf * 512, (hf + 1) * 512)
                    ew('pe', lambda e, hs=hs, cols=cols: e.matmul(bank(1), lhsT=WA[0:64, cols], rhs=T['TA'][0:64, hs],
                                                                  start=True, stop=True), ['WA', 'TA'], ['pb1'])
                    ew('act', lambda e, hs=hs, hp=hp: e.activation(out=T['SG'][:, hs], in_=bank(1), func=AF.Sigmoid,
                                                                   bias=prm[:, 0, hp:hp + 1]), ['pb1', 'prm'], ['SG'])
                    ew('pe', lambda e, hs=hs, cols=cols: e.matmul(bank(2), lhsT=WA[64:128, cols], rhs=T['TA'][64:128, hs],
                                                                  start=True, stop=True), ['WA', 'TA'], ['pb2'])
                    ew('act', lambda e, hs=hs, hp=hp: e.activation(out=T['AA'][:, hs], in_=bank(2), func=AF.Sigmoid,
                                                                   bias=prm[:, 1, hp:hp + 1]), ['pb2', 'prm'], ['AA'])
                    ew('pe', lambda e, hs=hs, cols=cols: e.matmul(bank(3), lhsT=G2a[:, cols], rhs=T['TG0'][:, hs],
                                                                  start=True, stop=False), ['G2', 'TG0'], ['pb3', 'pb3'])
                    ew('pe', lambda e, hs=hs, cols=cols: e.matmul(bank(3), lhsT=G2b[0:32, cols], rhs=T['TG1'][0:32, hs],
                                                                  start=False, stop=True), ['G2', 'TG1'], ['pb3', 'pb3'])
                    ew('act', lambda e, hs=hs: e.copy(out=T['GG'][:, hs], in_=bank(3)), ['pb3', 'pb3'], ['GG'])
                ew('dve', lambda e, hp=hp: e.tensor_scalar(out=T['KK'], in0=T['K'], scalar1=prm[:, 2, hp:hp + 1],
                                                           scalar2=None, op0=ALU.mult), ['K', 'prm'], ['KK'])
                ew('act', lambda e: e.activation(out=T['TQ'], in_=T['KK'], func=AF.Square), ['KK'], ['TQ'])
                for hf in range(2):
                    hs = slice(hf * 512, (hf + 1) * 512)
                    ew('pe', lambda e, hs=hs: e.matmul(bank(4), lhsT=bones, rhs=T['TQ'][:, hs], start=True, stop=True),
                       ['bones', 'TQ'], ['pb4'])
                    ew('dve', lambda e, hs=hs: e.tensor_scalar(out=T['T1'][:, hs], in0=bank(4), scalar1=1e-19,
                                                               scalar2=None, op0=ALU.max), ['pb4'], ['T1'])
                ew('act', lambda e: e.activation(out=T['T1'], in_=T['T1'], func=AF.Ln), ['T1'], ['T1'])
                ew('act', lambda e: e.activation(out=T['T1'], in_=T['T1'], func=AF.Exp, scale=-0.5), ['T1'], ['T1'])
                ew('dve', lambda e: e.tensor_tensor(out=T['KK'], in0=T['KK'], in1=T['T1'], op=ALU.mult),
                   ['KK', 'T1'], ['KK'])
                ew('dve', lambda e, hp=hp: e.tensor_scalar(out=T['T1'], in0=T['AA'], scalar1=-1.0,
                                                           scalar2=prm[:, 3, hp:hp + 1], op0=ALU.add, op1=ALU.mult),
                   ['AA', 'prm'], ['T1'])
                ew('dve', lambda e: e.scalar_tensor_tensor(out=T['KM'], in0=T['T1'], scalar=1.0, in1=T['K'],
                                                           op0=ALU.add, op1=ALU.mult), ['T1', 'K'], ['KM'])
                ew('pool', lambda e: e.tensor_tensor(out=T['T1'], in0=T['R'], in1=T['KM'], op=ALU.mult),
                   ['R', 'KM'], ['T1'])
                ew('pool', lambda e, hp=hp: e.tensor_scalar(out=T['TQ'], in0=T['T1'], scalar1=prm[:, 4, hp:hp + 1],
                                                            scalar2=None, op0=ALU.mult), ['T1', 'prm'], ['TQ'])
                for hf in range(2):
                    hs = slice(hf * 512, (hf + 1) * 512)
                    ew('pe', lambda e, hs=hs: e.matmul(bank(5), lhsT=bones, rhs=T['TQ'][:, hs], start=True, stop=True),
                       ['bones', 'TQ'], ['pb5'])
                    ew('dve', lambda e, hs=hs: e.tensor_tensor(out=T['BV'][:, hs], in0=T['V'][:, hs], in1=bank(5),
                                                               op=ALU.mult), ['pb5', 'V'], ['BV'])
                ew('pool', lambda e: e.tensor_tensor(out=T['BVEC'], in0=T['KK'], in1=T['AA'], op=ALU.mult),
                   ['KK', 'AA'], ['BVEC'])
                ew('dve', lambda e: e.tensor_tensor_scan(out=T['CS'], data0=mrow, data1=T['SG'], initial=0.0,
                                                         op0=ALU.mult, op1=ALU.add), ['mrow', 'SG'], ['CS'])
                ew('pool', lambda e: e.tensor_tensor(out=T['T2'], in0=T['CS'], in1=T['SG'], op=ALU.subtract),
                   ['CS', 'SG'], ['T2'])
                ew('act', lambda e: e.activation(out=T['EP'], in_=T['CS'], func=AF.Exp, scale=-C0), ['CS'], ['EP'])
                ew('act', lambda e: e.activation(out=T['EN'], in_=T['CS'], func=AF.Exp, scale=C0), ['CS'], ['EN'])
                ew('act', lambda e: e.activation(out=T['EPM'], in_=T['T2'], func=AF.Exp, scale=-C0), ['T2'], ['EPM'])
                ew('dve', lambda e: e.tensor_tensor(
                    out=ch3(T['T3']), in0=ch3(T['CS']), in1=ch3(T['CS'])[:, :, 127:128].to_broadcast([128, NCH, 128]),
                    op=ALU.subtract), ['CS'], ['T3'])
                ew('act', lambda e: e.activation(out=T['EC'], in_=T['T3'], func=AF.Exp, scale=C0), ['T3'], ['EC'])
                ew('act', lambda e: e.activation(out=PC.rearrange("p (c o) -> p c o", o=1),
                                                 in_=ch3(T['CS'])[:, :, 127:128], func=AF.Exp, scale=-C0),
                   ['CS'], ['PC'])
                ew('dve', lambda e: e.scalar_tensor_tensor(out=AR[:, :, 0, :], in0=ch3(T['EPM']), scalar=-1.0,
                                                           in1=ch3(T['KK']), op0=ALU.mult, op1=ALU.mult),
                   ['EPM', 'KK'], ['AR0'])
                ew('pool', lambda e: e.tensor_tensor(out=AR[:, :, 1, :], in0=ch3(T['EP']), in1=ch3(T['R']), op=ALU.mult),
                   ['EP', 'R'], ['AR1'])
                ew('dve', lambda e: e.tensor_tensor(out=T['BT'], in0=T['EN'], in1=T['BVEC'], op=ALU.mult),
                   ['EN', 'BVEC'], ['BT'])
                ew('pool', lambda e: e.tensor_tensor(out=T['KT'], in0=T['EN'], in1=T['KM'], op=ALU.mult),
                   ['EN', 'KM'], ['KT'])
                ew('dve', lambda e: e.tensor_tensor(out=T['BH'], in0=T['EC'], in1=T['BVEC'], op=ALU.mult),
                   ['EC', 'BVEC'], ['BH'])
                ew('pool', lambda e: e.tensor_tensor(out=T['KH'], in0=T['EC'], in1=T['KM'], op=ALU.mult),
                   ['EC', 'KM'], ['KH'])
                ew('act', lambda e: e.copy(out=T['VT'], in_=T['V']), ['V'], ['VT'])
                pT = bank(0).bitcast(BF16)
                for c in range(NCH):
                    cs_ = slice(c * 128, (c + 1) * 128)
                    srcs = [(AR[:, c, 0, :], 'AR0'), (T['VT'][:, cs_], 'VT'), (T['BH'][:, cs_], 'BH'),
                            (T['KH'][:, cs_], 'KH')]
                    for q, (sap, sr) in enumerate(srcs):
                        ew('pe', lambda e, q=q, sap=sap: e.transpose(out=pT[:, q * 128:(q + 1) * 128], in_=sap,
                                                                     identity=ident), [sr, 'ident'], ['pb0'])
                    ew('act', lambda e, c=c: e.copy(out=TM4[:, c, :, :], in_=pT[:, 0:512].rearrange("p (q t) -> p q t", q=4)),
                       ['pb0'], ['TM4_%d' % c])

            def chunks(tb, inter):
                P1 = (1, 2)
                P2 = ((4, 5), (6, 7))

                def partA(c, sl):
                    cs_ = slice(c * 128, (c + 1) * 128)
                    tm = 'TM4_%d' % c
                    arc = AR[:, c, :, :].rearrange("p a t -> p (a t)")
                    b1 = P1[sl]
                    ps1 = bank(b1)
                    for h2 in range(2):
                        psl = slice(64 * h2, 64 * h2 + 64)
                        scr = 'SC%d_%d' % (sl, h2)
                        ew('pe', lambda e, ps1=ps1, psl=psl, cs_=cs_, arc=arc: e.matmul(
                            ps1[:, 0:256], lhsT=T['BT'][psl, cs_], rhs=arc[psl, :], start=True, stop=True),
                            ['BT', 'AR0', 'AR1'], ['pb%d' % b1])
                        ew('pe', lambda e, ps1=ps1, psl=psl, cs_=cs_, arc=arc: e.matmul(
                            ps1[:, 256:512], lhsT=T['KT'][psl, cs_], rhs=arc[psl, :], start=True, stop=True),
                            ['KT', 'AR0', 'AR1'], ['pb%d' % b1])
                        ew('dve', lambda e, ps1=ps1, h2=h2, sl=sl: e.tensor_tensor(out=SC[sl][h2], in0=mk4, in1=ps1,
                                                                                   op=ALU.mult),
                           ['pb%d' % b1, 'mk4'], [scr])
                        b2 = P2[sl][h2]
                        ew('pe', lambda e, b2=b2, psl=psl, cs_=cs_, c=c: e.matmul(
                            bank(b2)[:, 384:512], lhsT=AR[psl, c, 0, :], rhs=T['BT'][psl, cs_],
                            start=True, stop=True), ['AR0', 'BT'], ['pb%d' % b2])
                    for h2 in range(2):
                        b2 = P2[sl][h2]
                        ew('dve', lambda e, h2=h2, b2=b2, sl=sl: e.tensor_tensor(
                            out=LZ[sl][0][:, h2, 0:128], in0=mkL[:, h2, :], in1=bank(b2)[:, 384:512], op=ALU.mult),
                            ['pb%d' % b2, 'mkL'], ['LL%d_0_%d' % (sl, h2)])
                    for h2 in range(2):
                        pb = 64 * h2
                        ew('pe', lambda e, h2=h2, pb=pb, c=c, sl=sl: e.matmul(
                            bank(3)[:, 256 + h2 * 64:256 + (h2 + 1) * 64], lhsT=SC[sl][h2][:, 256:384],
                            rhs=TM4[:, c, 1, pb:pb + 64], start=True, stop=True), ['SC%d_%d' % (sl, h2), tm], ['pb3'])
                    ew('pool', lambda e, c=c, sl=sl: e.tensor_copy(
                        out=LZ[sl][0][:, :, 128:192], in_=TM4[:, c, 0, :].rearrange("p (h k) -> p h k", h=2)),
                        [tm], ['ZZ%d_0_0' % sl, 'ZZ%d_0_1' % sl])
                    for h2 in range(2):
                        ew('act', lambda e, h2=h2, sl=sl: e.copy(out=LZ[sl][0][:, h2, 192:256],
                                                                 in_=bank(3)[:, 256 + h2 * 64:256 + (h2 + 1) * 64]),
                           ['pb3'], ['ZZ%d_0_%d' % (sl, h2)])

                def partB(c, sl, n):
                    pp = n % 2
                    for h2 in range(2):
                        b2 = P2[sl][h2]
                        ps2 = bank(b2)
                        pbr = 'pb%d' % b2
                        ltn = SC[sl][h2][:, 0:128] if n == 0 else LZ[sl][pp][:, h2, 256:384]
                        rds = ['LL%d_%d_%d' % (sl, pp, h2), 'ZZ%d_%d_%d' % (sl, pp, h2)] + \
                            (['SC%d_%d' % (sl, h2)] if n == 0 else [])
                        if n < 6:
                            ew('pe', lambda e, ps2=ps2, ltn=ltn, pp=pp, h2=h2, sl=sl: e.matmul(
                                ps2[:, 0:256], lhsT=ltn, rhs=LZ[sl][pp][:, h2, 0:256], start=True, stop=True),
                                rds, [pbr])
                            ew('pe', lambda e, ps2=ps2, ltn=ltn, pp=pp, h2=h2, sl=sl: e.matmul(
                                ps2[:, 256:384], lhsT=LZ[sl][pp][:, h2, 0:128], rhs=ltn, start=True, stop=True),
                                rds, [pbr])
                        else:
                            ew('pe', lambda e, ps2=ps2, ltn=ltn, pp=pp, h2=h2, sl=sl: e.matmul(
                                ps2[:, 128:256], lhsT=ltn, rhs=LZ[sl][pp][:, h2, 128:256], start=True, stop=True),
                                rds, [pbr])

                    def cp(h2):
                        b2 = P2[sl][h2]
                        ew('act', lambda e, h2=h2, b2=b2: e.copy(
                            out=LZ[sl][1 - pp][:, h2, :].rearrange("p (s t) -> p s t", s=3)[:, 0:3:2, :],
                            in_=bank(b2)[:, 0:384].rearrange("p (s t) -> p s t", s=3)[:, 0:3:2, :]),
                            ['pb%d' % b2], ['LL%d_%d_%d' % (sl, 1 - pp, h2)])

                    def ad(h2):
                        b2 = P2[sl][h2]
                        ew('dve', lambda e, h2=h2, b2=b2: e.tensor_tensor(
                            out=LZ[sl][1 - pp][:, h2, 128:256], in0=LZ[sl][pp][:, h2, 128:256],
                            in1=bank(b2)[:, 128:256], op=ALU.add),
                            ['pb%d' % b2, 'ZZ%d_%d_%d' % (sl, pp, h2)], ['ZZ%d_%d_%d' % (sl, 1 - pp, h2)])
                    if n < 6:
                        cp(0)
                        ad(1)
                        cp(1)
                        ad(0)
                    else:
                        ad(0)
                        ad(1)

                def partC(c, sl):
                    tm = 'TM4_%d' % c
                    ZF = LZ[sl][1]
                    p3 = bank(0)[:, 192:384]
                    for h2 in range(2):
                        pb = 64 * h2
                        psl = slice(pb, pb + 64)
                        zr = 'ZZ%d_1_%d' % (sl, h2)
                        ew('pe', lambda e, h2=h2, pb=pb, psl=psl, c=c: e.matmul(
                            p3[psl, 0:64], lhsT=ZF[:, h2, 128:192], rhs=TM4[:, c, 2, pb:pb + 64],
                            start=True, stop=True, tile_position=(0, pb)), [zr, tm], ['pb0'])
                        ew('pe', lambda e, h2=h2, pb=pb, psl=psl: e.matmul(
                            p3[psl, 64:192], lhsT=ZF[:, h2, 128:192], rhs=SC[sl][h2][:, 128:256],
                            start=True, stop=True, tile_position=(0, pb)), [zr, 'SC%d_%d' % (sl, h2)], ['pb0'])
                    for h2 in range(2):
                        psl = slice(64 * h2, 64 * h2 + 64)
                        ew('dve', lambda e, c=c, psl=psl, h2=h2: e.scalar_tensor_tensor(
                            out=MCz[sl][psl, h2, :], in0=E2[psl, :], scalar=PC[psl, c:c + 1], in1=p3[psl, 0:64],
                            op0=ALU.mult, op1=ALU.add), ['E2', 'PC', 'pb0'], ['MC%d' % sl])
                    ew('dve', lambda e, c=c: e.tensor_tensor(out=QT[sl], in0=AR[:, c, 1, :], in1=p3[:, 64:192],
                                                             op=ALU.add), ['pb0', 'AR1'], ['QT%d' % sl])

                def partD(c, sl):
                    cs_ = slice(c * 128, (c + 1) * 128)
                    tm = 'TM4_%d' % c
                    ZF = LZ[sl][1]
                    for h2 in range(2):
                        pb = 64 * h2
                        psl = slice(pb, pb + 64)
                        sb_ = 3 if h2 == 0 else 0
                        sr_ = 'pb%d' % sb_
                        psY = bank(sb_)[:, 0:128]
                        psS = bank(sb_)[:, 128:192]
                        UU = ZF[:, h2, 192:256]
                        zr = 'ZZ%d_1_%d' % (sl, h2)
                        scr = 'SC%d_%d' % (sl, h2)
                        ew('pe', lambda e, psl=psl, pb=pb, h2=h2, UU=UU, psY=psY: e.matmul(
                            psY[psl, :], lhsT=UU, rhs=SC[sl][h2][:, 128:256], start=True, stop=False,
                            tile_position=(0, pb)), [zr, scr], [sr_])
                        ew('pe', lambda e, psl=psl, pb=pb, h2=h2, c=c, psY=psY: e.matmul(
                            psY[psl, :], lhsT=TM4[:, c, 1, pb:pb + 64], rhs=SC[sl][h2][:, 384:512], start=False,
                            stop=False, tile_position=(0, pb)), [tm, scr], [sr_])
                        ew('pe', lambda e, psl=psl, pb=pb, psY=psY, h2=h2: e.matmul(
                            psY[psl, :], lhsT=STz[:, h2, :], rhs=QT[sl], start=False, stop=True,
                            tile_position=(0, pb)), ['ST', 'QT%d' % sl], [sr_])
                        ew('pe', lambda e, psl=psl, pb=pb, psS=psS, h2=h2: e.matmul(
                            psS[psl, :], lhsT=MCz[sl][:, h2, :], rhs=STz[:, h2, :], start=True, stop=False,
                            tile_position=(0, pb)), ['MC%d' % sl, 'ST'], [sr_])
                        ew('pe', lambda e, psl=psl, pb=pb, c=c, UU=UU, psS=psS: e.matmul(
                            psS[psl, :], lhsT=TM4[:, c, 2, pb:pb + 64], rhs=UU, start=False, stop=False,
                            tile_position=(0, pb)), [tm, zr], [sr_])
                        ew('pe', lambda e, psl=psl, pb=pb, c=c, psS=psS: e.matmul(
                            psS[psl, :], lhsT=TM4[:, c, 3, pb:pb + 64], rhs=TM4[:, c, 1, pb:pb + 64], start=False,
                            stop=True, tile_position=(0, pb)), [tm], [sr_])
                    for h2 in range(2):
                        pb = 64 * h2
                        psl = slice(pb, pb + 64)
                        sb_ = 3 if h2 == 0 else 0
                        sr_ = 'pb%d' % sb_
                        ew('act', lambda e, cs_=cs_, psl=psl, sb_=sb_: e.copy(out=T['Y32'][psl, cs_],
                                                                             in_=bank(sb_)[psl, 0:128]),
                           [sr_], ['Y32'])
                        ew('act', lambda e, psl=psl, sb_=sb_, h2=h2: e.copy(out=STz[psl, h2, :],
                                                                            in_=bank(sb_)[psl, 128:192]),
                           [sr_], ['ST'])

                for c0 in range(0, NCH, 2):
                    for sl in range(2):
                        partA(c0 + sl, sl)
                    for n in range(7):
                        for sl in range(2):
                            partB(c0 + sl, sl, n)
                        if n in (1, 3, 5):
                            inter()
                    for sl in range(2):
                        partC(c0 + sl, sl)
                    for sl in range(2):
                        partD(c0 + sl, sl)
                    inter()

            def outst(tb, hp=hp):
                t0 = tb * TB
                tsl = slice(t0, t0 + TB)
                ew('pool', lambda e: e.tensor_copy(out=T['YB'], in_=T['Y32']), ['Y32'], ['YB'])
                for hf in range(2):
                    hs = slice(hf * 512, (hf + 1) * 512)
                    ew('pe', lambda e, hs=hs: e.matmul(bank(1), lhsT=bones, rhs=T['YB'][:, hs], start=True, stop=True),
                       ['bones', 'YB'], ['pb1'])
                    ew('dve', lambda e, hs=hs: e.scalar_tensor_tensor(out=T['DD'][:, hs], in0=bank(1), scalar=-1.0 / 64,
                                                                      in1=T['Y32'][:, hs], op0=ALU.mult, op1=ALU.add),
                       ['pb1', 'Y32'], ['DD'])
                ew('act', lambda e: e.activation(out=T['TQ'], in_=T['DD'], func=AF.Square), ['DD'], ['TQ'])
                for hf in range(2):
                    hs = slice(hf * 512, (hf + 1) * 512)
                    ew('pe', lambda e, hs=hs: e.matmul(bank(2), lhsT=bones, rhs=T['TQ'][:, hs], start=True, stop=True),
                       ['bones', 'TQ'], ['pb2'])
                    ew('act', lambda e, hs=hs: e.activation(out=T['T1'][:, hs], in_=bank(2), func=AF.Ln, scale=1.0 / 64,
                                                            bias=GN_EPS), ['pb2'], ['T1'])
                ew('act', lambda e: e.activation(out=T['T1'], in_=T['T1'], func=AF.Exp, scale=-0.5), ['T1'], ['T1'])
                ew('dve', lambda e: e.tensor_tensor(out=T['DD'], in0=T['DD'], in1=T['T1'], op=ALU.mult),
                   ['DD', 'T1'], ['DD'])
                ew('dve', lambda e, hp=hp: e.tensor_scalar(out=T['DD'], in0=T['DD'], scalar1=prm[:, 5, hp:hp + 1],
                                                           scalar2=prm[:, 6, hp:hp + 1], op0=ALU.mult, op1=ALU.add),
                   ['DD', 'prm'], ['DD'])
                ew('pool', lambda e: e.tensor_tensor(out=T['DD'], in0=T['DD'], in1=T['BV'], op=ALU.add),
                   ['DD', 'BV'], ['DD'])
                ew('dve', lambda e: e.tensor_tensor(out=T['YO'], in0=T['DD'], in1=T['GG'], op=ALU.mult),
                   ['DD', 'GG'], ['YO'])
                pdma('sp', lambda e, hp=hp, tsl=tsl: e.dma_start(out=ya_d[hp, :, tsl], in_=T['YO']),
                     reads=[R_('YO')], writes=['d_ya'], dma_key=R_('stYO'))


            funcs[hp] = (prep, chunks, outst)

        def emit_all(lst):
            for a_ in lst:
                P.op(a_[0], a_[1], reads=a_[2], writes=a_[3], dma_key=a_[4])
        blocks = [(hp_, tb_) for hp_ in pairs for tb_ in range(NB)]
        par[0] = 0
        funcs[blocks[0][0]][0](blocks[0][1])
        pend_out = []
        for bi, (hp_, tb_) in enumerate(blocks):
            prep_f, chunks_f, outst_f = funcs[hp_]
            pend = list(pend_out)
            if bi + 1 < len(blocks):
                nhp, ntb = blocks[bi + 1]
                par[0] = (bi + 1) % 2
                coll[0] = pend
                funcs[nhp][0](ntb)
                coll[0] = None
            par[0] = bi % 2
            if tb_ == 0:
                ew('pool', lambda e: e.memset(STz, 0.0), [], ['ST'])
            nsl = 4 * (NCH // 2)
            step = (len(pend) + nsl - 1) // nsl if pend else 0
            pos = [0]

            def inter(pend=pend, step=step, pos=pos):
                open_banks = set()
                cnt = 0
                while pos[0] < len(pend) and (cnt < step or open_banks):
                    a_ = pend[pos[0]]
                    pos[0] += 1
                    cnt += 1
                    if a_[0] == 'pe':
                        open_banks.update(w_ for w_ in a_[3] if w_.startswith('pb'))
                    else:
                        open_banks.difference_update(r_ for r_ in a_[2] if r_.startswith('pb'))
                    P.op(a_[0], a_[1], reads=a_[2], writes=a_[3], dma_key=a_[4])
            chunks_f(tb_, inter)
            emit_all(pend[pos[0]:])
            par[0] = bi % 2
            pend_out = []
            coll[0] = pend_out
            outst_f(tb_)
            coll[0] = None
        emit_all(pend_out)

    yb_d = nc.dram_tensor("yb_scr", [6, 128, S], BF16, kind="ExternalOutput" if debug else "Internal").ap()
    DIL = (1, 4, 16)

    def attn_stage_full(js=range(4)):
        A.off = const_mark
        pre = 'a_'

        def R_(n):
            return pre + n

        def nm(x):
            if x.startswith('pb') or x in ('ident', 'ident_f'):
                return x
            return R_(x)

        def ew(eng, fn, reads, writes):
            P.op(eng, fn, reads=[nm(x) for x in reads], writes=[nm(x) for x in writes])
        QH = A.alloc([S], BF16)
        KH = A.alloc([S], BF16)
        VX = A.alloc([32, 64], BF16)
        ONES = A.alloc([64], BF16)
        OT = [A.alloc([S], F32) for _ in range(3)]
        DEN = [A.alloc([S], F32) for _ in range(3)]
        PT = [A.alloc([256], BF16) for _ in range(4)]
        mask2 = A.alloc([256], BF16)
        RD = [A.alloc([512], F32) for _ in range(2)]
        YBS = [A.alloc([512], BF16) for _ in range(2)]
        print("attn_stage arena", A.off)
        P.op('pool', lambda e: e.memset(mask2, 1.0), writes=[R_('mask')])
        P.op('pool', lambda e: e.affine_select(out=mask2[:, 0:128], in_=mask2[:, 0:128], pattern=[[1, 128]],
                                               compare_op=ALU.is_ge, fill=0.0, base=0, channel_multiplier=-1),
             reads=[R_('mask')], writes=[R_('mask')])
        P.op('pool', lambda e: e.affine_select(out=mask2[:, 128:256], in_=mask2[:, 128:256], pattern=[[-1, 128]],
                                               compare_op=ALU.is_ge, fill=0.0, base=0, channel_multiplier=1),
             reads=[R_('mask')], writes=[R_('mask')])
        P.op('pool', lambda e: e.memset(ONES, 1.0), writes=[R_('ONES')])

        tcount = [0]
        ccount = [0]
        for j in js:
            for g in range(3):
                d = DIL[g]
                nb = S // d // 128
                h = 4 * g + j
                pair = h // 2
                pb = 64 * (h % 2)
                vv = v_d.rearrange("(m d) c -> d m c", d=d)
                for r in range(d):
                    for n0 in range(0, nb, 8):
                        n1 = min(nb, n0 + 8)
                        P.op('sp', lambda e, r=r, h=h, nb=nb, vv=vv, n0=n0, n1=n1: e.dma_start(
                            out=VX[:, r * nb + n0:r * nb + n1, :],
                            in_=vv[r, n0 * 128:n1 * 128, h * 64:(h + 1) * 64].rearrange("(n i) c -> i n c", i=128)),
                            writes=[R_('VX')], dma_key=R_('ldV'))
                P.op('sp', lambda e, pair=pair, pb=pb: e.dma_start(out=QH[0:64, :], in_=qk_d[pair, pb:pb + 64, :]),
                     writes=[R_('QH')], dma_key=R_('ldQ'))
                P.op('sp', lambda e, pair=pair, pb=pb: e.dma_start(out=KH[0:64, :], in_=qk_d[6 + pair, pb:pb + 64, :]),
                     writes=[R_('KH')], dma_key=R_('ldK'))
                qv = QH.rearrange("p (m d) -> p d m", d=d)
                kv = KH.rearrange("p (m d) -> p d m", d=d)
                otr = 'OT%d' % g
                otv = OT[g].rearrange("p (m d) -> p d m", d=d)
                dnv = DEN[g].rearrange("p (m d) -> p d m", d=d)
                tiles = []
                for r in range(d):
                    for n in range(nb):
                        tiles.append((r, n, tcount[0]))
                        tcount[0] += 1

                def emit_score(r, n, ti, kv=kv, qv=qv, nb=nb):
                    nq = 256 if n + 1 < nb else 128
                    sbk = (1, 2, 6)[ti % 3]
                    ps = bank(sbk)[:, 0:nq]
                    pt = PT[ti % 4]
                    ptr = 'PT%d' % (ti % 4)
                    ew('pe', lambda e, ps=ps, r=r, n=n, nq=nq: e.matmul(
                        ps, lhsT=kv[0:64, r, 128 * n:128 * n + 128], rhs=qv[0:64, r, 128 * n:128 * n + nq],
                        start=True, stop=True), ['KH', 'QH'], ['pb%d' % sbk])
                    ew('act', lambda e, ps=ps, pt=pt, nq=nq: e.activation(out=pt[:, 0:nq], in_=ps, func=AF.Exp),
                       ['pb%d' % sbk], [ptr])
                    ew('pool', lambda e, pt=pt, nq=nq: e.tensor_tensor(out=pt[:, 0:nq], in0=pt[:, 0:nq],
                                                                      in1=mask2[:, 0:nq], op=ALU.mult),
                       [ptr, 'mask'], [ptr])

                def emit_pv(r, n, ti, otv=otv, dnv=dnv, nb=nb, g=g, otr=otr):
                    pt = PT[ti % 4]
                    ptr = 'PT%d' % (ti % 4)
                    obk = 3 + ti % 2
                    po = bank(obk)[0:64, 0:128]
                    pdn = bank(obk)[0:64, 128:256]
                    vt = r * nb + n
                    has_prev = n > 0
                    if has_prev:
                        ppt = PT[(ti - 1) % 4]
                        pptr = 'PT%d' % ((ti - 1) % 4)
                        ew('pe', lambda e, ppt=ppt, vt=vt: e.matmul(
                            po, lhsT=VX[:, vt - 1, :], rhs=ppt[:, 128:256], start=True, stop=False),
                            ['VX', pptr], ['pb%d' % obk])
                    ew('pe', lambda e, pt=pt, vt=vt: e.matmul(
                        po, lhsT=VX[:, vt, :], rhs=pt[:, 0:128], start=(not has_prev), stop=True),
                        ['VX', ptr], ['pb%d' % obk])
                    if has_prev:
                        ew('pe', lambda e, ppt=ppt: e.matmul(
                            pdn, lhsT=ONES, rhs=ppt[:, 128:256], start=True, stop=False),
                            ['ONES', pptr], ['pb%d' % obk])
                    ew('pe', lambda e, pt=pt: e.matmul(
                        pdn, lhsT=ONES, rhs=pt[:, 0:128], start=(not has_prev), stop=True),
                        ['ONES', ptr], ['pb%d' % obk])
                    ew('dve', lambda e, r=r, n=n: e.tensor_copy(
                        out=otv[0:64, r, 128 * n:128 * n + 128], in_=po), ['pb%d' % obk], [otr])
                    ew('dve', lambda e, r=r, n=n: e.tensor_copy(
                        out=dnv[0:64, r, 128 * n:128 * n + 128], in_=pdn), ['pb%d' % obk], ['DEN%d' % g])

                LAG = 2
                for idx, (r, n, ti) in enumerate(tiles):
                    emit_score(r, n, ti)
                    if idx >= LAG:
                        emit_pv(*tiles[idx - LAG])
                for idx in range(max(0, len(tiles) - LAG), len(tiles)):
                    emit_pv(*tiles[idx])
            for ck in range(S // 512):
                csl = slice(ck * 512, (ck + 1) * 512)
                cc = ccount[0]
                ccount[0] += 1
                rd = RD[cc % 2]
                ew('pool', lambda e, rd=rd, csl=csl: e.tensor_tensor(out=rd[0:64, :], in0=DEN[0][0:64, csl],
                                                                     in1=DEN[1][0:64, csl], op=ALU.add),
                   ['DEN0', 'DEN1'], ['RD%d' % (cc % 2)])
                ew('pool', lambda e, rd=rd, csl=csl: e.tensor_tensor(out=rd[0:64, :], in0=rd[0:64, :],
                                                                     in1=DEN[2][0:64, csl], op=ALU.add),
                   ['RD%d' % (cc % 2), 'DEN2'], ['RD%d' % (cc % 2)])
                ew('dve', lambda e, rd=rd: e.reciprocal(out=rd[0:64, :], in_=rd[0:64, :]), ['RD%d' % (cc % 2)],
                   ['RD%d' % (cc % 2)])
                for g in range(3):
                    h = 4 * g + j
                    pair = h // 2
                    pb = 64 * (h % 2)
                    yi = (cc * 3 + g) % 2
                    ys = YBS[yi]
                    ew('pool', lambda e, g=g, csl=csl, rd=rd, ys=ys: e.tensor_tensor(
                        out=ys[0:64, :], in0=OT[g][0:64, csl], in1=rd[0:64, :], op=ALU.mult),
                        ['OT%d' % g, 'RD%d' % (cc % 2)], ['YBS%d' % yi])
                    P.op('sp', lambda e, pair=pair, pb=pb, csl=csl, ys=ys: e.dma_start(
                        out=yb_d[pair, pb:pb + 64, csl], in_=ys[0:64, :]),
                        reads=[R_('YBS%d' % yi)], writes=['d_yb'], dma_key=R_('stYB%d' % yi))

    wpr_d = din("w_proj_rwkv", [1024, 1024])
    wpa_d = din("w_proj_attn", [768, 1024])
    wo_d = din("w_out", [1024, 1024])
    x2_d = nc.dram_tensor("x2_scr", [S, D], F32, kind="ExternalOutput" if debug else "Internal").ap()

    def merge_stage():
        A.off = const_mark
        pre = 'm_'

        def R_(n):
            return pre + n

        def nm(x):
            if x.startswith('pb'):
                return x
            return R_(x)

        def ew(eng, fn, reads, writes):
            P.op(eng, fn, reads=[nm(x) for x in reads], writes=[nm(x) for x in writes])
        Wr = A.alloc([8, 1024], BF16)
        Wa = A.alloc([6, 1024], BF16)
        Wo = A.alloc([8, 1024], BF16)
        X = [A.alloc([NSUB, D], F32) for _ in range(2)]
        YA = [A.alloc([8, TT], BF16) for _ in range(2)]
        YB = [A.alloc([6, TT], BF16) for _ in range(2)]
        G = [A.alloc([16, TT], BF16) for _ in range(2)]
        MT = A.alloc([8, TT], BF16)
        t1 = [A.alloc([TT], F32) for _ in range(2)]
        t2 = [A.alloc([TT], F32) for _ in range(2)]
        print("merge_stage arena", A.off)
        for kc in range(8):
            P.op('pool', lambda e, kc=kc: e.dma_start(out=Wr[:, kc, :], in_=wpr_d[kc * 128:(kc + 1) * 128, :]),
                 writes=[R_('Wr')], dma_key=R_('w'))
            P.op('pool', lambda e, kc=kc: e.dma_start(out=Wo[:, kc, :], in_=wo_d[kc * 128:(kc + 1) * 128, :]),
                 writes=[R_('Wo')], dma_key=R_('w'))
        for kc in range(6):
            P.op('pool', lambda e, kc=kc: e.dma_start(out=Wa[:, kc, :], in_=wpa_d[kc * 128:(kc + 1) * 128, :]),
                 writes=[R_('Wa')], dma_key=R_('w'))
        srcv = x1_d.rearrange("(n s p) d -> n p s d", p=128, s=NSUB)
        dstv = x2_d.rearrange("(n s p) d -> n p s d", p=128, s=NSUB)
        for it in range(NT):
            sl = it % 2
            tsl = slice(it * TT, (it + 1) * TT)
            sfx = '%d' % sl
            P.op('sp', lambda e, it=it, sl=sl: e.dma_start(out=X[sl], in_=srcv[it]), writes=[R_('X' + sfx)],
                 dma_key=R_('ldX' + sfx))
            P.op('sp', lambda e, sl=sl, tsl=tsl: e.dma_start(out=YA[sl], in_=ya_d.rearrange("b p t -> p b t")[:, :, tsl]),
                 writes=[R_('YA' + sfx)], dma_key=R_('ldYA' + sfx))
            P.op('sp', lambda e, sl=sl, tsl=tsl: e.dma_start(out=YB[sl], in_=yb_d.rearrange("b p t -> p b t")[:, :, tsl]),
                 writes=[R_('YB' + sfx)], dma_key=R_('ldYB' + sfx))
            P.op('sp', lambda e, sl=sl, tsl=tsl: e.dma_start(out=G[sl], in_=gate_d.rearrange("b p t -> p b t")[:, :, tsl]),
                 writes=[R_('G' + sfx)], dma_key=R_('ldG' + sfx))
            for c in range(8):
                bk = 1 + c % 2
                pg = bank(bk)
                for kc in range(8):
                    ew('pe', lambda e, c=c, kc=kc, pg=pg, sl=sl: e.matmul(
                        pg[:, 0:TT], lhsT=Wr[:, kc, c * 128:(c + 1) * 128], rhs=YA[sl][:, kc, :],
                        start=(kc == 0), stop=(kc == 7)), ['Wr', 'YA' + sfx], ['pb%d' % bk])
                for kc in range(6):
                    ew('pe', lambda e, c=c, kc=kc, pg=pg, sl=sl: e.matmul(
                        pg[:, TT:2 * TT], lhsT=Wa[:, kc, c * 128:(c + 1) * 128], rhs=YB[sl][:, kc, :],
                        start=(kc == 0), stop=(kc == 5)), ['Wa', 'YB' + sfx], ['pb%d' % bk])
                q = c % 2
                ew('dve', lambda e, c=c, pg=pg, sl=sl, q=q: e.tensor_tensor(out=t1[q], in0=G[sl][:, c, :],
                                                                           in1=pg[:, 0:TT], op=ALU.mult),
                   ['G' + sfx, 'pb%d' % bk], ['t1%d' % q])
                ew('dve', lambda e, c=c, pg=pg, sl=sl, q=q: e.tensor_tensor(out=t2[q], in0=G[sl][:, 8 + c, :],
                                                                           in1=pg[:, TT:2 * TT], op=ALU.mult),
                   ['G' + sfx, 'pb%d' % bk], ['t2%d' % q])
                ew('pool', lambda e, c=c, q=q: e.tensor_tensor(out=MT[:, c, :], in0=t1[q], in1=t2[q], op=ALU.add),
                   ['t1%d' % q, 't2%d' % q], ['MT'])
            for s in range(NSUB):
                for dh in range(2):
                    bk = 3 + (s * 2 + dh) % 2
                    pd = bank(bk)
                    for c in range(8):
                        ew('pe', lambda e, c=c, s=s, dh=dh, pd=pd: e.matmul(
                            pd, lhsT=MT[:, c, s * 128:(s + 1) * 128], rhs=Wo[:, c, dh * 512:(dh + 1) * 512],
                            start=(c == 0), stop=(c == 7)), ['MT', 'Wo'], ['pb%d' % bk])
                    ew('dve', lambda e, s=s, dh=dh, pd=pd, sl=sl: e.tensor_tensor(
                        out=X[sl][:, s, dh * 512:(dh + 1) * 512], in0=X[sl][:, s, dh * 512:(dh + 1) * 512], in1=pd,
                        op=ALU.add), ['pb%d' % bk, 'X' + sfx], ['X' + sfx])
            P.op('sp', lambda e, it=it, sl=sl: e.dma_start(out=dstv[it], in_=X[sl]),
                 reads=[R_('X' + sfx)], writes=['d_x2'], dma_key=R_('stX' + sfx))

    if 'ffn1' in stages:
        ffn_stage(0, x, x1_d)
        P.sync_all()
    if 'proj' in stages:
        proj_stage()
        P.sync_all()
    if 'rwkv' in stages:
        rwkv_stage(range(NPAIRS_DBG))
        P.sync_all()
    if 'attn' in stages:
        attn_stage_full(JS_DBG)
        P.sync_all()
    if 'merge' in stages:
        merge_stage()
        P.sync_all()
    if 'ffn2' in stages:
        ffn_stage(1, x2_d if 'merge' in stages else x1_d, out)
    P.sync_all()
    P.op('sp', None)
    P.finalize_and_emit(stack)
    stack.close()
    return nc


_CACHE = {}


SHARED_KEYS = ['ffn1_norm', 'ffn1_w_in', 'ffn1_w_out', 'ffn2_norm', 'ffn2_w_in', 'ffn2_w_out',
               'w_in', 'mix_norm', 'rwkv_mu', 'b_gate', 'attn_q_norm', 'attn_k_norm',
               'w_proj_rwkv', 'w_proj_attn', 'w_out', 'rwkv_w2', 'rwkv_a2', 'rwkv_g2', 'rwkv_w0', 'rwkv_a0', 'rwkv_k_k', 'rwkv_k_a', 'rwkv_r_k', 'rwkv_ln_w', 'rwkv_ln_b']


def make_shared(inputs):
    shared = {}
    for k in SHARED_KEYS:
        v = np.asarray(inputs[k], dtype=np.float32)
        v = v.reshape(v.shape[1:])
        if k == 'rwkv_r_k':
            v = v.reshape(-1)
        shared[k] = np.ascontiguousarray(v)
    return shared


def kernel(**inputs):
    if 'nc' not in _CACHE:
        _CACHE['nc'] = build_program()
    nc = _CACHE['nc']
    x = np.ascontiguousarray(inputs['x'], dtype=np.float32)
    shared = make_shared(inputs)
    in_maps = []
    for c in range(NCORES):
        m = dict(shared)
        m['x'] = x[c]
        in_maps.append(m)
    res = run_bass_kernel_spmd(nc, in_maps, core_ids=list(range(NCORES)))
    return np.stack([np.asarray(r['out']) for r in res.results], axis=0)
```

```python
import numpy as np
from contextlib import ExitStack
import concourse.bass as bass
import concourse.mybir as mybir
from concourse.bass_utils import run_bass_kernel_spmd
from concourse.alu_op_type import AluOpType as ALU

F32 = mybir.dt.float32
BF16 = mybir.dt.bfloat16
AF = mybir.ActivationFunctionType
AX = mybir.AxisListType

S = 4096
D = 1024
DFF = 2816
NCORES = 8
RMS_EPS = 1e-6

ENGS = ['pe', 'act', 'dve', 'pool', 'sp']
MAXOPS = [0]
SEM_LIM = 30000
DMA_LIM = 1800


class Prog:
    def __init__(self, nc):
        self.nc = nc
        self.ops = []
        self.eng_ops = {e: [] for e in ENGS}
        self.last_w = {}
        self.readers = {}
        self.dma_cnt = {}
        self.barrier = {e: None for e in ENGS}

    def op(self, eng, fn, reads=(), writes=(), dma_key=None):
        mo = MAXOPS[0]
        if mo and len(self.ops) >= mo and fn is not None:
            return None
        if mo and len(self.ops) == mo - 1 and fn is not None:
            print("LAST OP:", eng, fn.__code__.co_firstlineno, reads, writes)
        oid = len(self.ops)
        deps = set()
        dma_deps = {}
        writes = list(writes) + [r for r in reads if (r.startswith('pb') or r.startswith('ps')) and r not in writes]

        def add(o):
            od = self.ops[o]
            if od['dma_key'] is not None:
                k = od['dma_key']
                dma_deps[k] = self.dma_cnt[k]
            else:
                deps.add(o)
        for r in reads:
            if r in self.last_w:
                add(self.last_w[r])
        for w in writes:
            if w in self.last_w:
                add(self.last_w[w])
            for rd in self.readers.get(w, {}).values():
                add(rd)
        if self.barrier[eng] is not None:
            bd, bdma = self.barrier[eng]
            for o in bd:
                deps.add(o)
            for k, v in bdma.items():
                dma_deps[k] = max(dma_deps.get(k, 0), v)
            self.barrier[eng] = None
        cnt = None
        if dma_key is not None:
            self.dma_cnt[dma_key] = self.dma_cnt.get(dma_key, 0) + 1
            cnt = self.dma_cnt[dma_key]
        o = dict(id=oid, eng=eng, fn=fn, deps=deps, dma_deps=dma_deps, dma_key=dma_key,
                 dma_cnt=cnt, idx=len(self.eng_ops[eng]), sig=False)
        self.ops.append(o)
        self.eng_ops[eng].append(o)
        ch = eng if dma_key is None else 'dma:' + dma_key
        for r in reads:
            self.readers.setdefault(r, {})[ch] = oid
        for w in writes:
            self.last_w[w] = oid
            self.readers[w] = {}
        return oid

    def sync_all(self):
        bd = set()
        for e in ENGS:
            for o in reversed(self.eng_ops[e]):
                if o['dma_key'] is None and o['fn'] is not None:
                    bd.add(o['id'])
                    break
        bdma = dict(self.dma_cnt)
        for e in ENGS:
            self.barrier[e] = (set(bd), dict(bdma))

    def finalize_and_emit(self, stack):
        nc = self.nc
        for o in self.ops:
            per = {}
            for d in o['deps']:
                od = self.ops[d]
                if od['eng'] == 'pe' and o['eng'] == 'pe':
                    continue
                e = od['eng']
                if e not in per or self.ops[per[e]]['idx'] < od['idx']:
                    per[e] = d
            o['cdeps'] = per
            for d in per.values():
                self.ops[d]['sig'] = True
        sems = {}

        def get_sem(name):
            return sems[name]
        for e in ENGS:
            c = 0
            for o in self.eng_ops[e]:
                if o['dma_key'] is None and o['sig']:
                    c += 1
                    o['sigval'] = c
        for o in self.ops:
            waits = {}
            for e, d in o['cdeps'].items():
                v = self.ops[d]['sigval']
                key = ('c_%s_%d' % (e, (v - 1) // SEM_LIM))
                val = (v - 1) % SEM_LIM + 1
                waits[key] = max(waits.get(key, 0), val)
            for k, n in o['dma_deps'].items():
                key = ('d_%s_%d' % (k, (n - 1) // DMA_LIM))
                val = 16 * ((n - 1) % DMA_LIM + 1)
                waits[key] = max(waits.get(key, 0), val)
            o['waits'] = waits
        names = set()
        for o in self.ops:
            names.update(o['waits'].keys())
            if o['dma_key'] is not None:
                names.add('d_%s_%d' % (o['dma_key'], (o['dma_cnt'] - 1) // DMA_LIM))
            elif o['sig']:
                names.add('c_%s_%d' % (o['eng'], (o['sigval'] - 1) // SEM_LIM))
        for nm in sorted(names):
            sems[nm] = stack.enter_context(nc.semaphore(nm))
        print("n_sems", len(names), "n_ops", len(self.ops), {e: len(v) for e, v in self.eng_ops.items()})
        block = stack.enter_context(nc.Block())
        decos = {'pe': block.tensor, 'act': block.scalar, 'dve': block.vector,
                 'pool': block.gpsimd, 'sp': block.sync}
        for e in ENGS:
            ops = self.eng_ops[e]

            def body(eng, ops=ops, e=e):
                waited = {}
                for o in ops:
                    for key, val in o['waits'].items():
                        if waited.get(key, 0) >= val:
                            continue
                        waited[key] = val
                        eng.wait_ge(get_sem(key), val)
                    if o['fn'] is None:
                        continue
                    ins = o['fn'](eng)
                    if o['dma_key'] is not None:
                        n = o['dma_cnt']
                        ins.then_inc(get_sem('d_%s_%d' % (o['dma_key'], (n - 1) // DMA_LIM)), 16)
                    elif o['sig']:
                        v = o['sigval']
                        ins.then_inc(get_sem('c_%s_%d' % (e, (v - 1) // SEM_LIM)), 1)
            decos[e](body)


class Arena:
    def __init__(self, tensor, nbytes):
        self.t = tensor
        self.nbytes = nbytes
        self.off = 0

    def alloc(self, shape, dtype, parts=128):
        n = int(np.prod(shape))
        esz = 4 if dtype == F32 else 2
        nb = n * esz
        nb_al = (nb + 63) // 64 * 64
        assert self.off + nb_al <= self.nbytes, ("SBUF arena overflow", self.off, nb_al)
        ap = self.t[0:parts, self.off // 2:(self.off + nb) // 2]
        self.off += nb_al
        if dtype == F32:
            ap = ap.bitcast(F32)
        if len(shape) == 2:
            ap = ap.rearrange("p (a b) -> p a b", a=shape[0], b=shape[1])
        elif len(shape) == 3:
            ap = ap.rearrange("p (a b c) -> p a b c", a=shape[0], b=shape[1], c=shape[2])
        return ap


def build_program(debug=False, NPAIRS_DBG=8, stages=('ffn1', 'proj', 'rwkv', 'attn', 'merge', 'ffn2'), NB_DBG=None,
                  JS_DBG=range(4), NT_DBG=None):
    nc = bass.Bass("TRN2", target_bir_lowering=False)
    P = Prog(nc)

    def din(name, shape):
        return nc.dram_tensor(name, list(shape), F32, kind="ExternalInput").ap()
    x = din("x", [S, D])
    ffn_norm = [din("ffn1_norm", [D]), din("ffn2_norm", [D])]
    ffn_win = [din("ffn1_w_in", [D, 2 * DFF]), din("ffn2_w_in", [D, 2 * DFF])]
    ffn_wout = [din("ffn1_w_out", [DFF, D]), din("ffn2_w_out", [DFF, D])]
    out = nc.dram_tensor("out", [S, D], F32, kind="ExternalOutput").ap()
    x1_d = nc.dram_tensor("x1_scr", [S, D], F32, kind="ExternalOutput" if debug else "Internal").ap()

    stack = ExitStack()
    ARENA_BYTES = 207 * 1024
    arena_t = stack.enter_context(nc.sbuf_tensor("arena", [128, ARENA_BYTES // 2], BF16))
    A = Arena(arena_t, ARENA_BYTES)
    psum = stack.enter_context(nc.psum_tensor("psum", [128, 4096], F32))

    def bank(b, n=512, off=0):
        return psum[:, b * 512 + off:b * 512 + off + n]

    ident_f = A.alloc([128], F32)
    ident = A.alloc([128], BF16)
    ones_col = A.alloc([1], F32)

    P.op('pool', lambda e: e.memset(ident_f, 0.0), writes=['ident_f'])
    P.op('pool', lambda e: e.affine_select(out=ident_f, in_=ident_f, pattern=[[-1, 128]],
                                           compare_op=ALU.not_equal, fill=1.0, base=0, channel_multiplier=1),
         reads=['ident_f'], writes=['ident_f'])
    P.op('dve', lambda e: e.tensor_copy(out=ident, in_=ident_f), reads=['ident_f'], writes=['ident'])

    const_mark = A.off

    TT = 256
    NSUB = TT // 128
    NT = S // TT if NT_DBG is None else NT_DBG
    KC = D // 128
    FC = DFF // 128

    def ffn_stage(si, src, dst):
        A.off = const_mark
        TT = 512
        NSUB = TT // 128
        NT = (S // TT) if NT_DBG is None else NT_DBG
        W1 = A.alloc([KC, 2 * DFF], BF16)
        W2 = A.alloc([FC, D], BF16)
        gb = A.alloc([D], F32)
        xt = [A.alloc([NSUB, D], F32)] * 2
        xc = [A.alloc([512], F32) for _ in range(3)]
        hb = [A.alloc([D], BF16) for _ in range(2)]
        hT = [A.alloc([KC, TT], BF16) for _ in range(2)]
        actT = A.alloc([FC, TT], BF16)
        sg = [A.alloc([TT], F32) for _ in range(2)]
        ss = A.alloc([8], F32)
        pre = 's%d_' % si
        w1v = ffn_win[si].rearrange("(kc p) f -> p kc f", p=128)
        CH = 1408
        for kc in range(KC):
            for c in range(2 * DFF // CH):
                P.op('pool', lambda e, kc=kc, c=c: e.dma_start(out=W1[:, kc, c * CH:(c + 1) * CH],
                                                              in_=w1v[:, kc, c * CH:(c + 1) * CH]),
                     writes=[pre + 'W1'], dma_key=pre + 'W1')
        w2v = ffn_wout[si].rearrange("(fc p) d -> p fc d", p=128)
        for fc in range(FC):
            P.op('pool', lambda e, fc=fc: e.dma_start(out=W2[:, fc, :], in_=w2v[:, fc, :]),
                 writes=[pre + 'W2'], dma_key=pre + 'W2')
        P.op('sp', lambda e: e.dma_start(out=gb, in_=ffn_norm[si].partition_broadcast(128)),
             writes=[pre + 'gb'], dma_key=pre + 'gb')
        srcv = src.rearrange("(n s p) d -> n p s d", p=128, s=NSUB)
        dstv = dst.rearrange("(n s p) d -> n p s d", p=128, s=NSUB)
        for it in range(NT):
            sl = it % 2
            X = xt[sl]
            xr = pre + 'xt'
            if it == 0:
                P.op('sp', lambda e, X=X: e.dma_start(out=X, in_=srcv[0]), writes=[xr], dma_key=xr)
            HT = hT[sl]
            for s in range(NSUB):
                hs = (it * NSUB + s) % 2
                H = hb[hs]
                hr = pre + 'hb%d' % hs
                P.op('act', lambda e, X=X, s=s, H=H: e.activation(out=H, in_=X[:, s, :], func=AF.Square,
                                                             accum_out=ss[:, 0:1]),
                     reads=[xr], writes=[hr, pre + 'ss'])
                P.op('act', lambda e: e.activation(out=ss[:, 1:2], in_=ss[:, 0:1], func=AF.Sqrt,
                                                   scale=1.0 / D, bias=RMS_EPS),
                     reads=[pre + 'ss'], writes=[pre + 'ss1'])
                P.op('dve', lambda e: e.reciprocal(out=ss[:, 2:3], in_=ss[:, 1:2]),
                     reads=[pre + 'ss1'], writes=[pre + 'ss2'])
                P.op('dve', lambda e, X=X, s=s, H=H: e.scalar_tensor_tensor(
                    out=H, in0=X[:, s, :], scalar=ss[:, 2:3], in1=gb, op0=ALU.mult, op1=ALU.mult),
                    reads=[xr, pre + 'ss2', pre + 'gb'], writes=[hr])
                pT = bank(0).bitcast(BF16)
                for kc in range(KC):
                    P.op('pe', lambda e, kc=kc, H=H, pT=pT: e.transpose(
                        out=pT[:, kc * 128:(kc + 1) * 128], in_=H[:, kc * 128:(kc + 1) * 128], identity=ident),
                        reads=[hr, 'ident'], writes=['psT'])
                P.op('act', lambda e, HT=HT, s=s, pT=pT: e.copy(
                    out=HT[:, :, s * 128:(s + 1) * 128], in_=pT.rearrange("p (k t) -> p k t", k=KC)),
                    reads=['psT'], writes=[pre + 'hT%d' % sl])
            if it + 1 < NT:
                P.op('sp', lambda e, it=it, X=X: e.dma_start(out=X, in_=srcv[it + 1]), writes=[xr], dma_key=xr)
            for fc in range(FC):
                bg = 1 + 2 * (fc % 2)
                bu = bg + 1
                pgate = bank(bg)
                pup = bank(bu)
                for half, pdst, br in ((0, pgate, bg), (1, pup, bu)):
                    col = half * DFF + fc * 128
                    for kc in range(KC):
                        P.op('pe', lambda e, kc=kc, col=col, pdst=pdst, HT=HT: e.matmul(
                            pdst, lhsT=W1[:, kc, col:col + 128], rhs=HT[:, kc, :],
                            start=(kc == 0), stop=(kc == KC - 1)),
                            reads=[pre + 'W1', pre + 'hT%d' % sl], writes=['psG%d' % br])
                SG = sg[fc % 2]
                P.op('act', lambda e, pgate=pgate, SG=SG: e.activation(out=SG, in_=pgate, func=AF.Silu),
                     reads=['psG%d' % bg], writes=[pre + 'sg%d' % (fc % 2)])
                P.op('dve', lambda e, pup=pup, SG=SG, fc=fc: e.tensor_tensor(
                    out=actT[:, fc, :], in0=SG, in1=pup, op=ALU.mult),
                    reads=['psG%d' % bu, pre + 'sg%d' % (fc % 2)], writes=[pre + 'actT'])
            for s in range(NSUB):
                for dh in range(2):
                    b = 5 + (s * 2 + dh) % 2
                    pd = bank(b)
                    for fc in range(FC):
                        P.op('pe', lambda e, fc=fc, s=s, dh=dh, pd=pd: e.matmul(
                            pd, lhsT=actT[:, fc, s * 128:(s + 1) * 128], rhs=W2[:, fc, dh * 512:(dh + 1) * 512],
                            start=(fc == 0), stop=(fc == FC - 1)),
                            reads=[pre + 'actT', pre + 'W2'], writes=['psD%d' % b])
                    k = (it * NSUB * 2 + s * 2 + dh) % 3
                    XC = xc[k]
                    xcr = pre + 'xc%d' % k
                    P.op('sp', lambda e, it=it, s=s, dh=dh, XC=XC: e.dma_start(
                        out=XC, in_=srcv[it][:, s, dh * 512:(dh + 1) * 512]), writes=[xcr], dma_key=xcr)
                    P.op('dve', lambda e, XC=XC, pd=pd: e.scalar_tensor_tensor(
                        out=XC, in0=pd, scalar=0.5, in1=XC, op0=ALU.mult, op1=ALU.add),
                        reads=['psD%d' % b, xcr], writes=[xcr])
                    P.op('sp', lambda e, it=it, s=s, dh=dh, XC=XC: e.dma_start(
                        out=dstv[it][:, s, dh * 512:(dh + 1) * 512], in_=XC),
                        reads=[xcr], writes=[pre + 'dst'], dma_key=xcr)

    NCOL = 7712
    w_in = din("w_in", [D, NCOL])
    mix_norm = din("mix_norm", [D])
    rwkv_mu = din("rwkv_mu", [3360])
    b_gate = din("b_gate", [2048])
    qn = din("attn_q_norm", [64])
    kn = din("attn_k_norm", [64])
    kscr = "ExternalOutput" if debug else "Internal"
    if 'proj' not in stages:
        kscr = "ExternalInput"
    rkv_d = nc.dram_tensor("rkv_scr", [24, 128, S], F32, kind=kscr).ap()
    ta_d = nc.dram_tensor("ta_scr", [128, S], BF16, kind=kscr).ap()
    tg_d = nc.dram_tensor("tg_scr", [160, S], BF16, kind=kscr).ap()
    qk_d = nc.dram_tensor("qk_scr", [12, 128, S], BF16, kind=kscr).ap()
    v_d = nc.dram_tensor("v_scr", [S, 768], BF16, kind=kscr).ap()
    gate_d = nc.dram_tensor("gate_scr", [16, 128, S], BF16, kind=kscr).ap()

    def proj_stage():
        A.off = const_mark
        pre = 'p_'
        W = A.alloc([KC, NCOL], BF16)
        gb = A.alloc([D], F32)
        X = A.alloc([NSUB, D], F32)
        hb = [A.alloc([D], BF16) for _ in range(2)]
        hT = [A.alloc([KC, TT], BF16) for _ in range(2)]
        ss = A.alloc([8], F32)
        mu_t = A.alloc([27], F32)
        bg_t = A.alloc([16], F32)
        qg_t = A.alloc([2], F32)
        carry = A.alloc([27], F32)
        psb = [A.alloc([TT + 1], F32) for _ in range(2)]
        tmp = [A.alloc([TT], F32) for _ in range(2)]
        sq = [A.alloc([TT], BF16) for _ in range(2)]
        lnb = [A.alloc([TT], F32) for _ in range(2)]
        rkv_st = A.alloc([24, TT], F32)
        ta_st = A.alloc([TT], BF16)
        tg_st = A.alloc([2, TT], BF16)
        qk_st = A.alloc([12, TT], BF16)
        gate_st = A.alloc([16, TT], BF16)
        v_st = A.alloc([NSUB, 768], BF16)
        bones = A.alloc([128], BF16)
        print("proj_stage arena", A.off)
        wv = w_in.rearrange("(kc p) f -> p kc f", p=128)
        CH = 964
        for kc in range(KC):
            for c in range(NCOL // CH):
                P.op('pool', lambda e, kc=kc, c=c: e.dma_start(out=W[:, kc, c * CH:(c + 1) * CH],
                                                              in_=wv[:, kc, c * CH:(c + 1) * CH]),
                     writes=[pre + 'W'], dma_key=pre + 'W')
        P.op('sp', lambda e: e.dma_start(out=gb, in_=mix_norm.partition_broadcast(128)),
             writes=[pre + 'gb'], dma_key=pre + 'par')
        P.op('sp', lambda e: e.dma_start(out=mu_t[:, 0:26], in_=rwkv_mu[0:3328].rearrange("(b p) -> p b", p=128),
                                         allow_slow_non_contiguous=True), writes=[pre + 'mu'], dma_key=pre + 'par')
        P.op('sp', lambda e: e.dma_start(out=mu_t[0:32, 26:27], in_=rwkv_mu[3328:3360].rearrange("(p o) -> p o", o=1)),
             writes=[pre + 'mu'], dma_key=pre + 'par')
        P.op('sp', lambda e: e.dma_start(out=bg_t, in_=b_gate.rearrange("(b p) -> p b", p=128),
                                         allow_slow_non_contiguous=True), writes=[pre + 'bg'], dma_key=pre + 'par')
        for hh in range(2):
            P.op('sp', lambda e, hh=hh: e.dma_start(out=qg_t[hh * 64:(hh + 1) * 64, 0:1],
                                                     in_=qn.rearrange("(p o) -> p o", o=1)),
                 writes=[pre + 'qg'], dma_key=pre + 'par')
            P.op('sp', lambda e, hh=hh: e.dma_start(out=qg_t[hh * 64:(hh + 1) * 64, 1:2],
                                                     in_=kn.rearrange("(p o) -> p o", o=1)),
                 writes=[pre + 'qg'], dma_key=pre + 'par')
        P.op('pool', lambda e: e.tensor_scalar(out=qg_t[:, 0:1], in0=qg_t[:, 0:1], scalar1=0.125, scalar2=None,
                                               op0=ALU.mult), reads=[pre + 'qg'], writes=[pre + 'qg'])
        P.op('pool', lambda e: e.memset(carry, 0.0), writes=[pre + 'carry%d' % i for i in range(27)])
        P.op('pool', lambda e: e.memset(bones, 0.0), writes=[pre + 'bones'])
        P.op('pool', lambda e: e.memset(bones[0:64, 0:64], 1.0), reads=[pre + 'bones'], writes=[pre + 'bones'])
        P.op('pool', lambda e: e.memset(bones[64:128, 64:128], 1.0), reads=[pre + 'bones'], writes=[pre + 'bones'])

        blocks = []
        for b in range(24):
            blocks.append((b * 128, 128, 'rkv', b))
        blocks.append((3072, 128, 'ta', 24))
        blocks.append((3200, 128, 'tg0', 25))
        blocks.append((3328, 32, 'tg1', 26))
        for b in range(6):
            blocks.append((3360 + b * 128, 128, 'q', b))
        for b in range(6):
            blocks.append((4128 + b * 128, 128, 'k', 6 + b))
        for b in range(16):
            blocks.append((5664 + b * 128, 128, 'gate', b))

        srcv = x1_d.rearrange("(n s p) d -> n p s d", p=128, s=NSUB)
        xr = pre + 'X'
        for it in range(NT):
            t0 = it * TT
            sl = it % 2
            if it == 0:
                P.op('sp', lambda e: e.dma_start(out=X, in_=srcv[0]), writes=[xr], dma_key=xr)
            HT = hT[sl]
            htr = pre + 'hT%d' % sl
            for s in range(NSUB):
                hs = (it * NSUB + s) % 2
                H = hb[hs]
                hr = pre + 'hb%d' % hs
                P.op('act', lambda e, s=s, H=H: e.activation(out=H, in_=X[:, s, :], func=AF.Square,
                                                             accum_out=ss[:, 0:1]),
                     reads=[xr], writes=[hr, pre + 'ss'])
                P.op('act', lambda e: e.activation(out=ss[:, 1:2], in_=ss[:, 0:1], func=AF.Sqrt,
                                                   scale=1.0 / D, bias=RMS_EPS),
                     reads=[pre + 'ss'], writes=[pre + 'ss1'])
                P.op('dve', lambda e: e.reciprocal(out=ss[:, 2:3], in_=ss[:, 1:2]),
                     reads=[pre + 'ss1'], writes=[pre + 'ss2'])
                P.op('dve', lambda e, s=s, H=H: e.scalar_tensor_tensor(
                    out=H, in0=X[:, s, :], scalar=ss[:, 2:3], in1=gb, op0=ALU.mult, op1=ALU.mult),
                    reads=[xr, pre + 'ss2', pre + 'gb'], writes=[hr])
                pT = bank(0).bitcast(BF16)
                for kc in range(KC):
                    P.op('pe', lambda e, kc=kc, H=H, pT=pT: e.transpose(
                        out=pT[:, kc * 128:(kc + 1) * 128], in_=H[:, kc * 128:(kc + 1) * 128], identity=ident),
                        reads=[hr, 'ident'], writes=['psT'])
                P.op('act', lambda e, HT=HT, s=s, pT=pT: e.copy(
                    out=HT[:, :, s * 128:(s + 1) * 128], in_=pT.rearrange("p (k t) -> p k t", k=KC)),
                    reads=['psT'], writes=[htr])

            if it + 1 < NT:
                P.op('sp', lambda e, it=it: e.dma_start(out=X, in_=srcv[it + 1]), writes=[xr], dma_key=xr)
            pending = []

            def flush(keep=0):
                while len(pending) > keep:
                    flush_one()

            def flush_one():
                pg, pgr, j, kind, idx = pending.pop(0)
                pss = bank(5)[:, 0:TT]
                P.op('pe', lambda e, j=j, pss=pss: e.matmul(pss, lhsT=bones, rhs=sq[j], start=True, stop=True),
                     reads=[pre + 'sq%d' % j, pre + 'bones'], writes=['pss'])
                P.op('act', lambda e, j=j, pss=pss: e.activation(out=lnb[j], in_=pss, func=AF.Ln,
                                                                 scale=1.0 / 64, bias=RMS_EPS),
                     reads=['pss'], writes=[pre + 'lnb%d' % j])
                P.op('act', lambda e, j=j: e.activation(out=lnb[j], in_=lnb[j], func=AF.Exp, scale=-0.5),
                     reads=[pre + 'lnb%d' % j], writes=[pre + 'lnb%d' % j])
                c = 0 if kind == 'q' else 1
                P.op('dve', lambda e, j=j, pg=pg, idx=idx, c=c: e.scalar_tensor_tensor(
                    out=qk_st[:, idx, :], in0=pg, scalar=qg_t[:, c:c + 1], in1=lnb[j], op0=ALU.mult, op1=ALU.mult),
                    reads=[pgr, pre + 'lnb%d' % j, pre + 'qg'], writes=[pre + 'qk_st'])

            for bi, (col0, M, kind, idx) in enumerate(blocks):
                b = 1 + bi % 4
                pgr = 'psG%d' % b
                pg = bank(b)[0:M, 0:TT]
                j = bi % 2
                for kc in range(KC):
                    P.op('pe', lambda e, kc=kc, col0=col0, M=M, pg=pg, HT=HT: e.matmul(
                        pg, lhsT=W[:, kc, col0:col0 + M], rhs=HT[:, kc, :], start=(kc == 0), stop=(kc == KC - 1)),
                        reads=[pre + 'W', htr], writes=[pgr])
                flush(keep=1 if kind in ('q', 'k') else 0)
                if kind in ('rkv', 'ta', 'tg0', 'tg1'):
                    cr = pre + 'carry%d' % idx
                    pbr = pre + 'psb%d' % j
                    tr = pre + 'tmp%d' % j
                    PS = psb[j][0:M]
                    TM = tmp[j][0:M]
                    P.op('pool', lambda e, PS=PS, idx=idx, M=M: e.tensor_copy(out=PS[:, 0:1], in_=carry[0:M, idx:idx + 1]),
                         reads=[cr], writes=[pbr])
                    P.op('act', lambda e, PS=PS, pg=pg: e.copy(out=PS[:, 1:TT + 1], in_=pg),
                         reads=[pgr, pbr], writes=[pbr])
                    P.op('dve', lambda e, PS=PS, TM=TM: e.tensor_tensor(out=TM, in0=PS[:, 0:TT], in1=PS[:, 1:TT + 1],
                                                                        op=ALU.subtract),
                         reads=[pbr], writes=[tr])
                    P.op('pool', lambda e, PS=PS, idx=idx, M=M: e.tensor_copy(out=carry[0:M, idx:idx + 1],
                                                                              in_=PS[:, TT:TT + 1]),
                         reads=[pbr], writes=[cr])
                    if kind == 'rkv':
                        P.op('dve', lambda e, PS=PS, TM=TM, idx=idx: e.scalar_tensor_tensor(
                            out=rkv_st[:, idx, :], in0=TM, scalar=mu_t[:, idx:idx + 1], in1=PS[:, 1:TT + 1],
                            op0=ALU.mult, op1=ALU.add),
                            reads=[tr, pbr, pre + 'mu'], writes=[pre + 'rkv_st%d' % (idx // 8)])
                    else:
                        P.op('dve', lambda e, PS=PS, TM=TM, idx=idx, M=M: e.scalar_tensor_tensor(
                            out=TM, in0=TM, scalar=mu_t[0:M, idx:idx + 1], in1=PS[:, 1:TT + 1],
                            op0=ALU.mult, op1=ALU.add),
                            reads=[tr, pbr, pre + 'mu'], writes=[tr])
                        if kind == 'ta':
                            P.op('act', lambda e, TM=TM: e.activation(out=ta_st[0:64], in_=TM[0:64], func=AF.Tanh),
                                 reads=[tr], writes=[pre + 'ta_st'])
                            P.op('act', lambda e, TM=TM: e.copy(out=ta_st[64:128], in_=TM[64:128]),
                                 reads=[tr], writes=[pre + 'ta_st'])
                        elif kind == 'tg0':
                            P.op('act', lambda e, TM=TM: e.activation(out=tg_st[:, 0, :], in_=TM, func=AF.Sigmoid),
                                 reads=[tr], writes=[pre + 'tg_st'])
                        else:
                            P.op('act', lambda e, TM=TM: e.activation(out=tg_st[0:32, 1, :], in_=TM, func=AF.Sigmoid),
                                 reads=[tr], writes=[pre + 'tg_st'])
                elif kind in ('q', 'k'):
                    P.op('act', lambda e, pg=pg, j=j: e.activation(out=sq[j], in_=pg, func=AF.Square),
                         reads=[pgr], writes=[pre + 'sq%d' % j])
                    pending.append((pg, pgr, j, kind, idx))
                else:
                    P.op('act', lambda e, pg=pg, idx=idx: e.activation(out=gate_st[:, idx, :], in_=pg, func=AF.Sigmoid,
                                                                       bias=bg_t[:, idx:idx + 1]),
                         reads=[pgr, pre + 'bg'], writes=[pre + 'gate_st'])
            flush()
            for s in range(NSUB):
                for (c0, n, b) in ((4896, 512, 6), (5408, 256, 7)):
                    pv = bank(b)[:, 0:n]
                    for kc in range(KC):
                        P.op('pe', lambda e, kc=kc, s=s, c0=c0, n=n, pv=pv, HT=HT: e.matmul(
                            pv, lhsT=HT[:, kc, s * 128:(s + 1) * 128], rhs=W[:, kc, c0:c0 + n],
                            start=(kc == 0), stop=(kc == KC - 1)),
                            reads=[pre + 'W', htr], writes=['psV%d' % b])
                P.op('act', lambda e, s=s: e.copy(out=v_st[:, s, 0:512], in_=bank(6)),
                     reads=['psV6'], writes=[pre + 'v_st'])
                P.op('dve', lambda e, s=s: e.tensor_copy(out=v_st[:, s, 512:768], in_=bank(7)[:, 0:256]),
                     reads=['psV7'], writes=[pre + 'v_st'])
            rv = rkv_d.rearrange("b p t -> p b t")
            for g in range(3):
                P.op('sp', lambda e, g=g, t0=t0: e.dma_start(out=rv[:, g * 8:(g + 1) * 8, t0:t0 + TT],
                                                             in_=rkv_st[:, g * 8:(g + 1) * 8, :]),
                     reads=[pre + 'rkv_st%d' % g], writes=['d_rkv'], dma_key=pre + 'rkv_st%d' % g)
            P.op('sp', lambda e, t0=t0: e.dma_start(out=ta_d[:, t0:t0 + TT], in_=ta_st),
                 reads=[pre + 'ta_st'], writes=['d_ta'], dma_key=pre + 'ta_st')
            P.op('sp', lambda e, t0=t0: e.dma_start(out=tg_d[0:128, t0:t0 + TT], in_=tg_st[:, 0, :]),
                 reads=[pre + 'tg_st'], writes=['d_tg'], dma_key=pre + 'tg_st')
            P.op('sp', lambda e, t0=t0: e.dma_start(out=tg_d[128:160, t0:t0 + TT], in_=tg_st[0:32, 1, :]),
                 reads=[pre + 'tg_st'], writes=['d_tg'], dma_key=pre + 'tg_st')
            P.op('sp', lambda e, t0=t0: e.dma_start(out=qk_d.rearrange("b p t -> p b t")[:, :, t0:t0 + TT], in_=qk_st),
                 reads=[pre + 'qk_st'], writes=['d_qk'], dma_key=pre + 'qk_st')
            P.op('sp', lambda e, t0=t0: e.dma_start(out=gate_d.rearrange("b p t -> p b t")[:, :, t0:t0 + TT],
                                                    in_=gate_st),
                 reads=[pre + 'gate_st'], writes=['d_gate'], dma_key=pre + 'gate_st')
            P.op('sp', lambda e, t0=t0: e.dma_start(
                out=v_d[t0:t0 + TT, :].rearrange("(s p) c -> p s c", p=128), in_=v_st),
                reads=[pre + 'v_st'], writes=['d_v'], dma_key=pre + 'v_st')

    w2_d = din("rwkv_w2", [64, 1024])
    a2_d = din("rwkv_a2", [64, 1024])
    g2_d = din("rwkv_g2", [160, 1024])
    prm_names = ['rwkv_w0', 'rwkv_a0', 'rwkv_k_k', 'rwkv_k_a', 'rwkv_r_k', 'rwkv_ln_w', 'rwkv_ln_b']
    prm_d = [din(n, [1024]) for n in prm_names]
    ya_d = nc.dram_tensor("ya_scr", [8, 128, S], BF16, kind="ExternalOutput" if debug else "Internal").ap()
    TB = 1024
    NCH = TB // 128
    NB = S // TB if NB_DBG is None else NB_DBG
    C0 = float(np.exp(-0.5))
    GN_EPS = 64e-5

    def rwkv_stage(pairs=range(8)):
        A.off = const_mark
        pre = 'r_'
        WA = A.alloc([1024], BF16)
        G2a = A.alloc([1024], BF16)
        G2b = A.alloc([1024], BF16)
        prm = A.alloc([7, 8], F32)
        bones = A.alloc([128], BF16)
        mk4 = A.alloc([512], F32)
        mkL = A.alloc([2, 128], F32)
        E2 = A.alloc([64], F32)
        mrow = A.alloc([TB], F32)
        f32names = ['R', 'K', 'V', 'SG', 'AA', 'GG', 'KK', 'KM', 'T1', 'BVEC', 'CS', 'T2', 'T3', 'EP', 'EN', 'EPM',
                    'EC', 'BV', 'Y32', 'DD']
        T = {n: A.alloc([TB], F32) for n in f32names}
        bfnames = ['TA', 'TG0', 'TG1', 'TQ', 'BT', 'KT', 'BH', 'KH', 'VT', 'YB', 'YO']
        for n in bfnames:
            T[n] = A.alloc([TB], BF16)
        AR = A.alloc([NCH, 2, 128], BF16)
        TM4 = A.alloc([NCH, 4, 128], BF16)
        PC = A.alloc([NCH], F32)
        SC = [[A.alloc([512], BF16) for _ in range(2)] for _ in range(2)]
        LZ = [[A.alloc([2, 384], BF16) for _ in range(2)] for _ in range(2)]
        MCz = [A.alloc([2, 64], BF16) for _ in range(2)]
        QT = [A.alloc([128], BF16) for _ in range(2)]
        STz = A.alloc([2, 64], BF16)
        DBT = ['BT', 'KT', 'GG', 'BV', 'Y32']
        T2 = {n: [T[n], A.alloc([TB], BF16 if n in ('BT', 'KT') else F32)] for n in DBT}
        AR2 = [AR, A.alloc([NCH, 2, 128], BF16)]
        TM42 = [TM4, A.alloc([NCH, 4, 128], BF16)]
        PC2 = [PC, A.alloc([NCH], F32)]
        par = [0]
        coll = [None]

        class TP(dict):
            def __getitem__(self, k):
                if k in T2:
                    return T2[k][par[0]]
                return dict.__getitem__(self, k)

        class BP(object):
            def __init__(self, bufs):
                self.bufs = bufs

            def __getitem__(self, idx):
                return self.bufs[par[0]][idx]

            def rearrange(self, *a_, **k_):
                return self.bufs[par[0]].rearrange(*a_, **k_)
        T = TP(T)
        AR = BP(AR2)
        TM4 = BP(TM42)
        PC = BP(PC2)
        DBNAMES = set(DBT) | {'AR0', 'AR1', 'PC'} | {'TM4_%d' % c_ for c_ in range(NCH)}
        print("rwkv_stage arena", A.off)

        def R_(n):
            return pre + n
        P.op('pool', lambda e: e.dma_start(out=WA[0:64, :], in_=w2_d), writes=[R_('WA')], dma_key=R_('w'))
        P.op('pool', lambda e: e.dma_start(out=WA[64:128, :], in_=a2_d), writes=[R_('WA')], dma_key=R_('w'))
        P.op('pool', lambda e: e.dma_start(out=G2a, in_=g2_d[0:128, :]), writes=[R_('G2')], dma_key=R_('w'))
        P.op('pool', lambda e: e.dma_start(out=G2b[0:32, :], in_=g2_d[128:160, :]), writes=[R_('G2')], dma_key=R_('w'))
        for i in range(7):
            P.op('sp', lambda e, i=i: e.dma_start(out=prm[:, i, :], in_=prm_d[i].rearrange("(b p) -> p b", p=128),
                                                   allow_slow_non_contiguous=True),
                 writes=[R_('prm')], dma_key=R_('par'))
        P.op('pool', lambda e: e.memset(bones, 0.0), writes=[R_('bones')])
        P.op('pool', lambda e: e.memset(bones[0:64, 0:64], 1.0), reads=[R_('bones')], writes=[R_('bones')])
        P.op('pool', lambda e: e.memset(bones[64:128, 64:128], 1.0), reads=[R_('bones')], writes=[R_('bones')])
        P.op('pool', lambda e: e.memset(mk4, 1.0), writes=[R_('mk4')])
        for q in range(4):
            base = -1 if q % 2 == 0 else 0
            P.op('pool', lambda e, q=q, base=base: e.affine_select(
                out=mk4[:, q * 128:(q + 1) * 128], in_=mk4[:, q * 128:(q + 1) * 128], pattern=[[1, 128]],
                compare_op=ALU.is_ge, fill=0.0, base=base, channel_multiplier=-1),
                reads=[R_('mk4')], writes=[R_('mk4')])
        P.op('pool', lambda e: e.memset(mkL, 1.0), writes=[R_('mkL')])
        P.op('pool', lambda e: e.affine_select(out=mkL, in_=mkL, pattern=[[0, 2], [-1, 128]], compare_op=ALU.is_ge,
                                               fill=0.0, base=-1, channel_multiplier=1),
             reads=[R_('mkL')], writes=[R_('mkL')])
        P.op('pool', lambda e: e.tensor_copy(out=E2[0:64, :], in_=ident_f[0:64, 0:64]), reads=['ident_f'], writes=[R_('E2')])
        P.op('pool', lambda e: e.tensor_copy(out=E2[64:128, :], in_=ident_f[64:128, 64:128]), reads=['ident_f'],
             writes=[R_('E2')])
        P.op('pool', lambda e: e.memset(mrow, 1.0), writes=[R_('mrow')])
        P.op('pool', lambda e: e.memset(mrow.rearrange("p (c t) -> p c t", t=128)[:, :, 0:1], 0.0),
             reads=[R_('mrow')], writes=[R_('mrow')])

        def ch3(ap):
            return ap.rearrange("p (c t) -> p c t", t=128)

        def nm(x):
            if x.startswith('pb') or x in ('ident', 'ident_f'):
                return x
            if x in DBNAMES:
                return R_(x) + '@%d' % par[0]
            return R_(x)

        def pdma(eng, fn, reads=(), writes=(), dma_key=None):
            p_ = par[0]

            def fn2(e, fn=fn, p_=p_):
                par[0] = p_
                return fn(e)
            args = (eng, fn2, list(reads), list(writes), dma_key)
            if coll[0] is not None:
                coll[0].append(args)
            else:
                P.op(args[0], args[1], reads=args[2], writes=args[3], dma_key=args[4])

        def ew(eng, fn, reads, writes):
            pdma(eng, fn, [nm(x) for x in reads], [nm(x) for x in writes])

        for sl_ in range(2):
            ew('pool', lambda e, sl_=sl_: e.memset(MCz[sl_], 0.0), [], ['MC%d' % sl_])
        funcs = {}
        for hp in pairs:
            cols = slice(hp * 128, (hp + 1) * 128)
            def prep(tb, hp=hp, cols=cols):
                t0 = tb * TB
                tsl = slice(t0, t0 + TB)
                for i, n in enumerate(['R', 'K', 'V']):
                    pdma('sp', lambda e, i=i, n=n, hp=hp, tsl=tsl: e.dma_start(out=T[n], in_=rkv_d[i * 8 + hp, :, tsl]),
                         writes=[R_(n)], dma_key=R_('ld' + n))
                pdma('sp', lambda e, tsl=tsl: e.dma_start(out=T['TA'], in_=ta_d[:, tsl]), writes=[R_('TA')],
                     dma_key=R_('ldTA'))
                pdma('sp', lambda e, tsl=tsl: e.dma_start(out=T['TG0'], in_=tg_d[0:128, tsl]), writes=[R_('TG0')],
                     dma_key=R_('ldTG0'))
                pdma('sp', lambda e, tsl=tsl: e.dma_start(out=T['TG1'][0:32], in_=tg_d[128:160, tsl]),
                     writes=[R_('TG1')], dma_key=R_('ldTG1'))
                for hf in range(2):
                    hs = slice(hf * 512, (hf + 1) * 512)
                    ew('pe', lambda e, hs=hs, cols=cols: e.matmul(bank(1), lhsT=WA[0:64, cols], rhs=T['TA'][0:64, hs],
                                                                  start=True, stop=True), ['WA', 'TA'], ['pb1'])
                    ew('act', lambda e, hs=hs, hp=hp: e.activation(out=T['SG'][:, hs], in_=bank(1), func=AF.Sigmoid,
                                                                   bias=prm[:, 0, hp:hp + 1]), ['pb1', 'prm'], ['SG'])
                    ew('pe', lambda e, hs=hs, cols=cols: e.matmul(bank(2), lhsT=WA[64:128, cols], rhs=T['TA'][64:128, hs],
                                                                  start=True, stop=True), ['WA', 'TA'], ['pb2'])
                    ew('act', lambda e, hs=hs, hp=hp: e.activation(out=T['AA'][:, hs], in_=bank(2), func=AF.Sigmoid,
                                                                   bias=prm[:, 1, hp:hp + 1]), ['pb2', 'prm'], ['AA'])
                    ew('pe', lambda e, hs=hs, cols=cols: e.matmul(bank(3), lhsT=G2a[:, cols], rhs=T['TG0'][:, hs],
                                                                  start=True, stop=False), ['G2', 'TG0'], ['pb3', 'pb3'])
                    ew('pe', lambda e, hs=hs, cols=cols: e.matmul(bank(3), lhsT=G2b[0:32, cols], rhs=T['TG1'][0:32, hs],
                                                                  start=False, stop=True), ['G2', 'TG1'], ['pb3', 'pb3'])
                    ew('act', lambda e, hs=hs: e.copy(out=T['GG'][:, hs], in_=bank(3)), ['pb3', 'pb3'], ['GG'])
                ew('dve', lambda e, hp=hp: e.tensor_scalar(out=T['KK'], in0=T['K'], scalar1=prm[:, 2, hp:hp + 1],
                                                           scalar2=None, op0=ALU.mult), ['K', 'prm'], ['KK'])
                ew('act', lambda e: e.activation(out=T['TQ'], in_=T['KK'], func=AF.Square), ['KK'], ['TQ'])
                for hf in range(2):
                    hs = slice(hf * 512, (hf + 1) * 512)
                    ew('pe', lambda e, hs=hs: e.matmul(bank(4), lhsT=bones, rhs=T['TQ'][:, hs], start=True, stop=True),
                       ['bones', 'TQ'], ['pb4'])
                    ew('dve', lambda e, hs=hs: e.tensor_scalar(out=T['T1'][:, hs], in0=bank(4), scalar1=1e-19,
                                                               scalar2=None, op0=ALU.max), ['pb4'], ['T1'])
                ew('act', lambda e: e.activation(out=T['T1'], in_=T['T1'], func=AF.Ln), ['T1'], ['T1'])
                ew('act', lambda e: e.activation(out=T['T1'], in_=T['T1'], func=AF.Exp, scale=-0.5), ['T1'], ['T1'])
                ew('dve', lambda e: e.tensor_tensor(out=T['KK'], in0=T['KK'], in1=T['T1'], op=ALU.mult),
                   ['KK', 'T1'], ['KK'])
                ew('dve', lambda e, hp=hp: e.tensor_scalar(out=T['T1'], in0=T['AA'], scalar1=-1.0,
                                                           scalar2=prm[:, 3, hp:hp + 1], op0=ALU.add, op1=ALU.mult),
                   ['AA', 'prm'], ['T1'])
                ew('dve', lambda e: e.scalar_tensor_tensor(out=T['KM'], in0=T['T1'], scalar=1.0, in1=T['K'],
                                                           op0=ALU.add, op1=ALU.mult), ['T1', 'K'], ['KM'])
                ew('pool', lambda e: e.tensor_tensor(out=T['T1'], in0=T['R'], in1=T['KM'], op=ALU.mult),
                   ['R', 'KM'], ['T1'])
                ew('pool', lambda e, hp=hp: e.tensor_scalar(out=T['TQ'], in0=T['T1'], scalar1=prm[:, 4, hp:hp + 1],
                                                            scalar2=None, op0=ALU.mult), ['T1', 'prm'], ['TQ'])
                for hf in range(2):
                    hs = slice(hf * 512, (hf + 1) * 512)
                    ew('pe', lambda e, hs=hs: e.matmul(bank(5), lhsT=bones, rhs=T['TQ'][:, hs], start=True, stop=True),
                       ['bones', 'TQ'], ['pb5'])
                    ew('dve', lambda e, hs=hs: e.tensor_tensor(out=T['BV'][:, hs], in0=T['V'][:, hs], in1=bank(5),
                                                               op=ALU.mult), ['pb5', 'V'], ['BV'])
                ew('pool', lambda e: e.tensor_tensor(out=T['BVEC'], in0=T['KK'], in1=T['AA'], op=ALU.mult),
                   ['KK', 'AA'], ['BVEC'])
                ew('dve', lambda e: e.tensor_tensor_scan(out=T['CS'], data0=mrow, data1=T['SG'], initial=0.0,
                                                         op0=ALU.mult, op1=ALU.add), ['mrow', 'SG'], ['CS'])
                ew('pool', lambda e: e.tensor_tensor(out=T['T2'], in0=T['CS'], in1=T['SG'], op=ALU.subtract),
                   ['CS', 'SG'], ['T2'])
                ew('act', lambda e: e.activation(out=T['EP'], in_=T['CS'], func=AF.Exp, scale=-C0), ['CS'], ['EP'])
                ew('act', lambda e: e.activation(out=T['EN'], in_=T['CS'], func=AF.Exp, scale=C0), ['CS'], ['EN'])
                ew('act', lambda e: e.activation(out=T['EPM'], in_=T['T2'], func=AF.Exp, scale=-C0), ['T2'], ['EPM'])
                ew('dve', lambda e: e.tensor_tensor(
                    out=ch3(T['T3']), in0=ch3(T['CS']), in1=ch3(T['CS'])[:, :, 127:128].to_broadcast([128, NCH, 128]),
                    op=ALU.subtract), ['CS'], ['T3'])
                ew('act', lambda e: e.activation(out=T['EC'], in_=T['T3'], func=AF.Exp, scale=C0), ['T3'], ['EC'])
                ew('act', lambda e: e.activation(out=PC.rearrange("p (c o) -> p c o", o=1),
                                                 in_=ch3(T['CS'])[:, :, 127:128], func=AF.Exp, scale=-C0),
                   ['CS'], ['PC'])
                ew('dve', lambda e: e.scalar_tensor_tensor(out=AR[:, :, 0, :], in0=ch3(T['EPM']), scalar=-1.0,
                                                           in1=ch3(T['KK']), op0=ALU.mult, op1=ALU.mult),
                   ['EPM', 'KK'], ['AR0'])
                ew('pool', lambda e: e.tensor_tensor(out=AR[:, :, 1, :], in0=ch3(T['EP']), in1=ch3(T['R']), op=ALU.mult),
                   ['EP', 'R'], ['AR1'])
                ew('dve', lambda e: e.tensor_tensor(out=T['BT'], in0=T['EN'], in1=T['BVEC'], op=ALU.mult),
                   ['EN', 'BVEC'], ['BT'])
                ew('pool', lambda e: e.tensor_tensor(out=T['KT'], in0=T['EN'], in1=T['KM'], op=ALU.mult),
                   ['EN', 'KM'], ['KT'])
                ew('dve', lambda e: e.tensor_tensor(out=T['BH'], in0=T['EC'], in1=T['BVEC'], op=ALU.mult),
                   ['EC', 'BVEC'], ['BH'])
                ew('pool', lambda e: e.tensor_tensor(out=T['KH'], in0=T['EC'], in1=T['KM'], op=ALU.mult),
                   ['EC', 'KM'], ['KH'])
                ew('act', lambda e: e.copy(out=T['VT'], in_=T['V']), ['V'], ['VT'])
                pT = bank(0).bitcast(BF16)
                for c in range(NCH):
                    cs_ = slice(c * 128, (c + 1) * 128)
                    srcs = [(AR[:, c, 0, :], 'AR0'), (T['VT'][:, cs_], 'VT'), (T['BH'][:, cs_], 'BH'),
                            (T['KH'][:, cs_], 'KH')]
                    for q, (sap, sr) in enumerate(srcs):
                        ew('pe', lambda e, q=q, sap=sap: e.transpose(out=pT[:, q * 128:(q + 1) * 128], in_=sap,
                                                                     identity=ident), [sr, 'ident'], ['pb0'])
                    ew('act', lambda e, c=c: e.copy(out=TM4[:, c, :, :], in_=pT[:, 0:512].rearrange("p (q t) -> p q t", q=4)),
                       ['pb0'], ['TM4_%d' % c])

            def chunks(tb, inter):
                P1 = (1, 2)
                P2 = ((4, 5), (6, 7))

                def partA(c, sl):
                    cs_ = slice(c * 128, (c + 1) * 128)
                    tm = 'TM4_%d' % c
                    arc = AR[:, c, :, :].rearrange("p a t -> p (a t)")
                    b1 = P1[sl]
                    ps1 = bank(b1)
                    for h2 in range(2):
                        psl = slice(64 * h2, 64 * h2 + 64)
                        scr = 'SC%d_%d' % (sl, h2)
                        ew('pe', lambda e, ps1=ps1, psl=psl, cs_=cs_, arc=arc: e.matmul(
                            ps1[:, 0:256], lhsT=T['BT'][psl, cs_], rhs=arc[psl, :], start=True, stop=True),
                            ['BT', 'AR0', 'AR1'], ['pb%d' % b1])
                        ew('pe', lambda e, ps1=ps1, psl=psl, cs_=cs_, arc=arc: e.matmul(
                            ps1[:, 256:512], lhsT=T['KT'][psl, cs_], rhs=arc[psl, :], start=True, stop=True),
                            ['KT', 'AR0', 'AR1'], ['pb%d' % b1])
                        ew('dve', lambda e, ps1=ps1, h2=h2, sl=sl: e.tensor_tensor(out=SC[sl][h2], in0=mk4, in1=ps1,
                                                                                   op=ALU.mult),
                           ['pb%d' % b1, 'mk4'], [scr])
                        b2 = P2[sl][h2]
                        ew('pe', lambda e, b2=b2, psl=psl, cs_=cs_, c=c: e.matmul(
                            bank(b2)[:, 384:512], lhsT=AR[psl, c, 0, :], rhs=T['BT'][psl, cs_],
                            start=True, stop=True), ['AR0', 'BT'], ['pb%d' % b2])
                    for h2 in range(2):
                        b2 = P2[sl][h2]
                        ew('dve', lambda e, h2=h2, b2=b2, sl=sl: e.tensor_tensor(
                            out=LZ[sl][0][:, h2, 0:128], in0=mkL[:, h2, :], in1=bank(b2)[:, 384:512], op=ALU.mult),
                            ['pb%d' % b2, 'mkL'], ['LL%d_0_%d' % (sl, h2)])
                    for h2 in range(2):
                        pb = 64 * h2
                        ew('pe', lambda e, h2=h2, pb=pb, c=c, sl=sl: e.matmul(
                            bank(3)[:, 256 + h2 * 64:256 + (h2 + 1) * 64], lhsT=SC[sl][h2][:, 256:384],
                            rhs=TM4[:, c, 1, pb:pb + 64], start=True, stop=True), ['SC%d_%d' % (sl, h2), tm], ['pb3'])
                    ew('pool', lambda e, c=c, sl=sl: e.tensor_copy(
                        out=LZ[sl][0][:, :, 128:192], in_=TM4[:, c, 0, :].rearrange("p (h k) -> p h k", h=2)),
                        [tm], ['ZZ%d_0_0' % sl, 'ZZ%d_0_1' % sl])
                    for h2 in range(2):
                        ew('act', lambda e, h2=h2, sl=sl: e.copy(out=LZ[sl][0][:, h2, 192:256],
                                                                 in_=bank(3)[:, 256 + h2 * 64:256 + (h2 + 1) * 64]),
                           ['pb3'], ['ZZ%d_0_%d' % (sl, h2)])

                def partB(c, sl, n):
                    pp = n % 2
                    for h2 in range(2):
                        b2 = P2[sl][h2]
                        ps2 = bank(b2)
                        pbr = 'pb%d' % b2
                        ltn = SC[sl][h2][:, 0:128] if n == 0 else LZ[sl][pp][:, h2, 256:384]
                        rds = ['LL%d_%d_%d' % (sl, pp, h2), 'ZZ%d_%d_%d' % (sl, pp, h2)] + \
                            (['SC%d_%d' % (sl, h2)] if n == 0 else [])
                        if n < 6:
                            ew('pe', lambda e, ps2=ps2, ltn=ltn, pp=pp, h2=h2, sl=sl: e.matmul(
                                ps2[:, 0:256], lhsT=ltn, rhs=LZ[sl][pp][:, h2, 0:256], start=True, stop=True),
                                rds, [pbr])
                            ew('pe', lambda e, ps2=ps2, ltn=ltn, pp=pp, h2=h2, sl=sl: e.matmul(
                                ps2[:, 256:384], lhsT=LZ[sl][pp][:, h2, 0:128], rhs=ltn, start=True, stop=True),
                                rds, [pbr])
                        else:
                            ew('pe', lambda e, ps2=ps2, ltn=ltn, pp=pp, h2=h2, sl=sl: e.matmul(
                                ps2[:, 128:256], lhsT=ltn, rhs=LZ[sl][pp][:, h2, 128:256], start=True, stop=True),
                                rds, [pbr])

                    def cp(h2):
                        b2 = P2[sl][h2]
                        ew('act', lambda e, h2=h2, b2=b2: e.copy(
                            out=LZ[sl][1 - pp][:, h2, :].rearrange("p (s t) -> p s t", s=3)[:, 0:3:2, :],
                            in_=bank(b2)[:, 0:384].rearrange("p (s t) -> p s t", s=3)[:, 0:3:2, :]),
                            ['pb%d' % b2], ['LL%d_%d_%d' % (sl, 1 - pp, h2)])

                    def ad(h2):
                        b2 = P2[sl][h2]
                        ew('dve', lambda e, h2=h2, b2=b2: e.tensor_tensor(
                            out=LZ[sl][1 - pp][:, h2, 128:256], in0=LZ[sl][pp][:, h2, 128:256],
                            in1=bank(b2)[:, 128:256], op=ALU.add),
                            ['pb%d' % b2, 'ZZ%d_%d_%d' % (sl, pp, h2)], ['ZZ%d_%d_%d' % (sl, 1 - pp, h2)])
                    if n < 6:
                        cp(0)
                        ad(1)
                        cp(1)
                        ad(0)
                    else:
                        ad(0)
                        ad(1)

                def partC(c, sl):
                    tm = 'TM4_%d' % c
                    ZF = LZ[sl][1]
                    p3 = bank(0)[:, 192:384]
                    for h2 in range(2):
                        pb = 64 * h2
                        psl = slice(pb, pb + 64)
                        zr = 'ZZ%d_1_%d' % (sl, h2)
                        ew('pe', lambda e, h2=h2, pb=pb, psl=psl, c=c: e.matmul(
                            p3[psl, 0:64], lhsT=ZF[:, h2, 128:192], rhs=TM4[:, c, 2, pb:pb + 64],
                            start=True, stop=True, tile_position=(0, pb)), [zr, tm], ['pb0'])
                        ew('pe', lambda e, h2=h2, pb=pb, psl=psl: e.matmul(
                            p3[psl, 64:192], lhsT=ZF[:, h2, 128:192], rhs=SC[sl][h2][:, 128:256],
                            start=True, stop=True, tile_position=(0, pb)), [zr, 'SC%d_%d' % (sl, h2)], ['pb0'])
                    for h2 in range(2):
                        psl = slice(64 * h2, 64 * h2 + 64)
                        ew('dve', lambda e, c=c, psl=psl, h2=h2: e.scalar_tensor_tensor(
                            out=MCz[sl][psl, h2, :], in0=E2[psl, :], scalar=PC[psl, c:c + 1], in1=p3[psl, 0:64],
                            op0=ALU.mult, op1=ALU.add), ['E2', 'PC', 'pb0'], ['MC%d' % sl])
                    ew('dve', lambda e, c=c: e.tensor_tensor(out=QT[sl], in0=AR[:, c, 1, :], in1=p3[:, 64:192],
                                                             op=ALU.add), ['pb0', 'AR1'], ['QT%d' % sl])

                def partD(c, sl):
                    cs_ = slice(c * 128, (c + 1) * 128)
                    tm = 'TM4_%d' % c
                    ZF = LZ[sl][1]
                    for h2 in range(2):
                        pb = 64 * h2
                        psl = slice(pb, pb + 64)
                        sb_ = 3 if h2 == 0 else 0
                        sr_ = 'pb%d' % sb_
                        psY = bank(sb_)[:, 0:128]
                        psS = bank(sb_)[:, 128:192]
                        UU = ZF[:, h2, 192:256]
                        zr = 'ZZ%d_1_%d' % (sl, h2)
                        scr = 'SC%d_%d' % (sl, h2)
                        ew('pe', lambda e, psl=psl, pb=pb, h2=h2, UU=UU, psY=psY: e.matmul(
                            psY[psl, :], lhsT=UU, rhs=SC[sl][h2][:, 128:256], start=True, stop=False,
                            tile_position=(0, pb)), [zr, scr], [sr_])
                        ew('pe', lambda e, psl=psl, pb=pb, h2=h2, c=c, psY=psY: e.matmul(
                            psY[psl, :], lhsT=TM4[:, c, 1, pb:pb + 64], rhs=SC[sl][h2][:, 384:512], start=False,
                            stop=False, tile_position=(0, pb)), [tm, scr], [sr_])
                        ew('pe', lambda e, psl=psl, pb=pb, psY=psY, h2=h2: e.matmul(
                            psY[psl, :], lhsT=STz[:, h2, :], rhs=QT[sl], start=False, stop=True,
                            tile_position=(0, pb)), ['ST', 'QT%d' % sl], [sr_])
                        ew('pe', lambda e, psl=psl, pb=pb, psS=psS, h2=h2: e.matmul(
                            psS[psl, :], lhsT=MCz[sl][:, h2, :], rhs=STz[:, h2, :], start=True, stop=False,
                            tile_position=(0, pb)), ['MC%d' % sl, 'ST'], [sr_])
                        ew('pe', lambda e, psl=psl, pb=pb, c=c, UU=UU, psS=psS: e.matmul(
                            psS[psl, :], lhsT=TM4[:, c, 2, pb:pb + 64], rhs=UU, start=False, stop=False,
                            tile_position=(0, pb)), [tm, zr], [sr_])
                        ew('pe', lambda e, psl=psl, pb=pb, c=c, psS=psS: e.matmul(
                            psS[psl, :], lhsT=TM4[:, c, 3, pb:pb + 64], rhs=TM4[:, c, 1, pb:pb + 64], start=False,
                            stop=True, tile_position=(0, pb)), [tm], [sr_])
                    for h2 in range(2):
                        pb = 64 * h2
                        psl = slice(pb, pb + 64)
                        sb_ = 3 if h2 == 0 else 0
                        sr_ = 'pb%d' % sb_
                        ew('act', lambda e, cs_=cs_, psl=psl, sb_=sb_: e.copy(out=T['Y32'][psl, cs_],
                                                                             in_=bank(sb_)[psl, 0:128]),
                           [sr_], ['Y32'])
                        ew('dve', lambda e, psl=psl, sb_=sb_, h2=h2: e.tensor_copy(out=STz[psl, h2, :],
                                                                                   in_=bank(sb_)[psl, 128:192]),
                           [sr_], ['ST'])

                for c0 in range(0, NCH, 2):
                    for sl in range(2):
                        partA(c0 + sl, sl)
                    for n in range(7):
                        for sl in range(2):
                            partB(c0 + sl, sl, n)
                        if n in (1, 3, 5):
                            inter()
                    for sl in range(2):
                        partC(c0 + sl, sl)
                    for sl in range(2):
                        partD(c0 + sl, sl)
                    inter()

            def outst(tb, hp=hp):
                t0 = tb * TB
                tsl = slice(t0, t0 + TB)
                ew('pool', lambda e: e.tensor_copy(out=T['YB'], in_=T['Y32']), ['Y32'], ['YB'])
                for hf in range(2):
                    hs = slice(hf * 512, (hf + 1) * 512)
                    ew('pe', lambda e, hs=hs: e.matmul(bank(1), lhsT=bones, rhs=T['YB'][:, hs], start=True, stop=True),
                       ['bones', 'YB'], ['pb1'])
                    ew('dve', lambda e, hs=hs: e.scalar_tensor_tensor(out=T['DD'][:, hs], in0=bank(1), scalar=-1.0 / 64,
                                                                      in1=T['Y32'][:, hs], op0=ALU.mult, op1=ALU.add),
                       ['pb1', 'Y32'], ['DD'])
                ew('act', lambda e: e.activation(out=T['TQ'], in_=T['DD'], func=AF.Square), ['DD'], ['TQ'])
                for hf in range(2):
                    hs = slice(hf * 512, (hf + 1) * 512)
                    ew('pe', lambda e, hs=hs: e.matmul(bank(2), lhsT=bones, rhs=T['TQ'][:, hs], start=True, stop=True),
                       ['bones', 'TQ'], ['pb2'])
                    ew('act', lambda e, hs=hs: e.activation(out=T['T1'][:, hs], in_=bank(2), func=AF.Ln, scale=1.0 / 64,
                                                            bias=GN_EPS), ['pb2'], ['T1'])
                ew('act', lambda e: e.activation(out=T['T1'], in_=T['T1'], func=AF.Exp, scale=-0.5), ['T1'], ['T1'])
                ew('dve', lambda e: e.tensor_tensor(out=T['DD'], in0=T['DD'], in1=T['T1'], op=ALU.mult),
                   ['DD', 'T1'], ['DD'])
                ew('dve', lambda e, hp=hp: e.tensor_scalar(out=T['DD'], in0=T['DD'], scalar1=prm[:, 5, hp:hp + 1],
                                                           scalar2=prm[:, 6, hp:hp + 1], op0=ALU.mult, op1=ALU.add),
                   ['DD', 'prm'], ['DD'])
                ew('pool', lambda e: e.tensor_tensor(out=T['DD'], in0=T['DD'], in1=T['BV'], op=ALU.add),
                   ['DD', 'BV'], ['DD'])
                ew('dve', lambda e: e.tensor_tensor(out=T['YO'], in0=T['DD'], in1=T['GG'], op=ALU.mult),
                   ['DD', 'GG'], ['YO'])
                pdma('sp', lambda e, hp=hp, tsl=tsl: e.dma_start(out=ya_d[hp, :, tsl], in_=T['YO']),
                     reads=[R_('YO')], writes=['d_ya'], dma_key=R_('stYO'))


            funcs[hp] = (prep, chunks, outst)

        def emit_all(lst):
            for a_ in lst:
                P.op(a_[0], a_[1], reads=a_[2], writes=a_[3], dma_key=a_[4])
        blocks = [(hp_, tb_) for hp_ in pairs for tb_ in range(NB)]
        par[0] = 0
        funcs[blocks[0][0]][0](blocks[0][1])
        pend_out = []
        for bi, (hp_, tb_) in enumerate(blocks):
            prep_f, chunks_f, outst_f = funcs[hp_]
            pend = list(pend_out)
            if bi + 1 < len(blocks):
                nhp, ntb = blocks[bi + 1]
                par[0] = (bi + 1) % 2
                coll[0] = pend
                funcs[nhp][0](ntb)
                coll[0] = None
            par[0] = bi % 2
            if tb_ == 0:
                ew('pool', lambda e: e.memset(STz, 0.0), [], ['ST'])
            nsl = 4 * (NCH // 2)
            step = (len(pend) + nsl - 1) // nsl if pend else 0
            pos = [0]

            def inter(pend=pend, step=step, pos=pos):
                open_banks = set()
                cnt = 0
                while pos[0] < len(pend) and (cnt < step or open_banks):
                    a_ = pend[pos[0]]
                    pos[0] += 1
                    cnt += 1
                    if a_[0] == 'pe':
                        open_banks.update(w_ for w_ in a_[3] if w_.startswith('pb'))
                    else:
                        open_banks.difference_update(r_ for r_ in a_[2] if r_.startswith('pb'))
                    P.op(a_[0], a_[1], reads=a_[2], writes=a_[3], dma_key=a_[4])
            chunks_f(tb_, inter)
            emit_all(pend[pos[0]:])
            par[0] = bi % 2
            pend_out = []
            coll[0] = pend_out
            outst_f(tb_)
            coll[0] = None
        emit_all(pend_out)

    yb_d = nc.dram_tensor("yb_scr", [6, 128, S], BF16, kind="ExternalOutput" if debug else "Internal").ap()
    DIL = (1, 4, 16)

    def attn_stage_full(js=range(4)):
        A.off = const_mark
        pre = 'a_'

        def R_(n):
            return pre + n

        def nm(x):
            if x.startswith('pb') or x in ('ident', 'ident_f'):
                return x
            return R_(x)

        def ew(eng, fn, reads, writes):
            P.op(eng, fn, reads=[nm(x) for x in reads], writes=[nm(x) for x in writes])
        QH = A.alloc([S], BF16)
        KH = A.alloc([S], BF16)
        VX = A.alloc([32, 64], BF16)
        ONES = A.alloc([64], BF16)
        OT = [A.alloc([S], F32) for _ in range(3)]
        DEN = [A.alloc([S], F32) for _ in range(3)]
        PT = [A.alloc([256], BF16) for _ in range(4)]
        mask2 = A.alloc([256], BF16)
        RD = [A.alloc([512], F32) for _ in range(2)]
        YBS = [A.alloc([512], BF16) for _ in range(2)]
        print("attn_stage arena", A.off)
        P.op('pool', lambda e: e.memset(mask2, 1.0), writes=[R_('mask')])
        P.op('pool', lambda e: e.affine_select(out=mask2[:, 0:128], in_=mask2[:, 0:128], pattern=[[1, 128]],
                                               compare_op=ALU.is_ge, fill=0.0, base=0, channel_multiplier=-1),
             reads=[R_('mask')], writes=[R_('mask')])
        P.op('pool', lambda e: e.affine_select(out=mask2[:, 128:256], in_=mask2[:, 128:256], pattern=[[-1, 128]],
                                               compare_op=ALU.is_ge, fill=0.0, base=0, channel_multiplier=1),
             reads=[R_('mask')], writes=[R_('mask')])
        P.op('pool', lambda e: e.memset(ONES, 1.0), writes=[R_('ONES')])

        tcount = [0]
        ccount = [0]
        for j in js:
            for g in range(3):
                d = DIL[g]
                nb = S // d // 128
                h = 4 * g + j
                pair = h // 2
                pb = 64 * (h % 2)
                vv = v_d.rearrange("(m d) c -> d m c", d=d)
                for r in range(d):
                    for n0 in range(0, nb, 8):
                        n1 = min(nb, n0 + 8)
                        P.op('sp', lambda e, r=r, h=h, nb=nb, vv=vv, n0=n0, n1=n1: e.dma_start(
                            out=VX[:, r * nb + n0:r * nb + n1, :],
                            in_=vv[r, n0 * 128:n1 * 128, h * 64:(h + 1) * 64].rearrange("(n i) c -> i n c", i=128)),
                            writes=[R_('VX')], dma_key=R_('ldV'))
                P.op('sp', lambda e, pair=pair, pb=pb: e.dma_start(out=QH[0:64, :], in_=qk_d[pair, pb:pb + 64, :]),
                     writes=[R_('QH')], dma_key=R_('ldQ'))
                P.op('sp', lambda e, pair=pair, pb=pb: e.dma_start(out=KH[0:64, :], in_=qk_d[6 + pair, pb:pb + 64, :]),
                     writes=[R_('KH')], dma_key=R_('ldK'))
                qv = QH.rearrange("p (m d) -> p d m", d=d)
                kv = KH.rearrange("p (m d) -> p d m", d=d)
                otr = 'OT%d' % g
                otv = OT[g].rearrange("p (m d) -> p d m", d=d)
                dnv = DEN[g].rearrange("p (m d) -> p d m", d=d)
                tiles = []
                for r in range(d):
                    for n in range(nb):
                        tiles.append((r, n, tcount[0]))
                        tcount[0] += 1

                def emit_score(r, n, ti, kv=kv, qv=qv, nb=nb):
                    nq = 256 if n + 1 < nb else 128
                    sbk = (1, 2, 6)[ti % 3]
                    ps = bank(sbk)[:, 0:nq]
                    pt = PT[ti % 4]
                    ptr = 'PT%d' % (ti % 4)
                    ew('pe', lambda e, ps=ps, r=r, n=n, nq=nq: e.matmul(
                        ps, lhsT=kv[0:64, r, 128 * n:128 * n + 128], rhs=qv[0:64, r, 128 * n:128 * n + nq],
                        start=True, stop=True), ['KH', 'QH'], ['pb%d' % sbk])
                    ew('act', lambda e, ps=ps, pt=pt, nq=nq: e.activation(out=pt[:, 0:nq], in_=ps, func=AF.Exp),
                       ['pb%d' % sbk], [ptr])
                    ew('pool', lambda e, pt=pt, nq=nq: e.tensor_tensor(out=pt[:, 0:nq], in0=pt[:, 0:nq],
                                                                      in1=mask2[:, 0:nq], op=ALU.mult),
                       [ptr, 'mask'], [ptr])

                def emit_pv(r, n, ti, otv=otv, dnv=dnv, nb=nb, g=g, otr=otr):
                    pt = PT[ti % 4]
                    ptr = 'PT%d' % (ti % 4)
                    obk = 3 + ti % 2
                    po = bank(obk)[0:64, 0:128]
                    pdn = bank(obk)[0:64, 128:256]
                    vt = r * nb + n
                    has_prev = n > 0
                    if has_prev:
                        ppt = PT[(ti - 1) % 4]
                        pptr = 'PT%d' % ((ti - 1) % 4)
                        ew('pe', lambda e, ppt=ppt, vt=vt: e.matmul(
                            po, lhsT=VX[:, vt - 1, :], rhs=ppt[:, 128:256], start=True, stop=False),
                            ['VX', pptr], ['pb%d' % obk])
                    ew('pe', lambda e, pt=pt, vt=vt: e.matmul(
                        po, lhsT=VX[:, vt, :], rhs=pt[:, 0:128], start=(not has_prev), stop=True),
                        ['VX', ptr], ['pb%d' % obk])
                    if has_prev:
                        ew('pe', lambda e, ppt=ppt: e.matmul(
                            pdn, lhsT=ONES, rhs=ppt[:, 128:256], start=True, stop=False),
                            ['ONES', pptr], ['pb%d' % obk])
                    ew('pe', lambda e, pt=pt: e.matmul(
                        pdn, lhsT=ONES, rhs=pt[:, 0:128], start=(not has_prev), stop=True),
                        ['ONES', ptr], ['pb%d' % obk])
                    ew('dve', lambda e, r=r, n=n: e.tensor_copy(
                        out=otv[0:64, r, 128 * n:128 * n + 128], in_=po), ['pb%d' % obk], [otr])
                    ew('dve', lambda e, r=r, n=n: e.tensor_copy(
                        out=dnv[0:64, r, 128 * n:128 * n + 128], in_=pdn), ['pb%d' % obk], ['DEN%d' % g])

                LAG = 2
                for idx, (r, n, ti) in enumerate(tiles):
                    emit_score(r, n, ti)
                    if idx >= LAG:
                        emit_pv(*tiles[idx - LAG])
                for idx in range(max(0, len(tiles) - LAG), len(tiles)):
                    emit_pv(*tiles[idx])
            for ck in range(S // 512):
                csl = slice(ck * 512, (ck + 1) * 512)
                cc = ccount[0]
                ccount[0] += 1
                rd = RD[cc % 2]
                ew('pool', lambda e, rd=rd, csl=csl: e.tensor_tensor(out=rd[0:64, :], in0=DEN[0][0:64, csl],
                                                                     in1=DEN[1][0:64, csl], op=ALU.add),
                   ['DEN0', 'DEN1'], ['RD%d' % (cc % 2)])
                ew('pool', lambda e, rd=rd, csl=csl: e.tensor_tensor(out=rd[0:64, :], in0=rd[0:64, :],
                                                                     in1=DEN[2][0:64, csl], op=ALU.add),
                   ['RD%d' % (cc % 2), 'DEN2'], ['RD%d' % (cc % 2)])
                ew('dve', lambda e, rd=rd: e.reciprocal(out=rd[0:64, :], in_=rd[0:64, :]), ['RD%d' % (cc % 2)],
                   ['RD%d' % (cc % 2)])
                for g in range(3):
                    h = 4 * g + j
                    pair = h // 2
                    pb = 64 * (h % 2)
                    yi = (cc * 3 + g) % 2
                    ys = YBS[yi]
                    ew('pool', lambda e, g=g, csl=csl, rd=rd, ys=ys: e.tensor_tensor(
                        out=ys[0:64, :], in0=OT[g][0:64, csl], in1=rd[0:64, :], op=ALU.mult),
                        ['OT%d' % g, 'RD%d' % (cc % 2)], ['YBS%d' % yi])
                    P.op('sp', lambda e, pair=pair, pb=pb, csl=csl, ys=ys: e.dma_start(
                        out=yb_d[pair, pb:pb + 64, csl], in_=ys[0:64, :]),
                        reads=[R_('YBS%d' % yi)], writes=['d_yb'], dma_key=R_('stYB%d' % yi))

    wpr_d = din("w_proj_rwkv", [1024, 1024])
    wpa_d = din("w_proj_attn", [768, 1024])
    wo_d = din("w_out", [1024, 1024])
    x2_d = nc.dram_tensor("x2_scr", [S, D], F32, kind="ExternalOutput" if debug else "Internal").ap()

    def merge_stage():
        A.off = const_mark
        pre = 'm_'

        def R_(n):
            return pre + n

        def nm(x):
            if x.startswith('pb'):
                return x
            return R_(x)

        def ew(eng, fn, reads, writes):
            P.op(eng, fn, reads=[nm(x) for x in reads], writes=[nm(x) for x in writes])
        Wr = A.alloc([8, 1024], BF16)
        Wa = A.alloc([6, 1024], BF16)
        Wo = A.alloc([8, 1024], BF16)
        X = [A.alloc([NSUB, D], F32) for _ in range(2)]
        YA = [A.alloc([8, TT], BF16) for _ in range(2)]
        YB = [A.alloc([6, TT], BF16) for _ in range(2)]
        G = [A.alloc([16, TT], BF16) for _ in range(2)]
        MT = A.alloc([8, TT], BF16)
        t1 = [A.alloc([TT], F32) for _ in range(2)]
        t2 = [A.alloc([TT], F32) for _ in range(2)]
        print("merge_stage arena", A.off)
        for kc in range(8):
            P.op('pool', lambda e, kc=kc: e.dma_start(out=Wr[:, kc, :], in_=wpr_d[kc * 128:(kc + 1) * 128, :]),
                 writes=[R_('Wr')], dma_key=R_('w'))
            P.op('pool', lambda e, kc=kc: e.dma_start(out=Wo[:, kc, :], in_=wo_d[kc * 128:(kc + 1) * 128, :]),
                 writes=[R_('Wo')], dma_key=R_('w'))
        for kc in range(6):
            P.op('pool', lambda e, kc=kc: e.dma_start(out=Wa[:, kc, :], in_=wpa_d[kc * 128:(kc + 1) * 128, :]),
                 writes=[R_('Wa')], dma_key=R_('w'))
        srcv = x1_d.rearrange("(n s p) d -> n p s d", p=128, s=NSUB)
        dstv = x2_d.rearrange("(n s p) d -> n p s d", p=128, s=NSUB)
        for it in range(NT):
            sl = it % 2
            tsl = slice(it * TT, (it + 1) * TT)
            sfx = '%d' % sl
            P.op('sp', lambda e, it=it, sl=sl: e.dma_start(out=X[sl], in_=srcv[it]), writes=[R_('X' + sfx)],
                 dma_key=R_('ldX' + sfx))
            P.op('sp', lambda e, sl=sl, tsl=tsl: e.dma_start(out=YA[sl], in_=ya_d.rearrange("b p t -> p b t")[:, :, tsl]),
                 writes=[R_('YA' + sfx)], dma_key=R_('ldYA' + sfx))
            P.op('sp', lambda e, sl=sl, tsl=tsl: e.dma_start(out=YB[sl], in_=yb_d.rearrange("b p t -> p b t")[:, :, tsl]),
                 writes=[R_('YB' + sfx)], dma_key=R_('ldYB' + sfx))
            P.op('sp', lambda e, sl=sl, tsl=tsl: e.dma_start(out=G[sl], in_=gate_d.rearrange("b p t -> p b t")[:, :, tsl]),
                 writes=[R_('G' + sfx)], dma_key=R_('ldG' + sfx))
            for c in range(8):
                bk = 1 + c % 2
                pg = bank(bk)
                for kc in range(8):
                    ew('pe', lambda e, c=c, kc=kc, pg=pg, sl=sl: e.matmul(
                        pg[:, 0:TT], lhsT=Wr[:, kc, c * 128:(c + 1) * 128], rhs=YA[sl][:, kc, :],
                        start=(kc == 0), stop=(kc == 7)), ['Wr', 'YA' + sfx], ['pb%d' % bk])
                for kc in range(6):
                    ew('pe', lambda e, c=c, kc=kc, pg=pg, sl=sl: e.matmul(
                        pg[:, TT:2 * TT], lhsT=Wa[:, kc, c * 128:(c + 1) * 128], rhs=YB[sl][:, kc, :],
                        start=(kc == 0), stop=(kc == 5)), ['Wa', 'YB' + sfx], ['pb%d' % bk])
                q = c % 2
                ew('dve', lambda e, c=c, pg=pg, sl=sl, q=q: e.tensor_tensor(out=t1[q], in0=G[sl][:, c, :],
                                                                           in1=pg[:, 0:TT], op=ALU.mult),
                   ['G' + sfx, 'pb%d' % bk], ['t1%d' % q])
                ew('dve', lambda e, c=c, pg=pg, sl=sl, q=q: e.tensor_tensor(out=t2[q], in0=G[sl][:, 8 + c, :],
                                                                           in1=pg[:, TT:2 * TT], op=ALU.mult),
                   ['G' + sfx, 'pb%d' % bk], ['t2%d' % q])
                ew('pool', lambda e, c=c, q=q: e.tensor_tensor(out=MT[:, c, :], in0=t1[q], in1=t2[q], op=ALU.add),
                   ['t1%d' % q, 't2%d' % q], ['MT'])
            for s in range(NSUB):
                for dh in range(2):
                    bk = 3 + (s * 2 + dh) % 2
                    pd = bank(bk)
                    for c in range(8):
                        ew('pe', lambda e, c=c, s=s, dh=dh, pd=pd: e.matmul(
                            pd, lhsT=MT[:, c, s * 128:(s + 1) * 128], rhs=Wo[:, c, dh * 512:(dh + 1) * 512],
                            start=(c == 0), stop=(c == 7)), ['MT', 'Wo'], ['pb%d' % bk])
                    ew('dve', lambda e, s=s, dh=dh, pd=pd, sl=sl: e.tensor_tensor(
                        out=X[sl][:, s, dh * 512:(dh + 1) * 512], in0=X[sl][:, s, dh * 512:(dh + 1) * 512], in1=pd,
                        op=ALU.add), ['pb%d' % bk, 'X' + sfx], ['X' + sfx])
            P.op('sp', lambda e, it=it, sl=sl: e.dma_start(out=dstv[it], in_=X[sl]),
                 reads=[R_('X' + sfx)], writes=['d_x2'], dma_key=R_('stX' + sfx))

    if 'ffn1' in stages:
        ffn_stage(0, x, x1_d)
        P.sync_all()
    if 'proj' in stages:
        proj_stage()
        P.sync_all()
    if 'rwkv' in stages:
        rwkv_stage(range(NPAIRS_DBG))
        P.sync_all()
    if 'attn' in stages:
        attn_stage_full(JS_DBG)
        P.sync_all()
    if 'merge' in stages:
        merge_stage()
        P.sync_all()
    if 'ffn2' in stages:
        ffn_stage(1, x2_d if 'merge' in stages else x1_d, out)
    P.sync_all()
    P.op('sp', None)
    P.finalize_and_emit(stack)
    stack.close()
    return nc


_CACHE = {}


SHARED_KEYS = ['ffn1_norm', 'ffn1_w_in', 'ffn1_w_out', 'ffn2_norm', 'ffn2_w_in', 'ffn2_w_out',
               'w_in', 'mix_norm', 'rwkv_mu', 'b_gate', 'attn_q_norm', 'attn_k_norm',
               'w_proj_rwkv', 'w_proj_attn', 'w_out', 'rwkv_w2', 'rwkv_a2', 'rwkv_g2', 'rwkv_w0', 'rwkv_a0', 'rwkv_k_k', 'rwkv_k_a', 'rwkv_r_k', 'rwkv_ln_w', 'rwkv_ln_b']


def make_shared(inputs):
    shared = {}
    for k in SHARED_KEYS:
        v = np.asarray(inputs[k], dtype=np.float32)
        v = v.reshape(v.shape[1:])
        if k == 'rwkv_r_k':
            v = v.reshape(-1)
        shared[k] = np.ascontiguousarray(v)
    return shared


def kernel(**inputs):
    if 'nc' not in _CACHE:
        _CACHE['nc'] = build_program()
    nc = _CACHE['nc']
    x = np.ascontiguousarray(inputs['x'], dtype=np.float32)
    shared = make_shared(inputs)
    in_maps = []
    for c in range(NCORES):
        m = dict(shared)
        m['x'] = x[c]
        in_maps.append(m)
    res = run_bass_kernel_spmd(nc, in_maps, core_ids=list(range(NCORES)))
    return np.stack([np.asarray(r['out']) for r in res.results], axis=0)
```
